# Optimizing a Trainium2 kernel written in Bass

```python
import jax
import jax.numpy as jnp
from jax import lax
import numpy as np


D_MODEL = 2048
BATCH = 8
SEQ = 4096
DEPTH = 2

N_MIXERS = 4
BRANCH_WIDTH = D_MODEL // N_MIXERS
D_FF = 4 * D_MODEL
ROPE_THETA = 500000.0
NORM_EPS = 1e-6
Q_BLOCK = 128
MASK_VALUE = -1e30

MLA_NOPE_DIM = 128
MLA_ROPE_DIM = 64
MLA_V_DIM = 128
MLA_HEADS = BRANCH_WIDTH // MLA_V_DIM
MLA_Q_RANK = 384
MLA_KV_RANK = 128

NSA_HEAD_DIM = 128
NSA_HEADS = BRANCH_WIDTH // NSA_HEAD_DIM
NSA_ROT_DIM = NSA_HEAD_DIM // 4
NSA_CMP_LEN = 32
NSA_CMP_STRIDE = 16
NSA_CMP_HIDDEN = 128
NSA_SEL_BLOCK = 64
NSA_TOP_N = 16
NSA_WINDOW = 512
NSA_SEL_Q_CHUNK = 64
NSA_FORCED_SCORE = 1000.0

RWKV_HEAD_DIM = 64
RWKV_HEADS = BRANCH_WIDTH // RWKV_HEAD_DIM
RWKV_DECAY_LORA = 96
RWKV_A_LORA = 96
RWKV_GATE_LORA = 256
RWKV_GN_EPS = 64e-5
RWKV_IN = 3 * BRANCH_WIDTH + RWKV_DECAY_LORA + RWKV_A_LORA + RWKV_GATE_LORA

RET_HEADS = 4
RET_V_DIM = BRANCH_WIDTH // RET_HEADS
RET_K_DIM = RET_V_DIM // 2
RET_CHUNK = 128
RET_THETA = 10000.0

IN_SPLITS = (
    MLA_Q_RANK, MLA_KV_RANK, MLA_ROPE_DIM,
    NSA_HEADS * NSA_HEAD_DIM,
    NSA_HEAD_DIM, NSA_HEAD_DIM,
    NSA_HEAD_DIM, NSA_HEAD_DIM,
    NSA_HEAD_DIM, NSA_HEAD_DIM,
    NSA_HEADS * 3,
    RWKV_IN,
    RET_HEADS * RET_K_DIM, RET_HEADS * RET_K_DIM, BRANCH_WIDTH, BRANCH_WIDTH,
    N_MIXERS * D_MODEL,
)
IN_WIDTH = sum(IN_SPLITS)

kernel_name = 'hybrid_mla_nsa_rwkv7_retention_block'


def split_cols(h, sizes):
    offs = np.cumsum(np.array(sizes))[:-1].tolist()
    return jnp.split(h, offs, axis=-1)


def rms_norm(x, g, eps=NORM_EPS):
    xf = x.astype(jnp.float32)
    y = xf * lax.rsqrt(jnp.mean(xf * xf, axis=-1, keepdims=True) + eps)
    return (y * g.astype(jnp.float32)).astype(x.dtype)


def head_norm(o, eps):
    o = o.astype(jnp.float32)
    c = o - jnp.mean(o, axis=-1, keepdims=True)
    return c * lax.rsqrt(jnp.mean(c * c, axis=-1, keepdims=True) + eps)


def rope_inv_freq(rot_dim, theta):
    return jnp.float32(theta) ** (-jnp.arange(0, rot_dim, 2, dtype=jnp.float32) / rot_dim)


def rotary_tables(seq_len, inv_freq):
    ang = jnp.arange(seq_len, dtype=jnp.float32)[:, None] * inv_freq[None, :]
    return jnp.cos(ang), jnp.sin(ang)


def apply_rotary(x, cos, sin):
    half = cos.shape[-1]
    x1, x2, rest = x[..., :half], x[..., half:2 * half], x[..., 2 * half:]
    c = cos[:, None, :].astype(x.dtype)
    s = sin[:, None, :].astype(x.dtype)
    return jnp.concatenate([x1 * c - x2 * s, x2 * c + x1 * s, rest], axis=-1)


def masked_softmax(s, mask):
    s = jnp.where(mask, s.astype(jnp.float32), MASK_VALUE)
    p = jax.nn.softmax(s, axis=-1)
    return jnp.where(mask, p, 0.0)


def mla_mixer(q_lat, kv_lat, k_rope, g_q, g_kv, w_uq, w_ukv, cos, sin):
    B, S, _ = q_lat.shape
    H, Dn, Dr, Dv = MLA_HEADS, MLA_NOPE_DIM, MLA_ROPE_DIM, MLA_V_DIM
    q = (rms_norm(q_lat, g_q) @ w_uq).reshape(B, S, H, Dn + Dr)
    q_nope, q_rope = q[..., :Dn], apply_rotary(q[..., Dn:], cos, sin)
    kv = (rms_norm(kv_lat, g_kv) @ w_ukv).reshape(B, S, H, Dn + Dv)
    k_nope, v = kv[..., :Dn], kv[..., Dn:]
    k_rope = apply_rotary(k_rope[:, :, None, :], cos, sin)[:, :, 0]
    scale = (Dn + Dr) ** -0.5
    nq = S // Q_BLOCK
    to_blocks = lambda t: t.reshape(B, nq, Q_BLOCK, H, t.shape[-1]).transpose(1, 0, 2, 3, 4)
    kpos = jnp.arange(S)

    def attend(args):
        qn, qr, q0 = args
        s = jnp.einsum('bqhd,bkhd->bhqk', qn, k_nope) + jnp.einsum('bqhd,bkd->bhqk', qr, k_rope)
        qpos = q0 + jnp.arange(Q_BLOCK)
        p = masked_softmax(s * scale, kpos[None, :] <= qpos[:, None])
        return jnp.einsum('bhqk,bkhd->bqhd', p.astype(v.dtype), v)

    o = lax.map(attend, (to_blocks(q_nope), to_blocks(q_rope), jnp.arange(nq) * Q_BLOCK))
    return o.transpose(1, 0, 2, 3, 4).reshape(B, S, H * Dv)


def nsa_mixer(q, k_cmp, v_cmp, k_slc, v_slc, k_win, v_win, gate_logits, cmp_pos, cmp_w1, cmp_w2, cos, sin):
    B, S, _ = q.shape
    H, Dh = NSA_HEADS, NSA_HEAD_DIM
    scale = Dh ** -0.5
    pos = jnp.arange(S)
    q = apply_rotary(q.reshape(B, S, H, Dh), cos, sin)
    rot_k = lambda k: apply_rotary(k[:, :, None, :], cos, sin)[:, :, 0]
    k_cmp, k_slc, k_win = rot_k(k_cmp), rot_k(k_slc), rot_k(k_win)

    n_cmp = (S - NSA_CMP_LEN) // NSA_CMP_STRIDE + 1
    cmp_start = jnp.arange(n_cmp) * NSA_CMP_STRIDE
    cmp_end = cmp_start + NSA_CMP_LEN - 1
    cmp_idx = cmp_start[:, None] + jnp.arange(NSA_CMP_LEN)[None, :]
    raw = jnp.stack([k_cmp, v_cmp])[:, :, cmp_idx] + cmp_pos[:, None, None]
    raw = raw.reshape(2, B, n_cmp, NSA_CMP_LEN * Dh)
    hid = jax.nn.gelu(jnp.einsum('zbnf,zfe->zbne', raw, cmp_w1))
    kvc = jnp.einsum('zbne,zed->zbnd', hid, cmp_w2)
    kc, vc = kvc[0], kvc[1]
    s_cmp = jnp.einsum('bshd,bnd->bhsn', q, kc) * scale
    p_cmp = masked_softmax(s_cmp, cmp_end[None, :] <= pos[:, None])
    o_cmp = jnp.einsum('bhsn,bnd->bshd', p_cmp.astype(vc.dtype), vc)

    n_blk = S // NSA_SEL_BLOCK
    jb = jnp.arange(n_blk)
    blk_start = jb * NSA_SEL_BLOCK
    cover = ((cmp_start[:, None] <= blk_start[None, :] + NSA_SEL_BLOCK - 1)
             & (cmp_end[:, None] >= blk_start[None, :])).astype(jnp.float32)
    imp = jnp.einsum('bhsn,nj->bsj', p_cmp, cover)
    cur = pos // NSA_SEL_BLOCK
    forced = (jb[None, :] == 0) | (jb[None, :] == cur[:, None]) | (jb[None, :] == cur[:, None] - 1)
    visible = jb[None, :] <= cur[:, None]
    imp = jnp.where(visible, imp + NSA_FORCED_SCORE * forced, -jnp.inf)
    n_sel = min(NSA_TOP_N, n_blk)
    _, sel = lax.top_k(imp, n_sel)
    kb = k_slc.reshape(B, n_blk, NSA_SEL_BLOCK, Dh)
    vb = v_slc.reshape(B, n_blk, NSA_SEL_BLOCK, Dh)
    gather = jax.vmap(lambda blocks, idx: blocks[idx])
    qc_len = NSA_SEL_Q_CHUNK
    nqc = S // qc_len
    offs = jnp.arange(NSA_SEL_BLOCK)

    def attend_sel(args):
        qc, ic, q0 = args
        ks = gather(kb, ic).reshape(B, qc_len, n_sel * NSA_SEL_BLOCK, Dh)
        vs = gather(vb, ic).reshape(B, qc_len, n_sel * NSA_SEL_BLOCK, Dh)
        kpos = (ic[..., None] * NSA_SEL_BLOCK + offs).reshape(B, qc_len, n_sel * NSA_SEL_BLOCK)
        qpos = q0 + jnp.arange(qc_len)
        s = jnp.einsum('bqhd,bqkd->bhqk', qc, ks) * scale
        p = masked_softmax(s, (kpos <= qpos[None, :, None])[:, None])
        return jnp.einsum('bhqk,bqkd->bqhd', p.astype(vs.dtype), vs)

    o_slc = lax.map(attend_sel, (q.reshape(B, nqc, qc_len, H, Dh).transpose(1, 0, 2, 3, 4),
                                 sel.reshape(B, nqc, qc_len, n_sel).transpose(1, 0, 2, 3),
                                 jnp.arange(nqc) * qc_len))
    o_slc = o_slc.transpose(1, 0, 2, 3, 4).reshape(B, S, H, Dh)

    nqb = S // Q_BLOCK
    span = NSA_WINDOW + Q_BLOCK
    band = jnp.arange(nqb)[:, None] * Q_BLOCK + jnp.arange(span)[None, :]
    pad = ((0, 0), (NSA_WINDOW, 0), (0, 0))
    kw = jnp.pad(k_win, pad)[:, band]
    vw = jnp.pad(v_win, pad)[:, band]
    kpos_w = band - NSA_WINDOW
    qpos_w = pos.reshape(nqb, Q_BLOCK)
    dist = qpos_w[:, :, None] - kpos_w[:, None, :]
    wmask = (dist >= 0) & (dist < NSA_WINDOW) & (kpos_w[:, None, :] >= 0)
    s_win = jnp.einsum('bnqhd,bnkd->bhnqk', q.reshape(B, nqb, Q_BLOCK, H, Dh), kw) * scale
    p_win = masked_softmax(s_win, wmask)
    o_win = jnp.einsum('bhnqk,bnkd->bnqhd', p_win.astype(vw.dtype), vw).reshape(B, S, H, Dh)

    g = jax.nn.sigmoid(gate_logits.astype(jnp.float32)).reshape(B, S, H, 3).astype(q.dtype)
    o = g[..., 0:1] * o_cmp + g[..., 1:2] * o_slc + g[..., 2:3] * o_win
    return o.reshape(B, S, H * Dh)


def rwkv7_mixer(z, mu, w0, w2, a0, a2, g2, k_k, k_a, r_k, gn_w, gn_b):
    B, S, _ = z.shape
    H, N, C = RWKV_HEADS, RWKV_HEAD_DIM, BRANCH_WIDTH
    z = z.astype(jnp.float32)
    z_prev = jnp.pad(z, ((0, 0), (1, 0), (0, 0)))[:, :-1]
    z = z + (z_prev - z) * mu
    r, k, v, wd, ad, gd = split_cols(z, (C, C, C, RWKV_DECAY_LORA, RWKV_A_LORA, RWKV_GATE_LORA))
    w_log = -jax.nn.softplus(-(w0 + jnp.tanh(wd) @ w2)) - 0.5
    decay = jnp.exp(-jnp.exp(w_log))
    a = jax.nn.sigmoid(a0 + ad @ a2)
    g = jax.nn.sigmoid(gd) @ g2
    heads = lambda t: t.reshape(B, S, H, N)
    kk = heads(k * k_k)
    kk = kk * lax.rsqrt(jnp.maximum(jnp.sum(kk * kk, axis=-1, keepdims=True), 1e-24))
    k = k * (1.0 + (a - 1.0) * k_a)
    r, k, v, a, decay = heads(r), heads(k), heads(v), heads(a), heads(decay)

    def step(state, inp):
        r_t, w_t, k_t, v_t, kk_t, a_t = inp
        sa = jnp.einsum('bhvk,bhk->bhv', state, -kk_t)
        state = (state * w_t[:, :, None, :] + sa[..., None] * (kk_t * a_t)[:, :, None, :]
                 + v_t[..., None] * k_t[:, :, None, :])
        return state, jnp.einsum('bhvk,bhk->bhv', state, r_t)

    tmaj = lambda t: jnp.moveaxis(t, 1, 0)
    _, o = lax.scan(step, jnp.zeros((B, H, N, N), jnp.float32),
                    (tmaj(r), tmaj(decay), tmaj(k), tmaj(v), tmaj(kk), tmaj(a)))
    o = head_norm(jnp.moveaxis(o, 0, 1), RWKV_GN_EPS).reshape(B, S, C) * gn_w + gn_b
    bonus = jnp.sum(r * k * r_k, axis=-1, keepdims=True) * v
    return (o + bonus.reshape(B, S, C)) * g


def retention_mixer(q, k, v, gate, cos, sin):
    B, S, _ = q.shape
    H, Dk, Dv, C = RET_HEADS, RET_K_DIM, RET_V_DIM, RET_CHUNK
    f32 = jnp.float32
    q = apply_rotary(q.astype(f32).reshape(B, S, H, Dk), cos, sin)
    k = apply_rotary(k.astype(f32).reshape(B, S, H, Dk), cos, sin) * Dk ** -0.5
    v = v.astype(f32).reshape(B, S, H, Dv)
    nc = S // C
    to_chunks = lambda t: t.reshape(B, nc, C, H, t.shape[-1]).transpose(0, 3, 1, 2, 4)
    qc, kc, vc = to_chunks(q), to_chunks(k), to_chunks(v)
    log_gamma = jnp.log1p(-jnp.exp2(-5.0 - jnp.arange(H, dtype=f32)))
    n = jnp.arange(C, dtype=f32)
    dist = n[:, None] - n[None, :]
    inner_decay = jnp.where(dist >= 0, jnp.exp(jnp.maximum(dist, 0.0) * log_gamma[:, None, None]), 0.0)
    q_decay = jnp.exp((n + 1.0) * log_gamma[:, None])
    k_decay = jnp.exp((C - 1.0 - n) * log_gamma[:, None])
    chunk_decay = jnp.exp(C * log_gamma)
    scores = jnp.einsum('bhcnd,bhcmd->bhcnm', qc, kc) * inner_decay[:, None]
    o_inner = jnp.einsum('bhcnm,bhcme->bhcne', scores, vc)
    u = jnp.einsum('bhcmd,bhcme->cbhde', kc * k_decay[:, None, :, None], vc)

    def step(state, u_c):
        return state * chunk_decay[:, None, None] + u_c, state

    _, prev = lax.scan(step, jnp.zeros((B, H, Dk, Dv), f32), u)
    o_cross = jnp.einsum('bhcnd,cbhde->bhcne', qc * q_decay[:, None, :, None], prev)
    o = (o_inner + o_cross).transpose(0, 2, 3, 1, 4).reshape(B, S, H, Dv)
    return head_norm(o, NORM_EPS).reshape(B, S, H * Dv) * jax.nn.silu(gate.astype(f32))


def setup_inputs(seed: int = 0) -> dict:
    key = jax.random.key(seed)
    ks = jax.random.split(key, 25)
    f32 = jnp.float32
    nrm = lambda k, shape, scale: scale * jax.random.normal(k, shape, f32)
    L, C = DEPTH, BRANCH_WIDTH
    return {
        'x': nrm(ks[0], (BATCH, SEQ, D_MODEL), 1.0),
        'w_in': nrm(ks[1], (L, D_MODEL, IN_WIDTH), D_MODEL ** -0.5),
        'w_branch': nrm(ks[2], (L, N_MIXERS, C, D_MODEL), C ** -0.5),
        'w_out': nrm(ks[3], (L, D_MODEL, D_MODEL), D_MODEL ** -0.5),
        'w_up': nrm(ks[4], (L, D_MODEL, D_FF), D_MODEL ** -0.5),
        'w_down': nrm(ks[5], (L, D_FF, D_MODEL), D_FF ** -0.5),
        'norm_gains': 1.0 + nrm(ks[6], (L, 4, D_MODEL), 0.02),
        'mla_g_q': 1.0 + nrm(ks[7], (L, MLA_Q_RANK), 0.02),
        'mla_g_kv': 1.0 + nrm(ks[8], (L, MLA_KV_RANK), 0.02),
        'mla_w_uq': nrm(ks[9], (L, MLA_Q_RANK, MLA_HEADS * (MLA_NOPE_DIM + MLA_ROPE_DIM)), MLA_Q_RANK ** -0.5),
        'mla_w_ukv': nrm(ks[10], (L, MLA_KV_RANK, MLA_HEADS * (MLA_NOPE_DIM + MLA_V_DIM)), MLA_KV_RANK ** -0.5),
        'nsa_cmp_pos': nrm(ks[11], (L, 2, NSA_CMP_LEN, NSA_HEAD_DIM), 0.1),
        'nsa_cmp_w1': nrm(ks[12], (L, 2, NSA_CMP_LEN * NSA_HEAD_DIM, NSA_CMP_HIDDEN), (NSA_CMP_LEN * NSA_HEAD_DIM) ** -0.5),
        'nsa_cmp_w2': nrm(ks[13], (L, 2, NSA_CMP_HIDDEN, NSA_HEAD_DIM), NSA_CMP_HIDDEN ** -0.5),
        'rwkv_mu': jax.random.uniform(ks[14], (L, RWKV_IN), f32),
        'rwkv_w0': jax.random.uniform(ks[15], (L, C), f32, -6.0, -1.0),
        'rwkv_w2': nrm(ks[16], (L, RWKV_DECAY_LORA, C), 0.1 * RWKV_DECAY_LORA ** -0.5),
        'rwkv_a0': nrm(ks[17], (L, C), 0.1),
        'rwkv_a2': nrm(ks[18], (L, RWKV_A_LORA, C), RWKV_A_LORA ** -0.5),
        'rwkv_g2': nrm(ks[19], (L, RWKV_GATE_LORA, C), RWKV_GATE_LORA ** -0.5),
        'rwkv_k_k': 0.85 + nrm(ks[20], (L, C), 0.05),
        'rwkv_k_a': 1.0 + nrm(ks[21], (L, C), 0.05),
        'rwkv_r_k': nrm(ks[22], (L, RWKV_HEADS, RWKV_HEAD_DIM), 0.1),
        'rwkv_gn_w': 1.0 + nrm(ks[23], (L, C), 0.02),
        'rwkv_gn_b': nrm(ks[24], (L, C), 0.02),
    }


def reference(x, w_in, w_branch, w_out, w_up, w_down, norm_gains, mla_g_q, mla_g_kv, mla_w_uq, mla_w_ukv,
              nsa_cmp_pos, nsa_cmp_w1, nsa_cmp_w2, rwkv_mu, rwkv_w0, rwkv_w2, rwkv_a0, rwkv_a2, rwkv_g2,
              rwkv_k_k, rwkv_k_a, rwkv_r_k, rwkv_gn_w, rwkv_gn_b):
    B, S, _ = x.shape
    dt = x.dtype
    mla_cos, mla_sin = rotary_tables(S, rope_inv_freq(MLA_ROPE_DIM, ROPE_THETA))
    nsa_cos, nsa_sin = rotary_tables(S, rope_inv_freq(NSA_ROT_DIM, ROPE_THETA))
    ret_cos, ret_sin = rotary_tables(S, jnp.float32(RET_THETA) ** (-jnp.linspace(0.0, 1.0, RET_K_DIM // 2, dtype=jnp.float32)))
    for l in range(DEPTH):
        h = rms_norm(x, norm_gains[l, 0])
        (mla_ql, mla_kvl, mla_kr, nsa_q, nsa_kc, nsa_vc, nsa_ks, nsa_vs, nsa_kw, nsa_vw, nsa_g,
         rwkv_z, ret_q, ret_k, ret_v, ret_g, merge_logits) = split_cols(h @ w_in[l], IN_SPLITS)
        branches = (
            mla_mixer(mla_ql, mla_kvl, mla_kr, mla_g_q[l], mla_g_kv[l], mla_w_uq[l], mla_w_ukv[l], mla_cos, mla_sin),
            nsa_mixer(nsa_q, nsa_kc, nsa_vc, nsa_ks, nsa_vs, nsa_kw, nsa_vw, nsa_g,
                      nsa_cmp_pos[l], nsa_cmp_w1[l], nsa_cmp_w2[l], nsa_cos, nsa_sin),
            rwkv7_mixer(rwkv_z, rwkv_mu[l], rwkv_w0[l], rwkv_w2[l], rwkv_a0[l], rwkv_a2[l], rwkv_g2[l],
                        rwkv_k_k[l], rwkv_k_a[l], rwkv_r_k[l], rwkv_gn_w[l], rwkv_gn_b[l]).astype(dt),
            retention_mixer(ret_q, ret_k, ret_v, ret_g, ret_cos, ret_sin).astype(dt),
        )
        merged = None
        for m, br in enumerate(branches):
            gate = jax.nn.sigmoid(merge_logits[..., m * D_MODEL:(m + 1) * D_MODEL])
            term = gate * (br @ w_branch[l, m])
            merged = term if merged is None else merged + term
        x = x + rms_norm(merged @ w_out[l], norm_gains[l, 1])
        h = rms_norm(x, norm_gains[l, 2])
        ff = jnp.square(jax.nn.relu(h @ w_up[l])) @ w_down[l]
        x = x + rms_norm(ff, norm_gains[l, 3])
    return x
```

```python
import contextlib
import numpy as np
import concourse.bass as bass
import concourse.mybir as mybir
from concourse.bass_utils import run_bass_kernel_spmd

F32 = mybir.dt.float32
F32R = mybir.dt.float32r
RW_FAST = True


def fr(ap):
    return ap.bitcast(F32R) if RW_FAST else ap
BF16 = mybir.dt.bfloat16
AF = mybir.ActivationFunctionType
ALU = mybir.AluOpType
AX = mybir.AxisListType

D_MODEL = 2048
DEPTH = 2
BW = 512
D_FF = 8192
NORM_EPS = 1e-6
IN_WIDTH = 13580
OFF_MLA = 0
OFF_NSA = 576
OFF_RWKV = 1868
OFF_RET = 3852
OFF_GATE = 5388

SELF_SYNC = {'pe': False, 'act': False, 'dve': True, 'pool': True, 'sp': False}


class Reg:
    __slots__ = ('w', 'r')

    def __init__(self):
        self.w = None
        self.r = {}


class Buf:
    def __init__(self, t, nreg=1, excl=False):
        self.t = t
        self.regs = [Reg() for _ in range(nreg)]
        self.excl = excl

    @property
    def reg(self):
        return self.regs[0]

    def __getitem__(self, idx):
        return self.t[idx]


class Ctx:
    def __init__(self):
        self.nc = bass.Bass("TRN2", target_bir_lowering=False)
        nc = self.nc
        self.E = {'pe': nc.tensor, 'act': nc.scalar, 'dve': nc.vector, 'pool': nc.gpsimd, 'sp': nc.sync}
        self.sem = {e: nc.alloc_semaphore("s_" + e) for e in ['pe', 'act', 'dve', 'pool']}
        self.cnt = {e: 0 for e in self.sem}
        self.NDS = 48
        self.dsem = [nc.alloc_semaphore("d%d" % i) for i in range(self.NDS)]
        self.dcnt = [0] * self.NDS
        self.dpool = {'sp': list(range(0, 20)), 'act': list(range(20, 34)), 'pool': list(range(34, 48))}
        self.dnext = {'sp': 0, 'act': 0, 'pool': 0}
        self.known = {e: {} for e in self.E}
        self.ninst = 0
        self.uid = 0

    def name(self, p):
        self.uid += 1
        return "%s_%d" % (p, self.uid)

    def sb(self, stack, shape, dtype, nreg=1, name="sb"):
        t = stack.enter_context(self.nc.sbuf_tensor(self.name(name), list(shape), dtype))
        return Buf(t, nreg)

    def ps(self, stack, shape, dtype=F32, nreg=1, name="ps"):
        t = stack.enter_context(self.nc.psum_tensor(self.name(name), list(shape), dtype))
        return Buf(t, nreg, excl=True)

    def dram(self, name, shape, dtype, kind="Internal"):
        return self.nc.dram_tensor(name, list(shape), dtype, kind=kind).ap()

    def _wait(self, e, kind, val, force=False):
        if isinstance(kind, str):
            if kind == e and not SELF_SYNC[e] and not (force and e in self.sem):
                return
            sem = self.sem[kind]
            v = val
        else:
            idx = kind[1]
            sem = self.dsem[idx]
            v = val * 16
        k = self.known[e]
        if k.get(kind, 0) >= v:
            return
        self.E[e].wait_ge(sem, v)
        self.ninst += 1
        k[kind] = v

    def _deps(self, e, reads, writes, force=False):
        for r in reads:
            if r.w is not None:
                self._wait(e, r.w[0], r.w[1], force)
        for w in writes:
            if w.w is not None:
                self._wait(e, w.w[0], w.w[1], force)
            for kind, val in w.r.items():
                self._wait(e, kind, val, force)

    def _commit(self, tok, reads, writes):
        kind, val = tok
        for r in reads:
            if r.r.get(kind, 0) < val:
                r.r[kind] = val
        for w in writes:
            w.w = tok
            w.r = {}

    @staticmethod
    def _regs(lst):
        out = []
        for x in lst:
            if isinstance(x, Buf):
                out.extend(x.regs)
            elif isinstance(x, Reg):
                out.append(x)
            elif x is None:
                pass
            else:
                raise TypeError(type(x))
        return out

    def op(self, e, reads, writes, fn):
        writes = list(writes) + [x for x in reads if isinstance(x, Buf) and x.excl]
        reads = [x for x in reads if not (isinstance(x, Buf) and x.excl)]
        reads = self._regs(reads)
        writes = self._regs(writes)
        self._deps(e, reads, writes)
        inst = fn(self.E[e])
        self.cnt[e] += 1
        self.ninst += 1
        inst.then_inc(self.sem[e], 1)
        self._commit((e, self.cnt[e]), reads, writes)
        return inst

    def dma(self, q, out_ap, in_ap, reads=(), writes=(), **kw):
        reads = self._regs(reads)
        writes = self._regs(writes)
        self._deps(q, reads, writes, force=True)
        pool = self.dpool[q]
        idx = pool[self.dnext[q] % len(pool)]
        self.dnext[q] += 1
        if self.dcnt[idx] > 0:
            self._wait(q, ('d', idx), self.dcnt[idx])
        self.E[q].dma_start(out=out_ap, in_=in_ap, **kw).then_inc(self.dsem[idx], 16)
        self.dcnt[idx] += 1
        self.ninst += 1
        self._commit((('d', idx), self.dcnt[idx]), reads, writes)

    def barrier(self):
        for e in self.E:
            for o in self.sem:
                if o != e and self.cnt[o] > 0:
                    self._wait(e, o, self.cnt[o])
            for i in range(self.NDS):
                if self.dcnt[i] > 0:
                    self._wait(e, ('d', i), self.dcnt[i])
            if e in self.sem and self.cnt[e] > 0:
                k = self.known[e]
                if k.get(e, 0) < self.cnt[e]:
                    self.E[e].wait_ge(self.sem[e], self.cnt[e])
                    k[e] = self.cnt[e]


def mm(cx, out_buf, out_ap, lhsT_ap, rhs_ap, reads, start, stop, **kw):
    return cx.op('pe', reads, [out_buf],
                 lambda e: e.matmul(out_ap, lhsT_ap, rhs_ap, start=start, stop=stop, **kw))


def transp(cx, out_buf, out_ap, in_ap, ident_ap, reads):
    return cx.op('pe', reads, [out_buf], lambda e: e.transpose(out_ap, in_ap, ident_ap))


def phase_cast(cx, pairs):
    CH = 4096
    NB = 6
    with contextlib.ExitStack() as st:
        stg = [cx.sb(st, [128, CH], F32, name="cst") for _ in range(NB)]
        outb = [cx.sb(st, [128, CH], BF16, name="cob") for _ in range(NB)]
        engs = ['dve', 'pool', 'act']
        k = 0
        for src, dst in pairs:
            n = 1
            for s in src.shape:
                n *= s
            assert n % 128 == 0
            per = n // 128
            names = " ".join("a%d" % i for i in range(len(src.shape)))
            s2 = src.rearrange("%s -> (%s)" % (names, names)).rearrange("(p f) -> p f", p=128)
            d2 = dst.rearrange("%s -> (%s)" % (names, names)).rearrange("(p f) -> p f", p=128)
            for c0 in range(0, per, CH):
                c1 = min(per, c0 + CH)
                w = c1 - c0
                i = k % NB
                cx.dma('sp', stg[i][:, :w], s2[:, c0:c1], writes=[stg[i]])
                e = engs[k % 3]
                if e == 'act':
                    cx.op(e, [stg[i]], [outb[i]], lambda en: en.copy(outb[i][:, :w], stg[i][:, :w]))
                else:
                    cx.op(e, [stg[i]], [outb[i]], lambda en: en.tensor_copy(outb[i][:, :w], stg[i][:, :w]))
                cx.dma('act' if k % 2 else 'pool', d2[:, c0:c1], outb[i][:, :w], reads=[outb[i]])
                k += 1
    cx.barrier()


def load_bcast_row(cx, q, buf, row_ap, n):
    cx.dma(q, buf[:, :n], row_ap.partition_broadcast(128), writes=[buf])


def rms_rstd(cx, x_buf, x_ap, n, ss_buf, junk_buf, eps=NORM_EPS):
    cx.op('act', [x_buf], [junk_buf, ss_buf],
          lambda e: e.activation(out=junk_buf[:, :n], in_=x_ap, func=AF.Square, accum_out=ss_buf[:, 0:1]))
    cx.op('dve', [ss_buf], [ss_buf],
          lambda e: e.tensor_scalar(ss_buf[:, 0:1], ss_buf[:, 0:1], 1.0 / n, eps, ALU.mult, ALU.add))
    cx.op('pool', [ss_buf, cx.neghalf], [ss_buf],
          lambda e: e.tensor_tensor(ss_buf[:, 0:1], ss_buf[:, 0:1], cx.neghalf[:, 0:1], ALU.pow))


def setup_consts(cx, st, ident_d):
    cx.ident = cx.sb(st, [128, 128], BF16, name="ident")
    cx.dma('sp', cx.ident[:, :], ident_d[:, :], writes=[cx.ident])
    cx.identf = cx.sb(st, [128, 128], F32, name="identf")
    cx.op('dve', [cx.ident], [cx.identf], lambda e: e.tensor_copy(cx.identf[:, :], cx.ident[:, :]))
    cx.neghalf = cx.sb(st, [128, 1], F32, name="neghalf")
    cx.op('pool', [], [cx.neghalf], lambda e: e.memset(cx.neghalf[:, :], -0.5))
    cx.ones_bf = cx.sb(st, [128, 128], BF16, name="ones_bf")
    cx.op('pool', [], [cx.ones_bf], lambda e: e.memset(cx.ones_bf[:, :], 1.0))
    cx.ones_f = cx.sb(st, [128, 128], F32, name="ones_f")
    cx.op('pool', [], [cx.ones_f], lambda e: e.memset(cx.ones_f[:, :], 1.0))


def phase_in(cx, S, x_d, g_row, w_bf, P_d):
    G = 512
    KC = D_MODEL // 128
    chunks = []
    c = 0
    while c < OFF_GATE:
        chunks.append((c, min(c + 512, OFF_GATE), False))
        c += 512
    c = OFF_GATE
    while c < IN_WIDTH:
        chunks.append((c, c + 512, True))
        c += 512
    wv = w_bf.rearrange("(kc p) n -> p kc n", p=128)
    with contextlib.ExitStack() as st:
        gB = cx.sb(st, [128, D_MODEL], F32, name="gB")
        load_bcast_row(cx, 'sp', gB, g_row, D_MODEL)
        xt = [cx.sb(st, [128, D_MODEL], F32, name="xt") for _ in range(2)]
        junk = cx.sb(st, [128, D_MODEL], BF16, name="junk")
        ss = [cx.sb(st, [128, 1], F32, name="ss") for _ in range(2)]
        hb = [cx.sb(st, [128, D_MODEL], BF16, name="hb") for _ in range(2)]
        hT = [cx.sb(st, [128, KC, G], BF16, name="hT") for _ in range(2)]
        wb = [cx.sb(st, [128, KC, 512], BF16, name="wb") for _ in range(2)]
        ob = [cx.sb(st, [128, 512], F32, name="ob") for _ in range(4)]
        ptr = [cx.ps(st, [128, 8, 128], BF16, name="ptr") for _ in range(2)]
        pmm = [cx.ps(st, [128, 512], F32, name="pmm") for _ in range(4)]
        ntr = 0
        nmm = 0
        nw = 0
        for gi in range(S // G):
            hTg = hT[gi % 2]
            for tl in range(G // 128):
                tt = gi * (G // 128) + tl
                xb = xt[tt % 2]
                sb_ = ss[tt % 2]
                hbb = hb[tt % 2]
                cx.dma('sp', xb[:, :], x_d[tt * 128:(tt + 1) * 128, :], writes=[xb])
                rms_rstd(cx, xb, xb[:, :], D_MODEL, sb_, junk)
                cx.op('dve', [xb, sb_, gB], [hbb],
                      lambda e: e.scalar_tensor_tensor(out=hbb[:, :], in0=xb[:, :], scalar=sb_[:, 0:1],
                                                       in1=gB[:, :], op0=ALU.mult, op1=ALU.mult))
                for k4 in range(KC // 4):
                    pt = ptr[ntr % 2]
                    ntr += 1
                    for j in range(4):
                        kc = k4 * 4 + j
                        transp(cx, pt, pt[:, j, :], hbb[:, kc * 128:(kc + 1) * 128], cx.ident[:, :], [hbb, cx.ident])
                    eng = 'act' if (k4 % 2 == 0) else 'dve'
                    dst = hTg[:, k4 * 4:(k4 + 1) * 4, tl * 128:(tl + 1) * 128]
                    if eng == 'act':
                        cx.op('act', [pt], [hTg], lambda e: e.copy(dst, pt[:, 0:4, :]))
                    else:
                        cx.op('dve', [pt], [hTg], lambda e: e.tensor_copy(dst, pt[:, 0:4, :]))
            for (c0, c1, sig) in chunks:
                w = c1 - c0
                wbb = wb[nw % 2]
                nw += 1
                cx.dma('sp', wbb[:, :, :w], wv[:, :, c0:c1], writes=[wbb])
                for tl in range(G // 128):
                    tt = gi * (G // 128) + tl
                    pm = pmm[nmm % 4]
                    obb = ob[nmm % 4]
                    nmm += 1
                    for kc in range(KC):
                        mm(cx, pm, pm[:, :w], hTg[:, kc, tl * 128:(tl + 1) * 128], wbb[:, kc, :w],
                           [hTg, wbb], kc == 0, kc == KC - 1)
                    if sig:
                        cx.op('act', [pm], [obb],
                              lambda e: e.activation(out=obb[:, :w], in_=pm[:, :w], func=AF.Sigmoid))
                    elif nmm % 2 == 0:
                        cx.op('dve', [pm], [obb], lambda e: e.tensor_copy(obb[:, :w], pm[:, :w]))
                    else:
                        cx.op('act', [pm], [obb], lambda e: e.copy(obb[:, :w], pm[:, :w]))
                    cx.dma('pool', P_d[tt * 128:(tt + 1) * 128, c0:c1], obb[:, :w], reads=[obb])
    cx.barrier()


def evac(cx, eng, src_buf, src_ap, dst_buf, dst_ap, extra_reads=()):
    if eng == 'act':
        cx.op('act', [src_buf] + list(extra_reads), [dst_buf], lambda e: e.copy(dst_ap, src_ap))
    else:
        cx.op(eng, [src_buf] + list(extra_reads), [dst_buf], lambda e: e.tensor_copy(dst_ap, src_ap))


class TrPool:
    def __init__(self, cx, st, n=2, dtype=BF16):
        self.cx = cx
        self.bufs = [cx.ps(st, [128, 8 if dtype == BF16 else 4, 128], dtype, name="ptr") for _ in range(n)]
        self.k = 0
        self.dtype = dtype

    def transpose_cols(self, src_buf, src_ap_fn, nblk, dst_buf, dst_ap_fn, rows=128, blkw=128):
        cx = self.cx
        ident = cx.ident if self.dtype == BF16 else cx.identf
        j = 0
        while j < nblk:
            cnt = min(4, nblk - j)
            pt = self.bufs[self.k % len(self.bufs)]
            eng = 'act' if self.k % 2 == 0 else 'dve'
            self.k += 1
            for i in range(cnt):
                transp(cx, pt, pt[:blkw, i, :rows], src_ap_fn(j + i), ident[:rows, :rows], [src_buf, ident])
            evac(cx, eng, pt, pt[:blkw, :cnt, :rows], dst_buf, dst_ap_fn(j, cnt))
            j += cnt


def phase_merge(cx, S, ysrcs, wbr_bf, P_d, M_d):
    with contextlib.ExitStack() as st:
        wbr = cx.sb(st, [128, 16, D_MODEL], BF16, name="wbr")
        wv = wbr_bf.rearrange("m (kc p) n -> p (m kc) n", p=128)
        for q in range(4):
            cx.dma('sp', wbr[:, q * 4:(q + 1) * 4, :], wv[:, q * 4:(q + 1) * 4, :], writes=[wbr])
        yt = [cx.sb(st, [128, BW], F32, name="yt") for _ in range(3)]
        yb = [cx.sb(st, [128, BW], BF16, name="yb") for _ in range(2)]
        yT = [cx.sb(st, [128, 4, 128], BF16, name="yT") for _ in range(2)]
        sg = [cx.sb(st, [128, D_MODEL], F32, name="sg") for _ in range(2)]
        mg = [cx.sb(st, [128, D_MODEL], F32, name="mg") for _ in range(2)]
        tmp = [cx.sb(st, [128, 512], F32, name="tmp") for _ in range(2)]
        trp = TrPool(cx, st)
        pmm = [cx.ps(st, [128, 512], F32, name="pmm") for _ in range(4)]
        k = 0
        for tt in range(S // 128):
            rows = slice(tt * 128, (tt + 1) * 128)
            mgb = mg[tt % 2]
            for m in range(4):
                k += 1
                y0 = yt[k % 3]
                cx.dma('sp', y0[:, :], ysrcs[m][0][rows, :], writes=[y0])
                for extra in ysrcs[m][1:]:
                    k += 1
                    y1 = yt[k % 3]
                    cx.dma('sp', y1[:, :], extra[rows, :], writes=[y1])
                    cx.op('pool', [y0, y1], [y0], lambda e: e.tensor_tensor(y0[:, :], y0[:, :], y1[:, :], ALU.add))
                ybb = yb[m % 2]
                cx.op('pool', [y0], [ybb], lambda e: e.tensor_copy(ybb[:, :], y0[:, :]))
                yTb = yT[m % 2]
                trp.transpose_cols(ybb, lambda j: ybb[:, j * 128:(j + 1) * 128], 4, yTb,
                                   lambda j0, cnt: yTb[:, j0:j0 + cnt, :])
                sgb = sg[m % 2]
                cx.dma('act', sgb[:, :], P_d[rows, OFF_GATE + m * D_MODEL:OFF_GATE + (m + 1) * D_MODEL], writes=[sgb])
                for nc_ in range(4):
                    cs = slice(nc_ * 512, (nc_ + 1) * 512)
                    pm = pmm[(m * 4 + nc_) % 4]
                    for kc in range(4):
                        mm(cx, pm, pm[:, :], yTb[:, kc, :], wbr[:, m * 4 + kc, cs], [yTb, wbr], kc == 0, kc == 3)
                    if m == 0:
                        cx.op('dve', [pm, sgb], [mgb],
                              lambda e: e.tensor_tensor(mgb[:, cs], pm[:, :], sgb[:, cs], ALU.mult))
                    else:
                        tb = tmp[nc_ % 2]
                        cx.op('dve', [pm, sgb], [tb],
                              lambda e: e.tensor_tensor(tb[:, :], pm[:, :], sgb[:, cs], ALU.mult))
                        cx.op('pool', [tb, mgb], [mgb],
                              lambda e: e.tensor_tensor(mgb[:, cs], mgb[:, cs], tb[:, :], ALU.add))
            cx.dma('pool', M_d[rows, :], mgb[:, :], reads=[mgb])
    cx.barrier()


def phase_out(cx, S, M_d, wout_bf, Z_d):
    with contextlib.ExitStack() as st:
        wo = cx.sb(st, [128, 16, D_MODEL], BF16, name="wo")
        wv = wout_bf.rearrange("(kc p) n -> p kc n", p=128)
        for q in range(4):
            cx.dma('sp', wo[:, q * 4:(q + 1) * 4, :], wv[:, q * 4:(q + 1) * 4, :], writes=[wo])
        mt = [cx.sb(st, [128, D_MODEL], F32, name="mt") for _ in range(2)]
        mb = [cx.sb(st, [128, D_MODEL], BF16, name="mb") for _ in range(2)]
        mT = [cx.sb(st, [128, 16, 128], BF16, name="mT") for _ in range(2)]
        ob = [cx.sb(st, [128, 512], F32, name="ob") for _ in range(4)]
        trp = TrPool(cx, st)
        pmm = [cx.ps(st, [128, 512], F32, name="pmm") for _ in range(4)]
        k = 0
        for tt in range(S // 128):
            rows = slice(tt * 128, (tt + 1) * 128)
            mtb, mbb, mTb = mt[tt % 2], mb[tt % 2], mT[tt % 2]
            cx.dma('sp', mtb[:, :], M_d[rows, :], writes=[mtb])
            cx.op('pool', [mtb], [mbb], lambda e: e.tensor_copy(mbb[:, :], mtb[:, :]))
            trp.transpose_cols(mbb, lambda j: mbb[:, j * 128:(j + 1) * 128], 16, mTb,
                               lambda j0, cnt: mTb[:, j0:j0 + cnt, :])
            for nc_ in range(4):
                cs = slice(nc_ * 512, (nc_ + 1) * 512)
                pm = pmm[k % 4]
                obb = ob[k % 4]
                k += 1
                for kc in range(16):
                    mm(cx, pm, pm[:, :], mTb[:, kc, :], wo[:, kc, cs], [mTb, wo], kc == 0, kc == 15)
                evac(cx, 'act' if k % 2 else 'dve', pm, pm[:, :], obb, obb[:, :])
                cx.dma('pool', Z_d[rows, cs], obb[:, :], reads=[obb])
    cx.barrier()


def phase_normres(cx, S, x_d, Z_d, g_row, out_d):
    with contextlib.ExitStack() as st:
        gB = cx.sb(st, [128, D_MODEL], F32, name="gB")
        load_bcast_row(cx, 'sp', gB, g_row, D_MODEL)
        zt = [cx.sb(st, [128, D_MODEL], F32, name="zt") for _ in range(2)]
        xt = [cx.sb(st, [128, D_MODEL], F32, name="xt") for _ in range(2)]
        ot = [cx.sb(st, [128, D_MODEL], F32, name="ot") for _ in range(2)]
        junk = cx.sb(st, [128, D_MODEL], BF16, name="junk")
        ss = [cx.sb(st, [128, 1], F32, name="ss") for _ in range(2)]
        for tt in range(S // 128):
            rows = slice(tt * 128, (tt + 1) * 128)
            z, x, o, s_ = zt[tt % 2], xt[tt % 2], ot[tt % 2], ss[tt % 2]
            cx.dma('sp', z[:, :], Z_d[rows, :], writes=[z])
            cx.dma('act', x[:, :], x_d[rows, :], writes=[x])
            rms_rstd(cx, z, z[:, :], D_MODEL, s_, junk)
            cx.op('dve', [z, s_, gB], [o],
                  lambda e: e.scalar_tensor_tensor(out=o[:, :], in0=z[:, :], scalar=s_[:, 0:1], in1=gB[:, :],
                                                   op0=ALU.mult, op1=ALU.mult))
            cx.op('pool', [o, x], [o], lambda e: e.tensor_tensor(o[:, :], o[:, :], x[:, :], ALU.add))
            cx.dma('pool', out_d[rows, :], o[:, :], reads=[o])
    cx.barrier()


def phase_ffn(cx, S, x_d, g_row, wup_bf, wdn_bf, Z_d):
    G = 512 if S >= 512 else S
    NT = G // 128
    KC = D_MODEL // 128
    FC = D_FF // 128
    UW = 256
    wuv = wup_bf.rearrange("(kc p) f -> p kc f", p=128)
    wdv = wdn_bf.rearrange("(fc p) n -> p fc n", p=128)
    with contextlib.ExitStack() as st:
        gB = cx.sb(st, [128, D_MODEL], F32, name="gB")
        load_bcast_row(cx, 'sp', gB, g_row, D_MODEL)
        xt = [cx.sb(st, [128, D_MODEL], F32, name="xt") for _ in range(2)]
        junk = cx.sb(st, [128, D_MODEL], BF16, name="junk")
        ss = [cx.sb(st, [128, 1], F32, name="ss") for _ in range(2)]
        hb = [cx.sb(st, [128, D_MODEL], BF16, name="hb") for _ in range(2)]
        hT = cx.sb(st, [128, KC, G], BF16, name="hT")
        aT = cx.sb(st, [128, FC, G], BF16, name="aT")
        wu = [cx.sb(st, [128, KC, UW], BF16, name="wu") for _ in range(2)]
        wd = [cx.sb(st, [128, 8, 512], BF16, name="wd") for _ in range(2)]
        rl = [cx.sb(st, [128, G], F32, name="rl") for _ in range(2)]
        ob = [cx.sb(st, [128, 512], F32, name="ob") for _ in range(4)]
        trp = TrPool(cx, st, n=1)
        pup = [cx.ps(st, [128, G], F32, name="pup") for _ in range(2)]
        pdn = [cx.ps(st, [128, 512], F32, name="pdn") for _ in range(NT)]
        nu = 0
        nd = 0
        no = 0
        for gi in range(S // G):
            for tl in range(NT):
                tt = gi * NT + tl
                x, s_, h = xt[tt % 2], ss[tt % 2], hb[tt % 2]
                cx.dma('sp', x[:, :], x_d[tt * 128:(tt + 1) * 128, :], writes=[x])
                rms_rstd(cx, x, x[:, :], D_MODEL, s_, junk)
                cx.op('dve', [x, s_, gB], [h],
                      lambda e: e.scalar_tensor_tensor(out=h[:, :], in0=x[:, :], scalar=s_[:, 0:1], in1=gB[:, :],
                                                       op0=ALU.mult, op1=ALU.mult))
                trp.transpose_cols(h, lambda j: h[:, j * 128:(j + 1) * 128], KC, hT,
                                   lambda j0, cnt: hT[:, j0:j0 + cnt, tl * 128:(tl + 1) * 128])
            for uc in range(D_FF // UW):
                wub = wu[nu % 2]
                nu += 1
                cx.dma('sp', wub[:, :, :], wuv[:, :, uc * UW:(uc + 1) * UW], writes=[wub])
                for j in range(UW // 128):
                    fc = uc * (UW // 128) + j
                    pu = pup[fc % 2]
                    r = rl[fc % 2]
                    for kc in range(KC):
                        mm(cx, pu, pu[:, :], wub[:, kc, j * 128:(j + 1) * 128], hT[:, kc, :], [wub, hT],
                           kc == 0, kc == KC - 1)
                    cx.op('act', [pu], [r], lambda e: e.activation(out=r[:, :], in_=pu[:, :], func=AF.Relu))
                    eng = 'dve' if fc % 2 == 0 else 'pool'
                    cx.op(eng, [r], [aT], lambda e: e.tensor_tensor(aT[:, fc, :], r[:, :], r[:, :], ALU.mult))
            for nc_ in range(4):
                cs = slice(nc_ * 512, (nc_ + 1) * 512)
                for fg in range(FC // 8):
                    wdb = wd[nd % 2]
                    nd += 1
                    cx.dma('act', wdb[:, :, :], wdv[:, fg * 8:(fg + 1) * 8, cs], writes=[wdb])
                    for f8 in range(8):
                        fc = fg * 8 + f8
                        for tl in range(NT):
                            mm(cx, pdn[tl], pdn[tl][:, :], aT[:, fc, tl * 128:(tl + 1) * 128], wdb[:, f8, :],
                               [aT, wdb], fc == 0, fc == FC - 1)
                for tl in range(NT):
                    tt = gi * NT + tl
                    o = ob[no % 4]
                    no += 1
                    evac(cx, 'act' if no % 2 else 'dve', pdn[tl], pdn[tl][:, :], o, o[:, :])
                    cx.dma('pool', Z_d[tt * 128:(tt + 1) * 128, cs], o[:, :], reads=[o])
    cx.barrier()


def rope_tm(cx, src, x1, x2, c, s, dst, o1, o2, tmps, scale=None):
    (ta, tap), (tb, tbp) = tmps
    cx.op('dve', [src] + c[:1] + [], [ta], lambda e: e.tensor_tensor(tap, x1, c[1], ALU.mult))
    cx.op('pool', [src] + s[:1], [tb], lambda e: e.tensor_tensor(tbp, x2, s[1], ALU.mult))
    cx.op('dve', [ta, tb], [dst], lambda e: e.tensor_tensor(o1, tap, tbp, ALU.subtract))
    cx.op('pool', [src] + c[:1], [ta], lambda e: e.tensor_tensor(tap, x2, c[1], ALU.mult))
    cx.op('dve', [src] + s[:1], [tb], lambda e: e.tensor_tensor(tbp, x1, s[1], ALU.mult))
    cx.op('pool', [ta, tb], [dst], lambda e: e.tensor_tensor(o2, tap, tbp, ALU.add))
    if scale is not None:
        cx.op('pool', [dst], [dst], lambda e: e.tensor_scalar(o1, o1, scale, None, ALU.mult))
        cx.op('pool', [dst], [dst], lambda e: e.tensor_scalar(o2, o2, scale, None, ALU.mult))


def bcast_scalar_max(cx, st, trp_f, run_buf, out_col):
    pt = trp_f.bufs[0]
    transp(cx, pt, pt[0:1, 0, :], run_buf[:, 0:1], cx.identf[:, :], [run_buf, cx.identf])
    row = cx.sb(st, [1, 128], F32, name="mxrow")
    one = cx.sb(st, [1, 1], F32, name="mxone")
    evac(cx, 'dve', pt, pt[0:1, 0, :], row, row[:, :])
    cx.op('dve', [row], [one], lambda e: e.tensor_reduce(out=one[:, :], in_=row[:, :], axis=AX.X, op=ALU.max))
    mm(cx, pt, pt[:, 1, 0:1], cx.ones_f[0:1, :], one[0:1, 0:1], [cx.ones_f, one], True, True)
    evac(cx, 'dve', pt, pt[:, 1, 0:1], out_col, out_col[:, 0:1])


class AttnRes:
    def __init__(self, cx, st, W):
        self.sT = [cx.ps(st, [128, 512], F32, name="sT") for _ in range(2)]
        self.acc = [cx.ps(st, [128, 2, 256], F32, name="acc") for _ in range(4)]
        self.pT = [cx.sb(st, [128, 512], BF16, name="pT") for _ in range(3)]
        self.n = 0
        self.nq = 0


def attn_core(cx, res, S, qchunks, kchunks, vaug_fn, W, blocks_fn, epilogue, mode='softmax', qbs=None):
    QW = min(512, S)
    NJ = QW // 128
    for QB in (range(S // QW) if qbs is None else qbs):
        q0 = QB * QW
        blocks = blocks_fn(QB)
        accs = [res.acc[j] for j in range(NJ)]
        res.nq += 1
        for bi, b in enumerate(blocks):
            sT = res.sT[res.n % 2]
            pT = res.pT[res.n % 3]
            res.n += 1
            nk, k0 = b['nk'], b['k0']
            nmm = len(qchunks) + (1 if b.get('extra') else 0)
            i = 0
            for (qb_, qap), (kb_, kap) in zip(qchunks, kchunks):
                ka = kap(k0, nk) if callable(kap) else kap[:, k0:k0 + nk]
                mm(cx, sT, sT[:nk, :QW], ka, qap[:, q0:q0 + QW], [qb_, kb_], i == 0, i == nmm - 1)
                i += 1
            if b.get('extra'):
                lb, lap, rb, rap = b['extra']
                mm(cx, sT, sT[:nk, :QW], lap, rap, list(lb) + list(rb), False, True)
            if mode == 'softmax':
                cx.op('act', [sT], [pT], lambda e: e.activation(out=pT[:nk, :QW], in_=sT[:nk, :QW], func=AF.Exp))
                if b.get('mask'):
                    mb, map_ = b['mask']
                    cx.op('pool' if nk == 128 else 'dve', [pT, mb], [pT],
                          lambda e: e.tensor_tensor(pT[:nk, :QW], pT[:nk, :QW], map_, ALU.mult))
            else:
                c, gb, gap = b['decay']
                cx.op('dve', [sT, gb], [pT],
                      lambda e: e.scalar_tensor_tensor(out=pT[:nk, :QW], in0=sT[:nk, :QW], scalar=float(c), in1=gap,
                                                       op0=ALU.mult, op1=ALU.mult))
            vb, vap = vaug_fn(b['kb'], nk)
            for j in range(NJ):
                a = accs[j]
                mm(cx, a, a[:, 0, :W], pT[:nk, j * 128:(j + 1) * 128], vap, [pT, vb],
                   bi == 0, bi == len(blocks) - 1)
        for j in range(NJ):
            epilogue(QB, j, accs[j], accs[j][:, 0, :W])


def causal_blocks(QB, QW, cmask):
    out = []
    nd = QW // 128
    for kb in range(nd * (QB + 1)):
        i = kb - nd * QB
        out.append(dict(kb=kb, k0=kb * 128, nk=128, mask=(cmask, cmask[:, i, :QW]) if i >= 0 else None))
    return out


MLA_SCALE = 192 ** -0.5


def phase_mla(cx, S, P_d, gq_row, gkv_row, wuq_bf, wukv_bf, cos_d, sin_d, cmask_d, Y_d, scr):
    NT = S // 128
    G = min(512, S)
    NG = G // 128
    with contextlib.ExitStack() as st:
        with contextlib.ExitStack() as s1:
            gq = cx.sb(s1, [128, 384], F32, name="gq")
            gkv = cx.sb(s1, [128, 128], F32, name="gkv")
            load_bcast_row(cx, 'sp', gq, gq_row, 384)
            load_bcast_row(cx, 'sp', gkv, gkv_row, 128)
            wuq = cx.sb(s1, [128, 3, 768], BF16, name="wuq")
            cx.dma('sp', wuq[:, :, :], wuq_bf.rearrange("(kc p) n -> p kc n", p=128), writes=[wuq])
            wukv = cx.sb(s1, [128, 1024], BF16, name="wukv")
            cx.dma('sp', wukv[:, :], wukv_bf[:, :], writes=[wukv])
            pm = [cx.sb(s1, [128, 576], F32, name="pm") for _ in range(2)]
            cs_t = [cx.sb(s1, [128, 64], F32, name="cs") for _ in range(2)]
            junk = cx.sb(s1, [128, 768], BF16, name="junk")
            ss = [cx.sb(s1, [128, 1], F32, name="ss") for _ in range(2)]
            nb = [cx.sb(s1, [128, 384], BF16, name="nb") for _ in range(2)]
            nT = [cx.sb(s1, [128, 3, 128], BF16, name="nT") for _ in range(2)]
            qf = [cx.sb(s1, [128, 4, 256], F32, name="qf") for _ in range(2)]
            qs = [cx.sb(s1, [128, 4, 193], BF16, name="qs") for _ in range(2)]
            qsf = [cx.sb(s1, [128, 4, 64], F32, name="qsf") for _ in range(2)]
            t1 = cx.sb(s1, [128, 4, 32], F32, name="t1")
            t2 = cx.sb(s1, [128, 4, 32], F32, name="t2")
            kr = [cx.sb(s1, [128, 65], BF16, name="kr") for _ in range(2)]
            krf = [cx.sb(s1, [128, 64], F32, name="krf") for _ in range(2)]
            kb16 = [cx.sb(s1, [128, 4, 128], BF16, name="kb16") for _ in range(2)]
            va = [cx.sb(s1, [128, 4, 129], BF16, name="va") for _ in range(2)]
            sq = cx.sb(s1, [128, 4, 256], F32, name="sq")
            n4 = [cx.sb(s1, [128, 4], F32, name="n4") for _ in range(2)]
            n1 = [cx.sb(s1, [128, 1], F32, name="n1") for _ in range(2)]
            kmx = cx.sb(s1, [128, 1], F32, name="kmx")
            kmax = cx.sb(s1, [128, 1], F32, name="kmax")
            gA = cx.sb(s1, [128, 4, G], BF16, name="gA")
            gB_ = cx.sb(s1, [65, 4, G], BF16, name="gB_")
            trp = TrPool(cx, s1)
            trf = TrPool(cx, s1, n=1, dtype=F32)
            pq = [cx.ps(s1, [128, 512], F32, name="pq") for _ in range(2)]
            pq2 = [cx.ps(s1, [128, 512], F32, name="pq2") for _ in range(2)]
            cx.op('pool', [], [kmx], lambda e: e.memset(kmx[:, :], 0.0))

            def load_norm_T(tt, c0, n, gbuf, k):
                p = pm[k % 2]
                cx.dma('sp', p[:, :], P_d[tt * 128:(tt + 1) * 128, OFF_MLA:OFF_MLA + 576], writes=[p])
                s_ = ss[k % 2]
                rms_rstd(cx, p, p[:, c0:c0 + n], n, s_, junk)
                nbb = nb[k % 2]
                cx.op('dve', [p, s_, gbuf], [nbb],
                      lambda e: e.scalar_tensor_tensor(out=nbb[:, :n], in0=p[:, c0:c0 + n], scalar=s_[:, 0:1],
                                                       in1=gbuf[:, :n], op0=ALU.mult, op1=ALU.mult))
                nTb = nT[k % 2]
                trp.transpose_cols(nbb, lambda j: nbb[:, j * 128:(j + 1) * 128], n // 128, nTb,
                                   lambda j0, cnt: nTb[:, j0:j0 + cnt, :])
                return p, nTb

            for tt in range(NT):
                tl = tt % NG
                p, nTb = load_norm_T(tt, 384, 128, gkv, tt)
                c_t = cs_t[tt % 2]
                cx.dma('act', c_t[:, 0:32], cos_d[tt * 128:(tt + 1) * 128, :], writes=[c_t])
                cx.dma('act', c_t[:, 32:64], sin_d[tt * 128:(tt + 1) * 128, :], writes=[c_t])
                pa, pb = pq[tt % 2], pq2[tt % 2]
                mm(cx, pa, pa[:, :], nTb[:, 0, :], wukv[:, 0:512], [nTb, wukv], True, True)
                mm(cx, pb, pb[:, :], nTb[:, 0, :], wukv[:, 512:1024], [nTb, wukv], True, True)
                q = qf[tt % 2]
                evac(cx, 'act', pa, pa[:, :].rearrange("p (h c) -> p h c", h=2), q, q[:, 0:2, :])
                evac(cx, 'dve', pb, pb[:, :].rearrange("p (h c) -> p h c", h=2), q, q[:, 2:4, :])
                k16, vab, krb, krfb = kb16[tt % 2], va[tt % 2], kr[tt % 2], krf[tt % 2]
                cx.op('pool', [q], [k16], lambda e: e.tensor_copy(k16[:, :, :], q[:, :, 0:128]))
                cx.op('pool', [q], [vab], lambda e: e.tensor_copy(vab[:, :, 0:128], q[:, :, 128:256]))
                cx.op('pool', [], [vab], lambda e: e.memset(vab[:, :, 128:129], 1.0))
                rope_tm(cx, p, p[:, 512:544], p[:, 544:576], [c_t, c_t[:, 0:32]], [c_t, c_t[:, 32:64]],
                        krfb, krfb[:, 0:32], krfb[:, 32:64], [(t1, t1[:, 0, :]), (t2, t2[:, 0, :])])
                cx.op('pool', [krfb], [krb], lambda e: e.tensor_copy(krb[:, 0:64], krfb[:, :]))
                cx.op('pool', [], [krb], lambda e: e.memset(krb[:, 64:65], 1.0))
                cx.op('dve', [q], [sq], lambda e: e.tensor_tensor(sq[:, :, 0:128], q[:, :, 0:128], q[:, :, 0:128], ALU.mult))
                n4b, n1b = n4[tt % 2], n1[tt % 2]
                cx.op('dve', [sq], [n4b], lambda e: e.tensor_reduce(out=n4b[:, :], in_=sq[:, :, 0:128], axis=AX.X, op=ALU.add))
                cx.op('dve', [n4b], [n1b], lambda e: e.tensor_reduce(out=n1b[:, :], in_=n4b[:, :], axis=AX.X, op=ALU.max))
                cx.op('act', [krfb], [junk, n4b],
                      lambda e: e.activation(out=junk[:, :64], in_=krfb[:, :], func=AF.Square, accum_out=n4b[:, 0:1]))
                cx.op('dve', [n4b, n1b], [n1b], lambda e: e.tensor_tensor(n1b[:, :], n1b[:, :], n4b[:, 0:1], ALU.add))
                cx.op('dve', [n1b, kmx], [kmx], lambda e: e.tensor_tensor(kmx[:, :], kmx[:, :], n1b[:, :], ALU.max))
                trp.transpose_cols(k16, lambda j: k16[:, j, :], 4, gA,
                                   lambda j0, cnt: gA[:, j0:j0 + cnt, tl * 128:(tl + 1) * 128])
                trp.transpose_cols(krb, lambda j: krb[:, :], 1, gB_,
                                   lambda j0, cnt: gB_[:65, 0:1, tl * 128:(tl + 1) * 128], blkw=65)
                cx.dma('pool', scr['va'][tt * 128:(tt + 1) * 128, :, :], vab[:, :, :], reads=[vab])
                if tl == NG - 1:
                    g0 = (tt // NG) * G
                    cx.dma('pool', scr['knT'][:, :, g0:g0 + G].rearrange("h d s -> d h s"), gA[:, :, :], reads=[gA])
                    cx.dma('pool', scr['krT'][:, g0:g0 + G], gB_[:65, 0, :], reads=[gB_])
            bcast_scalar_max(cx, s1, trf, kmx, kmax)
            cx.op('act', [kmax], [kmax], lambda e: e.activation(out=kmax[:, :], in_=kmax[:, :], func=AF.Sqrt))
            for tt in range(NT):
                tl = tt % NG
                p, nTb = load_norm_T(tt, 0, 384, gq, tt)
                c_t = cs_t[tt % 2]
                cx.dma('act', c_t[:, 0:32], cos_d[tt * 128:(tt + 1) * 128, :], writes=[c_t])
                cx.dma('act', c_t[:, 32:64], sin_d[tt * 128:(tt + 1) * 128, :], writes=[c_t])
                pa, pb = pq[tt % 2], pq2[tt % 2]
                for kc in range(3):
                    mm(cx, pa, pa[:, :], nTb[:, kc, :], wuq[:, kc, 0:512], [nTb, wuq], kc == 0, kc == 2)
                for kc in range(3):
                    mm(cx, pb, pb[:, :256], nTb[:, kc, :], wuq[:, kc, 512:768], [nTb, wuq], kc == 0, kc == 2)
                q = qf[tt % 2]
                qv = q[:, :, :].rearrange("p h c -> p (h c)")
                evac(cx, 'act', pa, pa[:, :], q, qv[:, 0:512])
                evac(cx, 'dve', pb, pb[:, :256], q, qv[:, 512:768])
                qh = qv[:, 0:768].rearrange("p (h c) -> p h c", h=4)
                cx.op('dve', [q], [sq], lambda e: e.tensor_tensor(sq[:, :, 0:192], qh, qh, ALU.mult))
                n4b = n4[tt % 2]
                cx.op('dve', [sq], [n4b], lambda e: e.tensor_reduce(out=n4b[:, :], in_=sq[:, :, 0:192], axis=AX.X, op=ALU.add))
                cx.op('act', [n4b], [n4b], lambda e: e.activation(out=n4b[:, :], in_=n4b[:, :], func=AF.Sqrt))
                cx.op('dve', [n4b, kmax], [n4b],
                      lambda e: e.tensor_scalar(n4b[:, :], n4b[:, :], kmax[:, 0:1], -MLA_SCALE, ALU.mult, ALU.mult))
                qsb, qsfb = qs[tt % 2], qsf[tt % 2]
                cb = c_t[:, 0:32].unsqueeze(1).broadcast_to([128, 4, 32])
                sb_ = c_t[:, 32:64].unsqueeze(1).broadcast_to([128, 4, 32])
                rope_tm(cx, q, qh[:, :, 128:160], qh[:, :, 160:192], [c_t, cb], [c_t, sb_],
                        qsfb, qsfb[:, :, 0:32], qsfb[:, :, 32:64], [(t1, t1[:, :, :]), (t2, t2[:, :, :])])
                cx.op('act', [q], [qsb], lambda e: e.activation(out=qsb[:, :, 0:128], in_=qh[:, :, 0:128], func=AF.Copy, scale=MLA_SCALE))
                cx.op('act', [qsfb], [qsb], lambda e: e.activation(out=qsb[:, :, 128:192], in_=qsfb[:, :, :], func=AF.Copy, scale=MLA_SCALE))
                cx.op('pool', [n4b], [qsb], lambda e: e.tensor_copy(qsb[:, :, 192:193], n4b[:, :].unsqueeze(2)))
                trp.transpose_cols(qsb, lambda j: qsb[:, j, 0:128], 4, gA,
                                   lambda j0, cnt: gA[:, j0:j0 + cnt, tl * 128:(tl + 1) * 128])
                trp.transpose_cols(qsb, lambda j: qsb[:, j, 128:193], 4, gB_,
                                   lambda j0, cnt: gB_[:65, j0:j0 + cnt, tl * 128:(tl + 1) * 128], blkw=65)
                if tl == NG - 1:
                    g0 = (tt // NG) * G
                    cx.dma('pool', scr['qnT'][:, :, g0:g0 + G].rearrange("h d s -> d h s"), gA[:, :, :], reads=[gA])
                    cx.dma('pool', scr['qrT'][:, :, g0:g0 + G].rearrange("h d s -> d h s"), gB_[:65, :, :], reads=[gB_])
        cx.barrier()
        res = AttnRes(cx, st, 129)
        cmask = cx.sb(st, [128, 4, 512], BF16, name="cmask")
        cx.dma('sp', cmask[:, :, :], cmask_d.rearrange("i k q -> k i q"), writes=[cmask])
        krT = cx.sb(st, [65, S], BF16, name="krT")
        cx.dma('sp', krT[:, :], scr['krT'][:, :], writes=[krT])
        qn = [cx.sb(st, [128, S], BF16, name="qn") for _ in range(2)]
        qr = [cx.sb(st, [65, S], BF16, name="qr") for _ in range(2)]
        kn = [cx.sb(st, [128, S], BF16, name="kn") for _ in range(2)]
        vv = [cx.sb(st, [128, NT, 129], BF16, name="vv") for _ in range(2)]
        rc = [cx.sb(st, [128, 1], F32, name="rc") for _ in range(2)]
        ot = [cx.sb(st, [128, 128], F32, name="ot") for _ in range(2)]
        cnt = [0]
        QW = min(512, S)
        for h in range(4):
            a, b, c, v = qn[h % 2], qr[h % 2], kn[h % 2], vv[h % 2]
            cx.dma('sp', a[:, :], scr['qnT'][h], writes=[a])
            cx.dma('sp', b[:, :], scr['qrT'][h], writes=[b])
            cx.dma('sp', c[:, :], scr['knT'][h], writes=[c])
            cx.dma('sp', v[:, :, :], scr['va'][:, h, :].rearrange("(t p) c -> p t c", p=128), writes=[v])

            def epi(QB, j, accb, acc_ap, h=h):
                k = cnt[0]
                cnt[0] += 1
                r, o = rc[k % 2], ot[k % 2]
                cx.op('dve', [accb], [r], lambda e: e.tensor_scalar(r[:, :], acc_ap[:, 128:129], 1e-30, None, ALU.add))
                cx.op('dve', [r], [r], lambda e: e.reciprocal(r[:, :], r[:, :]))
                cx.op('act', [accb, r], [o], lambda e: e.activation(out=o[:, :], in_=acc_ap[:, 0:128], func=AF.Copy, scale=r[:, 0:1]))
                t0 = QB * QW + j * 128
                cx.dma('pool', Y_d[t0:t0 + 128, h * 128:(h + 1) * 128], o[:, :], reads=[o])

            attn_core(cx, res, S, [(a, a[:, :]), (b, b[:65, :])], [(c, c[:, :]), (krT, krT[:65, :])],
                      lambda kb, nk, v=v: (v, v[:nk, kb, :]), 129,
                      lambda QB: causal_blocks(QB, QW, cmask), epi)
    cx.barrier()


def head_norm_tm(cx, src_buf, src_ap, n, eps, cbuf, c_ap, s1, s2, junk):
    cx.op('dve', [src_buf], [s1], lambda e: e.tensor_reduce(out=s1[:, 0:1], in_=src_ap, axis=AX.X, op=ALU.add))
    cx.op('dve', [s1], [s1], lambda e: e.tensor_scalar(s1[:, 0:1], s1[:, 0:1], 1.0 / n, None, ALU.mult))
    cx.op('dve', [src_buf, s1], [cbuf], lambda e: e.tensor_scalar(c_ap, src_ap, s1[:, 0:1], None, ALU.subtract))
    rms_rstd(cx, cbuf, c_ap, n, s2, junk, eps=eps)


def phase_ret(cx, S, P_d, cos_d, sin_d, gdec_d, Y_d, scr):
    NT = S // 128
    G = min(512, S)
    NG = G // 128
    QW = min(512, S)
    with contextlib.ExitStack() as st:
        with contextlib.ExitStack() as s1:
            pr = [cx.sb(s1, [128, 1024], F32, name="pr") for _ in range(2)]
            cs_t = [cx.sb(s1, [128, 64], F32, name="cs") for _ in range(2)]
            ro = [cx.sb(s1, [128, 8, 64], F32, name="ro") for _ in range(2)]
            rb = [cx.sb(s1, [128, 8, 64], BF16, name="rb") for _ in range(2)]
            vb = [cx.sb(s1, [128, 512], BF16, name="vb") for _ in range(2)]
            t1 = cx.sb(s1, [128, 8, 32], F32, name="t1")
            t2 = cx.sb(s1, [128, 8, 32], F32, name="t2")
            gQ = cx.sb(s1, [64, 8, G], BF16, name="gQ")
            trp = TrPool(cx, s1)
            for tt in range(NT):
                tl = tt % NG
                rows = slice(tt * 128, (tt + 1) * 128)
                p, c_t, r, rbb, v = pr[tt % 2], cs_t[tt % 2], ro[tt % 2], rb[tt % 2], vb[tt % 2]
                cx.dma('sp', p[:, :], P_d[rows, OFF_RET:OFF_RET + 1024], writes=[p])
                cx.dma('act', c_t[:, 0:32], cos_d[rows, :], writes=[c_t])
                cx.dma('act', c_t[:, 32:64], sin_d[rows, :], writes=[c_t])
                qk = p[:, 0:512].rearrange("p (h c) -> p h c", h=8)
                cb = c_t[:, 0:32].unsqueeze(1).broadcast_to([128, 8, 32])
                sb_ = c_t[:, 32:64].unsqueeze(1).broadcast_to([128, 8, 32])
                rope_tm(cx, p, qk[:, :, 0:32], qk[:, :, 32:64], [c_t, cb], [c_t, sb_],
                        r, r[:, :, 0:32], r[:, :, 32:64], [(t1, t1[:, :, :]), (t2, t2[:, :, :])])
                cx.op('act', [r], [rbb], lambda e: e.copy(rbb[:, 0:4, :], r[:, 0:4, :]))
                cx.op('act', [r], [rbb], lambda e: e.activation(out=rbb[:, 4:8, :], in_=r[:, 4:8, :], func=AF.Copy, scale=0.125))
                cx.op('pool', [p], [v], lambda e: e.tensor_copy(v[:, :], p[:, 512:1024]))
                trp.transpose_cols(rbb, lambda j: rbb[:, j, :], 8, gQ,
                                   lambda j0, cnt: gQ[:64, j0:j0 + cnt, tl * 128:(tl + 1) * 128], blkw=64)
                cx.dma('pool', scr['rv'][rows, :, :].rearrange("s h c -> s (h c)"), v[:, :], reads=[v])
                if tl == NG - 1:
                    g0 = (tt // NG) * G
                    cx.dma('pool', scr['rqT'][:, :, g0:g0 + G].rearrange("h d s -> d h s"), gQ[:64, 0:4, :], reads=[gQ])
                    cx.dma('pool', scr['rkT'][:, :, g0:g0 + G].rearrange("h d s -> d h s"), gQ[:64, 4:8, :], reads=[gQ])
        cx.barrier()
        res = AttnRes(cx, st, 128)
        gd = [cx.sb(st, [128, 5, 512], F32, name="gd") for _ in range(2)]
        qT = [cx.sb(st, [64, S], BF16, name="qT") for _ in range(2)]
        kT = [cx.sb(st, [64, S], BF16, name="kT") for _ in range(2)]
        vv = [cx.sb(st, [128, NT, 128], BF16, name="vv") for _ in range(2)]
        gt = [cx.sb(st, [128, 128], F32, name="gt") for _ in range(2)]
        cb_ = [cx.sb(st, [128, 128], F32, name="cb") for _ in range(2)]
        ot = [cx.sb(st, [128, 128], F32, name="ot") for _ in range(2)]
        sA = [cx.sb(st, [128, 1], F32, name="sA") for _ in range(2)]
        sB = [cx.sb(st, [128, 1], F32, name="sB") for _ in range(2)]
        junk = cx.sb(st, [128, 128], BF16, name="junk")
        cnt = [0]
        for h in range(4):
            gamma = 1.0 - 2.0 ** (-5 - h)
            a, c, v, g = qT[h % 2], kT[h % 2], vv[h % 2], gd[h % 2]
            cx.dma('sp', a[:, :], scr['rqT'][h], writes=[a])
            cx.dma('sp', c[:, :], scr['rkT'][h], writes=[c])
            cx.dma('sp', v[:, :, :], scr['rv'][:, h, :].rearrange("(t p) c -> p t c", p=128), writes=[v])
            cx.dma('sp', g[:, :, :], gdec_d[h].rearrange("i k q -> k i q"), writes=[g])
            if 'dbg' in scr and h == 0:
                cx.dma('sp', scr['dbg'][0], a[:, :], reads=[a])
                cx.dma('sp', scr['dbg'][1], c[:, :], reads=[c])

            def blocks(QB, g=g, gamma=gamma):
                out = []
                nd = QW // 128
                for kb in range(nd * (QB + 1)):
                    i = kb - nd * QB
                    if i >= 0:
                        out.append(dict(kb=kb, k0=kb * 128, nk=128, decay=(1.0, g, g[:, 1 + i, :QW])))
                    else:
                        cc = gamma ** (QB * QW - kb * 128)
                        if cc < 1e-30:
                            cc = 0.0
                        out.append(dict(kb=kb, k0=kb * 128, nk=128, decay=(cc, g, g[:, 0, :QW])))
                return out

            def epi(QB, j, accb, acc_ap, h=h):
                k = cnt[0]
                cnt[0] += 1
                t0 = QB * QW + j * 128
                gtb, cbb, o, s_a, s_b = gt[k % 2], cb_[k % 2], ot[k % 2], sA[k % 2], sB[k % 2]
                cx.dma('act', gtb[:, :], P_d[t0:t0 + 128, OFF_RET + 1024 + h * 128:OFF_RET + 1024 + (h + 1) * 128], writes=[gtb])
                cx.op('act', [gtb], [gtb], lambda e: e.activation(out=gtb[:, :], in_=gtb[:, :], func=AF.Silu))
                head_norm_tm(cx, accb, acc_ap, 128, NORM_EPS, cbb, cbb[:, :], s_a, s_b, junk)
                cx.op('dve', [cbb, s_b, gtb], [o],
                      lambda e: e.scalar_tensor_tensor(out=o[:, :], in0=cbb[:, :], scalar=s_b[:, 0:1], in1=gtb[:, :],
                                                       op0=ALU.mult, op1=ALU.mult))
                cx.dma('pool', Y_d[t0:t0 + 128, h * 128:(h + 1) * 128], o[:, :], reads=[o])

            attn_core(cx, res, S, [(a, a[:64, :])], [(c, c[:64, :])], lambda kb, nk, v=v: (v, v[:nk, kb, :]), 128,
                      blocks, epi, mode='decay')
    cx.barrier()


def bc8(ap):
    return ap.unsqueeze(2).broadcast_to([128, 8, 64])


def v3(ap):
    return ap.rearrange("p (h c) -> p h c", h=8)


def phase_rwkv(cx, S, l, P_d, W, Wb, C, Y_d, scr):
    NT = S // 128
    RW = scr['rw']
    names6 = ['rr', 'lw', 'k2', 'vv', 'kn', 'aa']
    with contextlib.ExitStack() as st:
        def brow(name, n, src):
            b = cx.sb(st, [128, n], F32, name=name)
            load_bcast_row(cx, 'sp', b, src, n)
            return b
        muB = brow("muB", 1984, W['rwkv_mu'][l])
        w0B = brow("w0B", 512, W['rwkv_w0'][l])
        a0B = brow("a0B", 512, W['rwkv_a0'][l])
        kkB = brow("kkB", 512, W['rwkv_k_k'][l])
        kaB = brow("kaB", 512, W['rwkv_k_a'][l])
        rkB = brow("rkB", 512, W['rwkv_r_k'][l].rearrange("h c -> (h c)"))
        w2 = cx.sb(st, [96, 512], BF16, name="w2")
        a2 = cx.sb(st, [96, 512], BF16, name="a2")
        g2 = cx.sb(st, [128, 2, 512], BF16, name="g2")
        cx.dma('sp', w2[:, :], Wb['rwkv_w2'][l], writes=[w2])
        cx.dma('sp', a2[:, :], Wb['rwkv_a2'][l], writes=[a2])
        cx.dma('sp', g2[:, :, :], Wb['rwkv_g2'][l].rearrange("(kc p) n -> p kc n", p=128), writes=[g2])
        z = [cx.sb(st, [128, 1984], F32, name="z") for _ in range(2)]
        zp = [cx.sb(st, [128, 1984], F32, name="zp") for _ in range(2)]
        lo = [cx.sb(st, [128, 512], BF16, name="lo") for _ in range(2)]
        loT = [cx.sb(st, [128, 4, 128], BF16, name="loT") for _ in range(2)]
        o7 = [cx.sb(st, [128, 7, 512], F32, name="o7") for _ in range(2)]
        s8 = [cx.sb(st, [128, 8], F32, name="s8") for _ in range(2)]
        b8 = [cx.sb(st, [128, 8], F32, name="b8") for _ in range(2)]
        trp = TrPool(cx, st)
        pp = [cx.ps(st, [128, 512], F32, name="pp") for _ in range(3)]
        for tt in range(NT):
            rows = slice(tt * 128, (tt + 1) * 128)
            zb, zpb, lob, loTb, o, s8b, b8b = z[tt % 2], zp[tt % 2], lo[tt % 2], loT[tt % 2], o7[tt % 2], s8[tt % 2], b8[tt % 2]
            cx.dma('sp', zb[:, :], P_d[rows, OFF_RWKV:OFF_RWKV + 1984], writes=[zb])
            if tt == 0:
                cx.op('pool', [], [zpb], lambda e: e.memset(zpb[0:1, :], 0.0))
                cx.dma('act', zpb[1:128, :], P_d[0:127, OFF_RWKV:OFF_RWKV + 1984], writes=[zpb])
            else:
                cx.dma('act', zpb[:, :], P_d[tt * 128 - 1:tt * 128 + 127, OFF_RWKV:OFF_RWKV + 1984], writes=[zpb])
            cx.op('pool', [zpb, zb], [zpb], lambda e: e.tensor_tensor(zpb[:, :], zpb[:, :], zb[:, :], ALU.subtract))
            cx.op('dve', [zpb, muB], [zpb], lambda e: e.tensor_tensor(zpb[:, :], zpb[:, :], muB[:, :], ALU.mult))
            cx.op('pool', [zpb, zb], [zb], lambda e: e.tensor_tensor(zb[:, :], zb[:, :], zpb[:, :], ALU.add))
            r_, k_, v_ = zb[:, 0:512], zb[:, 512:1024], zb[:, 1024:1536]
            cx.op('act', [zb], [lob], lambda e: e.activation(out=lob[:, 0:96], in_=zb[:, 1536:1632], func=AF.Tanh))
            cx.op('act', [zb], [lob], lambda e: e.copy(lob[:, 128:224], zb[:, 1632:1728]))
            cx.op('act', [zb], [lob], lambda e: e.activation(out=lob[:, 256:512], in_=zb[:, 1728:1984], func=AF.Sigmoid))
            trp.transpose_cols(lob, lambda j: lob[:, j * 128:j * 128 + 96], 2, loTb,
                               lambda j0, cnt: loTb[:96, j0:j0 + cnt, :], blkw=96)
            trp.transpose_cols(lob, lambda j: lob[:, 256 + j * 128:384 + j * 128], 2, loTb,
                               lambda j0, cnt: loTb[:, 2 + j0:2 + j0 + cnt, :])
            pu, pa, pg = pp
            mm(cx, pu, pu[:, :], loTb[:96, 0, :], w2[:96, :], [loTb, w2], True, True)
            mm(cx, pa, pa[:, :], loTb[:96, 1, :], a2[:96, :], [loTb, a2], True, True)
            mm(cx, pg, pg[:, :], loTb[:, 2, :], g2[:, 0, :], [loTb, g2], True, False)
            mm(cx, pg, pg[:, :], loTb[:, 3, :], g2[:, 1, :], [loTb, g2], False, True)
            lw_, k2_, kn_, aa_, gg_, t1_, t2_ = [o[:, i, :] for i in range(7)]
            cx.op('dve', [pu, w0B], [o], lambda e: e.tensor_tensor(t1_, pu[:, :], w0B[:, :], ALU.add))
            cx.op('act', [o], [o], lambda e: e.activation(out=t1_, in_=t1_, func=AF.Sigmoid))
            cx.op('pool', [o], [o], lambda e: e.tensor_scalar(lw_, t1_, -0.6065306597126334, None, ALU.mult))
            cx.op('dve', [pa, a0B], [o], lambda e: e.tensor_tensor(t2_, pa[:, :], a0B[:, :], ALU.add))
            cx.op('act', [o], [o], lambda e: e.activation(out=aa_, in_=t2_, func=AF.Sigmoid))
            cx.op('act', [pg], [o], lambda e: e.copy(gg_, pg[:, :]))
            cx.op('dve', [zb, kkB], [o], lambda e: e.tensor_tensor(kn_, k_, kkB[:, :], ALU.mult))
            cx.op('pool', [o], [o], lambda e: e.tensor_tensor(t1_, kn_, kn_, ALU.mult))
            cx.op('dve', [o], [s8b], lambda e: e.tensor_reduce(out=s8b[:, :], in_=v3(t1_), axis=AX.X, op=ALU.add))
            cx.op('dve', [s8b], [s8b], lambda e: e.tensor_scalar(s8b[:, :], s8b[:, :], 1e-24, None, ALU.max))
            cx.op('pool', [s8b, cx.neghalf], [s8b],
                  lambda e: e.tensor_tensor(s8b[:, :], s8b[:, :], cx.neghalf[:, 0:1].to_broadcast([128, 8]), ALU.pow))
            cx.op('dve', [o, s8b], [o], lambda e: e.tensor_tensor(v3(kn_), v3(kn_), bc8(s8b[:, :]), ALU.mult))
            cx.op('dve', [o, kaB], [o],
                  lambda e: e.scalar_tensor_tensor(out=t2_, in0=aa_, scalar=-1.0, in1=kaB[:, :], op0=ALU.add, op1=ALU.mult))
            cx.op('pool', [o], [o], lambda e: e.tensor_scalar(t2_, t2_, 1.0, None, ALU.add))
            cx.op('dve', [o, zb], [o], lambda e: e.tensor_tensor(k2_, k_, t2_, ALU.mult))
            cx.op('pool', [o, zb], [o], lambda e: e.tensor_tensor(t1_, r_, k2_, ALU.mult))
            cx.op('dve', [o, rkB], [o], lambda e: e.tensor_tensor(t1_, t1_, rkB[:, :], ALU.mult))
            cx.op('dve', [o], [b8b], lambda e: e.tensor_reduce(out=b8b[:, :], in_=v3(t1_), axis=AX.X, op=ALU.add))
            cx.dma('pool', RW['rr'][rows, :], r_, reads=[zb])
            cx.dma('pool', RW['vv'][rows, :], v_, reads=[zb])
            cx.dma('pool', RW['lw'][rows, :], lw_, reads=[o])
            cx.dma('pool', RW['k2'][rows, :], k2_, reads=[o])
            cx.dma('pool', RW['kn'][rows, :], kn_, reads=[o])
            cx.dma('pool', RW['aa'][rows, :], aa_, reads=[o])
            cx.dma('pool', RW['gg'][rows, :], gg_, reads=[o])
            cx.dma('pool', RW['bc'][rows, :], b8b[:, :], reads=[b8b])
    cx.barrier()
    with contextlib.ExitStack() as st:
        rwm = cx.sb(st, [128, 384], F32, name="rwm")
        cx.dma('sp', rwm[:, :], C['rwm'][:, :], writes=[rwm])
        mask4 = cx.sb(st, [128, 512], F32, name="mask4")
        cx.op('pool', [rwm], [mask4], lambda e: e.tensor_copy(mask4[:, 0:256], rwm[:, 0:256]))
        cx.op('pool', [rwm], [mask4], lambda e: e.tensor_copy(mask4[:, 256:512], rwm[:, 0:256]))
        gwB = cx.sb(st, [128, 512], F32, name="gwB")
        gbB = cx.sb(st, [128, 512], F32, name="gbB")
        load_bcast_row(cx, 'sp', gwB, W['rwkv_gn_w'][l], 512)
        load_bcast_row(cx, 'sp', gbB, W['rwkv_gn_b'][l], 512)
        IN = [cx.sb(st, [128, 6, 512], F32, name="IN") for _ in range(2)]
        EL = cx.sb(st, [128, 3, 512], F32, name="EL")
        TM = cx.sb(st, [128, 4, 512], F32, name="TM")
        XT = cx.sb(st, [64, 8, 4, 128], F32, name="XT")
        MM_ = cx.sb(st, [128, 8, 512], F32, name="MM")
        XX = [cx.sb(st, [128, 8, 2, 128], F32, name="XX") for _ in range(2)]
        NTb = cx.sb(st, [128, 8, 128], F32, name="NT")
        ST = cx.sb(st, [64, 8, 64], F32, name="ST")
        STs = cx.sb(st, [64, 8, 64], F32, name="STs")
        pc = cx.sb(st, [64, 8], F32, name="pc")
        Yb = cx.sb(st, [128, 8, 64], F32, name="Yb")
        Ub = cx.sb(st, [128, 8, 64], F32, name="Ub")
        Ob = cx.sb(st, [128, 512], F32, name="Ob")
        G3 = [cx.sb(st, [128, 512], F32, name="G3") for _ in range(2)]
        b8 = [cx.sb(st, [128, 8], F32, name="b8") for _ in range(2)]
        m8 = cx.sb(st, [128, 8], F32, name="m8")
        r8 = cx.sb(st, [128, 8], F32, name="r8")
        t512 = cx.sb(st, [128, 512], F32, name="t512")
        yo = [cx.sb(st, [128, 512], F32, name="yo") for _ in range(2)]
        ptr = [cx.ps(st, [128, 512], F32, name="ptr") for _ in range(2)]
        pA = cx.ps(st, [128, 512], F32, name="pA")
        pD = [cx.ps(st, [128, 4, 128], F32, name="pD") for _ in range(2)]
        pY = cx.ps(st, [128, 512], F32, name="pY")
        pU = cx.ps(st, [128, 512], F32, name="pU")
        pO = cx.ps(st, [128, 512], F32, name="pO")
        cx.op('pool', [], [ST], lambda e: e.memset(ST[:, :, :], 0.0))
        MUs, MUi, MLs = rwm[:, 0:128], rwm[:, 128:256], rwm[:, 256:384]
        for c in range(NT):
            rows = slice(c * 128, (c + 1) * 128)
            I6 = IN[c % 2]
            for i, nm in enumerate(names6):
                cx.dma('sp' if i % 2 == 0 else 'act', I6[:, i, :], RW[nm][rows, :], writes=[I6])
            rr, lw, k2, vv, kn, aa = [I6[:, i, :] for i in range(6)]
            pL = ptr[0]
            mm(cx, pL, pL[:, :], MUi, lw, [rwm, I6], True, True)
            cx.op('act', [pL], [EL], lambda e: e.activation(out=EL[:, 0, :], in_=pL[:, :], func=AF.Exp))
            cx.op('act', [pL], [EL], lambda e: e.activation(out=EL[:, 1, :], in_=pL[:, :], func=AF.Exp, scale=-1.0))
            cx.op('dve', [pL, I6], [EL], lambda e: e.tensor_tensor(EL[:, 2, :], pL[:, :], lw, ALU.subtract))
            cx.op('act', [EL], [EL], lambda e: e.activation(out=EL[:, 2, :], in_=EL[:, 2, :], func=AF.Exp))
            cx.op('dve', [I6, EL], [TM],
                  lambda e: e.scalar_tensor_tensor(out=TM[:, 0, :], in0=kn, scalar=-1.0, in1=EL[:, 2, :], op0=ALU.mult, op1=ALU.mult))
            cx.op('pool', [I6, EL], [TM], lambda e: e.tensor_tensor(TM[:, 1, :], rr, EL[:, 0, :], ALU.mult))
            cx.op('dve', [I6], [TM], lambda e: e.tensor_tensor(TM[:, 2, :], kn, aa, ALU.mult))
            cx.op('dve', [TM, EL], [TM], lambda e: e.tensor_tensor(TM[:, 2, :], TM[:, 2, :], EL[:, 1, :], ALU.mult))
            cx.op('pool', [I6, EL], [TM], lambda e: e.tensor_tensor(TM[:, 3, :], k2, EL[:, 1, :], ALU.mult))
            ppc = ptr[1]
            for h in range(8):
                mm(cx, ppc, ppc[:64, h:h + 1], lw[:, h * 64:(h + 1) * 64], cx.ones_f[:, 0:1], [I6, cx.ones_f], True, True)
            cx.op('act', [ppc], [pc], lambda e: e.activation(out=pc[:, :], in_=ppc[:64, 0:8], func=AF.Exp))
            cx.op('pool', [ST, pc], [STs],
                  lambda e: e.tensor_tensor(STs[:, :, :], ST[:, :, :], pc[:, :].unsqueeze(2).broadcast_to([64, 8, 64]), ALU.mult))
            k = 0
            for q in range(4):
                for hh in range(2):
                    pt = ptr[k % 2]
                    k += 1
                    for j in range(4):
                        h = hh * 4 + j
                        transp(cx, pt, pt[:64, j * 128:(j + 1) * 128], TM[:, q, h * 64:(h + 1) * 64], cx.identf[:, :], [TM, cx.identf])
                    evac(cx, 'act' if k % 2 else 'dve', pt, pt[:64, :].rearrange("p (j t) -> p j t", j=4), XT,
                         fr(XT[:, hh * 4:(hh + 1) * 4, q, :]))
            for h in range(8):
                ar = XT[:, h, 0:2, :].rearrange("p q t -> p (q t)")
                mm(cx, pA, pA[:, 0:256], fr(XT[:, h, 2, :]), fr(ar), [XT], True, True)
                mm(cx, pA, pA[:, 256:512], fr(XT[:, h, 3, :]), fr(ar), [XT], True, True)
                pd = pD[h % 2]
                mm(cx, pd, pd[:, 0, :], fr(XT[:, h, 0, :]), fr(XT[:, h, 2, :]), [XT], True, True)
                cx.op('dve', [pA, mask4], [MM_], lambda e: e.tensor_tensor(MM_[:, h, :], pA[:, :], mask4[:, :], ALU.mult))
                cx.op('dve', [pd, rwm], [XX[0]], lambda e: e.tensor_tensor(fr(XX[0][:, h, 1, :]), pd[:, 0, :], MLs, ALU.mult))
                cx.op('pool', [MM_], [XX[0]], lambda e: e.tensor_copy(fr(XX[0][:, h, 0, :]), MM_[:, h, 0:128]))
                cx.op('pool', [MM_, cx.identf], [NTb], lambda e: e.tensor_tensor(fr(NTb[:, h, :]), MM_[:, h, 0:128], cx.identf[:, :], ALU.add))
            for lev in range(1, 7):
                cur, nxt = XX[(lev - 1) % 2], XX[lev % 2]
                for p in range(4):
                    pd = pD[p % 2]
                    for j in range(2):
                        h = 2 * p + j
                        if lev < 6:
                            mm(cx, pd, pd[:, 2 * j, :], fr(cur[:, h, 1, :]), fr(cur[:, h, 0, :]), [cur], True, True)
                        mm(cx, pd, pd[:, 2 * j + 1, :], fr(cur[:, h, 0, :]), fr(cur[:, h, 1, :]), [cur], True, True)
                    if lev < 6:
                        evac(cx, 'act' if p % 2 else 'dve', pd, pd[:, :, :], nxt,
                             fr(nxt[:, 2 * p:2 * p + 2, :, :].rearrange("p h q t -> p (h q) t")))
                    else:
                        for j in range(2):
                            evac(cx, 'act' if j else 'dve', pd, pd[:, 2 * j + 1, :], nxt, fr(nxt[:, 2 * p + j, 1, :]))
                for p in range(4):
                    pd = pD[p % 2]
                    for j in range(2):
                        h = 2 * p + j
                        mm(cx, pd, pd[:, j, :], fr(nxt[:, h, 1, :]), fr(NTb[:, h, :]), [nxt, NTb], True, True)
                    cx.op('dve', [pd, NTb], [NTb],
                          lambda e: e.tensor_tensor(fr(NTb[:, 2 * p:2 * p + 2, :]), NTb[:, 2 * p:2 * p + 2, :], pd[:, 0:2, :], ALU.add))
            for h in range(8):
                hs = slice(h * 64, (h + 1) * 64)
                mm(cx, pY, pY[:, hs], XT[:, h, 0, :], ST[:, h, :], [XT, ST], True, False)
                mm(cx, pY, pY[:, hs], MM_[:, h, 256:384], vv[:, hs], [MM_, I6], False, True)
            evac(cx, 'dve', pY, pY[:, 0:256], Yb, Yb[:, 0:4, :].rearrange("p h c -> p (h c)"))
            evac(cx, 'act', pY, pY[:, 256:512], Yb, Yb[:, 4:8, :].rearrange("p h c -> p (h c)"))
            for h in range(8):
                hs = slice(h * 64, (h + 1) * 64)
                mm(cx, pU, pU[:, hs], NTb[:, h, :], Yb[:, h, :], [NTb, Yb], True, True)
            evac(cx, 'dve', pU, pU[:, 0:256], Ub, Ub[:, 0:4, :].rearrange("p h c -> p (h c)"))
            evac(cx, 'act', pU, pU[:, 256:512], Ub, Ub[:, 4:8, :].rearrange("p h c -> p (h c)"))
            for h in range(8):
                hs = slice(h * 64, (h + 1) * 64)
                mm(cx, pY, pY[:64, hs], TM[:, 2, hs], Ub[:, h, :], [TM, Ub], True, False)
                mm(cx, pY, pY[:64, hs], TM[:, 3, hs], vv[:, hs], [TM, I6], False, True)
            for h in range(8):
                hs = slice(h * 64, (h + 1) * 64)
                mm(cx, pO, pO[:, hs], XT[:, h, 1, :], ST[:, h, :], [XT, ST], True, False)
                mm(cx, pO, pO[:, hs], MM_[:, h, 128:256], Ub[:, h, :], [MM_, Ub], False, False)
                mm(cx, pO, pO[:, hs], MM_[:, h, 384:512], vv[:, hs], [MM_, I6], False, True)
            cx.op('dve', [pY, pc], [ST],
                  lambda e: e.tensor_tensor(ST[:, :, :], pY[:64, :].rearrange("p (h c) -> p h c", h=8),
                                            pc[:, :].unsqueeze(2).broadcast_to([64, 8, 64]), ALU.mult))
            cx.op('dve', [ST, STs], [ST], lambda e: e.tensor_tensor(ST[:, :, :], ST[:, :, :], STs[:, :, :], ALU.add))
            evac(cx, 'act', pO, pO[:, :], Ob, Ob[:, :])
            g3, b8b, y = G3[c % 2], b8[c % 2], yo[c % 2]
            cx.dma('sp', g3[:, :], RW['gg'][rows, :], writes=[g3])
            cx.dma('act', b8b[:, :], RW['bc'][rows, :], writes=[b8b])
            cx.op('dve', [Ob], [m8], lambda e: e.tensor_reduce(out=m8[:, :], in_=v3(Ob[:, :]), axis=AX.X, op=ALU.add))
            cx.op('dve', [m8], [m8], lambda e: e.tensor_scalar(m8[:, :], m8[:, :], 1.0 / 64, None, ALU.mult))
            cx.op('dve', [Ob, m8], [Ob], lambda e: e.tensor_tensor(v3(Ob[:, :]), v3(Ob[:, :]), bc8(m8[:, :]), ALU.subtract))
            cx.op('pool', [Ob], [t512], lambda e: e.tensor_tensor(t512[:, :], Ob[:, :], Ob[:, :], ALU.mult))
            cx.op('dve', [t512], [r8], lambda e: e.tensor_reduce(out=r8[:, :], in_=v3(t512[:, :]), axis=AX.X, op=ALU.add))
            cx.op('dve', [r8], [r8], lambda e: e.tensor_scalar(r8[:, :], r8[:, :], 1.0 / 64, 64e-5, ALU.mult, ALU.add))
            cx.op('pool', [r8, cx.neghalf], [r8],
                  lambda e: e.tensor_tensor(r8[:, :], r8[:, :], cx.neghalf[:, 0:1].to_broadcast([128, 8]), ALU.pow))
            cx.op('dve', [Ob, r8], [y], lambda e: e.tensor_tensor(v3(y[:, :]), v3(Ob[:, :]), bc8(r8[:, :]), ALU.mult))
            cx.op('pool', [y, gwB], [y], lambda e: e.tensor_tensor(y[:, :], y[:, :], gwB[:, :], ALU.mult))
            cx.op('pool', [y, gbB], [y], lambda e: e.tensor_tensor(y[:, :], y[:, :], gbB[:, :], ALU.add))
            cx.op('dve', [I6, b8b], [t512], lambda e: e.tensor_tensor(v3(t512[:, :]), v3(vv), bc8(b8b[:, :]), ALU.mult))
            cx.op('pool', [y, t512], [y], lambda e: e.tensor_tensor(y[:, :], y[:, :], t512[:, :], ALU.add))
            cx.op('dve', [y, g3], [y], lambda e: e.tensor_tensor(y[:, :], y[:, :], g3[:, :], ALU.mult))
            cx.dma('pool', Y_d[rows, :], y[:, :], reads=[y])
    cx.barrier()


NSA_SCALE = 128 ** -0.5
NSA_BIG = 30000.0
NSA_STOP = 0


def phase_nsa(cx, S, l, P_d, W, Wb, C, Youts, scr):
    NT = S // 128
    G = min(512, S)
    NG = G // 128
    QW = min(512, S)
    NJ = QW // 128
    Nc = (S - 32) // 16 + 1
    NKB = (Nc + 127) // 128
    N = scr['nsa']
    with contextlib.ExitStack() as st:
        pn = [cx.sb(st, [128, 1292], F32, name="pn") for _ in range(2)]
        cs_t = [cx.sb(st, [128, 32], F32, name="cs") for _ in range(2)]
        ro = [cx.sb(st, [128, 10, 32], F32, name="ro") for _ in range(2)]
        t1 = cx.sb(st, [128, 10, 16], F32, name="t1")
        t2 = cx.sb(st, [128, 10, 16], F32, name="t2")
        fb = [cx.sb(st, [128, 8, 128], BF16, name="fb") for _ in range(2)]
        va = [cx.sb(st, [128, 2, 129], BF16, name="va") for _ in range(2)]
        sq = cx.sb(st, [128, 6, 128], F32, name="sq")
        n6 = [cx.sb(st, [128, 6], F32, name="n6") for _ in range(2)]
        nqb = [cx.sb(st, [128, 4], BF16, name="nqb") for _ in range(2)]
        gt = [cx.sb(st, [128, 12], F32, name="gt") for _ in range(2)]
        kmx = cx.sb(st, [128, 2], F32, name="kmx")
        gA = cx.sb(st, [128, 8, G], BF16, name="gA")
        gN = cx.sb(st, [1, 4, G], BF16, name="gN")
        trp = TrPool(cx, st)
        trf = TrPool(cx, st, n=1, dtype=F32)
        cx.op('pool', [], [kmx], lambda e: e.memset(kmx[:, :], 0.0))
        for tt in range(NT):
            tl = tt % NG
            rows = slice(tt * 128, (tt + 1) * 128)
            p, c_t, r, f, v, n6b, nq_, g_ = pn[tt % 2], cs_t[tt % 2], ro[tt % 2], fb[tt % 2], va[tt % 2], n6[tt % 2], nqb[tt % 2], gt[tt % 2]
            cx.dma('sp', p[:, :], P_d[rows, OFF_NSA:OFF_NSA + 1292], writes=[p])
            cx.dma('act', c_t[:, 0:16], C['nsa_cos'][rows, :], writes=[c_t])
            cx.dma('act', c_t[:, 16:32], C['nsa_sin'][rows, :], writes=[c_t])
            blk = p[:, 0:1280].rearrange("p (b c) -> p b c", b=10)
            cb = c_t[:, 0:16].unsqueeze(1).broadcast_to([128, 10, 16])
            sb_ = c_t[:, 16:32].unsqueeze(1).broadcast_to([128, 10, 16])
            rope_tm(cx, p, blk[:, :, 0:16], blk[:, :, 16:32], [c_t, cb], [c_t, sb_],
                    r, r[:, :, 0:16], r[:, :, 16:32], [(t1, t1[:, :, :]), (t2, t2[:, :, :])])
            cx.op('act', [p], [f], lambda e: e.activation(out=f[:, 0:4, 32:128], in_=blk[:, 0:4, 32:128], func=AF.Copy, scale=NSA_SCALE))
            cx.op('act', [r], [f], lambda e: e.activation(out=f[:, 0:4, 0:32], in_=r[:, 0:4, :], func=AF.Copy, scale=NSA_SCALE))
            for dst, src in ((4, 4), (6, 6), (7, 8)):
                cx.op('pool', [p], [f], lambda e: e.tensor_copy(f[:, dst, 32:128], blk[:, src, 32:128]))
                cx.op('pool', [r], [f], lambda e: e.tensor_copy(f[:, dst, 0:32], r[:, src, :]))
            cx.op('pool', [p], [f], lambda e: e.tensor_copy(f[:, 5, :], blk[:, 5, :]))
            cx.op('pool', [p], [v], lambda e: e.tensor_copy(v[:, 0, 0:128], blk[:, 7, :]))
            cx.op('pool', [p], [v], lambda e: e.tensor_copy(v[:, 1, 0:128], blk[:, 9, :]))
            cx.op('pool', [], [v], lambda e: e.memset(v[:, :, 128:129], 1.0))
            cx.op('dve', [p], [sq], lambda e: e.tensor_tensor(sq[:, 0:4, :], blk[:, 0:4, :], blk[:, 0:4, :], ALU.mult))
            cx.op('dve', [p], [sq], lambda e: e.tensor_tensor(sq[:, 4, :], blk[:, 6, :], blk[:, 6, :], ALU.mult))
            cx.op('dve', [p], [sq], lambda e: e.tensor_tensor(sq[:, 5, :], blk[:, 8, :], blk[:, 8, :], ALU.mult))
            cx.op('dve', [sq], [n6b], lambda e: e.tensor_reduce(out=n6b[:, :], in_=sq[:, :, :], axis=AX.X, op=ALU.add))
            cx.op('dve', [n6b, kmx], [kmx], lambda e: e.tensor_tensor(kmx[:, :], kmx[:, :], n6b[:, 4:6], ALU.max))
            cx.op('act', [n6b], [n6b], lambda e: e.activation(out=n6b[:, 0:4], in_=n6b[:, 0:4], func=AF.Sqrt))
            cx.op('dve', [n6b], [nq_], lambda e: e.tensor_scalar(nq_[:, :], n6b[:, 0:4], -NSA_SCALE, None, ALU.mult))
            cx.dma('sp', g_[:, :], P_d[rows, OFF_NSA + 1280:OFF_NSA + 1292], writes=[g_])
            cx.op('act', [g_], [g_], lambda e: e.activation(out=g_[:, :], in_=g_[:, :], func=AF.Sigmoid))
            cx.dma('pool', N['ng'][rows, :], g_[:, :], reads=[g_])
            trp.transpose_cols(f, lambda j: f[:, j, :], 8, gA, lambda j0, cnt: gA[:, j0:j0 + cnt, tl * 128:(tl + 1) * 128])
            trp.transpose_cols(nq_, lambda j: nq_[:, j:j + 1], 4, gN,
                               lambda j0, cnt: gN[0:1, j0:j0 + cnt, tl * 128:(tl + 1) * 128], blkw=1)
            cx.dma('pool', N['vsa'][rows, :], v[:, 0, :], reads=[v])
            cx.dma('pool', N['vwa'][rows, :], v[:, 1, :], reads=[v])
            if tl == NG - 1:
                g0 = (tt // NG) * G
                cx.dma('pool', N['qT'][:, :, g0:g0 + G].rearrange("h d s -> d h s"), gA[:, 0:4, :], reads=[gA])
                for j, nm in ((4, 'kcT'), (5, 'vcT'), (6, 'ksT'), (7, 'kwT')):
                    cx.dma('pool', N[nm][:, g0:g0 + G], gA[:, j, :], reads=[gA])
                cx.dma('pool', N['nq'][:, g0:g0 + G].rearrange("(o h) s -> o h s", o=1), gN[0:1, :, :], reads=[gN])
        kms = cx.sb(st, [128, 1], F32, name="kms")
        kmw = cx.sb(st, [128, 1], F32, name="kmw")
        k1 = cx.sb(st, [128, 1], F32, name="k1")
        rowsb = cx.sb(st, [1, 2, 128], BF16, name="rowsb")
        for i, dstc in enumerate((kms, kmw)):
            cx.op('pool', [kmx], [k1], lambda e: e.tensor_copy(k1[:, :], kmx[:, i:i + 1]))
            bcast_scalar_max(cx, st, trf, k1, dstc)
            cx.op('act', [dstc], [dstc], lambda e: e.activation(out=dstc[:, :], in_=dstc[:, :], func=AF.Sqrt))
            cx.op('dve', [dstc], [rowsb], lambda e: e.tensor_copy(rowsb[0:1, i, :], dstc[0:1, 0:1].to_broadcast([1, 128])))
        cx.dma('pool', N['krow'].rearrange("a b -> (a b)").rearrange("(o n) -> o n", o=1), rowsb[0:1, :, :].rearrange("o a b -> o (a b)"), reads=[rowsb])
    cx.barrier()
    if NSA_STOP == 1:
        return
    with contextlib.ExitStack() as st:
        KC = cx.sb(st, [128, 256], BF16, name="KC")
        VCA = cx.sb(st, [128, 2, 193], BF16, name="VCA")
        krow = cx.sb(st, [128, 3, 128], BF16, name="krow")
        cx.op('pool', [], [krow], lambda e: e.memset(krow[:, :, :], 0.0))
        cx.dma('sp', krow[0:1, 0:2, :].rearrange("o a b -> o (a b)"), N['krow'].rearrange("a b -> (a b)").rearrange("(o n) -> o n", o=1), writes=[krow])
        cx.dma('sp', VCA[:, :, 129:193], C['cover'].rearrange("(kb p) j -> p kb j", p=128), writes=[VCA])
        cx.op('pool', [], [VCA], lambda e: e.memset(VCA[:, :, 0:129], 0.0))
        cx.op('pool', [], [VCA], lambda e: e.memset(VCA[:, :, 128:129], 1.0))
        cx.op('pool', [], [KC], lambda e: e.memset(KC[:, :], 0.0))
        with contextlib.ExitStack() as s2:
            pm = [cx.ps(s2, [128, 512], F32, name="pm") for _ in range(2)]
            xT = [cx.sb(s2, [128, S], BF16, name="xT") for _ in range(2)]
            cx.dma('sp', xT[0][:, :], N['kcT'][:, :], writes=[xT[0]])
            cx.dma('act', xT[1][:, :], N['vcT'][:, :], writes=[xT[1]])
            w1 = [cx.sb(s2, [128, 32, 128], BF16, name="w1") for _ in range(2)]
            w2 = [cx.sb(s2, [128, 128], BF16, name="w2") for _ in range(2)]
            posf = cx.sb(s2, [32, 2, 128], F32, name="posf")
            posb = cx.sb(s2, [32, 2, 128], BF16, name="posb")
            posT = cx.sb(s2, [128, 2, 32], BF16, name="posT")
            bias = cx.sb(s2, [128, 2], F32, name="bias")
            xs = cx.sb(s2, [128, 256], F32, name="xs")
            x2 = cx.sb(s2, [128, 256], F32, name="x2")
            hid = [cx.sb(s2, [128, 256], BF16, name="hid") for _ in range(2)]
            ksq = cx.sb(s2, [128, 256], BF16, name="ksq")
            one = cx.sb(s2, [1, 2], F32, name="one")
            trp = TrPool(cx, s2, n=1)
            for z in range(2):
                cx.dma('sp', w1[z][:, :, :], Wb['nsa_cmp_w1'][l][z].rearrange("(l d) e -> d l e", d=128), writes=[w1[z]])
                cx.dma('sp', w2[z][:, :], Wb['nsa_cmp_w2'][l][z], writes=[w2[z]])
            cx.dma('sp', posf[:, :, :], W['nsa_cmp_pos'][l].rearrange("z l d -> l z d"), writes=[posf])
            cx.op('dve', [posf], [posb], lambda e: e.tensor_copy(posb[:, :, :], posf[:, :, :]))
            trp.transpose_cols(posb, lambda j: posb[:, j, :], 2, posT, lambda j0, cnt: posT[:, j0:j0 + cnt, :], rows=32)
            for z in range(2):
                pb, ph = pm
                for ll in range(32):
                    mm(cx, pb, pb[:, z:z + 1], w1[z][:, ll, :], posT[:, z, ll:ll + 1], [w1[z], posT], ll == 0, ll == 31)
                evac(cx, 'dve', pb, pb[:, z:z + 1], bias, bias[:, z:z + 1])
                for ll in range(32):
                    mm(cx, ph, ph[:, :Nc], w1[z][:, ll, :], xT[z][:, ll:ll + 16 * (Nc - 1) + 1:16], [w1[z], xT[z]], ll == 0, ll == 31)
                cx.op('act', [ph, bias], [xs], lambda e: e.activation(out=xs[:, :Nc], in_=ph[:, :Nc], func=AF.Identity, bias=bias[:, z:z + 1]))
                cx.op('dve', [xs], [x2], lambda e: e.tensor_tensor(x2[:, :Nc], xs[:, :Nc], xs[:, :Nc], ALU.mult))
                cx.op('dve', [x2], [x2], lambda e: e.tensor_scalar(x2[:, :Nc], x2[:, :Nc], 0.044715, 1.0, ALU.mult, ALU.add))
                cx.op('dve', [x2, xs], [x2], lambda e: e.tensor_tensor(x2[:, :Nc], x2[:, :Nc], xs[:, :Nc], ALU.mult))
                cx.op('act', [x2], [x2], lambda e: e.activation(out=x2[:, :Nc], in_=x2[:, :Nc], func=AF.Tanh, scale=0.7978845608028654))
                cx.op('dve', [x2], [x2], lambda e: e.tensor_scalar(x2[:, :Nc], x2[:, :Nc], 1.0, 0.5, ALU.add, ALU.mult))
                cx.op('dve', [x2, xs], [hid[z]], lambda e: e.tensor_tensor(hid[z][:, :Nc], x2[:, :Nc], xs[:, :Nc], ALU.mult))
            pk = pm[0]
            mm(cx, pk, pk[:, :Nc], w2[0][:, :], hid[0][:, :Nc], [w2[0], hid[0]], True, True)
            evac(cx, 'act', pk, pk[:, :Nc], KC, KC[:, :Nc])
            cx.op('act', [pk], [ksq], lambda e: e.activation(out=ksq[:, :Nc], in_=pk[:, :Nc], func=AF.Square))
            pr = pm[1]
            mm(cx, pr, pr[0:1, :Nc], cx.ones_bf[:, 0:1], ksq[:, :Nc], [cx.ones_bf, ksq], True, True)
            cx.op('dve', [pr], [one], lambda e: e.tensor_reduce(out=one[0:1, 0:1], in_=pr[0:1, :Nc], axis=AX.X, op=ALU.max))
            cx.op('act', [one], [one], lambda e: e.activation(out=one[0:1, 0:1], in_=one[0:1, 0:1], func=AF.Sqrt))
            cx.op('dve', [one], [krow], lambda e: e.tensor_scalar(krow[0:1, 2, :], one[0:1, 0:1].to_broadcast([1, 128]), 1.02, None, ALU.mult))
            for kb in range(NKB):
                nk = min(128, Nc - kb * 128)
                pv = pm[kb % 2]
                mm(cx, pv, pv[:nk, 0:128], hid[1][:, kb * 128:kb * 128 + nk], w2[1][:, :], [hid[1], w2[1]], True, True)
                evac(cx, 'dve', pv, pv[:nk, 0:128], VCA, VCA[:nk, kb, 0:128])
        cx.barrier()
        if NSA_STOP == 2:
            return
        res = AttnRes(cx, st, 193)
        qT = [cx.sb(st, [128, S], BF16, name="qT") for _ in range(4)]
        nq = [cx.sb(st, [128, S], BF16, name="nq") for _ in range(4)]
        for h in range(4):
            cx.op('pool', [], [nq[h]], lambda e: e.memset(nq[h][:, :], 0.0))
            cx.dma('sp', qT[h][:, :], N['qT'][h], writes=[qT[h]])
            cx.dma('act', nq[h][0:1, :], N['nq'][h:h + 1, :], writes=[nq[h]])
        ksT = cx.sb(st, [128, S], BF16, name="ksT")
        kwT = cx.sb(st, [128, S], BF16, name="kwT")
        cx.dma('sp', ksT[:, :], N['ksT'][:, :], writes=[ksT])
        cx.dma('act', kwT[:, :], N['kwT'][:, :], writes=[kwT])
        vsa = cx.sb(st, [128, NT, 129], BF16, name="vsa")
        vwa = cx.sb(st, [128, NT, 129], BF16, name="vwa")
        cx.dma('sp', vsa[:, :, :], N['vsa'].rearrange("(t p) c -> p t c", p=128), writes=[vsa])
        cx.dma('act', vwa[:, :, :], N['vwa'].rearrange("(t p) c -> p t c", p=128), writes=[vwa])
        cmask = cx.sb(st, [128, 4, 512], BF16, name="cmask")
        wmask = cx.sb(st, [128, 4, 512], BF16, name="wmask")
        cx.dma('sp', cmask[:, :, :], C['cmask'].rearrange("i k q -> k i q"), writes=[cmask])
        cx.dma('sp', wmask[:, :, :], C['wmask'].rearrange("i k q -> k i q"), writes=[wmask])
        cmpm = cx.sb(st, [128, 2, S], BF16, name="cmpm")
        cx.dma('sp', cmpm[:, :, :], C['cmpmask'].rearrange("kb p q -> p kb q"), writes=[cmpm])
        Em = cx.sb(st, [64, S], BF16, name="Em")
        cx.dma('sp', Em[:, :], C['Emat'][:, :], writes=[Em])
        gts = cx.sb(st, [128, NT, 12], F32, name="gts")
        cx.dma('sp', gts[:, :, :], N['ng'].rearrange("(t p) c -> p t c", p=128), writes=[gts])
        imp = cx.sb(st, [128, 4, 64], F32, name="imp")
        fbt = [cx.sb(st, [128, 64], F32, name="fbt") for _ in range(2)]
        m8 = cx.sb(st, [128, 16], F32, name="m8")
        val2 = cx.sb(st, [128, 64], F32, name="val2")
        selb = cx.sb(st, [128, 64], BF16, name="selb")
        selT = cx.sb(st, [64, 512], BF16, name="selT")
        rc = [cx.sb(st, [128, 1], F32, name="rc") for _ in range(2)]
        rg = [cx.sb(st, [128, 1], F32, name="rg") for _ in range(2)]
        ot = [cx.sb(st, [128, 128], F32, name="ot") for _ in range(3)]
        it = [cx.sb(st, [128, 64], F32, name="it") for _ in range(2)]
        trp2 = TrPool(cx, st, n=1)
        cnt = [0]

        def make_epi(branch, h):
            def epi(QB, j, accb, acc_ap):
                k = cnt[0]
                cnt[0] += 1
                tt = QB * NJ + j
                r, rgb, o = rc[k % 2], rg[k % 2], ot[k % 3]
                cx.op('dve', [accb], [r], lambda e: e.tensor_scalar(r[:, :], acc_ap[:, 128:129], 1e-30, None, ALU.add))
                cx.op('dve', [r], [r], lambda e: e.reciprocal(r[:, :], r[:, :]))
                cx.op('dve', [r, gts], [rgb], lambda e: e.tensor_tensor(rgb[:, :], r[:, :], gts[:, tt, h * 3 + branch:h * 3 + branch + 1], ALU.mult))
                cx.op('act', [accb, rgb], [o], lambda e: e.activation(out=o[:, :], in_=acc_ap[:, 0:128], func=AF.Copy, scale=rgb[:, 0:1]))
                cx.dma('pool', Youts[branch][tt * 128:(tt + 1) * 128, h * 128:(h + 1) * 128], o[:, :], reads=[o])
                if branch == 0:
                    if h == 0:
                        cx.op('dve', [accb, r], [imp], lambda e: e.tensor_scalar(imp[:, j, :], acc_ap[:, 129:193], r[:, 0:1], None, ALU.mult))
                    else:
                        i_ = it[k % 2]
                        cx.op('dve', [accb, r], [i_], lambda e: e.tensor_scalar(i_[:, :], acc_ap[:, 129:193], r[:, 0:1], None, ALU.mult))
                        cx.op('pool', [i_, imp], [imp], lambda e: e.tensor_tensor(imp[:, j, :], imp[:, j, :], i_[:, :], ALU.add))
            return epi

        def cmp_blocks(QB):
            q0 = QB * QW
            return [dict(kb=kb, k0=kb * 128, nk=min(128, Nc - kb * 128),
                         mask=(cmpm, cmpm[:min(128, Nc - kb * 128), kb, q0:q0 + QW])) for kb in range(NKB)]

        def slc_blocks(QB):
            bl = causal_blocks(QB, QW, cmask)
            for b in bl:
                b['extra'] = ([Em], Em[:, b['k0']:b['k0'] + 128], [selT], selT[:, :QW])
            return bl

        def win_blocks(QB):
            out = []
            nd = QW // 128
            for kb in range(max(0, nd * QB - 4), nd * (QB + 1)):
                i = kb - nd * QB
                m = (cmask, cmask[:, i, :QW]) if i >= 0 else (wmask, wmask[:, i + 4, :QW])
                out.append(dict(kb=kb, k0=kb * 128, nk=128, mask=m))
            return out

        for QB in range(S // QW):
            for h in range(4):
                attn_core(cx, res, S, [(qT[h], qT[h][:, :]), (nq[h], nq[h][:, :])],
                          [(kwT, kwT[:, :]), (krow, lambda k0, nk: krow[:, 1, :nk])],
                          lambda kb, nk: (vwa, vwa[:nk, kb, :]), 129, win_blocks, make_epi(2, h), qbs=[QB])
            if NSA_STOP == 3:
                break
            for h in range(4 if NSA_STOP != 8 else 0):
                attn_core(cx, res, S, [(qT[h], qT[h][:, :]), (nq[h], nq[h][:, :])],
                          [(KC, KC[:, :]), (krow, lambda k0, nk: krow[:, 2, :nk])],
                          lambda kb, nk: (VCA, VCA[:nk, kb, :]), 193, cmp_blocks, make_epi(0, h), qbs=[QB])
            if NSA_STOP == 4:
                break
            for j in range(NJ if NSA_STOP != 8 else 0):
                tt = QB * NJ + j
                f_ = fbt[j % 2]
                cx.dma('sp', f_[:, :], C['fbias'][tt * 128:(tt + 1) * 128, :], writes=[f_])
                cx.op('dve', [imp, f_], [f_], lambda e: e.tensor_tensor(f_[:, :], f_[:, :], imp[:, j, :], ALU.add))
                cx.op('dve', [f_], [m8], lambda e: e.max(out=m8[:, 0:8], in_=f_[:, :]))
                cx.op('dve', [f_, m8], [val2], lambda e: e.match_replace(out=val2[:, :], in_to_replace=m8[:, 0:8], in_values=f_[:, :], imm_value=-3.0e38))
                cx.op('dve', [val2], [m8], lambda e: e.max(out=m8[:, 8:16], in_=val2[:, :]))
                cx.op('dve', [f_, m8], [val2], lambda e: e.tensor_scalar(val2[:, :], f_[:, :], m8[:, 15:16], None, ALU.is_ge))
                cx.op('dve', [val2], [selb], lambda e: e.tensor_scalar(selb[:, :], val2[:, :], -1.0, NSA_BIG, ALU.add, ALU.mult))
                trp2.transpose_cols(selb, lambda jj: selb[:, :], 1, selT, lambda j0, c_, j=j: selT[:64, j * 128:(j + 1) * 128].unsqueeze(1), blkw=64)
            if NSA_STOP == 5:
                break
            for h in range(4 if NSA_STOP not in (8, 9) else 0):
                attn_core(cx, res, S, [(qT[h], qT[h][:, :]), (nq[h], nq[h][:, :])],
                          [(ksT, ksT[:, :]), (krow, lambda k0, nk: krow[:, 0, :nk])],
                          lambda kb, nk: (vsa, vsa[:nk, kb, :]), 129, slc_blocks, make_epi(1, h), qbs=[QB])
    cx.barrier()


S_FULL = 4096
ENABLE = {'mla': True, 'nsa': True, 'rwkv': True, 'ret': True}


def phase_zero(cx, S, Y_d):
    with contextlib.ExitStack() as st:
        z = cx.sb(st, [128, 512], F32, name="z")
        cx.op('pool', [], [z], lambda e: e.memset(z[:, :], 0.0))
        for tt in range(S // 128):
            cx.dma('sp', Y_d[tt * 128:(tt + 1) * 128, :], z[:, :], reads=[z])
    cx.barrier()


def host_consts(S):
    import ml_dtypes
    bf = ml_dtypes.bfloat16
    c = {}
    c['ident'] = np.eye(128, dtype=np.float32).astype(bf)
    kk = np.arange(128)[:, None]
    qq = np.arange(512)[None, :]
    c['cmask'] = np.stack([(128 * i + kk <= qq) for i in range(4)]).astype(np.float32).astype(bf)
    t = np.arange(S, dtype=np.float32)[:, None]

    def tables(inv):
        ang = (t * inv[None, :].astype(np.float32)).astype(np.float32)
        return np.cos(ang).astype(np.float32), np.sin(ang).astype(np.float32)

    inv_mla = (np.float32(500000.0) ** (-np.arange(0, 64, 2, dtype=np.float32) / np.float32(64))).astype(np.float32)
    inv_nsa = (np.float32(500000.0) ** (-np.arange(0, 32, 2, dtype=np.float32) / np.float32(32))).astype(np.float32)
    inv_ret = (np.float32(10000.0) ** (-np.linspace(0.0, 1.0, 32, dtype=np.float32))).astype(np.float32)
    c['mla_cos'], c['mla_sin'] = tables(inv_mla)
    c['nsa_cos'], c['nsa_sin'] = tables(inv_nsa)
    c['ret_cos'], c['ret_sin'] = tables(inv_ret)
    kf = kk.astype(np.float64)
    qf = qq.astype(np.float64)
    gdec = np.zeros((4, 5, 128, 512), np.float32)
    for h in range(4):
        lg = np.log1p(-2.0 ** (-5 - h))
        gdec[h, 0] = np.exp((qf - kf) * lg)
        for i in range(4):
            d = qf - kf - 128 * i
            gdec[h, 1 + i] = np.where(d >= 0, np.exp(np.maximum(d, 0) * lg), 0.0)
    c['gdec'] = gdec
    si = np.arange(128)[:, None]
    ti = np.arange(128)[None, :]
    c['wmask'] = (1.0 - c['cmask'].astype(np.float32)).astype(bf)
    Nc = (S - 32) // 16 + 1
    n = np.arange(256)
    q = np.arange(S)
    cm = ((16 * n[:, None] + 31 <= q[None, :]) & (n[:, None] < Nc)).astype(np.float32)
    c['cmpmask'] = cm.reshape(2, 128, S).astype(bf)
    nblk = S // 64
    jb = np.arange(64)
    cstart = 16 * n
    cend = cstart + 31
    cover = ((cstart[:, None] <= jb[None, :] * 64 + 63) & (cend[:, None] >= jb[None, :] * 64) & (n[:, None] < Nc)
             & (jb[None, :] < nblk)).astype(np.float32)
    c['cover'] = cover.astype(bf)
    c['Emat'] = (q[None, :] // 64 == jb[:, None]).astype(np.float32).astype(bf)
    cur = q // 64
    forced = (jb[None, :] == 0) | (jb[None, :] == cur[:, None]) | (jb[None, :] == cur[:, None] - 1)
    visible = (jb[None, :] <= cur[:, None]) & (jb[None, :] < nblk)
    c['fbias'] = np.where(visible, 1000.0 * forced, -1.0e30).astype(np.float32)
    c['rwm'] = np.concatenate([(si < ti), (si <= ti), (si > ti)], axis=1).astype(np.float32)
    return c


CONST_SPECS = {'ident': ([128, 128], BF16), 'cmask': ([4, 128, 512], BF16),
               'mla_cos': (None, F32), 'mla_sin': (None, F32), 'nsa_cos': (None, F32), 'nsa_sin': (None, F32),
               'ret_cos': (None, F32), 'ret_sin': (None, F32), 'gdec': ([4, 5, 128, 512], F32), 'rwm': ([128, 384], F32), 'wmask': ([4, 128, 512], BF16), 'cmpmask': ('cmp', BF16),
               'cover': ([256, 64], BF16), 'Emat': ('E', BF16), 'fbias': ('fb', F32)}

WEIGHT_SHAPES = {
    'w_in': [DEPTH, D_MODEL, IN_WIDTH], 'w_branch': [DEPTH, 4, BW, D_MODEL], 'w_out': [DEPTH, D_MODEL, D_MODEL],
    'w_up': [DEPTH, D_MODEL, D_FF], 'w_down': [DEPTH, D_FF, D_MODEL], 'norm_gains': [DEPTH, 4, D_MODEL],
    'mla_g_q': [DEPTH, 384], 'mla_g_kv': [DEPTH, 128], 'mla_w_uq': [DEPTH, 384, 768], 'mla_w_ukv': [DEPTH, 128, 1024],
    'nsa_cmp_pos': [DEPTH, 2, 32, 128], 'nsa_cmp_w1': [DEPTH, 2, 4096, 128], 'nsa_cmp_w2': [DEPTH, 2, 128, 128],
    'rwkv_mu': [DEPTH, 1984], 'rwkv_w0': [DEPTH, 512], 'rwkv_w2': [DEPTH, 96, 512], 'rwkv_a0': [DEPTH, 512],
    'rwkv_a2': [DEPTH, 96, 512], 'rwkv_g2': [DEPTH, 256, 512], 'rwkv_k_k': [DEPTH, 512], 'rwkv_k_a': [DEPTH, 512],
    'rwkv_r_k': [DEPTH, 8, 64], 'rwkv_gn_w': [DEPTH, 512], 'rwkv_gn_b': [DEPTH, 512],
}
CAST = ['w_in', 'w_branch', 'w_out', 'w_up', 'w_down', 'mla_w_uq', 'mla_w_ukv', 'nsa_cmp_w1', 'nsa_cmp_w2',
        'rwkv_w2', 'rwkv_a2', 'rwkv_g2']


def build_program(S, depth=DEPTH):
    cx = Ctx()
    x_d = cx.dram("x", [S, D_MODEL], F32, kind="ExternalInput")
    W = {k: cx.dram(k, shp, F32, kind="ExternalInput") for k, shp in WEIGHT_SHAPES.items()}
    C = {}
    for k, (shp, dt) in CONST_SPECS.items():
        if shp is None:
            shp = [S, 16 if k.startswith('nsa') else 32]
        elif shp == 'cmp':
            shp = [2, 128, S]
        elif shp == 'E':
            shp = [64, S]
        elif shp == 'fb':
            shp = [S, 64]
        C[k] = cx.dram("c_" + k, shp, dt, kind="ExternalInput")
    y_d = cx.dram("y", [S, D_MODEL], F32, kind="ExternalOutput")
    Wb = {k: cx.dram(k + "_bf", WEIGHT_SHAPES[k], BF16) for k in CAST}
    P_d = cx.dram("P", [S, IN_WIDTH], F32)
    Y = [cx.dram("Y%d" % m, [S, BW], F32) for m in range(4)]
    Yn = [cx.dram("Yn%d" % m, [S, BW], F32) for m in range(2)]
    M_d = cx.dram("M", [S, D_MODEL], F32)
    Z_d = cx.dram("Z", [S, D_MODEL], F32)
    xa = cx.dram("xa", [S, D_MODEL], F32)
    xb = cx.dram("xb", [S, D_MODEL], F32)
    scr = dict(qnT=cx.dram("qnT", [4, 128, S], BF16), qrT=cx.dram("qrT", [4, 65, S], BF16),
               knT=cx.dram("knT", [4, 128, S], BF16), krT=cx.dram("krT", [65, S], BF16),
               va=cx.dram("va", [S, 4, 129], BF16),
               rqT=cx.dram("rqT", [4, 64, S], BF16), rkT=cx.dram("rkT", [4, 64, S], BF16),
               rv=cx.dram("rv", [S, 4, 128], BF16))
    scr['rw'] = {nm: cx.dram("rw_" + nm, [S, 512], F32) for nm in ['rr', 'lw', 'k2', 'vv', 'kn', 'aa', 'gg']}
    scr['rw']['bc'] = cx.dram("rw_bc", [S, 8], F32)
    scr['nsa'] = dict(qT=cx.dram("n_qT", [4, 128, S], BF16), nq=cx.dram("n_nq", [4, S], BF16),
                      kcT=cx.dram("n_kcT", [128, S], BF16), vcT=cx.dram("n_vcT", [128, S], BF16),
                      ksT=cx.dram("n_ksT", [128, S], BF16), kwT=cx.dram("n_kwT", [128, S], BF16),
                      vsa=cx.dram("n_vsa", [S, 129], BF16), vwa=cx.dram("n_vwa", [S, 129], BF16),
                      ng=cx.dram("n_ng", [S, 12], F32), krow=cx.dram("n_krow", [2, 128], BF16))
    st = contextlib.ExitStack()
    cx._st = st
    setup_consts(cx, st, C['ident'])
    phase_cast(cx, [(W[k], Wb[k]) for k in CAST])
    xin = x_d
    for l in range(depth):
        g = W['norm_gains'][l]
        phase_in(cx, S, xin, g[0], Wb['w_in'][l], P_d)
        if ENABLE['mla']:
            phase_mla(cx, S, P_d, W['mla_g_q'][l], W['mla_g_kv'][l], Wb['mla_w_uq'][l], Wb['mla_w_ukv'][l],
                      C['mla_cos'], C['mla_sin'], C['cmask'], Y[0], scr)
        else:
            phase_zero(cx, S, Y[0])
        if ENABLE['nsa']:
            phase_nsa(cx, S, l, P_d, W, Wb, C, [Y[1], Yn[0], Yn[1]], scr)
            ysrc1 = [Y[1], Yn[0], Yn[1]]
        else:
            phase_zero(cx, S, Y[1])
            ysrc1 = [Y[1]]
        if ENABLE['rwkv']:
            phase_rwkv(cx, S, l, P_d, W, Wb, C, Y[2], scr)
        else:
            phase_zero(cx, S, Y[2])
        if ENABLE['ret']:
            phase_ret(cx, S, P_d, C['ret_cos'], C['ret_sin'], C['gdec'], Y[3], scr)
        else:
            phase_zero(cx, S, Y[3])
        phase_merge(cx, S, [[Y[0]], ysrc1, [Y[2]], [Y[3]]], Wb['w_branch'][l], P_d, M_d)
        phase_out(cx, S, M_d, Wb['w_out'][l], Z_d)
        phase_normres(cx, S, xin, Z_d, g[1], xa)
        phase_ffn(cx, S, xa, g[2], Wb['w_up'][l], Wb['w_down'][l], Z_d)
        xnext = y_d if l == depth - 1 else xb
        phase_normres(cx, S, xa, Z_d, g[3], xnext)
        xin = xnext
    cx.barrier()
    return cx


_CACHE = {}


def kernel(**inputs):
    x = np.ascontiguousarray(np.asarray(inputs['x'], dtype=np.float32))
    B, S, _ = x.shape
    if S not in _CACHE:
        _CACHE[S] = (build_program(S), host_consts(S))
    cx, consts = _CACHE[S]
    base = {k: np.ascontiguousarray(np.asarray(inputs[k], dtype=np.float32)) for k in WEIGHT_SHAPES}
    for k, v in consts.items():
        base["c_" + k] = np.ascontiguousarray(v)
    in_maps = []
    for b in range(B):
        m = dict(base)
        m['x'] = x[b]
        in_maps.append(m)
    res = run_bass_kernel_spmd(cx.nc, in_maps, core_ids=list(range(B)))
    return np.stack([np.asarray(r['y'], dtype=np.float32) for r in res.results], axis=0)
```

```python
import contextlib
import numpy as np
import concourse.bass as bass
import concourse.mybir as mybir
from concourse.bass_utils import run_bass_kernel_spmd

F32 = mybir.dt.float32
F32R = mybir.dt.float32r
RW_FAST = True


def fr(ap):
    return ap.bitcast(F32R) if RW_FAST else ap
BF16 = mybir.dt.bfloat16
AF = mybir.ActivationFunctionType
ALU = mybir.AluOpType
AX = mybir.AxisListType

D_MODEL = 2048
DEPTH = 2
BW = 512
D_FF = 8192
NORM_EPS = 1e-6
IN_WIDTH = 13580
OFF_MLA = 0
OFF_NSA = 576
OFF_RWKV = 1868
OFF_RET = 3852
OFF_GATE = 5388

SELF_SYNC = {'pe': False, 'act': False, 'dve': True, 'pool': True, 'sp': False}


class Reg:
    __slots__ = ('w', 'r')

    def __init__(self):
        self.w = None
        self.r = {}


class Buf:
    def __init__(self, t, nreg=1, excl=False):
        self.t = t
        self.regs = [Reg() for _ in range(nreg)]
        self.excl = excl

    @property
    def reg(self):
        return self.regs[0]

    def __getitem__(self, idx):
        return self.t[idx]


class Ctx:
    def __init__(self):
        self.nc = bass.Bass("TRN2", target_bir_lowering=False)
        nc = self.nc
        self.E = {'pe': nc.tensor, 'act': nc.scalar, 'dve': nc.vector, 'pool': nc.gpsimd, 'sp': nc.sync}
        self.sem = {e: nc.alloc_semaphore("s_" + e) for e in ['pe', 'act', 'dve', 'pool']}
        self.cnt = {e: 0 for e in self.sem}
        self.NDS = 48
        self.dsem = [nc.alloc_semaphore("d%d" % i) for i in range(self.NDS)]
        self.dcnt = [0] * self.NDS
        self.dpool = {'sp': list(range(0, 20)), 'act': list(range(20, 34)), 'pool': list(range(34, 48))}
        self.dnext = {'sp': 0, 'act': 0, 'pool': 0}
        self.known = {e: {} for e in self.E}
        self.ninst = 0
        self.uid = 0

    def name(self, p):
        self.uid += 1
        return "%s_%d" % (p, self.uid)

    def sb(self, stack, shape, dtype, nreg=1, name="sb"):
        t = stack.enter_context(self.nc.sbuf_tensor(self.name(name), list(shape), dtype))
        return Buf(t, nreg)

    def ps(self, stack, shape, dtype=F32, nreg=1, name="ps"):
        t = stack.enter_context(self.nc.psum_tensor(self.name(name), list(shape), dtype))
        return Buf(t, nreg, excl=True)

    def dram(self, name, shape, dtype, kind="Internal"):
        return self.nc.dram_tensor(name, list(shape), dtype, kind=kind).ap()

    def _wait(self, e, kind, val, force=False):
        if isinstance(kind, str):
            if kind == e and not SELF_SYNC[e] and not (force and e in self.sem):
                return
            sem = self.sem[kind]
            v = val
        else:
            idx = kind[1]
            sem = self.dsem[idx]
            v = val * 16
        k = self.known[e]
        if k.get(kind, 0) >= v:
            return
        self.E[e].wait_ge(sem, v)
        self.ninst += 1
        k[kind] = v

    def _deps(self, e, reads, writes, force=False):
        for r in reads:
            if r.w is not None:
                self._wait(e, r.w[0], r.w[1], force)
        for w in writes:
            if w.w is not None:
                self._wait(e, w.w[0], w.w[1], force)
            for kind, val in w.r.items():
                self._wait(e, kind, val, force)

    def _commit(self, tok, reads, writes):
        kind, val = tok
        for r in reads:
            if r.r.get(kind, 0) < val:
                r.r[kind] = val
        for w in writes:
            w.w = tok
            w.r = {}

    @staticmethod
    def _regs(lst):
        out = []
        for x in lst:
            if isinstance(x, Buf):
                out.extend(x.regs)
            elif isinstance(x, Reg):
                out.append(x)
            elif x is None:
                pass
            else:
                raise TypeError(type(x))
        return out

    def op(self, e, reads, writes, fn):
        writes = list(writes) + [x for x in reads if isinstance(x, Buf) and x.excl]
        reads = [x for x in reads if not (isinstance(x, Buf) and x.excl)]
        reads = self._regs(reads)
        writes = self._regs(writes)
        self._deps(e, reads, writes)
        inst = fn(self.E[e])
        self.cnt[e] += 1
        self.ninst += 1
        inst.then_inc(self.sem[e], 1)
        self._commit((e, self.cnt[e]), reads, writes)
        return inst

    def dma(self, q, out_ap, in_ap, reads=(), writes=(), **kw):
        reads = self._regs(reads)
        writes = self._regs(writes)
        self._deps(q, reads, writes, force=True)
        pool = self.dpool[q]
        idx = pool[self.dnext[q] % len(pool)]
        self.dnext[q] += 1
        if self.dcnt[idx] > 0:
            self._wait(q, ('d', idx), self.dcnt[idx])
        self.E[q].dma_start(out=out_ap, in_=in_ap, **kw).then_inc(self.dsem[idx], 16)
        self.dcnt[idx] += 1
        self.ninst += 1
        self._commit((('d', idx), self.dcnt[idx]), reads, writes)

    def barrier(self):
        for e in self.E:
            for o in self.sem:
                if o != e and self.cnt[o] > 0:
                    self._wait(e, o, self.cnt[o])
            for i in range(self.NDS):
                if self.dcnt[i] > 0:
                    self._wait(e, ('d', i), self.dcnt[i])
            if e in self.sem and self.cnt[e] > 0:
                k = self.known[e]
                if k.get(e, 0) < self.cnt[e]:
                    self.E[e].wait_ge(self.sem[e], self.cnt[e])
                    k[e] = self.cnt[e]


def mm(cx, out_buf, out_ap, lhsT_ap, rhs_ap, reads, start, stop, **kw):
    return cx.op('pe', reads, [out_buf],
                 lambda e: e.matmul(out_ap, lhsT_ap, rhs_ap, start=start, stop=stop, **kw))


def transp(cx, out_buf, out_ap, in_ap, ident_ap, reads):
    return cx.op('pe', reads, [out_buf], lambda e: e.transpose(out_ap, in_ap, ident_ap))


def phase_cast(cx, pairs):
    CH = 4096
    NB = 6
    with contextlib.ExitStack() as st:
        stg = [cx.sb(st, [128, CH], F32, name="cst") for _ in range(NB)]
        outb = [cx.sb(st, [128, CH], BF16, name="cob") for _ in range(NB)]
        engs = ['dve', 'pool', 'act']
        k = 0
        for src, dst in pairs:
            n = 1
            for s in src.shape:
                n *= s
            assert n % 128 == 0
            per = n // 128
            names = " ".join("a%d" % i for i in range(len(src.shape)))
            s2 = src.rearrange("%s -> (%s)" % (names, names)).rearrange("(p f) -> p f", p=128)
            d2 = dst.rearrange("%s -> (%s)" % (names, names)).rearrange("(p f) -> p f", p=128)
            for c0 in range(0, per, CH):
                c1 = min(per, c0 + CH)
                w = c1 - c0
                i = k % NB
                cx.dma('sp', stg[i][:, :w], s2[:, c0:c1], writes=[stg[i]])
                e = engs[k % 3]
                if e == 'act':
                    cx.op(e, [stg[i]], [outb[i]], lambda en: en.copy(outb[i][:, :w], stg[i][:, :w]))
                else:
                    cx.op(e, [stg[i]], [outb[i]], lambda en: en.tensor_copy(outb[i][:, :w], stg[i][:, :w]))
                cx.dma('act' if k % 2 else 'pool', d2[:, c0:c1], outb[i][:, :w], reads=[outb[i]])
                k += 1
    cx.barrier()


def load_bcast_row(cx, q, buf, row_ap, n):
    cx.dma(q, buf[:, :n], row_ap.partition_broadcast(128), writes=[buf])


def rms_rstd(cx, x_buf, x_ap, n, ss_buf, junk_buf, eps=NORM_EPS):
    cx.op('act', [x_buf], [junk_buf, ss_buf],
          lambda e: e.activation(out=junk_buf[:, :n], in_=x_ap, func=AF.Square, accum_out=ss_buf[:, 0:1]))
    cx.op('dve', [ss_buf], [ss_buf],
          lambda e: e.tensor_scalar(ss_buf[:, 0:1], ss_buf[:, 0:1], 1.0 / n, eps, ALU.mult, ALU.add))
    cx.op('pool', [ss_buf, cx.neghalf], [ss_buf],
          lambda e: e.tensor_tensor(ss_buf[:, 0:1], ss_buf[:, 0:1], cx.neghalf[:, 0:1], ALU.pow))


def setup_consts(cx, st, ident_d):
    cx.ident = cx.sb(st, [128, 128], BF16, name="ident")
    cx.dma('sp', cx.ident[:, :], ident_d[:, :], writes=[cx.ident])
    cx.identf = cx.sb(st, [128, 128], F32, name="identf")
    cx.op('dve', [cx.ident], [cx.identf], lambda e: e.tensor_copy(cx.identf[:, :], cx.ident[:, :]))
    cx.neghalf = cx.sb(st, [128, 1], F32, name="neghalf")
    cx.op('pool', [], [cx.neghalf], lambda e: e.memset(cx.neghalf[:, :], -0.5))
    cx.ones_bf = cx.sb(st, [128, 128], BF16, name="ones_bf")
    cx.op('pool', [], [cx.ones_bf], lambda e: e.memset(cx.ones_bf[:, :], 1.0))
    cx.ones_f = cx.sb(st, [128, 128], F32, name="ones_f")
    cx.op('pool', [], [cx.ones_f], lambda e: e.memset(cx.ones_f[:, :], 1.0))


def phase_in(cx, S, x_d, g_row, w_bf, P_d):
    G = 512
    KC = D_MODEL // 128
    chunks = []
    c = 0
    while c < OFF_GATE:
        chunks.append((c, min(c + 512, OFF_GATE), False))
        c += 512
    c = OFF_GATE
    while c < IN_WIDTH:
        chunks.append((c, c + 512, True))
        c += 512
    wv = w_bf.rearrange("(kc p) n -> p kc n", p=128)
    with contextlib.ExitStack() as st:
        gB = cx.sb(st, [128, D_MODEL], F32, name="gB")
        load_bcast_row(cx, 'sp', gB, g_row, D_MODEL)
        xt = [cx.sb(st, [128, D_MODEL], F32, name="xt") for _ in range(2)]
        junk = cx.sb(st, [128, D_MODEL], BF16, name="junk")
        ss = [cx.sb(st, [128, 1], F32, name="ss") for _ in range(2)]
        hb = [cx.sb(st, [128, D_MODEL], BF16, name="hb") for _ in range(2)]
        hT = [cx.sb(st, [128, KC, G], BF16, name="hT") for _ in range(2)]
        wb = [cx.sb(st, [128, KC, 512], BF16, name="wb") for _ in range(2)]
        ob = [cx.sb(st, [128, 512], F32, name="ob") for _ in range(4)]
        ptr = [cx.ps(st, [128, 8, 128], BF16, name="ptr") for _ in range(2)]
        pmm = [cx.ps(st, [128, 512], F32, name="pmm") for _ in range(4)]
        ntr = 0
        nmm = 0
        nw = 0
        for gi in range(S // G):
            hTg = hT[gi % 2]
            for tl in range(G // 128):
                tt = gi * (G // 128) + tl
                xb = xt[tt % 2]
                sb_ = ss[tt % 2]
                hbb = hb[tt % 2]
                cx.dma('sp', xb[:, :], x_d[tt * 128:(tt + 1) * 128, :], writes=[xb])
                rms_rstd(cx, xb, xb[:, :], D_MODEL, sb_, junk)
                cx.op('dve', [xb, sb_, gB], [hbb],
                      lambda e: e.scalar_tensor_tensor(out=hbb[:, :], in0=xb[:, :], scalar=sb_[:, 0:1],
                                                       in1=gB[:, :], op0=ALU.mult, op1=ALU.mult))
                for k4 in range(KC // 4):
                    pt = ptr[ntr % 2]
                    ntr += 1
                    for j in range(4):
                        kc = k4 * 4 + j
                        transp(cx, pt, pt[:, j, :], hbb[:, kc * 128:(kc + 1) * 128], cx.ident[:, :], [hbb, cx.ident])
                    eng = 'act' if (k4 % 2 == 0) else 'dve'
                    dst = hTg[:, k4 * 4:(k4 + 1) * 4, tl * 128:(tl + 1) * 128]
                    if eng == 'act':
                        cx.op('act', [pt], [hTg], lambda e: e.copy(dst, pt[:, 0:4, :]))
                    else:
                        cx.op('dve', [pt], [hTg], lambda e: e.tensor_copy(dst, pt[:, 0:4, :]))
            for (c0, c1, sig) in chunks:
                w = c1 - c0
                wbb = wb[nw % 2]
                nw += 1
                cx.dma('sp', wbb[:, :, :w], wv[:, :, c0:c1], writes=[wbb])
                for tl in range(G // 128):
                    tt = gi * (G // 128) + tl
                    pm = pmm[nmm % 4]
                    obb = ob[nmm % 4]
                    nmm += 1
                    for kc in range(KC):
                        mm(cx, pm, pm[:, :w], hTg[:, kc, tl * 128:(tl + 1) * 128], wbb[:, kc, :w],
                           [hTg, wbb], kc == 0, kc == KC - 1)
                    if sig:
                        cx.op('act', [pm], [obb],
                              lambda e: e.activation(out=obb[:, :w], in_=pm[:, :w], func=AF.Sigmoid))
                    elif nmm % 2 == 0:
                        cx.op('dve', [pm], [obb], lambda e: e.tensor_copy(obb[:, :w], pm[:, :w]))
                    else:
                        cx.op('act', [pm], [obb], lambda e: e.copy(obb[:, :w], pm[:, :w]))
                    cx.dma('pool', P_d[tt * 128:(tt + 1) * 128, c0:c1], obb[:, :w], reads=[obb])
    cx.barrier()


def evac(cx, eng, src_buf, src_ap, dst_buf, dst_ap, extra_reads=()):
    if eng == 'act':
        cx.op('act', [src_buf] + list(extra_reads), [dst_buf], lambda e: e.copy(dst_ap, src_ap))
    else:
        cx.op(eng, [src_buf] + list(extra_reads), [dst_buf], lambda e: e.tensor_copy(dst_ap, src_ap))


class TrPool:
    def __init__(self, cx, st, n=2, dtype=BF16):
        self.cx = cx
        self.bufs = [cx.ps(st, [128, 8 if dtype == BF16 else 4, 128], dtype, name="ptr") for _ in range(n)]
        self.k = 0
        self.dtype = dtype

    def transpose_cols(self, src_buf, src_ap_fn, nblk, dst_buf, dst_ap_fn, rows=128, blkw=128):
        cx = self.cx
        ident = cx.ident if self.dtype == BF16 else cx.identf
        j = 0
        while j < nblk:
            cnt = min(4, nblk - j)
            pt = self.bufs[self.k % len(self.bufs)]
            eng = 'act' if self.k % 2 == 0 else 'dve'
            self.k += 1
            for i in range(cnt):
                transp(cx, pt, pt[:blkw, i, :rows], src_ap_fn(j + i), ident[:rows, :rows], [src_buf, ident])
            evac(cx, eng, pt, pt[:blkw, :cnt, :rows], dst_buf, dst_ap_fn(j, cnt))
            j += cnt


def phase_merge(cx, S, ysrcs, wbr_bf, P_d, M_d):
    with contextlib.ExitStack() as st:
        wbr = cx.sb(st, [128, 16, D_MODEL], BF16, name="wbr")
        wv = wbr_bf.rearrange("m (kc p) n -> p (m kc) n", p=128)
        for q in range(4):
            cx.dma('sp', wbr[:, q * 4:(q + 1) * 4, :], wv[:, q * 4:(q + 1) * 4, :], writes=[wbr])
        yt = [cx.sb(st, [128, BW], F32, name="yt") for _ in range(3)]
        yb = [cx.sb(st, [128, BW], BF16, name="yb") for _ in range(2)]
        yT = [cx.sb(st, [128, 4, 128], BF16, name="yT") for _ in range(2)]
        sg = [cx.sb(st, [128, D_MODEL], F32, name="sg") for _ in range(2)]
        mg = [cx.sb(st, [128, D_MODEL], F32, name="mg") for _ in range(2)]
        tmp = [cx.sb(st, [128, 512], F32, name="tmp") for _ in range(2)]
        trp = TrPool(cx, st)
        pmm = [cx.ps(st, [128, 512], F32, name="pmm") for _ in range(4)]
        k = 0
        for tt in range(S // 128):
            rows = slice(tt * 128, (tt + 1) * 128)
            mgb = mg[tt % 2]
            for m in range(4):
                k += 1
                y0 = yt[k % 3]
                cx.dma('sp', y0[:, :], ysrcs[m][0][rows, :], writes=[y0])
                for extra in ysrcs[m][1:]:
                    k += 1
                    y1 = yt[k % 3]
                    cx.dma('sp', y1[:, :], extra[rows, :], writes=[y1])
                    cx.op('pool', [y0, y1], [y0], lambda e: e.tensor_tensor(y0[:, :], y0[:, :], y1[:, :], ALU.add))
                ybb = yb[m % 2]
                cx.op('pool', [y0], [ybb], lambda e: e.tensor_copy(ybb[:, :], y0[:, :]))
                yTb = yT[m % 2]
                trp.transpose_cols(ybb, lambda j: ybb[:, j * 128:(j + 1) * 128], 4, yTb,
                                   lambda j0, cnt: yTb[:, j0:j0 + cnt, :])
                sgb = sg[m % 2]
                cx.dma('act', sgb[:, :], P_d[rows, OFF_GATE + m * D_MODEL:OFF_GATE + (m + 1) * D_MODEL], writes=[sgb])
                for nc_ in range(4):
                    cs = slice(nc_ * 512, (nc_ + 1) * 512)
                    pm = pmm[(m * 4 + nc_) % 4]
                    for kc in range(4):
                        mm(cx, pm, pm[:, :], yTb[:, kc, :], wbr[:, m * 4 + kc, cs], [yTb, wbr], kc == 0, kc == 3)
                    if m == 0:
                        cx.op('dve', [pm, sgb], [mgb],
                              lambda e: e.tensor_tensor(mgb[:, cs], pm[:, :], sgb[:, cs], ALU.mult))
                    else:
                        tb = tmp[nc_ % 2]
                        cx.op('dve', [pm, sgb], [tb],
                              lambda e: e.tensor_tensor(tb[:, :], pm[:, :], sgb[:, cs], ALU.mult))
                        cx.op('pool', [tb, mgb], [mgb],
                              lambda e: e.tensor_tensor(mgb[:, cs], mgb[:, cs], tb[:, :], ALU.add))
            cx.dma('pool', M_d[rows, :], mgb[:, :], reads=[mgb])
    cx.barrier()


def phase_out(cx, S, M_d, wout_bf, Z_d):
    with contextlib.ExitStack() as st:
        wo = cx.sb(st, [128, 16, D_MODEL], BF16, name="wo")
        wv = wout_bf.rearrange("(kc p) n -> p kc n", p=128)
        for q in range(4):
            cx.dma('sp', wo[:, q * 4:(q + 1) * 4, :], wv[:, q * 4:(q + 1) * 4, :], writes=[wo])
        mt = [cx.sb(st, [128, D_MODEL], F32, name="mt") for _ in range(2)]
        mb = [cx.sb(st, [128, D_MODEL], BF16, name="mb") for _ in range(2)]
        mT = [cx.sb(st, [128, 16, 128], BF16, name="mT") for _ in range(2)]
        ob = [cx.sb(st, [128, 512], F32, name="ob") for _ in range(4)]
        trp = TrPool(cx, st)
        pmm = [cx.ps(st, [128, 512], F32, name="pmm") for _ in range(4)]
        k = 0
        for tt in range(S // 128):
            rows = slice(tt * 128, (tt + 1) * 128)
            mtb, mbb, mTb = mt[tt % 2], mb[tt % 2], mT[tt % 2]
            cx.dma('sp', mtb[:, :], M_d[rows, :], writes=[mtb])
            cx.op('pool', [mtb], [mbb], lambda e: e.tensor_copy(mbb[:, :], mtb[:, :]))
            trp.transpose_cols(mbb, lambda j: mbb[:, j * 128:(j + 1) * 128], 16, mTb,
                               lambda j0, cnt: mTb[:, j0:j0 + cnt, :])
            for nc_ in range(4):
                cs = slice(nc_ * 512, (nc_ + 1) * 512)
                pm = pmm[k % 4]
                obb = ob[k % 4]
                k += 1
                for kc in range(16):
                    mm(cx, pm, pm[:, :], mTb[:, kc, :], wo[:, kc, cs], [mTb, wo], kc == 0, kc == 15)
                evac(cx, 'act' if k % 2 else 'dve', pm, pm[:, :], obb, obb[:, :])
                cx.dma('pool', Z_d[rows, cs], obb[:, :], reads=[obb])
    cx.barrier()


def phase_normres(cx, S, x_d, Z_d, g_row, out_d):
    with contextlib.ExitStack() as st:
        gB = cx.sb(st, [128, D_MODEL], F32, name="gB")
        load_bcast_row(cx, 'sp', gB, g_row, D_MODEL)
        zt = [cx.sb(st, [128, D_MODEL], F32, name="zt") for _ in range(2)]
        xt = [cx.sb(st, [128, D_MODEL], F32, name="xt") for _ in range(2)]
        ot = [cx.sb(st, [128, D_MODEL], F32, name="ot") for _ in range(2)]
        junk = cx.sb(st, [128, D_MODEL], BF16, name="junk")
        ss = [cx.sb(st, [128, 1], F32, name="ss") for _ in range(2)]
        for tt in range(S // 128):
            rows = slice(tt * 128, (tt + 1) * 128)
            z, x, o, s_ = zt[tt % 2], xt[tt % 2], ot[tt % 2], ss[tt % 2]
            cx.dma('sp', z[:, :], Z_d[rows, :], writes=[z])
            cx.dma('act', x[:, :], x_d[rows, :], writes=[x])
            rms_rstd(cx, z, z[:, :], D_MODEL, s_, junk)
            cx.op('dve', [z, s_, gB], [o],
                  lambda e: e.scalar_tensor_tensor(out=o[:, :], in0=z[:, :], scalar=s_[:, 0:1], in1=gB[:, :],
                                                   op0=ALU.mult, op1=ALU.mult))
            cx.op('pool', [o, x], [o], lambda e: e.tensor_tensor(o[:, :], o[:, :], x[:, :], ALU.add))
            cx.dma('pool', out_d[rows, :], o[:, :], reads=[o])
    cx.barrier()


def phase_ffn(cx, S, x_d, g_row, wup_bf, wdn_bf, Z_d):
    G = 512 if S >= 512 else S
    NT = G // 128
    KC = D_MODEL // 128
    FC = D_FF // 128
    UW = 256
    wuv = wup_bf.rearrange("(kc p) f -> p kc f", p=128)
    wdv = wdn_bf.rearrange("(fc p) n -> p fc n", p=128)
    with contextlib.ExitStack() as st:
        gB = cx.sb(st, [128, D_MODEL], F32, name="gB")
        load_bcast_row(cx, 'sp', gB, g_row, D_MODEL)
        xt = [cx.sb(st, [128, D_MODEL], F32, name="xt") for _ in range(2)]
        junk = cx.sb(st, [128, D_MODEL], BF16, name="junk")
        ss = [cx.sb(st, [128, 1], F32, name="ss") for _ in range(2)]
        hb = [cx.sb(st, [128, D_MODEL], BF16, name="hb") for _ in range(2)]
        hT = cx.sb(st, [128, KC, G], BF16, name="hT")
        aT = cx.sb(st, [128, FC, G], BF16, name="aT")
        wu = [cx.sb(st, [128, KC, UW], BF16, name="wu") for _ in range(2)]
        wd = [cx.sb(st, [128, 8, 512], BF16, name="wd") for _ in range(2)]
        rl = [cx.sb(st, [128, G], F32, name="rl") for _ in range(2)]
        ob = [cx.sb(st, [128, 512], F32, name="ob") for _ in range(4)]
        trp = TrPool(cx, st, n=1)
        pup = [cx.ps(st, [128, G], F32, name="pup") for _ in range(2)]
        pdn = [cx.ps(st, [128, 512], F32, name="pdn") for _ in range(NT)]
        nu = 0
        nd = 0
        no = 0
        for gi in range(S // G):
            for tl in range(NT):
                tt = gi * NT + tl
                x, s_, h = xt[tt % 2], ss[tt % 2], hb[tt % 2]
                cx.dma('sp', x[:, :], x_d[tt * 128:(tt + 1) * 128, :], writes=[x])
                rms_rstd(cx, x, x[:, :], D_MODEL, s_, junk)
                cx.op('dve', [x, s_, gB], [h],
                      lambda e: e.scalar_tensor_tensor(out=h[:, :], in0=x[:, :], scalar=s_[:, 0:1], in1=gB[:, :],
                                                       op0=ALU.mult, op1=ALU.mult))
                trp.transpose_cols(h, lambda j: h[:, j * 128:(j + 1) * 128], KC, hT,
                                   lambda j0, cnt: hT[:, j0:j0 + cnt, tl * 128:(tl + 1) * 128])
            for uc in range(D_FF // UW):
                wub = wu[nu % 2]
                nu += 1
                cx.dma('sp', wub[:, :, :], wuv[:, :, uc * UW:(uc + 1) * UW], writes=[wub])
                for j in range(UW // 128):
                    fc = uc * (UW // 128) + j
                    pu = pup[fc % 2]
                    r = rl[fc % 2]
                    for kc in range(KC):
                        mm(cx, pu, pu[:, :], wub[:, kc, j * 128:(j + 1) * 128], hT[:, kc, :], [wub, hT],
                           kc == 0, kc == KC - 1)
                    cx.op('act', [pu], [r], lambda e: e.activation(out=r[:, :], in_=pu[:, :], func=AF.Relu))
                    eng = 'dve' if fc % 2 == 0 else 'pool'
                    cx.op(eng, [r], [aT], lambda e: e.tensor_tensor(aT[:, fc, :], r[:, :], r[:, :], ALU.mult))
            for nc_ in range(4):
                cs = slice(nc_ * 512, (nc_ + 1) * 512)
                for fg in range(FC // 8):
                    wdb = wd[nd % 2]
                    nd += 1
                    cx.dma('act', wdb[:, :, :], wdv[:, fg * 8:(fg + 1) * 8, cs], writes=[wdb])
                    for f8 in range(8):
                        fc = fg * 8 + f8
                        for tl in range(NT):
                            mm(cx, pdn[tl], pdn[tl][:, :], aT[:, fc, tl * 128:(tl + 1) * 128], wdb[:, f8, :],
                               [aT, wdb], fc == 0, fc == FC - 1)
                for tl in range(NT):
                    tt = gi * NT + tl
                    o = ob[no % 4]
                    no += 1
                    evac(cx, 'act' if no % 2 else 'dve', pdn[tl], pdn[tl][:, :], o, o[:, :])
                    cx.dma('pool', Z_d[tt * 128:(tt + 1) * 128, cs], o[:, :], reads=[o])
    cx.barrier()


def rope_tm(cx, src, x1, x2, c, s, dst, o1, o2, tmps, scale=None):
    (ta, tap), (tb, tbp) = tmps
    cx.op('dve', [src] + c[:1] + [], [ta], lambda e: e.tensor_tensor(tap, x1, c[1], ALU.mult))
    cx.op('pool', [src] + s[:1], [tb], lambda e: e.tensor_tensor(tbp, x2, s[1], ALU.mult))
    cx.op('dve', [ta, tb], [dst], lambda e: e.tensor_tensor(o1, tap, tbp, ALU.subtract))
    cx.op('pool', [src] + c[:1], [ta], lambda e: e.tensor_tensor(tap, x2, c[1], ALU.mult))
    cx.op('dve', [src] + s[:1], [tb], lambda e: e.tensor_tensor(tbp, x1, s[1], ALU.mult))
    cx.op('pool', [ta, tb], [dst], lambda e: e.tensor_tensor(o2, tap, tbp, ALU.add))
    if scale is not None:
        cx.op('pool', [dst], [dst], lambda e: e.tensor_scalar(o1, o1, scale, None, ALU.mult))
        cx.op('pool', [dst], [dst], lambda e: e.tensor_scalar(o2, o2, scale, None, ALU.mult))


def bcast_scalar_max(cx, st, trp_f, run_buf, out_col):
    pt = trp_f.bufs[0]
    transp(cx, pt, pt[0:1, 0, :], run_buf[:, 0:1], cx.identf[:, :], [run_buf, cx.identf])
    row = cx.sb(st, [1, 128], F32, name="mxrow")
    one = cx.sb(st, [1, 1], F32, name="mxone")
    evac(cx, 'dve', pt, pt[0:1, 0, :], row, row[:, :])
    cx.op('dve', [row], [one], lambda e: e.tensor_reduce(out=one[:, :], in_=row[:, :], axis=AX.X, op=ALU.max))
    mm(cx, pt, pt[:, 1, 0:1], cx.ones_f[0:1, :], one[0:1, 0:1], [cx.ones_f, one], True, True)
    evac(cx, 'dve', pt, pt[:, 1, 0:1], out_col, out_col[:, 0:1])


class AttnRes:
    def __init__(self, cx, st, W):
        self.sT = [cx.ps(st, [128, 512], F32, name="sT") for _ in range(2)]
        self.acc = [cx.ps(st, [128, 2, 256], F32, name="acc") for _ in range(4)]
        self.pT = [cx.sb(st, [128, 512], BF16, name="pT") for _ in range(3)]
        self.n = 0
        self.nq = 0


def attn_core(cx, res, S, qchunks, kchunks, vaug_fn, W, blocks_fn, epilogue, mode='softmax', qbs=None):
    QW = min(512, S)
    NJ = QW // 128
    for QB in (range(S // QW) if qbs is None else qbs):
        q0 = QB * QW
        blocks = blocks_fn(QB)
        accs = [res.acc[(res.nq % 2) * 2 + (j // 2)] for j in range(NJ)]
        res.nq += 1

        def stage1(b):
            sT = res.sT[res.n % 2]
            pT = res.pT[res.n % 3]
            res.n += 1
            nk, k0 = b['nk'], b['k0']
            nmm = len(qchunks) + (1 if b.get('extra') else 0)
            i = 0
            for (qb_, qap), (kb_, kap) in zip(qchunks, kchunks):
                ka = kap(k0, nk) if callable(kap) else kap[:, k0:k0 + nk]
                mm(cx, sT, sT[:nk, :QW], ka, qap[:, q0:q0 + QW], [qb_, kb_], i == 0, i == nmm - 1)
                i += 1
            if b.get('extra'):
                lb, lap, rb, rap = b['extra']
                mm(cx, sT, sT[:nk, :QW], lap, rap, list(lb) + list(rb), False, True)
            if mode == 'softmax':
                cx.op('act', [sT], [pT], lambda e: e.activation(out=pT[:nk, :QW], in_=sT[:nk, :QW], func=AF.Exp))
                if b.get('mask'):
                    mb, map_ = b['mask']
                    cx.op('pool' if nk == 128 else 'dve', [pT, mb], [pT],
                          lambda e: e.tensor_tensor(pT[:nk, :QW], pT[:nk, :QW], map_, ALU.mult))
            else:
                c, gb, gap = b['decay']
                cx.op('dve', [sT, gb], [pT],
                      lambda e: e.scalar_tensor_tensor(out=pT[:nk, :QW], in0=sT[:nk, :QW], scalar=float(c), in1=gap,
                                                       op0=ALU.mult, op1=ALU.mult))
            return pT

        def stage2(bi, b, pT):
            nk = b['nk']
            vb, vap = vaug_fn(b['kb'], nk)
            for j in range(NJ):
                a = accs[j]
                mm(cx, a, a[:, j % 2, :W], pT[:nk, j * 128:(j + 1) * 128], vap, [pT, vb],
                   bi == 0 and j % 2 == 0, bi == len(blocks) - 1, skip_group_check=True)

        prev = None
        for bi, b in enumerate(blocks):
            pT = stage1(b)
            if prev is not None:
                stage2(*prev)
            prev = (bi, b, pT)
        stage2(*prev)
        for j in range(NJ):
            epilogue(QB, j, accs[j], accs[j][:, j % 2, :W])


def causal_blocks(QB, QW, cmask):
    out = []
    nd = QW // 128
    for kb in range(nd * (QB + 1)):
        i = kb - nd * QB
        out.append(dict(kb=kb, k0=kb * 128, nk=128, mask=(cmask, cmask[:, i, :QW]) if i >= 0 else None))
    return out


MLA_SCALE = 192 ** -0.5


def phase_mla(cx, S, P_d, gq_row, gkv_row, wuq_bf, wukv_bf, cos_d, sin_d, cmask_d, Y_d, scr):
    NT = S // 128
    G = min(512, S)
    NG = G // 128
    with contextlib.ExitStack() as st:
        with contextlib.ExitStack() as s1:
            gq = cx.sb(s1, [128, 384], F32, name="gq")
            gkv = cx.sb(s1, [128, 128], F32, name="gkv")
            load_bcast_row(cx, 'sp', gq, gq_row, 384)
            load_bcast_row(cx, 'sp', gkv, gkv_row, 128)
            wuq = cx.sb(s1, [128, 3, 768], BF16, name="wuq")
            cx.dma('sp', wuq[:, :, :], wuq_bf.rearrange("(kc p) n -> p kc n", p=128), writes=[wuq])
            wukv = cx.sb(s1, [128, 1024], BF16, name="wukv")
            cx.dma('sp', wukv[:, :], wukv_bf[:, :], writes=[wukv])
            pm = [cx.sb(s1, [128, 576], F32, name="pm") for _ in range(2)]
            cs_t = [cx.sb(s1, [128, 64], F32, name="cs") for _ in range(2)]
            junk = cx.sb(s1, [128, 768], BF16, name="junk")
            ss = [cx.sb(s1, [128, 1], F32, name="ss") for _ in range(2)]
            nb = [cx.sb(s1, [128, 384], BF16, name="nb") for _ in range(2)]
            nT = [cx.sb(s1, [128, 3, 128], BF16, name="nT") for _ in range(2)]
            qf = [cx.sb(s1, [128, 4, 256], F32, name="qf") for _ in range(2)]
            qs = [cx.sb(s1, [128, 4, 193], BF16, name="qs") for _ in range(2)]
            qsf = [cx.sb(s1, [128, 4, 64], F32, name="qsf") for _ in range(2)]
            t1 = cx.sb(s1, [128, 4, 32], F32, name="t1")
            t2 = cx.sb(s1, [128, 4, 32], F32, name="t2")
            kr = [cx.sb(s1, [128, 65], BF16, name="kr") for _ in range(2)]
            krf = [cx.sb(s1, [128, 64], F32, name="krf") for _ in range(2)]
            kb16 = [cx.sb(s1, [128, 4, 128], BF16, name="kb16") for _ in range(2)]
            va = [cx.sb(s1, [128, 4, 129], BF16, name="va") for _ in range(2)]
            sq = cx.sb(s1, [128, 4, 256], F32, name="sq")
            n4 = [cx.sb(s1, [128, 4], F32, name="n4") for _ in range(2)]
            n1 = [cx.sb(s1, [128, 1], F32, name="n1") for _ in range(2)]
            kmx = cx.sb(s1, [128, 1], F32, name="kmx")
            kmax = cx.sb(s1, [128, 1], F32, name="kmax")
            gA = cx.sb(s1, [128, 4, G], BF16, name="gA")
            gB_ = cx.sb(s1, [65, 4, G], BF16, name="gB_")
            trp = TrPool(cx, s1)
            trf = TrPool(cx, s1, n=1, dtype=F32)
            pq = [cx.ps(s1, [128, 512], F32, name="pq") for _ in range(2)]
            pq2 = [cx.ps(s1, [128, 512], F32, name="pq2") for _ in range(2)]
            cx.op('pool', [], [kmx], lambda e: e.memset(kmx[:, :], 0.0))

            def load_norm_T(tt, c0, n, gbuf, k):
                p = pm[k % 2]
                cx.dma('sp', p[:, :], P_d[tt * 128:(tt + 1) * 128, OFF_MLA:OFF_MLA + 576], writes=[p])
                s_ = ss[k % 2]
                rms_rstd(cx, p, p[:, c0:c0 + n], n, s_, junk)
                nbb = nb[k % 2]
                cx.op('dve', [p, s_, gbuf], [nbb],
                      lambda e: e.scalar_tensor_tensor(out=nbb[:, :n], in0=p[:, c0:c0 + n], scalar=s_[:, 0:1],
                                                       in1=gbuf[:, :n], op0=ALU.mult, op1=ALU.mult))
                nTb = nT[k % 2]
                trp.transpose_cols(nbb, lambda j: nbb[:, j * 128:(j + 1) * 128], n // 128, nTb,
                                   lambda j0, cnt: nTb[:, j0:j0 + cnt, :])
                return p, nTb

            for tt in range(NT):
                tl = tt % NG
                p, nTb = load_norm_T(tt, 384, 128, gkv, tt)
                c_t = cs_t[tt % 2]
                cx.dma('act', c_t[:, 0:32], cos_d[tt * 128:(tt + 1) * 128, :], writes=[c_t])
                cx.dma('act', c_t[:, 32:64], sin_d[tt * 128:(tt + 1) * 128, :], writes=[c_t])
                pa, pb = pq[tt % 2], pq2[tt % 2]
                mm(cx, pa, pa[:, :], nTb[:, 0, :], wukv[:, 0:512], [nTb, wukv], True, True)
                mm(cx, pb, pb[:, :], nTb[:, 0, :], wukv[:, 512:1024], [nTb, wukv], True, True)
                q = qf[tt % 2]
                evac(cx, 'act', pa, pa[:, :].rearrange("p (h c) -> p h c", h=2), q, q[:, 0:2, :])
                evac(cx, 'dve', pb, pb[:, :].rearrange("p (h c) -> p h c", h=2), q, q[:, 2:4, :])
                k16, vab, krb, krfb = kb16[tt % 2], va[tt % 2], kr[tt % 2], krf[tt % 2]
                cx.op('pool', [q], [k16], lambda e: e.tensor_copy(k16[:, :, :], q[:, :, 0:128]))
                cx.op('pool', [q], [vab], lambda e: e.tensor_copy(vab[:, :, 0:128], q[:, :, 128:256]))
                cx.op('pool', [], [vab], lambda e: e.memset(vab[:, :, 128:129], 1.0))
                rope_tm(cx, p, p[:, 512:544], p[:, 544:576], [c_t, c_t[:, 0:32]], [c_t, c_t[:, 32:64]],
                        krfb, krfb[:, 0:32], krfb[:, 32:64], [(t1, t1[:, 0, :]), (t2, t2[:, 0, :])])
                cx.op('pool', [krfb], [krb], lambda e: e.tensor_copy(krb[:, 0:64], krfb[:, :]))
                cx.op('pool', [], [krb], lambda e: e.memset(krb[:, 64:65], 1.0))
                cx.op('dve', [q], [sq], lambda e: e.tensor_tensor(sq[:, :, 0:128], q[:, :, 0:128], q[:, :, 0:128], ALU.mult))
                n4b, n1b = n4[tt % 2], n1[tt % 2]
                cx.op('dve', [sq], [n4b], lambda e: e.tensor_reduce(out=n4b[:, :], in_=sq[:, :, 0:128], axis=AX.X, op=ALU.add))
                cx.op('dve', [n4b], [n1b], lambda e: e.tensor_reduce(out=n1b[:, :], in_=n4b[:, :], axis=AX.X, op=ALU.max))
                cx.op('act', [krfb], [junk, n4b],
                      lambda e: e.activation(out=junk[:, :64], in_=krfb[:, :], func=AF.Square, accum_out=n4b[:, 0:1]))
                cx.op('dve', [n4b, n1b], [n1b], lambda e: e.tensor_tensor(n1b[:, :], n1b[:, :], n4b[:, 0:1], ALU.add))
                cx.op('dve', [n1b, kmx], [kmx], lambda e: e.tensor_tensor(kmx[:, :], kmx[:, :], n1b[:, :], ALU.max))
                trp.transpose_cols(k16, lambda j: k16[:, j, :], 4, gA,
                                   lambda j0, cnt: gA[:, j0:j0 + cnt, tl * 128:(tl + 1) * 128])
                trp.transpose_cols(krb, lambda j: krb[:, :], 1, gB_,
                                   lambda j0, cnt: gB_[:65, 0:1, tl * 128:(tl + 1) * 128], blkw=65)
                cx.dma('pool', scr['va'][tt * 128:(tt + 1) * 128, :, :], vab[:, :, :], reads=[vab])
                if tl == NG - 1:
                    g0 = (tt // NG) * G
                    cx.dma('pool', scr['knT'][:, :, g0:g0 + G].rearrange("h d s -> d h s"), gA[:, :, :], reads=[gA])
                    cx.dma('pool', scr['krT'][:, g0:g0 + G], gB_[:65, 0, :], reads=[gB_])
            bcast_scalar_max(cx, s1, trf, kmx, kmax)
            cx.op('act', [kmax], [kmax], lambda e: e.activation(out=kmax[:, :], in_=kmax[:, :], func=AF.Sqrt))
            for tt in range(NT):
                tl = tt % NG
                p, nTb = load_norm_T(tt, 0, 384, gq, tt)
                c_t = cs_t[tt % 2]
                cx.dma('act', c_t[:, 0:32], cos_d[tt * 128:(tt + 1) * 128, :], writes=[c_t])
                cx.dma('act', c_t[:, 32:64], sin_d[tt * 128:(tt + 1) * 128, :], writes=[c_t])
                pa, pb = pq[tt % 2], pq2[tt % 2]
                for kc in range(3):
                    mm(cx, pa, pa[:, :], nTb[:, kc, :], wuq[:, kc, 0:512], [nTb, wuq], kc == 0, kc == 2)
                for kc in range(3):
                    mm(cx, pb, pb[:, :256], nTb[:, kc, :], wuq[:, kc, 512:768], [nTb, wuq], kc == 0, kc == 2)
                q = qf[tt % 2]
                qv = q[:, :, :].rearrange("p h c -> p (h c)")
                evac(cx, 'act', pa, pa[:, :], q, qv[:, 0:512])
                evac(cx, 'dve', pb, pb[:, :256], q, qv[:, 512:768])
                qh = qv[:, 0:768].rearrange("p (h c) -> p h c", h=4)
                cx.op('dve', [q], [sq], lambda e: e.tensor_tensor(sq[:, :, 0:192], qh, qh, ALU.mult))
                n4b = n4[tt % 2]
                cx.op('dve', [sq], [n4b], lambda e: e.tensor_reduce(out=n4b[:, :], in_=sq[:, :, 0:192], axis=AX.X, op=ALU.add))
                cx.op('act', [n4b], [n4b], lambda e: e.activation(out=n4b[:, :], in_=n4b[:, :], func=AF.Sqrt))
                cx.op('dve', [n4b, kmax], [n4b],
                      lambda e: e.tensor_scalar(n4b[:, :], n4b[:, :], kmax[:, 0:1], -MLA_SCALE, ALU.mult, ALU.mult))
                qsb, qsfb = qs[tt % 2], qsf[tt % 2]
                cb = c_t[:, 0:32].unsqueeze(1).broadcast_to([128, 4, 32])
                sb_ = c_t[:, 32:64].unsqueeze(1).broadcast_to([128, 4, 32])
                rope_tm(cx, q, qh[:, :, 128:160], qh[:, :, 160:192], [c_t, cb], [c_t, sb_],
                        qsfb, qsfb[:, :, 0:32], qsfb[:, :, 32:64], [(t1, t1[:, :, :]), (t2, t2[:, :, :])])
                cx.op('act', [q], [qsb], lambda e: e.activation(out=qsb[:, :, 0:128], in_=qh[:, :, 0:128], func=AF.Copy, scale=MLA_SCALE))
                cx.op('act', [qsfb], [qsb], lambda e: e.activation(out=qsb[:, :, 128:192], in_=qsfb[:, :, :], func=AF.Copy, scale=MLA_SCALE))
                cx.op('pool', [n4b], [qsb], lambda e: e.tensor_copy(qsb[:, :, 192:193], n4b[:, :].unsqueeze(2)))
                trp.transpose_cols(qsb, lambda j: qsb[:, j, 0:128], 4, gA,
                                   lambda j0, cnt: gA[:, j0:j0 + cnt, tl * 128:(tl + 1) * 128])
                trp.transpose_cols(qsb, lambda j: qsb[:, j, 128:193], 4, gB_,
                                   lambda j0, cnt: gB_[:65, j0:j0 + cnt, tl * 128:(tl + 1) * 128], blkw=65)
                if tl == NG - 1:
                    g0 = (tt // NG) * G
                    cx.dma('pool', scr['qnT'][:, :, g0:g0 + G].rearrange("h d s -> d h s"), gA[:, :, :], reads=[gA])
                    cx.dma('pool', scr['qrT'][:, :, g0:g0 + G].rearrange("h d s -> d h s"), gB_[:65, :, :], reads=[gB_])
        cx.barrier()
        res = AttnRes(cx, st, 129)
        cmask = cx.sb(st, [128, 4, 512], BF16, name="cmask")
        cx.dma('sp', cmask[:, :, :], cmask_d.rearrange("i k q -> k i q"), writes=[cmask])
        krT = cx.sb(st, [65, S], BF16, name="krT")
        cx.dma('sp', krT[:, :], scr['krT'][:, :], writes=[krT])
        qn = [cx.sb(st, [128, S], BF16, name="qn") for _ in range(2)]
        qr = [cx.sb(st, [65, S], BF16, name="qr") for _ in range(2)]
        kn = [cx.sb(st, [128, S], BF16, name="kn") for _ in range(2)]
        vv = [cx.sb(st, [128, NT, 129], BF16, name="vv") for _ in range(2)]
        rc = [cx.sb(st, [128, 1], F32, name="rc") for _ in range(2)]
        ot = [cx.sb(st, [128, 128], F32, name="ot") for _ in range(2)]
        cnt = [0]
        QW = min(512, S)
        for h in range(4):
            a, b, c, v = qn[h % 2], qr[h % 2], kn[h % 2], vv[h % 2]
            cx.dma('sp', a[:, :], scr['qnT'][h], writes=[a])
            cx.dma('sp', b[:, :], scr['qrT'][h], writes=[b])
            cx.dma('sp', c[:, :], scr['knT'][h], writes=[c])
            cx.dma('sp', v[:, :, :], scr['va'][:, h, :].rearrange("(t p) c -> p t c", p=128), writes=[v])

            def epi(QB, j, accb, acc_ap, h=h):
                k = cnt[0]
                cnt[0] += 1
                r, o = rc[k % 2], ot[k % 2]
                cx.op('dve', [accb], [r], lambda e: e.tensor_scalar(r[:, :], acc_ap[:, 128:129], 1e-30, None, ALU.add))
                cx.op('dve', [r], [r], lambda e: e.reciprocal(r[:, :], r[:, :]))
                cx.op('act', [accb, r], [o], lambda e: e.activation(out=o[:, :], in_=acc_ap[:, 0:128], func=AF.Copy, scale=r[:, 0:1]))
                t0 = QB * QW + j * 128
                cx.dma('pool', Y_d[t0:t0 + 128, h * 128:(h + 1) * 128], o[:, :], reads=[o])

            attn_core(cx, res, S, [(a, a[:, :]), (b, b[:65, :])], [(c, c[:, :]), (krT, krT[:65, :])],
                      lambda kb, nk, v=v: (v, v[:nk, kb, :]), 129,
                      lambda QB: causal_blocks(QB, QW, cmask), epi)
    cx.barrier()


def head_norm_tm(cx, src_buf, src_ap, n, eps, cbuf, c_ap, s1, s2, junk):
    cx.op('dve', [src_buf], [s1], lambda e: e.tensor_reduce(out=s1[:, 0:1], in_=src_ap, axis=AX.X, op=ALU.add))
    cx.op('dve', [s1], [s1], lambda e: e.tensor_scalar(s1[:, 0:1], s1[:, 0:1], 1.0 / n, None, ALU.mult))
    cx.op('dve', [src_buf, s1], [cbuf], lambda e: e.tensor_scalar(c_ap, src_ap, s1[:, 0:1], None, ALU.subtract))
    rms_rstd(cx, cbuf, c_ap, n, s2, junk, eps=eps)


def phase_ret(cx, S, P_d, cos_d, sin_d, gdec_d, Y_d, scr):
    NT = S // 128
    G = min(512, S)
    NG = G // 128
    QW = min(512, S)
    with contextlib.ExitStack() as st:
        with contextlib.ExitStack() as s1:
            pr = [cx.sb(s1, [128, 1024], F32, name="pr") for _ in range(2)]
            cs_t = [cx.sb(s1, [128, 64], F32, name="cs") for _ in range(2)]
            ro = [cx.sb(s1, [128, 8, 64], F32, name="ro") for _ in range(2)]
            rb = [cx.sb(s1, [128, 8, 64], BF16, name="rb") for _ in range(2)]
            vb = [cx.sb(s1, [128, 512], BF16, name="vb") for _ in range(2)]
            t1 = cx.sb(s1, [128, 8, 32], F32, name="t1")
            t2 = cx.sb(s1, [128, 8, 32], F32, name="t2")
            gQ = cx.sb(s1, [64, 8, G], BF16, name="gQ")
            trp = TrPool(cx, s1)
            for tt in range(NT):
                tl = tt % NG
                rows = slice(tt * 128, (tt + 1) * 128)
                p, c_t, r, rbb, v = pr[tt % 2], cs_t[tt % 2], ro[tt % 2], rb[tt % 2], vb[tt % 2]
                cx.dma('sp', p[:, :], P_d[rows, OFF_RET:OFF_RET + 1024], writes=[p])
                cx.dma('act', c_t[:, 0:32], cos_d[rows, :], writes=[c_t])
                cx.dma('act', c_t[:, 32:64], sin_d[rows, :], writes=[c_t])
                qk = p[:, 0:512].rearrange("p (h c) -> p h c", h=8)
                cb = c_t[:, 0:32].unsqueeze(1).broadcast_to([128, 8, 32])
                sb_ = c_t[:, 32:64].unsqueeze(1).broadcast_to([128, 8, 32])
                rope_tm(cx, p, qk[:, :, 0:32], qk[:, :, 32:64], [c_t, cb], [c_t, sb_],
                        r, r[:, :, 0:32], r[:, :, 32:64], [(t1, t1[:, :, :]), (t2, t2[:, :, :])])
                cx.op('act', [r], [rbb], lambda e: e.copy(rbb[:, 0:4, :], r[:, 0:4, :]))
                cx.op('act', [r], [rbb], lambda e: e.activation(out=rbb[:, 4:8, :], in_=r[:, 4:8, :], func=AF.Copy, scale=0.125))
                cx.op('pool', [p], [v], lambda e: e.tensor_copy(v[:, :], p[:, 512:1024]))
                trp.transpose_cols(rbb, lambda j: rbb[:, j, :], 8, gQ,
                                   lambda j0, cnt: gQ[:64, j0:j0 + cnt, tl * 128:(tl + 1) * 128], blkw=64)
                cx.dma('pool', scr['rv'][rows, :, :].rearrange("s h c -> s (h c)"), v[:, :], reads=[v])
                if tl == NG - 1:
                    g0 = (tt // NG) * G
                    cx.dma('pool', scr['rqT'][:, :, g0:g0 + G].rearrange("h d s -> d h s"), gQ[:64, 0:4, :], reads=[gQ])
                    cx.dma('pool', scr['rkT'][:, :, g0:g0 + G].rearrange("h d s -> d h s"), gQ[:64, 4:8, :], reads=[gQ])
        cx.barrier()
        res = AttnRes(cx, st, 128)
        gd = [cx.sb(st, [128, 5, 512], F32, name="gd") for _ in range(2)]
        qT = [cx.sb(st, [64, S], BF16, name="qT") for _ in range(2)]
        kT = [cx.sb(st, [64, S], BF16, name="kT") for _ in range(2)]
        vv = [cx.sb(st, [128, NT, 128], BF16, name="vv") for _ in range(2)]
        gt = [cx.sb(st, [128, 128], F32, name="gt") for _ in range(2)]
        cb_ = [cx.sb(st, [128, 128], F32, name="cb") for _ in range(2)]
        ot = [cx.sb(st, [128, 128], F32, name="ot") for _ in range(2)]
        sA = [cx.sb(st, [128, 1], F32, name="sA") for _ in range(2)]
        sB = [cx.sb(st, [128, 1], F32, name="sB") for _ in range(2)]
        junk = cx.sb(st, [128, 128], BF16, name="junk")
        cnt = [0]
        for h in range(4):
            gamma = 1.0 - 2.0 ** (-5 - h)
            a, c, v, g = qT[h % 2], kT[h % 2], vv[h % 2], gd[h % 2]
            cx.dma('sp', a[:, :], scr['rqT'][h], writes=[a])
            cx.dma('sp', c[:, :], scr['rkT'][h], writes=[c])
            cx.dma('sp', v[:, :, :], scr['rv'][:, h, :].rearrange("(t p) c -> p t c", p=128), writes=[v])
            cx.dma('sp', g[:, :, :], gdec_d[h].rearrange("i k q -> k i q"), writes=[g])
            if 'dbg' in scr and h == 0:
                cx.dma('sp', scr['dbg'][0], a[:, :], reads=[a])
                cx.dma('sp', scr['dbg'][1], c[:, :], reads=[c])

            def blocks(QB, g=g, gamma=gamma):
                out = []
                nd = QW // 128
                for kb in range(nd * (QB + 1)):
                    i = kb - nd * QB
                    if i >= 0:
                        out.append(dict(kb=kb, k0=kb * 128, nk=128, decay=(1.0, g, g[:, 1 + i, :QW])))
                    else:
                        cc = gamma ** (QB * QW - kb * 128)
                        if cc < 1e-30:
                            cc = 0.0
                        out.append(dict(kb=kb, k0=kb * 128, nk=128, decay=(cc, g, g[:, 0, :QW])))
                return out

            def epi(QB, j, accb, acc_ap, h=h):
                k = cnt[0]
                cnt[0] += 1
                t0 = QB * QW + j * 128
                gtb, cbb, o, s_a, s_b = gt[k % 2], cb_[k % 2], ot[k % 2], sA[k % 2], sB[k % 2]
                cx.dma('act', gtb[:, :], P_d[t0:t0 + 128, OFF_RET + 1024 + h * 128:OFF_RET + 1024 + (h + 1) * 128], writes=[gtb])
                cx.op('act', [gtb], [gtb], lambda e: e.activation(out=gtb[:, :], in_=gtb[:, :], func=AF.Silu))
                head_norm_tm(cx, accb, acc_ap, 128, NORM_EPS, cbb, cbb[:, :], s_a, s_b, junk)
                cx.op('dve', [cbb, s_b, gtb], [o],
                      lambda e: e.scalar_tensor_tensor(out=o[:, :], in0=cbb[:, :], scalar=s_b[:, 0:1], in1=gtb[:, :],
                                                       op0=ALU.mult, op1=ALU.mult))
                cx.dma('pool', Y_d[t0:t0 + 128, h * 128:(h + 1) * 128], o[:, :], reads=[o])

            attn_core(cx, res, S, [(a, a[:64, :])], [(c, c[:64, :])], lambda kb, nk, v=v: (v, v[:nk, kb, :]), 128,
                      blocks, epi, mode='decay')
    cx.barrier()


def bc8(ap):
    return ap.unsqueeze(2).broadcast_to([128, 8, 64])


def v3(ap):
    return ap.rearrange("p (h c) -> p h c", h=8)


def phase_rwkv(cx, S, l, P_d, W, Wb, C, Y_d, scr):
    NT = S // 128
    RW = scr['rw']
    names6 = ['rr', 'lw', 'k2', 'vv', 'kn', 'aa']
    with contextlib.ExitStack() as st:
        def brow(name, n, src):
            b = cx.sb(st, [128, n], F32, name=name)
            load_bcast_row(cx, 'sp', b, src, n)
            return b
        muB = brow("muB", 1984, W['rwkv_mu'][l])
        w0B = brow("w0B", 512, W['rwkv_w0'][l])
        a0B = brow("a0B", 512, W['rwkv_a0'][l])
        kkB = brow("kkB", 512, W['rwkv_k_k'][l])
        kaB = brow("kaB", 512, W['rwkv_k_a'][l])
        rkB = brow("rkB", 512, W['rwkv_r_k'][l].rearrange("h c -> (h c)"))
        w2 = cx.sb(st, [96, 512], BF16, name="w2")
        a2 = cx.sb(st, [96, 512], BF16, name="a2")
        g2 = cx.sb(st, [128, 2, 512], BF16, name="g2")
        cx.dma('sp', w2[:, :], Wb['rwkv_w2'][l], writes=[w2])
        cx.dma('sp', a2[:, :], Wb['rwkv_a2'][l], writes=[a2])
        cx.dma('sp', g2[:, :, :], Wb['rwkv_g2'][l].rearrange("(kc p) n -> p kc n", p=128), writes=[g2])
        z = [cx.sb(st, [128, 1984], F32, name="z") for _ in range(2)]
        zp = [cx.sb(st, [128, 1984], F32, name="zp") for _ in range(2)]
        lo = [cx.sb(st, [128, 512], BF16, name="lo") for _ in range(2)]
        loT = [cx.sb(st, [128, 4, 128], BF16, name="loT") for _ in range(2)]
        o7 = [cx.sb(st, [128, 7, 512], F32, name="o7") for _ in range(2)]
        s8 = [cx.sb(st, [128, 8], F32, name="s8") for _ in range(2)]
        b8 = [cx.sb(st, [128, 8], F32, name="b8") for _ in range(2)]
        trp = TrPool(cx, st)
        pp = [cx.ps(st, [128, 512], F32, name="pp") for _ in range(3)]
        for tt in range(NT):
            rows = slice(tt * 128, (tt + 1) * 128)
            zb, zpb, lob, loTb, o, s8b, b8b = z[tt % 2], zp[tt % 2], lo[tt % 2], loT[tt % 2], o7[tt % 2], s8[tt % 2], b8[tt % 2]
            cx.dma('sp', zb[:, :], P_d[rows, OFF_RWKV:OFF_RWKV + 1984], writes=[zb])
            if tt == 0:
                cx.op('pool', [], [zpb], lambda e: e.memset(zpb[0:1, :], 0.0))
                cx.dma('act', zpb[1:128, :], P_d[0:127, OFF_RWKV:OFF_RWKV + 1984], writes=[zpb])
            else:
                cx.dma('act', zpb[:, :], P_d[tt * 128 - 1:tt * 128 + 127, OFF_RWKV:OFF_RWKV + 1984], writes=[zpb])
            cx.op('pool', [zpb, zb], [zpb], lambda e: e.tensor_tensor(zpb[:, :], zpb[:, :], zb[:, :], ALU.subtract))
            cx.op('dve', [zpb, muB], [zpb], lambda e: e.tensor_tensor(zpb[:, :], zpb[:, :], muB[:, :], ALU.mult))
            cx.op('pool', [zpb, zb], [zb], lambda e: e.tensor_tensor(zb[:, :], zb[:, :], zpb[:, :], ALU.add))
            r_, k_, v_ = zb[:, 0:512], zb[:, 512:1024], zb[:, 1024:1536]
            cx.op('act', [zb], [lob], lambda e: e.activation(out=lob[:, 0:96], in_=zb[:, 1536:1632], func=AF.Tanh))
            cx.op('act', [zb], [lob], lambda e: e.copy(lob[:, 128:224], zb[:, 1632:1728]))
            cx.op('act', [zb], [lob], lambda e: e.activation(out=lob[:, 256:512], in_=zb[:, 1728:1984], func=AF.Sigmoid))
            trp.transpose_cols(lob, lambda j: lob[:, j * 128:j * 128 + 96], 2, loTb,
                               lambda j0, cnt: loTb[:96, j0:j0 + cnt, :], blkw=96)
            trp.transpose_cols(lob, lambda j: lob[:, 256 + j * 128:384 + j * 128], 2, loTb,
                               lambda j0, cnt: loTb[:, 2 + j0:2 + j0 + cnt, :])
            pu, pa, pg = pp
            mm(cx, pu, pu[:, :], loTb[:96, 0, :], w2[:96, :], [loTb, w2], True, True)
            mm(cx, pa, pa[:, :], loTb[:96, 1, :], a2[:96, :], [loTb, a2], True, True)
            mm(cx, pg, pg[:, :], loTb[:, 2, :], g2[:, 0, :], [loTb, g2], True, False)
            mm(cx, pg, pg[:, :], loTb[:, 3, :], g2[:, 1, :], [loTb, g2], False, True)
            lw_, k2_, kn_, aa_, gg_, t1_, t2_ = [o[:, i, :] for i in range(7)]
            cx.op('dve', [pu, w0B], [o], lambda e: e.tensor_tensor(t1_, pu[:, :], w0B[:, :], ALU.add))
            cx.op('act', [o], [o], lambda e: e.activation(out=t1_, in_=t1_, func=AF.Sigmoid))
            cx.op('pool', [o], [o], lambda e: e.tensor_scalar(lw_, t1_, -0.6065306597126334, None, ALU.mult))
            cx.op('dve', [pa, a0B], [o], lambda e: e.tensor_tensor(t2_, pa[:, :], a0B[:, :], ALU.add))
            cx.op('act', [o], [o], lambda e: e.activation(out=aa_, in_=t2_, func=AF.Sigmoid))
            cx.op('act', [pg], [o], lambda e: e.copy(gg_, pg[:, :]))
            cx.op('dve', [zb, kkB], [o], lambda e: e.tensor_tensor(kn_, k_, kkB[:, :], ALU.mult))
            cx.op('pool', [o], [o], lambda e: e.tensor_tensor(t1_, kn_, kn_, ALU.mult))
            cx.op('dve', [o], [s8b], lambda e: e.tensor_reduce(out=s8b[:, :], in_=v3(t1_), axis=AX.X, op=ALU.add))
            cx.op('dve', [s8b], [s8b], lambda e: e.tensor_scalar(s8b[:, :], s8b[:, :], 1e-24, None, ALU.max))
            cx.op('pool', [s8b, cx.neghalf], [s8b],
                  lambda e: e.tensor_tensor(s8b[:, :], s8b[:, :], cx.neghalf[:, 0:1].to_broadcast([128, 8]), ALU.pow))
            cx.op('dve', [o, s8b], [o], lambda e: e.tensor_tensor(v3(kn_), v3(kn_), bc8(s8b[:, :]), ALU.mult))
            cx.op('dve', [o, kaB], [o],
                  lambda e: e.scalar_tensor_tensor(out=t2_, in0=aa_, scalar=-1.0, in1=kaB[:, :], op0=ALU.add, op1=ALU.mult))
            cx.op('pool', [o], [o], lambda e: e.tensor_scalar(t2_, t2_, 1.0, None, ALU.add))
            cx.op('dve', [o, zb], [o], lambda e: e.tensor_tensor(k2_, k_, t2_, ALU.mult))
            cx.op('pool', [o, zb], [o], lambda e: e.tensor_tensor(t1_, r_, k2_, ALU.mult))
            cx.op('dve', [o, rkB], [o], lambda e: e.tensor_tensor(t1_, t1_, rkB[:, :], ALU.mult))
            cx.op('dve', [o], [b8b], lambda e: e.tensor_reduce(out=b8b[:, :], in_=v3(t1_), axis=AX.X, op=ALU.add))
            cx.dma('pool', RW['rr'][rows, :], r_, reads=[zb])
            cx.dma('pool', RW['vv'][rows, :], v_, reads=[zb])
            cx.dma('pool', RW['lw'][rows, :], lw_, reads=[o])
            cx.dma('pool', RW['k2'][rows, :], k2_, reads=[o])
            cx.dma('pool', RW['kn'][rows, :], kn_, reads=[o])
            cx.dma('pool', RW['aa'][rows, :], aa_, reads=[o])
            cx.dma('pool', RW['gg'][rows, :], gg_, reads=[o])
            cx.dma('pool', RW['bc'][rows, :], b8b[:, :], reads=[b8b])
    cx.barrier()
    with contextlib.ExitStack() as st:
        rwm = cx.sb(st, [128, 384], F32, name="rwm")
        cx.dma('sp', rwm[:, :], C['rwm'][:, :], writes=[rwm])
        mask4 = cx.sb(st, [128, 512], F32, name="mask4")
        cx.op('pool', [rwm], [mask4], lambda e: e.tensor_copy(mask4[:, 0:256], rwm[:, 0:256]))
        cx.op('pool', [rwm], [mask4], lambda e: e.tensor_copy(mask4[:, 256:512], rwm[:, 0:256]))
        gwB = cx.sb(st, [128, 512], F32, name="gwB")
        gbB = cx.sb(st, [128, 512], F32, name="gbB")
        load_bcast_row(cx, 'sp', gwB, W['rwkv_gn_w'][l], 512)
        load_bcast_row(cx, 'sp', gbB, W['rwkv_gn_b'][l], 512)
        IN = [cx.sb(st, [128, 6, 512], F32, name="IN") for _ in range(2)]
        EL = cx.sb(st, [128, 3, 512], F32, name="EL")
        TM = cx.sb(st, [128, 4, 512], F32, name="TM")
        XT = cx.sb(st, [64, 8, 4, 128], F32, name="XT")
        MM_ = cx.sb(st, [128, 8, 512], F32, name="MM")
        XX = [cx.sb(st, [128, 8, 2, 128], F32, name="XX") for _ in range(2)]
        NTb = cx.sb(st, [128, 8, 128], F32, name="NT")
        ST = cx.sb(st, [64, 8, 64], F32, name="ST")
        STs = cx.sb(st, [64, 8, 64], F32, name="STs")
        pc = cx.sb(st, [64, 8], F32, name="pc")
        Yb = cx.sb(st, [128, 8, 64], F32, name="Yb")
        Ub = cx.sb(st, [128, 8, 64], F32, name="Ub")
        Ob = cx.sb(st, [128, 512], F32, name="Ob")
        G3 = [cx.sb(st, [128, 512], F32, name="G3") for _ in range(2)]
        b8 = [cx.sb(st, [128, 8], F32, name="b8") for _ in range(2)]
        m8 = cx.sb(st, [128, 8], F32, name="m8")
        r8 = cx.sb(st, [128, 8], F32, name="r8")
        t512 = cx.sb(st, [128, 512], F32, name="t512")
        yo = [cx.sb(st, [128, 512], F32, name="yo") for _ in range(2)]
        ptr = [cx.ps(st, [128, 512], F32, name="ptr") for _ in range(2)]
        pA = cx.ps(st, [128, 512], F32, name="pA")
        pD = [cx.ps(st, [128, 4, 128], F32, name="pD") for _ in range(2)]
        pY = cx.ps(st, [128, 512], F32, name="pY")
        pU = cx.ps(st, [128, 512], F32, name="pU")
        pO = cx.ps(st, [128, 512], F32, name="pO")
        cx.op('pool', [], [ST], lambda e: e.memset(ST[:, :, :], 0.0))
        MUs, MUi, MLs = rwm[:, 0:128], rwm[:, 128:256], rwm[:, 256:384]
        for c in range(NT):
            rows = slice(c * 128, (c + 1) * 128)
            I6 = IN[c % 2]
            for i, nm in enumerate(names6):
                cx.dma('sp' if i % 2 == 0 else 'act', I6[:, i, :], RW[nm][rows, :], writes=[I6])
            rr, lw, k2, vv, kn, aa = [I6[:, i, :] for i in range(6)]
            pL = ptr[0]
            mm(cx, pL, pL[:, :], MUi, lw, [rwm, I6], True, True)
            cx.op('act', [pL], [EL], lambda e: e.activation(out=EL[:, 0, :], in_=pL[:, :], func=AF.Exp))
            cx.op('act', [pL], [EL], lambda e: e.activation(out=EL[:, 1, :], in_=pL[:, :], func=AF.Exp, scale=-1.0))
            cx.op('dve', [pL, I6], [EL], lambda e: e.tensor_tensor(EL[:, 2, :], pL[:, :], lw, ALU.subtract))
            cx.op('act', [EL], [EL], lambda e: e.activation(out=EL[:, 2, :], in_=EL[:, 2, :], func=AF.Exp))
            cx.op('dve', [I6, EL], [TM],
                  lambda e: e.scalar_tensor_tensor(out=TM[:, 0, :], in0=kn, scalar=-1.0, in1=EL[:, 2, :], op0=ALU.mult, op1=ALU.mult))
            cx.op('pool', [I6, EL], [TM], lambda e: e.tensor_tensor(TM[:, 1, :], rr, EL[:, 0, :], ALU.mult))
            cx.op('dve', [I6], [TM], lambda e: e.tensor_tensor(TM[:, 2, :], kn, aa, ALU.mult))
            cx.op('dve', [TM, EL], [TM], lambda e: e.tensor_tensor(TM[:, 2, :], TM[:, 2, :], EL[:, 1, :], ALU.mult))
            cx.op('pool', [I6, EL], [TM], lambda e: e.tensor_tensor(TM[:, 3, :], k2, EL[:, 1, :], ALU.mult))
            ppc = ptr[1]
            for h in range(8):
                mm(cx, ppc, ppc[:64, h:h + 1], lw[:, h * 64:(h + 1) * 64], cx.ones_f[:, 0:1], [I6, cx.ones_f], True, True)
            cx.op('act', [ppc], [pc], lambda e: e.activation(out=pc[:, :], in_=ppc[:64, 0:8], func=AF.Exp))
            cx.op('pool', [ST, pc], [STs],
                  lambda e: e.tensor_tensor(STs[:, :, :], ST[:, :, :], pc[:, :].unsqueeze(2).broadcast_to([64, 8, 64]), ALU.mult))
            k = 0
            for q in range(4):
                for hh in range(2):
                    pt = ptr[k % 2]
                    k += 1
                    for j in range(4):
                        h = hh * 4 + j
                        transp(cx, pt, pt[:64, j * 128:(j + 1) * 128], TM[:, q, h * 64:(h + 1) * 64], cx.identf[:, :], [TM, cx.identf])
                    evac(cx, 'act' if k % 2 else 'dve', pt, pt[:64, :].rearrange("p (j t) -> p j t", j=4), XT,
                         fr(XT[:, hh * 4:(hh + 1) * 4, q, :]))
            for h in range(8):
                ar = XT[:, h, 0:2, :].rearrange("p q t -> p (q t)")
                mm(cx, pA, pA[:, 0:256], fr(XT[:, h, 2, :]), fr(ar), [XT], True, True)
                mm(cx, pA, pA[:, 256:512], fr(XT[:, h, 3, :]), fr(ar), [XT], True, True)
                pd = pD[h % 2]
                mm(cx, pd, pd[:, 0, :], fr(XT[:, h, 0, :]), fr(XT[:, h, 2, :]), [XT], True, True)
                cx.op('dve', [pA, mask4], [MM_], lambda e: e.tensor_tensor(MM_[:, h, :], pA[:, :], mask4[:, :], ALU.mult))
                cx.op('dve', [pd, rwm], [XX[0]], lambda e: e.tensor_tensor(fr(XX[0][:, h, 1, :]), pd[:, 0, :], MLs, ALU.mult))
                cx.op('pool', [MM_], [XX[0]], lambda e: e.tensor_copy(fr(XX[0][:, h, 0, :]), MM_[:, h, 0:128]))
                cx.op('pool', [MM_, cx.identf], [NTb], lambda e: e.tensor_tensor(fr(NTb[:, h, :]), MM_[:, h, 0:128], cx.identf[:, :], ALU.add))
            for lev in range(1, 7):
                cur, nxt = XX[(lev - 1) % 2], XX[lev % 2]
                for p in range(4):
                    pd = pD[p % 2]
                    for j in range(2):
                        h = 2 * p + j
                        if lev < 6:
                            mm(cx, pd, pd[:, 2 * j, :], fr(cur[:, h, 1, :]), fr(cur[:, h, 0, :]), [cur], True, True)
                        mm(cx, pd, pd[:, 2 * j + 1, :], fr(cur[:, h, 0, :]), fr(cur[:, h, 1, :]), [cur], True, True)
                    if lev < 6:
                        evac(cx, 'act' if p % 2 else 'dve', pd, pd[:, :, :], nxt,
                             fr(nxt[:, 2 * p:2 * p + 2, :, :].rearrange("p h q t -> p (h q) t")))
                    else:
                        for j in range(2):
                            evac(cx, 'act' if j else 'dve', pd, pd[:, 2 * j + 1, :], nxt, fr(nxt[:, 2 * p + j, 1, :]))
                for p in range(4):
                    pd = pD[p % 2]
                    for j in range(2):
                        h = 2 * p + j
                        mm(cx, pd, pd[:, j, :], fr(nxt[:, h, 1, :]), fr(NTb[:, h, :]), [nxt, NTb], True, True)
                    cx.op('dve', [pd, NTb], [NTb],
                          lambda e: e.tensor_tensor(fr(NTb[:, 2 * p:2 * p + 2, :]), NTb[:, 2 * p:2 * p + 2, :], pd[:, 0:2, :], ALU.add))
            for h in range(8):
                hs = slice(h * 64, (h + 1) * 64)
                mm(cx, pY, pY[:, hs], XT[:, h, 0, :], ST[:, h, :], [XT, ST], True, False)
                mm(cx, pY, pY[:, hs], MM_[:, h, 256:384], vv[:, hs], [MM_, I6], False, True)
            evac(cx, 'dve', pY, pY[:, 0:256], Yb, Yb[:, 0:4, :].rearrange("p h c -> p (h c)"))
            evac(cx, 'act', pY, pY[:, 256:512], Yb, Yb[:, 4:8, :].rearrange("p h c -> p (h c)"))
            for h in range(8):
                hs = slice(h * 64, (h + 1) * 64)
                mm(cx, pU, pU[:, hs], NTb[:, h, :], Yb[:, h, :], [NTb, Yb], True, True)
            evac(cx, 'dve', pU, pU[:, 0:256], Ub, Ub[:, 0:4, :].rearrange("p h c -> p (h c)"))
            evac(cx, 'act', pU, pU[:, 256:512], Ub, Ub[:, 4:8, :].rearrange("p h c -> p (h c)"))
            for h in range(8):
                hs = slice(h * 64, (h + 1) * 64)
                mm(cx, pY, pY[:64, hs], TM[:, 2, hs], Ub[:, h, :], [TM, Ub], True, False)
                mm(cx, pY, pY[:64, hs], TM[:, 3, hs], vv[:, hs], [TM, I6], False, True)
            for h in range(8):
                hs = slice(h * 64, (h + 1) * 64)
                mm(cx, pO, pO[:, hs], XT[:, h, 1, :], ST[:, h, :], [XT, ST], True, False)
                mm(cx, pO, pO[:, hs], MM_[:, h, 128:256], Ub[:, h, :], [MM_, Ub], False, False)
                mm(cx, pO, pO[:, hs], MM_[:, h, 384:512], vv[:, hs], [MM_, I6], False, True)
            cx.op('dve', [pY, pc], [ST],
                  lambda e: e.tensor_tensor(ST[:, :, :], pY[:64, :].rearrange("p (h c) -> p h c", h=8),
                                            pc[:, :].unsqueeze(2).broadcast_to([64, 8, 64]), ALU.mult))
            cx.op('dve', [ST, STs], [ST], lambda e: e.tensor_tensor(ST[:, :, :], ST[:, :, :], STs[:, :, :], ALU.add))
            evac(cx, 'act', pO, pO[:, :], Ob, Ob[:, :])
            g3, b8b, y = G3[c % 2], b8[c % 2], yo[c % 2]
            cx.dma('sp', g3[:, :], RW['gg'][rows, :], writes=[g3])
            cx.dma('act', b8b[:, :], RW['bc'][rows, :], writes=[b8b])
            cx.op('dve', [Ob], [m8], lambda e: e.tensor_reduce(out=m8[:, :], in_=v3(Ob[:, :]), axis=AX.X, op=ALU.add))
            cx.op('dve', [m8], [m8], lambda e: e.tensor_scalar(m8[:, :], m8[:, :], 1.0 / 64, None, ALU.mult))
            cx.op('dve', [Ob, m8], [Ob], lambda e: e.tensor_tensor(v3(Ob[:, :]), v3(Ob[:, :]), bc8(m8[:, :]), ALU.subtract))
            cx.op('pool', [Ob], [t512], lambda e: e.tensor_tensor(t512[:, :], Ob[:, :], Ob[:, :], ALU.mult))
            cx.op('dve', [t512], [r8], lambda e: e.tensor_reduce(out=r8[:, :], in_=v3(t512[:, :]), axis=AX.X, op=ALU.add))
            cx.op('dve', [r8], [r8], lambda e: e.tensor_scalar(r8[:, :], r8[:, :], 1.0 / 64, 64e-5, ALU.mult, ALU.add))
            cx.op('pool', [r8, cx.neghalf], [r8],
                  lambda e: e.tensor_tensor(r8[:, :], r8[:, :], cx.neghalf[:, 0:1].to_broadcast([128, 8]), ALU.pow))
            cx.op('dve', [Ob, r8], [y], lambda e: e.tensor_tensor(v3(y[:, :]), v3(Ob[:, :]), bc8(r8[:, :]), ALU.mult))
            cx.op('pool', [y, gwB], [y], lambda e: e.tensor_tensor(y[:, :], y[:, :], gwB[:, :], ALU.mult))
            cx.op('pool', [y, gbB], [y], lambda e: e.tensor_tensor(y[:, :], y[:, :], gbB[:, :], ALU.add))
            cx.op('dve', [I6, b8b], [t512], lambda e: e.tensor_tensor(v3(t512[:, :]), v3(vv), bc8(b8b[:, :]), ALU.mult))
            cx.op('pool', [y, t512], [y], lambda e: e.tensor_tensor(y[:, :], y[:, :], t512[:, :], ALU.add))
            cx.op('dve', [y, g3], [y], lambda e: e.tensor_tensor(y[:, :], y[:, :], g3[:, :], ALU.mult))
            cx.dma('pool', Y_d[rows, :], y[:, :], reads=[y])
    cx.barrier()


NSA_SCALE = 128 ** -0.5
NSA_BIG = 30000.0
NSA_STOP = 0


def phase_nsa(cx, S, l, P_d, W, Wb, C, Youts, scr):
    NT = S // 128
    G = min(512, S)
    NG = G // 128
    QW = min(512, S)
    NJ = QW // 128
    Nc = (S - 32) // 16 + 1
    NKB = (Nc + 127) // 128
    N = scr['nsa']
    with contextlib.ExitStack() as st:
        pn = [cx.sb(st, [128, 1292], F32, name="pn") for _ in range(2)]
        cs_t = [cx.sb(st, [128, 32], F32, name="cs") for _ in range(2)]
        ro = [cx.sb(st, [128, 10, 32], F32, name="ro") for _ in range(2)]
        t1 = cx.sb(st, [128, 10, 16], F32, name="t1")
        t2 = cx.sb(st, [128, 10, 16], F32, name="t2")
        fb = [cx.sb(st, [128, 8, 128], BF16, name="fb") for _ in range(2)]
        va = [cx.sb(st, [128, 2, 129], BF16, name="va") for _ in range(2)]
        sq = cx.sb(st, [128, 6, 128], F32, name="sq")
        n6 = [cx.sb(st, [128, 6], F32, name="n6") for _ in range(2)]
        nqb = [cx.sb(st, [128, 4], BF16, name="nqb") for _ in range(2)]
        gt = [cx.sb(st, [128, 12], F32, name="gt") for _ in range(2)]
        kmx = cx.sb(st, [128, 2], F32, name="kmx")
        gA = cx.sb(st, [128, 8, G], BF16, name="gA")
        gN = cx.sb(st, [1, 4, G], BF16, name="gN")
        trp = TrPool(cx, st)
        trf = TrPool(cx, st, n=1, dtype=F32)
        cx.op('pool', [], [kmx], lambda e: e.memset(kmx[:, :], 0.0))
        for tt in range(NT):
            tl = tt % NG
            rows = slice(tt * 128, (tt + 1) * 128)
            p, c_t, r, f, v, n6b, nq_, g_ = pn[tt % 2], cs_t[tt % 2], ro[tt % 2], fb[tt % 2], va[tt % 2], n6[tt % 2], nqb[tt % 2], gt[tt % 2]
            cx.dma('sp', p[:, :], P_d[rows, OFF_NSA:OFF_NSA + 1292], writes=[p])
            cx.dma('act', c_t[:, 0:16], C['nsa_cos'][rows, :], writes=[c_t])
            cx.dma('act', c_t[:, 16:32], C['nsa_sin'][rows, :], writes=[c_t])
            blk = p[:, 0:1280].rearrange("p (b c) -> p b c", b=10)
            cb = c_t[:, 0:16].unsqueeze(1).broadcast_to([128, 10, 16])
            sb_ = c_t[:, 16:32].unsqueeze(1).broadcast_to([128, 10, 16])
            rope_tm(cx, p, blk[:, :, 0:16], blk[:, :, 16:32], [c_t, cb], [c_t, sb_],
                    r, r[:, :, 0:16], r[:, :, 16:32], [(t1, t1[:, :, :]), (t2, t2[:, :, :])])
            cx.op('act', [p], [f], lambda e: e.activation(out=f[:, 0:4, 32:128], in_=blk[:, 0:4, 32:128], func=AF.Copy, scale=NSA_SCALE))
            cx.op('act', [r], [f], lambda e: e.activation(out=f[:, 0:4, 0:32], in_=r[:, 0:4, :], func=AF.Copy, scale=NSA_SCALE))
            for dst, src in ((4, 4), (6, 6), (7, 8)):
                cx.op('pool', [p], [f], lambda e: e.tensor_copy(f[:, dst, 32:128], blk[:, src, 32:128]))
                cx.op('pool', [r], [f], lambda e: e.tensor_copy(f[:, dst, 0:32], r[:, src, :]))
            cx.op('pool', [p], [f], lambda e: e.tensor_copy(f[:, 5, :], blk[:, 5, :]))
            cx.op('pool', [p], [v], lambda e: e.tensor_copy(v[:, 0, 0:128], blk[:, 7, :]))
            cx.op('pool', [p], [v], lambda e: e.tensor_copy(v[:, 1, 0:128], blk[:, 9, :]))
            cx.op('pool', [], [v], lambda e: e.memset(v[:, :, 128:129], 1.0))
            cx.op('dve', [p], [sq], lambda e: e.tensor_tensor(sq[:, 0:4, :], blk[:, 0:4, :], blk[:, 0:4, :], ALU.mult))
            cx.op('dve', [p], [sq], lambda e: e.tensor_tensor(sq[:, 4, :], blk[:, 6, :], blk[:, 6, :], ALU.mult))
            cx.op('dve', [p], [sq], lambda e: e.tensor_tensor(sq[:, 5, :], blk[:, 8, :], blk[:, 8, :], ALU.mult))
            cx.op('dve', [sq], [n6b], lambda e: e.tensor_reduce(out=n6b[:, :], in_=sq[:, :, :], axis=AX.X, op=ALU.add))
            cx.op('dve', [n6b, kmx], [kmx], lambda e: e.tensor_tensor(kmx[:, :], kmx[:, :], n6b[:, 4:6], ALU.max))
            cx.op('act', [n6b], [n6b], lambda e: e.activation(out=n6b[:, 0:4], in_=n6b[:, 0:4], func=AF.Sqrt))
            cx.op('dve', [n6b], [nq_], lambda e: e.tensor_scalar(nq_[:, :], n6b[:, 0:4], -NSA_SCALE, None, ALU.mult))
            cx.dma('sp', g_[:, :], P_d[rows, OFF_NSA + 1280:OFF_NSA + 1292], writes=[g_])
            cx.op('act', [g_], [g_], lambda e: e.activation(out=g_[:, :], in_=g_[:, :], func=AF.Sigmoid))
            cx.dma('pool', N['ng'][rows, :], g_[:, :], reads=[g_])
            trp.transpose_cols(f, lambda j: f[:, j, :], 8, gA, lambda j0, cnt: gA[:, j0:j0 + cnt, tl * 128:(tl + 1) * 128])
            trp.transpose_cols(nq_, lambda j: nq_[:, j:j + 1], 4, gN,
                               lambda j0, cnt: gN[0:1, j0:j0 + cnt, tl * 128:(tl + 1) * 128], blkw=1)
            cx.dma('pool', N['vsa'][rows, :], v[:, 0, :], reads=[v])
            cx.dma('pool', N['vwa'][rows, :], v[:, 1, :], reads=[v])
            if tl == NG - 1:
                g0 = (tt // NG) * G
                cx.dma('pool', N['qT'][:, :, g0:g0 + G].rearrange("h d s -> d h s"), gA[:, 0:4, :], reads=[gA])
                for j, nm in ((4, 'kcT'), (5, 'vcT'), (6, 'ksT'), (7, 'kwT')):
                    cx.dma('pool', N[nm][:, g0:g0 + G], gA[:, j, :], reads=[gA])
                cx.dma('pool', N['nq'][:, g0:g0 + G].rearrange("(o h) s -> o h s", o=1), gN[0:1, :, :], reads=[gN])
        kms = cx.sb(st, [128, 1], F32, name="kms")
        kmw = cx.sb(st, [128, 1], F32, name="kmw")
        k1 = cx.sb(st, [128, 1], F32, name="k1")
        rowsb = cx.sb(st, [1, 2, 128], BF16, name="rowsb")
        for i, dstc in enumerate((kms, kmw)):
            cx.op('pool', [kmx], [k1], lambda e: e.tensor_copy(k1[:, :], kmx[:, i:i + 1]))
            bcast_scalar_max(cx, st, trf, k1, dstc)
            cx.op('act', [dstc], [dstc], lambda e: e.activation(out=dstc[:, :], in_=dstc[:, :], func=AF.Sqrt))
            cx.op('dve', [dstc], [rowsb], lambda e: e.tensor_copy(rowsb[0:1, i, :], dstc[0:1, 0:1].to_broadcast([1, 128])))
        cx.dma('pool', N['krow'].rearrange("a b -> (a b)").rearrange("(o n) -> o n", o=1), rowsb[0:1, :, :].rearrange("o a b -> o (a b)"), reads=[rowsb])
    cx.barrier()
    if NSA_STOP == 1:
        return
    with contextlib.ExitStack() as st:
        KC = cx.sb(st, [128, 256], BF16, name="KC")
        VCA = cx.sb(st, [128, 2, 193], BF16, name="VCA")
        krow = cx.sb(st, [128, 3, 128], BF16, name="krow")
        cx.op('pool', [], [krow], lambda e: e.memset(krow[:, :, :], 0.0))
        cx.dma('sp', krow[0:1, 0:2, :].rearrange("o a b -> o (a b)"), N['krow'].rearrange("a b -> (a b)").rearrange("(o n) -> o n", o=1), writes=[krow])
        cx.dma('sp', VCA[:, :, 129:193], C['cover'].rearrange("(kb p) j -> p kb j", p=128), writes=[VCA])
        cx.op('pool', [], [VCA], lambda e: e.memset(VCA[:, :, 0:129], 0.0))
        cx.op('pool', [], [VCA], lambda e: e.memset(VCA[:, :, 128:129], 1.0))
        cx.op('pool', [], [KC], lambda e: e.memset(KC[:, :], 0.0))
        with contextlib.ExitStack() as s2:
            pm = [cx.ps(s2, [128, 512], F32, name="pm") for _ in range(2)]
            xT = [cx.sb(s2, [128, S], BF16, name="xT") for _ in range(2)]
            cx.dma('sp', xT[0][:, :], N['kcT'][:, :], writes=[xT[0]])
            cx.dma('act', xT[1][:, :], N['vcT'][:, :], writes=[xT[1]])
            w1 = [cx.sb(s2, [128, 32, 128], BF16, name="w1") for _ in range(2)]
            w2 = [cx.sb(s2, [128, 128], BF16, name="w2") for _ in range(2)]
            posf = cx.sb(s2, [32, 2, 128], F32, name="posf")
            posb = cx.sb(s2, [32, 2, 128], BF16, name="posb")
            posT = cx.sb(s2, [128, 2, 32], BF16, name="posT")
            bias = cx.sb(s2, [128, 2], F32, name="bias")
            xs = cx.sb(s2, [128, 256], F32, name="xs")
            x2 = cx.sb(s2, [128, 256], F32, name="x2")
            hid = [cx.sb(s2, [128, 256], BF16, name="hid") for _ in range(2)]
            ksq = cx.sb(s2, [128, 256], BF16, name="ksq")
            one = cx.sb(s2, [1, 2], F32, name="one")
            trp = TrPool(cx, s2, n=1)
            for z in range(2):
                cx.dma('sp', w1[z][:, :, :], Wb['nsa_cmp_w1'][l][z].rearrange("(l d) e -> d l e", d=128), writes=[w1[z]])
                cx.dma('sp', w2[z][:, :], Wb['nsa_cmp_w2'][l][z], writes=[w2[z]])
            cx.dma('sp', posf[:, :, :], W['nsa_cmp_pos'][l].rearrange("z l d -> l z d"), writes=[posf])
            cx.op('dve', [posf], [posb], lambda e: e.tensor_copy(posb[:, :, :], posf[:, :, :]))
            trp.transpose_cols(posb, lambda j: posb[:, j, :], 2, posT, lambda j0, cnt: posT[:, j0:j0 + cnt, :], rows=32)
            for z in range(2):
                pb, ph = pm
                for ll in range(32):
                    mm(cx, pb, pb[:, z:z + 1], w1[z][:, ll, :], posT[:, z, ll:ll + 1], [w1[z], posT], ll == 0, ll == 31)
                evac(cx, 'dve', pb, pb[:, z:z + 1], bias, bias[:, z:z + 1])
                for ll in range(32):
                    mm(cx, ph, ph[:, :Nc], w1[z][:, ll, :], xT[z][:, ll:ll + 16 * (Nc - 1) + 1:16], [w1[z], xT[z]], ll == 0, ll == 31)
                cx.op('act', [ph, bias], [xs], lambda e: e.activation(out=xs[:, :Nc], in_=ph[:, :Nc], func=AF.Identity, bias=bias[:, z:z + 1]))
                cx.op('dve', [xs], [x2], lambda e: e.tensor_tensor(x2[:, :Nc], xs[:, :Nc], xs[:, :Nc], ALU.mult))
                cx.op('dve', [x2], [x2], lambda e: e.tensor_scalar(x2[:, :Nc], x2[:, :Nc], 0.044715, 1.0, ALU.mult, ALU.add))
                cx.op('dve', [x2, xs], [x2], lambda e: e.tensor_tensor(x2[:, :Nc], x2[:, :Nc], xs[:, :Nc], ALU.mult))
                cx.op('act', [x2], [x2], lambda e: e.activation(out=x2[:, :Nc], in_=x2[:, :Nc], func=AF.Tanh, scale=0.7978845608028654))
                cx.op('dve', [x2], [x2], lambda e: e.tensor_scalar(x2[:, :Nc], x2[:, :Nc], 1.0, 0.5, ALU.add, ALU.mult))
                cx.op('dve', [x2, xs], [hid[z]], lambda e: e.tensor_tensor(hid[z][:, :Nc], x2[:, :Nc], xs[:, :Nc], ALU.mult))
            pk = pm[0]
            mm(cx, pk, pk[:, :Nc], w2[0][:, :], hid[0][:, :Nc], [w2[0], hid[0]], True, True)
            evac(cx, 'act', pk, pk[:, :Nc], KC, KC[:, :Nc])
            cx.op('act', [pk], [ksq], lambda e: e.activation(out=ksq[:, :Nc], in_=pk[:, :Nc], func=AF.Square))
            pr = pm[1]
            mm(cx, pr, pr[0:1, :Nc], cx.ones_bf[:, 0:1], ksq[:, :Nc], [cx.ones_bf, ksq], True, True)
            cx.op('dve', [pr], [one], lambda e: e.tensor_reduce(out=one[0:1, 0:1], in_=pr[0:1, :Nc], axis=AX.X, op=ALU.max))
            cx.op('act', [one], [one], lambda e: e.activation(out=one[0:1, 0:1], in_=one[0:1, 0:1], func=AF.Sqrt))
            cx.op('dve', [one], [krow], lambda e: e.tensor_scalar(krow[0:1, 2, :], one[0:1, 0:1].to_broadcast([1, 128]), 1.02, None, ALU.mult))
            for kb in range(NKB):
                nk = min(128, Nc - kb * 128)
                pv = pm[kb % 2]
                mm(cx, pv, pv[:nk, 0:128], hid[1][:, kb * 128:kb * 128 + nk], w2[1][:, :], [hid[1], w2[1]], True, True)
                evac(cx, 'dve', pv, pv[:nk, 0:128], VCA, VCA[:nk, kb, 0:128])
        cx.barrier()
        if NSA_STOP == 2:
            return
        res = AttnRes(cx, st, 193)
        qT = [cx.sb(st, [128, S], BF16, name="qT") for _ in range(4)]
        nq = [cx.sb(st, [128, S], BF16, name="nq") for _ in range(4)]
        for h in range(4):
            cx.op('pool', [], [nq[h]], lambda e: e.memset(nq[h][:, :], 0.0))
            cx.dma('sp', qT[h][:, :], N['qT'][h], writes=[qT[h]])
            cx.dma('act', nq[h][0:1, :], N['nq'][h:h + 1, :], writes=[nq[h]])
        ksT = cx.sb(st, [128, S], BF16, name="ksT")
        kwT = cx.sb(st, [128, S], BF16, name="kwT")
        cx.dma('sp', ksT[:, :], N['ksT'][:, :], writes=[ksT])
        cx.dma('act', kwT[:, :], N['kwT'][:, :], writes=[kwT])
        vsa = cx.sb(st, [128, NT, 129], BF16, name="vsa")
        vwa = cx.sb(st, [128, NT, 129], BF16, name="vwa")
        cx.dma('sp', vsa[:, :, :], N['vsa'].rearrange("(t p) c -> p t c", p=128), writes=[vsa])
        cx.dma('act', vwa[:, :, :], N['vwa'].rearrange("(t p) c -> p t c", p=128), writes=[vwa])
        cmask = cx.sb(st, [128, 4, 512], BF16, name="cmask")
        wmask = cx.sb(st, [128, 4, 512], BF16, name="wmask")
        cx.dma('sp', cmask[:, :, :], C['cmask'].rearrange("i k q -> k i q"), writes=[cmask])
        cx.dma('sp', wmask[:, :, :], C['wmask'].rearrange("i k q -> k i q"), writes=[wmask])
        cmpm = cx.sb(st, [128, 2, S], BF16, name="cmpm")
        cx.dma('sp', cmpm[:, :, :], C['cmpmask'].rearrange("kb p q -> p kb q"), writes=[cmpm])
        Em = cx.sb(st, [64, S], BF16, name="Em")
        cx.dma('sp', Em[:, :], C['Emat'][:, :], writes=[Em])
        gts = cx.sb(st, [128, NT, 12], F32, name="gts")
        cx.dma('sp', gts[:, :, :], N['ng'].rearrange("(t p) c -> p t c", p=128), writes=[gts])
        imp = cx.sb(st, [128, 4, 64], F32, name="imp")
        fbt = [cx.sb(st, [128, 64], F32, name="fbt") for _ in range(2)]
        m8 = cx.sb(st, [128, 16], F32, name="m8")
        val2 = cx.sb(st, [128, 64], F32, name="val2")
        selb = cx.sb(st, [128, 64], BF16, name="selb")
        selT = cx.sb(st, [64, 512], BF16, name="selT")
        rc = [cx.sb(st, [128, 1], F32, name="rc") for _ in range(2)]
        rg = [cx.sb(st, [128, 1], F32, name="rg") for _ in range(2)]
        ot = [cx.sb(st, [128, 128], F32, name="ot") for _ in range(3)]
        it = [cx.sb(st, [128, 64], F32, name="it") for _ in range(2)]
        trp2 = TrPool(cx, st, n=1)
        cnt = [0]

        def make_epi(branch, h):
            def epi(QB, j, accb, acc_ap):
                k = cnt[0]
                cnt[0] += 1
                tt = QB * NJ + j
                r, rgb, o = rc[k % 2], rg[k % 2], ot[k % 3]
                cx.op('dve', [accb], [r], lambda e: e.tensor_scalar(r[:, :], acc_ap[:, 128:129], 1e-30, None, ALU.add))
                cx.op('dve', [r], [r], lambda e: e.reciprocal(r[:, :], r[:, :]))
                cx.op('dve', [r, gts], [rgb], lambda e: e.tensor_tensor(rgb[:, :], r[:, :], gts[:, tt, h * 3 + branch:h * 3 + branch + 1], ALU.mult))
                cx.op('act', [accb, rgb], [o], lambda e: e.activation(out=o[:, :], in_=acc_ap[:, 0:128], func=AF.Copy, scale=rgb[:, 0:1]))
                cx.dma('pool', Youts[branch][tt * 128:(tt + 1) * 128, h * 128:(h + 1) * 128], o[:, :], reads=[o])
                if branch == 0:
                    if h == 0:
                        cx.op('dve', [accb, r], [imp], lambda e: e.tensor_scalar(imp[:, j, :], acc_ap[:, 129:193], r[:, 0:1], None, ALU.mult))
                    else:
                        i_ = it[k % 2]
                        cx.op('dve', [accb, r], [i_], lambda e: e.tensor_scalar(i_[:, :], acc_ap[:, 129:193], r[:, 0:1], None, ALU.mult))
                        cx.op('pool', [i_, imp], [imp], lambda e: e.tensor_tensor(imp[:, j, :], imp[:, j, :], i_[:, :], ALU.add))
            return epi

        def cmp_blocks(QB):
            q0 = QB * QW
            return [dict(kb=kb, k0=kb * 128, nk=min(128, Nc - kb * 128),
                         mask=(cmpm, cmpm[:min(128, Nc - kb * 128), kb, q0:q0 + QW])) for kb in range(NKB)]

        def slc_blocks(QB):
            bl = causal_blocks(QB, QW, cmask)
            for b in bl:
                b['extra'] = ([Em], Em[:, b['k0']:b['k0'] + 128], [selT], selT[:, :QW])
            return bl

        def win_blocks(QB):
            out = []
            nd = QW // 128
            for kb in range(max(0, nd * QB - 4), nd * (QB + 1)):
                i = kb - nd * QB
                m = (cmask, cmask[:, i, :QW]) if i >= 0 else (wmask, wmask[:, i + 4, :QW])
                out.append(dict(kb=kb, k0=kb * 128, nk=128, mask=m))
            return out

        for QB in range(S // QW):
            for h in range(4):
                attn_core(cx, res, S, [(qT[h], qT[h][:, :]), (nq[h], nq[h][:, :])],
                          [(kwT, kwT[:, :]), (krow, lambda k0, nk: krow[:, 1, :nk])],
                          lambda kb, nk: (vwa, vwa[:nk, kb, :]), 129, win_blocks, make_epi(2, h), qbs=[QB])
            if NSA_STOP == 3:
                break
            for h in range(4 if NSA_STOP != 8 else 0):
                attn_core(cx, res, S, [(qT[h], qT[h][:, :]), (nq[h], nq[h][:, :])],
                          [(KC, KC[:, :]), (krow, lambda k0, nk: krow[:, 2, :nk])],
                          lambda kb, nk: (VCA, VCA[:nk, kb, :]), 193, cmp_blocks, make_epi(0, h), qbs=[QB])
            if NSA_STOP == 4:
                break
            for j in range(NJ if NSA_STOP != 8 else 0):
                tt = QB * NJ + j
                f_ = fbt[j % 2]
                cx.dma('sp', f_[:, :], C['fbias'][tt * 128:(tt + 1) * 128, :], writes=[f_])
                cx.op('dve', [imp, f_], [f_], lambda e: e.tensor_tensor(f_[:, :], f_[:, :], imp[:, j, :], ALU.add))
                cx.op('dve', [f_], [m8], lambda e: e.max(out=m8[:, 0:8], in_=f_[:, :]))
                cx.op('dve', [f_, m8], [val2], lambda e: e.match_replace(out=val2[:, :], in_to_replace=m8[:, 0:8], in_values=f_[:, :], imm_value=-3.0e38))
                cx.op('dve', [val2], [m8], lambda e: e.max(out=m8[:, 8:16], in_=val2[:, :]))
                cx.op('dve', [f_, m8], [val2], lambda e: e.tensor_scalar(val2[:, :], f_[:, :], m8[:, 15:16], None, ALU.is_ge))
                cx.op('dve', [val2], [selb], lambda e: e.tensor_scalar(selb[:, :], val2[:, :], -1.0, NSA_BIG, ALU.add, ALU.mult))
                trp2.transpose_cols(selb, lambda jj: selb[:, :], 1, selT, lambda j0, c_, j=j: selT[:64, j * 128:(j + 1) * 128].unsqueeze(1), blkw=64)
            if NSA_STOP == 5:
                break
            for h in range(4 if NSA_STOP not in (8, 9) else 0):
                attn_core(cx, res, S, [(qT[h], qT[h][:, :]), (nq[h], nq[h][:, :])],
                          [(ksT, ksT[:, :]), (krow, lambda k0, nk: krow[:, 0, :nk])],
                          lambda kb, nk: (vsa, vsa[:nk, kb, :]), 129, slc_blocks, make_epi(1, h), qbs=[QB])
    cx.barrier()


S_FULL = 4096
ENABLE = {'mla': True, 'nsa': True, 'rwkv': True, 'ret': True}


def phase_zero(cx, S, Y_d):
    with contextlib.ExitStack() as st:
        z = cx.sb(st, [128, 512], F32, name="z")
        cx.op('pool', [], [z], lambda e: e.memset(z[:, :], 0.0))
        for tt in range(S // 128):
            cx.dma('sp', Y_d[tt * 128:(tt + 1) * 128, :], z[:, :], reads=[z])
    cx.barrier()


def host_consts(S):
    import ml_dtypes
    bf = ml_dtypes.bfloat16
    c = {}
    c['ident'] = np.eye(128, dtype=np.float32).astype(bf)
    kk = np.arange(128)[:, None]
    qq = np.arange(512)[None, :]
    c['cmask'] = np.stack([(128 * i + kk <= qq) for i in range(4)]).astype(np.float32).astype(bf)
    t = np.arange(S, dtype=np.float32)[:, None]

    def tables(inv):
        ang = (t * inv[None, :].astype(np.float32)).astype(np.float32)
        return np.cos(ang).astype(np.float32), np.sin(ang).astype(np.float32)

    inv_mla = (np.float32(500000.0) ** (-np.arange(0, 64, 2, dtype=np.float32) / np.float32(64))).astype(np.float32)
    inv_nsa = (np.float32(500000.0) ** (-np.arange(0, 32, 2, dtype=np.float32) / np.float32(32))).astype(np.float32)
    inv_ret = (np.float32(10000.0) ** (-np.linspace(0.0, 1.0, 32, dtype=np.float32))).astype(np.float32)
    c['mla_cos'], c['mla_sin'] = tables(inv_mla)
    c['nsa_cos'], c['nsa_sin'] = tables(inv_nsa)
    c['ret_cos'], c['ret_sin'] = tables(inv_ret)
    kf = kk.astype(np.float64)
    qf = qq.astype(np.float64)
    gdec = np.zeros((4, 5, 128, 512), np.float32)
    for h in range(4):
        lg = np.log1p(-2.0 ** (-5 - h))
        gdec[h, 0] = np.exp((qf - kf) * lg)
        for i in range(4):
            d = qf - kf - 128 * i
            gdec[h, 1 + i] = np.where(d >= 0, np.exp(np.maximum(d, 0) * lg), 0.0)
    c['gdec'] = gdec
    si = np.arange(128)[:, None]
    ti = np.arange(128)[None, :]
    c['wmask'] = (1.0 - c['cmask'].astype(np.float32)).astype(bf)
    Nc = (S - 32) // 16 + 1
    n = np.arange(256)
    q = np.arange(S)
    cm = ((16 * n[:, None] + 31 <= q[None, :]) & (n[:, None] < Nc)).astype(np.float32)
    c['cmpmask'] = cm.reshape(2, 128, S).astype(bf)
    nblk = S // 64
    jb = np.arange(64)
    cstart = 16 * n
    cend = cstart + 31
    cover = ((cstart[:, None] <= jb[None, :] * 64 + 63) & (cend[:, None] >= jb[None, :] * 64) & (n[:, None] < Nc)
             & (jb[None, :] < nblk)).astype(np.float32)
    c['cover'] = cover.astype(bf)
    c['Emat'] = (q[None, :] // 64 == jb[:, None]).astype(np.float32).astype(bf)
    cur = q // 64
    forced = (jb[None, :] == 0) | (jb[None, :] == cur[:, None]) | (jb[None, :] == cur[:, None] - 1)
    visible = (jb[None, :] <= cur[:, None]) & (jb[None, :] < nblk)
    c['fbias'] = np.where(visible, 1000.0 * forced, -1.0e30).astype(np.float32)
    c['rwm'] = np.concatenate([(si < ti), (si <= ti), (si > ti)], axis=1).astype(np.float32)
    return c


CONST_SPECS = {'ident': ([128, 128], BF16), 'cmask': ([4, 128, 512], BF16),
               'mla_cos': (None, F32), 'mla_sin': (None, F32), 'nsa_cos': (None, F32), 'nsa_sin': (None, F32),
               'ret_cos': (None, F32), 'ret_sin': (None, F32), 'gdec': ([4, 5, 128, 512], F32), 'rwm': ([128, 384], F32), 'wmask': ([4, 128, 512], BF16), 'cmpmask': ('cmp', BF16),
               'cover': ([256, 64], BF16), 'Emat': ('E', BF16), 'fbias': ('fb', F32)}

WEIGHT_SHAPES = {
    'w_in': [DEPTH, D_MODEL, IN_WIDTH], 'w_branch': [DEPTH, 4, BW, D_MODEL], 'w_out': [DEPTH, D_MODEL, D_MODEL],
    'w_up': [DEPTH, D_MODEL, D_FF], 'w_down': [DEPTH, D_FF, D_MODEL], 'norm_gains': [DEPTH, 4, D_MODEL],
    'mla_g_q': [DEPTH, 384], 'mla_g_kv': [DEPTH, 128], 'mla_w_uq': [DEPTH, 384, 768], 'mla_w_ukv': [DEPTH, 128, 1024],
    'nsa_cmp_pos': [DEPTH, 2, 32, 128], 'nsa_cmp_w1': [DEPTH, 2, 4096, 128], 'nsa_cmp_w2': [DEPTH, 2, 128, 128],
    'rwkv_mu': [DEPTH, 1984], 'rwkv_w0': [DEPTH, 512], 'rwkv_w2': [DEPTH, 96, 512], 'rwkv_a0': [DEPTH, 512],
    'rwkv_a2': [DEPTH, 96, 512], 'rwkv_g2': [DEPTH, 256, 512], 'rwkv_k_k': [DEPTH, 512], 'rwkv_k_a': [DEPTH, 512],
    'rwkv_r_k': [DEPTH, 8, 64], 'rwkv_gn_w': [DEPTH, 512], 'rwkv_gn_b': [DEPTH, 512],
}
CAST = ['w_in', 'w_branch', 'w_out', 'w_up', 'w_down', 'mla_w_uq', 'mla_w_ukv', 'nsa_cmp_w1', 'nsa_cmp_w2',
        'rwkv_w2', 'rwkv_a2', 'rwkv_g2']


def build_program(S, depth=DEPTH):
    cx = Ctx()
    x_d = cx.dram("x", [S, D_MODEL], F32, kind="ExternalInput")
    W = {k: cx.dram(k, shp, F32, kind="ExternalInput") for k, shp in WEIGHT_SHAPES.items()}
    C = {}
    for k, (shp, dt) in CONST_SPECS.items():
        if shp is None:
            shp = [S, 16 if k.startswith('nsa') else 32]
        elif shp == 'cmp':
            shp = [2, 128, S]
        elif shp == 'E':
            shp = [64, S]
        elif shp == 'fb':
            shp = [S, 64]
        C[k] = cx.dram("c_" + k, shp, dt, kind="ExternalInput")
    y_d = cx.dram("y", [S, D_MODEL], F32, kind="ExternalOutput")
    Wb = {k: cx.dram(k + "_bf", WEIGHT_SHAPES[k], BF16) for k in CAST}
    P_d = cx.dram("P", [S, IN_WIDTH], F32)
    Y = [cx.dram("Y%d" % m, [S, BW], F32) for m in range(4)]
    Yn = [cx.dram("Yn%d" % m, [S, BW], F32) for m in range(2)]
    M_d = cx.dram("M", [S, D_MODEL], F32)
    Z_d = cx.dram("Z", [S, D_MODEL], F32)
    xa = cx.dram("xa", [S, D_MODEL], F32)
    xb = cx.dram("xb", [S, D_MODEL], F32)
    scr = dict(qnT=cx.dram("qnT", [4, 128, S], BF16), qrT=cx.dram("qrT", [4, 65, S], BF16),
               knT=cx.dram("knT", [4, 128, S], BF16), krT=cx.dram("krT", [65, S], BF16),
               va=cx.dram("va", [S, 4, 129], BF16),
               rqT=cx.dram("rqT", [4, 64, S], BF16), rkT=cx.dram("rkT", [4, 64, S], BF16),
               rv=cx.dram("rv", [S, 4, 128], BF16))
    scr['rw'] = {nm: cx.dram("rw_" + nm, [S, 512], F32) for nm in ['rr', 'lw', 'k2', 'vv', 'kn', 'aa', 'gg']}
    scr['rw']['bc'] = cx.dram("rw_bc", [S, 8], F32)
    scr['nsa'] = dict(qT=cx.dram("n_qT", [4, 128, S], BF16), nq=cx.dram("n_nq", [4, S], BF16),
                      kcT=cx.dram("n_kcT", [128, S], BF16), vcT=cx.dram("n_vcT", [128, S], BF16),
                      ksT=cx.dram("n_ksT", [128, S], BF16), kwT=cx.dram("n_kwT", [128, S], BF16),
                      vsa=cx.dram("n_vsa", [S, 129], BF16), vwa=cx.dram("n_vwa", [S, 129], BF16),
                      ng=cx.dram("n_ng", [S, 12], F32), krow=cx.dram("n_krow", [2, 128], BF16))
    st = contextlib.ExitStack()
    cx._st = st
    setup_consts(cx, st, C['ident'])
    phase_cast(cx, [(W[k], Wb[k]) for k in CAST])
    xin = x_d
    for l in range(depth):
        g = W['norm_gains'][l]
        phase_in(cx, S, xin, g[0], Wb['w_in'][l], P_d)
        if ENABLE['mla']:
            phase_mla(cx, S, P_d, W['mla_g_q'][l], W['mla_g_kv'][l], Wb['mla_w_uq'][l], Wb['mla_w_ukv'][l],
                      C['mla_cos'], C['mla_sin'], C['cmask'], Y[0], scr)
        else:
            phase_zero(cx, S, Y[0])
        if ENABLE['nsa']:
            phase_nsa(cx, S, l, P_d, W, Wb, C, [Y[1], Yn[0], Yn[1]], scr)
            ysrc1 = [Y[1], Yn[0], Yn[1]]
        else:
            phase_zero(cx, S, Y[1])
            ysrc1 = [Y[1]]
        if ENABLE['rwkv']:
            phase_rwkv(cx, S, l, P_d, W, Wb, C, Y[2], scr)
        else:
            phase_zero(cx, S, Y[2])
        if ENABLE['ret']:
            phase_ret(cx, S, P_d, C['ret_cos'], C['ret_sin'], C['gdec'], Y[3], scr)
        else:
            phase_zero(cx, S, Y[3])
        phase_merge(cx, S, [[Y[0]], ysrc1, [Y[2]], [Y[3]]], Wb['w_branch'][l], P_d, M_d)
        phase_out(cx, S, M_d, Wb['w_out'][l], Z_d)
        phase_normres(cx, S, xin, Z_d, g[1], xa)
        phase_ffn(cx, S, xa, g[2], Wb['w_up'][l], Wb['w_down'][l], Z_d)
        xnext = y_d if l == depth - 1 else xb
        phase_normres(cx, S, xa, Z_d, g[3], xnext)
        xin = xnext
    cx.barrier()
    return cx


_CACHE = {}


def kernel(**inputs):
    x = np.ascontiguousarray(np.asarray(inputs['x'], dtype=np.float32))
    B, S, _ = x.shape
    if S not in _CACHE:
        _CACHE[S] = (build_program(S), host_consts(S))
    cx, consts = _CACHE[S]
    base = {k: np.ascontiguousarray(np.asarray(inputs[k], dtype=np.float32)) for k in WEIGHT_SHAPES}
    for k, v in consts.items():
        base["c_" + k] = np.ascontiguousarray(v)
    in_maps = []
    for b in range(B):
        m = dict(base)
        m['x'] = x[b]
        in_maps.append(m)
    res = run_bass_kernel_spmd(cx.nc, in_maps, core_ids=list(range(B)))
    return np.stack([np.asarray(r['y'], dtype=np.float32) for r in res.results], axis=0)
```

```python
import contextlib
import numpy as np
import concourse.bass as bass
import concourse.mybir as mybir
from concourse.bass_utils import run_bass_kernel_spmd

F32 = mybir.dt.float32
F32R = mybir.dt.float32r
RW_FAST = True


def fr(ap):
    return ap.bitcast(F32R) if RW_FAST else ap
BF16 = mybir.dt.bfloat16
AF = mybir.ActivationFunctionType
ALU = mybir.AluOpType
AX = mybir.AxisListType

D_MODEL = 2048
DEPTH = 2
BW = 512
D_FF = 8192
NORM_EPS = 1e-6
IN_WIDTH = 13580
OFF_MLA = 0
OFF_NSA = 576
OFF_RWKV = 1868
OFF_RET = 3852
OFF_GATE = 5388

SELF_SYNC = {'pe': False, 'act': False, 'dve': True, 'pool': True, 'sp': False}


class Reg:
    __slots__ = ('w', 'r')

    def __init__(self):
        self.w = None
        self.r = {}


class Buf:
    def __init__(self, t, nreg=1, excl=False):
        self.t = t
        self.regs = [Reg() for _ in range(nreg)]
        self.excl = excl

    @property
    def reg(self):
        return self.regs[0]

    def __getitem__(self, idx):
        return self.t[idx]


class Ctx:
    def __init__(self):
        self.nc = bass.Bass("TRN2", target_bir_lowering=False)
        nc = self.nc
        self.E = {'pe': nc.tensor, 'act': nc.scalar, 'dve': nc.vector, 'pool': nc.gpsimd, 'sp': nc.sync}
        self.sem = {e: nc.alloc_semaphore("s_" + e) for e in ['pe', 'act', 'dve', 'pool']}
        self.cnt = {e: 0 for e in self.sem}
        self.NDS = 48
        self.dsem = [nc.alloc_semaphore("d%d" % i) for i in range(self.NDS)]
        self.dcnt = [0] * self.NDS
        self.dpool = {'sp': list(range(0, 20)), 'act': list(range(20, 34)), 'pool': list(range(34, 48))}
        self.dnext = {'sp': 0, 'act': 0, 'pool': 0}
        self.known = {e: {} for e in self.E}
        self.ninst = 0
        self.uid = 0
        self._rec = None

    def name(self, p):
        self.uid += 1
        return "%s_%d" % (p, self.uid)

    def sb(self, stack, shape, dtype, nreg=1, name="sb"):
        t = stack.enter_context(self.nc.sbuf_tensor(self.name(name), list(shape), dtype))
        return Buf(t, nreg)

    def ps(self, stack, shape, dtype=F32, nreg=1, name="ps"):
        t = stack.enter_context(self.nc.psum_tensor(self.name(name), list(shape), dtype))
        return Buf(t, nreg, excl=True)

    def dram(self, name, shape, dtype, kind="Internal"):
        return self.nc.dram_tensor(name, list(shape), dtype, kind=kind).ap()

    def _wait(self, e, kind, val, force=False):
        if isinstance(kind, str):
            if kind == e and not SELF_SYNC[e] and not (force and e in self.sem):
                return
            sem = self.sem[kind]
            v = val
        else:
            idx = kind[1]
            sem = self.dsem[idx]
            v = val * 16
        k = self.known[e]
        if k.get(kind, 0) >= v:
            return
        self.E[e].wait_ge(sem, v)
        self.ninst += 1
        k[kind] = v

    def _deps(self, e, reads, writes, force=False):
        for r in reads:
            if r.w is not None:
                self._wait(e, r.w[0], r.w[1], force)
        for w in writes:
            if w.w is not None:
                self._wait(e, w.w[0], w.w[1], force)
            for kind, val in w.r.items():
                self._wait(e, kind, val, force)

    def _commit(self, tok, reads, writes):
        kind, val = tok
        for r in reads:
            if r.r.get(kind, 0) < val:
                r.r[kind] = val
        for w in writes:
            w.w = tok
            w.r = {}

    @staticmethod
    def _regs(lst):
        out = []
        for x in lst:
            if isinstance(x, Buf):
                out.extend(x.regs)
            elif isinstance(x, Reg):
                out.append(x)
            elif x is None:
                pass
            else:
                raise TypeError(type(x))
        return out

    def op(self, e, reads, writes, fn):
        if self._rec is not None:
            self._rec.append(lambda: self._op(e, reads, writes, fn))
            return None
        return self._op(e, reads, writes, fn)

    def _op(self, e, reads, writes, fn):
        writes = list(writes) + [x for x in reads if isinstance(x, Buf) and x.excl]
        reads = [x for x in reads if not (isinstance(x, Buf) and x.excl)]
        reads = self._regs(reads)
        writes = self._regs(writes)
        self._deps(e, reads, writes)
        inst = fn(self.E[e])
        self.cnt[e] += 1
        self.ninst += 1
        inst.then_inc(self.sem[e], 1)
        self._commit((e, self.cnt[e]), reads, writes)
        return inst

    def dma(self, q, out_ap, in_ap, reads=(), writes=(), **kw):
        if self._rec is not None:
            self._rec.append(lambda: self._dma(q, out_ap, in_ap, reads, writes, **kw))
            return
        self._dma(q, out_ap, in_ap, reads, writes, **kw)

    def _dma(self, q, out_ap, in_ap, reads=(), writes=(), **kw):
        reads = self._regs(reads)
        writes = self._regs(writes)
        self._deps(q, reads, writes, force=True)
        pool = self.dpool[q]
        idx = pool[self.dnext[q] % len(pool)]
        self.dnext[q] += 1
        if self.dcnt[idx] > 0:
            self._wait(q, ('d', idx), self.dcnt[idx])
        self.E[q].dma_start(out=out_ap, in_=in_ap, **kw).then_inc(self.dsem[idx], 16)
        self.dcnt[idx] += 1
        self.ninst += 1
        self._commit((('d', idx), self.dcnt[idx]), reads, writes)

    def barrier(self):
        for e in self.E:
            for o in self.sem:
                if o != e and self.cnt[o] > 0:
                    self._wait(e, o, self.cnt[o])
            for i in range(self.NDS):
                if self.dcnt[i] > 0:
                    self._wait(e, ('d', i), self.dcnt[i])
            if e in self.sem and self.cnt[e] > 0:
                k = self.known[e]
                if k.get(e, 0) < self.cnt[e]:
                    self.E[e].wait_ge(self.sem[e], self.cnt[e])
                    k[e] = self.cnt[e]


INTERLEAVE = 2


def run_tiles(cx, body, NT):
    for t0 in range(0, NT, INTERLEAVE):
        lists = []
        posts = []
        for t in range(t0, min(NT, t0 + INTERLEAVE)):
            cx._rec = []
            post = body(t)
            lists.append(cx._rec)
            cx._rec = None
            if post is not None:
                posts.append(post)
        for i in range(max(len(l) for l in lists)):
            for l in lists:
                if i < len(l):
                    l[i]()
        for p in posts:
            p()


def atomic(cx, fn):
    if cx._rec is None:
        return fn()
    rec = cx._rec

    def unit():
        saved = cx._rec
        cx._rec = None
        fn()
        cx._rec = saved
    rec.append(unit)


def mm(cx, out_buf, out_ap, lhsT_ap, rhs_ap, reads, start, stop, **kw):
    return cx.op('pe', reads, [out_buf],
                 lambda e: e.matmul(out_ap, lhsT_ap, rhs_ap, start=start, stop=stop, **kw))


def transp(cx, out_buf, out_ap, in_ap, ident_ap, reads):
    return cx.op('pe', reads, [out_buf], lambda e: e.transpose(out_ap, in_ap, ident_ap))


def phase_cast(cx, pairs):
    CH = 4096
    NB = 6
    with contextlib.ExitStack() as st:
        stg = [cx.sb(st, [128, CH], F32, name="cst") for _ in range(NB)]
        outb = [cx.sb(st, [128, CH], BF16, name="cob") for _ in range(NB)]
        engs = ['dve', 'pool', 'act']
        k = 0
        for src, dst in pairs:
            n = 1
            for s in src.shape:
                n *= s
            assert n % 128 == 0
            per = n // 128
            names = " ".join("a%d" % i for i in range(len(src.shape)))
            s2 = src.rearrange("%s -> (%s)" % (names, names)).rearrange("(p f) -> p f", p=128)
            d2 = dst.rearrange("%s -> (%s)" % (names, names)).rearrange("(p f) -> p f", p=128)
            for c0 in range(0, per, CH):
                c1 = min(per, c0 + CH)
                w = c1 - c0
                i = k % NB
                cx.dma('sp', stg[i][:, :w], s2[:, c0:c1], writes=[stg[i]])
                e = engs[k % 3]
                if e == 'act':
                    cx.op(e, [stg[i]], [outb[i]], lambda en: en.copy(outb[i][:, :w], stg[i][:, :w]))
                else:
                    cx.op(e, [stg[i]], [outb[i]], lambda en: en.tensor_copy(outb[i][:, :w], stg[i][:, :w]))
                cx.dma('act' if k % 2 else 'pool', d2[:, c0:c1], outb[i][:, :w], reads=[outb[i]])
                k += 1
    cx.barrier()


def load_bcast_row(cx, q, buf, row_ap, n):
    cx.dma(q, buf[:, :n], row_ap.partition_broadcast(128), writes=[buf])


def rms_rstd(cx, x_buf, x_ap, n, ss_buf, junk_buf, eps=NORM_EPS):
    cx.op('act', [x_buf], [junk_buf, ss_buf],
          lambda e: e.activation(out=junk_buf[:, :n], in_=x_ap, func=AF.Square, accum_out=ss_buf[:, 0:1]))
    cx.op('dve', [ss_buf], [ss_buf],
          lambda e: e.tensor_scalar(ss_buf[:, 0:1], ss_buf[:, 0:1], 1.0 / n, eps, ALU.mult, ALU.add))
    cx.op('pool', [ss_buf, cx.neghalf], [ss_buf],
          lambda e: e.tensor_tensor(ss_buf[:, 0:1], ss_buf[:, 0:1], cx.neghalf[:, 0:1], ALU.pow))


def setup_consts(cx, st, ident_d):
    cx.ident = cx.sb(st, [128, 128], BF16, name="ident")
    cx.dma('sp', cx.ident[:, :], ident_d[:, :], writes=[cx.ident])
    cx.identf = cx.sb(st, [128, 128], F32, name="identf")
    cx.op('dve', [cx.ident], [cx.identf], lambda e: e.tensor_copy(cx.identf[:, :], cx.ident[:, :]))
    cx.neghalf = cx.sb(st, [128, 1], F32, name="neghalf")
    cx.op('pool', [], [cx.neghalf], lambda e: e.memset(cx.neghalf[:, :], -0.5))
    cx.ones_bf = cx.sb(st, [128, 128], BF16, name="ones_bf")
    cx.op('pool', [], [cx.ones_bf], lambda e: e.memset(cx.ones_bf[:, :], 1.0))
    cx.ones_f = cx.sb(st, [128, 128], F32, name="ones_f")
    cx.op('pool', [], [cx.ones_f], lambda e: e.memset(cx.ones_f[:, :], 1.0))


def phase_in(cx, S, x_d, g_row, w_bf, P_d):
    G = 512
    KC = D_MODEL // 128
    chunks = []
    c = 0
    while c < OFF_GATE:
        chunks.append((c, min(c + 512, OFF_GATE), False))
        c += 512
    c = OFF_GATE
    while c < IN_WIDTH:
        chunks.append((c, c + 512, True))
        c += 512
    wv = w_bf.rearrange("(kc p) n -> p kc n", p=128)
    with contextlib.ExitStack() as st:
        gB = cx.sb(st, [128, D_MODEL], F32, name="gB")
        load_bcast_row(cx, 'sp', gB, g_row, D_MODEL)
        xt = [cx.sb(st, [128, D_MODEL], F32, name="xt") for _ in range(2)]
        junk = cx.sb(st, [128, D_MODEL], BF16, name="junk")
        ss = [cx.sb(st, [128, 1], F32, name="ss") for _ in range(2)]
        hb = [cx.sb(st, [128, D_MODEL], BF16, name="hb") for _ in range(2)]
        hT = [cx.sb(st, [128, KC, G], BF16, name="hT") for _ in range(2)]
        wb = [cx.sb(st, [128, KC, 512], BF16, name="wb") for _ in range(2)]
        ob = [cx.sb(st, [128, 512], F32, name="ob") for _ in range(4)]
        ptr = [cx.ps(st, [128, 8, 128], BF16, name="ptr") for _ in range(2)]
        pmm = [cx.ps(st, [128, 512], F32, name="pmm") for _ in range(4)]
        ntr = 0
        nmm = 0
        nw = 0
        for gi in range(S // G):
            hTg = hT[gi % 2]
            for tl in range(G // 128):
                tt = gi * (G // 128) + tl
                xb = xt[tt % 2]
                sb_ = ss[tt % 2]
                hbb = hb[tt % 2]
                cx.dma('sp', xb[:, :], x_d[tt * 128:(tt + 1) * 128, :], writes=[xb])
                rms_rstd(cx, xb, xb[:, :], D_MODEL, sb_, junk)
                cx.op('dve', [xb, sb_, gB], [hbb],
                      lambda e: e.scalar_tensor_tensor(out=hbb[:, :], in0=xb[:, :], scalar=sb_[:, 0:1],
                                                       in1=gB[:, :], op0=ALU.mult, op1=ALU.mult))
                for k4 in range(KC // 4):
                    pt = ptr[ntr % 2]
                    ntr += 1
                    for j in range(4):
                        kc = k4 * 4 + j
                        transp(cx, pt, pt[:, j, :], hbb[:, kc * 128:(kc + 1) * 128], cx.ident[:, :], [hbb, cx.ident])
                    eng = 'act' if (k4 % 2 == 0) else 'dve'
                    dst = hTg[:, k4 * 4:(k4 + 1) * 4, tl * 128:(tl + 1) * 128]
                    if eng == 'act':
                        cx.op('act', [pt], [hTg], lambda e: e.copy(dst, pt[:, 0:4, :]))
                    else:
                        cx.op('dve', [pt], [hTg], lambda e: e.tensor_copy(dst, pt[:, 0:4, :]))
            for (c0, c1, sig) in chunks:
                w = c1 - c0
                wbb = wb[nw % 2]
                nw += 1
                cx.dma('sp', wbb[:, :, :w], wv[:, :, c0:c1], writes=[wbb])
                for tl in range(G // 128):
                    tt = gi * (G // 128) + tl
                    pm = pmm[nmm % 4]
                    obb = ob[nmm % 4]
                    nmm += 1
                    for kc in range(KC):
                        mm(cx, pm, pm[:, :w], hTg[:, kc, tl * 128:(tl + 1) * 128], wbb[:, kc, :w],
                           [hTg, wbb], kc == 0, kc == KC - 1)
                    if sig:
                        cx.op('act', [pm], [obb],
                              lambda e: e.activation(out=obb[:, :w], in_=pm[:, :w], func=AF.Sigmoid))
                    elif nmm % 2 == 0:
                        cx.op('dve', [pm], [obb], lambda e: e.tensor_copy(obb[:, :w], pm[:, :w]))
                    else:
                        cx.op('act', [pm], [obb], lambda e: e.copy(obb[:, :w], pm[:, :w]))
                    cx.dma('pool', P_d[tt * 128:(tt + 1) * 128, c0:c1], obb[:, :w], reads=[obb])
    cx.barrier()


def evac(cx, eng, src_buf, src_ap, dst_buf, dst_ap, extra_reads=()):
    if eng == 'act':
        cx.op('act', [src_buf] + list(extra_reads), [dst_buf], lambda e: e.copy(dst_ap, src_ap))
    else:
        cx.op(eng, [src_buf] + list(extra_reads), [dst_buf], lambda e: e.tensor_copy(dst_ap, src_ap))


class TrPool:
    def __init__(self, cx, st, n=2, dtype=BF16):
        self.cx = cx
        self.bufs = [cx.ps(st, [128, 8 if dtype == BF16 else 4, 128], dtype, name="ptr") for _ in range(n)]
        self.k = 0
        self.dtype = dtype

    def transpose_cols(self, src_buf, src_ap_fn, nblk, dst_buf, dst_ap_fn, rows=128, blkw=128):
        cx = self.cx
        if cx._rec is not None:
            atomic(cx, lambda: self.transpose_cols(src_buf, src_ap_fn, nblk, dst_buf, dst_ap_fn, rows, blkw))
            return
        ident = cx.ident if self.dtype == BF16 else cx.identf
        j = 0
        while j < nblk:
            cnt = min(4, nblk - j)
            pt = self.bufs[self.k % len(self.bufs)]
            eng = 'act' if self.k % 2 == 0 else 'dve'
            self.k += 1
            for i in range(cnt):
                transp(cx, pt, pt[:blkw, i, :rows], src_ap_fn(j + i), ident[:rows, :rows], [src_buf, ident])
            evac(cx, eng, pt, pt[:blkw, :cnt, :rows], dst_buf, dst_ap_fn(j, cnt))
            j += cnt


def phase_merge(cx, S, ysrcs, wbr_bf, P_d, M_d):
    with contextlib.ExitStack() as st:
        wbr = cx.sb(st, [128, 16, D_MODEL], BF16, name="wbr")
        wv = wbr_bf.rearrange("m (kc p) n -> p (m kc) n", p=128)
        for q in range(4):
            cx.dma('sp', wbr[:, q * 4:(q + 1) * 4, :], wv[:, q * 4:(q + 1) * 4, :], writes=[wbr])
        yt = [cx.sb(st, [128, BW], F32, name="yt") for _ in range(3)]
        yb = [cx.sb(st, [128, BW], BF16, name="yb") for _ in range(2)]
        yT = [cx.sb(st, [128, 4, 128], BF16, name="yT") for _ in range(2)]
        sg = [cx.sb(st, [128, D_MODEL], F32, name="sg") for _ in range(2)]
        mg = [cx.sb(st, [128, D_MODEL], F32, name="mg") for _ in range(2)]
        tmp = [cx.sb(st, [128, 512], F32, name="tmp") for _ in range(2)]
        trp = TrPool(cx, st)
        pmm = [cx.ps(st, [128, 512], F32, name="pmm") for _ in range(4)]
        k = 0
        for tt in range(S // 128):
            rows = slice(tt * 128, (tt + 1) * 128)
            mgb = mg[tt % 2]
            for m in range(4):
                k += 1
                y0 = yt[k % 3]
                cx.dma('sp', y0[:, :], ysrcs[m][0][rows, :], writes=[y0])
                for extra in ysrcs[m][1:]:
                    k += 1
                    y1 = yt[k % 3]
                    cx.dma('sp', y1[:, :], extra[rows, :], writes=[y1])
                    cx.op('pool', [y0, y1], [y0], lambda e: e.tensor_tensor(y0[:, :], y0[:, :], y1[:, :], ALU.add))
                ybb = yb[m % 2]
                cx.op('pool', [y0], [ybb], lambda e: e.tensor_copy(ybb[:, :], y0[:, :]))
                yTb = yT[m % 2]
                trp.transpose_cols(ybb, lambda j: ybb[:, j * 128:(j + 1) * 128], 4, yTb,
                                   lambda j0, cnt: yTb[:, j0:j0 + cnt, :])
                sgb = sg[m % 2]
                cx.dma('act', sgb[:, :], P_d[rows, OFF_GATE + m * D_MODEL:OFF_GATE + (m + 1) * D_MODEL], writes=[sgb])
                for nc_ in range(4):
                    cs = slice(nc_ * 512, (nc_ + 1) * 512)
                    pm = pmm[(m * 4 + nc_) % 4]
                    for kc in range(4):
                        mm(cx, pm, pm[:, :], yTb[:, kc, :], wbr[:, m * 4 + kc, cs], [yTb, wbr], kc == 0, kc == 3)
                    if m == 0:
                        cx.op('dve', [pm, sgb], [mgb],
                              lambda e: e.tensor_tensor(mgb[:, cs], pm[:, :], sgb[:, cs], ALU.mult))
                    else:
                        tb = tmp[nc_ % 2]
                        cx.op('dve', [pm, sgb], [tb],
                              lambda e: e.tensor_tensor(tb[:, :], pm[:, :], sgb[:, cs], ALU.mult))
                        cx.op('pool', [tb, mgb], [mgb],
                              lambda e: e.tensor_tensor(mgb[:, cs], mgb[:, cs], tb[:, :], ALU.add))
            cx.dma('pool', M_d[rows, :], mgb[:, :], reads=[mgb])
    cx.barrier()


def phase_out(cx, S, M_d, wout_bf, Z_d):
    with contextlib.ExitStack() as st:
        wo = cx.sb(st, [128, 16, D_MODEL], BF16, name="wo")
        wv = wout_bf.rearrange("(kc p) n -> p kc n", p=128)
        for q in range(4):
            cx.dma('sp', wo[:, q * 4:(q + 1) * 4, :], wv[:, q * 4:(q + 1) * 4, :], writes=[wo])
        mt = [cx.sb(st, [128, D_MODEL], F32, name="mt") for _ in range(2)]
        mb = [cx.sb(st, [128, D_MODEL], BF16, name="mb") for _ in range(2)]
        mT = [cx.sb(st, [128, 16, 128], BF16, name="mT") for _ in range(2)]
        ob = [cx.sb(st, [128, 512], F32, name="ob") for _ in range(4)]
        trp = TrPool(cx, st)
        pmm = [cx.ps(st, [128, 512], F32, name="pmm") for _ in range(4)]
        k = 0
        for tt in range(S // 128):
            rows = slice(tt * 128, (tt + 1) * 128)
            mtb, mbb, mTb = mt[tt % 2], mb[tt % 2], mT[tt % 2]
            cx.dma('sp', mtb[:, :], M_d[rows, :], writes=[mtb])
            cx.op('pool', [mtb], [mbb], lambda e: e.tensor_copy(mbb[:, :], mtb[:, :]))
            trp.transpose_cols(mbb, lambda j: mbb[:, j * 128:(j + 1) * 128], 16, mTb,
                               lambda j0, cnt: mTb[:, j0:j0 + cnt, :])
            for nc_ in range(4):
                cs = slice(nc_ * 512, (nc_ + 1) * 512)
                pm = pmm[k % 4]
                obb = ob[k % 4]
                k += 1
                for kc in range(16):
                    mm(cx, pm, pm[:, :], mTb[:, kc, :], wo[:, kc, cs], [mTb, wo], kc == 0, kc == 15)
                evac(cx, 'act' if k % 2 else 'dve', pm, pm[:, :], obb, obb[:, :])
                cx.dma('pool', Z_d[rows, cs], obb[:, :], reads=[obb])
    cx.barrier()


def phase_normres(cx, S, x_d, Z_d, g_row, out_d):
    with contextlib.ExitStack() as st:
        gB = cx.sb(st, [128, D_MODEL], F32, name="gB")
        load_bcast_row(cx, 'sp', gB, g_row, D_MODEL)
        zt = [cx.sb(st, [128, D_MODEL], F32, name="zt") for _ in range(2)]
        xt = [cx.sb(st, [128, D_MODEL], F32, name="xt") for _ in range(2)]
        ot = [cx.sb(st, [128, D_MODEL], F32, name="ot") for _ in range(2)]
        junk = cx.sb(st, [128, D_MODEL], BF16, name="junk")
        ss = [cx.sb(st, [128, 1], F32, name="ss") for _ in range(2)]
        for tt in range(S // 128):
            rows = slice(tt * 128, (tt + 1) * 128)
            z, x, o, s_ = zt[tt % 2], xt[tt % 2], ot[tt % 2], ss[tt % 2]
            cx.dma('sp', z[:, :], Z_d[rows, :], writes=[z])
            cx.dma('act', x[:, :], x_d[rows, :], writes=[x])
            rms_rstd(cx, z, z[:, :], D_MODEL, s_, junk)
            cx.op('dve', [z, s_, gB], [o],
                  lambda e: e.scalar_tensor_tensor(out=o[:, :], in0=z[:, :], scalar=s_[:, 0:1], in1=gB[:, :],
                                                   op0=ALU.mult, op1=ALU.mult))
            cx.op('pool', [o, x], [o], lambda e: e.tensor_tensor(o[:, :], o[:, :], x[:, :], ALU.add))
            cx.dma('pool', out_d[rows, :], o[:, :], reads=[o])
    cx.barrier()


def phase_ffn(cx, S, x_d, g_row, wup_bf, wdn_bf, Z_d):
    G = 512 if S >= 512 else S
    NT = G // 128
    KC = D_MODEL // 128
    FC = D_FF // 128
    UW = 256
    wuv = wup_bf.rearrange("(kc p) f -> p kc f", p=128)
    wdv = wdn_bf.rearrange("(fc p) n -> p fc n", p=128)
    with contextlib.ExitStack() as st:
        gB = cx.sb(st, [128, D_MODEL], F32, name="gB")
        load_bcast_row(cx, 'sp', gB, g_row, D_MODEL)
        xt = [cx.sb(st, [128, D_MODEL], F32, name="xt") for _ in range(2)]
        junk = cx.sb(st, [128, D_MODEL], BF16, name="junk")
        ss = [cx.sb(st, [128, 1], F32, name="ss") for _ in range(2)]
        hb = [cx.sb(st, [128, D_MODEL], BF16, name="hb") for _ in range(2)]
        hT = cx.sb(st, [128, KC, G], BF16, name="hT")
        aT = cx.sb(st, [128, FC, G], BF16, name="aT")
        wu = [cx.sb(st, [128, KC, UW], BF16, name="wu") for _ in range(2)]
        wd = [cx.sb(st, [128, 8, 512], BF16, name="wd") for _ in range(2)]
        rl = [cx.sb(st, [128, G], F32, name="rl") for _ in range(2)]
        ob = [cx.sb(st, [128, 512], F32, name="ob") for _ in range(4)]
        trp = TrPool(cx, st, n=1)
        pup = [cx.ps(st, [128, G], F32, name="pup") for _ in range(2)]
        pdn = [cx.ps(st, [128, 512], F32, name="pdn") for _ in range(NT)]
        nu = 0
        nd = 0
        no = 0
        for gi in range(S // G):
            for tl in range(NT):
                tt = gi * NT + tl
                x, s_, h = xt[tt % 2], ss[tt % 2], hb[tt % 2]
                cx.dma('sp', x[:, :], x_d[tt * 128:(tt + 1) * 128, :], writes=[x])
                rms_rstd(cx, x, x[:, :], D_MODEL, s_, junk)
                cx.op('dve', [x, s_, gB], [h],
                      lambda e: e.scalar_tensor_tensor(out=h[:, :], in0=x[:, :], scalar=s_[:, 0:1], in1=gB[:, :],
                                                       op0=ALU.mult, op1=ALU.mult))
                trp.transpose_cols(h, lambda j: h[:, j * 128:(j + 1) * 128], KC, hT,
                                   lambda j0, cnt: hT[:, j0:j0 + cnt, tl * 128:(tl + 1) * 128])
            for uc in range(D_FF // UW):
                wub = wu[nu % 2]
                nu += 1
                cx.dma('sp', wub[:, :, :], wuv[:, :, uc * UW:(uc + 1) * UW], writes=[wub])
                for j in range(UW // 128):
                    fc = uc * (UW // 128) + j
                    pu = pup[fc % 2]
                    r = rl[fc % 2]
                    for kc in range(KC):
                        mm(cx, pu, pu[:, :], wub[:, kc, j * 128:(j + 1) * 128], hT[:, kc, :], [wub, hT],
                           kc == 0, kc == KC - 1)
                    cx.op('act', [pu], [r], lambda e: e.activation(out=r[:, :], in_=pu[:, :], func=AF.Relu))
                    eng = 'dve' if fc % 2 == 0 else 'pool'
                    cx.op(eng, [r], [aT], lambda e: e.tensor_tensor(aT[:, fc, :], r[:, :], r[:, :], ALU.mult))
            for nc_ in range(4):
                cs = slice(nc_ * 512, (nc_ + 1) * 512)
                for fg in range(FC // 8):
                    wdb = wd[nd % 2]
                    nd += 1
                    cx.dma('act', wdb[:, :, :], wdv[:, fg * 8:(fg + 1) * 8, cs], writes=[wdb])
                    for f8 in range(8):
                        fc = fg * 8 + f8
                        for tl in range(NT):
                            mm(cx, pdn[tl], pdn[tl][:, :], aT[:, fc, tl * 128:(tl + 1) * 128], wdb[:, f8, :],
                               [aT, wdb], fc == 0, fc == FC - 1)
                for tl in range(NT):
                    tt = gi * NT + tl
                    o = ob[no % 4]
                    no += 1
                    evac(cx, 'act' if no % 2 else 'dve', pdn[tl], pdn[tl][:, :], o, o[:, :])
                    cx.dma('pool', Z_d[tt * 128:(tt + 1) * 128, cs], o[:, :], reads=[o])
    cx.barrier()


def rope_tm(cx, src, x1, x2, c, s, dst, o1, o2, tmps, scale=None):
    (ta, tap), (tb, tbp) = tmps
    cx.op('dve', [src] + c[:1] + [], [ta], lambda e: e.tensor_tensor(tap, x1, c[1], ALU.mult))
    cx.op('pool', [src] + s[:1], [tb], lambda e: e.tensor_tensor(tbp, x2, s[1], ALU.mult))
    cx.op('dve', [ta, tb], [dst], lambda e: e.tensor_tensor(o1, tap, tbp, ALU.subtract))
    cx.op('pool', [src] + c[:1], [ta], lambda e: e.tensor_tensor(tap, x2, c[1], ALU.mult))
    cx.op('dve', [src] + s[:1], [tb], lambda e: e.tensor_tensor(tbp, x1, s[1], ALU.mult))
    cx.op('pool', [ta, tb], [dst], lambda e: e.tensor_tensor(o2, tap, tbp, ALU.add))
    if scale is not None:
        cx.op('pool', [dst], [dst], lambda e: e.tensor_scalar(o1, o1, scale, None, ALU.mult))
        cx.op('pool', [dst], [dst], lambda e: e.tensor_scalar(o2, o2, scale, None, ALU.mult))


def bcast_scalar_max(cx, st, trp_f, run_buf, out_col):
    pt = trp_f.bufs[0]
    transp(cx, pt, pt[0:1, 0, :], run_buf[:, 0:1], cx.identf[:, :], [run_buf, cx.identf])
    row = cx.sb(st, [1, 128], F32, name="mxrow")
    one = cx.sb(st, [1, 1], F32, name="mxone")
    evac(cx, 'dve', pt, pt[0:1, 0, :], row, row[:, :])
    cx.op('dve', [row], [one], lambda e: e.tensor_reduce(out=one[:, :], in_=row[:, :], axis=AX.X, op=ALU.max))
    mm(cx, pt, pt[:, 1, 0:1], cx.ones_f[0:1, :], one[0:1, 0:1], [cx.ones_f, one], True, True)
    evac(cx, 'dve', pt, pt[:, 1, 0:1], out_col, out_col[:, 0:1])


class AttnRes:
    def __init__(self, cx, st, W):
        self.sT = [cx.ps(st, [128, 512], F32, name="sT") for _ in range(2)]
        self.acc = [cx.ps(st, [128, 2, 256], F32, name="acc") for _ in range(4)]
        self.pT = [cx.sb(st, [128, 512], BF16, name="pT") for _ in range(3)]
        self.n = 0
        self.nq = 0


def attn_core(cx, res, S, qchunks, kchunks, vaug_fn, W, blocks_fn, epilogue, mode='softmax', qbs=None):
    QW = min(512, S)
    NJ = QW // 128
    for QB in (range(S // QW) if qbs is None else qbs):
        q0 = QB * QW
        blocks = blocks_fn(QB)
        accs = [res.acc[(res.nq % 2) * 2 + (j // 2)] for j in range(NJ)]
        res.nq += 1

        def stage1(b):
            sT = res.sT[res.n % 2]
            pT = res.pT[res.n % 3]
            res.n += 1
            nk, k0 = b['nk'], b['k0']
            nmm = len(qchunks) + (1 if b.get('extra') else 0)
            i = 0
            for (qb_, qap), (kb_, kap) in zip(qchunks, kchunks):
                ka = kap(k0, nk) if callable(kap) else kap[:, k0:k0 + nk]
                mm(cx, sT, sT[:nk, :QW], ka, qap[:, q0:q0 + QW], [qb_, kb_], i == 0, i == nmm - 1)
                i += 1
            if b.get('extra'):
                lb, lap, rb, rap = b['extra']
                mm(cx, sT, sT[:nk, :QW], lap, rap, list(lb) + list(rb), False, True)
            if mode == 'softmax':
                cx.op('act', [sT], [pT], lambda e: e.activation(out=pT[:nk, :QW], in_=sT[:nk, :QW], func=AF.Exp))
                if b.get('mask'):
                    mb, map_ = b['mask']
                    cx.op('pool' if nk == 128 else 'dve', [pT, mb], [pT],
                          lambda e: e.tensor_tensor(pT[:nk, :QW], pT[:nk, :QW], map_, ALU.mult))
            else:
                c, gb, gap = b['decay']
                cx.op('dve', [sT, gb], [pT],
                      lambda e: e.scalar_tensor_tensor(out=pT[:nk, :QW], in0=sT[:nk, :QW], scalar=float(c), in1=gap,
                                                       op0=ALU.mult, op1=ALU.mult))
            return pT

        def stage2(bi, b, pT):
            nk = b['nk']
            vb, vap = vaug_fn(b['kb'], nk)
            for j in range(NJ):
                a = accs[j]
                mm(cx, a, a[:, j % 2, :W], pT[:nk, j * 128:(j + 1) * 128], vap, [pT, vb],
                   bi == 0 and j % 2 == 0, bi == len(blocks) - 1, skip_group_check=True)

        prev = None
        for bi, b in enumerate(blocks):
            pT = stage1(b)
            if prev is not None:
                stage2(*prev)
            prev = (bi, b, pT)
        stage2(*prev)
        for j in range(NJ):
            epilogue(QB, j, accs[j], accs[j][:, j % 2, :W])


def causal_blocks(QB, QW, cmask):
    out = []
    nd = QW // 128
    for kb in range(nd * (QB + 1)):
        i = kb - nd * QB
        out.append(dict(kb=kb, k0=kb * 128, nk=128, mask=(cmask, cmask[:, i, :QW]) if i >= 0 else None))
    return out


MLA_SCALE = 192 ** -0.5


def phase_mla(cx, S, P_d, gq_row, gkv_row, wuq_bf, wukv_bf, cos_d, sin_d, cmask_d, Y_d, scr):
    NT = S // 128
    G = min(512, S)
    NG = G // 128
    with contextlib.ExitStack() as st:
        with contextlib.ExitStack() as s1:
            gq = cx.sb(s1, [128, 384], F32, name="gq")
            gkv = cx.sb(s1, [128, 128], F32, name="gkv")
            load_bcast_row(cx, 'sp', gq, gq_row, 384)
            load_bcast_row(cx, 'sp', gkv, gkv_row, 128)
            wuq = cx.sb(s1, [128, 3, 768], BF16, name="wuq")
            cx.dma('sp', wuq[:, :, :], wuq_bf.rearrange("(kc p) n -> p kc n", p=128), writes=[wuq])
            wukv = cx.sb(s1, [128, 1024], BF16, name="wukv")
            cx.dma('sp', wukv[:, :], wukv_bf[:, :], writes=[wukv])
            pm = [cx.sb(s1, [128, 576], F32, name="pm") for _ in range(2)]
            cs_t = [cx.sb(s1, [128, 64], F32, name="cs") for _ in range(2)]
            junk = cx.sb(s1, [128, 768], BF16, name="junk")
            ss = [cx.sb(s1, [128, 1], F32, name="ss") for _ in range(2)]
            nb = [cx.sb(s1, [128, 384], BF16, name="nb") for _ in range(2)]
            nT = [cx.sb(s1, [128, 3, 128], BF16, name="nT") for _ in range(2)]
            qf = [cx.sb(s1, [128, 4, 256], F32, name="qf") for _ in range(2)]
            qs = [cx.sb(s1, [128, 4, 193], BF16, name="qs") for _ in range(2)]
            qsf = [cx.sb(s1, [128, 4, 64], F32, name="qsf") for _ in range(2)]
            t1s = [cx.sb(s1, [128, 4, 32], F32, name="t1") for _ in range(2)]
            t2s = [cx.sb(s1, [128, 4, 32], F32, name="t2") for _ in range(2)]
            kr = [cx.sb(s1, [128, 65], BF16, name="kr") for _ in range(2)]
            krf = [cx.sb(s1, [128, 64], F32, name="krf") for _ in range(2)]
            kb16 = [cx.sb(s1, [128, 4, 128], BF16, name="kb16") for _ in range(2)]
            va = [cx.sb(s1, [128, 4, 129], BF16, name="va") for _ in range(2)]
            sqs = [cx.sb(s1, [128, 4, 256], F32, name="sq") for _ in range(2)]
            n4 = [cx.sb(s1, [128, 4], F32, name="n4") for _ in range(2)]
            n1 = [cx.sb(s1, [128, 1], F32, name="n1") for _ in range(2)]
            kmx = cx.sb(s1, [128, 1], F32, name="kmx")
            kmax = cx.sb(s1, [128, 1], F32, name="kmax")
            gA = cx.sb(s1, [128, 4, G], BF16, name="gA")
            gB_ = cx.sb(s1, [65, 4, G], BF16, name="gB_")
            trp = TrPool(cx, s1)
            trf = TrPool(cx, s1, n=1, dtype=F32)
            pq = [cx.ps(s1, [128, 512], F32, name="pq") for _ in range(2)]
            pq2 = [cx.ps(s1, [128, 512], F32, name="pq2") for _ in range(2)]
            cx.op('pool', [], [kmx], lambda e: e.memset(kmx[:, :], 0.0))

            def load_norm_T(tt, c0, n, gbuf, k):
                p = pm[k % 2]
                cx.dma('sp', p[:, :], P_d[tt * 128:(tt + 1) * 128, OFF_MLA:OFF_MLA + 576], writes=[p])
                s_ = ss[k % 2]
                rms_rstd(cx, p, p[:, c0:c0 + n], n, s_, junk)
                nbb = nb[k % 2]
                cx.op('dve', [p, s_, gbuf], [nbb],
                      lambda e: e.scalar_tensor_tensor(out=nbb[:, :n], in0=p[:, c0:c0 + n], scalar=s_[:, 0:1],
                                                       in1=gbuf[:, :n], op0=ALU.mult, op1=ALU.mult))
                nTb = nT[k % 2]
                trp.transpose_cols(nbb, lambda j: nbb[:, j * 128:(j + 1) * 128], n // 128, nTb,
                                   lambda j0, cnt: nTb[:, j0:j0 + cnt, :])
                return p, nTb

            def body(tt):
                t1, t2, sq = t1s[tt % 2], t2s[tt % 2], sqs[tt % 2]
                tl = tt % NG
                p, nTb = load_norm_T(tt, 384, 128, gkv, tt)
                c_t = cs_t[tt % 2]
                cx.dma('act', c_t[:, 0:32], cos_d[tt * 128:(tt + 1) * 128, :], writes=[c_t])
                cx.dma('act', c_t[:, 32:64], sin_d[tt * 128:(tt + 1) * 128, :], writes=[c_t])
                pa, pb = pq[tt % 2], pq2[tt % 2]
                mm(cx, pa, pa[:, :], nTb[:, 0, :], wukv[:, 0:512], [nTb, wukv], True, True)
                mm(cx, pb, pb[:, :], nTb[:, 0, :], wukv[:, 512:1024], [nTb, wukv], True, True)
                q = qf[tt % 2]
                evac(cx, 'act', pa, pa[:, :].rearrange("p (h c) -> p h c", h=2), q, q[:, 0:2, :])
                evac(cx, 'dve', pb, pb[:, :].rearrange("p (h c) -> p h c", h=2), q, q[:, 2:4, :])
                k16, vab, krb, krfb = kb16[tt % 2], va[tt % 2], kr[tt % 2], krf[tt % 2]
                cx.op('pool', [q], [k16], lambda e: e.tensor_copy(k16[:, :, :], q[:, :, 0:128]))
                cx.op('pool', [q], [vab], lambda e: e.tensor_copy(vab[:, :, 0:128], q[:, :, 128:256]))
                cx.op('pool', [], [vab], lambda e: e.memset(vab[:, :, 128:129], 1.0))
                rope_tm(cx, p, p[:, 512:544], p[:, 544:576], [c_t, c_t[:, 0:32]], [c_t, c_t[:, 32:64]],
                        krfb, krfb[:, 0:32], krfb[:, 32:64], [(t1, t1[:, 0, :]), (t2, t2[:, 0, :])])
                cx.op('pool', [krfb], [krb], lambda e: e.tensor_copy(krb[:, 0:64], krfb[:, :]))
                cx.op('pool', [], [krb], lambda e: e.memset(krb[:, 64:65], 1.0))
                cx.op('dve', [q], [sq], lambda e: e.tensor_tensor(sq[:, :, 0:128], q[:, :, 0:128], q[:, :, 0:128], ALU.mult))
                n4b, n1b = n4[tt % 2], n1[tt % 2]
                cx.op('dve', [sq], [n4b], lambda e: e.tensor_reduce(out=n4b[:, :], in_=sq[:, :, 0:128], axis=AX.X, op=ALU.add))
                cx.op('dve', [n4b], [n1b], lambda e: e.tensor_reduce(out=n1b[:, :], in_=n4b[:, :], axis=AX.X, op=ALU.max))
                cx.op('act', [krfb], [junk, n4b],
                      lambda e: e.activation(out=junk[:, :64], in_=krfb[:, :], func=AF.Square, accum_out=n4b[:, 0:1]))
                cx.op('dve', [n4b, n1b], [n1b], lambda e: e.tensor_tensor(n1b[:, :], n1b[:, :], n4b[:, 0:1], ALU.add))
                cx.op('dve', [n1b, kmx], [kmx], lambda e: e.tensor_tensor(kmx[:, :], kmx[:, :], n1b[:, :], ALU.max))
                trp.transpose_cols(k16, lambda j: k16[:, j, :], 4, gA,
                                   lambda j0, cnt: gA[:, j0:j0 + cnt, tl * 128:(tl + 1) * 128])
                trp.transpose_cols(krb, lambda j: krb[:, :], 1, gB_,
                                   lambda j0, cnt: gB_[:65, 0:1, tl * 128:(tl + 1) * 128], blkw=65)
                cx.dma('pool', scr['va'][tt * 128:(tt + 1) * 128, :, :], vab[:, :, :], reads=[vab])
                if tl == NG - 1:
                    def post():
                        g0 = (tt // NG) * G
                        cx.dma('pool', scr['knT'][:, :, g0:g0 + G].rearrange("h d s -> d h s"), gA[:, :, :], reads=[gA])
                        cx.dma('pool', scr['krT'][:, g0:g0 + G], gB_[:65, 0, :], reads=[gB_])
                    return post
            run_tiles(cx, body, NT)
            bcast_scalar_max(cx, s1, trf, kmx, kmax)
            cx.op('act', [kmax], [kmax], lambda e: e.activation(out=kmax[:, :], in_=kmax[:, :], func=AF.Sqrt))
            def body(tt):
                t1, t2, sq = t1s[tt % 2], t2s[tt % 2], sqs[tt % 2]
                tl = tt % NG
                p, nTb = load_norm_T(tt, 0, 384, gq, tt)
                c_t = cs_t[tt % 2]
                cx.dma('act', c_t[:, 0:32], cos_d[tt * 128:(tt + 1) * 128, :], writes=[c_t])
                cx.dma('act', c_t[:, 32:64], sin_d[tt * 128:(tt + 1) * 128, :], writes=[c_t])
                pa, pb = pq[tt % 2], pq2[tt % 2]
                for kc in range(3):
                    mm(cx, pa, pa[:, :], nTb[:, kc, :], wuq[:, kc, 0:512], [nTb, wuq], kc == 0, kc == 2)
                for kc in range(3):
                    mm(cx, pb, pb[:, :256], nTb[:, kc, :], wuq[:, kc, 512:768], [nTb, wuq], kc == 0, kc == 2)
                q = qf[tt % 2]
                qv = q[:, :, :].rearrange("p h c -> p (h c)")
                evac(cx, 'act', pa, pa[:, :], q, qv[:, 0:512])
                evac(cx, 'dve', pb, pb[:, :256], q, qv[:, 512:768])
                qh = qv[:, 0:768].rearrange("p (h c) -> p h c", h=4)
                cx.op('dve', [q], [sq], lambda e: e.tensor_tensor(sq[:, :, 0:192], qh, qh, ALU.mult))
                n4b = n4[tt % 2]
                cx.op('dve', [sq], [n4b], lambda e: e.tensor_reduce(out=n4b[:, :], in_=sq[:, :, 0:192], axis=AX.X, op=ALU.add))
                cx.op('act', [n4b], [n4b], lambda e: e.activation(out=n4b[:, :], in_=n4b[:, :], func=AF.Sqrt))
                cx.op('dve', [n4b, kmax], [n4b],
                      lambda e: e.tensor_scalar(n4b[:, :], n4b[:, :], kmax[:, 0:1], -MLA_SCALE, ALU.mult, ALU.mult))
                qsb, qsfb = qs[tt % 2], qsf[tt % 2]
                cb = c_t[:, 0:32].unsqueeze(1).broadcast_to([128, 4, 32])
                sb_ = c_t[:, 32:64].unsqueeze(1).broadcast_to([128, 4, 32])
                rope_tm(cx, q, qh[:, :, 128:160], qh[:, :, 160:192], [c_t, cb], [c_t, sb_],
                        qsfb, qsfb[:, :, 0:32], qsfb[:, :, 32:64], [(t1, t1[:, :, :]), (t2, t2[:, :, :])])
                cx.op('act', [q], [qsb], lambda e: e.activation(out=qsb[:, :, 0:128], in_=qh[:, :, 0:128], func=AF.Copy, scale=MLA_SCALE))
                cx.op('act', [qsfb], [qsb], lambda e: e.activation(out=qsb[:, :, 128:192], in_=qsfb[:, :, :], func=AF.Copy, scale=MLA_SCALE))
                cx.op('pool', [n4b], [qsb], lambda e: e.tensor_copy(qsb[:, :, 192:193], n4b[:, :].unsqueeze(2)))
                trp.transpose_cols(qsb, lambda j: qsb[:, j, 0:128], 4, gA,
                                   lambda j0, cnt: gA[:, j0:j0 + cnt, tl * 128:(tl + 1) * 128])
                trp.transpose_cols(qsb, lambda j: qsb[:, j, 128:193], 4, gB_,
                                   lambda j0, cnt: gB_[:65, j0:j0 + cnt, tl * 128:(tl + 1) * 128], blkw=65)
                if tl == NG - 1:
                    def post():
                        g0 = (tt // NG) * G
                        cx.dma('pool', scr['qnT'][:, :, g0:g0 + G].rearrange("h d s -> d h s"), gA[:, :, :], reads=[gA])
                        cx.dma('pool', scr['qrT'][:, :, g0:g0 + G].rearrange("h d s -> d h s"), gB_[:65, :, :], reads=[gB_])
                    return post
            run_tiles(cx, body, NT)
        cx.barrier()
        res = AttnRes(cx, st, 129)
        cmask = cx.sb(st, [128, 4, 512], BF16, name="cmask")
        cx.dma('sp', cmask[:, :, :], cmask_d.rearrange("i k q -> k i q"), writes=[cmask])
        krT = cx.sb(st, [65, S], BF16, name="krT")
        cx.dma('sp', krT[:, :], scr['krT'][:, :], writes=[krT])
        qn = [cx.sb(st, [128, S], BF16, name="qn") for _ in range(2)]
        qr = [cx.sb(st, [65, S], BF16, name="qr") for _ in range(2)]
        kn = [cx.sb(st, [128, S], BF16, name="kn") for _ in range(2)]
        vv = [cx.sb(st, [128, NT, 129], BF16, name="vv") for _ in range(2)]
        rc = [cx.sb(st, [128, 1], F32, name="rc") for _ in range(2)]
        ot = [cx.sb(st, [128, 128], F32, name="ot") for _ in range(2)]
        cnt = [0]
        QW = min(512, S)
        for h in range(4):
            a, b, c, v = qn[h % 2], qr[h % 2], kn[h % 2], vv[h % 2]
            cx.dma('sp', a[:, :], scr['qnT'][h], writes=[a])
            cx.dma('sp', b[:, :], scr['qrT'][h], writes=[b])
            cx.dma('sp', c[:, :], scr['knT'][h], writes=[c])
            cx.dma('sp', v[:, :, :], scr['va'][:, h, :].rearrange("(t p) c -> p t c", p=128), writes=[v])

            def epi(QB, j, accb, acc_ap, h=h):
                k = cnt[0]
                cnt[0] += 1
                r, o = rc[k % 2], ot[k % 2]
                cx.op('dve', [accb], [r], lambda e: e.tensor_scalar(r[:, :], acc_ap[:, 128:129], 1e-30, None, ALU.add))
                cx.op('dve', [r], [r], lambda e: e.reciprocal(r[:, :], r[:, :]))
                cx.op('act', [accb, r], [o], lambda e: e.activation(out=o[:, :], in_=acc_ap[:, 0:128], func=AF.Copy, scale=r[:, 0:1]))
                t0 = QB * QW + j * 128
                cx.dma('pool', Y_d[t0:t0 + 128, h * 128:(h + 1) * 128], o[:, :], reads=[o])

            attn_core(cx, res, S, [(a, a[:, :]), (b, b[:65, :])], [(c, c[:, :]), (krT, krT[:65, :])],
                      lambda kb, nk, v=v: (v, v[:nk, kb, :]), 129,
                      lambda QB: causal_blocks(QB, QW, cmask), epi)
    cx.barrier()


def head_norm_tm(cx, src_buf, src_ap, n, eps, cbuf, c_ap, s1, s2, junk):
    cx.op('dve', [src_buf], [s1], lambda e: e.tensor_reduce(out=s1[:, 0:1], in_=src_ap, axis=AX.X, op=ALU.add))
    cx.op('dve', [s1], [s1], lambda e: e.tensor_scalar(s1[:, 0:1], s1[:, 0:1], 1.0 / n, None, ALU.mult))
    cx.op('dve', [src_buf, s1], [cbuf], lambda e: e.tensor_scalar(c_ap, src_ap, s1[:, 0:1], None, ALU.subtract))
    rms_rstd(cx, cbuf, c_ap, n, s2, junk, eps=eps)


def phase_ret(cx, S, P_d, cos_d, sin_d, gdec_d, Y_d, scr):
    NT = S // 128
    G = min(512, S)
    NG = G // 128
    QW = min(512, S)
    with contextlib.ExitStack() as st:
        with contextlib.ExitStack() as s1:
            pr = [cx.sb(s1, [128, 1024], F32, name="pr") for _ in range(2)]
            cs_t = [cx.sb(s1, [128, 64], F32, name="cs") for _ in range(2)]
            ro = [cx.sb(s1, [128, 8, 64], F32, name="ro") for _ in range(2)]
            rb = [cx.sb(s1, [128, 8, 64], BF16, name="rb") for _ in range(2)]
            vb = [cx.sb(s1, [128, 512], BF16, name="vb") for _ in range(2)]
            t1s = [cx.sb(s1, [128, 8, 32], F32, name="t1") for _ in range(2)]
            t2s = [cx.sb(s1, [128, 8, 32], F32, name="t2") for _ in range(2)]
            gQ = cx.sb(s1, [64, 8, G], BF16, name="gQ")
            trp = TrPool(cx, s1)
            def body(tt):
                t1, t2 = t1s[tt % 2], t2s[tt % 2]
                tl = tt % NG
                rows = slice(tt * 128, (tt + 1) * 128)
                p, c_t, r, rbb, v = pr[tt % 2], cs_t[tt % 2], ro[tt % 2], rb[tt % 2], vb[tt % 2]
                cx.dma('sp', p[:, :], P_d[rows, OFF_RET:OFF_RET + 1024], writes=[p])
                cx.dma('act', c_t[:, 0:32], cos_d[rows, :], writes=[c_t])
                cx.dma('act', c_t[:, 32:64], sin_d[rows, :], writes=[c_t])
                qk = p[:, 0:512].rearrange("p (h c) -> p h c", h=8)
                cb = c_t[:, 0:32].unsqueeze(1).broadcast_to([128, 8, 32])
                sb_ = c_t[:, 32:64].unsqueeze(1).broadcast_to([128, 8, 32])
                rope_tm(cx, p, qk[:, :, 0:32], qk[:, :, 32:64], [c_t, cb], [c_t, sb_],
                        r, r[:, :, 0:32], r[:, :, 32:64], [(t1, t1[:, :, :]), (t2, t2[:, :, :])])
                cx.op('act', [r], [rbb], lambda e: e.copy(rbb[:, 0:4, :], r[:, 0:4, :]))
                cx.op('act', [r], [rbb], lambda e: e.activation(out=rbb[:, 4:8, :], in_=r[:, 4:8, :], func=AF.Copy, scale=0.125))
                cx.op('pool', [p], [v], lambda e: e.tensor_copy(v[:, :], p[:, 512:1024]))
                trp.transpose_cols(rbb, lambda j: rbb[:, j, :], 8, gQ,
                                   lambda j0, cnt: gQ[:64, j0:j0 + cnt, tl * 128:(tl + 1) * 128], blkw=64)
                cx.dma('pool', scr['rv'][rows, :, :].rearrange("s h c -> s (h c)"), v[:, :], reads=[v])
                if tl == NG - 1:
                    def post():
                        g0 = (tt // NG) * G
                        cx.dma('pool', scr['rqT'][:, :, g0:g0 + G].rearrange("h d s -> d h s"), gQ[:64, 0:4, :], reads=[gQ])
                        cx.dma('pool', scr['rkT'][:, :, g0:g0 + G].rearrange("h d s -> d h s"), gQ[:64, 4:8, :], reads=[gQ])
                    return post
            run_tiles(cx, body, NT)
        cx.barrier()
        res = AttnRes(cx, st, 128)
        gd = [cx.sb(st, [128, 5, 512], F32, name="gd") for _ in range(2)]
        qT = [cx.sb(st, [64, S], BF16, name="qT") for _ in range(2)]
        kT = [cx.sb(st, [64, S], BF16, name="kT") for _ in range(2)]
        vv = [cx.sb(st, [128, NT, 128], BF16, name="vv") for _ in range(2)]
        gt = [cx.sb(st, [128, 128], F32, name="gt") for _ in range(2)]
        cb_ = [cx.sb(st, [128, 128], F32, name="cb") for _ in range(2)]
        ot = [cx.sb(st, [128, 128], F32, name="ot") for _ in range(2)]
        sA = [cx.sb(st, [128, 1], F32, name="sA") for _ in range(2)]
        sB = [cx.sb(st, [128, 1], F32, name="sB") for _ in range(2)]
        junk = cx.sb(st, [128, 128], BF16, name="junk")
        cnt = [0]
        for h in range(4):
            gamma = 1.0 - 2.0 ** (-5 - h)
            a, c, v, g = qT[h % 2], kT[h % 2], vv[h % 2], gd[h % 2]
            cx.dma('sp', a[:, :], scr['rqT'][h], writes=[a])
            cx.dma('sp', c[:, :], scr['rkT'][h], writes=[c])
            cx.dma('sp', v[:, :, :], scr['rv'][:, h, :].rearrange("(t p) c -> p t c", p=128), writes=[v])
            cx.dma('sp', g[:, :, :], gdec_d[h].rearrange("i k q -> k i q"), writes=[g])
            if 'dbg' in scr and h == 0:
                cx.dma('sp', scr['dbg'][0], a[:, :], reads=[a])
                cx.dma('sp', scr['dbg'][1], c[:, :], reads=[c])

            def blocks(QB, g=g, gamma=gamma):
                out = []
                nd = QW // 128
                for kb in range(nd * (QB + 1)):
                    i = kb - nd * QB
                    if i >= 0:
                        out.append(dict(kb=kb, k0=kb * 128, nk=128, decay=(1.0, g, g[:, 1 + i, :QW])))
                    else:
                        cc = gamma ** (QB * QW - kb * 128)
                        if cc < 1e-30:
                            cc = 0.0
                        out.append(dict(kb=kb, k0=kb * 128, nk=128, decay=(cc, g, g[:, 0, :QW])))
                return out

            def epi(QB, j, accb, acc_ap, h=h):
                k = cnt[0]
                cnt[0] += 1
                t0 = QB * QW + j * 128
                gtb, cbb, o, s_a, s_b = gt[k % 2], cb_[k % 2], ot[k % 2], sA[k % 2], sB[k % 2]
                cx.dma('act', gtb[:, :], P_d[t0:t0 + 128, OFF_RET + 1024 + h * 128:OFF_RET + 1024 + (h + 1) * 128], writes=[gtb])
                cx.op('act', [gtb], [gtb], lambda e: e.activation(out=gtb[:, :], in_=gtb[:, :], func=AF.Silu))
                head_norm_tm(cx, accb, acc_ap, 128, NORM_EPS, cbb, cbb[:, :], s_a, s_b, junk)
                cx.op('dve', [cbb, s_b, gtb], [o],
                      lambda e: e.scalar_tensor_tensor(out=o[:, :], in0=cbb[:, :], scalar=s_b[:, 0:1], in1=gtb[:, :],
                                                       op0=ALU.mult, op1=ALU.mult))
                cx.dma('pool', Y_d[t0:t0 + 128, h * 128:(h + 1) * 128], o[:, :], reads=[o])

            attn_core(cx, res, S, [(a, a[:64, :])], [(c, c[:64, :])], lambda kb, nk, v=v: (v, v[:nk, kb, :]), 128,
                      blocks, epi, mode='decay')
    cx.barrier()


def bc8(ap):
    return ap.unsqueeze(2).broadcast_to([128, 8, 64])


def v3(ap):
    return ap.rearrange("p (h c) -> p h c", h=8)


def phase_rwkv(cx, S, l, P_d, W, Wb, C, Y_d, scr):
    NT = S // 128
    RW = scr['rw']
    names6 = ['rr', 'lw', 'k2', 'vv', 'kn', 'aa']
    with contextlib.ExitStack() as st:
        def brow(name, n, src):
            b = cx.sb(st, [128, n], F32, name=name)
            load_bcast_row(cx, 'sp', b, src, n)
            return b
        muB = brow("muB", 1984, W['rwkv_mu'][l])
        w0B = brow("w0B", 512, W['rwkv_w0'][l])
        a0B = brow("a0B", 512, W['rwkv_a0'][l])
        kkB = brow("kkB", 512, W['rwkv_k_k'][l])
        kaB = brow("kaB", 512, W['rwkv_k_a'][l])
        rkB = brow("rkB", 512, W['rwkv_r_k'][l].rearrange("h c -> (h c)"))
        w2 = cx.sb(st, [96, 512], BF16, name="w2")
        a2 = cx.sb(st, [96, 512], BF16, name="a2")
        g2 = cx.sb(st, [128, 2, 512], BF16, name="g2")
        cx.dma('sp', w2[:, :], Wb['rwkv_w2'][l], writes=[w2])
        cx.dma('sp', a2[:, :], Wb['rwkv_a2'][l], writes=[a2])
        cx.dma('sp', g2[:, :, :], Wb['rwkv_g2'][l].rearrange("(kc p) n -> p kc n", p=128), writes=[g2])
        z = [cx.sb(st, [128, 1984], F32, name="z") for _ in range(2)]
        zp = [cx.sb(st, [128, 1984], F32, name="zp") for _ in range(2)]
        lo = [cx.sb(st, [128, 512], BF16, name="lo") for _ in range(2)]
        loT = [cx.sb(st, [128, 4, 128], BF16, name="loT") for _ in range(2)]
        o7 = [cx.sb(st, [128, 7, 512], F32, name="o7") for _ in range(2)]
        s8 = [cx.sb(st, [128, 8], F32, name="s8") for _ in range(2)]
        b8 = [cx.sb(st, [128, 8], F32, name="b8") for _ in range(2)]
        trp = TrPool(cx, st)
        pps = [[cx.ps(st, [128, 512], F32, name="pp") for _ in range(3)] for _ in range(2)]
        def body(tt):
            rows = slice(tt * 128, (tt + 1) * 128)
            zb, zpb, lob, loTb, o, s8b, b8b = z[tt % 2], zp[tt % 2], lo[tt % 2], loT[tt % 2], o7[tt % 2], s8[tt % 2], b8[tt % 2]
            cx.dma('sp', zb[:, :], P_d[rows, OFF_RWKV:OFF_RWKV + 1984], writes=[zb])
            if tt == 0:
                cx.op('pool', [], [zpb], lambda e: e.memset(zpb[0:1, :], 0.0))
                cx.dma('act', zpb[1:128, :], P_d[0:127, OFF_RWKV:OFF_RWKV + 1984], writes=[zpb])
            else:
                cx.dma('act', zpb[:, :], P_d[tt * 128 - 1:tt * 128 + 127, OFF_RWKV:OFF_RWKV + 1984], writes=[zpb])
            cx.op('pool', [zpb, zb], [zpb], lambda e: e.tensor_tensor(zpb[:, :], zpb[:, :], zb[:, :], ALU.subtract))
            cx.op('dve', [zpb, muB], [zpb], lambda e: e.tensor_tensor(zpb[:, :], zpb[:, :], muB[:, :], ALU.mult))
            cx.op('pool', [zpb, zb], [zb], lambda e: e.tensor_tensor(zb[:, :], zb[:, :], zpb[:, :], ALU.add))
            r_, k_, v_ = zb[:, 0:512], zb[:, 512:1024], zb[:, 1024:1536]
            cx.op('act', [zb], [lob], lambda e: e.activation(out=lob[:, 0:96], in_=zb[:, 1536:1632], func=AF.Tanh))
            cx.op('act', [zb], [lob], lambda e: e.copy(lob[:, 128:224], zb[:, 1632:1728]))
            cx.op('act', [zb], [lob], lambda e: e.activation(out=lob[:, 256:512], in_=zb[:, 1728:1984], func=AF.Sigmoid))
            trp.transpose_cols(lob, lambda j: lob[:, j * 128:j * 128 + 96], 2, loTb,
                               lambda j0, cnt: loTb[:96, j0:j0 + cnt, :], blkw=96)
            trp.transpose_cols(lob, lambda j: lob[:, 256 + j * 128:384 + j * 128], 2, loTb,
                               lambda j0, cnt: loTb[:, 2 + j0:2 + j0 + cnt, :])
            pu, pa, pg = pps[tt % 2]
            mm(cx, pu, pu[:, :], loTb[:96, 0, :], w2[:96, :], [loTb, w2], True, True)
            mm(cx, pa, pa[:, :], loTb[:96, 1, :], a2[:96, :], [loTb, a2], True, True)
            mm(cx, pg, pg[:, :], loTb[:, 2, :], g2[:, 0, :], [loTb, g2], True, False)
            mm(cx, pg, pg[:, :], loTb[:, 3, :], g2[:, 1, :], [loTb, g2], False, True)
            lw_, k2_, kn_, aa_, gg_, t1_, t2_ = [o[:, i, :] for i in range(7)]
            cx.op('dve', [pu, w0B], [o], lambda e: e.tensor_tensor(t1_, pu[:, :], w0B[:, :], ALU.add))
            cx.op('act', [o], [o], lambda e: e.activation(out=t1_, in_=t1_, func=AF.Sigmoid))
            cx.op('pool', [o], [o], lambda e: e.tensor_scalar(lw_, t1_, -0.6065306597126334, None, ALU.mult))
            cx.op('dve', [pa, a0B], [o], lambda e: e.tensor_tensor(t2_, pa[:, :], a0B[:, :], ALU.add))
            cx.op('act', [o], [o], lambda e: e.activation(out=aa_, in_=t2_, func=AF.Sigmoid))
            cx.op('act', [pg], [o], lambda e: e.copy(gg_, pg[:, :]))
            cx.op('dve', [zb, kkB], [o], lambda e: e.tensor_tensor(kn_, k_, kkB[:, :], ALU.mult))
            cx.op('pool', [o], [o], lambda e: e.tensor_tensor(t1_, kn_, kn_, ALU.mult))
            cx.op('dve', [o], [s8b], lambda e: e.tensor_reduce(out=s8b[:, :], in_=v3(t1_), axis=AX.X, op=ALU.add))
            cx.op('dve', [s8b], [s8b], lambda e: e.tensor_scalar(s8b[:, :], s8b[:, :], 1e-24, None, ALU.max))
            cx.op('pool', [s8b, cx.neghalf], [s8b],
                  lambda e: e.tensor_tensor(s8b[:, :], s8b[:, :], cx.neghalf[:, 0:1].to_broadcast([128, 8]), ALU.pow))
            cx.op('dve', [o, s8b], [o], lambda e: e.tensor_tensor(v3(kn_), v3(kn_), bc8(s8b[:, :]), ALU.mult))
            cx.op('dve', [o, kaB], [o],
                  lambda e: e.scalar_tensor_tensor(out=t2_, in0=aa_, scalar=-1.0, in1=kaB[:, :], op0=ALU.add, op1=ALU.mult))
            cx.op('pool', [o], [o], lambda e: e.tensor_scalar(t2_, t2_, 1.0, None, ALU.add))
            cx.op('dve', [o, zb], [o], lambda e: e.tensor_tensor(k2_, k_, t2_, ALU.mult))
            cx.op('pool', [o, zb], [o], lambda e: e.tensor_tensor(t1_, r_, k2_, ALU.mult))
            cx.op('dve', [o, rkB], [o], lambda e: e.tensor_tensor(t1_, t1_, rkB[:, :], ALU.mult))
            cx.op('dve', [o], [b8b], lambda e: e.tensor_reduce(out=b8b[:, :], in_=v3(t1_), axis=AX.X, op=ALU.add))
            cx.dma('pool', RW['rr'][rows, :], r_, reads=[zb])
            cx.dma('pool', RW['vv'][rows, :], v_, reads=[zb])
            cx.dma('pool', RW['lw'][rows, :], lw_, reads=[o])
            cx.dma('pool', RW['k2'][rows, :], k2_, reads=[o])
            cx.dma('pool', RW['kn'][rows, :], kn_, reads=[o])
            cx.dma('pool', RW['aa'][rows, :], aa_, reads=[o])
            cx.dma('pool', RW['gg'][rows, :], gg_, reads=[o])
            cx.dma('pool', RW['bc'][rows, :], b8b[:, :], reads=[b8b])
        run_tiles(cx, body, NT)
    cx.barrier()
    with contextlib.ExitStack() as st:
        rwm = cx.sb(st, [128, 384], F32, name="rwm")
        cx.dma('sp', rwm[:, :], C['rwm'][:, :], writes=[rwm])
        mask4 = cx.sb(st, [128, 512], F32, name="mask4")
        cx.op('pool', [rwm], [mask4], lambda e: e.tensor_copy(mask4[:, 0:256], rwm[:, 0:256]))
        cx.op('pool', [rwm], [mask4], lambda e: e.tensor_copy(mask4[:, 256:512], rwm[:, 0:256]))
        gwB = cx.sb(st, [128, 512], F32, name="gwB")
        gbB = cx.sb(st, [128, 512], F32, name="gbB")
        load_bcast_row(cx, 'sp', gwB, W['rwkv_gn_w'][l], 512)
        load_bcast_row(cx, 'sp', gbB, W['rwkv_gn_b'][l], 512)
        IN = [cx.sb(st, [128, 6, 512], F32, name="IN") for _ in range(2)]
        EL = cx.sb(st, [128, 3, 512], F32, name="EL")
        TM = cx.sb(st, [128, 4, 512], F32, name="TM")
        XT = cx.sb(st, [64, 8, 4, 128], F32, name="XT")
        MM_ = cx.sb(st, [128, 8, 512], F32, name="MM")
        XX = [cx.sb(st, [128, 8, 2, 128], F32, name="XX") for _ in range(2)]
        NTb = cx.sb(st, [128, 8, 128], F32, name="NT")
        ST = cx.sb(st, [64, 8, 64], F32, name="ST")
        STs = cx.sb(st, [64, 8, 64], F32, name="STs")
        pc = cx.sb(st, [64, 8], F32, name="pc")
        Yb = cx.sb(st, [128, 8, 64], F32, name="Yb")
        Ub = cx.sb(st, [128, 8, 64], F32, name="Ub")
        Ob = cx.sb(st, [128, 512], F32, name="Ob")
        G3 = [cx.sb(st, [128, 512], F32, name="G3") for _ in range(2)]
        b8 = [cx.sb(st, [128, 8], F32, name="b8") for _ in range(2)]
        m8 = cx.sb(st, [128, 8], F32, name="m8")
        r8 = cx.sb(st, [128, 8], F32, name="r8")
        t512 = cx.sb(st, [128, 512], F32, name="t512")
        yo = [cx.sb(st, [128, 512], F32, name="yo") for _ in range(2)]
        ptr = [cx.ps(st, [128, 512], F32, name="ptr") for _ in range(2)]
        pA = cx.ps(st, [128, 512], F32, name="pA")
        pD = [cx.ps(st, [128, 4, 128], F32, name="pD") for _ in range(2)]
        pY = cx.ps(st, [128, 512], F32, name="pY")
        pU = cx.ps(st, [128, 512], F32, name="pU")
        pO = cx.ps(st, [128, 512], F32, name="pO")
        cx.op('pool', [], [ST], lambda e: e.memset(ST[:, :, :], 0.0))
        MUs, MUi, MLs = rwm[:, 0:128], rwm[:, 128:256], rwm[:, 256:384]
        for c in range(NT):
            rows = slice(c * 128, (c + 1) * 128)
            I6 = IN[c % 2]
            for i, nm in enumerate(names6):
                cx.dma('sp' if i % 2 == 0 else 'act', I6[:, i, :], RW[nm][rows, :], writes=[I6])
            rr, lw, k2, vv, kn, aa = [I6[:, i, :] for i in range(6)]
            pL = ptr[0]
            mm(cx, pL, pL[:, :], MUi, lw, [rwm, I6], True, True)
            cx.op('act', [pL], [EL], lambda e: e.activation(out=EL[:, 0, :], in_=pL[:, :], func=AF.Exp))
            cx.op('act', [pL], [EL], lambda e: e.activation(out=EL[:, 1, :], in_=pL[:, :], func=AF.Exp, scale=-1.0))
            cx.op('dve', [pL, I6], [EL], lambda e: e.tensor_tensor(EL[:, 2, :], pL[:, :], lw, ALU.subtract))
            cx.op('act', [EL], [EL], lambda e: e.activation(out=EL[:, 2, :], in_=EL[:, 2, :], func=AF.Exp))
            cx.op('dve', [I6, EL], [TM],
                  lambda e: e.scalar_tensor_tensor(out=TM[:, 0, :], in0=kn, scalar=-1.0, in1=EL[:, 2, :], op0=ALU.mult, op1=ALU.mult))
            cx.op('pool', [I6, EL], [TM], lambda e: e.tensor_tensor(TM[:, 1, :], rr, EL[:, 0, :], ALU.mult))
            cx.op('dve', [I6], [TM], lambda e: e.tensor_tensor(TM[:, 2, :], kn, aa, ALU.mult))
            cx.op('dve', [TM, EL], [TM], lambda e: e.tensor_tensor(TM[:, 2, :], TM[:, 2, :], EL[:, 1, :], ALU.mult))
            cx.op('pool', [I6, EL], [TM], lambda e: e.tensor_tensor(TM[:, 3, :], k2, EL[:, 1, :], ALU.mult))
            ppc = ptr[1]
            for h in range(8):
                mm(cx, ppc, ppc[:64, h:h + 1], lw[:, h * 64:(h + 1) * 64], cx.ones_f[:, 0:1], [I6, cx.ones_f], True, True)
            cx.op('act', [ppc], [pc], lambda e: e.activation(out=pc[:, :], in_=ppc[:64, 0:8], func=AF.Exp))
            cx.op('pool', [ST, pc], [STs],
                  lambda e: e.tensor_tensor(STs[:, :, :], ST[:, :, :], pc[:, :].unsqueeze(2).broadcast_to([64, 8, 64]), ALU.mult))
            k = 0
            for q in range(4):
                for hh in range(2):
                    pt = ptr[k % 2]
                    k += 1
                    for j in range(4):
                        h = hh * 4 + j
                        transp(cx, pt, pt[:64, j * 128:(j + 1) * 128], TM[:, q, h * 64:(h + 1) * 64], cx.identf[:, :], [TM, cx.identf])
                    evac(cx, 'act' if k % 2 else 'dve', pt, pt[:64, :].rearrange("p (j t) -> p j t", j=4), XT,
                         fr(XT[:, hh * 4:(hh + 1) * 4, q, :]))
            for h in range(8):
                ar = XT[:, h, 0:2, :].rearrange("p q t -> p (q t)")
                mm(cx, pA, pA[:, 0:256], fr(XT[:, h, 2, :]), fr(ar), [XT], True, True)
                mm(cx, pA, pA[:, 256:512], fr(XT[:, h, 3, :]), fr(ar), [XT], True, True)
                pd = pD[h % 2]
                mm(cx, pd, pd[:, 0, :], fr(XT[:, h, 0, :]), fr(XT[:, h, 2, :]), [XT], True, True)
                cx.op('dve', [pA, mask4], [MM_], lambda e: e.tensor_tensor(MM_[:, h, :], pA[:, :], mask4[:, :], ALU.mult))
                cx.op('dve', [pd, rwm], [XX[0]], lambda e: e.tensor_tensor(fr(XX[0][:, h, 1, :]), pd[:, 0, :], MLs, ALU.mult))
                cx.op('pool', [MM_], [XX[0]], lambda e: e.tensor_copy(fr(XX[0][:, h, 0, :]), MM_[:, h, 0:128]))
                cx.op('pool', [MM_, cx.identf], [NTb], lambda e: e.tensor_tensor(fr(NTb[:, h, :]), MM_[:, h, 0:128], cx.identf[:, :], ALU.add))
            for lev in range(1, 7):
                cur, nxt = XX[(lev - 1) % 2], XX[lev % 2]
                for p in range(4):
                    pd = pD[p % 2]
                    for j in range(2):
                        h = 2 * p + j
                        if lev < 6:
                            mm(cx, pd, pd[:, 2 * j, :], fr(cur[:, h, 1, :]), fr(cur[:, h, 0, :]), [cur], True, True)
                        mm(cx, pd, pd[:, 2 * j + 1, :], fr(cur[:, h, 0, :]), fr(cur[:, h, 1, :]), [cur], True, True)
                    if lev < 6:
                        evac(cx, 'act' if p % 2 else 'dve', pd, pd[:, :, :], nxt,
                             fr(nxt[:, 2 * p:2 * p + 2, :, :].rearrange("p h q t -> p (h q) t")))
                    else:
                        for j in range(2):
                            evac(cx, 'act' if j else 'dve', pd, pd[:, 2 * j + 1, :], nxt, fr(nxt[:, 2 * p + j, 1, :]))
                for p in range(4):
                    pd = pD[p % 2]
                    for j in range(2):
                        h = 2 * p + j
                        mm(cx, pd, pd[:, j, :], fr(nxt[:, h, 1, :]), fr(NTb[:, h, :]), [nxt, NTb], True, True)
                    cx.op('dve', [pd, NTb], [NTb],
                          lambda e: e.tensor_tensor(fr(NTb[:, 2 * p:2 * p + 2, :]), NTb[:, 2 * p:2 * p + 2, :], pd[:, 0:2, :], ALU.add))
            for h in range(8):
                hs = slice(h * 64, (h + 1) * 64)
                mm(cx, pY, pY[:, hs], XT[:, h, 0, :], ST[:, h, :], [XT, ST], True, False)
                mm(cx, pY, pY[:, hs], MM_[:, h, 256:384], vv[:, hs], [MM_, I6], False, True)
            evac(cx, 'dve', pY, pY[:, 0:256], Yb, Yb[:, 0:4, :].rearrange("p h c -> p (h c)"))
            evac(cx, 'act', pY, pY[:, 256:512], Yb, Yb[:, 4:8, :].rearrange("p h c -> p (h c)"))
            for h in range(8):
                hs = slice(h * 64, (h + 1) * 64)
                mm(cx, pU, pU[:, hs], NTb[:, h, :], Yb[:, h, :], [NTb, Yb], True, True)
            evac(cx, 'dve', pU, pU[:, 0:256], Ub, Ub[:, 0:4, :].rearrange("p h c -> p (h c)"))
            evac(cx, 'act', pU, pU[:, 256:512], Ub, Ub[:, 4:8, :].rearrange("p h c -> p (h c)"))
            for h in range(8):
                hs = slice(h * 64, (h + 1) * 64)
                mm(cx, pY, pY[:64, hs], TM[:, 2, hs], Ub[:, h, :], [TM, Ub], True, False)
                mm(cx, pY, pY[:64, hs], TM[:, 3, hs], vv[:, hs], [TM, I6], False, True)
            for h in range(8):
                hs = slice(h * 64, (h + 1) * 64)
                mm(cx, pO, pO[:, hs], XT[:, h, 1, :], ST[:, h, :], [XT, ST], True, False)
                mm(cx, pO, pO[:, hs], MM_[:, h, 128:256], Ub[:, h, :], [MM_, Ub], False, False)
                mm(cx, pO, pO[:, hs], MM_[:, h, 384:512], vv[:, hs], [MM_, I6], False, True)
            cx.op('dve', [pY, pc], [ST],
                  lambda e: e.tensor_tensor(ST[:, :, :], pY[:64, :].rearrange("p (h c) -> p h c", h=8),
                                            pc[:, :].unsqueeze(2).broadcast_to([64, 8, 64]), ALU.mult))
            cx.op('dve', [ST, STs], [ST], lambda e: e.tensor_tensor(ST[:, :, :], ST[:, :, :], STs[:, :, :], ALU.add))
            evac(cx, 'act', pO, pO[:, :], Ob, Ob[:, :])
            g3, b8b, y = G3[c % 2], b8[c % 2], yo[c % 2]
            cx.dma('sp', g3[:, :], RW['gg'][rows, :], writes=[g3])
            cx.dma('act', b8b[:, :], RW['bc'][rows, :], writes=[b8b])
            cx.op('dve', [Ob], [m8], lambda e: e.tensor_reduce(out=m8[:, :], in_=v3(Ob[:, :]), axis=AX.X, op=ALU.add))
            cx.op('dve', [m8], [m8], lambda e: e.tensor_scalar(m8[:, :], m8[:, :], 1.0 / 64, None, ALU.mult))
            cx.op('dve', [Ob, m8], [Ob], lambda e: e.tensor_tensor(v3(Ob[:, :]), v3(Ob[:, :]), bc8(m8[:, :]), ALU.subtract))
            cx.op('pool', [Ob], [t512], lambda e: e.tensor_tensor(t512[:, :], Ob[:, :], Ob[:, :], ALU.mult))
            cx.op('dve', [t512], [r8], lambda e: e.tensor_reduce(out=r8[:, :], in_=v3(t512[:, :]), axis=AX.X, op=ALU.add))
            cx.op('dve', [r8], [r8], lambda e: e.tensor_scalar(r8[:, :], r8[:, :], 1.0 / 64, 64e-5, ALU.mult, ALU.add))
            cx.op('pool', [r8, cx.neghalf], [r8],
                  lambda e: e.tensor_tensor(r8[:, :], r8[:, :], cx.neghalf[:, 0:1].to_broadcast([128, 8]), ALU.pow))
            cx.op('dve', [Ob, r8], [y], lambda e: e.tensor_tensor(v3(y[:, :]), v3(Ob[:, :]), bc8(r8[:, :]), ALU.mult))
            cx.op('pool', [y, gwB], [y], lambda e: e.tensor_tensor(y[:, :], y[:, :], gwB[:, :], ALU.mult))
            cx.op('pool', [y, gbB], [y], lambda e: e.tensor_tensor(y[:, :], y[:, :], gbB[:, :], ALU.add))
            cx.op('dve', [I6, b8b], [t512], lambda e: e.tensor_tensor(v3(t512[:, :]), v3(vv), bc8(b8b[:, :]), ALU.mult))
            cx.op('pool', [y, t512], [y], lambda e: e.tensor_tensor(y[:, :], y[:, :], t512[:, :], ALU.add))
            cx.op('dve', [y, g3], [y], lambda e: e.tensor_tensor(y[:, :], y[:, :], g3[:, :], ALU.mult))
            cx.dma('pool', Y_d[rows, :], y[:, :], reads=[y])
    cx.barrier()


NSA_SCALE = 128 ** -0.5
NSA_BIG = 30000.0
NSA_STOP = 0


def phase_nsa(cx, S, l, P_d, W, Wb, C, Youts, scr):
    NT = S // 128
    G = min(512, S)
    NG = G // 128
    QW = min(512, S)
    NJ = QW // 128
    Nc = (S - 32) // 16 + 1
    NKB = (Nc + 127) // 128
    N = scr['nsa']
    with contextlib.ExitStack() as st:
        pn = [cx.sb(st, [128, 1292], F32, name="pn") for _ in range(2)]
        cs_t = [cx.sb(st, [128, 32], F32, name="cs") for _ in range(2)]
        ro = [cx.sb(st, [128, 10, 32], F32, name="ro") for _ in range(2)]
        t1s = [cx.sb(st, [128, 10, 16], F32, name="t1") for _ in range(2)]
        t2s = [cx.sb(st, [128, 10, 16], F32, name="t2") for _ in range(2)]
        fb = [cx.sb(st, [128, 8, 128], BF16, name="fb") for _ in range(2)]
        va = [cx.sb(st, [128, 2, 129], BF16, name="va") for _ in range(2)]
        sqs = [cx.sb(st, [128, 6, 128], F32, name="sq") for _ in range(2)]
        n6 = [cx.sb(st, [128, 6], F32, name="n6") for _ in range(2)]
        nqb = [cx.sb(st, [128, 4], BF16, name="nqb") for _ in range(2)]
        gt = [cx.sb(st, [128, 12], F32, name="gt") for _ in range(2)]
        kmx = cx.sb(st, [128, 2], F32, name="kmx")
        gA = cx.sb(st, [128, 8, G], BF16, name="gA")
        gN = cx.sb(st, [1, 4, G], BF16, name="gN")
        trp = TrPool(cx, st)
        trf = TrPool(cx, st, n=1, dtype=F32)
        cx.op('pool', [], [kmx], lambda e: e.memset(kmx[:, :], 0.0))
        def body(tt):
            t1, t2, sq = t1s[tt % 2], t2s[tt % 2], sqs[tt % 2]
            tl = tt % NG
            rows = slice(tt * 128, (tt + 1) * 128)
            p, c_t, r, f, v, n6b, nq_, g_ = pn[tt % 2], cs_t[tt % 2], ro[tt % 2], fb[tt % 2], va[tt % 2], n6[tt % 2], nqb[tt % 2], gt[tt % 2]
            cx.dma('sp', p[:, :], P_d[rows, OFF_NSA:OFF_NSA + 1292], writes=[p])
            cx.dma('act', c_t[:, 0:16], C['nsa_cos'][rows, :], writes=[c_t])
            cx.dma('act', c_t[:, 16:32], C['nsa_sin'][rows, :], writes=[c_t])
            blk = p[:, 0:1280].rearrange("p (b c) -> p b c", b=10)
            cb = c_t[:, 0:16].unsqueeze(1).broadcast_to([128, 10, 16])
            sb_ = c_t[:, 16:32].unsqueeze(1).broadcast_to([128, 10, 16])
            rope_tm(cx, p, blk[:, :, 0:16], blk[:, :, 16:32], [c_t, cb], [c_t, sb_],
                    r, r[:, :, 0:16], r[:, :, 16:32], [(t1, t1[:, :, :]), (t2, t2[:, :, :])])
            cx.op('act', [p], [f], lambda e: e.activation(out=f[:, 0:4, 32:128], in_=blk[:, 0:4, 32:128], func=AF.Copy, scale=NSA_SCALE))
            cx.op('act', [r], [f], lambda e: e.activation(out=f[:, 0:4, 0:32], in_=r[:, 0:4, :], func=AF.Copy, scale=NSA_SCALE))
            for dst, src in ((4, 4), (6, 6), (7, 8)):
                cx.op('pool', [p], [f], lambda e, dst=dst, src=src: e.tensor_copy(f[:, dst, 32:128], blk[:, src, 32:128]))
                cx.op('pool', [r], [f], lambda e, dst=dst, src=src: e.tensor_copy(f[:, dst, 0:32], r[:, src, :]))
            cx.op('pool', [p], [f], lambda e: e.tensor_copy(f[:, 5, :], blk[:, 5, :]))
            cx.op('pool', [p], [v], lambda e: e.tensor_copy(v[:, 0, 0:128], blk[:, 7, :]))
            cx.op('pool', [p], [v], lambda e: e.tensor_copy(v[:, 1, 0:128], blk[:, 9, :]))
            cx.op('pool', [], [v], lambda e: e.memset(v[:, :, 128:129], 1.0))
            cx.op('dve', [p], [sq], lambda e: e.tensor_tensor(sq[:, 0:4, :], blk[:, 0:4, :], blk[:, 0:4, :], ALU.mult))
            cx.op('dve', [p], [sq], lambda e: e.tensor_tensor(sq[:, 4, :], blk[:, 6, :], blk[:, 6, :], ALU.mult))
            cx.op('dve', [p], [sq], lambda e: e.tensor_tensor(sq[:, 5, :], blk[:, 8, :], blk[:, 8, :], ALU.mult))
            cx.op('dve', [sq], [n6b], lambda e: e.tensor_reduce(out=n6b[:, :], in_=sq[:, :, :], axis=AX.X, op=ALU.add))
            cx.op('dve', [n6b, kmx], [kmx], lambda e: e.tensor_tensor(kmx[:, :], kmx[:, :], n6b[:, 4:6], ALU.max))
            cx.op('act', [n6b], [n6b], lambda e: e.activation(out=n6b[:, 0:4], in_=n6b[:, 0:4], func=AF.Sqrt))
            cx.op('dve', [n6b], [nq_], lambda e: e.tensor_scalar(nq_[:, :], n6b[:, 0:4], -NSA_SCALE, None, ALU.mult))
            cx.dma('sp', g_[:, :], P_d[rows, OFF_NSA + 1280:OFF_NSA + 1292], writes=[g_])
            cx.op('act', [g_], [g_], lambda e: e.activation(out=g_[:, :], in_=g_[:, :], func=AF.Sigmoid))
            cx.dma('pool', N['ng'][rows, :], g_[:, :], reads=[g_])
            trp.transpose_cols(f, lambda j: f[:, j, :], 8, gA, lambda j0, cnt: gA[:, j0:j0 + cnt, tl * 128:(tl + 1) * 128])
            trp.transpose_cols(nq_, lambda j: nq_[:, j:j + 1], 4, gN,
                               lambda j0, cnt: gN[0:1, j0:j0 + cnt, tl * 128:(tl + 1) * 128], blkw=1)
            cx.dma('pool', N['vsa'][rows, :], v[:, 0, :], reads=[v])
            cx.dma('pool', N['vwa'][rows, :], v[:, 1, :], reads=[v])
            if tl == NG - 1:
                def post():
                    g0 = (tt // NG) * G
                    cx.dma('pool', N['qT'][:, :, g0:g0 + G].rearrange("h d s -> d h s"), gA[:, 0:4, :], reads=[gA])
                    for j, nm in ((4, 'kcT'), (5, 'vcT'), (6, 'ksT'), (7, 'kwT')):
                        cx.dma('pool', N[nm][:, g0:g0 + G], gA[:, j, :], reads=[gA])
                    cx.dma('pool', N['nq'][:, g0:g0 + G].rearrange("(o h) s -> o h s", o=1), gN[0:1, :, :], reads=[gN])
                return post
        run_tiles(cx, body, NT)
        kms = cx.sb(st, [128, 1], F32, name="kms")
        kmw = cx.sb(st, [128, 1], F32, name="kmw")
        k1 = cx.sb(st, [128, 1], F32, name="k1")
        rowsb = cx.sb(st, [1, 2, 128], BF16, name="rowsb")
        for i, dstc in enumerate((kms, kmw)):
            cx.op('pool', [kmx], [k1], lambda e: e.tensor_copy(k1[:, :], kmx[:, i:i + 1]))
            bcast_scalar_max(cx, st, trf, k1, dstc)
            cx.op('act', [dstc], [dstc], lambda e: e.activation(out=dstc[:, :], in_=dstc[:, :], func=AF.Sqrt))
            cx.op('dve', [dstc], [rowsb], lambda e: e.tensor_copy(rowsb[0:1, i, :], dstc[0:1, 0:1].to_broadcast([1, 128])))
        cx.dma('pool', N['krow'].rearrange("a b -> (a b)").rearrange("(o n) -> o n", o=1), rowsb[0:1, :, :].rearrange("o a b -> o (a b)"), reads=[rowsb])
    cx.barrier()
    if NSA_STOP == 1:
        return
    with contextlib.ExitStack() as st:
        KC = cx.sb(st, [128, 256], BF16, name="KC")
        VCA = cx.sb(st, [128, 2, 193], BF16, name="VCA")
        krow = cx.sb(st, [128, 3, 128], BF16, name="krow")
        cx.op('pool', [], [krow], lambda e: e.memset(krow[:, :, :], 0.0))
        cx.dma('sp', krow[0:1, 0:2, :].rearrange("o a b -> o (a b)"), N['krow'].rearrange("a b -> (a b)").rearrange("(o n) -> o n", o=1), writes=[krow])
        cx.dma('sp', VCA[:, :, 129:193], C['cover'].rearrange("(kb p) j -> p kb j", p=128), writes=[VCA])
        cx.op('pool', [], [VCA], lambda e: e.memset(VCA[:, :, 0:129], 0.0))
        cx.op('pool', [], [VCA], lambda e: e.memset(VCA[:, :, 128:129], 1.0))
        cx.op('pool', [], [KC], lambda e: e.memset(KC[:, :], 0.0))
        with contextlib.ExitStack() as s2:
            pm = [cx.ps(s2, [128, 512], F32, name="pm") for _ in range(2)]
            xT = [cx.sb(s2, [128, S], BF16, name="xT") for _ in range(2)]
            cx.dma('sp', xT[0][:, :], N['kcT'][:, :], writes=[xT[0]])
            cx.dma('act', xT[1][:, :], N['vcT'][:, :], writes=[xT[1]])
            w1 = [cx.sb(s2, [128, 32, 128], BF16, name="w1") for _ in range(2)]
            w2 = [cx.sb(s2, [128, 128], BF16, name="w2") for _ in range(2)]
            posf = cx.sb(s2, [32, 2, 128], F32, name="posf")
            posb = cx.sb(s2, [32, 2, 128], BF16, name="posb")
            posT = cx.sb(s2, [128, 2, 32], BF16, name="posT")
            bias = cx.sb(s2, [128, 2], F32, name="bias")
            xs = cx.sb(s2, [128, 256], F32, name="xs")
            x2 = cx.sb(s2, [128, 256], F32, name="x2")
            hid = [cx.sb(s2, [128, 256], BF16, name="hid") for _ in range(2)]
            ksq = cx.sb(s2, [128, 256], BF16, name="ksq")
            one = cx.sb(s2, [1, 2], F32, name="one")
            trp = TrPool(cx, s2, n=1)
            for z in range(2):
                cx.dma('sp', w1[z][:, :, :], Wb['nsa_cmp_w1'][l][z].rearrange("(l d) e -> d l e", d=128), writes=[w1[z]])
                cx.dma('sp', w2[z][:, :], Wb['nsa_cmp_w2'][l][z], writes=[w2[z]])
            cx.dma('sp', posf[:, :, :], W['nsa_cmp_pos'][l].rearrange("z l d -> l z d"), writes=[posf])
            cx.op('dve', [posf], [posb], lambda e: e.tensor_copy(posb[:, :, :], posf[:, :, :]))
            trp.transpose_cols(posb, lambda j: posb[:, j, :], 2, posT, lambda j0, cnt: posT[:, j0:j0 + cnt, :], rows=32)
            for z in range(2):
                pb, ph = pm
                for ll in range(32):
                    mm(cx, pb, pb[:, z:z + 1], w1[z][:, ll, :], posT[:, z, ll:ll + 1], [w1[z], posT], ll == 0, ll == 31)
                evac(cx, 'dve', pb, pb[:, z:z + 1], bias, bias[:, z:z + 1])
                for ll in range(32):
                    mm(cx, ph, ph[:, :Nc], w1[z][:, ll, :], xT[z][:, ll:ll + 16 * (Nc - 1) + 1:16], [w1[z], xT[z]], ll == 0, ll == 31)
                cx.op('act', [ph, bias], [xs], lambda e: e.activation(out=xs[:, :Nc], in_=ph[:, :Nc], func=AF.Identity, bias=bias[:, z:z + 1]))
                cx.op('dve', [xs], [x2], lambda e: e.tensor_tensor(x2[:, :Nc], xs[:, :Nc], xs[:, :Nc], ALU.mult))
                cx.op('dve', [x2], [x2], lambda e: e.tensor_scalar(x2[:, :Nc], x2[:, :Nc], 0.044715, 1.0, ALU.mult, ALU.add))
                cx.op('dve', [x2, xs], [x2], lambda e: e.tensor_tensor(x2[:, :Nc], x2[:, :Nc], xs[:, :Nc], ALU.mult))
                cx.op('act', [x2], [x2], lambda e: e.activation(out=x2[:, :Nc], in_=x2[:, :Nc], func=AF.Tanh, scale=0.7978845608028654))
                cx.op('dve', [x2], [x2], lambda e: e.tensor_scalar(x2[:, :Nc], x2[:, :Nc], 1.0, 0.5, ALU.add, ALU.mult))
                cx.op('dve', [x2, xs], [hid[z]], lambda e: e.tensor_tensor(hid[z][:, :Nc], x2[:, :Nc], xs[:, :Nc], ALU.mult))
            pk = pm[0]
            mm(cx, pk, pk[:, :Nc], w2[0][:, :], hid[0][:, :Nc], [w2[0], hid[0]], True, True)
            evac(cx, 'act', pk, pk[:, :Nc], KC, KC[:, :Nc])
            cx.op('act', [pk], [ksq], lambda e: e.activation(out=ksq[:, :Nc], in_=pk[:, :Nc], func=AF.Square))
            pr = pm[1]
            mm(cx, pr, pr[0:1, :Nc], cx.ones_bf[:, 0:1], ksq[:, :Nc], [cx.ones_bf, ksq], True, True)
            cx.op('dve', [pr], [one], lambda e: e.tensor_reduce(out=one[0:1, 0:1], in_=pr[0:1, :Nc], axis=AX.X, op=ALU.max))
            cx.op('act', [one], [one], lambda e: e.activation(out=one[0:1, 0:1], in_=one[0:1, 0:1], func=AF.Sqrt))
            cx.op('dve', [one], [krow], lambda e: e.tensor_scalar(krow[0:1, 2, :], one[0:1, 0:1].to_broadcast([1, 128]), 1.02, None, ALU.mult))
            for kb in range(NKB):
                nk = min(128, Nc - kb * 128)
                pv = pm[kb % 2]
                mm(cx, pv, pv[:nk, 0:128], hid[1][:, kb * 128:kb * 128 + nk], w2[1][:, :], [hid[1], w2[1]], True, True)
                evac(cx, 'dve', pv, pv[:nk, 0:128], VCA, VCA[:nk, kb, 0:128])
        cx.barrier()
        if NSA_STOP == 2:
            return
        res = AttnRes(cx, st, 193)
        qT = [cx.sb(st, [128, S], BF16, name="qT") for _ in range(4)]
        nq = [cx.sb(st, [128, S], BF16, name="nq") for _ in range(4)]
        for h in range(4):
            cx.op('pool', [], [nq[h]], lambda e: e.memset(nq[h][:, :], 0.0))
            cx.dma('sp', qT[h][:, :], N['qT'][h], writes=[qT[h]])
            cx.dma('act', nq[h][0:1, :], N['nq'][h:h + 1, :], writes=[nq[h]])
        ksT = cx.sb(st, [128, S], BF16, name="ksT")
        kwT = cx.sb(st, [128, S], BF16, name="kwT")
        cx.dma('sp', ksT[:, :], N['ksT'][:, :], writes=[ksT])
        cx.dma('act', kwT[:, :], N['kwT'][:, :], writes=[kwT])
        vsa = cx.sb(st, [128, NT, 129], BF16, name="vsa")
        vwa = cx.sb(st, [128, NT, 129], BF16, name="vwa")
        cx.dma('sp', vsa[:, :, :], N['vsa'].rearrange("(t p) c -> p t c", p=128), writes=[vsa])
        cx.dma('act', vwa[:, :, :], N['vwa'].rearrange("(t p) c -> p t c", p=128), writes=[vwa])
        cmask = cx.sb(st, [128, 4, 512], BF16, name="cmask")
        wmask = cx.sb(st, [128, 4, 512], BF16, name="wmask")
        cx.dma('sp', cmask[:, :, :], C['cmask'].rearrange("i k q -> k i q"), writes=[cmask])
        cx.dma('sp', wmask[:, :, :], C['wmask'].rearrange("i k q -> k i q"), writes=[wmask])
        cmpm = cx.sb(st, [128, 2, S], BF16, name="cmpm")
        cx.dma('sp', cmpm[:, :, :], C['cmpmask'].rearrange("kb p q -> p kb q"), writes=[cmpm])
        Em = cx.sb(st, [64, S], BF16, name="Em")
        cx.dma('sp', Em[:, :], C['Emat'][:, :], writes=[Em])
        gts = cx.sb(st, [128, NT, 12], F32, name="gts")
        cx.dma('sp', gts[:, :, :], N['ng'].rearrange("(t p) c -> p t c", p=128), writes=[gts])
        imp = cx.sb(st, [128, 4, 64], F32, name="imp")
        fbt = [cx.sb(st, [128, 64], F32, name="fbt") for _ in range(2)]
        m8 = cx.sb(st, [128, 16], F32, name="m8")
        val2 = cx.sb(st, [128, 64], F32, name="val2")
        selb = cx.sb(st, [128, 64], BF16, name="selb")
        selT = cx.sb(st, [64, 512], BF16, name="selT")
        rc = [cx.sb(st, [128, 1], F32, name="rc") for _ in range(2)]
        rg = [cx.sb(st, [128, 1], F32, name="rg") for _ in range(2)]
        ot = [cx.sb(st, [128, 128], F32, name="ot") for _ in range(3)]
        it = [cx.sb(st, [128, 64], F32, name="it") for _ in range(2)]
        trp2 = TrPool(cx, st, n=1)
        cnt = [0]

        def make_epi(branch, h):
            def epi(QB, j, accb, acc_ap):
                k = cnt[0]
                cnt[0] += 1
                tt = QB * NJ + j
                r, rgb, o = rc[k % 2], rg[k % 2], ot[k % 3]
                cx.op('dve', [accb], [r], lambda e: e.tensor_scalar(r[:, :], acc_ap[:, 128:129], 1e-30, None, ALU.add))
                cx.op('dve', [r], [r], lambda e: e.reciprocal(r[:, :], r[:, :]))
                cx.op('dve', [r, gts], [rgb], lambda e: e.tensor_tensor(rgb[:, :], r[:, :], gts[:, tt, h * 3 + branch:h * 3 + branch + 1], ALU.mult))
                cx.op('act', [accb, rgb], [o], lambda e: e.activation(out=o[:, :], in_=acc_ap[:, 0:128], func=AF.Copy, scale=rgb[:, 0:1]))
                cx.dma('pool', Youts[branch][tt * 128:(tt + 1) * 128, h * 128:(h + 1) * 128], o[:, :], reads=[o])
                if branch == 0:
                    if h == 0:
                        cx.op('dve', [accb, r], [imp], lambda e: e.tensor_scalar(imp[:, j, :], acc_ap[:, 129:193], r[:, 0:1], None, ALU.mult))
                    else:
                        i_ = it[k % 2]
                        cx.op('dve', [accb, r], [i_], lambda e: e.tensor_scalar(i_[:, :], acc_ap[:, 129:193], r[:, 0:1], None, ALU.mult))
                        cx.op('pool', [i_, imp], [imp], lambda e: e.tensor_tensor(imp[:, j, :], imp[:, j, :], i_[:, :], ALU.add))
            return epi

        def cmp_blocks(QB):
            q0 = QB * QW
            return [dict(kb=kb, k0=kb * 128, nk=min(128, Nc - kb * 128),
                         mask=(cmpm, cmpm[:min(128, Nc - kb * 128), kb, q0:q0 + QW])) for kb in range(NKB)]

        def slc_blocks(QB):
            bl = causal_blocks(QB, QW, cmask)
            for b in bl:
                b['extra'] = ([Em], Em[:, b['k0']:b['k0'] + 128], [selT], selT[:, :QW])
            return bl

        def win_blocks(QB):
            out = []
            nd = QW // 128
            for kb in range(max(0, nd * QB - 4), nd * (QB + 1)):
                i = kb - nd * QB
                m = (cmask, cmask[:, i, :QW]) if i >= 0 else (wmask, wmask[:, i + 4, :QW])
                out.append(dict(kb=kb, k0=kb * 128, nk=128, mask=m))
            return out

        for QB in range(S // QW):
            for h in range(4):
                attn_core(cx, res, S, [(qT[h], qT[h][:, :]), (nq[h], nq[h][:, :])],
                          [(kwT, kwT[:, :]), (krow, lambda k0, nk: krow[:, 1, :nk])],
                          lambda kb, nk: (vwa, vwa[:nk, kb, :]), 129, win_blocks, make_epi(2, h), qbs=[QB])
            if NSA_STOP == 3:
                break
            for h in range(4 if NSA_STOP != 8 else 0):
                attn_core(cx, res, S, [(qT[h], qT[h][:, :]), (nq[h], nq[h][:, :])],
                          [(KC, KC[:, :]), (krow, lambda k0, nk: krow[:, 2, :nk])],
                          lambda kb, nk: (VCA, VCA[:nk, kb, :]), 193, cmp_blocks, make_epi(0, h), qbs=[QB])
            if NSA_STOP == 4:
                break
            for j in range(NJ if NSA_STOP != 8 else 0):
                tt = QB * NJ + j
                f_ = fbt[j % 2]
                cx.dma('sp', f_[:, :], C['fbias'][tt * 128:(tt + 1) * 128, :], writes=[f_])
                cx.op('dve', [imp, f_], [f_], lambda e: e.tensor_tensor(f_[:, :], f_[:, :], imp[:, j, :], ALU.add))
                cx.op('dve', [f_], [m8], lambda e: e.max(out=m8[:, 0:8], in_=f_[:, :]))
                cx.op('dve', [f_, m8], [val2], lambda e: e.match_replace(out=val2[:, :], in_to_replace=m8[:, 0:8], in_values=f_[:, :], imm_value=-3.0e38))
                cx.op('dve', [val2], [m8], lambda e: e.max(out=m8[:, 8:16], in_=val2[:, :]))
                cx.op('dve', [f_, m8], [val2], lambda e: e.tensor_scalar(val2[:, :], f_[:, :], m8[:, 15:16], None, ALU.is_ge))
                cx.op('dve', [val2], [selb], lambda e: e.tensor_scalar(selb[:, :], val2[:, :], -1.0, NSA_BIG, ALU.add, ALU.mult))
                trp2.transpose_cols(selb, lambda jj: selb[:, :], 1, selT, lambda j0, c_, j=j: selT[:64, j * 128:(j + 1) * 128].unsqueeze(1), blkw=64)
            if NSA_STOP == 5:
                break
            for h in range(4 if NSA_STOP not in (8, 9) else 0):
                attn_core(cx, res, S, [(qT[h], qT[h][:, :]), (nq[h], nq[h][:, :])],
                          [(ksT, ksT[:, :]), (krow, lambda k0, nk: krow[:, 0, :nk])],
                          lambda kb, nk: (vsa, vsa[:nk, kb, :]), 129, slc_blocks, make_epi(1, h), qbs=[QB])
    cx.barrier()


S_FULL = 4096
ENABLE = {'mla': True, 'nsa': True, 'rwkv': True, 'ret': True}


def phase_zero(cx, S, Y_d):
    with contextlib.ExitStack() as st:
        z = cx.sb(st, [128, 512], F32, name="z")
        cx.op('pool', [], [z], lambda e: e.memset(z[:, :], 0.0))
        for tt in range(S // 128):
            cx.dma('sp', Y_d[tt * 128:(tt + 1) * 128, :], z[:, :], reads=[z])
    cx.barrier()


def host_consts(S):
    import ml_dtypes
    bf = ml_dtypes.bfloat16
    c = {}
    c['ident'] = np.eye(128, dtype=np.float32).astype(bf)
    kk = np.arange(128)[:, None]
    qq = np.arange(512)[None, :]
    c['cmask'] = np.stack([(128 * i + kk <= qq) for i in range(4)]).astype(np.float32).astype(bf)
    t = np.arange(S, dtype=np.float32)[:, None]

    def tables(inv):
        ang = (t * inv[None, :].astype(np.float32)).astype(np.float32)
        return np.cos(ang).astype(np.float32), np.sin(ang).astype(np.float32)

    inv_mla = (np.float32(500000.0) ** (-np.arange(0, 64, 2, dtype=np.float32) / np.float32(64))).astype(np.float32)
    inv_nsa = (np.float32(500000.0) ** (-np.arange(0, 32, 2, dtype=np.float32) / np.float32(32))).astype(np.float32)
    inv_ret = (np.float32(10000.0) ** (-np.linspace(0.0, 1.0, 32, dtype=np.float32))).astype(np.float32)
    c['mla_cos'], c['mla_sin'] = tables(inv_mla)
    c['nsa_cos'], c['nsa_sin'] = tables(inv_nsa)
    c['ret_cos'], c['ret_sin'] = tables(inv_ret)
    kf = kk.astype(np.float64)
    qf = qq.astype(np.float64)
    gdec = np.zeros((4, 5, 128, 512), np.float32)
    for h in range(4):
        lg = np.log1p(-2.0 ** (-5 - h))
        gdec[h, 0] = np.exp((qf - kf) * lg)
        for i in range(4):
            d = qf - kf - 128 * i
            gdec[h, 1 + i] = np.where(d >= 0, np.exp(np.maximum(d, 0) * lg), 0.0)
    c['gdec'] = gdec
    si = np.arange(128)[:, None]
    ti = np.arange(128)[None, :]
    c['wmask'] = (1.0 - c['cmask'].astype(np.float32)).astype(bf)
    Nc = (S - 32) // 16 + 1
    n = np.arange(256)
    q = np.arange(S)
    cm = ((16 * n[:, None] + 31 <= q[None, :]) & (n[:, None] < Nc)).astype(np.float32)
    c['cmpmask'] = cm.reshape(2, 128, S).astype(bf)
    nblk = S // 64
    jb = np.arange(64)
    cstart = 16 * n
    cend = cstart + 31
    cover = ((cstart[:, None] <= jb[None, :] * 64 + 63) & (cend[:, None] >= jb[None, :] * 64) & (n[:, None] < Nc)
             & (jb[None, :] < nblk)).astype(np.float32)
    c['cover'] = cover.astype(bf)
    c['Emat'] = (q[None, :] // 64 == jb[:, None]).astype(np.float32).astype(bf)
    cur = q // 64
    forced = (jb[None, :] == 0) | (jb[None, :] == cur[:, None]) | (jb[None, :] == cur[:, None] - 1)
    visible = (jb[None, :] <= cur[:, None]) & (jb[None, :] < nblk)
    c['fbias'] = np.where(visible, 1000.0 * forced, -1.0e30).astype(np.float32)
    c['rwm'] = np.concatenate([(si < ti), (si <= ti), (si > ti)], axis=1).astype(np.float32)
    return c


CONST_SPECS = {'ident': ([128, 128], BF16), 'cmask': ([4, 128, 512], BF16),
               'mla_cos': (None, F32), 'mla_sin': (None, F32), 'nsa_cos': (None, F32), 'nsa_sin': (None, F32),
               'ret_cos': (None, F32), 'ret_sin': (None, F32), 'gdec': ([4, 5, 128, 512], F32), 'rwm': ([128, 384], F32), 'wmask': ([4, 128, 512], BF16), 'cmpmask': ('cmp', BF16),
               'cover': ([256, 64], BF16), 'Emat': ('E', BF16), 'fbias': ('fb', F32)}

WEIGHT_SHAPES = {
    'w_in': [DEPTH, D_MODEL, IN_WIDTH], 'w_branch': [DEPTH, 4, BW, D_MODEL], 'w_out': [DEPTH, D_MODEL, D_MODEL],
    'w_up': [DEPTH, D_MODEL, D_FF], 'w_down': [DEPTH, D_FF, D_MODEL], 'norm_gains': [DEPTH, 4, D_MODEL],
    'mla_g_q': [DEPTH, 384], 'mla_g_kv': [DEPTH, 128], 'mla_w_uq': [DEPTH, 384, 768], 'mla_w_ukv': [DEPTH, 128, 1024],
    'nsa_cmp_pos': [DEPTH, 2, 32, 128], 'nsa_cmp_w1': [DEPTH, 2, 4096, 128], 'nsa_cmp_w2': [DEPTH, 2, 128, 128],
    'rwkv_mu': [DEPTH, 1984], 'rwkv_w0': [DEPTH, 512], 'rwkv_w2': [DEPTH, 96, 512], 'rwkv_a0': [DEPTH, 512],
    'rwkv_a2': [DEPTH, 96, 512], 'rwkv_g2': [DEPTH, 256, 512], 'rwkv_k_k': [DEPTH, 512], 'rwkv_k_a': [DEPTH, 512],
    'rwkv_r_k': [DEPTH, 8, 64], 'rwkv_gn_w': [DEPTH, 512], 'rwkv_gn_b': [DEPTH, 512],
}
CAST = ['w_in', 'w_branch', 'w_out', 'w_up', 'w_down', 'mla_w_uq', 'mla_w_ukv', 'nsa_cmp_w1', 'nsa_cmp_w2',
        'rwkv_w2', 'rwkv_a2', 'rwkv_g2']


def build_program(S, depth=DEPTH):
    cx = Ctx()
    x_d = cx.dram("x", [S, D_MODEL], F32, kind="ExternalInput")
    W = {k: cx.dram(k, shp, F32, kind="ExternalInput") for k, shp in WEIGHT_SHAPES.items()}
    C = {}
    for k, (shp, dt) in CONST_SPECS.items():
        if shp is None:
            shp = [S, 16 if k.startswith('nsa') else 32]
        elif shp == 'cmp':
            shp = [2, 128, S]
        elif shp == 'E':
            shp = [64, S]
        elif shp == 'fb':
            shp = [S, 64]
        C[k] = cx.dram("c_" + k, shp, dt, kind="ExternalInput")
    y_d = cx.dram("y", [S, D_MODEL], F32, kind="ExternalOutput")
    Wb = {k: cx.dram(k + "_bf", WEIGHT_SHAPES[k], BF16) for k in CAST}
    P_d = cx.dram("P", [S, IN_WIDTH], F32)
    Y = [cx.dram("Y%d" % m, [S, BW], F32) for m in range(4)]
    Yn = [cx.dram("Yn%d" % m, [S, BW], F32) for m in range(2)]
    M_d = cx.dram("M", [S, D_MODEL], F32)
    Z_d = cx.dram("Z", [S, D_MODEL], F32)
    xa = cx.dram("xa", [S, D_MODEL], F32)
    xb = cx.dram("xb", [S, D_MODEL], F32)
    scr = dict(qnT=cx.dram("qnT", [4, 128, S], BF16), qrT=cx.dram("qrT", [4, 65, S], BF16),
               knT=cx.dram("knT", [4, 128, S], BF16), krT=cx.dram("krT", [65, S], BF16),
               va=cx.dram("va", [S, 4, 129], BF16),
               rqT=cx.dram("rqT", [4, 64, S], BF16), rkT=cx.dram("rkT", [4, 64, S], BF16),
               rv=cx.dram("rv", [S, 4, 128], BF16))
    scr['rw'] = {nm: cx.dram("rw_" + nm, [S, 512], F32) for nm in ['rr', 'lw', 'k2', 'vv', 'kn', 'aa', 'gg']}
    scr['rw']['bc'] = cx.dram("rw_bc", [S, 8], F32)
    scr['nsa'] = dict(qT=cx.dram("n_qT", [4, 128, S], BF16), nq=cx.dram("n_nq", [4, S], BF16),
                      kcT=cx.dram("n_kcT", [128, S], BF16), vcT=cx.dram("n_vcT", [128, S], BF16),
                      ksT=cx.dram("n_ksT", [128, S], BF16), kwT=cx.dram("n_kwT", [128, S], BF16),
                      vsa=cx.dram("n_vsa", [S, 129], BF16), vwa=cx.dram("n_vwa", [S, 129], BF16),
                      ng=cx.dram("n_ng", [S, 12], F32), krow=cx.dram("n_krow", [2, 128], BF16))
    st = contextlib.ExitStack()
    cx._st = st
    setup_consts(cx, st, C['ident'])
    phase_cast(cx, [(W[k], Wb[k]) for k in CAST])
    xin = x_d
    for l in range(depth):
        g = W['norm_gains'][l]
        phase_in(cx, S, xin, g[0], Wb['w_in'][l], P_d)
        if ENABLE['mla']:
            phase_mla(cx, S, P_d, W['mla_g_q'][l], W['mla_g_kv'][l], Wb['mla_w_uq'][l], Wb['mla_w_ukv'][l],
                      C['mla_cos'], C['mla_sin'], C['cmask'], Y[0], scr)
        else:
            phase_zero(cx, S, Y[0])
        if ENABLE['nsa']:
            phase_nsa(cx, S, l, P_d, W, Wb, C, [Y[1], Yn[0], Yn[1]], scr)
            ysrc1 = [Y[1], Yn[0], Yn[1]]
        else:
            phase_zero(cx, S, Y[1])
            ysrc1 = [Y[1]]
        if ENABLE['rwkv']:
            phase_rwkv(cx, S, l, P_d, W, Wb, C, Y[2], scr)
        else:
            phase_zero(cx, S, Y[2])
        if ENABLE['ret']:
            phase_ret(cx, S, P_d, C['ret_cos'], C['ret_sin'], C['gdec'], Y[3], scr)
        else:
            phase_zero(cx, S, Y[3])
        phase_merge(cx, S, [[Y[0]], ysrc1, [Y[2]], [Y[3]]], Wb['w_branch'][l], P_d, M_d)
        phase_out(cx, S, M_d, Wb['w_out'][l], Z_d)
        phase_normres(cx, S, xin, Z_d, g[1], xa)
        phase_ffn(cx, S, xa, g[2], Wb['w_up'][l], Wb['w_down'][l], Z_d)
        xnext = y_d if l == depth - 1 else xb
        phase_normres(cx, S, xa, Z_d, g[3], xnext)
        xin = xnext
    cx.barrier()
    return cx


_CACHE = {}


def kernel(**inputs):
    x = np.ascontiguousarray(np.asarray(inputs['x'], dtype=np.float32))
    B, S, _ = x.shape
    if S not in _CACHE:
        _CACHE[S] = (build_program(S), host_consts(S))
    cx, consts = _CACHE[S]
    base = {k: np.ascontiguousarray(np.asarray(inputs[k], dtype=np.float32)) for k in WEIGHT_SHAPES}
    for k, v in consts.items():
        base["c_" + k] = np.ascontiguousarray(v)
    in_maps = []
    for b in range(B):
        m = dict(base)
        m['x'] = x[b]
        in_maps.append(m)
    res = run_bass_kernel_spmd(cx.nc, in_maps, core_ids=list(range(B)))
    return np.stack([np.asarray(r['y'], dtype=np.float32) for r in res.results], axis=0)
```

```python
import contextlib
import numpy as np
import concourse.bass as bass
import concourse.mybir as mybir
from concourse.bass_utils import run_bass_kernel_spmd

F32 = mybir.dt.float32
F32R = mybir.dt.float32r
RW_FAST = True


def fr(ap):
    return ap.bitcast(F32R) if RW_FAST else ap
BF16 = mybir.dt.bfloat16
AF = mybir.ActivationFunctionType
ALU = mybir.AluOpType
AX = mybir.AxisListType

D_MODEL = 2048
DEPTH = 2
BW = 512
D_FF = 8192
NORM_EPS = 1e-6
IN_WIDTH = 13580
OFF_MLA = 0
OFF_NSA = 576
OFF_RWKV = 1868
OFF_RET = 3852
OFF_GATE = 5388

SELF_SYNC = {'pe': False, 'act': False, 'dve': True, 'pool': True, 'sp': False}


class Reg:
    __slots__ = ('w', 'r')

    def __init__(self):
        self.w = None
        self.r = {}


class Buf:
    def __init__(self, t, nreg=1, excl=False):
        self.t = t
        self.regs = [Reg() for _ in range(nreg)]
        self.excl = excl

    @property
    def reg(self):
        return self.regs[0]

    def __getitem__(self, idx):
        return self.t[idx]


class Ctx:
    def __init__(self):
        self.nc = bass.Bass("TRN2", target_bir_lowering=False)
        nc = self.nc
        self.E = {'pe': nc.tensor, 'act': nc.scalar, 'dve': nc.vector, 'pool': nc.gpsimd, 'sp': nc.sync}
        self.sem = {e: nc.alloc_semaphore("s_" + e) for e in ['pe', 'act', 'dve', 'pool']}
        self.cnt = {e: 0 for e in self.sem}
        self.NDS = 48
        self.dsem = [nc.alloc_semaphore("d%d" % i) for i in range(self.NDS)]
        self.dcnt = [0] * self.NDS
        self.dpool = {'sp': list(range(0, 20)), 'act': list(range(20, 34)), 'pool': list(range(34, 48))}
        self.dnext = {'sp': 0, 'act': 0, 'pool': 0}
        self.known = {e: {} for e in self.E}
        self.ninst = 0
        self.uid = 0
        self._rec = None

    def name(self, p):
        self.uid += 1
        return "%s_%d" % (p, self.uid)

    def sb(self, stack, shape, dtype, nreg=1, name="sb"):
        t = stack.enter_context(self.nc.sbuf_tensor(self.name(name), list(shape), dtype))
        return Buf(t, nreg)

    def ps(self, stack, shape, dtype=F32, nreg=1, name="ps"):
        t = stack.enter_context(self.nc.psum_tensor(self.name(name), list(shape), dtype))
        return Buf(t, nreg, excl=True)

    def dram(self, name, shape, dtype, kind="Internal"):
        return self.nc.dram_tensor(name, list(shape), dtype, kind=kind).ap()

    def _wait(self, e, kind, val, force=False):
        if isinstance(kind, str):
            if kind == e and not SELF_SYNC[e] and not (force and e in self.sem):
                return
            sem = self.sem[kind]
            v = val
        else:
            idx = kind[1]
            sem = self.dsem[idx]
            v = val * 16
        k = self.known[e]
        if k.get(kind, 0) >= v:
            return
        self.E[e].wait_ge(sem, v)
        self.ninst += 1
        k[kind] = v

    def _deps(self, e, reads, writes, force=False):
        for r in reads:
            if r.w is not None:
                self._wait(e, r.w[0], r.w[1], force)
        for w in writes:
            if w.w is not None:
                self._wait(e, w.w[0], w.w[1], force)
            for kind, val in w.r.items():
                self._wait(e, kind, val, force)

    def _commit(self, tok, reads, writes):
        kind, val = tok
        for r in reads:
            if r.r.get(kind, 0) < val:
                r.r[kind] = val
        for w in writes:
            w.w = tok
            w.r = {}

    @staticmethod
    def _regs(lst):
        out = []
        for x in lst:
            if isinstance(x, Buf):
                out.extend(x.regs)
            elif isinstance(x, Reg):
                out.append(x)
            elif x is None:
                pass
            else:
                raise TypeError(type(x))
        return out

    def op(self, e, reads, writes, fn):
        if self._rec is not None:
            self._rec.append(lambda: self._op(e, reads, writes, fn))
            return None
        return self._op(e, reads, writes, fn)

    def _op(self, e, reads, writes, fn):
        writes = list(writes) + [x for x in reads if isinstance(x, Buf) and x.excl]
        reads = [x for x in reads if not (isinstance(x, Buf) and x.excl)]
        reads = self._regs(reads)
        writes = self._regs(writes)
        self._deps(e, reads, writes)
        inst = fn(self.E[e])
        self.cnt[e] += 1
        self.ninst += 1
        inst.then_inc(self.sem[e], 1)
        self._commit((e, self.cnt[e]), reads, writes)
        return inst

    def dma(self, q, out_ap, in_ap, reads=(), writes=(), **kw):
        if self._rec is not None:
            self._rec.append(lambda: self._dma(q, out_ap, in_ap, reads, writes, **kw))
            return
        self._dma(q, out_ap, in_ap, reads, writes, **kw)

    def _dma(self, q, out_ap, in_ap, reads=(), writes=(), **kw):
        reads = self._regs(reads)
        writes = self._regs(writes)
        self._deps(q, reads, writes, force=True)
        pool = self.dpool[q]
        idx = pool[self.dnext[q] % len(pool)]
        self.dnext[q] += 1
        if self.dcnt[idx] > 0:
            self._wait(q, ('d', idx), self.dcnt[idx])
        self.E[q].dma_start(out=out_ap, in_=in_ap, **kw).then_inc(self.dsem[idx], 16)
        self.dcnt[idx] += 1
        self.ninst += 1
        self._commit((('d', idx), self.dcnt[idx]), reads, writes)

    def barrier(self):
        for e in self.E:
            for o in self.sem:
                if o != e and self.cnt[o] > 0:
                    self._wait(e, o, self.cnt[o])
            for i in range(self.NDS):
                if self.dcnt[i] > 0:
                    self._wait(e, ('d', i), self.dcnt[i])
            if e in self.sem and self.cnt[e] > 0:
                k = self.known[e]
                if k.get(e, 0) < self.cnt[e]:
                    self.E[e].wait_ge(self.sem[e], self.cnt[e])
                    k[e] = self.cnt[e]


INTERLEAVE = 2


def run_tiles(cx, body, NT):
    for t0 in range(0, NT, INTERLEAVE):
        lists = []
        posts = []
        for t in range(t0, min(NT, t0 + INTERLEAVE)):
            cx._rec = []
            post = body(t)
            lists.append(cx._rec)
            cx._rec = None
            if post is not None:
                posts.append(post)
        for i in range(max(len(l) for l in lists)):
            for l in lists:
                if i < len(l):
                    l[i]()
        for p in posts:
            p()


def atomic(cx, fn):
    if cx._rec is None:
        return fn()
    rec = cx._rec

    def unit():
        saved = cx._rec
        cx._rec = None
        fn()
        cx._rec = saved
    rec.append(unit)


def mm(cx, out_buf, out_ap, lhsT_ap, rhs_ap, reads, start, stop, **kw):
    return cx.op('pe', reads, [out_buf],
                 lambda e: e.matmul(out_ap, lhsT_ap, rhs_ap, start=start, stop=stop, **kw))


def transp(cx, out_buf, out_ap, in_ap, ident_ap, reads):
    return cx.op('pe', reads, [out_buf], lambda e: e.transpose(out_ap, in_ap, ident_ap))


def phase_cast(cx, pairs):
    CH = 4096
    NB = 6
    with contextlib.ExitStack() as st:
        stg = [cx.sb(st, [128, CH], F32, name="cst") for _ in range(NB)]
        outb = [cx.sb(st, [128, CH], BF16, name="cob") for _ in range(NB)]
        engs = ['dve', 'pool', 'act']
        k = 0
        for src, dst in pairs:
            n = 1
            for s in src.shape:
                n *= s
            assert n % 128 == 0
            per = n // 128
            names = " ".join("a%d" % i for i in range(len(src.shape)))
            s2 = src.rearrange("%s -> (%s)" % (names, names)).rearrange("(p f) -> p f", p=128)
            d2 = dst.rearrange("%s -> (%s)" % (names, names)).rearrange("(p f) -> p f", p=128)
            for c0 in range(0, per, CH):
                c1 = min(per, c0 + CH)
                w = c1 - c0
                i = k % NB
                cx.dma('sp', stg[i][:, :w], s2[:, c0:c1], writes=[stg[i]])
                e = engs[k % 3]
                if e == 'act':
                    cx.op(e, [stg[i]], [outb[i]], lambda en: en.copy(outb[i][:, :w], stg[i][:, :w]))
                else:
                    cx.op(e, [stg[i]], [outb[i]], lambda en: en.tensor_copy(outb[i][:, :w], stg[i][:, :w]))
                cx.dma('act' if k % 2 else 'pool', d2[:, c0:c1], outb[i][:, :w], reads=[outb[i]])
                k += 1
    cx.barrier()


def load_bcast_row(cx, q, buf, row_ap, n):
    cx.dma(q, buf[:, :n], row_ap.partition_broadcast(128), writes=[buf])


def rms_rstd(cx, x_buf, x_ap, n, ss_buf, junk_buf, eps=NORM_EPS):
    cx.op('act', [x_buf], [junk_buf, ss_buf],
          lambda e: e.activation(out=junk_buf[:, :n], in_=x_ap, func=AF.Square, accum_out=ss_buf[:, 0:1]))
    cx.op('dve', [ss_buf], [ss_buf],
          lambda e: e.tensor_scalar(ss_buf[:, 0:1], ss_buf[:, 0:1], 1.0 / n, eps, ALU.mult, ALU.add))
    cx.op('pool', [ss_buf, cx.neghalf], [ss_buf],
          lambda e: e.tensor_tensor(ss_buf[:, 0:1], ss_buf[:, 0:1], cx.neghalf[:, 0:1], ALU.pow))


def setup_consts(cx, st, ident_d):
    cx.ident = cx.sb(st, [128, 128], BF16, name="ident")
    cx.dma('sp', cx.ident[:, :], ident_d[:, :], writes=[cx.ident])
    cx.identf = cx.sb(st, [128, 128], F32, name="identf")
    cx.op('dve', [cx.ident], [cx.identf], lambda e: e.tensor_copy(cx.identf[:, :], cx.ident[:, :]))
    cx.neghalf = cx.sb(st, [128, 1], F32, name="neghalf")
    cx.op('pool', [], [cx.neghalf], lambda e: e.memset(cx.neghalf[:, :], -0.5))
    cx.ones_bf = cx.sb(st, [128, 128], BF16, name="ones_bf")
    cx.op('pool', [], [cx.ones_bf], lambda e: e.memset(cx.ones_bf[:, :], 1.0))
    cx.ones_f = cx.sb(st, [128, 128], F32, name="ones_f")
    cx.op('pool', [], [cx.ones_f], lambda e: e.memset(cx.ones_f[:, :], 1.0))


def phase_in(cx, S, x_d, g_row, w_bf, P_d):
    G = 512
    KC = D_MODEL // 128
    chunks = []
    c = 0
    while c < OFF_GATE:
        chunks.append((c, min(c + 512, OFF_GATE), False))
        c += 512
    c = OFF_GATE
    while c < IN_WIDTH:
        chunks.append((c, c + 512, True))
        c += 512
    wv = w_bf.rearrange("(kc p) n -> p kc n", p=128)
    with contextlib.ExitStack() as st:
        gB = cx.sb(st, [128, D_MODEL], F32, name="gB")
        load_bcast_row(cx, 'sp', gB, g_row, D_MODEL)
        xt = [cx.sb(st, [128, D_MODEL], F32, name="xt") for _ in range(2)]
        junk = cx.sb(st, [128, D_MODEL], BF16, name="junk")
        ss = [cx.sb(st, [128, 1], F32, name="ss") for _ in range(2)]
        hb = [cx.sb(st, [128, D_MODEL], BF16, name="hb") for _ in range(2)]
        hT = [cx.sb(st, [128, KC, G], BF16, name="hT") for _ in range(2)]
        wb = [cx.sb(st, [128, KC, 512], BF16, name="wb") for _ in range(2)]
        ob = [cx.sb(st, [128, 512], F32, name="ob") for _ in range(4)]
        ptr = [cx.ps(st, [128, 8, 128], BF16, name="ptr") for _ in range(2)]
        pmm = [cx.ps(st, [128, 512], F32, name="pmm") for _ in range(4)]
        ntr = 0
        nmm = 0
        nw = 0
        for gi in range(S // G):
            hTg = hT[gi % 2]
            for tl in range(G // 128):
                tt = gi * (G // 128) + tl
                xb = xt[tt % 2]
                sb_ = ss[tt % 2]
                hbb = hb[tt % 2]
                cx.dma('sp', xb[:, :], x_d[tt * 128:(tt + 1) * 128, :], writes=[xb])
                rms_rstd(cx, xb, xb[:, :], D_MODEL, sb_, junk)
                cx.op('dve', [xb, sb_, gB], [hbb],
                      lambda e: e.scalar_tensor_tensor(out=hbb[:, :], in0=xb[:, :], scalar=sb_[:, 0:1],
                                                       in1=gB[:, :], op0=ALU.mult, op1=ALU.mult))
                for k4 in range(KC // 4):
                    pt = ptr[ntr % 2]
                    ntr += 1
                    for j in range(4):
                        kc = k4 * 4 + j
                        transp(cx, pt, pt[:, j, :], hbb[:, kc * 128:(kc + 1) * 128], cx.ident[:, :], [hbb, cx.ident])
                    eng = 'act' if (k4 % 2 == 0) else 'dve'
                    dst = hTg[:, k4 * 4:(k4 + 1) * 4, tl * 128:(tl + 1) * 128]
                    if eng == 'act':
                        cx.op('act', [pt], [hTg], lambda e: e.copy(dst, pt[:, 0:4, :]))
                    else:
                        cx.op('dve', [pt], [hTg], lambda e: e.tensor_copy(dst, pt[:, 0:4, :]))
            for (c0, c1, sig) in chunks:
                w = c1 - c0
                wbb = wb[nw % 2]
                nw += 1
                cx.dma('sp', wbb[:, :, :w], wv[:, :, c0:c1], writes=[wbb])
                for tl in range(G // 128):
                    tt = gi * (G // 128) + tl
                    pm = pmm[nmm % 4]
                    obb = ob[nmm % 4]
                    nmm += 1
                    for kc in range(KC):
                        mm(cx, pm, pm[:, :w], hTg[:, kc, tl * 128:(tl + 1) * 128], wbb[:, kc, :w],
                           [hTg, wbb], kc == 0, kc == KC - 1)
                    if sig:
                        cx.op('act', [pm], [obb],
                              lambda e: e.activation(out=obb[:, :w], in_=pm[:, :w], func=AF.Sigmoid))
                    elif nmm % 2 == 0:
                        cx.op('dve', [pm], [obb], lambda e: e.tensor_copy(obb[:, :w], pm[:, :w]))
                    else:
                        cx.op('act', [pm], [obb], lambda e: e.copy(obb[:, :w], pm[:, :w]))
                    cx.dma('pool', P_d[tt * 128:(tt + 1) * 128, c0:c1], obb[:, :w], reads=[obb])
    cx.barrier()


def evac(cx, eng, src_buf, src_ap, dst_buf, dst_ap, extra_reads=()):
    if eng == 'act':
        cx.op('act', [src_buf] + list(extra_reads), [dst_buf], lambda e: e.copy(dst_ap, src_ap))
    else:
        cx.op(eng, [src_buf] + list(extra_reads), [dst_buf], lambda e: e.tensor_copy(dst_ap, src_ap))


class TrPool:
    def __init__(self, cx, st, n=2, dtype=BF16):
        self.cx = cx
        self.bufs = [cx.ps(st, [128, 8 if dtype == BF16 else 4, 128], dtype, name="ptr") for _ in range(n)]
        self.k = 0
        self.dtype = dtype

    def transpose_cols(self, src_buf, src_ap_fn, nblk, dst_buf, dst_ap_fn, rows=128, blkw=128):
        cx = self.cx
        if cx._rec is not None:
            atomic(cx, lambda: self.transpose_cols(src_buf, src_ap_fn, nblk, dst_buf, dst_ap_fn, rows, blkw))
            return
        ident = cx.ident if self.dtype == BF16 else cx.identf
        j = 0
        while j < nblk:
            cnt = min(4, nblk - j)
            pt = self.bufs[self.k % len(self.bufs)]
            eng = 'act' if self.k % 2 == 0 else 'dve'
            self.k += 1
            for i in range(cnt):
                transp(cx, pt, pt[:blkw, i, :rows], src_ap_fn(j + i), ident[:rows, :rows], [src_buf, ident])
            evac(cx, eng, pt, pt[:blkw, :cnt, :rows], dst_buf, dst_ap_fn(j, cnt))
            j += cnt


def phase_merge(cx, S, ysrcs, wbr_bf, P_d, M_d):
    with contextlib.ExitStack() as st:
        wbr = cx.sb(st, [128, 16, D_MODEL], BF16, name="wbr")
        wv = wbr_bf.rearrange("m (kc p) n -> p (m kc) n", p=128)
        for q in range(4):
            cx.dma('sp', wbr[:, q * 4:(q + 1) * 4, :], wv[:, q * 4:(q + 1) * 4, :], writes=[wbr])
        yt = [cx.sb(st, [128, BW], F32, name="yt") for _ in range(3)]
        yb = [cx.sb(st, [128, BW], BF16, name="yb") for _ in range(2)]
        yT = [cx.sb(st, [128, 4, 128], BF16, name="yT") for _ in range(2)]
        sg = [cx.sb(st, [128, D_MODEL], F32, name="sg") for _ in range(2)]
        mg = [cx.sb(st, [128, D_MODEL], F32, name="mg") for _ in range(2)]
        tmp = [cx.sb(st, [128, 512], F32, name="tmp") for _ in range(2)]
        trp = TrPool(cx, st)
        pmm = [cx.ps(st, [128, 512], F32, name="pmm") for _ in range(4)]
        k = 0
        for tt in range(S // 128):
            rows = slice(tt * 128, (tt + 1) * 128)
            mgb = mg[tt % 2]
            for m in range(4):
                k += 1
                y0 = yt[k % 3]
                cx.dma('sp', y0[:, :], ysrcs[m][0][rows, :], writes=[y0])
                for extra in ysrcs[m][1:]:
                    k += 1
                    y1 = yt[k % 3]
                    cx.dma('sp', y1[:, :], extra[rows, :], writes=[y1])
                    cx.op('pool', [y0, y1], [y0], lambda e: e.tensor_tensor(y0[:, :], y0[:, :], y1[:, :], ALU.add))
                ybb = yb[m % 2]
                cx.op('pool', [y0], [ybb], lambda e: e.tensor_copy(ybb[:, :], y0[:, :]))
                yTb = yT[m % 2]
                trp.transpose_cols(ybb, lambda j: ybb[:, j * 128:(j + 1) * 128], 4, yTb,
                                   lambda j0, cnt: yTb[:, j0:j0 + cnt, :])
                sgb = sg[m % 2]
                cx.dma('act', sgb[:, :], P_d[rows, OFF_GATE + m * D_MODEL:OFF_GATE + (m + 1) * D_MODEL], writes=[sgb])
                for nc_ in range(4):
                    cs = slice(nc_ * 512, (nc_ + 1) * 512)
                    pm = pmm[(m * 4 + nc_) % 4]
                    for kc in range(4):
                        mm(cx, pm, pm[:, :], yTb[:, kc, :], wbr[:, m * 4 + kc, cs], [yTb, wbr], kc == 0, kc == 3)
                    if m == 0:
                        cx.op('dve', [pm, sgb], [mgb],
                              lambda e: e.tensor_tensor(mgb[:, cs], pm[:, :], sgb[:, cs], ALU.mult))
                    else:
                        tb = tmp[nc_ % 2]
                        cx.op('dve', [pm, sgb], [tb],
                              lambda e: e.tensor_tensor(tb[:, :], pm[:, :], sgb[:, cs], ALU.mult))
                        cx.op('pool', [tb, mgb], [mgb],
                              lambda e: e.tensor_tensor(mgb[:, cs], mgb[:, cs], tb[:, :], ALU.add))
            cx.dma('pool', M_d[rows, :], mgb[:, :], reads=[mgb])
    cx.barrier()


def phase_out(cx, S, M_d, wout_bf, Z_d):
    with contextlib.ExitStack() as st:
        wo = cx.sb(st, [128, 16, D_MODEL], BF16, name="wo")
        wv = wout_bf.rearrange("(kc p) n -> p kc n", p=128)
        for q in range(4):
            cx.dma('sp', wo[:, q * 4:(q + 1) * 4, :], wv[:, q * 4:(q + 1) * 4, :], writes=[wo])
        mt = [cx.sb(st, [128, D_MODEL], F32, name="mt") for _ in range(2)]
        mb = [cx.sb(st, [128, D_MODEL], BF16, name="mb") for _ in range(2)]
        mT = [cx.sb(st, [128, 16, 128], BF16, name="mT") for _ in range(2)]
        ob = [cx.sb(st, [128, 512], F32, name="ob") for _ in range(4)]
        trp = TrPool(cx, st)
        pmm = [cx.ps(st, [128, 512], F32, name="pmm") for _ in range(4)]
        k = 0
        for tt in range(S // 128):
            rows = slice(tt * 128, (tt + 1) * 128)
            mtb, mbb, mTb = mt[tt % 2], mb[tt % 2], mT[tt % 2]
            cx.dma('sp', mtb[:, :], M_d[rows, :], writes=[mtb])
            cx.op('pool', [mtb], [mbb], lambda e: e.tensor_copy(mbb[:, :], mtb[:, :]))
            trp.transpose_cols(mbb, lambda j: mbb[:, j * 128:(j + 1) * 128], 16, mTb,
                               lambda j0, cnt: mTb[:, j0:j0 + cnt, :])
            for nc_ in range(4):
                cs = slice(nc_ * 512, (nc_ + 1) * 512)
                pm = pmm[k % 4]
                obb = ob[k % 4]
                k += 1
                for kc in range(16):
                    mm(cx, pm, pm[:, :], mTb[:, kc, :], wo[:, kc, cs], [mTb, wo], kc == 0, kc == 15)
                evac(cx, 'act' if k % 2 else 'dve', pm, pm[:, :], obb, obb[:, :])
                cx.dma('pool', Z_d[rows, cs], obb[:, :], reads=[obb])
    cx.barrier()


def phase_normres(cx, S, x_d, Z_d, g_row, out_d):
    with contextlib.ExitStack() as st:
        gB = cx.sb(st, [128, D_MODEL], F32, name="gB")
        load_bcast_row(cx, 'sp', gB, g_row, D_MODEL)
        zt = [cx.sb(st, [128, D_MODEL], F32, name="zt") for _ in range(2)]
        xt = [cx.sb(st, [128, D_MODEL], F32, name="xt") for _ in range(2)]
        ot = [cx.sb(st, [128, D_MODEL], F32, name="ot") for _ in range(2)]
        junk = cx.sb(st, [128, D_MODEL], BF16, name="junk")
        ss = [cx.sb(st, [128, 1], F32, name="ss") for _ in range(2)]
        for tt in range(S // 128):
            rows = slice(tt * 128, (tt + 1) * 128)
            z, x, o, s_ = zt[tt % 2], xt[tt % 2], ot[tt % 2], ss[tt % 2]
            cx.dma('sp', z[:, :], Z_d[rows, :], writes=[z])
            cx.dma('act', x[:, :], x_d[rows, :], writes=[x])
            rms_rstd(cx, z, z[:, :], D_MODEL, s_, junk)
            cx.op('dve', [z, s_, gB], [o],
                  lambda e: e.scalar_tensor_tensor(out=o[:, :], in0=z[:, :], scalar=s_[:, 0:1], in1=gB[:, :],
                                                   op0=ALU.mult, op1=ALU.mult))
            cx.op('pool', [o, x], [o], lambda e: e.tensor_tensor(o[:, :], o[:, :], x[:, :], ALU.add))
            cx.dma('pool', out_d[rows, :], o[:, :], reads=[o])
    cx.barrier()


def phase_ffn(cx, S, x_d, g_row, wup_bf, wdn_bf, Z_d):
    G = 512 if S >= 512 else S
    NT = G // 128
    KC = D_MODEL // 128
    FC = D_FF // 128
    UW = 256
    wuv = wup_bf.rearrange("(kc p) f -> p kc f", p=128)
    wdv = wdn_bf.rearrange("(fc p) n -> p fc n", p=128)
    with contextlib.ExitStack() as st:
        gB = cx.sb(st, [128, D_MODEL], F32, name="gB")
        load_bcast_row(cx, 'sp', gB, g_row, D_MODEL)
        xt = [cx.sb(st, [128, D_MODEL], F32, name="xt") for _ in range(2)]
        junk = cx.sb(st, [128, D_MODEL], BF16, name="junk")
        ss = [cx.sb(st, [128, 1], F32, name="ss") for _ in range(2)]
        hb = [cx.sb(st, [128, D_MODEL], BF16, name="hb") for _ in range(2)]
        hT = cx.sb(st, [128, KC, G], BF16, name="hT")
        aT = cx.sb(st, [128, FC, G], BF16, name="aT")
        wu = [cx.sb(st, [128, KC, UW], BF16, name="wu") for _ in range(2)]
        wd = [cx.sb(st, [128, 8, 512], BF16, name="wd") for _ in range(2)]
        rl = [cx.sb(st, [128, G], F32, name="rl") for _ in range(2)]
        ob = [cx.sb(st, [128, 512], F32, name="ob") for _ in range(4)]
        trp = TrPool(cx, st, n=1)
        pup = [cx.ps(st, [128, G], F32, name="pup") for _ in range(2)]
        pdn = [cx.ps(st, [128, 512], F32, name="pdn") for _ in range(NT)]
        nu = 0
        nd = 0
        no = 0
        for gi in range(S // G):
            for tl in range(NT):
                tt = gi * NT + tl
                x, s_, h = xt[tt % 2], ss[tt % 2], hb[tt % 2]
                cx.dma('sp', x[:, :], x_d[tt * 128:(tt + 1) * 128, :], writes=[x])
                rms_rstd(cx, x, x[:, :], D_MODEL, s_, junk)
                cx.op('dve', [x, s_, gB], [h],
                      lambda e: e.scalar_tensor_tensor(out=h[:, :], in0=x[:, :], scalar=s_[:, 0:1], in1=gB[:, :],
                                                       op0=ALU.mult, op1=ALU.mult))
                trp.transpose_cols(h, lambda j: h[:, j * 128:(j + 1) * 128], KC, hT,
                                   lambda j0, cnt: hT[:, j0:j0 + cnt, tl * 128:(tl + 1) * 128])
            for uc in range(D_FF // UW):
                wub = wu[nu % 2]
                nu += 1
                cx.dma('sp', wub[:, :, :], wuv[:, :, uc * UW:(uc + 1) * UW], writes=[wub])
                for j in range(UW // 128):
                    fc = uc * (UW // 128) + j
                    pu = pup[fc % 2]
                    r = rl[fc % 2]
                    for kc in range(KC):
                        mm(cx, pu, pu[:, :], wub[:, kc, j * 128:(j + 1) * 128], hT[:, kc, :], [wub, hT],
                           kc == 0, kc == KC - 1)
                    cx.op('act', [pu], [r], lambda e: e.activation(out=r[:, :], in_=pu[:, :], func=AF.Relu))
                    eng = 'dve' if fc % 2 == 0 else 'pool'
                    cx.op(eng, [r], [aT], lambda e: e.tensor_tensor(aT[:, fc, :], r[:, :], r[:, :], ALU.mult))
            for nc_ in range(4):
                cs = slice(nc_ * 512, (nc_ + 1) * 512)
                for fg in range(FC // 8):
                    wdb = wd[nd % 2]
                    nd += 1
                    cx.dma('act', wdb[:, :, :], wdv[:, fg * 8:(fg + 1) * 8, cs], writes=[wdb])
                    for f8 in range(8):
                        fc = fg * 8 + f8
                        for tl in range(NT):
                            mm(cx, pdn[tl], pdn[tl][:, :], aT[:, fc, tl * 128:(tl + 1) * 128], wdb[:, f8, :],
                               [aT, wdb], fc == 0, fc == FC - 1)
                for tl in range(NT):
                    tt = gi * NT + tl
                    o = ob[no % 4]
                    no += 1
                    evac(cx, 'act' if no % 2 else 'dve', pdn[tl], pdn[tl][:, :], o, o[:, :])
                    cx.dma('pool', Z_d[tt * 128:(tt + 1) * 128, cs], o[:, :], reads=[o])
    cx.barrier()


def rope_tm(cx, src, x1, x2, c, s, dst, o1, o2, tmps, scale=None):
    (ta, tap), (tb, tbp) = tmps
    cx.op('dve', [src] + c[:1] + [], [ta], lambda e: e.tensor_tensor(tap, x1, c[1], ALU.mult))
    cx.op('pool', [src] + s[:1], [tb], lambda e: e.tensor_tensor(tbp, x2, s[1], ALU.mult))
    cx.op('dve', [ta, tb], [dst], lambda e: e.tensor_tensor(o1, tap, tbp, ALU.subtract))
    cx.op('pool', [src] + c[:1], [ta], lambda e: e.tensor_tensor(tap, x2, c[1], ALU.mult))
    cx.op('dve', [src] + s[:1], [tb], lambda e: e.tensor_tensor(tbp, x1, s[1], ALU.mult))
    cx.op('pool', [ta, tb], [dst], lambda e: e.tensor_tensor(o2, tap, tbp, ALU.add))
    if scale is not None:
        cx.op('pool', [dst], [dst], lambda e: e.tensor_scalar(o1, o1, scale, None, ALU.mult))
        cx.op('pool', [dst], [dst], lambda e: e.tensor_scalar(o2, o2, scale, None, ALU.mult))


def bcast_scalar_max(cx, st, trp_f, run_buf, out_col):
    pt = trp_f.bufs[0]
    transp(cx, pt, pt[0:1, 0, :], run_buf[:, 0:1], cx.identf[:, :], [run_buf, cx.identf])
    row = cx.sb(st, [1, 128], F32, name="mxrow")
    one = cx.sb(st, [1, 1], F32, name="mxone")
    evac(cx, 'dve', pt, pt[0:1, 0, :], row, row[:, :])
    cx.op('dve', [row], [one], lambda e: e.tensor_reduce(out=one[:, :], in_=row[:, :], axis=AX.X, op=ALU.max))
    mm(cx, pt, pt[:, 1, 0:1], cx.ones_f[0:1, :], one[0:1, 0:1], [cx.ones_f, one], True, True)
    evac(cx, 'dve', pt, pt[:, 1, 0:1], out_col, out_col[:, 0:1])


class AttnRes:
    def __init__(self, cx, st, W):
        self.sT = [cx.ps(st, [128, 512], F32, name="sT") for _ in range(2)]
        self.acc = [cx.ps(st, [128, 2, 256], F32, name="acc") for _ in range(4)]
        self.pT = [cx.sb(st, [128, 512], BF16, name="pT") for _ in range(3)]
        self.n = 0
        self.nq = 0


def attn_core(cx, res, S, qchunks, kchunks, vaug_fn, W, blocks_fn, epilogue, mode='softmax', qbs=None):
    QW = min(512, S)
    NJ = QW // 128
    for QB in (range(S // QW) if qbs is None else qbs):
        q0 = QB * QW
        blocks = blocks_fn(QB)
        accs = [res.acc[(res.nq % 2) * 2 + (j // 2)] for j in range(NJ)]
        res.nq += 1

        def stage1(b):
            sT = res.sT[res.n % 2]
            pT = res.pT[res.n % 3]
            res.n += 1
            nk, k0 = b['nk'], b['k0']
            nmm = len(qchunks) + (1 if b.get('extra') else 0)
            i = 0
            for (qb_, qap), (kb_, kap) in zip(qchunks, kchunks):
                ka = kap(k0, nk) if callable(kap) else kap[:, k0:k0 + nk]
                mm(cx, sT, sT[:nk, :QW], ka, qap[:, q0:q0 + QW], [qb_, kb_], i == 0, i == nmm - 1)
                i += 1
            if b.get('extra'):
                lb, lap, rb, rap = b['extra']
                mm(cx, sT, sT[:nk, :QW], lap, rap, list(lb) + list(rb), False, True)
            if mode == 'softmax':
                cx.op('act', [sT], [pT], lambda e: e.activation(out=pT[:nk, :QW], in_=sT[:nk, :QW], func=AF.Exp))
                if b.get('mask'):
                    mb, map_ = b['mask']
                    cx.op('pool' if nk == 128 else 'dve', [pT, mb], [pT],
                          lambda e: e.tensor_tensor(pT[:nk, :QW], pT[:nk, :QW], map_, ALU.mult))
            else:
                c, gb, gap = b['decay']
                cx.op('dve', [sT, gb], [pT],
                      lambda e: e.scalar_tensor_tensor(out=pT[:nk, :QW], in0=sT[:nk, :QW], scalar=float(c), in1=gap,
                                                       op0=ALU.mult, op1=ALU.mult))
            return pT

        def stage2(bi, b, pT):
            nk = b['nk']
            vb, vap = vaug_fn(b['kb'], nk)
            for j in range(NJ):
                a = accs[j]
                mm(cx, a, a[:, j % 2, :W], pT[:nk, j * 128:(j + 1) * 128], vap, [pT, vb],
                   bi == 0 and j % 2 == 0, bi == len(blocks) - 1, skip_group_check=True)

        prev = None
        for bi, b in enumerate(blocks):
            pT = stage1(b)
            if prev is not None:
                stage2(*prev)
            prev = (bi, b, pT)
        stage2(*prev)
        for j in range(NJ):
            epilogue(QB, j, accs[j], accs[j][:, j % 2, :W])


def causal_blocks(QB, QW, cmask):
    out = []
    nd = QW // 128
    for kb in range(nd * (QB + 1)):
        i = kb - nd * QB
        out.append(dict(kb=kb, k0=kb * 128, nk=128, mask=(cmask, cmask[:, i, :QW]) if i >= 0 else None))
    return out


MLA_SCALE = 192 ** -0.5


def phase_mla(cx, S, P_d, gq_row, gkv_row, wuq_bf, wukv_bf, cos_d, sin_d, cmask_d, Y_d, scr):
    NT = S // 128
    G = min(512, S)
    NG = G // 128
    with contextlib.ExitStack() as st:
        with contextlib.ExitStack() as s1:
            gq = cx.sb(s1, [128, 384], F32, name="gq")
            gkv = cx.sb(s1, [128, 128], F32, name="gkv")
            load_bcast_row(cx, 'sp', gq, gq_row, 384)
            load_bcast_row(cx, 'sp', gkv, gkv_row, 128)
            wuq = cx.sb(s1, [128, 3, 768], BF16, name="wuq")
            cx.dma('sp', wuq[:, :, :], wuq_bf.rearrange("(kc p) n -> p kc n", p=128), writes=[wuq])
            wukv = cx.sb(s1, [128, 1024], BF16, name="wukv")
            cx.dma('sp', wukv[:, :], wukv_bf[:, :], writes=[wukv])
            pm = [cx.sb(s1, [128, 576], F32, name="pm") for _ in range(2)]
            cs_t = [cx.sb(s1, [128, 64], F32, name="cs") for _ in range(2)]
            junk = cx.sb(s1, [128, 768], BF16, name="junk")
            ss = [cx.sb(s1, [128, 1], F32, name="ss") for _ in range(2)]
            nb = [cx.sb(s1, [128, 384], BF16, name="nb") for _ in range(2)]
            nT = [cx.sb(s1, [128, 3, 128], BF16, name="nT") for _ in range(2)]
            qf = [cx.sb(s1, [128, 4, 256], F32, name="qf") for _ in range(2)]
            qs = [cx.sb(s1, [128, 4, 193], BF16, name="qs") for _ in range(2)]
            qsf = [cx.sb(s1, [128, 4, 64], F32, name="qsf") for _ in range(2)]
            t1s = [cx.sb(s1, [128, 4, 32], F32, name="t1") for _ in range(2)]
            t2s = [cx.sb(s1, [128, 4, 32], F32, name="t2") for _ in range(2)]
            kr = [cx.sb(s1, [128, 65], BF16, name="kr") for _ in range(2)]
            krf = [cx.sb(s1, [128, 64], F32, name="krf") for _ in range(2)]
            kb16 = [cx.sb(s1, [128, 4, 128], BF16, name="kb16") for _ in range(2)]
            va = [cx.sb(s1, [128, 4, 129], BF16, name="va") for _ in range(2)]
            sqs = [cx.sb(s1, [128, 4, 256], F32, name="sq") for _ in range(2)]
            n4 = [cx.sb(s1, [128, 4], F32, name="n4") for _ in range(2)]
            n1 = [cx.sb(s1, [128, 1], F32, name="n1") for _ in range(2)]
            kmx = cx.sb(s1, [128, 1], F32, name="kmx")
            kmax = cx.sb(s1, [128, 1], F32, name="kmax")
            gA = cx.sb(s1, [128, 4, G], BF16, name="gA")
            gB_ = cx.sb(s1, [65, 4, G], BF16, name="gB_")
            trp = TrPool(cx, s1)
            trf = TrPool(cx, s1, n=1, dtype=F32)
            pq = [cx.ps(s1, [128, 512], F32, name="pq") for _ in range(2)]
            pq2 = [cx.ps(s1, [128, 512], F32, name="pq2") for _ in range(2)]
            cx.op('pool', [], [kmx], lambda e: e.memset(kmx[:, :], 0.0))

            def load_norm_T(tt, c0, n, gbuf, k):
                p = pm[k % 2]
                cx.dma('sp', p[:, :], P_d[tt * 128:(tt + 1) * 128, OFF_MLA:OFF_MLA + 576], writes=[p])
                s_ = ss[k % 2]
                rms_rstd(cx, p, p[:, c0:c0 + n], n, s_, junk)
                nbb = nb[k % 2]
                cx.op('dve', [p, s_, gbuf], [nbb],
                      lambda e: e.scalar_tensor_tensor(out=nbb[:, :n], in0=p[:, c0:c0 + n], scalar=s_[:, 0:1],
                                                       in1=gbuf[:, :n], op0=ALU.mult, op1=ALU.mult))
                nTb = nT[k % 2]
                trp.transpose_cols(nbb, lambda j: nbb[:, j * 128:(j + 1) * 128], n // 128, nTb,
                                   lambda j0, cnt: nTb[:, j0:j0 + cnt, :])
                return p, nTb

            def body(tt):
                t1, t2, sq = t1s[tt % 2], t2s[tt % 2], sqs[tt % 2]
                tl = tt % NG
                p, nTb = load_norm_T(tt, 384, 128, gkv, tt)
                c_t = cs_t[tt % 2]
                cx.dma('act', c_t[:, 0:32], cos_d[tt * 128:(tt + 1) * 128, :], writes=[c_t])
                cx.dma('act', c_t[:, 32:64], sin_d[tt * 128:(tt + 1) * 128, :], writes=[c_t])
                pa, pb = pq[tt % 2], pq2[tt % 2]
                mm(cx, pa, pa[:, :], nTb[:, 0, :], wukv[:, 0:512], [nTb, wukv], True, True)
                mm(cx, pb, pb[:, :], nTb[:, 0, :], wukv[:, 512:1024], [nTb, wukv], True, True)
                q = qf[tt % 2]
                evac(cx, 'act', pa, pa[:, :].rearrange("p (h c) -> p h c", h=2), q, q[:, 0:2, :])
                evac(cx, 'dve', pb, pb[:, :].rearrange("p (h c) -> p h c", h=2), q, q[:, 2:4, :])
                k16, vab, krb, krfb = kb16[tt % 2], va[tt % 2], kr[tt % 2], krf[tt % 2]
                cx.op('pool', [q], [k16], lambda e: e.tensor_copy(k16[:, :, :], q[:, :, 0:128]))
                cx.op('pool', [q], [vab], lambda e: e.tensor_copy(vab[:, :, 0:128], q[:, :, 128:256]))
                cx.op('pool', [], [vab], lambda e: e.memset(vab[:, :, 128:129], 1.0))
                rope_tm(cx, p, p[:, 512:544], p[:, 544:576], [c_t, c_t[:, 0:32]], [c_t, c_t[:, 32:64]],
                        krfb, krfb[:, 0:32], krfb[:, 32:64], [(t1, t1[:, 0, :]), (t2, t2[:, 0, :])])
                cx.op('pool', [krfb], [krb], lambda e: e.tensor_copy(krb[:, 0:64], krfb[:, :]))
                cx.op('pool', [], [krb], lambda e: e.memset(krb[:, 64:65], 1.0))
                cx.op('dve', [q], [sq], lambda e: e.tensor_tensor(sq[:, :, 0:128], q[:, :, 0:128], q[:, :, 0:128], ALU.mult))
                n4b, n1b = n4[tt % 2], n1[tt % 2]
                cx.op('dve', [sq], [n4b], lambda e: e.tensor_reduce(out=n4b[:, :], in_=sq[:, :, 0:128], axis=AX.X, op=ALU.add))
                cx.op('dve', [n4b], [n1b], lambda e: e.tensor_reduce(out=n1b[:, :], in_=n4b[:, :], axis=AX.X, op=ALU.max))
                cx.op('act', [krfb], [junk, n4b],
                      lambda e: e.activation(out=junk[:, :64], in_=krfb[:, :], func=AF.Square, accum_out=n4b[:, 0:1]))
                cx.op('dve', [n4b, n1b], [n1b], lambda e: e.tensor_tensor(n1b[:, :], n1b[:, :], n4b[:, 0:1], ALU.add))
                cx.op('dve', [n1b, kmx], [kmx], lambda e: e.tensor_tensor(kmx[:, :], kmx[:, :], n1b[:, :], ALU.max))
                trp.transpose_cols(k16, lambda j: k16[:, j, :], 4, gA,
                                   lambda j0, cnt: gA[:, j0:j0 + cnt, tl * 128:(tl + 1) * 128])
                trp.transpose_cols(krb, lambda j: krb[:, :], 1, gB_,
                                   lambda j0, cnt: gB_[:65, 0:1, tl * 128:(tl + 1) * 128], blkw=65)
                cx.dma('pool', scr['va'][tt * 128:(tt + 1) * 128, :, :], vab[:, :, :], reads=[vab])
                if tl == NG - 1:
                    def post():
                        g0 = (tt // NG) * G
                        cx.dma('pool', scr['knT'][:, :, g0:g0 + G].rearrange("h d s -> d h s"), gA[:, :, :], reads=[gA])
                        cx.dma('pool', scr['krT'][:, g0:g0 + G], gB_[:65, 0, :], reads=[gB_])
                    return post
            run_tiles(cx, body, NT)
            bcast_scalar_max(cx, s1, trf, kmx, kmax)
            cx.op('act', [kmax], [kmax], lambda e: e.activation(out=kmax[:, :], in_=kmax[:, :], func=AF.Sqrt))
            def body(tt):
                t1, t2, sq = t1s[tt % 2], t2s[tt % 2], sqs[tt % 2]
                tl = tt % NG
                p, nTb = load_norm_T(tt, 0, 384, gq, tt)
                c_t = cs_t[tt % 2]
                cx.dma('act', c_t[:, 0:32], cos_d[tt * 128:(tt + 1) * 128, :], writes=[c_t])
                cx.dma('act', c_t[:, 32:64], sin_d[tt * 128:(tt + 1) * 128, :], writes=[c_t])
                pa, pb = pq[tt % 2], pq2[tt % 2]
                for kc in range(3):
                    mm(cx, pa, pa[:, :], nTb[:, kc, :], wuq[:, kc, 0:512], [nTb, wuq], kc == 0, kc == 2)
                for kc in range(3):
                    mm(cx, pb, pb[:, :256], nTb[:, kc, :], wuq[:, kc, 512:768], [nTb, wuq], kc == 0, kc == 2)
                q = qf[tt % 2]
                qv = q[:, :, :].rearrange("p h c -> p (h c)")
                evac(cx, 'act', pa, pa[:, :], q, qv[:, 0:512])
                evac(cx, 'dve', pb, pb[:, :256], q, qv[:, 512:768])
                qh = qv[:, 0:768].rearrange("p (h c) -> p h c", h=4)
                cx.op('dve', [q], [sq], lambda e: e.tensor_tensor(sq[:, :, 0:192], qh, qh, ALU.mult))
                n4b = n4[tt % 2]
                cx.op('dve', [sq], [n4b], lambda e: e.tensor_reduce(out=n4b[:, :], in_=sq[:, :, 0:192], axis=AX.X, op=ALU.add))
                cx.op('act', [n4b], [n4b], lambda e: e.activation(out=n4b[:, :], in_=n4b[:, :], func=AF.Sqrt))
                cx.op('dve', [n4b, kmax], [n4b],
                      lambda e: e.tensor_scalar(n4b[:, :], n4b[:, :], kmax[:, 0:1], -MLA_SCALE, ALU.mult, ALU.mult))
                qsb, qsfb = qs[tt % 2], qsf[tt % 2]
                cb = c_t[:, 0:32].unsqueeze(1).broadcast_to([128, 4, 32])
                sb_ = c_t[:, 32:64].unsqueeze(1).broadcast_to([128, 4, 32])
                rope_tm(cx, q, qh[:, :, 128:160], qh[:, :, 160:192], [c_t, cb], [c_t, sb_],
                        qsfb, qsfb[:, :, 0:32], qsfb[:, :, 32:64], [(t1, t1[:, :, :]), (t2, t2[:, :, :])])
                cx.op('act', [q], [qsb], lambda e: e.activation(out=qsb[:, :, 0:128], in_=qh[:, :, 0:128], func=AF.Copy, scale=MLA_SCALE))
                cx.op('act', [qsfb], [qsb], lambda e: e.activation(out=qsb[:, :, 128:192], in_=qsfb[:, :, :], func=AF.Copy, scale=MLA_SCALE))
                cx.op('pool', [n4b], [qsb], lambda e: e.tensor_copy(qsb[:, :, 192:193], n4b[:, :].unsqueeze(2)))
                trp.transpose_cols(qsb, lambda j: qsb[:, j, 0:128], 4, gA,
                                   lambda j0, cnt: gA[:, j0:j0 + cnt, tl * 128:(tl + 1) * 128])
                trp.transpose_cols(qsb, lambda j: qsb[:, j, 128:193], 4, gB_,
                                   lambda j0, cnt: gB_[:65, j0:j0 + cnt, tl * 128:(tl + 1) * 128], blkw=65)
                if tl == NG - 1:
                    def post():
                        g0 = (tt // NG) * G
                        cx.dma('pool', scr['qnT'][:, :, g0:g0 + G].rearrange("h d s -> d h s"), gA[:, :, :], reads=[gA])
                        cx.dma('pool', scr['qrT'][:, :, g0:g0 + G].rearrange("h d s -> d h s"), gB_[:65, :, :], reads=[gB_])
                    return post
            run_tiles(cx, body, NT)
        cx.barrier()
        res = AttnRes(cx, st, 129)
        cmask = cx.sb(st, [128, 4, 512], BF16, name="cmask")
        cx.dma('sp', cmask[:, :, :], cmask_d.rearrange("i k q -> k i q"), writes=[cmask])
        krT = cx.sb(st, [65, S], BF16, name="krT")
        cx.dma('sp', krT[:, :], scr['krT'][:, :], writes=[krT])
        qn = [cx.sb(st, [128, S], BF16, name="qn") for _ in range(2)]
        qr = [cx.sb(st, [65, S], BF16, name="qr") for _ in range(2)]
        kn = [cx.sb(st, [128, S], BF16, name="kn") for _ in range(2)]
        vv = [cx.sb(st, [128, NT, 129], BF16, name="vv") for _ in range(2)]
        rc = [cx.sb(st, [128, 1], F32, name="rc") for _ in range(2)]
        ot = [cx.sb(st, [128, 128], F32, name="ot") for _ in range(2)]
        cnt = [0]
        QW = min(512, S)
        for h in range(4):
            a, b, c, v = qn[h % 2], qr[h % 2], kn[h % 2], vv[h % 2]
            cx.dma('sp', a[:, :], scr['qnT'][h], writes=[a])
            cx.dma('sp', b[:, :], scr['qrT'][h], writes=[b])
            cx.dma('sp', c[:, :], scr['knT'][h], writes=[c])
            cx.dma('sp', v[:, :, :], scr['va'][:, h, :].rearrange("(t p) c -> p t c", p=128), writes=[v])

            def epi(QB, j, accb, acc_ap, h=h):
                k = cnt[0]
                cnt[0] += 1
                r, o = rc[k % 2], ot[k % 2]
                cx.op('dve', [accb], [r], lambda e: e.tensor_scalar(r[:, :], acc_ap[:, 128:129], 1e-30, None, ALU.add))
                cx.op('dve', [r], [r], lambda e: e.reciprocal(r[:, :], r[:, :]))
                cx.op('act', [accb, r], [o], lambda e: e.activation(out=o[:, :], in_=acc_ap[:, 0:128], func=AF.Copy, scale=r[:, 0:1]))
                t0 = QB * QW + j * 128
                cx.dma('pool', Y_d[t0:t0 + 128, h * 128:(h + 1) * 128], o[:, :], reads=[o])

            attn_core(cx, res, S, [(a, a[:, :]), (b, b[:65, :])], [(c, c[:, :]), (krT, krT[:65, :])],
                      lambda kb, nk, v=v: (v, v[:nk, kb, :]), 129,
                      lambda QB: causal_blocks(QB, QW, cmask), epi)
    cx.barrier()


def head_norm_tm(cx, src_buf, src_ap, n, eps, cbuf, c_ap, s1, s2, junk):
    cx.op('dve', [src_buf], [s1], lambda e: e.tensor_reduce(out=s1[:, 0:1], in_=src_ap, axis=AX.X, op=ALU.add))
    cx.op('dve', [s1], [s1], lambda e: e.tensor_scalar(s1[:, 0:1], s1[:, 0:1], 1.0 / n, None, ALU.mult))
    cx.op('dve', [src_buf, s1], [cbuf], lambda e: e.tensor_scalar(c_ap, src_ap, s1[:, 0:1], None, ALU.subtract))
    rms_rstd(cx, cbuf, c_ap, n, s2, junk, eps=eps)


def phase_ret(cx, S, P_d, cos_d, sin_d, gdec_d, Y_d, scr):
    NT = S // 128
    G = min(512, S)
    NG = G // 128
    QW = min(512, S)
    with contextlib.ExitStack() as st:
        with contextlib.ExitStack() as s1:
            pr = [cx.sb(s1, [128, 1024], F32, name="pr") for _ in range(2)]
            cs_t = [cx.sb(s1, [128, 64], F32, name="cs") for _ in range(2)]
            ro = [cx.sb(s1, [128, 8, 64], F32, name="ro") for _ in range(2)]
            rb = [cx.sb(s1, [128, 8, 64], BF16, name="rb") for _ in range(2)]
            vb = [cx.sb(s1, [128, 512], BF16, name="vb") for _ in range(2)]
            t1s = [cx.sb(s1, [128, 8, 32], F32, name="t1") for _ in range(2)]
            t2s = [cx.sb(s1, [128, 8, 32], F32, name="t2") for _ in range(2)]
            gQ = cx.sb(s1, [64, 8, G], BF16, name="gQ")
            trp = TrPool(cx, s1)
            def body(tt):
                t1, t2 = t1s[tt % 2], t2s[tt % 2]
                tl = tt % NG
                rows = slice(tt * 128, (tt + 1) * 128)
                p, c_t, r, rbb, v = pr[tt % 2], cs_t[tt % 2], ro[tt % 2], rb[tt % 2], vb[tt % 2]
                cx.dma('sp', p[:, :], P_d[rows, OFF_RET:OFF_RET + 1024], writes=[p])
                cx.dma('act', c_t[:, 0:32], cos_d[rows, :], writes=[c_t])
                cx.dma('act', c_t[:, 32:64], sin_d[rows, :], writes=[c_t])
                qk = p[:, 0:512].rearrange("p (h c) -> p h c", h=8)
                cb = c_t[:, 0:32].unsqueeze(1).broadcast_to([128, 8, 32])
                sb_ = c_t[:, 32:64].unsqueeze(1).broadcast_to([128, 8, 32])
                rope_tm(cx, p, qk[:, :, 0:32], qk[:, :, 32:64], [c_t, cb], [c_t, sb_],
                        r, r[:, :, 0:32], r[:, :, 32:64], [(t1, t1[:, :, :]), (t2, t2[:, :, :])])
                cx.op('act', [r], [rbb], lambda e: e.copy(rbb[:, 0:4, :], r[:, 0:4, :]))
                cx.op('act', [r], [rbb], lambda e: e.activation(out=rbb[:, 4:8, :], in_=r[:, 4:8, :], func=AF.Copy, scale=0.125))
                cx.op('pool', [p], [v], lambda e: e.tensor_copy(v[:, :], p[:, 512:1024]))
                trp.transpose_cols(rbb, lambda j: rbb[:, j, :], 8, gQ,
                                   lambda j0, cnt: gQ[:64, j0:j0 + cnt, tl * 128:(tl + 1) * 128], blkw=64)
                cx.dma('pool', scr['rv'][rows, :, :].rearrange("s h c -> s (h c)"), v[:, :], reads=[v])
                if tl == NG - 1:
                    def post():
                        g0 = (tt // NG) * G
                        cx.dma('pool', scr['rqT'][:, :, g0:g0 + G].rearrange("h d s -> d h s"), gQ[:64, 0:4, :], reads=[gQ])
                        cx.dma('pool', scr['rkT'][:, :, g0:g0 + G].rearrange("h d s -> d h s"), gQ[:64, 4:8, :], reads=[gQ])
                    return post
            run_tiles(cx, body, NT)
        cx.barrier()
        res = AttnRes(cx, st, 128)
        gd = [cx.sb(st, [128, 5, 512], F32, name="gd") for _ in range(2)]
        qT = [cx.sb(st, [64, S], BF16, name="qT") for _ in range(2)]
        kT = [cx.sb(st, [64, S], BF16, name="kT") for _ in range(2)]
        vv = [cx.sb(st, [128, NT, 128], BF16, name="vv") for _ in range(2)]
        gt = [cx.sb(st, [128, 128], F32, name="gt") for _ in range(2)]
        cb_ = [cx.sb(st, [128, 128], F32, name="cb") for _ in range(2)]
        ot = [cx.sb(st, [128, 128], F32, name="ot") for _ in range(2)]
        sA = [cx.sb(st, [128, 1], F32, name="sA") for _ in range(2)]
        sB = [cx.sb(st, [128, 1], F32, name="sB") for _ in range(2)]
        junk = cx.sb(st, [128, 128], BF16, name="junk")
        cnt = [0]
        for h in range(4):
            gamma = 1.0 - 2.0 ** (-5 - h)
            a, c, v, g = qT[h % 2], kT[h % 2], vv[h % 2], gd[h % 2]
            cx.dma('sp', a[:, :], scr['rqT'][h], writes=[a])
            cx.dma('sp', c[:, :], scr['rkT'][h], writes=[c])
            cx.dma('sp', v[:, :, :], scr['rv'][:, h, :].rearrange("(t p) c -> p t c", p=128), writes=[v])
            cx.dma('sp', g[:, :, :], gdec_d[h].rearrange("i k q -> k i q"), writes=[g])
            if 'dbg' in scr and h == 0:
                cx.dma('sp', scr['dbg'][0], a[:, :], reads=[a])
                cx.dma('sp', scr['dbg'][1], c[:, :], reads=[c])

            def blocks(QB, g=g, gamma=gamma):
                out = []
                nd = QW // 128
                for kb in range(nd * (QB + 1)):
                    i = kb - nd * QB
                    if i >= 0:
                        out.append(dict(kb=kb, k0=kb * 128, nk=128, decay=(1.0, g, g[:, 1 + i, :QW])))
                    else:
                        cc = gamma ** (QB * QW - kb * 128)
                        if cc < 1e-30:
                            cc = 0.0
                        out.append(dict(kb=kb, k0=kb * 128, nk=128, decay=(cc, g, g[:, 0, :QW])))
                return out

            def epi(QB, j, accb, acc_ap, h=h):
                k = cnt[0]
                cnt[0] += 1
                t0 = QB * QW + j * 128
                gtb, cbb, o, s_a, s_b = gt[k % 2], cb_[k % 2], ot[k % 2], sA[k % 2], sB[k % 2]
                cx.dma('act', gtb[:, :], P_d[t0:t0 + 128, OFF_RET + 1024 + h * 128:OFF_RET + 1024 + (h + 1) * 128], writes=[gtb])
                cx.op('act', [gtb], [gtb], lambda e: e.activation(out=gtb[:, :], in_=gtb[:, :], func=AF.Silu))
                head_norm_tm(cx, accb, acc_ap, 128, NORM_EPS, cbb, cbb[:, :], s_a, s_b, junk)
                cx.op('dve', [cbb, s_b, gtb], [o],
                      lambda e: e.scalar_tensor_tensor(out=o[:, :], in0=cbb[:, :], scalar=s_b[:, 0:1], in1=gtb[:, :],
                                                       op0=ALU.mult, op1=ALU.mult))
                cx.dma('pool', Y_d[t0:t0 + 128, h * 128:(h + 1) * 128], o[:, :], reads=[o])

            attn_core(cx, res, S, [(a, a[:64, :])], [(c, c[:64, :])], lambda kb, nk, v=v: (v, v[:nk, kb, :]), 128,
                      blocks, epi, mode='decay')
    cx.barrier()


def bc8(ap):
    return ap.unsqueeze(2).broadcast_to([128, 8, 64])


def v3(ap):
    return ap.rearrange("p (h c) -> p h c", h=8)


def phase_rwkv(cx, S, l, P_d, W, Wb, C, Y_d, scr):
    NT = S // 128
    RW = scr['rw']
    names6 = ['rr', 'lw', 'k2', 'vv', 'kn', 'aa']
    with contextlib.ExitStack() as st:
        def brow(name, n, src):
            b = cx.sb(st, [128, n], F32, name=name)
            load_bcast_row(cx, 'sp', b, src, n)
            return b
        muB = brow("muB", 1984, W['rwkv_mu'][l])
        w0B = brow("w0B", 512, W['rwkv_w0'][l])
        a0B = brow("a0B", 512, W['rwkv_a0'][l])
        kkB = brow("kkB", 512, W['rwkv_k_k'][l])
        kaB = brow("kaB", 512, W['rwkv_k_a'][l])
        rkB = brow("rkB", 512, W['rwkv_r_k'][l].rearrange("h c -> (h c)"))
        w2 = cx.sb(st, [96, 512], BF16, name="w2")
        a2 = cx.sb(st, [96, 512], BF16, name="a2")
        g2 = cx.sb(st, [128, 2, 512], BF16, name="g2")
        cx.dma('sp', w2[:, :], Wb['rwkv_w2'][l], writes=[w2])
        cx.dma('sp', a2[:, :], Wb['rwkv_a2'][l], writes=[a2])
        cx.dma('sp', g2[:, :, :], Wb['rwkv_g2'][l].rearrange("(kc p) n -> p kc n", p=128), writes=[g2])
        z = [cx.sb(st, [128, 1984], F32, name="z") for _ in range(2)]
        zp = [cx.sb(st, [128, 1984], F32, name="zp") for _ in range(2)]
        lo = [cx.sb(st, [128, 512], BF16, name="lo") for _ in range(2)]
        loT = [cx.sb(st, [128, 4, 128], BF16, name="loT") for _ in range(2)]
        o7 = [cx.sb(st, [128, 7, 512], F32, name="o7") for _ in range(2)]
        s8 = [cx.sb(st, [128, 8], F32, name="s8") for _ in range(2)]
        b8 = [cx.sb(st, [128, 8], F32, name="b8") for _ in range(2)]
        trp = TrPool(cx, st)
        pps = [[cx.ps(st, [128, 512], F32, name="pp") for _ in range(3)] for _ in range(2)]
        def body(tt):
            rows = slice(tt * 128, (tt + 1) * 128)
            zb, zpb, lob, loTb, o, s8b, b8b = z[tt % 2], zp[tt % 2], lo[tt % 2], loT[tt % 2], o7[tt % 2], s8[tt % 2], b8[tt % 2]
            cx.dma('sp', zb[:, :], P_d[rows, OFF_RWKV:OFF_RWKV + 1984], writes=[zb])
            if tt == 0:
                cx.op('pool', [], [zpb], lambda e: e.memset(zpb[0:1, :], 0.0))
                cx.dma('act', zpb[1:128, :], P_d[0:127, OFF_RWKV:OFF_RWKV + 1984], writes=[zpb])
            else:
                cx.dma('act', zpb[:, :], P_d[tt * 128 - 1:tt * 128 + 127, OFF_RWKV:OFF_RWKV + 1984], writes=[zpb])
            cx.op('pool', [zpb, zb], [zpb], lambda e: e.tensor_tensor(zpb[:, :], zpb[:, :], zb[:, :], ALU.subtract))
            cx.op('dve', [zpb, muB], [zpb], lambda e: e.tensor_tensor(zpb[:, :], zpb[:, :], muB[:, :], ALU.mult))
            cx.op('pool', [zpb, zb], [zb], lambda e: e.tensor_tensor(zb[:, :], zb[:, :], zpb[:, :], ALU.add))
            r_, k_, v_ = zb[:, 0:512], zb[:, 512:1024], zb[:, 1024:1536]
            cx.op('act', [zb], [lob], lambda e: e.activation(out=lob[:, 0:96], in_=zb[:, 1536:1632], func=AF.Tanh))
            cx.op('act', [zb], [lob], lambda e: e.copy(lob[:, 128:224], zb[:, 1632:1728]))
            cx.op('act', [zb], [lob], lambda e: e.activation(out=lob[:, 256:512], in_=zb[:, 1728:1984], func=AF.Sigmoid))
            trp.transpose_cols(lob, lambda j: lob[:, j * 128:j * 128 + 96], 2, loTb,
                               lambda j0, cnt: loTb[:96, j0:j0 + cnt, :], blkw=96)
            trp.transpose_cols(lob, lambda j: lob[:, 256 + j * 128:384 + j * 128], 2, loTb,
                               lambda j0, cnt: loTb[:, 2 + j0:2 + j0 + cnt, :])
            pu, pa, pg = pps[tt % 2]
            mm(cx, pu, pu[:, :], loTb[:96, 0, :], w2[:96, :], [loTb, w2], True, True)
            mm(cx, pa, pa[:, :], loTb[:96, 1, :], a2[:96, :], [loTb, a2], True, True)
            mm(cx, pg, pg[:, :], loTb[:, 2, :], g2[:, 0, :], [loTb, g2], True, False)
            mm(cx, pg, pg[:, :], loTb[:, 3, :], g2[:, 1, :], [loTb, g2], False, True)
            lw_, k2_, kn_, aa_, gg_, t1_, t2_ = [o[:, i, :] for i in range(7)]
            cx.op('dve', [pu, w0B], [o], lambda e: e.tensor_tensor(t1_, pu[:, :], w0B[:, :], ALU.add))
            cx.op('act', [o], [o], lambda e: e.activation(out=t1_, in_=t1_, func=AF.Sigmoid))
            cx.op('pool', [o], [o], lambda e: e.tensor_scalar(lw_, t1_, -0.6065306597126334, None, ALU.mult))
            cx.op('dve', [pa, a0B], [o], lambda e: e.tensor_tensor(t2_, pa[:, :], a0B[:, :], ALU.add))
            cx.op('act', [o], [o], lambda e: e.activation(out=aa_, in_=t2_, func=AF.Sigmoid))
            cx.op('act', [pg], [o], lambda e: e.copy(gg_, pg[:, :]))
            cx.op('dve', [zb, kkB], [o], lambda e: e.tensor_tensor(kn_, k_, kkB[:, :], ALU.mult))
            cx.op('pool', [o], [o], lambda e: e.tensor_tensor(t1_, kn_, kn_, ALU.mult))
            cx.op('dve', [o], [s8b], lambda e: e.tensor_reduce(out=s8b[:, :], in_=v3(t1_), axis=AX.X, op=ALU.add))
            cx.op('dve', [s8b], [s8b], lambda e: e.tensor_scalar(s8b[:, :], s8b[:, :], 1e-24, None, ALU.max))
            cx.op('pool', [s8b, cx.neghalf], [s8b],
                  lambda e: e.tensor_tensor(s8b[:, :], s8b[:, :], cx.neghalf[:, 0:1].to_broadcast([128, 8]), ALU.pow))
            cx.op('dve', [o, s8b], [o], lambda e: e.tensor_tensor(v3(kn_), v3(kn_), bc8(s8b[:, :]), ALU.mult))
            cx.op('dve', [o, kaB], [o],
                  lambda e: e.scalar_tensor_tensor(out=t2_, in0=aa_, scalar=-1.0, in1=kaB[:, :], op0=ALU.add, op1=ALU.mult))
            cx.op('pool', [o], [o], lambda e: e.tensor_scalar(t2_, t2_, 1.0, None, ALU.add))
            cx.op('dve', [o, zb], [o], lambda e: e.tensor_tensor(k2_, k_, t2_, ALU.mult))
            cx.op('pool', [o, zb], [o], lambda e: e.tensor_tensor(t1_, r_, k2_, ALU.mult))
            cx.op('dve', [o, rkB], [o], lambda e: e.tensor_tensor(t1_, t1_, rkB[:, :], ALU.mult))
            cx.op('dve', [o], [b8b], lambda e: e.tensor_reduce(out=b8b[:, :], in_=v3(t1_), axis=AX.X, op=ALU.add))
            cx.dma('pool', RW['rr'][rows, :], r_, reads=[zb])
            cx.dma('pool', RW['vv'][rows, :], v_, reads=[zb])
            cx.dma('pool', RW['lw'][rows, :], lw_, reads=[o])
            cx.dma('pool', RW['k2'][rows, :], k2_, reads=[o])
            cx.dma('pool', RW['kn'][rows, :], kn_, reads=[o])
            cx.dma('pool', RW['aa'][rows, :], aa_, reads=[o])
            cx.dma('pool', RW['gg'][rows, :], gg_, reads=[o])
            cx.dma('pool', RW['bc'][rows, :], b8b[:, :], reads=[b8b])
        run_tiles(cx, body, NT)
    cx.barrier()
    with contextlib.ExitStack() as st:
        rwm = cx.sb(st, [128, 384], F32, name="rwm")
        cx.dma('sp', rwm[:, :], C['rwm'][:, :], writes=[rwm])
        mask4 = cx.sb(st, [128, 512], F32, name="mask4")
        cx.op('pool', [rwm], [mask4], lambda e: e.tensor_copy(mask4[:, 0:256], rwm[:, 0:256]))
        cx.op('pool', [rwm], [mask4], lambda e: e.tensor_copy(mask4[:, 256:512], rwm[:, 0:256]))
        gwB = cx.sb(st, [128, 512], F32, name="gwB")
        gbB = cx.sb(st, [128, 512], F32, name="gbB")
        load_bcast_row(cx, 'sp', gwB, W['rwkv_gn_w'][l], 512)
        load_bcast_row(cx, 'sp', gbB, W['rwkv_gn_b'][l], 512)
        IN = [cx.sb(st, [128, 6, 512], F32, name="IN") for _ in range(2)]
        ELs = [cx.sb(st, [128, 3, 512], F32, name="EL") for _ in range(2)]
        TMs = [cx.sb(st, [128, 4, 512], F32, name="TM") for _ in range(2)]
        XTs = [cx.sb(st, [64, 8, 4, 128], F32, name="XT") for _ in range(2)]
        MMs = [cx.sb(st, [128, 8, 512], F32, name="MM") for _ in range(2)]
        XXs = [[cx.sb(st, [128, 8, 2, 128], F32, name="XX") for _ in range(2)] for _ in range(2)]
        NTs = [cx.sb(st, [128, 8, 128], F32, name="NT") for _ in range(2)]
        pcs = [cx.sb(st, [64, 8], F32, name="pc") for _ in range(2)]
        ST = cx.sb(st, [64, 8, 64], F32, name="ST")
        STs = cx.sb(st, [64, 8, 64], F32, name="STs")
        Yb = cx.sb(st, [128, 8, 64], F32, name="Yb")
        Ub = cx.sb(st, [128, 8, 64], F32, name="Ub")
        Ob = cx.sb(st, [128, 512], F32, name="Ob")
        G3 = [cx.sb(st, [128, 512], F32, name="G3") for _ in range(2)]
        b8 = [cx.sb(st, [128, 8], F32, name="b8") for _ in range(2)]
        m8 = cx.sb(st, [128, 8], F32, name="m8")
        r8 = cx.sb(st, [128, 8], F32, name="r8")
        t512 = cx.sb(st, [128, 512], F32, name="t512")
        yo = [cx.sb(st, [128, 512], F32, name="yo") for _ in range(2)]
        PTp = [cx.ps(st, [128, 512], F32, name="ptr") for _ in range(2)]
        PDp = [cx.ps(st, [128, 4, 128], F32, name="pD") for _ in range(3)]
        pY = cx.ps(st, [128, 512], F32, name="pY")
        pU = cx.ps(st, [128, 512], F32, name="pU")
        pO = cx.ps(st, [128, 512], F32, name="pO")
        cnt = {'pt': 0, 'pd': 0}

        def get_pt():
            cnt['pt'] += 1
            return PTp[cnt['pt'] % 2]

        def get_pd():
            cnt['pd'] += 1
            return PDp[cnt['pd'] % 3]

        cx.op('pool', [], [ST], lambda e: e.memset(ST[:, :, :], 0.0))
        MUs, MUi, MLs = rwm[:, 0:128], rwm[:, 128:256], rwm[:, 256:384]

        def pre(c):
            rows = slice(c * 128, (c + 1) * 128)
            I6, EL, TM, XT, MM_, XX, NTb, pc = IN[c % 2], ELs[c % 2], TMs[c % 2], XTs[c % 2], MMs[c % 2], XXs[c % 2], NTs[c % 2], pcs[c % 2]
            for i, nm in enumerate(names6):
                cx.dma('sp' if i % 2 == 0 else 'act', I6[:, i, :], RW[nm][rows, :], writes=[I6])
            rr, lw, k2, vv, kn, aa = [I6[:, i, :] for i in range(6)]

            def u_cumsum():
                pL = get_pt()
                mm(cx, pL, pL[:, :], MUi, lw, [rwm, I6], True, True)
                cx.op('act', [pL], [EL], lambda e: e.activation(out=EL[:, 0, :], in_=pL[:, :], func=AF.Exp))
                cx.op('act', [pL], [EL], lambda e: e.activation(out=EL[:, 1, :], in_=pL[:, :], func=AF.Exp, scale=-1.0))
                cx.op('dve', [pL, I6], [EL], lambda e: e.tensor_tensor(EL[:, 2, :], pL[:, :], lw, ALU.subtract))
            atomic(cx, u_cumsum)
            cx.op('act', [EL], [EL], lambda e: e.activation(out=EL[:, 2, :], in_=EL[:, 2, :], func=AF.Exp))
            cx.op('dve', [I6, EL], [TM],
                  lambda e: e.scalar_tensor_tensor(out=TM[:, 0, :], in0=kn, scalar=-1.0, in1=EL[:, 2, :], op0=ALU.mult, op1=ALU.mult))
            cx.op('pool', [I6, EL], [TM], lambda e: e.tensor_tensor(TM[:, 1, :], rr, EL[:, 0, :], ALU.mult))
            cx.op('dve', [I6], [TM], lambda e: e.tensor_tensor(TM[:, 2, :], kn, aa, ALU.mult))
            cx.op('dve', [TM, EL], [TM], lambda e: e.tensor_tensor(TM[:, 2, :], TM[:, 2, :], EL[:, 1, :], ALU.mult))
            cx.op('pool', [I6, EL], [TM], lambda e: e.tensor_tensor(TM[:, 3, :], k2, EL[:, 1, :], ALU.mult))

            def u_pc():
                ppc = get_pt()
                for h in range(8):
                    mm(cx, ppc, ppc[:64, h:h + 1], lw[:, h * 64:(h + 1) * 64], cx.ones_f[:, 0:1], [I6, cx.ones_f], True, True)
                cx.op('act', [ppc], [pc], lambda e: e.activation(out=pc[:, :], in_=ppc[:64, 0:8], func=AF.Exp))
            atomic(cx, u_pc)

            def u_tr(q, hh, k):
                pt = get_pt()
                for j in range(4):
                    h = hh * 4 + j
                    transp(cx, pt, pt[:64, j * 128:(j + 1) * 128], TM[:, q, h * 64:(h + 1) * 64], cx.identf[:, :], [TM, cx.identf])
                evac(cx, 'act' if k % 2 else 'dve', pt, pt[:64, :].rearrange("p (j t) -> p j t", j=4), XT,
                     fr(XT[:, hh * 4:(hh + 1) * 4, q, :]))
            k = 0
            for q in range(4):
                for hh in range(2):
                    k += 1
                    atomic(cx, lambda q=q, hh=hh, k=k: u_tr(q, hh, k))

            def u_setup(h):
                pA = get_pt()
                pd = get_pd()
                ar = XT[:, h, 0:2, :].rearrange("p q t -> p (q t)")
                mm(cx, pA, pA[:, 0:256], fr(XT[:, h, 2, :]), fr(ar), [XT], True, True)
                mm(cx, pA, pA[:, 256:512], fr(XT[:, h, 3, :]), fr(ar), [XT], True, True)
                mm(cx, pd, pd[:, 0, :], fr(XT[:, h, 0, :]), fr(XT[:, h, 2, :]), [XT], True, True)
                cx.op('dve', [pA, mask4], [MM_], lambda e: e.tensor_tensor(MM_[:, h, :], pA[:, :], mask4[:, :], ALU.mult))
                cx.op('dve', [pd, rwm], [XX[0]], lambda e: e.tensor_tensor(fr(XX[0][:, h, 1, :]), pd[:, 0, :], MLs, ALU.mult))
                cx.op('pool', [MM_], [XX[0]], lambda e: e.tensor_copy(fr(XX[0][:, h, 0, :]), MM_[:, h, 0:128]))
                cx.op('pool', [MM_, cx.identf], [NTb], lambda e: e.tensor_tensor(fr(NTb[:, h, :]), MM_[:, h, 0:128], cx.identf[:, :], ALU.add))
            for h in range(8):
                atomic(cx, lambda h=h: u_setup(h))

            def u_sq(lev, p):
                cur, nxt = XX[(lev - 1) % 2], XX[lev % 2]
                pd = get_pd()
                for j in range(2):
                    h = 2 * p + j
                    if lev < 6:
                        mm(cx, pd, pd[:, 2 * j, :], fr(cur[:, h, 1, :]), fr(cur[:, h, 0, :]), [cur], True, True)
                    mm(cx, pd, pd[:, 2 * j + 1, :], fr(cur[:, h, 0, :]), fr(cur[:, h, 1, :]), [cur], True, True)
                if lev < 6:
                    evac(cx, 'act' if p % 2 else 'dve', pd, pd[:, :, :], nxt,
                         fr(nxt[:, 2 * p:2 * p + 2, :, :].rearrange("p h q t -> p (h q) t")))
                else:
                    for j in range(2):
                        evac(cx, 'act' if j else 'dve', pd, pd[:, 2 * j + 1, :], nxt, fr(nxt[:, 2 * p + j, 1, :]))

            def u_n(lev, p):
                nxt = XX[lev % 2]
                pd = get_pd()
                for j in range(2):
                    h = 2 * p + j
                    mm(cx, pd, pd[:, j, :], fr(nxt[:, h, 1, :]), fr(NTb[:, h, :]), [nxt, NTb], True, True)
                cx.op('dve', [pd, NTb], [NTb],
                      lambda e: e.tensor_tensor(fr(NTb[:, 2 * p:2 * p + 2, :]), NTb[:, 2 * p:2 * p + 2, :], pd[:, 0:2, :], ALU.add))
            for lev in range(1, 7):
                for p in range(4):
                    atomic(cx, lambda lev=lev, p=p: u_sq(lev, p))
                for p in range(4):
                    atomic(cx, lambda lev=lev, p=p: u_n(lev, p))

        def seq(c):
            rows = slice(c * 128, (c + 1) * 128)
            I6, TM, XT, MM_, NTb, pc = IN[c % 2], TMs[c % 2], XTs[c % 2], MMs[c % 2], NTs[c % 2], pcs[c % 2]
            vv = I6[:, 3, :]
            cx.op('pool', [ST, pc], [STs],
                  lambda e: e.tensor_tensor(STs[:, :, :], ST[:, :, :], pc[:, :].unsqueeze(2).broadcast_to([64, 8, 64]), ALU.mult))
            for h in range(8):
                hs = slice(h * 64, (h + 1) * 64)
                mm(cx, pY, pY[:, hs], XT[:, h, 0, :], ST[:, h, :], [XT, ST], True, False)
                mm(cx, pY, pY[:, hs], MM_[:, h, 256:384], vv[:, hs], [MM_, I6], False, True)
            evac(cx, 'dve', pY, pY[:, 0:256], Yb, Yb[:, 0:4, :].rearrange("p h c -> p (h c)"))
            evac(cx, 'act', pY, pY[:, 256:512], Yb, Yb[:, 4:8, :].rearrange("p h c -> p (h c)"))
            for h in range(8):
                hs = slice(h * 64, (h + 1) * 64)
                mm(cx, pU, pU[:, hs], NTb[:, h, :], Yb[:, h, :], [NTb, Yb], True, True)
            evac(cx, 'dve', pU, pU[:, 0:256], Ub, Ub[:, 0:4, :].rearrange("p h c -> p (h c)"))
            evac(cx, 'act', pU, pU[:, 256:512], Ub, Ub[:, 4:8, :].rearrange("p h c -> p (h c)"))
            for h in range(8):
                hs = slice(h * 64, (h + 1) * 64)
                mm(cx, pY, pY[:64, hs], TM[:, 2, hs], Ub[:, h, :], [TM, Ub], True, False)
                mm(cx, pY, pY[:64, hs], TM[:, 3, hs], vv[:, hs], [TM, I6], False, True)
            for h in range(8):
                hs = slice(h * 64, (h + 1) * 64)
                mm(cx, pO, pO[:, hs], XT[:, h, 1, :], ST[:, h, :], [XT, ST], True, False)
                mm(cx, pO, pO[:, hs], MM_[:, h, 128:256], Ub[:, h, :], [MM_, Ub], False, False)
                mm(cx, pO, pO[:, hs], MM_[:, h, 384:512], vv[:, hs], [MM_, I6], False, True)
            cx.op('dve', [pY, pc], [ST],
                  lambda e: e.tensor_tensor(ST[:, :, :], pY[:64, :].rearrange("p (h c) -> p h c", h=8),
                                            pc[:, :].unsqueeze(2).broadcast_to([64, 8, 64]), ALU.mult))
            cx.op('dve', [ST, STs], [ST], lambda e: e.tensor_tensor(ST[:, :, :], ST[:, :, :], STs[:, :, :], ALU.add))
            evac(cx, 'act', pO, pO[:, :], Ob, Ob[:, :])
            g3, b8b, y = G3[c % 2], b8[c % 2], yo[c % 2]
            cx.dma('sp', g3[:, :], RW['gg'][rows, :], writes=[g3])
            cx.dma('act', b8b[:, :], RW['bc'][rows, :], writes=[b8b])
            cx.op('dve', [Ob], [m8], lambda e: e.tensor_reduce(out=m8[:, :], in_=v3(Ob[:, :]), axis=AX.X, op=ALU.add))
            cx.op('dve', [m8], [m8], lambda e: e.tensor_scalar(m8[:, :], m8[:, :], 1.0 / 64, None, ALU.mult))
            cx.op('dve', [Ob, m8], [Ob], lambda e: e.tensor_tensor(v3(Ob[:, :]), v3(Ob[:, :]), bc8(m8[:, :]), ALU.subtract))
            cx.op('pool', [Ob], [t512], lambda e: e.tensor_tensor(t512[:, :], Ob[:, :], Ob[:, :], ALU.mult))
            cx.op('dve', [t512], [r8], lambda e: e.tensor_reduce(out=r8[:, :], in_=v3(t512[:, :]), axis=AX.X, op=ALU.add))
            cx.op('dve', [r8], [r8], lambda e: e.tensor_scalar(r8[:, :], r8[:, :], 1.0 / 64, 64e-5, ALU.mult, ALU.add))
            cx.op('pool', [r8, cx.neghalf], [r8],
                  lambda e: e.tensor_tensor(r8[:, :], r8[:, :], cx.neghalf[:, 0:1].to_broadcast([128, 8]), ALU.pow))
            cx.op('dve', [Ob, r8], [y], lambda e: e.tensor_tensor(v3(y[:, :]), v3(Ob[:, :]), bc8(r8[:, :]), ALU.mult))
            cx.op('pool', [y, gwB], [y], lambda e: e.tensor_tensor(y[:, :], y[:, :], gwB[:, :], ALU.mult))
            cx.op('pool', [y, gbB], [y], lambda e: e.tensor_tensor(y[:, :], y[:, :], gbB[:, :], ALU.add))
            cx.op('dve', [I6, b8b], [t512], lambda e: e.tensor_tensor(v3(t512[:, :]), v3(vv), bc8(b8b[:, :]), ALU.mult))
            cx.op('pool', [y, t512], [y], lambda e: e.tensor_tensor(y[:, :], y[:, :], t512[:, :], ALU.add))
            cx.op('dve', [y, g3], [y], lambda e: e.tensor_tensor(y[:, :], y[:, :], g3[:, :], ALU.mult))
            cx.dma('pool', Y_d[rows, :], y[:, :], reads=[y])

        for c0 in range(0, NT, 2):
            cs = list(range(c0, min(NT, c0 + 2)))
            lists = []
            for c in cs:
                cx._rec = []
                pre(c)
                lists.append(cx._rec)
                cx._rec = None
            for i in range(max(len(l_) for l_ in lists)):
                for l_ in lists:
                    if i < len(l_):
                        l_[i]()
            for c in cs:
                seq(c)
    cx.barrier()


NSA_SCALE = 128 ** -0.5
NSA_BIG = 30000.0
NSA_STOP = 0


def phase_nsa(cx, S, l, P_d, W, Wb, C, Youts, scr):
    NT = S // 128
    G = min(512, S)
    NG = G // 128
    QW = min(512, S)
    NJ = QW // 128
    Nc = (S - 32) // 16 + 1
    NKB = (Nc + 127) // 128
    N = scr['nsa']
    with contextlib.ExitStack() as st:
        pn = [cx.sb(st, [128, 1292], F32, name="pn") for _ in range(2)]
        cs_t = [cx.sb(st, [128, 32], F32, name="cs") for _ in range(2)]
        ro = [cx.sb(st, [128, 10, 32], F32, name="ro") for _ in range(2)]
        t1s = [cx.sb(st, [128, 10, 16], F32, name="t1") for _ in range(2)]
        t2s = [cx.sb(st, [128, 10, 16], F32, name="t2") for _ in range(2)]
        fb = [cx.sb(st, [128, 8, 128], BF16, name="fb") for _ in range(2)]
        va = [cx.sb(st, [128, 2, 129], BF16, name="va") for _ in range(2)]
        sqs = [cx.sb(st, [128, 6, 128], F32, name="sq") for _ in range(2)]
        n6 = [cx.sb(st, [128, 6], F32, name="n6") for _ in range(2)]
        nqb = [cx.sb(st, [128, 4], BF16, name="nqb") for _ in range(2)]
        gt = [cx.sb(st, [128, 12], F32, name="gt") for _ in range(2)]
        kmx = cx.sb(st, [128, 2], F32, name="kmx")
        gA = cx.sb(st, [128, 8, G], BF16, name="gA")
        gN = cx.sb(st, [1, 4, G], BF16, name="gN")
        trp = TrPool(cx, st)
        trf = TrPool(cx, st, n=1, dtype=F32)
        cx.op('pool', [], [kmx], lambda e: e.memset(kmx[:, :], 0.0))
        def body(tt):
            t1, t2, sq = t1s[tt % 2], t2s[tt % 2], sqs[tt % 2]
            tl = tt % NG
            rows = slice(tt * 128, (tt + 1) * 128)
            p, c_t, r, f, v, n6b, nq_, g_ = pn[tt % 2], cs_t[tt % 2], ro[tt % 2], fb[tt % 2], va[tt % 2], n6[tt % 2], nqb[tt % 2], gt[tt % 2]
            cx.dma('sp', p[:, :], P_d[rows, OFF_NSA:OFF_NSA + 1292], writes=[p])
            cx.dma('act', c_t[:, 0:16], C['nsa_cos'][rows, :], writes=[c_t])
            cx.dma('act', c_t[:, 16:32], C['nsa_sin'][rows, :], writes=[c_t])
            blk = p[:, 0:1280].rearrange("p (b c) -> p b c", b=10)
            cb = c_t[:, 0:16].unsqueeze(1).broadcast_to([128, 10, 16])
            sb_ = c_t[:, 16:32].unsqueeze(1).broadcast_to([128, 10, 16])
            rope_tm(cx, p, blk[:, :, 0:16], blk[:, :, 16:32], [c_t, cb], [c_t, sb_],
                    r, r[:, :, 0:16], r[:, :, 16:32], [(t1, t1[:, :, :]), (t2, t2[:, :, :])])
            cx.op('act', [p], [f], lambda e: e.activation(out=f[:, 0:4, 32:128], in_=blk[:, 0:4, 32:128], func=AF.Copy, scale=NSA_SCALE))
            cx.op('act', [r], [f], lambda e: e.activation(out=f[:, 0:4, 0:32], in_=r[:, 0:4, :], func=AF.Copy, scale=NSA_SCALE))
            for dst, src in ((4, 4), (6, 6), (7, 8)):
                cx.op('pool', [p], [f], lambda e, dst=dst, src=src: e.tensor_copy(f[:, dst, 32:128], blk[:, src, 32:128]))
                cx.op('pool', [r], [f], lambda e, dst=dst, src=src: e.tensor_copy(f[:, dst, 0:32], r[:, src, :]))
            cx.op('pool', [p], [f], lambda e: e.tensor_copy(f[:, 5, :], blk[:, 5, :]))
            cx.op('pool', [p], [v], lambda e: e.tensor_copy(v[:, 0, 0:128], blk[:, 7, :]))
            cx.op('pool', [p], [v], lambda e: e.tensor_copy(v[:, 1, 0:128], blk[:, 9, :]))
            cx.op('pool', [], [v], lambda e: e.memset(v[:, :, 128:129], 1.0))
            cx.op('dve', [p], [sq], lambda e: e.tensor_tensor(sq[:, 0:4, :], blk[:, 0:4, :], blk[:, 0:4, :], ALU.mult))
            cx.op('dve', [p], [sq], lambda e: e.tensor_tensor(sq[:, 4, :], blk[:, 6, :], blk[:, 6, :], ALU.mult))
            cx.op('dve', [p], [sq], lambda e: e.tensor_tensor(sq[:, 5, :], blk[:, 8, :], blk[:, 8, :], ALU.mult))
            cx.op('dve', [sq], [n6b], lambda e: e.tensor_reduce(out=n6b[:, :], in_=sq[:, :, :], axis=AX.X, op=ALU.add))
            cx.op('dve', [n6b, kmx], [kmx], lambda e: e.tensor_tensor(kmx[:, :], kmx[:, :], n6b[:, 4:6], ALU.max))
            cx.op('act', [n6b], [n6b], lambda e: e.activation(out=n6b[:, 0:4], in_=n6b[:, 0:4], func=AF.Sqrt))
            cx.op('dve', [n6b], [nq_], lambda e: e.tensor_scalar(nq_[:, :], n6b[:, 0:4], -NSA_SCALE, None, ALU.mult))
            cx.dma('sp', g_[:, :], P_d[rows, OFF_NSA + 1280:OFF_NSA + 1292], writes=[g_])
            cx.op('act', [g_], [g_], lambda e: e.activation(out=g_[:, :], in_=g_[:, :], func=AF.Sigmoid))
            cx.dma('pool', N['ng'][rows, :], g_[:, :], reads=[g_])
            trp.transpose_cols(f, lambda j: f[:, j, :], 8, gA, lambda j0, cnt: gA[:, j0:j0 + cnt, tl * 128:(tl + 1) * 128])
            trp.transpose_cols(nq_, lambda j: nq_[:, j:j + 1], 4, gN,
                               lambda j0, cnt: gN[0:1, j0:j0 + cnt, tl * 128:(tl + 1) * 128], blkw=1)
            cx.dma('pool', N['vsa'][rows, :], v[:, 0, :], reads=[v])
            cx.dma('pool', N['vwa'][rows, :], v[:, 1, :], reads=[v])
            if tl == NG - 1:
                def post():
                    g0 = (tt // NG) * G
                    cx.dma('pool', N['qT'][:, :, g0:g0 + G].rearrange("h d s -> d h s"), gA[:, 0:4, :], reads=[gA])
                    for j, nm in ((4, 'kcT'), (5, 'vcT'), (6, 'ksT'), (7, 'kwT')):
                        cx.dma('pool', N[nm][:, g0:g0 + G], gA[:, j, :], reads=[gA])
                    cx.dma('pool', N['nq'][:, g0:g0 + G].rearrange("(o h) s -> o h s", o=1), gN[0:1, :, :], reads=[gN])
                return post
        run_tiles(cx, body, NT)
        kms = cx.sb(st, [128, 1], F32, name="kms")
        kmw = cx.sb(st, [128, 1], F32, name="kmw")
        k1 = cx.sb(st, [128, 1], F32, name="k1")
        rowsb = cx.sb(st, [1, 2, 128], BF16, name="rowsb")
        for i, dstc in enumerate((kms, kmw)):
            cx.op('pool', [kmx], [k1], lambda e: e.tensor_copy(k1[:, :], kmx[:, i:i + 1]))
            bcast_scalar_max(cx, st, trf, k1, dstc)
            cx.op('act', [dstc], [dstc], lambda e: e.activation(out=dstc[:, :], in_=dstc[:, :], func=AF.Sqrt))
            cx.op('dve', [dstc], [rowsb], lambda e: e.tensor_copy(rowsb[0:1, i, :], dstc[0:1, 0:1].to_broadcast([1, 128])))
        cx.dma('pool', N['krow'].rearrange("a b -> (a b)").rearrange("(o n) -> o n", o=1), rowsb[0:1, :, :].rearrange("o a b -> o (a b)"), reads=[rowsb])
    cx.barrier()
    if NSA_STOP == 1:
        return
    with contextlib.ExitStack() as st:
        KC = cx.sb(st, [128, 256], BF16, name="KC")
        VCA = cx.sb(st, [128, 2, 193], BF16, name="VCA")
        krow = cx.sb(st, [128, 3, 128], BF16, name="krow")
        cx.op('pool', [], [krow], lambda e: e.memset(krow[:, :, :], 0.0))
        cx.dma('sp', krow[0:1, 0:2, :].rearrange("o a b -> o (a b)"), N['krow'].rearrange("a b -> (a b)").rearrange("(o n) -> o n", o=1), writes=[krow])
        cx.dma('sp', VCA[:, :, 129:193], C['cover'].rearrange("(kb p) j -> p kb j", p=128), writes=[VCA])
        cx.op('pool', [], [VCA], lambda e: e.memset(VCA[:, :, 0:129], 0.0))
        cx.op('pool', [], [VCA], lambda e: e.memset(VCA[:, :, 128:129], 1.0))
        cx.op('pool', [], [KC], lambda e: e.memset(KC[:, :], 0.0))
        with contextlib.ExitStack() as s2:
            pm = [cx.ps(s2, [128, 512], F32, name="pm") for _ in range(2)]
            xT = [cx.sb(s2, [128, S], BF16, name="xT") for _ in range(2)]
            cx.dma('sp', xT[0][:, :], N['kcT'][:, :], writes=[xT[0]])
            cx.dma('act', xT[1][:, :], N['vcT'][:, :], writes=[xT[1]])
            w1 = [cx.sb(s2, [128, 32, 128], BF16, name="w1") for _ in range(2)]
            w2 = [cx.sb(s2, [128, 128], BF16, name="w2") for _ in range(2)]
            posf = cx.sb(s2, [32, 2, 128], F32, name="posf")
            posb = cx.sb(s2, [32, 2, 128], BF16, name="posb")
            posT = cx.sb(s2, [128, 2, 32], BF16, name="posT")
            bias = cx.sb(s2, [128, 2], F32, name="bias")
            xs = cx.sb(s2, [128, 256], F32, name="xs")
            x2 = cx.sb(s2, [128, 256], F32, name="x2")
            hid = [cx.sb(s2, [128, 256], BF16, name="hid") for _ in range(2)]
            ksq = cx.sb(s2, [128, 256], BF16, name="ksq")
            one = cx.sb(s2, [1, 2], F32, name="one")
            trp = TrPool(cx, s2, n=1)
            for z in range(2):
                cx.dma('sp', w1[z][:, :, :], Wb['nsa_cmp_w1'][l][z].rearrange("(l d) e -> d l e", d=128), writes=[w1[z]])
                cx.dma('sp', w2[z][:, :], Wb['nsa_cmp_w2'][l][z], writes=[w2[z]])
            cx.dma('sp', posf[:, :, :], W['nsa_cmp_pos'][l].rearrange("z l d -> l z d"), writes=[posf])
            cx.op('dve', [posf], [posb], lambda e: e.tensor_copy(posb[:, :, :], posf[:, :, :]))
            trp.transpose_cols(posb, lambda j: posb[:, j, :], 2, posT, lambda j0, cnt: posT[:, j0:j0 + cnt, :], rows=32)
            for z in range(2):
                pb, ph = pm
                for ll in range(32):
                    mm(cx, pb, pb[:, z:z + 1], w1[z][:, ll, :], posT[:, z, ll:ll + 1], [w1[z], posT], ll == 0, ll == 31)
                evac(cx, 'dve', pb, pb[:, z:z + 1], bias, bias[:, z:z + 1])
                for ll in range(32):
                    mm(cx, ph, ph[:, :Nc], w1[z][:, ll, :], xT[z][:, ll:ll + 16 * (Nc - 1) + 1:16], [w1[z], xT[z]], ll == 0, ll == 31)
                cx.op('act', [ph, bias], [xs], lambda e: e.activation(out=xs[:, :Nc], in_=ph[:, :Nc], func=AF.Identity, bias=bias[:, z:z + 1]))
                cx.op('dve', [xs], [x2], lambda e: e.tensor_tensor(x2[:, :Nc], xs[:, :Nc], xs[:, :Nc], ALU.mult))
                cx.op('dve', [x2], [x2], lambda e: e.tensor_scalar(x2[:, :Nc], x2[:, :Nc], 0.044715, 1.0, ALU.mult, ALU.add))
                cx.op('dve', [x2, xs], [x2], lambda e: e.tensor_tensor(x2[:, :Nc], x2[:, :Nc], xs[:, :Nc], ALU.mult))
                cx.op('act', [x2], [x2], lambda e: e.activation(out=x2[:, :Nc], in_=x2[:, :Nc], func=AF.Tanh, scale=0.7978845608028654))
                cx.op('dve', [x2], [x2], lambda e: e.tensor_scalar(x2[:, :Nc], x2[:, :Nc], 1.0, 0.5, ALU.add, ALU.mult))
                cx.op('dve', [x2, xs], [hid[z]], lambda e: e.tensor_tensor(hid[z][:, :Nc], x2[:, :Nc], xs[:, :Nc], ALU.mult))
            pk = pm[0]
            mm(cx, pk, pk[:, :Nc], w2[0][:, :], hid[0][:, :Nc], [w2[0], hid[0]], True, True)
            evac(cx, 'act', pk, pk[:, :Nc], KC, KC[:, :Nc])
            cx.op('act', [pk], [ksq], lambda e: e.activation(out=ksq[:, :Nc], in_=pk[:, :Nc], func=AF.Square))
            pr = pm[1]
            mm(cx, pr, pr[0:1, :Nc], cx.ones_bf[:, 0:1], ksq[:, :Nc], [cx.ones_bf, ksq], True, True)
            cx.op('dve', [pr], [one], lambda e: e.tensor_reduce(out=one[0:1, 0:1], in_=pr[0:1, :Nc], axis=AX.X, op=ALU.max))
            cx.op('act', [one], [one], lambda e: e.activation(out=one[0:1, 0:1], in_=one[0:1, 0:1], func=AF.Sqrt))
            cx.op('dve', [one], [krow], lambda e: e.tensor_scalar(krow[0:1, 2, :], one[0:1, 0:1].to_broadcast([1, 128]), 1.02, None, ALU.mult))
            for kb in range(NKB):
                nk = min(128, Nc - kb * 128)
                pv = pm[kb % 2]
                mm(cx, pv, pv[:nk, 0:128], hid[1][:, kb * 128:kb * 128 + nk], w2[1][:, :], [hid[1], w2[1]], True, True)
                evac(cx, 'dve', pv, pv[:nk, 0:128], VCA, VCA[:nk, kb, 0:128])
        cx.barrier()
        if NSA_STOP == 2:
            return
        res = AttnRes(cx, st, 193)
        qT = [cx.sb(st, [128, S], BF16, name="qT") for _ in range(4)]
        nq = [cx.sb(st, [128, S], BF16, name="nq") for _ in range(4)]
        for h in range(4):
            cx.op('pool', [], [nq[h]], lambda e: e.memset(nq[h][:, :], 0.0))
            cx.dma('sp', qT[h][:, :], N['qT'][h], writes=[qT[h]])
            cx.dma('act', nq[h][0:1, :], N['nq'][h:h + 1, :], writes=[nq[h]])
        ksT = cx.sb(st, [128, S], BF16, name="ksT")
        kwT = cx.sb(st, [128, S], BF16, name="kwT")
        cx.dma('sp', ksT[:, :], N['ksT'][:, :], writes=[ksT])
        cx.dma('act', kwT[:, :], N['kwT'][:, :], writes=[kwT])
        vsa = cx.sb(st, [128, NT, 129], BF16, name="vsa")
        vwa = cx.sb(st, [128, NT, 129], BF16, name="vwa")
        cx.dma('sp', vsa[:, :, :], N['vsa'].rearrange("(t p) c -> p t c", p=128), writes=[vsa])
        cx.dma('act', vwa[:, :, :], N['vwa'].rearrange("(t p) c -> p t c", p=128), writes=[vwa])
        cmask = cx.sb(st, [128, 4, 512], BF16, name="cmask")
        wmask = cx.sb(st, [128, 4, 512], BF16, name="wmask")
        cx.dma('sp', cmask[:, :, :], C['cmask'].rearrange("i k q -> k i q"), writes=[cmask])
        cx.dma('sp', wmask[:, :, :], C['wmask'].rearrange("i k q -> k i q"), writes=[wmask])
        cmpm = cx.sb(st, [128, 2, S], BF16, name="cmpm")
        cx.dma('sp', cmpm[:, :, :], C['cmpmask'].rearrange("kb p q -> p kb q"), writes=[cmpm])
        Em = cx.sb(st, [64, S], BF16, name="Em")
        cx.dma('sp', Em[:, :], C['Emat'][:, :], writes=[Em])
        gts = cx.sb(st, [128, NT, 12], F32, name="gts")
        cx.dma('sp', gts[:, :, :], N['ng'].rearrange("(t p) c -> p t c", p=128), writes=[gts])
        imp = cx.sb(st, [128, 4, 64], F32, name="imp")
        fbt = [cx.sb(st, [128, 64], F32, name="fbt") for _ in range(2)]
        m8 = cx.sb(st, [128, 16], F32, name="m8")
        val2 = cx.sb(st, [128, 64], F32, name="val2")
        selb = cx.sb(st, [128, 64], BF16, name="selb")
        selT = cx.sb(st, [64, 512], BF16, name="selT")
        rc = [cx.sb(st, [128, 1], F32, name="rc") for _ in range(2)]
        rg = [cx.sb(st, [128, 1], F32, name="rg") for _ in range(2)]
        ot = [cx.sb(st, [128, 128], F32, name="ot") for _ in range(3)]
        it = [cx.sb(st, [128, 64], F32, name="it") for _ in range(2)]
        trp2 = TrPool(cx, st, n=1)
        cnt = [0]

        def make_epi(branch, h):
            def epi(QB, j, accb, acc_ap):
                k = cnt[0]
                cnt[0] += 1
                tt = QB * NJ + j
                r, rgb, o = rc[k % 2], rg[k % 2], ot[k % 3]
                cx.op('dve', [accb], [r], lambda e: e.tensor_scalar(r[:, :], acc_ap[:, 128:129], 1e-30, None, ALU.add))
                cx.op('dve', [r], [r], lambda e: e.reciprocal(r[:, :], r[:, :]))
                cx.op('dve', [r, gts], [rgb], lambda e: e.tensor_tensor(rgb[:, :], r[:, :], gts[:, tt, h * 3 + branch:h * 3 + branch + 1], ALU.mult))
                cx.op('act', [accb, rgb], [o], lambda e: e.activation(out=o[:, :], in_=acc_ap[:, 0:128], func=AF.Copy, scale=rgb[:, 0:1]))
                cx.dma('pool', Youts[branch][tt * 128:(tt + 1) * 128, h * 128:(h + 1) * 128], o[:, :], reads=[o])
                if branch == 0:
                    if h == 0:
                        cx.op('dve', [accb, r], [imp], lambda e: e.tensor_scalar(imp[:, j, :], acc_ap[:, 129:193], r[:, 0:1], None, ALU.mult))
                    else:
                        i_ = it[k % 2]
                        cx.op('dve', [accb, r], [i_], lambda e: e.tensor_scalar(i_[:, :], acc_ap[:, 129:193], r[:, 0:1], None, ALU.mult))
                        cx.op('pool', [i_, imp], [imp], lambda e: e.tensor_tensor(imp[:, j, :], imp[:, j, :], i_[:, :], ALU.add))
            return epi

        def cmp_blocks(QB):
            q0 = QB * QW
            return [dict(kb=kb, k0=kb * 128, nk=min(128, Nc - kb * 128),
                         mask=(cmpm, cmpm[:min(128, Nc - kb * 128), kb, q0:q0 + QW])) for kb in range(NKB)]

        def slc_blocks(QB):
            bl = causal_blocks(QB, QW, cmask)
            for b in bl:
                b['extra'] = ([Em], Em[:, b['k0']:b['k0'] + 128], [selT], selT[:, :QW])
            return bl

        def win_blocks(QB):
            out = []
            nd = QW // 128
            for kb in range(max(0, nd * QB - 4), nd * (QB + 1)):
                i = kb - nd * QB
                m = (cmask, cmask[:, i, :QW]) if i >= 0 else (wmask, wmask[:, i + 4, :QW])
                out.append(dict(kb=kb, k0=kb * 128, nk=128, mask=m))
            return out

        for QB in range(S // QW):
            for h in range(4):
                attn_core(cx, res, S, [(qT[h], qT[h][:, :]), (nq[h], nq[h][:, :])],
                          [(kwT, kwT[:, :]), (krow, lambda k0, nk: krow[:, 1, :nk])],
                          lambda kb, nk: (vwa, vwa[:nk, kb, :]), 129, win_blocks, make_epi(2, h), qbs=[QB])
            if NSA_STOP == 3:
                break
            for h in range(4 if NSA_STOP != 8 else 0):
                attn_core(cx, res, S, [(qT[h], qT[h][:, :]), (nq[h], nq[h][:, :])],
                          [(KC, KC[:, :]), (krow, lambda k0, nk: krow[:, 2, :nk])],
                          lambda kb, nk: (VCA, VCA[:nk, kb, :]), 193, cmp_blocks, make_epi(0, h), qbs=[QB])
            if NSA_STOP == 4:
                break
            for j in range(NJ if NSA_STOP != 8 else 0):
                tt = QB * NJ + j
                f_ = fbt[j % 2]
                cx.dma('sp', f_[:, :], C['fbias'][tt * 128:(tt + 1) * 128, :], writes=[f_])
                cx.op('dve', [imp, f_], [f_], lambda e: e.tensor_tensor(f_[:, :], f_[:, :], imp[:, j, :], ALU.add))
                cx.op('dve', [f_], [m8], lambda e: e.max(out=m8[:, 0:8], in_=f_[:, :]))
                cx.op('dve', [f_, m8], [val2], lambda e: e.match_replace(out=val2[:, :], in_to_replace=m8[:, 0:8], in_values=f_[:, :], imm_value=-3.0e38))
                cx.op('dve', [val2], [m8], lambda e: e.max(out=m8[:, 8:16], in_=val2[:, :]))
                cx.op('dve', [f_, m8], [val2], lambda e: e.tensor_scalar(val2[:, :], f_[:, :], m8[:, 15:16], None, ALU.is_ge))
                cx.op('dve', [val2], [selb], lambda e: e.tensor_scalar(selb[:, :], val2[:, :], -1.0, NSA_BIG, ALU.add, ALU.mult))
                trp2.transpose_cols(selb, lambda jj: selb[:, :], 1, selT, lambda j0, c_, j=j: selT[:64, j * 128:(j + 1) * 128].unsqueeze(1), blkw=64)
            if NSA_STOP == 5:
                break
            for h in range(4 if NSA_STOP not in (8, 9) else 0):
                attn_core(cx, res, S, [(qT[h], qT[h][:, :]), (nq[h], nq[h][:, :])],
                          [(ksT, ksT[:, :]), (krow, lambda k0, nk: krow[:, 0, :nk])],
                          lambda kb, nk: (vsa, vsa[:nk, kb, :]), 129, slc_blocks, make_epi(1, h), qbs=[QB])
    cx.barrier()


S_FULL = 4096
ENABLE = {'mla': True, 'nsa': True, 'rwkv': True, 'ret': True}


def phase_zero(cx, S, Y_d):
    with contextlib.ExitStack() as st:
        z = cx.sb(st, [128, 512], F32, name="z")
        cx.op('pool', [], [z], lambda e: e.memset(z[:, :], 0.0))
        for tt in range(S // 128):
            cx.dma('sp', Y_d[tt * 128:(tt + 1) * 128, :], z[:, :], reads=[z])
    cx.barrier()


def host_consts(S):
    import ml_dtypes
    bf = ml_dtypes.bfloat16
    c = {}
    c['ident'] = np.eye(128, dtype=np.float32).astype(bf)
    kk = np.arange(128)[:, None]
    qq = np.arange(512)[None, :]
    c['cmask'] = np.stack([(128 * i + kk <= qq) for i in range(4)]).astype(np.float32).astype(bf)
    t = np.arange(S, dtype=np.float32)[:, None]

    def tables(inv):
        ang = (t * inv[None, :].astype(np.float32)).astype(np.float32)
        return np.cos(ang).astype(np.float32), np.sin(ang).astype(np.float32)

    inv_mla = (np.float32(500000.0) ** (-np.arange(0, 64, 2, dtype=np.float32) / np.float32(64))).astype(np.float32)
    inv_nsa = (np.float32(500000.0) ** (-np.arange(0, 32, 2, dtype=np.float32) / np.float32(32))).astype(np.float32)
    inv_ret = (np.float32(10000.0) ** (-np.linspace(0.0, 1.0, 32, dtype=np.float32))).astype(np.float32)
    c['mla_cos'], c['mla_sin'] = tables(inv_mla)
    c['nsa_cos'], c['nsa_sin'] = tables(inv_nsa)
    c['ret_cos'], c['ret_sin'] = tables(inv_ret)
    kf = kk.astype(np.float64)
    qf = qq.astype(np.float64)
    gdec = np.zeros((4, 5, 128, 512), np.float32)
    for h in range(4):
        lg = np.log1p(-2.0 ** (-5 - h))
        gdec[h, 0] = np.exp((qf - kf) * lg)
        for i in range(4):
            d = qf - kf - 128 * i
            gdec[h, 1 + i] = np.where(d >= 0, np.exp(np.maximum(d, 0) * lg), 0.0)
    c['gdec'] = gdec
    si = np.arange(128)[:, None]
    ti = np.arange(128)[None, :]
    c['wmask'] = (1.0 - c['cmask'].astype(np.float32)).astype(bf)
    Nc = (S - 32) // 16 + 1
    n = np.arange(256)
    q = np.arange(S)
    cm = ((16 * n[:, None] + 31 <= q[None, :]) & (n[:, None] < Nc)).astype(np.float32)
    c['cmpmask'] = cm.reshape(2, 128, S).astype(bf)
    nblk = S // 64
    jb = np.arange(64)
    cstart = 16 * n
    cend = cstart + 31
    cover = ((cstart[:, None] <= jb[None, :] * 64 + 63) & (cend[:, None] >= jb[None, :] * 64) & (n[:, None] < Nc)
             & (jb[None, :] < nblk)).astype(np.float32)
    c['cover'] = cover.astype(bf)
    c['Emat'] = (q[None, :] // 64 == jb[:, None]).astype(np.float32).astype(bf)
    cur = q // 64
    forced = (jb[None, :] == 0) | (jb[None, :] == cur[:, None]) | (jb[None, :] == cur[:, None] - 1)
    visible = (jb[None, :] <= cur[:, None]) & (jb[None, :] < nblk)
    c['fbias'] = np.where(visible, 1000.0 * forced, -1.0e30).astype(np.float32)
    c['rwm'] = np.concatenate([(si < ti), (si <= ti), (si > ti)], axis=1).astype(np.float32)
    return c


CONST_SPECS = {'ident': ([128, 128], BF16), 'cmask': ([4, 128, 512], BF16),
               'mla_cos': (None, F32), 'mla_sin': (None, F32), 'nsa_cos': (None, F32), 'nsa_sin': (None, F32),
               'ret_cos': (None, F32), 'ret_sin': (None, F32), 'gdec': ([4, 5, 128, 512], F32), 'rwm': ([128, 384], F32), 'wmask': ([4, 128, 512], BF16), 'cmpmask': ('cmp', BF16),
               'cover': ([256, 64], BF16), 'Emat': ('E', BF16), 'fbias': ('fb', F32)}

WEIGHT_SHAPES = {
    'w_in': [DEPTH, D_MODEL, IN_WIDTH], 'w_branch': [DEPTH, 4, BW, D_MODEL], 'w_out': [DEPTH, D_MODEL, D_MODEL],
    'w_up': [DEPTH, D_MODEL, D_FF], 'w_down': [DEPTH, D_FF, D_MODEL], 'norm_gains': [DEPTH, 4, D_MODEL],
    'mla_g_q': [DEPTH, 384], 'mla_g_kv': [DEPTH, 128], 'mla_w_uq': [DEPTH, 384, 768], 'mla_w_ukv': [DEPTH, 128, 1024],
    'nsa_cmp_pos': [DEPTH, 2, 32, 128], 'nsa_cmp_w1': [DEPTH, 2, 4096, 128], 'nsa_cmp_w2': [DEPTH, 2, 128, 128],
    'rwkv_mu': [DEPTH, 1984], 'rwkv_w0': [DEPTH, 512], 'rwkv_w2': [DEPTH, 96, 512], 'rwkv_a0': [DEPTH, 512],
    'rwkv_a2': [DEPTH, 96, 512], 'rwkv_g2': [DEPTH, 256, 512], 'rwkv_k_k': [DEPTH, 512], 'rwkv_k_a': [DEPTH, 512],
    'rwkv_r_k': [DEPTH, 8, 64], 'rwkv_gn_w': [DEPTH, 512], 'rwkv_gn_b': [DEPTH, 512],
}
CAST = ['w_in', 'w_branch', 'w_out', 'w_up', 'w_down', 'mla_w_uq', 'mla_w_ukv', 'nsa_cmp_w1', 'nsa_cmp_w2',
        'rwkv_w2', 'rwkv_a2', 'rwkv_g2']


def build_program(S, depth=DEPTH):
    cx = Ctx()
    x_d = cx.dram("x", [S, D_MODEL], F32, kind="ExternalInput")
    W = {k: cx.dram(k, shp, F32, kind="ExternalInput") for k, shp in WEIGHT_SHAPES.items()}
    C = {}
    for k, (shp, dt) in CONST_SPECS.items():
        if shp is None:
            shp = [S, 16 if k.startswith('nsa') else 32]
        elif shp == 'cmp':
            shp = [2, 128, S]
        elif shp == 'E':
            shp = [64, S]
        elif shp == 'fb':
            shp = [S, 64]
        C[k] = cx.dram("c_" + k, shp, dt, kind="ExternalInput")
    y_d = cx.dram("y", [S, D_MODEL], F32, kind="ExternalOutput")
    Wb = {k: cx.dram(k + "_bf", WEIGHT_SHAPES[k], BF16) for k in CAST}
    P_d = cx.dram("P", [S, IN_WIDTH], F32)
    Y = [cx.dram("Y%d" % m, [S, BW], F32) for m in range(4)]
    Yn = [cx.dram("Yn%d" % m, [S, BW], F32) for m in range(2)]
    M_d = cx.dram("M", [S, D_MODEL], F32)
    Z_d = cx.dram("Z", [S, D_MODEL], F32)
    xa = cx.dram("xa", [S, D_MODEL], F32)
    xb = cx.dram("xb", [S, D_MODEL], F32)
    scr = dict(qnT=cx.dram("qnT", [4, 128, S], BF16), qrT=cx.dram("qrT", [4, 65, S], BF16),
               knT=cx.dram("knT", [4, 128, S], BF16), krT=cx.dram("krT", [65, S], BF16),
               va=cx.dram("va", [S, 4, 129], BF16),
               rqT=cx.dram("rqT", [4, 64, S], BF16), rkT=cx.dram("rkT", [4, 64, S], BF16),
               rv=cx.dram("rv", [S, 4, 128], BF16))
    scr['rw'] = {nm: cx.dram("rw_" + nm, [S, 512], F32) for nm in ['rr', 'lw', 'k2', 'vv', 'kn', 'aa', 'gg']}
    scr['rw']['bc'] = cx.dram("rw_bc", [S, 8], F32)
    scr['nsa'] = dict(qT=cx.dram("n_qT", [4, 128, S], BF16), nq=cx.dram("n_nq", [4, S], BF16),
                      kcT=cx.dram("n_kcT", [128, S], BF16), vcT=cx.dram("n_vcT", [128, S], BF16),
                      ksT=cx.dram("n_ksT", [128, S], BF16), kwT=cx.dram("n_kwT", [128, S], BF16),
                      vsa=cx.dram("n_vsa", [S, 129], BF16), vwa=cx.dram("n_vwa", [S, 129], BF16),
                      ng=cx.dram("n_ng", [S, 12], F32), krow=cx.dram("n_krow", [2, 128], BF16))
    st = contextlib.ExitStack()
    cx._st = st
    setup_consts(cx, st, C['ident'])
    phase_cast(cx, [(W[k], Wb[k]) for k in CAST])
    xin = x_d
    for l in range(depth):
        g = W['norm_gains'][l]
        phase_in(cx, S, xin, g[0], Wb['w_in'][l], P_d)
        if ENABLE['mla']:
            phase_mla(cx, S, P_d, W['mla_g_q'][l], W['mla_g_kv'][l], Wb['mla_w_uq'][l], Wb['mla_w_ukv'][l],
                      C['mla_cos'], C['mla_sin'], C['cmask'], Y[0], scr)
        else:
            phase_zero(cx, S, Y[0])
        if ENABLE['nsa']:
            phase_nsa(cx, S, l, P_d, W, Wb, C, [Y[1], Yn[0], Yn[1]], scr)
            ysrc1 = [Y[1], Yn[0], Yn[1]]
        else:
            phase_zero(cx, S, Y[1])
            ysrc1 = [Y[1]]
        if ENABLE['rwkv']:
            phase_rwkv(cx, S, l, P_d, W, Wb, C, Y[2], scr)
        else:
            phase_zero(cx, S, Y[2])
        if ENABLE['ret']:
            phase_ret(cx, S, P_d, C['ret_cos'], C['ret_sin'], C['gdec'], Y[3], scr)
        else:
            phase_zero(cx, S, Y[3])
        phase_merge(cx, S, [[Y[0]], ysrc1, [Y[2]], [Y[3]]], Wb['w_branch'][l], P_d, M_d)
        phase_out(cx, S, M_d, Wb['w_out'][l], Z_d)
        phase_normres(cx, S, xin, Z_d, g[1], xa)
        phase_ffn(cx, S, xa, g[2], Wb['w_up'][l], Wb['w_down'][l], Z_d)
        xnext = y_d if l == depth - 1 else xb
        phase_normres(cx, S, xa, Z_d, g[3], xnext)
        xin = xnext
    cx.barrier()
    return cx


_CACHE = {}


def kernel(**inputs):
    x = np.ascontiguousarray(np.asarray(inputs['x'], dtype=np.float32))
    B, S, _ = x.shape
    if S not in _CACHE:
        _CACHE[S] = (build_program(S), host_consts(S))
    cx, consts = _CACHE[S]
    base = {k: np.ascontiguousarray(np.asarray(inputs[k], dtype=np.float32)) for k in WEIGHT_SHAPES}
    for k, v in consts.items():
        base["c_" + k] = np.ascontiguousarray(v)
    in_maps = []
    for b in range(B):
        m = dict(base)
        m['x'] = x[b]
        in_maps.append(m)
    res = run_bass_kernel_spmd(cx.nc, in_maps, core_ids=list(range(B)))
    return np.stack([np.asarray(r['y'], dtype=np.float32) for r in res.results], axis=0)
```

```python
import contextlib
import numpy as np
import concourse.bass as bass
import concourse.mybir as mybir
from concourse.bass_utils import run_bass_kernel_spmd

F32 = mybir.dt.float32
F32R = mybir.dt.float32r
RW_FAST = True


def fr(ap):
    return ap.bitcast(F32R) if RW_FAST else ap
BF16 = mybir.dt.bfloat16
AF = mybir.ActivationFunctionType
ALU = mybir.AluOpType
AX = mybir.AxisListType

D_MODEL = 2048
DEPTH = 2
BW = 512
D_FF = 8192
NORM_EPS = 1e-6
IN_WIDTH = 13580
OFF_MLA = 0
OFF_NSA = 576
OFF_RWKV = 1868
OFF_RET = 3852
OFF_GATE = 5388

SELF_SYNC = {'pe': False, 'act': False, 'dve': True, 'pool': True, 'sp': False}


class Reg:
    __slots__ = ('w', 'r')

    def __init__(self):
        self.w = None
        self.r = {}


class Buf:
    def __init__(self, t, nreg=1, excl=False):
        self.t = t
        self.regs = [Reg() for _ in range(nreg)]
        self.excl = excl

    @property
    def reg(self):
        return self.regs[0]

    def __getitem__(self, idx):
        return self.t[idx]


class Ctx:
    def __init__(self):
        self.nc = bass.Bass("TRN2", target_bir_lowering=False)
        nc = self.nc
        self.E = {'pe': nc.tensor, 'act': nc.scalar, 'dve': nc.vector, 'pool': nc.gpsimd, 'sp': nc.sync}
        self.sem = {e: nc.alloc_semaphore("s_" + e) for e in ['pe', 'act', 'dve', 'pool']}
        self.cnt = {e: 0 for e in self.sem}
        self.NDS = 48
        self.dsem = [nc.alloc_semaphore("d%d" % i) for i in range(self.NDS)]
        self.dcnt = [0] * self.NDS
        self.dpool = {'sp': list(range(0, 20)), 'act': list(range(20, 34)), 'pool': list(range(34, 48))}
        self.dnext = {'sp': 0, 'act': 0, 'pool': 0}
        self.known = {e: {} for e in self.E}
        self.ninst = 0
        self.uid = 0
        self._rec = None

    def name(self, p):
        self.uid += 1
        return "%s_%d" % (p, self.uid)

    def sb(self, stack, shape, dtype, nreg=1, name="sb"):
        t = stack.enter_context(self.nc.sbuf_tensor(self.name(name), list(shape), dtype))
        return Buf(t, nreg)

    def ps(self, stack, shape, dtype=F32, nreg=1, name="ps"):
        t = stack.enter_context(self.nc.psum_tensor(self.name(name), list(shape), dtype))
        return Buf(t, nreg, excl=True)

    def dram(self, name, shape, dtype, kind="Internal"):
        return self.nc.dram_tensor(name, list(shape), dtype, kind=kind).ap()

    def _wait(self, e, kind, val, force=False):
        if isinstance(kind, str):
            if kind == e and not SELF_SYNC[e] and not (force and e in self.sem):
                return
            sem = self.sem[kind]
            v = val
        else:
            idx = kind[1]
            sem = self.dsem[idx]
            v = val * 16
        k = self.known[e]
        if k.get(kind, 0) >= v:
            return
        self.E[e].wait_ge(sem, v)
        self.ninst += 1
        k[kind] = v

    def _deps(self, e, reads, writes, force=False):
        for r in reads:
            if r.w is not None:
                self._wait(e, r.w[0], r.w[1], force)
        for w in writes:
            if w.w is not None:
                self._wait(e, w.w[0], w.w[1], force)
            for kind, val in w.r.items():
                self._wait(e, kind, val, force)

    def _commit(self, tok, reads, writes):
        kind, val = tok
        for r in reads:
            if r.r.get(kind, 0) < val:
                r.r[kind] = val
        for w in writes:
            w.w = tok
            w.r = {}

    @staticmethod
    def _regs(lst):
        out = []
        for x in lst:
            if isinstance(x, Buf):
                out.extend(x.regs)
            elif isinstance(x, Reg):
                out.append(x)
            elif x is None:
                pass
            else:
                raise TypeError(type(x))
        return out

    def op(self, e, reads, writes, fn):
        if self._rec is not None:
            self._rec.append(lambda: self._op(e, reads, writes, fn))
            return None
        return self._op(e, reads, writes, fn)

    def _op(self, e, reads, writes, fn):
        writes = list(writes) + [x for x in reads if isinstance(x, Buf) and x.excl]
        reads = [x for x in reads if not (isinstance(x, Buf) and x.excl)]
        reads = self._regs(reads)
        writes = self._regs(writes)
        self._deps(e, reads, writes)
        inst = fn(self.E[e])
        self.cnt[e] += 1
        self.ninst += 1
        inst.then_inc(self.sem[e], 1)
        self._commit((e, self.cnt[e]), reads, writes)
        return inst

    def dma(self, q, out_ap, in_ap, reads=(), writes=(), **kw):
        if self._rec is not None:
            self._rec.append(lambda: self._dma(q, out_ap, in_ap, reads, writes, **kw))
            return
        self._dma(q, out_ap, in_ap, reads, writes, **kw)

    def _dma(self, q, out_ap, in_ap, reads=(), writes=(), **kw):
        reads = self._regs(reads)
        writes = self._regs(writes)
        self._deps(q, reads, writes, force=True)
        pool = self.dpool[q]
        idx = pool[self.dnext[q] % len(pool)]
        self.dnext[q] += 1
        if self.dcnt[idx] > 0:
            self._wait(q, ('d', idx), self.dcnt[idx])
        self.E[q].dma_start(out=out_ap, in_=in_ap, **kw).then_inc(self.dsem[idx], 16)
        self.dcnt[idx] += 1
        self.ninst += 1
        self._commit((('d', idx), self.dcnt[idx]), reads, writes)

    def barrier(self):
        for e in self.E:
            for o in self.sem:
                if o != e and self.cnt[o] > 0:
                    self._wait(e, o, self.cnt[o])
            for i in range(self.NDS):
                if self.dcnt[i] > 0:
                    self._wait(e, ('d', i), self.dcnt[i])
            if e in self.sem and self.cnt[e] > 0:
                k = self.known[e]
                if k.get(e, 0) < self.cnt[e]:
                    self.E[e].wait_ge(self.sem[e], self.cnt[e])
                    k[e] = self.cnt[e]


INTERLEAVE = 2


def run_tiles(cx, body, NT):
    for t0 in range(0, NT, INTERLEAVE):
        lists = []
        posts = []
        for t in range(t0, min(NT, t0 + INTERLEAVE)):
            cx._rec = []
            post = body(t)
            lists.append(cx._rec)
            cx._rec = None
            if post is not None:
                posts.append(post)
        for i in range(max(len(l) for l in lists)):
            for l in lists:
                if i < len(l):
                    l[i]()
        for p in posts:
            p()


def atomic(cx, fn):
    if cx._rec is None:
        return fn()
    rec = cx._rec

    def unit():
        saved = cx._rec
        cx._rec = None
        fn()
        cx._rec = saved
    rec.append(unit)


def mm(cx, out_buf, out_ap, lhsT_ap, rhs_ap, reads, start, stop, **kw):
    return cx.op('pe', reads, [out_buf],
                 lambda e: e.matmul(out_ap, lhsT_ap, rhs_ap, start=start, stop=stop, **kw))


def transp(cx, out_buf, out_ap, in_ap, ident_ap, reads):
    return cx.op('pe', reads, [out_buf], lambda e: e.transpose(out_ap, in_ap, ident_ap))


def phase_cast(cx, pairs):
    CH = 4096
    NB = 6
    with contextlib.ExitStack() as st:
        stg = [cx.sb(st, [128, CH], F32, name="cst") for _ in range(NB)]
        outb = [cx.sb(st, [128, CH], BF16, name="cob") for _ in range(NB)]
        engs = ['dve', 'pool', 'act']
        k = 0
        for src, dst in pairs:
            n = 1
            for s in src.shape:
                n *= s
            assert n % 128 == 0
            per = n // 128
            names = " ".join("a%d" % i for i in range(len(src.shape)))
            s2 = src.rearrange("%s -> (%s)" % (names, names)).rearrange("(p f) -> p f", p=128)
            d2 = dst.rearrange("%s -> (%s)" % (names, names)).rearrange("(p f) -> p f", p=128)
            for c0 in range(0, per, CH):
                c1 = min(per, c0 + CH)
                w = c1 - c0
                i = k % NB
                cx.dma('sp', stg[i][:, :w], s2[:, c0:c1], writes=[stg[i]])
                e = engs[k % 3]
                if e == 'act':
                    cx.op(e, [stg[i]], [outb[i]], lambda en: en.copy(outb[i][:, :w], stg[i][:, :w]))
                else:
                    cx.op(e, [stg[i]], [outb[i]], lambda en: en.tensor_copy(outb[i][:, :w], stg[i][:, :w]))
                cx.dma('act' if k % 2 else 'pool', d2[:, c0:c1], outb[i][:, :w], reads=[outb[i]])
                k += 1
    cx.barrier()


def load_bcast_row(cx, q, buf, row_ap, n):
    cx.dma(q, buf[:, :n], row_ap.partition_broadcast(128), writes=[buf])


def rms_rstd(cx, x_buf, x_ap, n, ss_buf, junk_buf, eps=NORM_EPS):
    cx.op('act', [x_buf], [junk_buf, ss_buf],
          lambda e: e.activation(out=junk_buf[:, :n], in_=x_ap, func=AF.Square, accum_out=ss_buf[:, 0:1]))
    cx.op('dve', [ss_buf], [ss_buf],
          lambda e: e.tensor_scalar(ss_buf[:, 0:1], ss_buf[:, 0:1], 1.0 / n, eps, ALU.mult, ALU.add))
    cx.op('pool', [ss_buf, cx.neghalf], [ss_buf],
          lambda e: e.tensor_tensor(ss_buf[:, 0:1], ss_buf[:, 0:1], cx.neghalf[:, 0:1], ALU.pow))


def setup_consts(cx, st, ident_d):
    cx.ident = cx.sb(st, [128, 128], BF16, name="ident")
    cx.dma('sp', cx.ident[:, :], ident_d[:, :], writes=[cx.ident])
    cx.identf = cx.sb(st, [128, 128], F32, name="identf")
    cx.op('dve', [cx.ident], [cx.identf], lambda e: e.tensor_copy(cx.identf[:, :], cx.ident[:, :]))
    cx.neghalf = cx.sb(st, [128, 1], F32, name="neghalf")
    cx.op('pool', [], [cx.neghalf], lambda e: e.memset(cx.neghalf[:, :], -0.5))
    cx.ones_bf = cx.sb(st, [128, 128], BF16, name="ones_bf")
    cx.op('pool', [], [cx.ones_bf], lambda e: e.memset(cx.ones_bf[:, :], 1.0))
    cx.ones_f = cx.sb(st, [128, 128], F32, name="ones_f")
    cx.op('pool', [], [cx.ones_f], lambda e: e.memset(cx.ones_f[:, :], 1.0))


def phase_in(cx, S, x_d, g_row, w_bf, P_d):
    G = 512
    KC = D_MODEL // 128
    chunks = []
    c = 0
    while c < OFF_GATE:
        chunks.append((c, min(c + 512, OFF_GATE), False))
        c += 512
    c = OFF_GATE
    while c < IN_WIDTH:
        chunks.append((c, c + 512, True))
        c += 512
    wv = w_bf.rearrange("(kc p) n -> p kc n", p=128)
    with contextlib.ExitStack() as st:
        gB = cx.sb(st, [128, D_MODEL], F32, name="gB")
        load_bcast_row(cx, 'sp', gB, g_row, D_MODEL)
        xt = [cx.sb(st, [128, D_MODEL], F32, name="xt") for _ in range(2)]
        junk = cx.sb(st, [128, D_MODEL], BF16, name="junk")
        ss = [cx.sb(st, [128, 1], F32, name="ss") for _ in range(2)]
        hb = [cx.sb(st, [128, D_MODEL], BF16, name="hb") for _ in range(2)]
        hT = [cx.sb(st, [128, KC, G], BF16, name="hT") for _ in range(2)]
        wb = [cx.sb(st, [128, KC, 512], BF16, name="wb") for _ in range(2)]
        ob = [cx.sb(st, [128, 512], F32, name="ob") for _ in range(4)]
        ptr = [cx.ps(st, [128, 8, 128], BF16, name="ptr") for _ in range(2)]
        pmm = [cx.ps(st, [128, 512], F32, name="pmm") for _ in range(4)]
        ntr = 0
        nmm = 0
        nw = 0
        for gi in range(S // G):
            hTg = hT[gi % 2]
            for tl in range(G // 128):
                tt = gi * (G // 128) + tl
                xb = xt[tt % 2]
                sb_ = ss[tt % 2]
                hbb = hb[tt % 2]
                cx.dma('sp', xb[:, :], x_d[tt * 128:(tt + 1) * 128, :], writes=[xb])
                rms_rstd(cx, xb, xb[:, :], D_MODEL, sb_, junk)
                cx.op('dve', [xb, sb_, gB], [hbb],
                      lambda e: e.scalar_tensor_tensor(out=hbb[:, :], in0=xb[:, :], scalar=sb_[:, 0:1],
                                                       in1=gB[:, :], op0=ALU.mult, op1=ALU.mult))
                for k4 in range(KC // 4):
                    pt = ptr[ntr % 2]
                    ntr += 1
                    for j in range(4):
                        kc = k4 * 4 + j
                        transp(cx, pt, pt[:, j, :], hbb[:, kc * 128:(kc + 1) * 128], cx.ident[:, :], [hbb, cx.ident])
                    eng = 'act' if (k4 % 2 == 0) else 'dve'
                    dst = hTg[:, k4 * 4:(k4 + 1) * 4, tl * 128:(tl + 1) * 128]
                    if eng == 'act':
                        cx.op('act', [pt], [hTg], lambda e: e.copy(dst, pt[:, 0:4, :]))
                    else:
                        cx.op('dve', [pt], [hTg], lambda e: e.tensor_copy(dst, pt[:, 0:4, :]))
            for (c0, c1, sig) in chunks:
                w = c1 - c0
                wbb = wb[nw % 2]
                nw += 1
                cx.dma('sp', wbb[:, :, :w], wv[:, :, c0:c1], writes=[wbb])
                for tl in range(G // 128):
                    tt = gi * (G // 128) + tl
                    pm = pmm[nmm % 4]
                    obb = ob[nmm % 4]
                    nmm += 1
                    for kc in range(KC):
                        mm(cx, pm, pm[:, :w], hTg[:, kc, tl * 128:(tl + 1) * 128], wbb[:, kc, :w],
                           [hTg, wbb], kc == 0, kc == KC - 1)
                    if sig:
                        cx.op('act', [pm], [obb],
                              lambda e: e.activation(out=obb[:, :w], in_=pm[:, :w], func=AF.Sigmoid))
                    elif nmm % 2 == 0:
                        cx.op('dve', [pm], [obb], lambda e: e.tensor_copy(obb[:, :w], pm[:, :w]))
                    else:
                        cx.op('act', [pm], [obb], lambda e: e.copy(obb[:, :w], pm[:, :w]))
                    cx.dma('pool', P_d[tt * 128:(tt + 1) * 128, c0:c1], obb[:, :w], reads=[obb])
    cx.barrier()


def evac(cx, eng, src_buf, src_ap, dst_buf, dst_ap, extra_reads=()):
    if eng == 'act':
        cx.op('act', [src_buf] + list(extra_reads), [dst_buf], lambda e: e.copy(dst_ap, src_ap))
    else:
        cx.op(eng, [src_buf] + list(extra_reads), [dst_buf], lambda e: e.tensor_copy(dst_ap, src_ap))


class TrPool:
    def __init__(self, cx, st, n=2, dtype=BF16):
        self.cx = cx
        self.bufs = [cx.ps(st, [128, 8 if dtype == BF16 else 4, 128], dtype, name="ptr") for _ in range(n)]
        self.k = 0
        self.dtype = dtype

    def transpose_cols(self, src_buf, src_ap_fn, nblk, dst_buf, dst_ap_fn, rows=128, blkw=128):
        cx = self.cx
        if cx._rec is not None:
            atomic(cx, lambda: self.transpose_cols(src_buf, src_ap_fn, nblk, dst_buf, dst_ap_fn, rows, blkw))
            return
        ident = cx.ident if self.dtype == BF16 else cx.identf
        j = 0
        while j < nblk:
            cnt = min(4, nblk - j)
            pt = self.bufs[self.k % len(self.bufs)]
            eng = 'act' if self.k % 2 == 0 else 'dve'
            self.k += 1
            for i in range(cnt):
                transp(cx, pt, pt[:blkw, i, :rows], src_ap_fn(j + i), ident[:rows, :rows], [src_buf, ident])
            evac(cx, eng, pt, pt[:blkw, :cnt, :rows], dst_buf, dst_ap_fn(j, cnt))
            j += cnt


def phase_merge(cx, S, ysrcs, wbr_bf, P_d, M_d):
    with contextlib.ExitStack() as st:
        wbr = cx.sb(st, [128, 16, D_MODEL], BF16, name="wbr")
        wv = wbr_bf.rearrange("m (kc p) n -> p (m kc) n", p=128)
        for q in range(4):
            cx.dma('sp', wbr[:, q * 4:(q + 1) * 4, :], wv[:, q * 4:(q + 1) * 4, :], writes=[wbr])
        yt = [cx.sb(st, [128, BW], F32, name="yt") for _ in range(3)]
        yb = [cx.sb(st, [128, BW], BF16, name="yb") for _ in range(2)]
        yT = [cx.sb(st, [128, 4, 128], BF16, name="yT") for _ in range(2)]
        sg = [cx.sb(st, [128, D_MODEL], F32, name="sg") for _ in range(2)]
        mg = [cx.sb(st, [128, D_MODEL], F32, name="mg") for _ in range(2)]
        tmp = [cx.sb(st, [128, 512], F32, name="tmp") for _ in range(2)]
        trp = TrPool(cx, st)
        pmm = [cx.ps(st, [128, 512], F32, name="pmm") for _ in range(4)]
        k = 0
        for tt in range(S // 128):
            rows = slice(tt * 128, (tt + 1) * 128)
            mgb = mg[tt % 2]
            for m in range(4):
                k += 1
                y0 = yt[k % 3]
                cx.dma('sp', y0[:, :], ysrcs[m][0][rows, :], writes=[y0])
                for extra in ysrcs[m][1:]:
                    k += 1
                    y1 = yt[k % 3]
                    cx.dma('sp', y1[:, :], extra[rows, :], writes=[y1])
                    cx.op('pool', [y0, y1], [y0], lambda e: e.tensor_tensor(y0[:, :], y0[:, :], y1[:, :], ALU.add))
                ybb = yb[m % 2]
                cx.op('act', [y0], [ybb], lambda e: e.copy(ybb[:, :], y0[:, :]))
                yTb = yT[m % 2]
                trp.transpose_cols(ybb, lambda j: ybb[:, j * 128:(j + 1) * 128], 4, yTb,
                                   lambda j0, cnt: yTb[:, j0:j0 + cnt, :])
                sgb = sg[m % 2]
                cx.dma('act', sgb[:, :], P_d[rows, OFF_GATE + m * D_MODEL:OFF_GATE + (m + 1) * D_MODEL], writes=[sgb])
                for nc_ in range(4):
                    cs = slice(nc_ * 512, (nc_ + 1) * 512)
                    pm = pmm[(m * 4 + nc_) % 4]
                    for kc in range(4):
                        mm(cx, pm, pm[:, :], yTb[:, kc, :], wbr[:, m * 4 + kc, cs], [yTb, wbr], kc == 0, kc == 3)
                    if m == 0:
                        cx.op('dve', [pm, sgb], [mgb],
                              lambda e: e.tensor_tensor(mgb[:, cs], pm[:, :], sgb[:, cs], ALU.mult))
                    else:
                        tb = tmp[nc_ % 2]
                        cx.op('dve', [pm, sgb], [tb],
                              lambda e: e.tensor_tensor(tb[:, :], pm[:, :], sgb[:, cs], ALU.mult))
                        cx.op('dve' if nc_ % 2 else 'pool', [tb, mgb], [mgb],
                              lambda e: e.tensor_tensor(mgb[:, cs], mgb[:, cs], tb[:, :], ALU.add))
            cx.dma('pool', M_d[rows, :], mgb[:, :], reads=[mgb])
    cx.barrier()


def phase_out(cx, S, M_d, wout_bf, Z_d):
    with contextlib.ExitStack() as st:
        wo = cx.sb(st, [128, 16, D_MODEL], BF16, name="wo")
        wv = wout_bf.rearrange("(kc p) n -> p kc n", p=128)
        for q in range(4):
            cx.dma('sp', wo[:, q * 4:(q + 1) * 4, :], wv[:, q * 4:(q + 1) * 4, :], writes=[wo])
        mt = [cx.sb(st, [128, D_MODEL], F32, name="mt") for _ in range(2)]
        mb = [cx.sb(st, [128, D_MODEL], BF16, name="mb") for _ in range(2)]
        mT = [cx.sb(st, [128, 16, 128], BF16, name="mT") for _ in range(2)]
        ob = [cx.sb(st, [128, 512], F32, name="ob") for _ in range(4)]
        trp = TrPool(cx, st)
        pmm = [cx.ps(st, [128, 512], F32, name="pmm") for _ in range(4)]
        k = 0
        for tt in range(S // 128):
            rows = slice(tt * 128, (tt + 1) * 128)
            mtb, mbb, mTb = mt[tt % 2], mb[tt % 2], mT[tt % 2]
            cx.dma('sp', mtb[:, :], M_d[rows, :], writes=[mtb])
            cx.op('act', [mtb], [mbb], lambda e: e.copy(mbb[:, :], mtb[:, :]))
            trp.transpose_cols(mbb, lambda j: mbb[:, j * 128:(j + 1) * 128], 16, mTb,
                               lambda j0, cnt: mTb[:, j0:j0 + cnt, :])
            for nc_ in range(4):
                cs = slice(nc_ * 512, (nc_ + 1) * 512)
                pm = pmm[k % 4]
                obb = ob[k % 4]
                k += 1
                for kc in range(16):
                    mm(cx, pm, pm[:, :], mTb[:, kc, :], wo[:, kc, cs], [mTb, wo], kc == 0, kc == 15)
                evac(cx, 'act' if k % 2 else 'dve', pm, pm[:, :], obb, obb[:, :])
                cx.dma('pool', Z_d[rows, cs], obb[:, :], reads=[obb])
    cx.barrier()


def phase_normres(cx, S, x_d, Z_d, g_row, out_d):
    with contextlib.ExitStack() as st:
        gB = cx.sb(st, [128, D_MODEL], F32, name="gB")
        load_bcast_row(cx, 'sp', gB, g_row, D_MODEL)
        zt = [cx.sb(st, [128, D_MODEL], F32, name="zt") for _ in range(2)]
        xt = [cx.sb(st, [128, D_MODEL], F32, name="xt") for _ in range(2)]
        ot = [cx.sb(st, [128, D_MODEL], F32, name="ot") for _ in range(2)]
        junk = cx.sb(st, [128, D_MODEL], BF16, name="junk")
        ss = [cx.sb(st, [128, 1], F32, name="ss") for _ in range(2)]
        def body(tt):
            rows = slice(tt * 128, (tt + 1) * 128)
            z, x, o, s_ = zt[tt % 2], xt[tt % 2], ot[tt % 2], ss[tt % 2]
            cx.dma('sp', z[:, :], Z_d[rows, :], writes=[z])
            cx.dma('act', x[:, :], x_d[rows, :], writes=[x])
            rms_rstd(cx, z, z[:, :], D_MODEL, s_, junk)
            cx.op('dve', [z, s_, gB], [o],
                  lambda e: e.scalar_tensor_tensor(out=o[:, :], in0=z[:, :], scalar=s_[:, 0:1], in1=gB[:, :],
                                                   op0=ALU.mult, op1=ALU.mult))
            cx.op('pool', [o, x], [o], lambda e: e.tensor_tensor(o[:, :], o[:, :], x[:, :], ALU.add))
            cx.dma('pool', out_d[rows, :], o[:, :], reads=[o])
        run_tiles(cx, body, S // 128)
    cx.barrier()


def phase_ffn(cx, S, x_d, g_row, wup_bf, wdn_bf, Z_d):
    G = 512 if S >= 512 else S
    NT = G // 128
    KC = D_MODEL // 128
    FC = D_FF // 128
    UW = 256
    wuv = wup_bf.rearrange("(kc p) f -> p kc f", p=128)
    wdv = wdn_bf.rearrange("(fc p) n -> p fc n", p=128)
    with contextlib.ExitStack() as st:
        gB = cx.sb(st, [128, D_MODEL], F32, name="gB")
        load_bcast_row(cx, 'sp', gB, g_row, D_MODEL)
        xt = [cx.sb(st, [128, D_MODEL], F32, name="xt") for _ in range(2)]
        junk = cx.sb(st, [128, D_MODEL], BF16, name="junk")
        ss = [cx.sb(st, [128, 1], F32, name="ss") for _ in range(2)]
        hb = [cx.sb(st, [128, D_MODEL], BF16, name="hb") for _ in range(2)]
        hT = cx.sb(st, [128, KC, G], BF16, name="hT")
        aT = cx.sb(st, [128, FC, G], BF16, name="aT")
        wu = [cx.sb(st, [128, KC, UW], BF16, name="wu") for _ in range(2)]
        wd = [cx.sb(st, [128, 8, 512], BF16, name="wd") for _ in range(2)]
        rl = [cx.sb(st, [128, G], F32, name="rl") for _ in range(2)]
        ob = [cx.sb(st, [128, 512], F32, name="ob") for _ in range(4)]
        trp = TrPool(cx, st, n=1)
        pup = [cx.ps(st, [128, G], F32, name="pup") for _ in range(2)]
        pdn = [cx.ps(st, [128, 512], F32, name="pdn") for _ in range(NT)]
        nu = 0
        nd = 0
        no = 0
        for gi in range(S // G):
            for tl in range(NT):
                tt = gi * NT + tl
                x, s_, h = xt[tt % 2], ss[tt % 2], hb[tt % 2]
                cx.dma('sp', x[:, :], x_d[tt * 128:(tt + 1) * 128, :], writes=[x])
                rms_rstd(cx, x, x[:, :], D_MODEL, s_, junk)
                cx.op('dve', [x, s_, gB], [h],
                      lambda e: e.scalar_tensor_tensor(out=h[:, :], in0=x[:, :], scalar=s_[:, 0:1], in1=gB[:, :],
                                                       op0=ALU.mult, op1=ALU.mult))
                trp.transpose_cols(h, lambda j: h[:, j * 128:(j + 1) * 128], KC, hT,
                                   lambda j0, cnt: hT[:, j0:j0 + cnt, tl * 128:(tl + 1) * 128])
            for uc in range(D_FF // UW):
                wub = wu[nu % 2]
                nu += 1
                cx.dma('sp', wub[:, :, :], wuv[:, :, uc * UW:(uc + 1) * UW], writes=[wub])
                for j in range(UW // 128):
                    fc = uc * (UW // 128) + j
                    pu = pup[fc % 2]
                    r = rl[fc % 2]
                    for kc in range(KC):
                        mm(cx, pu, pu[:, :], wub[:, kc, j * 128:(j + 1) * 128], hT[:, kc, :], [wub, hT],
                           kc == 0, kc == KC - 1)
                    cx.op('act', [pu], [r], lambda e: e.activation(out=r[:, :], in_=pu[:, :], func=AF.Relu))
                    eng = 'dve' if fc % 2 == 0 else 'pool'
                    cx.op(eng, [r], [aT], lambda e: e.tensor_tensor(aT[:, fc, :], r[:, :], r[:, :], ALU.mult))
            for nc_ in range(4):
                cs = slice(nc_ * 512, (nc_ + 1) * 512)
                for fg in range(FC // 8):
                    wdb = wd[nd % 2]
                    nd += 1
                    cx.dma('act', wdb[:, :, :], wdv[:, fg * 8:(fg + 1) * 8, cs], writes=[wdb])
                    for f8 in range(8):
                        fc = fg * 8 + f8
                        for tl in range(NT):
                            mm(cx, pdn[tl], pdn[tl][:, :], aT[:, fc, tl * 128:(tl + 1) * 128], wdb[:, f8, :],
                               [aT, wdb], fc == 0, fc == FC - 1)
                for tl in range(NT):
                    tt = gi * NT + tl
                    o = ob[no % 4]
                    no += 1
                    evac(cx, 'act' if no % 2 else 'dve', pdn[tl], pdn[tl][:, :], o, o[:, :])
                    cx.dma('pool', Z_d[tt * 128:(tt + 1) * 128, cs], o[:, :], reads=[o])
    cx.barrier()


def rope_tm(cx, src, x1, x2, c, s, dst, o1, o2, tmps, scale=None):
    (ta, tap), (tb, tbp) = tmps
    cx.op('dve', [src] + c[:1] + [], [ta], lambda e: e.tensor_tensor(tap, x1, c[1], ALU.mult))
    cx.op('pool', [src] + s[:1], [tb], lambda e: e.tensor_tensor(tbp, x2, s[1], ALU.mult))
    cx.op('dve', [ta, tb], [dst], lambda e: e.tensor_tensor(o1, tap, tbp, ALU.subtract))
    cx.op('pool', [src] + c[:1], [ta], lambda e: e.tensor_tensor(tap, x2, c[1], ALU.mult))
    cx.op('dve', [src] + s[:1], [tb], lambda e: e.tensor_tensor(tbp, x1, s[1], ALU.mult))
    cx.op('pool', [ta, tb], [dst], lambda e: e.tensor_tensor(o2, tap, tbp, ALU.add))
    if scale is not None:
        cx.op('pool', [dst], [dst], lambda e: e.tensor_scalar(o1, o1, scale, None, ALU.mult))
        cx.op('pool', [dst], [dst], lambda e: e.tensor_scalar(o2, o2, scale, None, ALU.mult))


def bcast_scalar_max(cx, st, trp_f, run_buf, out_col):
    pt = trp_f.bufs[0]
    transp(cx, pt, pt[0:1, 0, :], run_buf[:, 0:1], cx.identf[:, :], [run_buf, cx.identf])
    row = cx.sb(st, [1, 128], F32, name="mxrow")
    one = cx.sb(st, [1, 1], F32, name="mxone")
    evac(cx, 'dve', pt, pt[0:1, 0, :], row, row[:, :])
    cx.op('dve', [row], [one], lambda e: e.tensor_reduce(out=one[:, :], in_=row[:, :], axis=AX.X, op=ALU.max))
    mm(cx, pt, pt[:, 1, 0:1], cx.ones_f[0:1, :], one[0:1, 0:1], [cx.ones_f, one], True, True)
    evac(cx, 'dve', pt, pt[:, 1, 0:1], out_col, out_col[:, 0:1])


class AttnRes:
    def __init__(self, cx, st, W):
        self.sT = [cx.ps(st, [128, 512], F32, name="sT") for _ in range(2)]
        self.acc = [cx.ps(st, [128, 2, 256], F32, name="acc") for _ in range(4)]
        self.pT = [cx.sb(st, [128, 512], BF16, name="pT") for _ in range(3)]
        self.n = 0
        self.nq = 0


def attn_core(cx, res, S, qchunks, kchunks, vaug_fn, W, blocks_fn, epilogue, mode='softmax', qbs=None):
    QW = min(512, S)
    NJ = QW // 128
    for QB in (range(S // QW) if qbs is None else qbs):
        q0 = QB * QW
        blocks = blocks_fn(QB)
        accs = [res.acc[(res.nq % 2) * 2 + (j // 2)] for j in range(NJ)]
        res.nq += 1

        def stage1(b):
            sT = res.sT[res.n % 2]
            pT = res.pT[res.n % 3]
            res.n += 1
            nk, k0 = b['nk'], b['k0']
            nmm = len(qchunks) + (1 if b.get('extra') else 0)
            i = 0
            for (qb_, qap), (kb_, kap) in zip(qchunks, kchunks):
                ka = kap(k0, nk) if callable(kap) else kap[:, k0:k0 + nk]
                mm(cx, sT, sT[:nk, :QW], ka, qap[:, q0:q0 + QW], [qb_, kb_], i == 0, i == nmm - 1)
                i += 1
            if b.get('extra'):
                lb, lap, rb, rap = b['extra']
                mm(cx, sT, sT[:nk, :QW], lap, rap, list(lb) + list(rb), False, True)
            if mode == 'softmax':
                cx.op('act', [sT], [pT], lambda e: e.activation(out=pT[:nk, :QW], in_=sT[:nk, :QW], func=AF.Exp))
                if b.get('mask'):
                    mb, map_ = b['mask']
                    cx.op('pool' if nk == 128 else 'dve', [pT, mb], [pT],
                          lambda e: e.tensor_tensor(pT[:nk, :QW], pT[:nk, :QW], map_, ALU.mult))
            else:
                c, gb, gap = b['decay']
                cx.op('dve', [sT, gb], [pT],
                      lambda e: e.scalar_tensor_tensor(out=pT[:nk, :QW], in0=sT[:nk, :QW], scalar=float(c), in1=gap,
                                                       op0=ALU.mult, op1=ALU.mult))
            return pT

        def stage2(bi, b, pT):
            nk = b['nk']
            vb, vap = vaug_fn(b['kb'], nk)
            for j in range(NJ):
                a = accs[j]
                mm(cx, a, a[:, j % 2, :W], pT[:nk, j * 128:(j + 1) * 128], vap, [pT, vb],
                   bi == 0 and j % 2 == 0, bi == len(blocks) - 1, skip_group_check=True)

        prev = None
        for bi, b in enumerate(blocks):
            pT = stage1(b)
            if prev is not None:
                stage2(*prev)
            prev = (bi, b, pT)
        stage2(*prev)
        for j in range(NJ):
            epilogue(QB, j, accs[j], accs[j][:, j % 2, :W])


def causal_blocks(QB, QW, cmask):
    out = []
    nd = QW // 128
    for kb in range(nd * (QB + 1)):
        i = kb - nd * QB
        out.append(dict(kb=kb, k0=kb * 128, nk=128, mask=(cmask, cmask[:, i, :QW]) if i >= 0 else None))
    return out


MLA_SCALE = 192 ** -0.5


def phase_mla(cx, S, P_d, gq_row, gkv_row, wuq_bf, wukv_bf, cos_d, sin_d, cmask_d, Y_d, scr):
    NT = S // 128
    G = min(512, S)
    NG = G // 128
    with contextlib.ExitStack() as st:
        with contextlib.ExitStack() as s1:
            gq = cx.sb(s1, [128, 384], F32, name="gq")
            gkv = cx.sb(s1, [128, 128], F32, name="gkv")
            load_bcast_row(cx, 'sp', gq, gq_row, 384)
            load_bcast_row(cx, 'sp', gkv, gkv_row, 128)
            wuq = cx.sb(s1, [128, 3, 768], BF16, name="wuq")
            cx.dma('sp', wuq[:, :, :], wuq_bf.rearrange("(kc p) n -> p kc n", p=128), writes=[wuq])
            wukv = cx.sb(s1, [128, 1024], BF16, name="wukv")
            cx.dma('sp', wukv[:, :], wukv_bf[:, :], writes=[wukv])
            pm = [cx.sb(s1, [128, 576], F32, name="pm") for _ in range(2)]
            cs_t = [cx.sb(s1, [128, 64], F32, name="cs") for _ in range(2)]
            junk = cx.sb(s1, [128, 768], BF16, name="junk")
            ss = [cx.sb(s1, [128, 1], F32, name="ss") for _ in range(2)]
            nb = [cx.sb(s1, [128, 384], BF16, name="nb") for _ in range(2)]
            nT = [cx.sb(s1, [128, 3, 128], BF16, name="nT") for _ in range(2)]
            qf = [cx.sb(s1, [128, 4, 256], F32, name="qf") for _ in range(2)]
            qs = [cx.sb(s1, [128, 4, 193], BF16, name="qs") for _ in range(2)]
            qsf = [cx.sb(s1, [128, 4, 64], F32, name="qsf") for _ in range(2)]
            t1s = [cx.sb(s1, [128, 4, 32], F32, name="t1") for _ in range(2)]
            t2s = [cx.sb(s1, [128, 4, 32], F32, name="t2") for _ in range(2)]
            kr = [cx.sb(s1, [128, 65], BF16, name="kr") for _ in range(2)]
            krf = [cx.sb(s1, [128, 64], F32, name="krf") for _ in range(2)]
            kb16 = [cx.sb(s1, [128, 4, 128], BF16, name="kb16") for _ in range(2)]
            va = [cx.sb(s1, [128, 4, 129], BF16, name="va") for _ in range(2)]
            sqs = [cx.sb(s1, [128, 4, 256], F32, name="sq") for _ in range(2)]
            n4 = [cx.sb(s1, [128, 4], F32, name="n4") for _ in range(2)]
            n1 = [cx.sb(s1, [128, 1], F32, name="n1") for _ in range(2)]
            kmx = cx.sb(s1, [128, 1], F32, name="kmx")
            kmax = cx.sb(s1, [128, 1], F32, name="kmax")
            gA = cx.sb(s1, [128, 4, G], BF16, name="gA")
            gB_ = cx.sb(s1, [65, 4, G], BF16, name="gB_")
            trp = TrPool(cx, s1)
            trf = TrPool(cx, s1, n=1, dtype=F32)
            pq = [cx.ps(s1, [128, 512], F32, name="pq") for _ in range(2)]
            pq2 = [cx.ps(s1, [128, 512], F32, name="pq2") for _ in range(2)]
            cx.op('pool', [], [kmx], lambda e: e.memset(kmx[:, :], 0.0))

            def load_norm_T(tt, c0, n, gbuf, k):
                p = pm[k % 2]
                cx.dma('sp', p[:, :], P_d[tt * 128:(tt + 1) * 128, OFF_MLA:OFF_MLA + 576], writes=[p])
                s_ = ss[k % 2]
                rms_rstd(cx, p, p[:, c0:c0 + n], n, s_, junk)
                nbb = nb[k % 2]
                cx.op('dve', [p, s_, gbuf], [nbb],
                      lambda e: e.scalar_tensor_tensor(out=nbb[:, :n], in0=p[:, c0:c0 + n], scalar=s_[:, 0:1],
                                                       in1=gbuf[:, :n], op0=ALU.mult, op1=ALU.mult))
                nTb = nT[k % 2]
                trp.transpose_cols(nbb, lambda j: nbb[:, j * 128:(j + 1) * 128], n // 128, nTb,
                                   lambda j0, cnt: nTb[:, j0:j0 + cnt, :])
                return p, nTb

            def body(tt):
                t1, t2, sq = t1s[tt % 2], t2s[tt % 2], sqs[tt % 2]
                tl = tt % NG
                p, nTb = load_norm_T(tt, 384, 128, gkv, tt)
                c_t = cs_t[tt % 2]
                cx.dma('act', c_t[:, 0:32], cos_d[tt * 128:(tt + 1) * 128, :], writes=[c_t])
                cx.dma('act', c_t[:, 32:64], sin_d[tt * 128:(tt + 1) * 128, :], writes=[c_t])
                pa, pb = pq[tt % 2], pq2[tt % 2]
                mm(cx, pa, pa[:, :], nTb[:, 0, :], wukv[:, 0:512], [nTb, wukv], True, True)
                mm(cx, pb, pb[:, :], nTb[:, 0, :], wukv[:, 512:1024], [nTb, wukv], True, True)
                q = qf[tt % 2]
                evac(cx, 'act', pa, pa[:, :].rearrange("p (h c) -> p h c", h=2), q, q[:, 0:2, :])
                evac(cx, 'dve', pb, pb[:, :].rearrange("p (h c) -> p h c", h=2), q, q[:, 2:4, :])
                k16, vab, krb, krfb = kb16[tt % 2], va[tt % 2], kr[tt % 2], krf[tt % 2]
                cx.op('pool', [q], [k16], lambda e: e.tensor_copy(k16[:, :, :], q[:, :, 0:128]))
                cx.op('pool', [q], [vab], lambda e: e.tensor_copy(vab[:, :, 0:128], q[:, :, 128:256]))
                cx.op('pool', [], [vab], lambda e: e.memset(vab[:, :, 128:129], 1.0))
                rope_tm(cx, p, p[:, 512:544], p[:, 544:576], [c_t, c_t[:, 0:32]], [c_t, c_t[:, 32:64]],
                        krfb, krfb[:, 0:32], krfb[:, 32:64], [(t1, t1[:, 0, :]), (t2, t2[:, 0, :])])
                cx.op('pool', [krfb], [krb], lambda e: e.tensor_copy(krb[:, 0:64], krfb[:, :]))
                cx.op('pool', [], [krb], lambda e: e.memset(krb[:, 64:65], 1.0))
                cx.op('dve', [q], [sq], lambda e: e.tensor_tensor(sq[:, :, 0:128], q[:, :, 0:128], q[:, :, 0:128], ALU.mult))
                n4b, n1b = n4[tt % 2], n1[tt % 2]
                cx.op('dve', [sq], [n4b], lambda e: e.tensor_reduce(out=n4b[:, :], in_=sq[:, :, 0:128], axis=AX.X, op=ALU.add))
                cx.op('dve', [n4b], [n1b], lambda e: e.tensor_reduce(out=n1b[:, :], in_=n4b[:, :], axis=AX.X, op=ALU.max))
                cx.op('act', [krfb], [junk, n4b],
                      lambda e: e.activation(out=junk[:, :64], in_=krfb[:, :], func=AF.Square, accum_out=n4b[:, 0:1]))
                cx.op('dve', [n4b, n1b], [n1b], lambda e: e.tensor_tensor(n1b[:, :], n1b[:, :], n4b[:, 0:1], ALU.add))
                cx.op('dve', [n1b, kmx], [kmx], lambda e: e.tensor_tensor(kmx[:, :], kmx[:, :], n1b[:, :], ALU.max))
                trp.transpose_cols(k16, lambda j: k16[:, j, :], 4, gA,
                                   lambda j0, cnt: gA[:, j0:j0 + cnt, tl * 128:(tl + 1) * 128])
                trp.transpose_cols(krb, lambda j: krb[:, :], 1, gB_,
                                   lambda j0, cnt: gB_[:65, 0:1, tl * 128:(tl + 1) * 128], blkw=65)
                cx.dma('pool', scr['va'][tt * 128:(tt + 1) * 128, :, :], vab[:, :, :], reads=[vab])
                if tl == NG - 1:
                    def post():
                        g0 = (tt // NG) * G
                        cx.dma('pool', scr['knT'][:, :, g0:g0 + G].rearrange("h d s -> d h s"), gA[:, :, :], reads=[gA])
                        cx.dma('pool', scr['krT'][:, g0:g0 + G], gB_[:65, 0, :], reads=[gB_])
                    return post
            run_tiles(cx, body, NT)
            bcast_scalar_max(cx, s1, trf, kmx, kmax)
            cx.op('act', [kmax], [kmax], lambda e: e.activation(out=kmax[:, :], in_=kmax[:, :], func=AF.Sqrt))
            def body(tt):
                t1, t2, sq = t1s[tt % 2], t2s[tt % 2], sqs[tt % 2]
                tl = tt % NG
                p, nTb = load_norm_T(tt, 0, 384, gq, tt)
                c_t = cs_t[tt % 2]
                cx.dma('act', c_t[:, 0:32], cos_d[tt * 128:(tt + 1) * 128, :], writes=[c_t])
                cx.dma('act', c_t[:, 32:64], sin_d[tt * 128:(tt + 1) * 128, :], writes=[c_t])
                pa, pb = pq[tt % 2], pq2[tt % 2]
                for kc in range(3):
                    mm(cx, pa, pa[:, :], nTb[:, kc, :], wuq[:, kc, 0:512], [nTb, wuq], kc == 0, kc == 2)
                for kc in range(3):
                    mm(cx, pb, pb[:, :256], nTb[:, kc, :], wuq[:, kc, 512:768], [nTb, wuq], kc == 0, kc == 2)
                q = qf[tt % 2]
                qv = q[:, :, :].rearrange("p h c -> p (h c)")
                evac(cx, 'act', pa, pa[:, :], q, qv[:, 0:512])
                evac(cx, 'dve', pb, pb[:, :256], q, qv[:, 512:768])
                qh = qv[:, 0:768].rearrange("p (h c) -> p h c", h=4)
                cx.op('dve', [q], [sq], lambda e: e.tensor_tensor(sq[:, :, 0:192], qh, qh, ALU.mult))
                n4b = n4[tt % 2]
                cx.op('dve', [sq], [n4b], lambda e: e.tensor_reduce(out=n4b[:, :], in_=sq[:, :, 0:192], axis=AX.X, op=ALU.add))
                cx.op('act', [n4b], [n4b], lambda e: e.activation(out=n4b[:, :], in_=n4b[:, :], func=AF.Sqrt))
                cx.op('dve', [n4b, kmax], [n4b],
                      lambda e: e.tensor_scalar(n4b[:, :], n4b[:, :], kmax[:, 0:1], -MLA_SCALE, ALU.mult, ALU.mult))
                qsb, qsfb = qs[tt % 2], qsf[tt % 2]
                cb = c_t[:, 0:32].unsqueeze(1).broadcast_to([128, 4, 32])
                sb_ = c_t[:, 32:64].unsqueeze(1).broadcast_to([128, 4, 32])
                rope_tm(cx, q, qh[:, :, 128:160], qh[:, :, 160:192], [c_t, cb], [c_t, sb_],
                        qsfb, qsfb[:, :, 0:32], qsfb[:, :, 32:64], [(t1, t1[:, :, :]), (t2, t2[:, :, :])])
                cx.op('act', [q], [qsb], lambda e: e.activation(out=qsb[:, :, 0:128], in_=qh[:, :, 0:128], func=AF.Copy, scale=MLA_SCALE))
                cx.op('act', [qsfb], [qsb], lambda e: e.activation(out=qsb[:, :, 128:192], in_=qsfb[:, :, :], func=AF.Copy, scale=MLA_SCALE))
                cx.op('pool', [n4b], [qsb], lambda e: e.tensor_copy(qsb[:, :, 192:193], n4b[:, :].unsqueeze(2)))
                trp.transpose_cols(qsb, lambda j: qsb[:, j, 0:128], 4, gA,
                                   lambda j0, cnt: gA[:, j0:j0 + cnt, tl * 128:(tl + 1) * 128])
                trp.transpose_cols(qsb, lambda j: qsb[:, j, 128:193], 4, gB_,
                                   lambda j0, cnt: gB_[:65, j0:j0 + cnt, tl * 128:(tl + 1) * 128], blkw=65)
                if tl == NG - 1:
                    def post():
                        g0 = (tt // NG) * G
                        cx.dma('pool', scr['qnT'][:, :, g0:g0 + G].rearrange("h d s -> d h s"), gA[:, :, :], reads=[gA])
                        cx.dma('pool', scr['qrT'][:, :, g0:g0 + G].rearrange("h d s -> d h s"), gB_[:65, :, :], reads=[gB_])
                    return post
            run_tiles(cx, body, NT)
        cx.barrier()
        res = AttnRes(cx, st, 129)
        cmask = cx.sb(st, [128, 4, 512], BF16, name="cmask")
        cx.dma('sp', cmask[:, :, :], cmask_d.rearrange("i k q -> k i q"), writes=[cmask])
        krT = cx.sb(st, [65, S], BF16, name="krT")
        cx.dma('sp', krT[:, :], scr['krT'][:, :], writes=[krT])
        qn = [cx.sb(st, [128, S], BF16, name="qn") for _ in range(2)]
        qr = [cx.sb(st, [65, S], BF16, name="qr") for _ in range(2)]
        kn = [cx.sb(st, [128, S], BF16, name="kn") for _ in range(2)]
        vv = [cx.sb(st, [128, NT, 129], BF16, name="vv") for _ in range(2)]
        rc = [cx.sb(st, [128, 1], F32, name="rc") for _ in range(2)]
        ot = [cx.sb(st, [128, 128], F32, name="ot") for _ in range(2)]
        cnt = [0]
        QW = min(512, S)
        for h in range(4):
            a, b, c, v = qn[h % 2], qr[h % 2], kn[h % 2], vv[h % 2]
            cx.dma('sp', a[:, :], scr['qnT'][h], writes=[a])
            cx.dma('sp', b[:, :], scr['qrT'][h], writes=[b])
            cx.dma('sp', c[:, :], scr['knT'][h], writes=[c])
            cx.dma('sp', v[:, :, :], scr['va'][:, h, :].rearrange("(t p) c -> p t c", p=128), writes=[v])

            def epi(QB, j, accb, acc_ap, h=h):
                k = cnt[0]
                cnt[0] += 1
                r, o = rc[k % 2], ot[k % 2]
                cx.op('dve', [accb], [r], lambda e: e.tensor_scalar(r[:, :], acc_ap[:, 128:129], 1e-30, None, ALU.add))
                cx.op('dve', [r], [r], lambda e: e.reciprocal(r[:, :], r[:, :]))
                cx.op('act', [accb, r], [o], lambda e: e.activation(out=o[:, :], in_=acc_ap[:, 0:128], func=AF.Copy, scale=r[:, 0:1]))
                t0 = QB * QW + j * 128
                cx.dma('pool', Y_d[t0:t0 + 128, h * 128:(h + 1) * 128], o[:, :], reads=[o])

            attn_core(cx, res, S, [(a, a[:, :]), (b, b[:65, :])], [(c, c[:, :]), (krT, krT[:65, :])],
                      lambda kb, nk, v=v: (v, v[:nk, kb, :]), 129,
                      lambda QB: causal_blocks(QB, QW, cmask), epi)
    cx.barrier()


def head_norm_tm(cx, src_buf, src_ap, n, eps, cbuf, c_ap, s1, s2, junk):
    cx.op('dve', [src_buf], [s1], lambda e: e.tensor_reduce(out=s1[:, 0:1], in_=src_ap, axis=AX.X, op=ALU.add))
    cx.op('dve', [s1], [s1], lambda e: e.tensor_scalar(s1[:, 0:1], s1[:, 0:1], 1.0 / n, None, ALU.mult))
    cx.op('dve', [src_buf, s1], [cbuf], lambda e: e.tensor_scalar(c_ap, src_ap, s1[:, 0:1], None, ALU.subtract))
    rms_rstd(cx, cbuf, c_ap, n, s2, junk, eps=eps)


def phase_ret(cx, S, P_d, cos_d, sin_d, gdec_d, Y_d, scr):
    NT = S // 128
    G = min(512, S)
    NG = G // 128
    QW = min(512, S)
    with contextlib.ExitStack() as st:
        with contextlib.ExitStack() as s1:
            pr = [cx.sb(s1, [128, 1024], F32, name="pr") for _ in range(2)]
            cs_t = [cx.sb(s1, [128, 64], F32, name="cs") for _ in range(2)]
            ro = [cx.sb(s1, [128, 8, 64], F32, name="ro") for _ in range(2)]
            rb = [cx.sb(s1, [128, 8, 64], BF16, name="rb") for _ in range(2)]
            vb = [cx.sb(s1, [128, 512], BF16, name="vb") for _ in range(2)]
            t1s = [cx.sb(s1, [128, 8, 32], F32, name="t1") for _ in range(2)]
            t2s = [cx.sb(s1, [128, 8, 32], F32, name="t2") for _ in range(2)]
            gQ = cx.sb(s1, [64, 8, G], BF16, name="gQ")
            trp = TrPool(cx, s1)
            def body(tt):
                t1, t2 = t1s[tt % 2], t2s[tt % 2]
                tl = tt % NG
                rows = slice(tt * 128, (tt + 1) * 128)
                p, c_t, r, rbb, v = pr[tt % 2], cs_t[tt % 2], ro[tt % 2], rb[tt % 2], vb[tt % 2]
                cx.dma('sp', p[:, :], P_d[rows, OFF_RET:OFF_RET + 1024], writes=[p])
                cx.dma('act', c_t[:, 0:32], cos_d[rows, :], writes=[c_t])
                cx.dma('act', c_t[:, 32:64], sin_d[rows, :], writes=[c_t])
                qk = p[:, 0:512].rearrange("p (h c) -> p h c", h=8)
                cb = c_t[:, 0:32].unsqueeze(1).broadcast_to([128, 8, 32])
                sb_ = c_t[:, 32:64].unsqueeze(1).broadcast_to([128, 8, 32])
                rope_tm(cx, p, qk[:, :, 0:32], qk[:, :, 32:64], [c_t, cb], [c_t, sb_],
                        r, r[:, :, 0:32], r[:, :, 32:64], [(t1, t1[:, :, :]), (t2, t2[:, :, :])])
                cx.op('act', [r], [rbb], lambda e: e.copy(rbb[:, 0:4, :], r[:, 0:4, :]))
                cx.op('act', [r], [rbb], lambda e: e.activation(out=rbb[:, 4:8, :], in_=r[:, 4:8, :], func=AF.Copy, scale=0.125))
                cx.op('pool', [p], [v], lambda e: e.tensor_copy(v[:, :], p[:, 512:1024]))
                trp.transpose_cols(rbb, lambda j: rbb[:, j, :], 8, gQ,
                                   lambda j0, cnt: gQ[:64, j0:j0 + cnt, tl * 128:(tl + 1) * 128], blkw=64)
                cx.dma('pool', scr['rv'][rows, :, :].rearrange("s h c -> s (h c)"), v[:, :], reads=[v])
                if tl == NG - 1:
                    def post():
                        g0 = (tt // NG) * G
                        cx.dma('pool', scr['rqT'][:, :, g0:g0 + G].rearrange("h d s -> d h s"), gQ[:64, 0:4, :], reads=[gQ])
                        cx.dma('pool', scr['rkT'][:, :, g0:g0 + G].rearrange("h d s -> d h s"), gQ[:64, 4:8, :], reads=[gQ])
                    return post
            run_tiles(cx, body, NT)
        cx.barrier()
        res = AttnRes(cx, st, 128)
        gd = [cx.sb(st, [128, 5, 512], F32, name="gd") for _ in range(2)]
        qT = [cx.sb(st, [64, S], BF16, name="qT") for _ in range(2)]
        kT = [cx.sb(st, [64, S], BF16, name="kT") for _ in range(2)]
        vv = [cx.sb(st, [128, NT, 128], BF16, name="vv") for _ in range(2)]
        gt = [cx.sb(st, [128, 128], F32, name="gt") for _ in range(2)]
        cb_ = [cx.sb(st, [128, 128], F32, name="cb") for _ in range(2)]
        ot = [cx.sb(st, [128, 128], F32, name="ot") for _ in range(2)]
        sA = [cx.sb(st, [128, 1], F32, name="sA") for _ in range(2)]
        sB = [cx.sb(st, [128, 1], F32, name="sB") for _ in range(2)]
        junk = cx.sb(st, [128, 128], BF16, name="junk")
        cnt = [0]
        for h in range(4):
            gamma = 1.0 - 2.0 ** (-5 - h)
            a, c, v, g = qT[h % 2], kT[h % 2], vv[h % 2], gd[h % 2]
            cx.dma('sp', a[:, :], scr['rqT'][h], writes=[a])
            cx.dma('sp', c[:, :], scr['rkT'][h], writes=[c])
            cx.dma('sp', v[:, :, :], scr['rv'][:, h, :].rearrange("(t p) c -> p t c", p=128), writes=[v])
            cx.dma('sp', g[:, :, :], gdec_d[h].rearrange("i k q -> k i q"), writes=[g])
            if 'dbg' in scr and h == 0:
                cx.dma('sp', scr['dbg'][0], a[:, :], reads=[a])
                cx.dma('sp', scr['dbg'][1], c[:, :], reads=[c])

            def blocks(QB, g=g, gamma=gamma):
                out = []
                nd = QW // 128
                for kb in range(nd * (QB + 1)):
                    i = kb - nd * QB
                    if i >= 0:
                        out.append(dict(kb=kb, k0=kb * 128, nk=128, decay=(1.0, g, g[:, 1 + i, :QW])))
                    else:
                        cc = gamma ** (QB * QW - kb * 128)
                        if cc < 1e-30:
                            cc = 0.0
                        out.append(dict(kb=kb, k0=kb * 128, nk=128, decay=(cc, g, g[:, 0, :QW])))
                return out

            def epi(QB, j, accb, acc_ap, h=h):
                k = cnt[0]
                cnt[0] += 1
                t0 = QB * QW + j * 128
                gtb, cbb, o, s_a, s_b = gt[k % 2], cb_[k % 2], ot[k % 2], sA[k % 2], sB[k % 2]
                cx.dma('act', gtb[:, :], P_d[t0:t0 + 128, OFF_RET + 1024 + h * 128:OFF_RET + 1024 + (h + 1) * 128], writes=[gtb])
                cx.op('act', [gtb], [gtb], lambda e: e.activation(out=gtb[:, :], in_=gtb[:, :], func=AF.Silu))
                head_norm_tm(cx, accb, acc_ap, 128, NORM_EPS, cbb, cbb[:, :], s_a, s_b, junk)
                cx.op('dve', [cbb, s_b, gtb], [o],
                      lambda e: e.scalar_tensor_tensor(out=o[:, :], in0=cbb[:, :], scalar=s_b[:, 0:1], in1=gtb[:, :],
                                                       op0=ALU.mult, op1=ALU.mult))
                cx.dma('pool', Y_d[t0:t0 + 128, h * 128:(h + 1) * 128], o[:, :], reads=[o])

            attn_core(cx, res, S, [(a, a[:64, :])], [(c, c[:64, :])], lambda kb, nk, v=v: (v, v[:nk, kb, :]), 128,
                      blocks, epi, mode='decay')
    cx.barrier()


def bc8(ap):
    return ap.unsqueeze(2).broadcast_to([128, 8, 64])


def v3(ap):
    return ap.rearrange("p (h c) -> p h c", h=8)


def phase_rwkv(cx, S, l, P_d, W, Wb, C, Y_d, scr):
    NT = S // 128
    RW = scr['rw']
    names6 = ['rr', 'lw', 'k2', 'vv', 'kn', 'aa']
    with contextlib.ExitStack() as st:
        def brow(name, n, src):
            b = cx.sb(st, [128, n], F32, name=name)
            load_bcast_row(cx, 'sp', b, src, n)
            return b
        muB = brow("muB", 1984, W['rwkv_mu'][l])
        w0B = brow("w0B", 512, W['rwkv_w0'][l])
        a0B = brow("a0B", 512, W['rwkv_a0'][l])
        kkB = brow("kkB", 512, W['rwkv_k_k'][l])
        kaB = brow("kaB", 512, W['rwkv_k_a'][l])
        rkB = brow("rkB", 512, W['rwkv_r_k'][l].rearrange("h c -> (h c)"))
        w2 = cx.sb(st, [96, 512], BF16, name="w2")
        a2 = cx.sb(st, [96, 512], BF16, name="a2")
        g2 = cx.sb(st, [128, 2, 512], BF16, name="g2")
        cx.dma('sp', w2[:, :], Wb['rwkv_w2'][l], writes=[w2])
        cx.dma('sp', a2[:, :], Wb['rwkv_a2'][l], writes=[a2])
        cx.dma('sp', g2[:, :, :], Wb['rwkv_g2'][l].rearrange("(kc p) n -> p kc n", p=128), writes=[g2])
        z = [cx.sb(st, [128, 1984], F32, name="z") for _ in range(2)]
        zp = [cx.sb(st, [128, 1984], F32, name="zp") for _ in range(2)]
        lo = [cx.sb(st, [128, 512], BF16, name="lo") for _ in range(2)]
        loT = [cx.sb(st, [128, 4, 128], BF16, name="loT") for _ in range(2)]
        o7 = [cx.sb(st, [128, 7, 512], F32, name="o7") for _ in range(2)]
        s8 = [cx.sb(st, [128, 8], F32, name="s8") for _ in range(2)]
        b8 = [cx.sb(st, [128, 8], F32, name="b8") for _ in range(2)]
        trp = TrPool(cx, st)
        pps = [[cx.ps(st, [128, 512], F32, name="pp") for _ in range(3)] for _ in range(2)]
        def body(tt):
            rows = slice(tt * 128, (tt + 1) * 128)
            zb, zpb, lob, loTb, o, s8b, b8b = z[tt % 2], zp[tt % 2], lo[tt % 2], loT[tt % 2], o7[tt % 2], s8[tt % 2], b8[tt % 2]
            cx.dma('sp', zb[:, :], P_d[rows, OFF_RWKV:OFF_RWKV + 1984], writes=[zb])
            if tt == 0:
                cx.op('pool', [], [zpb], lambda e: e.memset(zpb[0:1, :], 0.0))
                cx.dma('act', zpb[1:128, :], P_d[0:127, OFF_RWKV:OFF_RWKV + 1984], writes=[zpb])
            else:
                cx.dma('act', zpb[:, :], P_d[tt * 128 - 1:tt * 128 + 127, OFF_RWKV:OFF_RWKV + 1984], writes=[zpb])
            cx.op('pool', [zpb, zb], [zpb], lambda e: e.tensor_tensor(zpb[:, :], zpb[:, :], zb[:, :], ALU.subtract))
            cx.op('dve', [zpb, muB], [zpb], lambda e: e.tensor_tensor(zpb[:, :], zpb[:, :], muB[:, :], ALU.mult))
            cx.op('pool', [zpb, zb], [zb], lambda e: e.tensor_tensor(zb[:, :], zb[:, :], zpb[:, :], ALU.add))
            r_, k_, v_ = zb[:, 0:512], zb[:, 512:1024], zb[:, 1024:1536]
            cx.op('act', [zb], [lob], lambda e: e.activation(out=lob[:, 0:96], in_=zb[:, 1536:1632], func=AF.Tanh))
            cx.op('act', [zb], [lob], lambda e: e.copy(lob[:, 128:224], zb[:, 1632:1728]))
            cx.op('act', [zb], [lob], lambda e: e.activation(out=lob[:, 256:512], in_=zb[:, 1728:1984], func=AF.Sigmoid))
            trp.transpose_cols(lob, lambda j: lob[:, j * 128:j * 128 + 96], 2, loTb,
                               lambda j0, cnt: loTb[:96, j0:j0 + cnt, :], blkw=96)
            trp.transpose_cols(lob, lambda j: lob[:, 256 + j * 128:384 + j * 128], 2, loTb,
                               lambda j0, cnt: loTb[:, 2 + j0:2 + j0 + cnt, :])
            pu, pa, pg = pps[tt % 2]
            mm(cx, pu, pu[:, :], loTb[:96, 0, :], w2[:96, :], [loTb, w2], True, True)
            mm(cx, pa, pa[:, :], loTb[:96, 1, :], a2[:96, :], [loTb, a2], True, True)
            mm(cx, pg, pg[:, :], loTb[:, 2, :], g2[:, 0, :], [loTb, g2], True, False)
            mm(cx, pg, pg[:, :], loTb[:, 3, :], g2[:, 1, :], [loTb, g2], False, True)
            lw_, k2_, kn_, aa_, gg_, t1_, t2_ = [o[:, i, :] for i in range(7)]
            cx.op('dve', [pu, w0B], [o], lambda e: e.tensor_tensor(t1_, pu[:, :], w0B[:, :], ALU.add))
            cx.op('act', [o], [o], lambda e: e.activation(out=t1_, in_=t1_, func=AF.Sigmoid))
            cx.op('pool', [o], [o], lambda e: e.tensor_scalar(lw_, t1_, -0.6065306597126334, None, ALU.mult))
            cx.op('dve', [pa, a0B], [o], lambda e: e.tensor_tensor(t2_, pa[:, :], a0B[:, :], ALU.add))
            cx.op('act', [o], [o], lambda e: e.activation(out=aa_, in_=t2_, func=AF.Sigmoid))
            cx.op('act', [pg], [o], lambda e: e.copy(gg_, pg[:, :]))
            cx.op('dve', [zb, kkB], [o], lambda e: e.tensor_tensor(kn_, k_, kkB[:, :], ALU.mult))
            cx.op('pool', [o], [o], lambda e: e.tensor_tensor(t1_, kn_, kn_, ALU.mult))
            cx.op('dve', [o], [s8b], lambda e: e.tensor_reduce(out=s8b[:, :], in_=v3(t1_), axis=AX.X, op=ALU.add))
            cx.op('dve', [s8b], [s8b], lambda e: e.tensor_scalar(s8b[:, :], s8b[:, :], 1e-24, None, ALU.max))
            cx.op('pool', [s8b, cx.neghalf], [s8b],
                  lambda e: e.tensor_tensor(s8b[:, :], s8b[:, :], cx.neghalf[:, 0:1].to_broadcast([128, 8]), ALU.pow))
            cx.op('dve', [o, s8b], [o], lambda e: e.tensor_tensor(v3(kn_), v3(kn_), bc8(s8b[:, :]), ALU.mult))
            cx.op('dve', [o, kaB], [o],
                  lambda e: e.scalar_tensor_tensor(out=t2_, in0=aa_, scalar=-1.0, in1=kaB[:, :], op0=ALU.add, op1=ALU.mult))
            cx.op('pool', [o], [o], lambda e: e.tensor_scalar(t2_, t2_, 1.0, None, ALU.add))
            cx.op('dve', [o, zb], [o], lambda e: e.tensor_tensor(k2_, k_, t2_, ALU.mult))
            cx.op('pool', [o, zb], [o], lambda e: e.tensor_tensor(t1_, r_, k2_, ALU.mult))
            cx.op('dve', [o, rkB], [o], lambda e: e.tensor_tensor(t1_, t1_, rkB[:, :], ALU.mult))
            cx.op('dve', [o], [b8b], lambda e: e.tensor_reduce(out=b8b[:, :], in_=v3(t1_), axis=AX.X, op=ALU.add))
            cx.dma('pool', RW['rr'][rows, :], r_, reads=[zb])
            cx.dma('pool', RW['vv'][rows, :], v_, reads=[zb])
            cx.dma('pool', RW['lw'][rows, :], lw_, reads=[o])
            cx.dma('pool', RW['k2'][rows, :], k2_, reads=[o])
            cx.dma('pool', RW['kn'][rows, :], kn_, reads=[o])
            cx.dma('pool', RW['aa'][rows, :], aa_, reads=[o])
            cx.dma('pool', RW['gg'][rows, :], gg_, reads=[o])
            cx.dma('pool', RW['bc'][rows, :], b8b[:, :], reads=[b8b])
        run_tiles(cx, body, NT)
    cx.barrier()
    with contextlib.ExitStack() as st:
        rwm = cx.sb(st, [128, 384], F32, name="rwm")
        cx.dma('sp', rwm[:, :], C['rwm'][:, :], writes=[rwm])
        mask4 = cx.sb(st, [128, 512], F32, name="mask4")
        cx.op('pool', [rwm], [mask4], lambda e: e.tensor_copy(mask4[:, 0:256], rwm[:, 0:256]))
        cx.op('pool', [rwm], [mask4], lambda e: e.tensor_copy(mask4[:, 256:512], rwm[:, 0:256]))
        gwB = cx.sb(st, [128, 512], F32, name="gwB")
        gbB = cx.sb(st, [128, 512], F32, name="gbB")
        load_bcast_row(cx, 'sp', gwB, W['rwkv_gn_w'][l], 512)
        load_bcast_row(cx, 'sp', gbB, W['rwkv_gn_b'][l], 512)
        IN = [cx.sb(st, [128, 6, 512], F32, name="IN") for _ in range(2)]
        ELs = [cx.sb(st, [128, 3, 512], F32, name="EL") for _ in range(2)]
        TMs = [cx.sb(st, [128, 4, 512], F32, name="TM") for _ in range(2)]
        XTs = [cx.sb(st, [64, 8, 4, 128], F32, name="XT") for _ in range(2)]
        MMs = [cx.sb(st, [128, 8, 512], F32, name="MM") for _ in range(2)]
        XXs = [[cx.sb(st, [128, 8, 2, 128], F32, name="XX") for _ in range(2)] for _ in range(2)]
        NTs = [cx.sb(st, [128, 8, 128], F32, name="NT") for _ in range(2)]
        pcs = [cx.sb(st, [64, 8], F32, name="pc") for _ in range(2)]
        ST = cx.sb(st, [64, 8, 64], F32, name="ST")
        STs = cx.sb(st, [64, 8, 64], F32, name="STs")
        Yb = cx.sb(st, [128, 8, 64], F32, name="Yb")
        Ub = cx.sb(st, [128, 8, 64], F32, name="Ub")
        Ob = cx.sb(st, [128, 512], F32, name="Ob")
        G3 = [cx.sb(st, [128, 512], F32, name="G3") for _ in range(2)]
        b8 = [cx.sb(st, [128, 8], F32, name="b8") for _ in range(2)]
        m8 = cx.sb(st, [128, 8], F32, name="m8")
        r8 = cx.sb(st, [128, 8], F32, name="r8")
        t512 = cx.sb(st, [128, 512], F32, name="t512")
        yo = [cx.sb(st, [128, 512], F32, name="yo") for _ in range(2)]
        PTp = [cx.ps(st, [128, 512], F32, name="ptr") for _ in range(2)]
        PDp = [cx.ps(st, [128, 4, 128], F32, name="pD") for _ in range(3)]
        pY = cx.ps(st, [128, 512], F32, name="pY")
        pU = cx.ps(st, [128, 512], F32, name="pU")
        pO = cx.ps(st, [128, 512], F32, name="pO")
        cnt = {'pt': 0, 'pd': 0}

        def get_pt():
            cnt['pt'] += 1
            return PTp[cnt['pt'] % 2]

        def get_pd():
            cnt['pd'] += 1
            return PDp[cnt['pd'] % 3]

        cx.op('pool', [], [ST], lambda e: e.memset(ST[:, :, :], 0.0))
        MUs, MUi, MLs = rwm[:, 0:128], rwm[:, 128:256], rwm[:, 256:384]

        def pre(c):
            rows = slice(c * 128, (c + 1) * 128)
            I6, EL, TM, XT, MM_, XX, NTb, pc = IN[c % 2], ELs[c % 2], TMs[c % 2], XTs[c % 2], MMs[c % 2], XXs[c % 2], NTs[c % 2], pcs[c % 2]
            for i, nm in enumerate(names6):
                cx.dma('sp' if i % 2 == 0 else 'act', I6[:, i, :], RW[nm][rows, :], writes=[I6])
            rr, lw, k2, vv, kn, aa = [I6[:, i, :] for i in range(6)]

            def u_cumsum():
                pL = get_pt()
                mm(cx, pL, pL[:, :], MUi, lw, [rwm, I6], True, True)
                cx.op('act', [pL], [EL], lambda e: e.activation(out=EL[:, 0, :], in_=pL[:, :], func=AF.Exp))
                cx.op('act', [pL], [EL], lambda e: e.activation(out=EL[:, 1, :], in_=pL[:, :], func=AF.Exp, scale=-1.0))
                cx.op('dve', [pL, I6], [EL], lambda e: e.tensor_tensor(EL[:, 2, :], pL[:, :], lw, ALU.subtract))
            atomic(cx, u_cumsum)
            cx.op('act', [EL], [EL], lambda e: e.activation(out=EL[:, 2, :], in_=EL[:, 2, :], func=AF.Exp))
            cx.op('dve', [I6, EL], [TM],
                  lambda e: e.scalar_tensor_tensor(out=TM[:, 0, :], in0=kn, scalar=-1.0, in1=EL[:, 2, :], op0=ALU.mult, op1=ALU.mult))
            cx.op('pool', [I6, EL], [TM], lambda e: e.tensor_tensor(TM[:, 1, :], rr, EL[:, 0, :], ALU.mult))
            cx.op('dve', [I6], [TM], lambda e: e.tensor_tensor(TM[:, 2, :], kn, aa, ALU.mult))
            cx.op('dve', [TM, EL], [TM], lambda e: e.tensor_tensor(TM[:, 2, :], TM[:, 2, :], EL[:, 1, :], ALU.mult))
            cx.op('pool', [I6, EL], [TM], lambda e: e.tensor_tensor(TM[:, 3, :], k2, EL[:, 1, :], ALU.mult))

            def u_pc():
                ppc = get_pt()
                for h in range(8):
                    mm(cx, ppc, ppc[:64, h:h + 1], lw[:, h * 64:(h + 1) * 64], cx.ones_f[:, 0:1], [I6, cx.ones_f], True, True)
                cx.op('act', [ppc], [pc], lambda e: e.activation(out=pc[:, :], in_=ppc[:64, 0:8], func=AF.Exp))
            atomic(cx, u_pc)

            def u_tr(q, hh, k):
                pt = get_pt()
                for j in range(4):
                    h = hh * 4 + j
                    transp(cx, pt, pt[:64, j * 128:(j + 1) * 128], TM[:, q, h * 64:(h + 1) * 64], cx.identf[:, :], [TM, cx.identf])
                evac(cx, 'act' if k % 2 else 'dve', pt, pt[:64, :].rearrange("p (j t) -> p j t", j=4), XT,
                     fr(XT[:, hh * 4:(hh + 1) * 4, q, :]))
            k = 0
            for q in range(4):
                for hh in range(2):
                    k += 1
                    atomic(cx, lambda q=q, hh=hh, k=k: u_tr(q, hh, k))

            def u_setup(h):
                pA = get_pt()
                pd = get_pd()
                ar = XT[:, h, 0:2, :].rearrange("p q t -> p (q t)")
                mm(cx, pA, pA[:, 0:256], fr(XT[:, h, 2, :]), fr(ar), [XT], True, True)
                mm(cx, pA, pA[:, 256:512], fr(XT[:, h, 3, :]), fr(ar), [XT], True, True)
                mm(cx, pd, pd[:, 0, :], fr(XT[:, h, 0, :]), fr(XT[:, h, 2, :]), [XT], True, True)
                cx.op('dve', [pA, mask4], [MM_], lambda e: e.tensor_tensor(MM_[:, h, :], pA[:, :], mask4[:, :], ALU.mult))
                cx.op('dve', [pd, rwm], [XX[0]], lambda e: e.tensor_tensor(fr(XX[0][:, h, 1, :]), pd[:, 0, :], MLs, ALU.mult))
                cx.op('pool', [MM_], [XX[0]], lambda e: e.tensor_copy(fr(XX[0][:, h, 0, :]), MM_[:, h, 0:128]))
                cx.op('pool', [MM_, cx.identf], [NTb], lambda e: e.tensor_tensor(fr(NTb[:, h, :]), MM_[:, h, 0:128], cx.identf[:, :], ALU.add))
            for h in range(8):
                atomic(cx, lambda h=h: u_setup(h))

            def u_sq(lev, p):
                cur, nxt = XX[(lev - 1) % 2], XX[lev % 2]
                pd = get_pd()
                for j in range(2):
                    h = 2 * p + j
                    if lev < 6:
                        mm(cx, pd, pd[:, 2 * j, :], fr(cur[:, h, 1, :]), fr(cur[:, h, 0, :]), [cur], True, True)
                    mm(cx, pd, pd[:, 2 * j + 1, :], fr(cur[:, h, 0, :]), fr(cur[:, h, 1, :]), [cur], True, True)
                if lev < 6:
                    evac(cx, 'act' if p % 2 else 'dve', pd, pd[:, :, :], nxt,
                         fr(nxt[:, 2 * p:2 * p + 2, :, :].rearrange("p h q t -> p (h q) t")))
                else:
                    for j in range(2):
                        evac(cx, 'act' if j else 'dve', pd, pd[:, 2 * j + 1, :], nxt, fr(nxt[:, 2 * p + j, 1, :]))

            def u_n(lev, p):
                nxt = XX[lev % 2]
                pd = get_pd()
                for j in range(2):
                    h = 2 * p + j
                    mm(cx, pd, pd[:, j, :], fr(nxt[:, h, 1, :]), fr(NTb[:, h, :]), [nxt, NTb], True, True)
                cx.op('dve', [pd, NTb], [NTb],
                      lambda e: e.tensor_tensor(fr(NTb[:, 2 * p:2 * p + 2, :]), NTb[:, 2 * p:2 * p + 2, :], pd[:, 0:2, :], ALU.add))
            for lev in range(1, 7):
                for p in range(4):
                    atomic(cx, lambda lev=lev, p=p: u_sq(lev, p))
                for p in range(4):
                    atomic(cx, lambda lev=lev, p=p: u_n(lev, p))

        def seq(c):
            rows = slice(c * 128, (c + 1) * 128)
            I6, TM, XT, MM_, NTb, pc = IN[c % 2], TMs[c % 2], XTs[c % 2], MMs[c % 2], NTs[c % 2], pcs[c % 2]
            vv = I6[:, 3, :]
            cx.op('pool', [ST, pc], [STs],
                  lambda e: e.tensor_tensor(STs[:, :, :], ST[:, :, :], pc[:, :].unsqueeze(2).broadcast_to([64, 8, 64]), ALU.mult))
            for h in range(8):
                hs = slice(h * 64, (h + 1) * 64)
                mm(cx, pY, pY[:, hs], XT[:, h, 0, :], ST[:, h, :], [XT, ST], True, False)
                mm(cx, pY, pY[:, hs], MM_[:, h, 256:384], vv[:, hs], [MM_, I6], False, True)
            evac(cx, 'dve', pY, pY[:, 0:256], Yb, Yb[:, 0:4, :].rearrange("p h c -> p (h c)"))
            evac(cx, 'act', pY, pY[:, 256:512], Yb, Yb[:, 4:8, :].rearrange("p h c -> p (h c)"))
            for h in range(8):
                hs = slice(h * 64, (h + 1) * 64)
                mm(cx, pU, pU[:, hs], NTb[:, h, :], Yb[:, h, :], [NTb, Yb], True, True)
            evac(cx, 'dve', pU, pU[:, 0:256], Ub, Ub[:, 0:4, :].rearrange("p h c -> p (h c)"))
            evac(cx, 'act', pU, pU[:, 256:512], Ub, Ub[:, 4:8, :].rearrange("p h c -> p (h c)"))
            for h in range(8):
                hs = slice(h * 64, (h + 1) * 64)
                mm(cx, pY, pY[:64, hs], TM[:, 2, hs], Ub[:, h, :], [TM, Ub], True, False)
                mm(cx, pY, pY[:64, hs], TM[:, 3, hs], vv[:, hs], [TM, I6], False, True)
            for h in range(8):
                hs = slice(h * 64, (h + 1) * 64)
                mm(cx, pO, pO[:, hs], XT[:, h, 1, :], ST[:, h, :], [XT, ST], True, False)
                mm(cx, pO, pO[:, hs], MM_[:, h, 128:256], Ub[:, h, :], [MM_, Ub], False, False)
                mm(cx, pO, pO[:, hs], MM_[:, h, 384:512], vv[:, hs], [MM_, I6], False, True)
            cx.op('dve', [pY, pc], [ST],
                  lambda e: e.tensor_tensor(ST[:, :, :], pY[:64, :].rearrange("p (h c) -> p h c", h=8),
                                            pc[:, :].unsqueeze(2).broadcast_to([64, 8, 64]), ALU.mult))
            cx.op('dve', [ST, STs], [ST], lambda e: e.tensor_tensor(ST[:, :, :], ST[:, :, :], STs[:, :, :], ALU.add))
            evac(cx, 'act', pO, pO[:, :], Ob, Ob[:, :])
            g3, b8b, y = G3[c % 2], b8[c % 2], yo[c % 2]
            cx.dma('sp', g3[:, :], RW['gg'][rows, :], writes=[g3])
            cx.dma('act', b8b[:, :], RW['bc'][rows, :], writes=[b8b])
            cx.op('dve', [Ob], [m8], lambda e: e.tensor_reduce(out=m8[:, :], in_=v3(Ob[:, :]), axis=AX.X, op=ALU.add))
            cx.op('dve', [m8], [m8], lambda e: e.tensor_scalar(m8[:, :], m8[:, :], 1.0 / 64, None, ALU.mult))
            cx.op('dve', [Ob, m8], [Ob], lambda e: e.tensor_tensor(v3(Ob[:, :]), v3(Ob[:, :]), bc8(m8[:, :]), ALU.subtract))
            cx.op('pool', [Ob], [t512], lambda e: e.tensor_tensor(t512[:, :], Ob[:, :], Ob[:, :], ALU.mult))
            cx.op('dve', [t512], [r8], lambda e: e.tensor_reduce(out=r8[:, :], in_=v3(t512[:, :]), axis=AX.X, op=ALU.add))
            cx.op('dve', [r8], [r8], lambda e: e.tensor_scalar(r8[:, :], r8[:, :], 1.0 / 64, 64e-5, ALU.mult, ALU.add))
            cx.op('pool', [r8, cx.neghalf], [r8],
                  lambda e: e.tensor_tensor(r8[:, :], r8[:, :], cx.neghalf[:, 0:1].to_broadcast([128, 8]), ALU.pow))
            cx.op('dve', [Ob, r8], [y], lambda e: e.tensor_tensor(v3(y[:, :]), v3(Ob[:, :]), bc8(r8[:, :]), ALU.mult))
            cx.op('pool', [y, gwB], [y], lambda e: e.tensor_tensor(y[:, :], y[:, :], gwB[:, :], ALU.mult))
            cx.op('pool', [y, gbB], [y], lambda e: e.tensor_tensor(y[:, :], y[:, :], gbB[:, :], ALU.add))
            cx.op('dve', [I6, b8b], [t512], lambda e: e.tensor_tensor(v3(t512[:, :]), v3(vv), bc8(b8b[:, :]), ALU.mult))
            cx.op('pool', [y, t512], [y], lambda e: e.tensor_tensor(y[:, :], y[:, :], t512[:, :], ALU.add))
            cx.op('dve', [y, g3], [y], lambda e: e.tensor_tensor(y[:, :], y[:, :], g3[:, :], ALU.mult))
            cx.dma('pool', Y_d[rows, :], y[:, :], reads=[y])

        for c0 in range(0, NT, 2):
            cs = list(range(c0, min(NT, c0 + 2)))
            lists = []
            for c in cs:
                cx._rec = []
                pre(c)
                lists.append(cx._rec)
                cx._rec = None
            for i in range(max(len(l_) for l_ in lists)):
                for l_ in lists:
                    if i < len(l_):
                        l_[i]()
            for c in cs:
                seq(c)
    cx.barrier()


NSA_SCALE = 128 ** -0.5
NSA_BIG = 30000.0
NSA_STOP = 0


def phase_nsa(cx, S, l, P_d, W, Wb, C, Youts, scr):
    NT = S // 128
    G = min(512, S)
    NG = G // 128
    QW = min(512, S)
    NJ = QW // 128
    Nc = (S - 32) // 16 + 1
    NKB = (Nc + 127) // 128
    N = scr['nsa']
    with contextlib.ExitStack() as st:
        pn = [cx.sb(st, [128, 1292], F32, name="pn") for _ in range(2)]
        cs_t = [cx.sb(st, [128, 32], F32, name="cs") for _ in range(2)]
        ro = [cx.sb(st, [128, 10, 32], F32, name="ro") for _ in range(2)]
        t1s = [cx.sb(st, [128, 10, 16], F32, name="t1") for _ in range(2)]
        t2s = [cx.sb(st, [128, 10, 16], F32, name="t2") for _ in range(2)]
        fb = [cx.sb(st, [128, 8, 128], BF16, name="fb") for _ in range(2)]
        va = [cx.sb(st, [128, 2, 129], BF16, name="va") for _ in range(2)]
        sqs = [cx.sb(st, [128, 6, 128], F32, name="sq") for _ in range(2)]
        n6 = [cx.sb(st, [128, 6], F32, name="n6") for _ in range(2)]
        nqb = [cx.sb(st, [128, 4], BF16, name="nqb") for _ in range(2)]
        gt = [cx.sb(st, [128, 12], F32, name="gt") for _ in range(2)]
        kmx = cx.sb(st, [128, 2], F32, name="kmx")
        gA = cx.sb(st, [128, 8, G], BF16, name="gA")
        gN = cx.sb(st, [1, 4, G], BF16, name="gN")
        trp = TrPool(cx, st)
        trf = TrPool(cx, st, n=1, dtype=F32)
        cx.op('pool', [], [kmx], lambda e: e.memset(kmx[:, :], 0.0))
        def body(tt):
            t1, t2, sq = t1s[tt % 2], t2s[tt % 2], sqs[tt % 2]
            tl = tt % NG
            rows = slice(tt * 128, (tt + 1) * 128)
            p, c_t, r, f, v, n6b, nq_, g_ = pn[tt % 2], cs_t[tt % 2], ro[tt % 2], fb[tt % 2], va[tt % 2], n6[tt % 2], nqb[tt % 2], gt[tt % 2]
            cx.dma('sp', p[:, :], P_d[rows, OFF_NSA:OFF_NSA + 1292], writes=[p])
            cx.dma('act', c_t[:, 0:16], C['nsa_cos'][rows, :], writes=[c_t])
            cx.dma('act', c_t[:, 16:32], C['nsa_sin'][rows, :], writes=[c_t])
            blk = p[:, 0:1280].rearrange("p (b c) -> p b c", b=10)
            cb = c_t[:, 0:16].unsqueeze(1).broadcast_to([128, 10, 16])
            sb_ = c_t[:, 16:32].unsqueeze(1).broadcast_to([128, 10, 16])
            rope_tm(cx, p, blk[:, :, 0:16], blk[:, :, 16:32], [c_t, cb], [c_t, sb_],
                    r, r[:, :, 0:16], r[:, :, 16:32], [(t1, t1[:, :, :]), (t2, t2[:, :, :])])
            cx.op('act', [p], [f], lambda e: e.activation(out=f[:, 0:4, 32:128], in_=blk[:, 0:4, 32:128], func=AF.Copy, scale=NSA_SCALE))
            cx.op('act', [r], [f], lambda e: e.activation(out=f[:, 0:4, 0:32], in_=r[:, 0:4, :], func=AF.Copy, scale=NSA_SCALE))
            for dst, src in ((4, 4), (6, 6), (7, 8)):
                cx.op('pool', [p], [f], lambda e, dst=dst, src=src: e.tensor_copy(f[:, dst, 32:128], blk[:, src, 32:128]))
                cx.op('pool', [r], [f], lambda e, dst=dst, src=src: e.tensor_copy(f[:, dst, 0:32], r[:, src, :]))
            cx.op('pool', [p], [f], lambda e: e.tensor_copy(f[:, 5, :], blk[:, 5, :]))
            cx.op('pool', [p], [v], lambda e: e.tensor_copy(v[:, 0, 0:128], blk[:, 7, :]))
            cx.op('pool', [p], [v], lambda e: e.tensor_copy(v[:, 1, 0:128], blk[:, 9, :]))
            cx.op('pool', [], [v], lambda e: e.memset(v[:, :, 128:129], 1.0))
            cx.op('dve', [p], [sq], lambda e: e.tensor_tensor(sq[:, 0:4, :], blk[:, 0:4, :], blk[:, 0:4, :], ALU.mult))
            cx.op('dve', [p], [sq], lambda e: e.tensor_tensor(sq[:, 4, :], blk[:, 6, :], blk[:, 6, :], ALU.mult))
            cx.op('dve', [p], [sq], lambda e: e.tensor_tensor(sq[:, 5, :], blk[:, 8, :], blk[:, 8, :], ALU.mult))
            cx.op('dve', [sq], [n6b], lambda e: e.tensor_reduce(out=n6b[:, :], in_=sq[:, :, :], axis=AX.X, op=ALU.add))
            cx.op('dve', [n6b, kmx], [kmx], lambda e: e.tensor_tensor(kmx[:, :], kmx[:, :], n6b[:, 4:6], ALU.max))
            cx.op('act', [n6b], [n6b], lambda e: e.activation(out=n6b[:, 0:4], in_=n6b[:, 0:4], func=AF.Sqrt))
            cx.op('dve', [n6b], [nq_], lambda e: e.tensor_scalar(nq_[:, :], n6b[:, 0:4], -NSA_SCALE, None, ALU.mult))
            cx.dma('sp', g_[:, :], P_d[rows, OFF_NSA + 1280:OFF_NSA + 1292], writes=[g_])
            cx.op('act', [g_], [g_], lambda e: e.activation(out=g_[:, :], in_=g_[:, :], func=AF.Sigmoid))
            cx.dma('pool', N['ng'][rows, :], g_[:, :], reads=[g_])
            trp.transpose_cols(f, lambda j: f[:, j, :], 8, gA, lambda j0, cnt: gA[:, j0:j0 + cnt, tl * 128:(tl + 1) * 128])
            trp.transpose_cols(nq_, lambda j: nq_[:, j:j + 1], 4, gN,
                               lambda j0, cnt: gN[0:1, j0:j0 + cnt, tl * 128:(tl + 1) * 128], blkw=1)
            cx.dma('pool', N['vsa'][rows, :], v[:, 0, :], reads=[v])
            cx.dma('pool', N['vwa'][rows, :], v[:, 1, :], reads=[v])
            if tl == NG - 1:
                def post():
                    g0 = (tt // NG) * G
                    cx.dma('pool', N['qT'][:, :, g0:g0 + G].rearrange("h d s -> d h s"), gA[:, 0:4, :], reads=[gA])
                    for j, nm in ((4, 'kcT'), (5, 'vcT'), (6, 'ksT'), (7, 'kwT')):
                        cx.dma('pool', N[nm][:, g0:g0 + G], gA[:, j, :], reads=[gA])
                    cx.dma('pool', N['nq'][:, g0:g0 + G].rearrange("(o h) s -> o h s", o=1), gN[0:1, :, :], reads=[gN])
                return post
        run_tiles(cx, body, NT)
        kms = cx.sb(st, [128, 1], F32, name="kms")
        kmw = cx.sb(st, [128, 1], F32, name="kmw")
        k1 = cx.sb(st, [128, 1], F32, name="k1")
        rowsb = cx.sb(st, [1, 2, 128], BF16, name="rowsb")
        for i, dstc in enumerate((kms, kmw)):
            cx.op('pool', [kmx], [k1], lambda e: e.tensor_copy(k1[:, :], kmx[:, i:i + 1]))
            bcast_scalar_max(cx, st, trf, k1, dstc)
            cx.op('act', [dstc], [dstc], lambda e: e.activation(out=dstc[:, :], in_=dstc[:, :], func=AF.Sqrt))
            cx.op('dve', [dstc], [rowsb], lambda e: e.tensor_copy(rowsb[0:1, i, :], dstc[0:1, 0:1].to_broadcast([1, 128])))
        cx.dma('pool', N['krow'].rearrange("a b -> (a b)").rearrange("(o n) -> o n", o=1), rowsb[0:1, :, :].rearrange("o a b -> o (a b)"), reads=[rowsb])
    cx.barrier()
    if NSA_STOP == 1:
        return
    with contextlib.ExitStack() as st:
        KC = cx.sb(st, [128, 256], BF16, name="KC")
        VCA = cx.sb(st, [128, 2, 193], BF16, name="VCA")
        krow = cx.sb(st, [128, 3, 128], BF16, name="krow")
        cx.op('pool', [], [krow], lambda e: e.memset(krow[:, :, :], 0.0))
        cx.dma('sp', krow[0:1, 0:2, :].rearrange("o a b -> o (a b)"), N['krow'].rearrange("a b -> (a b)").rearrange("(o n) -> o n", o=1), writes=[krow])
        cx.dma('sp', VCA[:, :, 129:193], C['cover'].rearrange("(kb p) j -> p kb j", p=128), writes=[VCA])
        cx.op('pool', [], [VCA], lambda e: e.memset(VCA[:, :, 0:129], 0.0))
        cx.op('pool', [], [VCA], lambda e: e.memset(VCA[:, :, 128:129], 1.0))
        cx.op('pool', [], [KC], lambda e: e.memset(KC[:, :], 0.0))
        with contextlib.ExitStack() as s2:
            pm = [cx.ps(s2, [128, 512], F32, name="pm") for _ in range(2)]
            xT = [cx.sb(s2, [128, S], BF16, name="xT") for _ in range(2)]
            cx.dma('sp', xT[0][:, :], N['kcT'][:, :], writes=[xT[0]])
            cx.dma('act', xT[1][:, :], N['vcT'][:, :], writes=[xT[1]])
            w1 = [cx.sb(s2, [128, 32, 128], BF16, name="w1") for _ in range(2)]
            w2 = [cx.sb(s2, [128, 128], BF16, name="w2") for _ in range(2)]
            posf = cx.sb(s2, [32, 2, 128], F32, name="posf")
            posb = cx.sb(s2, [32, 2, 128], BF16, name="posb")
            posT = cx.sb(s2, [128, 2, 32], BF16, name="posT")
            bias = cx.sb(s2, [128, 2], F32, name="bias")
            xs = cx.sb(s2, [128, 256], F32, name="xs")
            x2 = cx.sb(s2, [128, 256], F32, name="x2")
            hid = [cx.sb(s2, [128, 256], BF16, name="hid") for _ in range(2)]
            ksq = cx.sb(s2, [128, 256], BF16, name="ksq")
            one = cx.sb(s2, [1, 2], F32, name="one")
            trp = TrPool(cx, s2, n=1)
            for z in range(2):
                cx.dma('sp', w1[z][:, :, :], Wb['nsa_cmp_w1'][l][z].rearrange("(l d) e -> d l e", d=128), writes=[w1[z]])
                cx.dma('sp', w2[z][:, :], Wb['nsa_cmp_w2'][l][z], writes=[w2[z]])
            cx.dma('sp', posf[:, :, :], W['nsa_cmp_pos'][l].rearrange("z l d -> l z d"), writes=[posf])
            cx.op('dve', [posf], [posb], lambda e: e.tensor_copy(posb[:, :, :], posf[:, :, :]))
            trp.transpose_cols(posb, lambda j: posb[:, j, :], 2, posT, lambda j0, cnt: posT[:, j0:j0 + cnt, :], rows=32)
            for z in range(2):
                pb, ph = pm
                for ll in range(32):
                    mm(cx, pb, pb[:, z:z + 1], w1[z][:, ll, :], posT[:, z, ll:ll + 1], [w1[z], posT], ll == 0, ll == 31)
                evac(cx, 'dve', pb, pb[:, z:z + 1], bias, bias[:, z:z + 1])
                for ll in range(32):
                    mm(cx, ph, ph[:, :Nc], w1[z][:, ll, :], xT[z][:, ll:ll + 16 * (Nc - 1) + 1:16], [w1[z], xT[z]], ll == 0, ll == 31)
                cx.op('act', [ph, bias], [xs], lambda e: e.activation(out=xs[:, :Nc], in_=ph[:, :Nc], func=AF.Identity, bias=bias[:, z:z + 1]))
                cx.op('dve', [xs], [x2], lambda e: e.tensor_tensor(x2[:, :Nc], xs[:, :Nc], xs[:, :Nc], ALU.mult))
                cx.op('dve', [x2], [x2], lambda e: e.tensor_scalar(x2[:, :Nc], x2[:, :Nc], 0.044715, 1.0, ALU.mult, ALU.add))
                cx.op('dve', [x2, xs], [x2], lambda e: e.tensor_tensor(x2[:, :Nc], x2[:, :Nc], xs[:, :Nc], ALU.mult))
                cx.op('act', [x2], [x2], lambda e: e.activation(out=x2[:, :Nc], in_=x2[:, :Nc], func=AF.Tanh, scale=0.7978845608028654))
                cx.op('dve', [x2], [x2], lambda e: e.tensor_scalar(x2[:, :Nc], x2[:, :Nc], 1.0, 0.5, ALU.add, ALU.mult))
                cx.op('dve', [x2, xs], [hid[z]], lambda e: e.tensor_tensor(hid[z][:, :Nc], x2[:, :Nc], xs[:, :Nc], ALU.mult))
            pk = pm[0]
            mm(cx, pk, pk[:, :Nc], w2[0][:, :], hid[0][:, :Nc], [w2[0], hid[0]], True, True)
            evac(cx, 'act', pk, pk[:, :Nc], KC, KC[:, :Nc])
            cx.op('act', [pk], [ksq], lambda e: e.activation(out=ksq[:, :Nc], in_=pk[:, :Nc], func=AF.Square))
            pr = pm[1]
            mm(cx, pr, pr[0:1, :Nc], cx.ones_bf[:, 0:1], ksq[:, :Nc], [cx.ones_bf, ksq], True, True)
            cx.op('dve', [pr], [one], lambda e: e.tensor_reduce(out=one[0:1, 0:1], in_=pr[0:1, :Nc], axis=AX.X, op=ALU.max))
            cx.op('act', [one], [one], lambda e: e.activation(out=one[0:1, 0:1], in_=one[0:1, 0:1], func=AF.Sqrt))
            cx.op('dve', [one], [krow], lambda e: e.tensor_scalar(krow[0:1, 2, :], one[0:1, 0:1].to_broadcast([1, 128]), 1.02, None, ALU.mult))
            for kb in range(NKB):
                nk = min(128, Nc - kb * 128)
                pv = pm[kb % 2]
                mm(cx, pv, pv[:nk, 0:128], hid[1][:, kb * 128:kb * 128 + nk], w2[1][:, :], [hid[1], w2[1]], True, True)
                evac(cx, 'dve', pv, pv[:nk, 0:128], VCA, VCA[:nk, kb, 0:128])
        cx.barrier()
        if NSA_STOP == 2:
            return
        res = AttnRes(cx, st, 193)
        qT = [cx.sb(st, [128, S], BF16, name="qT") for _ in range(4)]
        nq = [cx.sb(st, [128, S], BF16, name="nq") for _ in range(4)]
        for h in range(4):
            cx.op('pool', [], [nq[h]], lambda e: e.memset(nq[h][:, :], 0.0))
            cx.dma('sp', qT[h][:, :], N['qT'][h], writes=[qT[h]])
            cx.dma('act', nq[h][0:1, :], N['nq'][h:h + 1, :], writes=[nq[h]])
        ksT = cx.sb(st, [128, S], BF16, name="ksT")
        kwT = cx.sb(st, [128, S], BF16, name="kwT")
        cx.dma('sp', ksT[:, :], N['ksT'][:, :], writes=[ksT])
        cx.dma('act', kwT[:, :], N['kwT'][:, :], writes=[kwT])
        vsa = cx.sb(st, [128, NT, 129], BF16, name="vsa")
        vwa = cx.sb(st, [128, NT, 129], BF16, name="vwa")
        cx.dma('sp', vsa[:, :, :], N['vsa'].rearrange("(t p) c -> p t c", p=128), writes=[vsa])
        cx.dma('act', vwa[:, :, :], N['vwa'].rearrange("(t p) c -> p t c", p=128), writes=[vwa])
        cmask = cx.sb(st, [128, 4, 512], BF16, name="cmask")
        wmask = cx.sb(st, [128, 4, 512], BF16, name="wmask")
        cx.dma('sp', cmask[:, :, :], C['cmask'].rearrange("i k q -> k i q"), writes=[cmask])
        cx.dma('sp', wmask[:, :, :], C['wmask'].rearrange("i k q -> k i q"), writes=[wmask])
        cmpm = cx.sb(st, [128, 2, S], BF16, name="cmpm")
        cx.dma('sp', cmpm[:, :, :], C['cmpmask'].rearrange("kb p q -> p kb q"), writes=[cmpm])
        Em = cx.sb(st, [64, S], BF16, name="Em")
        cx.dma('sp', Em[:, :], C['Emat'][:, :], writes=[Em])
        gts = cx.sb(st, [128, NT, 12], F32, name="gts")
        cx.dma('sp', gts[:, :, :], N['ng'].rearrange("(t p) c -> p t c", p=128), writes=[gts])
        imp = cx.sb(st, [128, 4, 64], F32, name="imp")
        fbt = [cx.sb(st, [128, 64], F32, name="fbt") for _ in range(2)]
        m8 = cx.sb(st, [128, 16], F32, name="m8")
        val2 = cx.sb(st, [128, 64], F32, name="val2")
        selb = cx.sb(st, [128, 64], BF16, name="selb")
        selT = cx.sb(st, [64, 512], BF16, name="selT")
        rc = [cx.sb(st, [128, 1], F32, name="rc") for _ in range(2)]
        rg = [cx.sb(st, [128, 1], F32, name="rg") for _ in range(2)]
        ot = [cx.sb(st, [128, 128], F32, name="ot") for _ in range(3)]
        it = [cx.sb(st, [128, 64], F32, name="it") for _ in range(2)]
        trp2 = TrPool(cx, st, n=1)
        cnt = [0]

        def make_epi(branch, h):
            def epi(QB, j, accb, acc_ap):
                k = cnt[0]
                cnt[0] += 1
                tt = QB * NJ + j
                r, rgb, o = rc[k % 2], rg[k % 2], ot[k % 3]
                cx.op('dve', [accb], [r], lambda e: e.tensor_scalar(r[:, :], acc_ap[:, 128:129], 1e-30, None, ALU.add))
                cx.op('dve', [r], [r], lambda e: e.reciprocal(r[:, :], r[:, :]))
                cx.op('dve', [r, gts], [rgb], lambda e: e.tensor_tensor(rgb[:, :], r[:, :], gts[:, tt, h * 3 + branch:h * 3 + branch + 1], ALU.mult))
                cx.op('act', [accb, rgb], [o], lambda e: e.activation(out=o[:, :], in_=acc_ap[:, 0:128], func=AF.Copy, scale=rgb[:, 0:1]))
                cx.dma('pool', Youts[branch][tt * 128:(tt + 1) * 128, h * 128:(h + 1) * 128], o[:, :], reads=[o])
                if branch == 0:
                    if h == 0:
                        cx.op('dve', [accb, r], [imp], lambda e: e.tensor_scalar(imp[:, j, :], acc_ap[:, 129:193], r[:, 0:1], None, ALU.mult))
                    else:
                        i_ = it[k % 2]
                        cx.op('dve', [accb, r], [i_], lambda e: e.tensor_scalar(i_[:, :], acc_ap[:, 129:193], r[:, 0:1], None, ALU.mult))
                        cx.op('pool', [i_, imp], [imp], lambda e: e.tensor_tensor(imp[:, j, :], imp[:, j, :], i_[:, :], ALU.add))
            return epi

        def cmp_blocks(QB):
            q0 = QB * QW
            return [dict(kb=kb, k0=kb * 128, nk=min(128, Nc - kb * 128),
                         mask=(cmpm, cmpm[:min(128, Nc - kb * 128), kb, q0:q0 + QW])) for kb in range(NKB)]

        def slc_blocks(QB):
            bl = causal_blocks(QB, QW, cmask)
            for b in bl:
                b['extra'] = ([Em], Em[:, b['k0']:b['k0'] + 128], [selT], selT[:, :QW])
            return bl

        def win_blocks(QB):
            out = []
            nd = QW // 128
            for kb in range(max(0, nd * QB - 4), nd * (QB + 1)):
                i = kb - nd * QB
                m = (cmask, cmask[:, i, :QW]) if i >= 0 else (wmask, wmask[:, i + 4, :QW])
                out.append(dict(kb=kb, k0=kb * 128, nk=128, mask=m))
            return out

        for QB in range(S // QW):
            for h in range(4):
                attn_core(cx, res, S, [(qT[h], qT[h][:, :]), (nq[h], nq[h][:, :])],
                          [(kwT, kwT[:, :]), (krow, lambda k0, nk: krow[:, 1, :nk])],
                          lambda kb, nk: (vwa, vwa[:nk, kb, :]), 129, win_blocks, make_epi(2, h), qbs=[QB])
            if NSA_STOP == 3:
                break
            for h in range(4 if NSA_STOP != 8 else 0):
                attn_core(cx, res, S, [(qT[h], qT[h][:, :]), (nq[h], nq[h][:, :])],
                          [(KC, KC[:, :]), (krow, lambda k0, nk: krow[:, 2, :nk])],
                          lambda kb, nk: (VCA, VCA[:nk, kb, :]), 193, cmp_blocks, make_epi(0, h), qbs=[QB])
            if NSA_STOP == 4:
                break
            for j in range(NJ if NSA_STOP != 8 else 0):
                tt = QB * NJ + j
                f_ = fbt[j % 2]
                cx.dma('sp', f_[:, :], C['fbias'][tt * 128:(tt + 1) * 128, :], writes=[f_])
                cx.op('dve', [imp, f_], [f_], lambda e: e.tensor_tensor(f_[:, :], f_[:, :], imp[:, j, :], ALU.add))
                cx.op('dve', [f_], [m8], lambda e: e.max(out=m8[:, 0:8], in_=f_[:, :]))
                cx.op('dve', [f_, m8], [val2], lambda e: e.match_replace(out=val2[:, :], in_to_replace=m8[:, 0:8], in_values=f_[:, :], imm_value=-3.0e38))
                cx.op('dve', [val2], [m8], lambda e: e.max(out=m8[:, 8:16], in_=val2[:, :]))
                cx.op('dve', [f_, m8], [val2], lambda e: e.tensor_scalar(val2[:, :], f_[:, :], m8[:, 15:16], None, ALU.is_ge))
                cx.op('dve', [val2], [selb], lambda e: e.tensor_scalar(selb[:, :], val2[:, :], -1.0, NSA_BIG, ALU.add, ALU.mult))
                trp2.transpose_cols(selb, lambda jj: selb[:, :], 1, selT, lambda j0, c_, j=j: selT[:64, j * 128:(j + 1) * 128].unsqueeze(1), blkw=64)
            if NSA_STOP == 5:
                break
            for h in range(4 if NSA_STOP not in (8, 9) else 0):
                attn_core(cx, res, S, [(qT[h], qT[h][:, :]), (nq[h], nq[h][:, :])],
                          [(ksT, ksT[:, :]), (krow, lambda k0, nk: krow[:, 0, :nk])],
                          lambda kb, nk: (vsa, vsa[:nk, kb, :]), 129, slc_blocks, make_epi(1, h), qbs=[QB])
    cx.barrier()


S_FULL = 4096
ENABLE = {'mla': True, 'nsa': True, 'rwkv': True, 'ret': True}


def phase_zero(cx, S, Y_d):
    with contextlib.ExitStack() as st:
        z = cx.sb(st, [128, 512], F32, name="z")
        cx.op('pool', [], [z], lambda e: e.memset(z[:, :], 0.0))
        for tt in range(S // 128):
            cx.dma('sp', Y_d[tt * 128:(tt + 1) * 128, :], z[:, :], reads=[z])
    cx.barrier()


def host_consts(S):
    import ml_dtypes
    bf = ml_dtypes.bfloat16
    c = {}
    c['ident'] = np.eye(128, dtype=np.float32).astype(bf)
    kk = np.arange(128)[:, None]
    qq = np.arange(512)[None, :]
    c['cmask'] = np.stack([(128 * i + kk <= qq) for i in range(4)]).astype(np.float32).astype(bf)
    t = np.arange(S, dtype=np.float32)[:, None]

    def tables(inv):
        ang = (t * inv[None, :].astype(np.float32)).astype(np.float32)
        return np.cos(ang).astype(np.float32), np.sin(ang).astype(np.float32)

    inv_mla = (np.float32(500000.0) ** (-np.arange(0, 64, 2, dtype=np.float32) / np.float32(64))).astype(np.float32)
    inv_nsa = (np.float32(500000.0) ** (-np.arange(0, 32, 2, dtype=np.float32) / np.float32(32))).astype(np.float32)
    inv_ret = (np.float32(10000.0) ** (-np.linspace(0.0, 1.0, 32, dtype=np.float32))).astype(np.float32)
    c['mla_cos'], c['mla_sin'] = tables(inv_mla)
    c['nsa_cos'], c['nsa_sin'] = tables(inv_nsa)
    c['ret_cos'], c['ret_sin'] = tables(inv_ret)
    kf = kk.astype(np.float64)
    qf = qq.astype(np.float64)
    gdec = np.zeros((4, 5, 128, 512), np.float32)
    for h in range(4):
        lg = np.log1p(-2.0 ** (-5 - h))
        gdec[h, 0] = np.exp((qf - kf) * lg)
        for i in range(4):
            d = qf - kf - 128 * i
            gdec[h, 1 + i] = np.where(d >= 0, np.exp(np.maximum(d, 0) * lg), 0.0)
    c['gdec'] = gdec
    si = np.arange(128)[:, None]
    ti = np.arange(128)[None, :]
    c['wmask'] = (1.0 - c['cmask'].astype(np.float32)).astype(bf)
    Nc = (S - 32) // 16 + 1
    n = np.arange(256)
    q = np.arange(S)
    cm = ((16 * n[:, None] + 31 <= q[None, :]) & (n[:, None] < Nc)).astype(np.float32)
    c['cmpmask'] = cm.reshape(2, 128, S).astype(bf)
    nblk = S // 64
    jb = np.arange(64)
    cstart = 16 * n
    cend = cstart + 31
    cover = ((cstart[:, None] <= jb[None, :] * 64 + 63) & (cend[:, None] >= jb[None, :] * 64) & (n[:, None] < Nc)
             & (jb[None, :] < nblk)).astype(np.float32)
    c['cover'] = cover.astype(bf)
    c['Emat'] = (q[None, :] // 64 == jb[:, None]).astype(np.float32).astype(bf)
    cur = q // 64
    forced = (jb[None, :] == 0) | (jb[None, :] == cur[:, None]) | (jb[None, :] == cur[:, None] - 1)
    visible = (jb[None, :] <= cur[:, None]) & (jb[None, :] < nblk)
    c['fbias'] = np.where(visible, 1000.0 * forced, -1.0e30).astype(np.float32)
    c['rwm'] = np.concatenate([(si < ti), (si <= ti), (si > ti)], axis=1).astype(np.float32)
    return c


CONST_SPECS = {'ident': ([128, 128], BF16), 'cmask': ([4, 128, 512], BF16),
               'mla_cos': (None, F32), 'mla_sin': (None, F32), 'nsa_cos': (None, F32), 'nsa_sin': (None, F32),
               'ret_cos': (None, F32), 'ret_sin': (None, F32), 'gdec': ([4, 5, 128, 512], F32), 'rwm': ([128, 384], F32), 'wmask': ([4, 128, 512], BF16), 'cmpmask': ('cmp', BF16),
               'cover': ([256, 64], BF16), 'Emat': ('E', BF16), 'fbias': ('fb', F32)}

WEIGHT_SHAPES = {
    'w_in': [DEPTH, D_MODEL, IN_WIDTH], 'w_branch': [DEPTH, 4, BW, D_MODEL], 'w_out': [DEPTH, D_MODEL, D_MODEL],
    'w_up': [DEPTH, D_MODEL, D_FF], 'w_down': [DEPTH, D_FF, D_MODEL], 'norm_gains': [DEPTH, 4, D_MODEL],
    'mla_g_q': [DEPTH, 384], 'mla_g_kv': [DEPTH, 128], 'mla_w_uq': [DEPTH, 384, 768], 'mla_w_ukv': [DEPTH, 128, 1024],
    'nsa_cmp_pos': [DEPTH, 2, 32, 128], 'nsa_cmp_w1': [DEPTH, 2, 4096, 128], 'nsa_cmp_w2': [DEPTH, 2, 128, 128],
    'rwkv_mu': [DEPTH, 1984], 'rwkv_w0': [DEPTH, 512], 'rwkv_w2': [DEPTH, 96, 512], 'rwkv_a0': [DEPTH, 512],
    'rwkv_a2': [DEPTH, 96, 512], 'rwkv_g2': [DEPTH, 256, 512], 'rwkv_k_k': [DEPTH, 512], 'rwkv_k_a': [DEPTH, 512],
    'rwkv_r_k': [DEPTH, 8, 64], 'rwkv_gn_w': [DEPTH, 512], 'rwkv_gn_b': [DEPTH, 512],
}
CAST = ['w_in', 'w_branch', 'w_out', 'w_up', 'w_down', 'mla_w_uq', 'mla_w_ukv', 'nsa_cmp_w1', 'nsa_cmp_w2',
        'rwkv_w2', 'rwkv_a2', 'rwkv_g2']


def build_program(S, depth=DEPTH):
    cx = Ctx()
    x_d = cx.dram("x", [S, D_MODEL], F32, kind="ExternalInput")
    W = {k: cx.dram(k, shp, F32, kind="ExternalInput") for k, shp in WEIGHT_SHAPES.items()}
    C = {}
    for k, (shp, dt) in CONST_SPECS.items():
        if shp is None:
            shp = [S, 16 if k.startswith('nsa') else 32]
        elif shp == 'cmp':
            shp = [2, 128, S]
        elif shp == 'E':
            shp = [64, S]
        elif shp == 'fb':
            shp = [S, 64]
        C[k] = cx.dram("c_" + k, shp, dt, kind="ExternalInput")
    y_d = cx.dram("y", [S, D_MODEL], F32, kind="ExternalOutput")
    Wb = {k: cx.dram(k + "_bf", WEIGHT_SHAPES[k], BF16) for k in CAST}
    P_d = cx.dram("P", [S, IN_WIDTH], F32)
    Y = [cx.dram("Y%d" % m, [S, BW], F32) for m in range(4)]
    Yn = [cx.dram("Yn%d" % m, [S, BW], F32) for m in range(2)]
    M_d = cx.dram("M", [S, D_MODEL], F32)
    Z_d = cx.dram("Z", [S, D_MODEL], F32)
    xa = cx.dram("xa", [S, D_MODEL], F32)
    xb = cx.dram("xb", [S, D_MODEL], F32)
    scr = dict(qnT=cx.dram("qnT", [4, 128, S], BF16), qrT=cx.dram("qrT", [4, 65, S], BF16),
               knT=cx.dram("knT", [4, 128, S], BF16), krT=cx.dram("krT", [65, S], BF16),
               va=cx.dram("va", [S, 4, 129], BF16),
               rqT=cx.dram("rqT", [4, 64, S], BF16), rkT=cx.dram("rkT", [4, 64, S], BF16),
               rv=cx.dram("rv", [S, 4, 128], BF16))
    scr['rw'] = {nm: cx.dram("rw_" + nm, [S, 512], F32) for nm in ['rr', 'lw', 'k2', 'vv', 'kn', 'aa', 'gg']}
    scr['rw']['bc'] = cx.dram("rw_bc", [S, 8], F32)
    scr['nsa'] = dict(qT=cx.dram("n_qT", [4, 128, S], BF16), nq=cx.dram("n_nq", [4, S], BF16),
                      kcT=cx.dram("n_kcT", [128, S], BF16), vcT=cx.dram("n_vcT", [128, S], BF16),
                      ksT=cx.dram("n_ksT", [128, S], BF16), kwT=cx.dram("n_kwT", [128, S], BF16),
                      vsa=cx.dram("n_vsa", [S, 129], BF16), vwa=cx.dram("n_vwa", [S, 129], BF16),
                      ng=cx.dram("n_ng", [S, 12], F32), krow=cx.dram("n_krow", [2, 128], BF16))
    st = contextlib.ExitStack()
    cx._st = st
    setup_consts(cx, st, C['ident'])
    phase_cast(cx, [(W[k], Wb[k]) for k in CAST])
    xin = x_d
    for l in range(depth):
        g = W['norm_gains'][l]
        phase_in(cx, S, xin, g[0], Wb['w_in'][l], P_d)
        if ENABLE['mla']:
            phase_mla(cx, S, P_d, W['mla_g_q'][l], W['mla_g_kv'][l], Wb['mla_w_uq'][l], Wb['mla_w_ukv'][l],
                      C['mla_cos'], C['mla_sin'], C['cmask'], Y[0], scr)
        else:
            phase_zero(cx, S, Y[0])
        if ENABLE['nsa']:
            phase_nsa(cx, S, l, P_d, W, Wb, C, [Y[1], Yn[0], Yn[1]], scr)
            ysrc1 = [Y[1], Yn[0], Yn[1]]
        else:
            phase_zero(cx, S, Y[1])
            ysrc1 = [Y[1]]
        if ENABLE['rwkv']:
            phase_rwkv(cx, S, l, P_d, W, Wb, C, Y[2], scr)
        else:
            phase_zero(cx, S, Y[2])
        if ENABLE['ret']:
            phase_ret(cx, S, P_d, C['ret_cos'], C['ret_sin'], C['gdec'], Y[3], scr)
        else:
            phase_zero(cx, S, Y[3])
        phase_merge(cx, S, [[Y[0]], ysrc1, [Y[2]], [Y[3]]], Wb['w_branch'][l], P_d, M_d)
        phase_out(cx, S, M_d, Wb['w_out'][l], Z_d)
        phase_normres(cx, S, xin, Z_d, g[1], xa)
        phase_ffn(cx, S, xa, g[2], Wb['w_up'][l], Wb['w_down'][l], Z_d)
        xnext = y_d if l == depth - 1 else xb
        phase_normres(cx, S, xa, Z_d, g[3], xnext)
        xin = xnext
    cx.barrier()
    return cx


_CACHE = {}


def kernel(**inputs):
    x = np.ascontiguousarray(np.asarray(inputs['x'], dtype=np.float32))
    B, S, _ = x.shape
    if S not in _CACHE:
        _CACHE[S] = (build_program(S), host_consts(S))
    cx, consts = _CACHE[S]
    base = {k: np.ascontiguousarray(np.asarray(inputs[k], dtype=np.float32)) for k in WEIGHT_SHAPES}
    for k, v in consts.items():
        base["c_" + k] = np.ascontiguousarray(v)
    in_maps = []
    for b in range(B):
        m = dict(base)
        m['x'] = x[b]
        in_maps.append(m)
    res = run_bass_kernel_spmd(cx.nc, in_maps, core_ids=list(range(B)))
    return np.stack([np.asarray(r['y'], dtype=np.float32) for r in res.results], axis=0)
```

```python
import contextlib
import numpy as np
import concourse.bass as bass
import concourse.mybir as mybir
from concourse.bass_utils import run_bass_kernel_spmd

F32 = mybir.dt.float32
F32R = mybir.dt.float32r
RW_FAST = True


def fr(ap):
    return ap.bitcast(F32R) if RW_FAST else ap
BF16 = mybir.dt.bfloat16
AF = mybir.ActivationFunctionType
ALU = mybir.AluOpType
AX = mybir.AxisListType

D_MODEL = 2048
DEPTH = 2
BW = 512
D_FF = 8192
NORM_EPS = 1e-6
IN_WIDTH = 13580
OFF_MLA = 0
OFF_NSA = 576
OFF_RWKV = 1868
OFF_RET = 3852
OFF_GATE = 5388

SELF_SYNC = {'pe': False, 'act': False, 'dve': True, 'pool': True, 'sp': False}


class Reg:
    __slots__ = ('w', 'r')

    def __init__(self):
        self.w = None
        self.r = {}


class Buf:
    def __init__(self, t, nreg=1, excl=False):
        self.t = t
        self.regs = [Reg() for _ in range(nreg)]
        self.excl = excl

    @property
    def reg(self):
        return self.regs[0]

    def __getitem__(self, idx):
        return self.t[idx]


class Ctx:
    def __init__(self):
        self.nc = bass.Bass("TRN2", target_bir_lowering=False)
        nc = self.nc
        self.E = {'pe': nc.tensor, 'act': nc.scalar, 'dve': nc.vector, 'pool': nc.gpsimd, 'sp': nc.sync}
        self.sem = {e: nc.alloc_semaphore("s_" + e) for e in ['pe', 'act', 'dve', 'pool']}
        self.cnt = {e: 0 for e in self.sem}
        self.NDS = 48
        self.dsem = [nc.alloc_semaphore("d%d" % i) for i in range(self.NDS)]
        self.dcnt = [0] * self.NDS
        self.dpool = {'sp': list(range(0, 20)), 'act': list(range(20, 34)), 'pool': list(range(34, 48))}
        self.dnext = {'sp': 0, 'act': 0, 'pool': 0}
        self.known = {e: {} for e in self.E}
        self.ninst = 0
        self.uid = 0
        self._rec = None

    def name(self, p):
        self.uid += 1
        return "%s_%d" % (p, self.uid)

    def sb(self, stack, shape, dtype, nreg=1, name="sb"):
        t = stack.enter_context(self.nc.sbuf_tensor(self.name(name), list(shape), dtype))
        return Buf(t, nreg)

    def ps(self, stack, shape, dtype=F32, nreg=1, name="ps"):
        t = stack.enter_context(self.nc.psum_tensor(self.name(name), list(shape), dtype))
        return Buf(t, nreg, excl=True)

    def dram(self, name, shape, dtype, kind="Internal"):
        return self.nc.dram_tensor(name, list(shape), dtype, kind=kind).ap()

    def _wait(self, e, kind, val, force=False):
        if isinstance(kind, str):
            if kind == e and not SELF_SYNC[e] and not (force and e in self.sem):
                return
            sem = self.sem[kind]
            v = val
        else:
            idx = kind[1]
            sem = self.dsem[idx]
            v = val * 16
        k = self.known[e]
        if k.get(kind, 0) >= v:
            return
        self.E[e].wait_ge(sem, v)
        self.ninst += 1
        k[kind] = v

    def _deps(self, e, reads, writes, force=False):
        for r in reads:
            if r.w is not None:
                self._wait(e, r.w[0], r.w[1], force)
        for w in writes:
            if w.w is not None:
                self._wait(e, w.w[0], w.w[1], force)
            for kind, val in w.r.items():
                self._wait(e, kind, val, force)

    def _commit(self, tok, reads, writes):
        kind, val = tok
        for r in reads:
            if r.r.get(kind, 0) < val:
                r.r[kind] = val
        for w in writes:
            w.w = tok
            w.r = {}

    @staticmethod
    def _regs(lst):
        out = []
        for x in lst:
            if isinstance(x, Buf):
                out.extend(x.regs)
            elif isinstance(x, Reg):
                out.append(x)
            elif x is None:
                pass
            else:
                raise TypeError(type(x))
        return out

    def op(self, e, reads, writes, fn):
        if self._rec is not None:
            self._rec.append(lambda: self._op(e, reads, writes, fn))
            return None
        return self._op(e, reads, writes, fn)

    def _op(self, e, reads, writes, fn):
        writes = list(writes) + [x for x in reads if isinstance(x, Buf) and x.excl]
        reads = [x for x in reads if not (isinstance(x, Buf) and x.excl)]
        reads = self._regs(reads)
        writes = self._regs(writes)
        self._deps(e, reads, writes)
        inst = fn(self.E[e])
        self.cnt[e] += 1
        self.ninst += 1
        inst.then_inc(self.sem[e], 1)
        self._commit((e, self.cnt[e]), reads, writes)
        return inst

    def dma(self, q, out_ap, in_ap, reads=(), writes=(), **kw):
        if self._rec is not None:
            self._rec.append(lambda: self._dma(q, out_ap, in_ap, reads, writes, **kw))
            return
        self._dma(q, out_ap, in_ap, reads, writes, **kw)

    def _dma(self, q, out_ap, in_ap, reads=(), writes=(), **kw):
        reads = self._regs(reads)
        writes = self._regs(writes)
        self._deps(q, reads, writes, force=True)
        pool = self.dpool[q]
        idx = pool[self.dnext[q] % len(pool)]
        self.dnext[q] += 1
        if self.dcnt[idx] > 0:
            self._wait(q, ('d', idx), self.dcnt[idx])
        self.E[q].dma_start(out=out_ap, in_=in_ap, **kw).then_inc(self.dsem[idx], 16)
        self.dcnt[idx] += 1
        self.ninst += 1
        self._commit((('d', idx), self.dcnt[idx]), reads, writes)

    def barrier(self):
        for e in self.E:
            for o in self.sem:
                if o != e and self.cnt[o] > 0:
                    self._wait(e, o, self.cnt[o])
            for i in range(self.NDS):
                if self.dcnt[i] > 0:
                    self._wait(e, ('d', i), self.dcnt[i])
            if e in self.sem and self.cnt[e] > 0:
                k = self.known[e]
                if k.get(e, 0) < self.cnt[e]:
                    self.E[e].wait_ge(self.sem[e], self.cnt[e])
                    k[e] = self.cnt[e]


INTERLEAVE = 2


def run_tiles(cx, body, NT):
    for t0 in range(0, NT, INTERLEAVE):
        lists = []
        posts = []
        for t in range(t0, min(NT, t0 + INTERLEAVE)):
            cx._rec = []
            post = body(t)
            lists.append(cx._rec)
            cx._rec = None
            if post is not None:
                posts.append(post)
        for i in range(max(len(l) for l in lists)):
            for l in lists:
                if i < len(l):
                    l[i]()
        for p in posts:
            p()


def atomic(cx, fn):
    if cx._rec is None:
        return fn()
    rec = cx._rec

    def unit():
        saved = cx._rec
        cx._rec = None
        fn()
        cx._rec = saved
    rec.append(unit)


def mm(cx, out_buf, out_ap, lhsT_ap, rhs_ap, reads, start, stop, **kw):
    return cx.op('pe', reads, [out_buf],
                 lambda e: e.matmul(out_ap, lhsT_ap, rhs_ap, start=start, stop=stop, **kw))


def transp(cx, out_buf, out_ap, in_ap, ident_ap, reads):
    return cx.op('pe', reads, [out_buf], lambda e: e.transpose(out_ap, in_ap, ident_ap))


def phase_cast(cx, pairs):
    CH = 4096
    NB = 6
    with contextlib.ExitStack() as st:
        stg = [cx.sb(st, [128, CH], F32, name="cst") for _ in range(NB)]
        outb = [cx.sb(st, [128, CH], BF16, name="cob") for _ in range(NB)]
        engs = ['dve', 'pool', 'act']
        k = 0
        for src, dst in pairs:
            n = 1
            for s in src.shape:
                n *= s
            assert n % 128 == 0
            per = n // 128
            names = " ".join("a%d" % i for i in range(len(src.shape)))
            s2 = src.rearrange("%s -> (%s)" % (names, names)).rearrange("(p f) -> p f", p=128)
            d2 = dst.rearrange("%s -> (%s)" % (names, names)).rearrange("(p f) -> p f", p=128)
            for c0 in range(0, per, CH):
                c1 = min(per, c0 + CH)
                w = c1 - c0
                i = k % NB
                cx.dma('sp', stg[i][:, :w], s2[:, c0:c1], writes=[stg[i]])
                e = engs[k % 3]
                if e == 'act':
                    cx.op(e, [stg[i]], [outb[i]], lambda en: en.copy(outb[i][:, :w], stg[i][:, :w]))
                else:
                    cx.op(e, [stg[i]], [outb[i]], lambda en: en.tensor_copy(outb[i][:, :w], stg[i][:, :w]))
                cx.dma('act' if k % 2 else 'pool', d2[:, c0:c1], outb[i][:, :w], reads=[outb[i]])
                k += 1
    cx.barrier()


def load_bcast_row(cx, q, buf, row_ap, n):
    cx.dma(q, buf[:, :n], row_ap.partition_broadcast(128), writes=[buf])


def rms_rstd(cx, x_buf, x_ap, n, ss_buf, junk_buf, eps=NORM_EPS):
    cx.op('act', [x_buf], [junk_buf, ss_buf],
          lambda e: e.activation(out=junk_buf[:, :n], in_=x_ap, func=AF.Square, accum_out=ss_buf[:, 0:1]))
    cx.op('dve', [ss_buf], [ss_buf],
          lambda e: e.tensor_scalar(ss_buf[:, 0:1], ss_buf[:, 0:1], 1.0 / n, eps, ALU.mult, ALU.add))
    cx.op('pool', [ss_buf, cx.neghalf], [ss_buf],
          lambda e: e.tensor_tensor(ss_buf[:, 0:1], ss_buf[:, 0:1], cx.neghalf[:, 0:1], ALU.pow))


def setup_consts(cx, st, ident_d):
    cx.ident = cx.sb(st, [128, 128], BF16, name="ident")
    cx.dma('sp', cx.ident[:, :], ident_d[:, :], writes=[cx.ident])
    cx.identf = cx.sb(st, [128, 128], F32, name="identf")
    cx.op('dve', [cx.ident], [cx.identf], lambda e: e.tensor_copy(cx.identf[:, :], cx.ident[:, :]))
    cx.neghalf = cx.sb(st, [128, 1], F32, name="neghalf")
    cx.op('pool', [], [cx.neghalf], lambda e: e.memset(cx.neghalf[:, :], -0.5))
    cx.ones_bf = cx.sb(st, [128, 128], BF16, name="ones_bf")
    cx.op('pool', [], [cx.ones_bf], lambda e: e.memset(cx.ones_bf[:, :], 1.0))
    cx.ones_f = cx.sb(st, [128, 128], F32, name="ones_f")
    cx.op('pool', [], [cx.ones_f], lambda e: e.memset(cx.ones_f[:, :], 1.0))


def phase_in(cx, S, x_d, g_row, w_bf, P_d):
    G = 512
    KC = D_MODEL // 128
    chunks = []
    c = 0
    while c < OFF_GATE:
        chunks.append((c, min(c + 512, OFF_GATE), False))
        c += 512
    c = OFF_GATE
    while c < IN_WIDTH:
        chunks.append((c, c + 512, True))
        c += 512
    wv = w_bf.rearrange("(kc p) n -> p kc n", p=128)
    with contextlib.ExitStack() as st:
        gB = cx.sb(st, [128, D_MODEL], F32, name="gB")
        load_bcast_row(cx, 'sp', gB, g_row, D_MODEL)
        xt = [cx.sb(st, [128, D_MODEL], F32, name="xt") for _ in range(2)]
        junk = cx.sb(st, [128, D_MODEL], BF16, name="junk")
        ss = [cx.sb(st, [128, 1], F32, name="ss") for _ in range(2)]
        hb = [cx.sb(st, [128, D_MODEL], BF16, name="hb") for _ in range(2)]
        hT = [cx.sb(st, [128, KC, G], BF16, name="hT") for _ in range(2)]
        wb = [cx.sb(st, [128, KC, 512], BF16, name="wb") for _ in range(2)]
        ob = [cx.sb(st, [128, 512], F32, name="ob") for _ in range(4)]
        ptr = [cx.ps(st, [128, 8, 128], BF16, name="ptr") for _ in range(2)]
        pmm = [cx.ps(st, [128, 512], F32, name="pmm") for _ in range(4)]
        ntr = 0
        nmm = 0
        nw = 0
        for gi in range(S // G):
            hTg = hT[gi % 2]
            for tl in range(G // 128):
                tt = gi * (G // 128) + tl
                xb = xt[tt % 2]
                sb_ = ss[tt % 2]
                hbb = hb[tt % 2]
                cx.dma('sp', xb[:, :], x_d[tt * 128:(tt + 1) * 128, :], writes=[xb])
                rms_rstd(cx, xb, xb[:, :], D_MODEL, sb_, junk)
                cx.op('dve', [xb, sb_, gB], [hbb],
                      lambda e: e.scalar_tensor_tensor(out=hbb[:, :], in0=xb[:, :], scalar=sb_[:, 0:1],
                                                       in1=gB[:, :], op0=ALU.mult, op1=ALU.mult))
                for k4 in range(KC // 4):
                    pt = ptr[ntr % 2]
                    ntr += 1
                    for j in range(4):
                        kc = k4 * 4 + j
                        transp(cx, pt, pt[:, j, :], hbb[:, kc * 128:(kc + 1) * 128], cx.ident[:, :], [hbb, cx.ident])
                    eng = 'act' if (k4 % 2 == 0) else 'dve'
                    dst = hTg[:, k4 * 4:(k4 + 1) * 4, tl * 128:(tl + 1) * 128]
                    if eng == 'act':
                        cx.op('act', [pt], [hTg], lambda e: e.copy(dst, pt[:, 0:4, :]))
                    else:
                        cx.op('dve', [pt], [hTg], lambda e: e.tensor_copy(dst, pt[:, 0:4, :]))
            for (c0, c1, sig) in chunks:
                w = c1 - c0
                wbb = wb[nw % 2]
                nw += 1
                cx.dma('sp', wbb[:, :, :w], wv[:, :, c0:c1], writes=[wbb])
                for tl in range(G // 128):
                    tt = gi * (G // 128) + tl
                    pm = pmm[nmm % 4]
                    obb = ob[nmm % 4]
                    nmm += 1
                    for kc in range(KC):
                        mm(cx, pm, pm[:, :w], hTg[:, kc, tl * 128:(tl + 1) * 128], wbb[:, kc, :w],
                           [hTg, wbb], kc == 0, kc == KC - 1)
                    if sig:
                        cx.op('act', [pm], [obb],
                              lambda e: e.activation(out=obb[:, :w], in_=pm[:, :w], func=AF.Sigmoid))
                    elif nmm % 2 == 0:
                        cx.op('dve', [pm], [obb], lambda e: e.tensor_copy(obb[:, :w], pm[:, :w]))
                    else:
                        cx.op('act', [pm], [obb], lambda e: e.copy(obb[:, :w], pm[:, :w]))
                    cx.dma('pool', P_d[tt * 128:(tt + 1) * 128, c0:c1], obb[:, :w], reads=[obb])
    cx.barrier()


def evac(cx, eng, src_buf, src_ap, dst_buf, dst_ap, extra_reads=()):
    if eng == 'act':
        cx.op('act', [src_buf] + list(extra_reads), [dst_buf], lambda e: e.copy(dst_ap, src_ap))
    else:
        cx.op(eng, [src_buf] + list(extra_reads), [dst_buf], lambda e: e.tensor_copy(dst_ap, src_ap))


class TrPool:
    def __init__(self, cx, st, n=2, dtype=BF16):
        self.cx = cx
        self.bufs = [cx.ps(st, [128, 8 if dtype == BF16 else 4, 128], dtype, name="ptr") for _ in range(n)]
        self.k = 0
        self.dtype = dtype

    def transpose_cols(self, src_buf, src_ap_fn, nblk, dst_buf, dst_ap_fn, rows=128, blkw=128):
        cx = self.cx
        if cx._rec is not None:
            atomic(cx, lambda: self.transpose_cols(src_buf, src_ap_fn, nblk, dst_buf, dst_ap_fn, rows, blkw))
            return
        ident = cx.ident if self.dtype == BF16 else cx.identf
        j = 0
        while j < nblk:
            cnt = min(4, nblk - j)
            pt = self.bufs[self.k % len(self.bufs)]
            eng = 'act' if self.k % 2 == 0 else 'dve'
            self.k += 1
            for i in range(cnt):
                transp(cx, pt, pt[:blkw, i, :rows], src_ap_fn(j + i), ident[:rows, :rows], [src_buf, ident])
            evac(cx, eng, pt, pt[:blkw, :cnt, :rows], dst_buf, dst_ap_fn(j, cnt))
            j += cnt


def phase_merge(cx, S, ysrcs, wbr_bf, P_d, M_d):
    with contextlib.ExitStack() as st:
        wbr = cx.sb(st, [128, 16, D_MODEL], BF16, name="wbr")
        wv = wbr_bf.rearrange("m (kc p) n -> p (m kc) n", p=128)
        for q in range(4):
            cx.dma('sp', wbr[:, q * 4:(q + 1) * 4, :], wv[:, q * 4:(q + 1) * 4, :], writes=[wbr])
        yt = [cx.sb(st, [128, BW], F32, name="yt") for _ in range(3)]
        yb = [cx.sb(st, [128, BW], BF16, name="yb") for _ in range(2)]
        yT = [cx.sb(st, [128, 4, 128], BF16, name="yT") for _ in range(2)]
        sg = [cx.sb(st, [128, D_MODEL], F32, name="sg") for _ in range(2)]
        mg = [cx.sb(st, [128, D_MODEL], F32, name="mg") for _ in range(2)]
        tmp = [cx.sb(st, [128, 512], F32, name="tmp") for _ in range(2)]
        trp = TrPool(cx, st)
        pmm = [cx.ps(st, [128, 512], F32, name="pmm") for _ in range(4)]
        k = 0
        for tt in range(S // 128):
            rows = slice(tt * 128, (tt + 1) * 128)
            mgb = mg[tt % 2]
            for m in range(4):
                k += 1
                y0 = yt[k % 3]
                cx.dma('sp', y0[:, :], ysrcs[m][0][rows, :], writes=[y0])
                for extra in ysrcs[m][1:]:
                    k += 1
                    y1 = yt[k % 3]
                    cx.dma('sp', y1[:, :], extra[rows, :], writes=[y1])
                    cx.op('pool', [y0, y1], [y0], lambda e: e.tensor_tensor(y0[:, :], y0[:, :], y1[:, :], ALU.add))
                ybb = yb[m % 2]
                cx.op('act', [y0], [ybb], lambda e: e.copy(ybb[:, :], y0[:, :]))
                yTb = yT[m % 2]
                trp.transpose_cols(ybb, lambda j: ybb[:, j * 128:(j + 1) * 128], 4, yTb,
                                   lambda j0, cnt: yTb[:, j0:j0 + cnt, :])
                sgb = sg[m % 2]
                cx.dma('act', sgb[:, :], P_d[rows, OFF_GATE + m * D_MODEL:OFF_GATE + (m + 1) * D_MODEL], writes=[sgb])
                for nc_ in range(4):
                    cs = slice(nc_ * 512, (nc_ + 1) * 512)
                    pm = pmm[(m * 4 + nc_) % 4]
                    for kc in range(4):
                        mm(cx, pm, pm[:, :], yTb[:, kc, :], wbr[:, m * 4 + kc, cs], [yTb, wbr], kc == 0, kc == 3)
                    if m == 0:
                        cx.op('dve', [pm, sgb], [mgb],
                              lambda e: e.tensor_tensor(mgb[:, cs], pm[:, :], sgb[:, cs], ALU.mult))
                    else:
                        tb = tmp[nc_ % 2]
                        cx.op('dve', [pm, sgb], [tb],
                              lambda e: e.tensor_tensor(tb[:, :], pm[:, :], sgb[:, cs], ALU.mult))
                        cx.op('dve' if nc_ % 2 else 'pool', [tb, mgb], [mgb],
                              lambda e: e.tensor_tensor(mgb[:, cs], mgb[:, cs], tb[:, :], ALU.add))
            cx.dma('pool', M_d[rows, :], mgb[:, :], reads=[mgb])
    cx.barrier()


def phase_out(cx, S, M_d, wout_bf, Z_d):
    with contextlib.ExitStack() as st:
        wo = cx.sb(st, [128, 16, D_MODEL], BF16, name="wo")
        wv = wout_bf.rearrange("(kc p) n -> p kc n", p=128)
        for q in range(4):
            cx.dma('sp', wo[:, q * 4:(q + 1) * 4, :], wv[:, q * 4:(q + 1) * 4, :], writes=[wo])
        mt = [cx.sb(st, [128, D_MODEL], F32, name="mt") for _ in range(2)]
        mb = [cx.sb(st, [128, D_MODEL], BF16, name="mb") for _ in range(2)]
        mT = [cx.sb(st, [128, 16, 128], BF16, name="mT") for _ in range(2)]
        ob = [cx.sb(st, [128, 512], F32, name="ob") for _ in range(4)]
        trp = TrPool(cx, st)
        pmm = [cx.ps(st, [128, 512], F32, name="pmm") for _ in range(4)]
        k = 0
        for tt in range(S // 128):
            rows = slice(tt * 128, (tt + 1) * 128)
            mtb, mbb, mTb = mt[tt % 2], mb[tt % 2], mT[tt % 2]
            cx.dma('sp', mtb[:, :], M_d[rows, :], writes=[mtb])
            cx.op('act', [mtb], [mbb], lambda e: e.copy(mbb[:, :], mtb[:, :]))
            trp.transpose_cols(mbb, lambda j: mbb[:, j * 128:(j + 1) * 128], 16, mTb,
                               lambda j0, cnt: mTb[:, j0:j0 + cnt, :])
            for nc_ in range(4):
                cs = slice(nc_ * 512, (nc_ + 1) * 512)
                pm = pmm[k % 4]
                obb = ob[k % 4]
                k += 1
                for kc in range(16):
                    mm(cx, pm, pm[:, :], mTb[:, kc, :], wo[:, kc, cs], [mTb, wo], kc == 0, kc == 15)
                evac(cx, 'act' if k % 2 else 'dve', pm, pm[:, :], obb, obb[:, :])
                cx.dma('pool', Z_d[rows, cs], obb[:, :], reads=[obb])
    cx.barrier()


def phase_normres(cx, S, x_d, Z_d, g_row, out_d):
    with contextlib.ExitStack() as st:
        gB = cx.sb(st, [128, D_MODEL], F32, name="gB")
        load_bcast_row(cx, 'sp', gB, g_row, D_MODEL)
        zt = [cx.sb(st, [128, D_MODEL], F32, name="zt") for _ in range(2)]
        xt = [cx.sb(st, [128, D_MODEL], F32, name="xt") for _ in range(2)]
        ot = [cx.sb(st, [128, D_MODEL], F32, name="ot") for _ in range(2)]
        junk = cx.sb(st, [128, D_MODEL], BF16, name="junk")
        ss = [cx.sb(st, [128, 1], F32, name="ss") for _ in range(2)]
        def body(tt):
            rows = slice(tt * 128, (tt + 1) * 128)
            z, x, o, s_ = zt[tt % 2], xt[tt % 2], ot[tt % 2], ss[tt % 2]
            cx.dma('sp', z[:, :], Z_d[rows, :], writes=[z])
            cx.dma('act', x[:, :], x_d[rows, :], writes=[x])
            rms_rstd(cx, z, z[:, :], D_MODEL, s_, junk)
            cx.op('dve', [z, s_, gB], [o],
                  lambda e: e.scalar_tensor_tensor(out=o[:, :], in0=z[:, :], scalar=s_[:, 0:1], in1=gB[:, :],
                                                   op0=ALU.mult, op1=ALU.mult))
            cx.op('pool', [o, x], [o], lambda e: e.tensor_tensor(o[:, :], o[:, :], x[:, :], ALU.add))
            cx.dma('pool', out_d[rows, :], o[:, :], reads=[o])
        run_tiles(cx, body, S // 128)
    cx.barrier()


def phase_ffn(cx, S, x_d, g_row, wup_bf, wdn_bf, Z_d):
    G = 512 if S >= 512 else S
    NT = G // 128
    KC = D_MODEL // 128
    FC = D_FF // 128
    UW = 256
    wuv = wup_bf.rearrange("(kc p) f -> p kc f", p=128)
    wdv = wdn_bf.rearrange("(fc p) n -> p fc n", p=128)
    with contextlib.ExitStack() as st:
        gB = cx.sb(st, [128, D_MODEL], F32, name="gB")
        load_bcast_row(cx, 'sp', gB, g_row, D_MODEL)
        xt = [cx.sb(st, [128, D_MODEL], F32, name="xt") for _ in range(2)]
        junk = cx.sb(st, [128, D_MODEL], BF16, name="junk")
        ss = [cx.sb(st, [128, 1], F32, name="ss") for _ in range(2)]
        hb = [cx.sb(st, [128, D_MODEL], BF16, name="hb") for _ in range(2)]
        hT = cx.sb(st, [128, KC, G], BF16, name="hT")
        aT = cx.sb(st, [128, FC, G], BF16, name="aT")
        wu = [cx.sb(st, [128, KC, UW], BF16, name="wu") for _ in range(2)]
        wd = [cx.sb(st, [128, 8, 512], BF16, name="wd") for _ in range(2)]
        rl = [cx.sb(st, [128, G], F32, name="rl") for _ in range(2)]
        ob = [cx.sb(st, [128, 512], F32, name="ob") for _ in range(4)]
        trp = TrPool(cx, st, n=1)
        pup = [cx.ps(st, [128, G], F32, name="pup") for _ in range(2)]
        pdn = [cx.ps(st, [128, 512], F32, name="pdn") for _ in range(NT)]
        nu = 0
        nd = 0
        no = 0
        for gi in range(S // G):
            for tl in range(NT):
                tt = gi * NT + tl
                x, s_, h = xt[tt % 2], ss[tt % 2], hb[tt % 2]
                cx.dma('sp', x[:, :], x_d[tt * 128:(tt + 1) * 128, :], writes=[x])
                rms_rstd(cx, x, x[:, :], D_MODEL, s_, junk)
                cx.op('dve', [x, s_, gB], [h],
                      lambda e: e.scalar_tensor_tensor(out=h[:, :], in0=x[:, :], scalar=s_[:, 0:1], in1=gB[:, :],
                                                       op0=ALU.mult, op1=ALU.mult))
                trp.transpose_cols(h, lambda j: h[:, j * 128:(j + 1) * 128], KC, hT,
                                   lambda j0, cnt: hT[:, j0:j0 + cnt, tl * 128:(tl + 1) * 128])
            for uc in range(D_FF // UW):
                wub = wu[nu % 2]
                nu += 1
                cx.dma('sp', wub[:, :, :], wuv[:, :, uc * UW:(uc + 1) * UW], writes=[wub])
                for j in range(UW // 128):
                    fc = uc * (UW // 128) + j
                    pu = pup[fc % 2]
                    r = rl[fc % 2]
                    for kc in range(KC):
                        mm(cx, pu, pu[:, :], wub[:, kc, j * 128:(j + 1) * 128], hT[:, kc, :], [wub, hT],
                           kc == 0, kc == KC - 1)
                    cx.op('act', [pu], [r], lambda e: e.activation(out=r[:, :], in_=pu[:, :], func=AF.Relu))
                    eng = 'dve' if fc % 2 == 0 else 'pool'
                    cx.op(eng, [r], [aT], lambda e: e.tensor_tensor(aT[:, fc, :], r[:, :], r[:, :], ALU.mult))
            for nc_ in range(4):
                cs = slice(nc_ * 512, (nc_ + 1) * 512)
                for fg in range(FC // 8):
                    wdb = wd[nd % 2]
                    nd += 1
                    cx.dma('act', wdb[:, :, :], wdv[:, fg * 8:(fg + 1) * 8, cs], writes=[wdb])
                    for f8 in range(8):
                        fc = fg * 8 + f8
                        for tl in range(NT):
                            mm(cx, pdn[tl], pdn[tl][:, :], aT[:, fc, tl * 128:(tl + 1) * 128], wdb[:, f8, :],
                               [aT, wdb], fc == 0, fc == FC - 1)
                for tl in range(NT):
                    tt = gi * NT + tl
                    o = ob[no % 4]
                    no += 1
                    evac(cx, 'act' if no % 2 else 'dve', pdn[tl], pdn[tl][:, :], o, o[:, :])
                    cx.dma('pool', Z_d[tt * 128:(tt + 1) * 128, cs], o[:, :], reads=[o])
    cx.barrier()


def rope_tm(cx, src, x1, x2, c, s, dst, o1, o2, tmps, scale=None):
    (ta, tap), (tb, tbp) = tmps
    cx.op('dve', [src] + c[:1] + [], [ta], lambda e: e.tensor_tensor(tap, x1, c[1], ALU.mult))
    cx.op('pool', [src] + s[:1], [tb], lambda e: e.tensor_tensor(tbp, x2, s[1], ALU.mult))
    cx.op('dve', [ta, tb], [dst], lambda e: e.tensor_tensor(o1, tap, tbp, ALU.subtract))
    cx.op('pool', [src] + c[:1], [ta], lambda e: e.tensor_tensor(tap, x2, c[1], ALU.mult))
    cx.op('dve', [src] + s[:1], [tb], lambda e: e.tensor_tensor(tbp, x1, s[1], ALU.mult))
    cx.op('pool', [ta, tb], [dst], lambda e: e.tensor_tensor(o2, tap, tbp, ALU.add))
    if scale is not None:
        cx.op('pool', [dst], [dst], lambda e: e.tensor_scalar(o1, o1, scale, None, ALU.mult))
        cx.op('pool', [dst], [dst], lambda e: e.tensor_scalar(o2, o2, scale, None, ALU.mult))


def bcast_scalar_max(cx, st, trp_f, run_buf, out_col):
    pt = trp_f.bufs[0]
    transp(cx, pt, pt[0:1, 0, :], run_buf[:, 0:1], cx.identf[:, :], [run_buf, cx.identf])
    row = cx.sb(st, [1, 128], F32, name="mxrow")
    one = cx.sb(st, [1, 1], F32, name="mxone")
    evac(cx, 'dve', pt, pt[0:1, 0, :], row, row[:, :])
    cx.op('dve', [row], [one], lambda e: e.tensor_reduce(out=one[:, :], in_=row[:, :], axis=AX.X, op=ALU.max))
    mm(cx, pt, pt[:, 1, 0:1], cx.ones_f[0:1, :], one[0:1, 0:1], [cx.ones_f, one], True, True)
    evac(cx, 'dve', pt, pt[:, 1, 0:1], out_col, out_col[:, 0:1])


class AttnRes:
    def __init__(self, cx, st, W):
        self.sT = [cx.ps(st, [128, 512], F32, name="sT") for _ in range(2)]
        self.acc = [cx.ps(st, [128, 2, 256], F32, name="acc") for _ in range(4)]
        self.pT = [cx.sb(st, [128, 512], BF16, name="pT") for _ in range(3)]
        self.n = 0
        self.nq = 0


def attn_core(cx, res, S, qchunks, kchunks, vaug_fn, W, blocks_fn, epilogue, mode='softmax', qbs=None):
    QW = min(512, S)
    NJ = QW // 128
    for QB in (range(S // QW) if qbs is None else qbs):
        q0 = QB * QW
        blocks = blocks_fn(QB)
        accs = [res.acc[(res.nq % 2) * 2 + (j // 2)] for j in range(NJ)]
        res.nq += 1

        def stage1(b):
            sT = res.sT[res.n % 2]
            pT = res.pT[res.n % 3]
            res.n += 1
            nk, k0 = b['nk'], b['k0']
            nmm = len(qchunks) + (1 if b.get('extra') else 0)
            i = 0
            for (qb_, qap), (kb_, kap) in zip(qchunks, kchunks):
                ka = kap(k0, nk) if callable(kap) else kap[:, k0:k0 + nk]
                mm(cx, sT, sT[:nk, :QW], ka, qap[:, q0:q0 + QW], [qb_, kb_], i == 0, i == nmm - 1)
                i += 1
            if b.get('extra'):
                lb, lap, rb, rap = b['extra']
                mm(cx, sT, sT[:nk, :QW], lap, rap, list(lb) + list(rb), False, True)
            if mode == 'softmax':
                cx.op('act', [sT], [pT], lambda e: e.activation(out=pT[:nk, :QW], in_=sT[:nk, :QW], func=AF.Exp))
                if b.get('mask'):
                    mb, map_ = b['mask']
                    cx.op('pool' if nk == 128 else 'dve', [pT, mb], [pT],
                          lambda e: e.tensor_tensor(pT[:nk, :QW], pT[:nk, :QW], map_, ALU.mult))
            else:
                c, gb, gap = b['decay']
                cx.op('dve', [sT, gb], [pT],
                      lambda e: e.scalar_tensor_tensor(out=pT[:nk, :QW], in0=sT[:nk, :QW], scalar=float(c), in1=gap,
                                                       op0=ALU.mult, op1=ALU.mult))
            return pT

        def stage2(bi, b, pT):
            nk = b['nk']
            vb, vap = vaug_fn(b['kb'], nk)
            for j in range(NJ):
                a = accs[j]
                mm(cx, a, a[:, j % 2, :W], pT[:nk, j * 128:(j + 1) * 128], vap, [pT, vb],
                   bi == 0 and j % 2 == 0, bi == len(blocks) - 1, skip_group_check=True)

        prev = None
        for bi, b in enumerate(blocks):
            pT = stage1(b)
            if prev is not None:
                stage2(*prev)
            prev = (bi, b, pT)
        stage2(*prev)
        for j in range(NJ):
            epilogue(QB, j, accs[j], accs[j][:, j % 2, :W])


def causal_blocks(QB, QW, cmask):
    out = []
    nd = QW // 128
    for kb in range(nd * (QB + 1)):
        i = kb - nd * QB
        out.append(dict(kb=kb, k0=kb * 128, nk=128, mask=(cmask, cmask[:, i, :QW]) if i >= 0 else None))
    return out


MLA_SCALE = 192 ** -0.5


def phase_mla(cx, S, P_d, gq_row, gkv_row, wuq_bf, wukv_bf, cos_d, sin_d, cmask_d, Y_d, scr):
    NT = S // 128
    G = min(512, S)
    NG = G // 128
    with contextlib.ExitStack() as st:
        with contextlib.ExitStack() as s1:
            gq = cx.sb(s1, [128, 384], F32, name="gq")
            gkv = cx.sb(s1, [128, 128], F32, name="gkv")
            load_bcast_row(cx, 'sp', gq, gq_row, 384)
            load_bcast_row(cx, 'sp', gkv, gkv_row, 128)
            wuq = cx.sb(s1, [128, 3, 768], BF16, name="wuq")
            cx.dma('sp', wuq[:, :, :], wuq_bf.rearrange("(kc p) n -> p kc n", p=128), writes=[wuq])
            wukv = cx.sb(s1, [128, 1024], BF16, name="wukv")
            cx.dma('sp', wukv[:, :], wukv_bf[:, :], writes=[wukv])
            pm = [cx.sb(s1, [128, 576], F32, name="pm") for _ in range(2)]
            cs_t = [cx.sb(s1, [128, 64], F32, name="cs") for _ in range(2)]
            junk = cx.sb(s1, [128, 768], BF16, name="junk")
            ss = [cx.sb(s1, [128, 1], F32, name="ss") for _ in range(2)]
            nb = [cx.sb(s1, [128, 384], BF16, name="nb") for _ in range(2)]
            nT = [cx.sb(s1, [128, 3, 128], BF16, name="nT") for _ in range(2)]
            qf = [cx.sb(s1, [128, 4, 256], F32, name="qf") for _ in range(2)]
            qs = [cx.sb(s1, [128, 4, 193], BF16, name="qs") for _ in range(2)]
            qsf = [cx.sb(s1, [128, 4, 64], F32, name="qsf") for _ in range(2)]
            t1s = [cx.sb(s1, [128, 4, 32], F32, name="t1") for _ in range(2)]
            t2s = [cx.sb(s1, [128, 4, 32], F32, name="t2") for _ in range(2)]
            kr = [cx.sb(s1, [128, 65], BF16, name="kr") for _ in range(2)]
            krf = [cx.sb(s1, [128, 64], F32, name="krf") for _ in range(2)]
            kb16 = [cx.sb(s1, [128, 4, 128], BF16, name="kb16") for _ in range(2)]
            va = [cx.sb(s1, [128, 4, 129], BF16, name="va") for _ in range(2)]
            sqs = [cx.sb(s1, [128, 4, 256], F32, name="sq") for _ in range(2)]
            n4 = [cx.sb(s1, [128, 4], F32, name="n4") for _ in range(2)]
            n1 = [cx.sb(s1, [128, 1], F32, name="n1") for _ in range(2)]
            kmx = cx.sb(s1, [128, 1], F32, name="kmx")
            kmax = cx.sb(s1, [128, 1], F32, name="kmax")
            gA = cx.sb(s1, [128, 4, G], BF16, name="gA")
            gB_ = cx.sb(s1, [65, 4, G], BF16, name="gB_")
            trp = TrPool(cx, s1)
            trf = TrPool(cx, s1, n=1, dtype=F32)
            pq = [cx.ps(s1, [128, 512], F32, name="pq") for _ in range(2)]
            pq2 = [cx.ps(s1, [128, 512], F32, name="pq2") for _ in range(2)]
            cx.op('pool', [], [kmx], lambda e: e.memset(kmx[:, :], 0.0))

            def load_norm_T(tt, c0, n, gbuf, k):
                p = pm[k % 2]
                cx.dma('sp', p[:, :], P_d[tt * 128:(tt + 1) * 128, OFF_MLA:OFF_MLA + 576], writes=[p])
                s_ = ss[k % 2]
                rms_rstd(cx, p, p[:, c0:c0 + n], n, s_, junk)
                nbb = nb[k % 2]
                cx.op('dve', [p, s_, gbuf], [nbb],
                      lambda e: e.scalar_tensor_tensor(out=nbb[:, :n], in0=p[:, c0:c0 + n], scalar=s_[:, 0:1],
                                                       in1=gbuf[:, :n], op0=ALU.mult, op1=ALU.mult))
                nTb = nT[k % 2]
                trp.transpose_cols(nbb, lambda j: nbb[:, j * 128:(j + 1) * 128], n // 128, nTb,
                                   lambda j0, cnt: nTb[:, j0:j0 + cnt, :])
                return p, nTb

            def body(tt):
                t1, t2, sq = t1s[tt % 2], t2s[tt % 2], sqs[tt % 2]
                tl = tt % NG
                p, nTb = load_norm_T(tt, 384, 128, gkv, tt)
                c_t = cs_t[tt % 2]
                cx.dma('act', c_t[:, 0:32], cos_d[tt * 128:(tt + 1) * 128, :], writes=[c_t])
                cx.dma('act', c_t[:, 32:64], sin_d[tt * 128:(tt + 1) * 128, :], writes=[c_t])
                pa, pb = pq[tt % 2], pq2[tt % 2]
                mm(cx, pa, pa[:, :], nTb[:, 0, :], wukv[:, 0:512], [nTb, wukv], True, True)
                mm(cx, pb, pb[:, :], nTb[:, 0, :], wukv[:, 512:1024], [nTb, wukv], True, True)
                q = qf[tt % 2]
                evac(cx, 'act', pa, pa[:, :].rearrange("p (h c) -> p h c", h=2), q, q[:, 0:2, :])
                evac(cx, 'dve', pb, pb[:, :].rearrange("p (h c) -> p h c", h=2), q, q[:, 2:4, :])
                k16, vab, krb, krfb = kb16[tt % 2], va[tt % 2], kr[tt % 2], krf[tt % 2]
                cx.op('pool', [q], [k16], lambda e: e.tensor_copy(k16[:, :, :], q[:, :, 0:128]))
                cx.op('pool', [q], [vab], lambda e: e.tensor_copy(vab[:, :, 0:128], q[:, :, 128:256]))
                cx.op('pool', [], [vab], lambda e: e.memset(vab[:, :, 128:129], 1.0))
                rope_tm(cx, p, p[:, 512:544], p[:, 544:576], [c_t, c_t[:, 0:32]], [c_t, c_t[:, 32:64]],
                        krfb, krfb[:, 0:32], krfb[:, 32:64], [(t1, t1[:, 0, :]), (t2, t2[:, 0, :])])
                cx.op('pool', [krfb], [krb], lambda e: e.tensor_copy(krb[:, 0:64], krfb[:, :]))
                cx.op('pool', [], [krb], lambda e: e.memset(krb[:, 64:65], 1.0))
                cx.op('dve', [q], [sq], lambda e: e.tensor_tensor(sq[:, :, 0:128], q[:, :, 0:128], q[:, :, 0:128], ALU.mult))
                n4b, n1b = n4[tt % 2], n1[tt % 2]
                cx.op('dve', [sq], [n4b], lambda e: e.tensor_reduce(out=n4b[:, :], in_=sq[:, :, 0:128], axis=AX.X, op=ALU.add))
                cx.op('dve', [n4b], [n1b], lambda e: e.tensor_reduce(out=n1b[:, :], in_=n4b[:, :], axis=AX.X, op=ALU.max))
                cx.op('act', [krfb], [junk, n4b],
                      lambda e: e.activation(out=junk[:, :64], in_=krfb[:, :], func=AF.Square, accum_out=n4b[:, 0:1]))
                cx.op('dve', [n4b, n1b], [n1b], lambda e: e.tensor_tensor(n1b[:, :], n1b[:, :], n4b[:, 0:1], ALU.add))
                cx.op('dve', [n1b, kmx], [kmx], lambda e: e.tensor_tensor(kmx[:, :], kmx[:, :], n1b[:, :], ALU.max))
                trp.transpose_cols(k16, lambda j: k16[:, j, :], 4, gA,
                                   lambda j0, cnt: gA[:, j0:j0 + cnt, tl * 128:(tl + 1) * 128])
                trp.transpose_cols(krb, lambda j: krb[:, :], 1, gB_,
                                   lambda j0, cnt: gB_[:65, 0:1, tl * 128:(tl + 1) * 128], blkw=65)
                cx.dma('pool', scr['va'][tt * 128:(tt + 1) * 128, :, :], vab[:, :, :], reads=[vab])
                if tl == NG - 1:
                    def post():
                        g0 = (tt // NG) * G
                        cx.dma('pool', scr['knT'][:, :, g0:g0 + G].rearrange("h d s -> d h s"), gA[:, :, :], reads=[gA])
                        cx.dma('pool', scr['krT'][:, g0:g0 + G], gB_[:65, 0, :], reads=[gB_])
                    return post
            run_tiles(cx, body, NT)
            bcast_scalar_max(cx, s1, trf, kmx, kmax)
            cx.op('act', [kmax], [kmax], lambda e: e.activation(out=kmax[:, :], in_=kmax[:, :], func=AF.Sqrt))
            def body(tt):
                t1, t2, sq = t1s[tt % 2], t2s[tt % 2], sqs[tt % 2]
                tl = tt % NG
                p, nTb = load_norm_T(tt, 0, 384, gq, tt)
                c_t = cs_t[tt % 2]
                cx.dma('act', c_t[:, 0:32], cos_d[tt * 128:(tt + 1) * 128, :], writes=[c_t])
                cx.dma('act', c_t[:, 32:64], sin_d[tt * 128:(tt + 1) * 128, :], writes=[c_t])
                pa, pb = pq[tt % 2], pq2[tt % 2]
                for kc in range(3):
                    mm(cx, pa, pa[:, :], nTb[:, kc, :], wuq[:, kc, 0:512], [nTb, wuq], kc == 0, kc == 2)
                for kc in range(3):
                    mm(cx, pb, pb[:, :256], nTb[:, kc, :], wuq[:, kc, 512:768], [nTb, wuq], kc == 0, kc == 2)
                q = qf[tt % 2]
                qv = q[:, :, :].rearrange("p h c -> p (h c)")
                evac(cx, 'act', pa, pa[:, :], q, qv[:, 0:512])
                evac(cx, 'dve', pb, pb[:, :256], q, qv[:, 512:768])
                qh = qv[:, 0:768].rearrange("p (h c) -> p h c", h=4)
                cx.op('dve', [q], [sq], lambda e: e.tensor_tensor(sq[:, :, 0:192], qh, qh, ALU.mult))
                n4b = n4[tt % 2]
                cx.op('dve', [sq], [n4b], lambda e: e.tensor_reduce(out=n4b[:, :], in_=sq[:, :, 0:192], axis=AX.X, op=ALU.add))
                cx.op('act', [n4b], [n4b], lambda e: e.activation(out=n4b[:, :], in_=n4b[:, :], func=AF.Sqrt))
                cx.op('dve', [n4b, kmax], [n4b],
                      lambda e: e.tensor_scalar(n4b[:, :], n4b[:, :], kmax[:, 0:1], -MLA_SCALE, ALU.mult, ALU.mult))
                qsb, qsfb = qs[tt % 2], qsf[tt % 2]
                cb = c_t[:, 0:32].unsqueeze(1).broadcast_to([128, 4, 32])
                sb_ = c_t[:, 32:64].unsqueeze(1).broadcast_to([128, 4, 32])
                rope_tm(cx, q, qh[:, :, 128:160], qh[:, :, 160:192], [c_t, cb], [c_t, sb_],
                        qsfb, qsfb[:, :, 0:32], qsfb[:, :, 32:64], [(t1, t1[:, :, :]), (t2, t2[:, :, :])])
                cx.op('act', [q], [qsb], lambda e: e.activation(out=qsb[:, :, 0:128], in_=qh[:, :, 0:128], func=AF.Copy, scale=MLA_SCALE))
                cx.op('act', [qsfb], [qsb], lambda e: e.activation(out=qsb[:, :, 128:192], in_=qsfb[:, :, :], func=AF.Copy, scale=MLA_SCALE))
                cx.op('pool', [n4b], [qsb], lambda e: e.tensor_copy(qsb[:, :, 192:193], n4b[:, :].unsqueeze(2)))
                trp.transpose_cols(qsb, lambda j: qsb[:, j, 0:128], 4, gA,
                                   lambda j0, cnt: gA[:, j0:j0 + cnt, tl * 128:(tl + 1) * 128])
                trp.transpose_cols(qsb, lambda j: qsb[:, j, 128:193], 4, gB_,
                                   lambda j0, cnt: gB_[:65, j0:j0 + cnt, tl * 128:(tl + 1) * 128], blkw=65)
                if tl == NG - 1:
                    def post():
                        g0 = (tt // NG) * G
                        cx.dma('pool', scr['qnT'][:, :, g0:g0 + G].rearrange("h d s -> d h s"), gA[:, :, :], reads=[gA])
                        cx.dma('pool', scr['qrT'][:, :, g0:g0 + G].rearrange("h d s -> d h s"), gB_[:65, :, :], reads=[gB_])
                    return post
            run_tiles(cx, body, NT)
        cx.barrier()
        res = AttnRes(cx, st, 129)
        cmask = cx.sb(st, [128, 4, 512], BF16, name="cmask")
        cx.dma('sp', cmask[:, :, :], cmask_d.rearrange("i k q -> k i q"), writes=[cmask])
        krT = cx.sb(st, [65, S], BF16, name="krT")
        cx.dma('sp', krT[:, :], scr['krT'][:, :], writes=[krT])
        qn = [cx.sb(st, [128, S], BF16, name="qn") for _ in range(2)]
        qr = [cx.sb(st, [65, S], BF16, name="qr") for _ in range(2)]
        kn = [cx.sb(st, [128, S], BF16, name="kn") for _ in range(2)]
        vv = [cx.sb(st, [128, NT, 129], BF16, name="vv") for _ in range(2)]
        rc = [cx.sb(st, [128, 1], F32, name="rc") for _ in range(2)]
        ot = [cx.sb(st, [128, 128], F32, name="ot") for _ in range(2)]
        cnt = [0]
        QW = min(512, S)
        for h in range(4):
            a, b, c, v = qn[h % 2], qr[h % 2], kn[h % 2], vv[h % 2]
            cx.dma('sp', a[:, :], scr['qnT'][h], writes=[a])
            cx.dma('sp', b[:, :], scr['qrT'][h], writes=[b])
            cx.dma('sp', c[:, :], scr['knT'][h], writes=[c])
            cx.dma('sp', v[:, :, :], scr['va'][:, h, :].rearrange("(t p) c -> p t c", p=128), writes=[v])

            def epi(QB, j, accb, acc_ap, h=h):
                k = cnt[0]
                cnt[0] += 1
                r, o = rc[k % 2], ot[k % 2]
                cx.op('dve', [accb], [r], lambda e: e.tensor_scalar(r[:, :], acc_ap[:, 128:129], 1e-30, None, ALU.add))
                cx.op('dve', [r], [r], lambda e: e.reciprocal(r[:, :], r[:, :]))
                cx.op('act', [accb, r], [o], lambda e: e.activation(out=o[:, :], in_=acc_ap[:, 0:128], func=AF.Copy, scale=r[:, 0:1]))
                t0 = QB * QW + j * 128
                cx.dma('pool', Y_d[t0:t0 + 128, h * 128:(h + 1) * 128], o[:, :], reads=[o])

            attn_core(cx, res, S, [(a, a[:, :]), (b, b[:65, :])], [(c, c[:, :]), (krT, krT[:65, :])],
                      lambda kb, nk, v=v: (v, v[:nk, kb, :]), 129,
                      lambda QB: causal_blocks(QB, QW, cmask), epi)
    cx.barrier()


def head_norm_tm(cx, src_buf, src_ap, n, eps, cbuf, c_ap, s1, s2, junk):
    cx.op('dve', [src_buf], [s1], lambda e: e.tensor_reduce(out=s1[:, 0:1], in_=src_ap, axis=AX.X, op=ALU.add))
    cx.op('dve', [s1], [s1], lambda e: e.tensor_scalar(s1[:, 0:1], s1[:, 0:1], 1.0 / n, None, ALU.mult))
    cx.op('dve', [src_buf, s1], [cbuf], lambda e: e.tensor_scalar(c_ap, src_ap, s1[:, 0:1], None, ALU.subtract))
    rms_rstd(cx, cbuf, c_ap, n, s2, junk, eps=eps)


def phase_ret(cx, S, P_d, cos_d, sin_d, gdec_d, Y_d, scr):
    NT = S // 128
    G = min(512, S)
    NG = G // 128
    QW = min(512, S)
    with contextlib.ExitStack() as st:
        with contextlib.ExitStack() as s1:
            pr = [cx.sb(s1, [128, 1024], F32, name="pr") for _ in range(2)]
            cs_t = [cx.sb(s1, [128, 64], F32, name="cs") for _ in range(2)]
            ro = [cx.sb(s1, [128, 8, 64], F32, name="ro") for _ in range(2)]
            rb = [cx.sb(s1, [128, 8, 64], BF16, name="rb") for _ in range(2)]
            vb = [cx.sb(s1, [128, 512], BF16, name="vb") for _ in range(2)]
            t1s = [cx.sb(s1, [128, 8, 32], F32, name="t1") for _ in range(2)]
            t2s = [cx.sb(s1, [128, 8, 32], F32, name="t2") for _ in range(2)]
            gQ = cx.sb(s1, [64, 8, G], BF16, name="gQ")
            trp = TrPool(cx, s1)
            def body(tt):
                t1, t2 = t1s[tt % 2], t2s[tt % 2]
                tl = tt % NG
                rows = slice(tt * 128, (tt + 1) * 128)
                p, c_t, r, rbb, v = pr[tt % 2], cs_t[tt % 2], ro[tt % 2], rb[tt % 2], vb[tt % 2]
                cx.dma('sp', p[:, :], P_d[rows, OFF_RET:OFF_RET + 1024], writes=[p])
                cx.dma('act', c_t[:, 0:32], cos_d[rows, :], writes=[c_t])
                cx.dma('act', c_t[:, 32:64], sin_d[rows, :], writes=[c_t])
                qk = p[:, 0:512].rearrange("p (h c) -> p h c", h=8)
                cb = c_t[:, 0:32].unsqueeze(1).broadcast_to([128, 8, 32])
                sb_ = c_t[:, 32:64].unsqueeze(1).broadcast_to([128, 8, 32])
                rope_tm(cx, p, qk[:, :, 0:32], qk[:, :, 32:64], [c_t, cb], [c_t, sb_],
                        r, r[:, :, 0:32], r[:, :, 32:64], [(t1, t1[:, :, :]), (t2, t2[:, :, :])])
                cx.op('act', [r], [rbb], lambda e: e.copy(rbb[:, 0:4, :], r[:, 0:4, :]))
                cx.op('act', [r], [rbb], lambda e: e.activation(out=rbb[:, 4:8, :], in_=r[:, 4:8, :], func=AF.Copy, scale=0.125))
                cx.op('pool', [p], [v], lambda e: e.tensor_copy(v[:, :], p[:, 512:1024]))
                trp.transpose_cols(rbb, lambda j: rbb[:, j, :], 8, gQ,
                                   lambda j0, cnt: gQ[:64, j0:j0 + cnt, tl * 128:(tl + 1) * 128], blkw=64)
                cx.dma('pool', scr['rv'][rows, :, :].rearrange("s h c -> s (h c)"), v[:, :], reads=[v])
                if tl == NG - 1:
                    def post():
                        g0 = (tt // NG) * G
                        cx.dma('pool', scr['rqT'][:, :, g0:g0 + G].rearrange("h d s -> d h s"), gQ[:64, 0:4, :], reads=[gQ])
                        cx.dma('pool', scr['rkT'][:, :, g0:g0 + G].rearrange("h d s -> d h s"), gQ[:64, 4:8, :], reads=[gQ])
                    return post
            run_tiles(cx, body, NT)
        cx.barrier()
        res = AttnRes(cx, st, 128)
        gd = [cx.sb(st, [128, 5, 512], F32, name="gd") for _ in range(2)]
        qT = [cx.sb(st, [64, S], BF16, name="qT") for _ in range(2)]
        kT = [cx.sb(st, [64, S], BF16, name="kT") for _ in range(2)]
        vv = [cx.sb(st, [128, NT, 128], BF16, name="vv") for _ in range(2)]
        gt = [cx.sb(st, [128, 128], F32, name="gt") for _ in range(2)]
        cb_ = [cx.sb(st, [128, 128], F32, name="cb") for _ in range(2)]
        ot = [cx.sb(st, [128, 128], F32, name="ot") for _ in range(2)]
        sA = [cx.sb(st, [128, 1], F32, name="sA") for _ in range(2)]
        sB = [cx.sb(st, [128, 1], F32, name="sB") for _ in range(2)]
        junk = cx.sb(st, [128, 128], BF16, name="junk")
        cnt = [0]
        for h in range(4):
            gamma = 1.0 - 2.0 ** (-5 - h)
            a, c, v, g = qT[h % 2], kT[h % 2], vv[h % 2], gd[h % 2]
            cx.dma('sp', a[:, :], scr['rqT'][h], writes=[a])
            cx.dma('sp', c[:, :], scr['rkT'][h], writes=[c])
            cx.dma('sp', v[:, :, :], scr['rv'][:, h, :].rearrange("(t p) c -> p t c", p=128), writes=[v])
            cx.dma('sp', g[:, :, :], gdec_d[h].rearrange("i k q -> k i q"), writes=[g])
            if 'dbg' in scr and h == 0:
                cx.dma('sp', scr['dbg'][0], a[:, :], reads=[a])
                cx.dma('sp', scr['dbg'][1], c[:, :], reads=[c])

            def blocks(QB, g=g, gamma=gamma):
                out = []
                nd = QW // 128
                for kb in range(nd * (QB + 1)):
                    i = kb - nd * QB
                    if i >= 0:
                        out.append(dict(kb=kb, k0=kb * 128, nk=128, decay=(1.0, g, g[:, 1 + i, :QW])))
                    else:
                        cc = gamma ** (QB * QW - kb * 128)
                        if cc < 1e-30:
                            cc = 0.0
                        out.append(dict(kb=kb, k0=kb * 128, nk=128, decay=(cc, g, g[:, 0, :QW])))
                return out

            def epi(QB, j, accb, acc_ap, h=h):
                k = cnt[0]
                cnt[0] += 1
                t0 = QB * QW + j * 128
                gtb, cbb, o, s_a, s_b = gt[k % 2], cb_[k % 2], ot[k % 2], sA[k % 2], sB[k % 2]
                cx.dma('act', gtb[:, :], P_d[t0:t0 + 128, OFF_RET + 1024 + h * 128:OFF_RET + 1024 + (h + 1) * 128], writes=[gtb])
                cx.op('act', [gtb], [gtb], lambda e: e.activation(out=gtb[:, :], in_=gtb[:, :], func=AF.Silu))
                head_norm_tm(cx, accb, acc_ap, 128, NORM_EPS, cbb, cbb[:, :], s_a, s_b, junk)
                cx.op('dve', [cbb, s_b, gtb], [o],
                      lambda e: e.scalar_tensor_tensor(out=o[:, :], in0=cbb[:, :], scalar=s_b[:, 0:1], in1=gtb[:, :],
                                                       op0=ALU.mult, op1=ALU.mult))
                cx.dma('pool', Y_d[t0:t0 + 128, h * 128:(h + 1) * 128], o[:, :], reads=[o])

            attn_core(cx, res, S, [(a, a[:64, :])], [(c, c[:64, :])], lambda kb, nk, v=v: (v, v[:nk, kb, :]), 128,
                      blocks, epi, mode='decay')
    cx.barrier()


def bc8(ap):
    return ap.unsqueeze(2).broadcast_to([128, 8, 64])


def v3(ap):
    return ap.rearrange("p (h c) -> p h c", h=8)


def phase_rwkv(cx, S, l, P_d, W, Wb, C, Y_d, scr):
    NT = S // 128
    RW = scr['rw']
    names6 = ['rr', 'lw', 'k2', 'vv', 'kn', 'aa']
    with contextlib.ExitStack() as st:
        def brow(name, n, src):
            b = cx.sb(st, [128, n], F32, name=name)
            load_bcast_row(cx, 'sp', b, src, n)
            return b
        muB = brow("muB", 1984, W['rwkv_mu'][l])
        w0B = brow("w0B", 512, W['rwkv_w0'][l])
        a0B = brow("a0B", 512, W['rwkv_a0'][l])
        kkB = brow("kkB", 512, W['rwkv_k_k'][l])
        kaB = brow("kaB", 512, W['rwkv_k_a'][l])
        rkB = brow("rkB", 512, W['rwkv_r_k'][l].rearrange("h c -> (h c)"))
        w2 = cx.sb(st, [96, 512], BF16, name="w2")
        a2 = cx.sb(st, [96, 512], BF16, name="a2")
        g2 = cx.sb(st, [128, 2, 512], BF16, name="g2")
        cx.dma('sp', w2[:, :], Wb['rwkv_w2'][l], writes=[w2])
        cx.dma('sp', a2[:, :], Wb['rwkv_a2'][l], writes=[a2])
        cx.dma('sp', g2[:, :, :], Wb['rwkv_g2'][l].rearrange("(kc p) n -> p kc n", p=128), writes=[g2])
        z = [cx.sb(st, [128, 1984], F32, name="z") for _ in range(2)]
        zp = [cx.sb(st, [128, 1984], F32, name="zp") for _ in range(2)]
        lo = [cx.sb(st, [128, 512], BF16, name="lo") for _ in range(2)]
        loT = [cx.sb(st, [128, 4, 128], BF16, name="loT") for _ in range(2)]
        o7 = [cx.sb(st, [128, 7, 512], F32, name="o7") for _ in range(2)]
        s8 = [cx.sb(st, [128, 8], F32, name="s8") for _ in range(2)]
        b8 = [cx.sb(st, [128, 8], F32, name="b8") for _ in range(2)]
        trp = TrPool(cx, st)
        pps = [[cx.ps(st, [128, 512], F32, name="pp") for _ in range(3)] for _ in range(2)]
        def body(tt):
            rows = slice(tt * 128, (tt + 1) * 128)
            zb, zpb, lob, loTb, o, s8b, b8b = z[tt % 2], zp[tt % 2], lo[tt % 2], loT[tt % 2], o7[tt % 2], s8[tt % 2], b8[tt % 2]
            cx.dma('sp', zb[:, :], P_d[rows, OFF_RWKV:OFF_RWKV + 1984], writes=[zb])
            if tt == 0:
                cx.op('pool', [], [zpb], lambda e: e.memset(zpb[0:1, :], 0.0))
                cx.dma('act', zpb[1:128, :], P_d[0:127, OFF_RWKV:OFF_RWKV + 1984], writes=[zpb])
            else:
                cx.dma('act', zpb[:, :], P_d[tt * 128 - 1:tt * 128 + 127, OFF_RWKV:OFF_RWKV + 1984], writes=[zpb])
            cx.op('dve', [zpb, zb], [zpb], lambda e: e.tensor_tensor(zpb[:, :], zpb[:, :], zb[:, :], ALU.subtract))
            cx.op('dve', [zpb, muB], [zpb], lambda e: e.tensor_tensor(zpb[:, :], zpb[:, :], muB[:, :], ALU.mult))
            cx.op('pool', [zpb, zb], [zb], lambda e: e.tensor_tensor(zb[:, :], zb[:, :], zpb[:, :], ALU.add))
            r_, k_, v_ = zb[:, 0:512], zb[:, 512:1024], zb[:, 1024:1536]
            cx.op('act', [zb], [lob], lambda e: e.activation(out=lob[:, 0:96], in_=zb[:, 1536:1632], func=AF.Tanh))
            cx.op('act', [zb], [lob], lambda e: e.copy(lob[:, 128:224], zb[:, 1632:1728]))
            cx.op('act', [zb], [lob], lambda e: e.activation(out=lob[:, 256:512], in_=zb[:, 1728:1984], func=AF.Sigmoid))
            trp.transpose_cols(lob, lambda j: lob[:, j * 128:j * 128 + 96], 2, loTb,
                               lambda j0, cnt: loTb[:96, j0:j0 + cnt, :], blkw=96)
            trp.transpose_cols(lob, lambda j: lob[:, 256 + j * 128:384 + j * 128], 2, loTb,
                               lambda j0, cnt: loTb[:, 2 + j0:2 + j0 + cnt, :])
            pu, pa, pg = pps[tt % 2]
            mm(cx, pu, pu[:, :], loTb[:96, 0, :], w2[:96, :], [loTb, w2], True, True)
            mm(cx, pa, pa[:, :], loTb[:96, 1, :], a2[:96, :], [loTb, a2], True, True)
            mm(cx, pg, pg[:, :], loTb[:, 2, :], g2[:, 0, :], [loTb, g2], True, False)
            mm(cx, pg, pg[:, :], loTb[:, 3, :], g2[:, 1, :], [loTb, g2], False, True)
            lw_, k2_, kn_, aa_, gg_, t1_, t2_ = [o[:, i, :] for i in range(7)]
            cx.op('dve', [pu, w0B], [o], lambda e: e.tensor_tensor(t1_, pu[:, :], w0B[:, :], ALU.add))
            cx.op('act', [o], [o], lambda e: e.activation(out=t1_, in_=t1_, func=AF.Sigmoid))
            cx.op('pool', [o], [o], lambda e: e.tensor_scalar(lw_, t1_, -0.6065306597126334, None, ALU.mult))
            cx.op('dve', [pa, a0B], [o], lambda e: e.tensor_tensor(t2_, pa[:, :], a0B[:, :], ALU.add))
            cx.op('act', [o], [o], lambda e: e.activation(out=aa_, in_=t2_, func=AF.Sigmoid))
            cx.op('act', [pg], [o], lambda e: e.copy(gg_, pg[:, :]))
            cx.op('dve', [zb, kkB], [o], lambda e: e.tensor_tensor(kn_, k_, kkB[:, :], ALU.mult))
            cx.op('pool', [o], [o], lambda e: e.tensor_tensor(t1_, kn_, kn_, ALU.mult))
            cx.op('dve', [o], [s8b], lambda e: e.tensor_reduce(out=s8b[:, :], in_=v3(t1_), axis=AX.X, op=ALU.add))
            cx.op('dve', [s8b], [s8b], lambda e: e.tensor_scalar(s8b[:, :], s8b[:, :], 1e-24, None, ALU.max))
            cx.op('pool', [s8b, cx.neghalf], [s8b],
                  lambda e: e.tensor_tensor(s8b[:, :], s8b[:, :], cx.neghalf[:, 0:1].to_broadcast([128, 8]), ALU.pow))
            cx.op('dve', [o, s8b], [o], lambda e: e.tensor_tensor(v3(kn_), v3(kn_), bc8(s8b[:, :]), ALU.mult))
            cx.op('dve', [o, kaB], [o],
                  lambda e: e.scalar_tensor_tensor(out=t2_, in0=aa_, scalar=-1.0, in1=kaB[:, :], op0=ALU.add, op1=ALU.mult))
            cx.op('pool', [o], [o], lambda e: e.tensor_scalar(t2_, t2_, 1.0, None, ALU.add))
            cx.op('dve', [o, zb], [o], lambda e: e.tensor_tensor(k2_, k_, t2_, ALU.mult))
            cx.op('pool', [o, zb], [o], lambda e: e.tensor_tensor(t1_, r_, k2_, ALU.mult))
            cx.op('dve', [o, rkB], [o], lambda e: e.tensor_tensor(t1_, t1_, rkB[:, :], ALU.mult))
            cx.op('dve', [o], [b8b], lambda e: e.tensor_reduce(out=b8b[:, :], in_=v3(t1_), axis=AX.X, op=ALU.add))
            cx.dma('pool', RW['rr'][rows, :], r_, reads=[zb])
            cx.dma('pool', RW['vv'][rows, :], v_, reads=[zb])
            cx.dma('pool', RW['lw'][rows, :], lw_, reads=[o])
            cx.dma('pool', RW['k2'][rows, :], k2_, reads=[o])
            cx.dma('pool', RW['kn'][rows, :], kn_, reads=[o])
            cx.dma('pool', RW['aa'][rows, :], aa_, reads=[o])
            cx.dma('pool', RW['gg'][rows, :], gg_, reads=[o])
            cx.dma('pool', RW['bc'][rows, :], b8b[:, :], reads=[b8b])
        run_tiles(cx, body, NT)
    cx.barrier()
    with contextlib.ExitStack() as st:
        rwm = cx.sb(st, [128, 384], F32, name="rwm")
        cx.dma('sp', rwm[:, :], C['rwm'][:, :], writes=[rwm])
        mask4 = cx.sb(st, [128, 512], F32, name="mask4")
        cx.op('pool', [rwm], [mask4], lambda e: e.tensor_copy(mask4[:, 0:256], rwm[:, 0:256]))
        cx.op('pool', [rwm], [mask4], lambda e: e.tensor_copy(mask4[:, 256:512], rwm[:, 0:256]))
        gwB = cx.sb(st, [128, 512], F32, name="gwB")
        gbB = cx.sb(st, [128, 512], F32, name="gbB")
        load_bcast_row(cx, 'sp', gwB, W['rwkv_gn_w'][l], 512)
        load_bcast_row(cx, 'sp', gbB, W['rwkv_gn_b'][l], 512)
        IN = [cx.sb(st, [128, 6, 512], F32, name="IN") for _ in range(2)]
        ELs = [cx.sb(st, [128, 3, 512], F32, name="EL") for _ in range(2)]
        TMs = [cx.sb(st, [128, 4, 512], F32, name="TM") for _ in range(2)]
        XTs = [cx.sb(st, [64, 8, 4, 128], F32, name="XT") for _ in range(2)]
        MMs = [cx.sb(st, [128, 8, 512], F32, name="MM") for _ in range(2)]
        XXs = [[cx.sb(st, [128, 8, 2, 128], F32, name="XX") for _ in range(2)] for _ in range(2)]
        NTs = [cx.sb(st, [128, 8, 128], F32, name="NT") for _ in range(2)]
        pcs = [cx.sb(st, [64, 8], F32, name="pc") for _ in range(2)]
        ST = cx.sb(st, [64, 8, 64], F32, name="ST")
        STs = cx.sb(st, [64, 8, 64], F32, name="STs")
        Yb = cx.sb(st, [128, 8, 64], F32, name="Yb")
        Ub = cx.sb(st, [128, 8, 64], F32, name="Ub")
        Ob = cx.sb(st, [128, 512], F32, name="Ob")
        G3 = [cx.sb(st, [128, 512], F32, name="G3") for _ in range(2)]
        b8 = [cx.sb(st, [128, 8], F32, name="b8") for _ in range(2)]
        m8 = cx.sb(st, [128, 8], F32, name="m8")
        r8 = cx.sb(st, [128, 8], F32, name="r8")
        t512 = cx.sb(st, [128, 512], F32, name="t512")
        yo = [cx.sb(st, [128, 512], F32, name="yo") for _ in range(2)]
        PTp = [cx.ps(st, [128, 512], F32, name="ptr") for _ in range(2)]
        PDp = [cx.ps(st, [128, 4, 128], F32, name="pD") for _ in range(3)]
        pY = cx.ps(st, [128, 512], F32, name="pY")
        pU = cx.ps(st, [128, 512], F32, name="pU")
        pO = cx.ps(st, [128, 512], F32, name="pO")
        cnt = {'pt': 0, 'pd': 0}

        def get_pt():
            cnt['pt'] += 1
            return PTp[cnt['pt'] % 2]

        def get_pd():
            cnt['pd'] += 1
            return PDp[cnt['pd'] % 3]

        cx.op('pool', [], [ST], lambda e: e.memset(ST[:, :, :], 0.0))
        MUs, MUi, MLs = rwm[:, 0:128], rwm[:, 128:256], rwm[:, 256:384]

        def pre(c):
            rows = slice(c * 128, (c + 1) * 128)
            I6, EL, TM, XT, MM_, XX, NTb, pc = IN[c % 2], ELs[c % 2], TMs[c % 2], XTs[c % 2], MMs[c % 2], XXs[c % 2], NTs[c % 2], pcs[c % 2]
            for i, nm in enumerate(names6):
                cx.dma('sp' if i % 2 == 0 else 'act', I6[:, i, :], RW[nm][rows, :], writes=[I6])
            rr, lw, k2, vv, kn, aa = [I6[:, i, :] for i in range(6)]

            def u_cumsum():
                pL = get_pt()
                mm(cx, pL, pL[:, :], MUi, lw, [rwm, I6], True, True)
                cx.op('act', [pL], [EL], lambda e: e.activation(out=EL[:, 0, :], in_=pL[:, :], func=AF.Exp))
                cx.op('act', [pL], [EL], lambda e: e.activation(out=EL[:, 1, :], in_=pL[:, :], func=AF.Exp, scale=-1.0))
                cx.op('dve', [pL, I6], [EL], lambda e: e.tensor_tensor(EL[:, 2, :], pL[:, :], lw, ALU.subtract))
            atomic(cx, u_cumsum)
            cx.op('act', [EL], [EL], lambda e: e.activation(out=EL[:, 2, :], in_=EL[:, 2, :], func=AF.Exp))
            cx.op('dve', [I6, EL], [TM],
                  lambda e: e.scalar_tensor_tensor(out=TM[:, 0, :], in0=kn, scalar=-1.0, in1=EL[:, 2, :], op0=ALU.mult, op1=ALU.mult))
            cx.op('pool', [I6, EL], [TM], lambda e: e.tensor_tensor(TM[:, 1, :], rr, EL[:, 0, :], ALU.mult))
            cx.op('dve', [I6], [TM], lambda e: e.tensor_tensor(TM[:, 2, :], kn, aa, ALU.mult))
            cx.op('dve', [TM, EL], [TM], lambda e: e.tensor_tensor(TM[:, 2, :], TM[:, 2, :], EL[:, 1, :], ALU.mult))
            cx.op('pool', [I6, EL], [TM], lambda e: e.tensor_tensor(TM[:, 3, :], k2, EL[:, 1, :], ALU.mult))

            def u_pc():
                ppc = get_pt()
                for h in range(8):
                    mm(cx, ppc, ppc[:64, h:h + 1], lw[:, h * 64:(h + 1) * 64], cx.ones_f[:, 0:1], [I6, cx.ones_f], True, True)
                cx.op('act', [ppc], [pc], lambda e: e.activation(out=pc[:, :], in_=ppc[:64, 0:8], func=AF.Exp))
            atomic(cx, u_pc)

            def u_tr(q, hh, k):
                pt = get_pt()
                for j in range(4):
                    h = hh * 4 + j
                    transp(cx, pt, pt[:64, j * 128:(j + 1) * 128], TM[:, q, h * 64:(h + 1) * 64], cx.identf[:, :], [TM, cx.identf])
                evac(cx, 'act' if k % 2 else 'dve', pt, pt[:64, :].rearrange("p (j t) -> p j t", j=4), XT,
                     fr(XT[:, hh * 4:(hh + 1) * 4, q, :]))
            k = 0
            for q in range(4):
                for hh in range(2):
                    k += 1
                    atomic(cx, lambda q=q, hh=hh, k=k: u_tr(q, hh, k))

            def u_setup(h):
                pA = get_pt()
                pd = get_pd()
                ar = XT[:, h, 0:2, :].rearrange("p q t -> p (q t)")
                mm(cx, pA, pA[:, 0:256], fr(XT[:, h, 2, :]), fr(ar), [XT], True, True)
                mm(cx, pA, pA[:, 256:512], fr(XT[:, h, 3, :]), fr(ar), [XT], True, True)
                mm(cx, pd, pd[:, 0, :], fr(XT[:, h, 0, :]), fr(XT[:, h, 2, :]), [XT], True, True)
                cx.op('dve', [pA, mask4], [MM_], lambda e: e.tensor_tensor(MM_[:, h, :], pA[:, :], mask4[:, :], ALU.mult))
                cx.op('dve', [pd, rwm], [XX[0]], lambda e: e.tensor_tensor(fr(XX[0][:, h, 1, :]), pd[:, 0, :], MLs, ALU.mult))
                cx.op('pool', [MM_], [XX[0]], lambda e: e.tensor_copy(fr(XX[0][:, h, 0, :]), MM_[:, h, 0:128]))
                cx.op('pool', [MM_, cx.identf], [NTb], lambda e: e.tensor_tensor(fr(NTb[:, h, :]), MM_[:, h, 0:128], cx.identf[:, :], ALU.add))
            for h in range(8):
                atomic(cx, lambda h=h: u_setup(h))

            def u_sq(lev, p):
                cur, nxt = XX[(lev - 1) % 2], XX[lev % 2]
                pd = get_pd()
                for j in range(2):
                    h = 2 * p + j
                    if lev < 6:
                        mm(cx, pd, pd[:, 2 * j, :], fr(cur[:, h, 1, :]), fr(cur[:, h, 0, :]), [cur], True, True)
                    mm(cx, pd, pd[:, 2 * j + 1, :], fr(cur[:, h, 0, :]), fr(cur[:, h, 1, :]), [cur], True, True)
                if lev < 6:
                    evac(cx, 'act' if p % 2 else 'dve', pd, pd[:, :, :], nxt,
                         fr(nxt[:, 2 * p:2 * p + 2, :, :].rearrange("p h q t -> p (h q) t")))
                else:
                    for j in range(2):
                        evac(cx, 'act' if j else 'dve', pd, pd[:, 2 * j + 1, :], nxt, fr(nxt[:, 2 * p + j, 1, :]))

            def u_n(lev, p):
                nxt = XX[lev % 2]
                pd = get_pd()
                for j in range(2):
                    h = 2 * p + j
                    mm(cx, pd, pd[:, j, :], fr(nxt[:, h, 1, :]), fr(NTb[:, h, :]), [nxt, NTb], True, True)
                cx.op('dve', [pd, NTb], [NTb],
                      lambda e: e.tensor_tensor(fr(NTb[:, 2 * p:2 * p + 2, :]), NTb[:, 2 * p:2 * p + 2, :], pd[:, 0:2, :], ALU.add))
            for lev in range(1, 7):
                for p in range(4):
                    atomic(cx, lambda lev=lev, p=p: u_sq(lev, p))
                for p in range(4):
                    atomic(cx, lambda lev=lev, p=p: u_n(lev, p))

        def seq(c):
            rows = slice(c * 128, (c + 1) * 128)
            I6, TM, XT, MM_, NTb, pc = IN[c % 2], TMs[c % 2], XTs[c % 2], MMs[c % 2], NTs[c % 2], pcs[c % 2]
            vv = I6[:, 3, :]
            cx.op('pool', [ST, pc], [STs],
                  lambda e: e.tensor_tensor(STs[:, :, :], ST[:, :, :], pc[:, :].unsqueeze(2).broadcast_to([64, 8, 64]), ALU.mult))
            for h in range(8):
                hs = slice(h * 64, (h + 1) * 64)
                mm(cx, pY, pY[:, hs], XT[:, h, 0, :], ST[:, h, :], [XT, ST], True, False)
                mm(cx, pY, pY[:, hs], MM_[:, h, 256:384], vv[:, hs], [MM_, I6], False, True)
            evac(cx, 'dve', pY, pY[:, 0:256], Yb, Yb[:, 0:4, :].rearrange("p h c -> p (h c)"))
            evac(cx, 'act', pY, pY[:, 256:512], Yb, Yb[:, 4:8, :].rearrange("p h c -> p (h c)"))
            for h in range(8):
                hs = slice(h * 64, (h + 1) * 64)
                mm(cx, pU, pU[:, hs], NTb[:, h, :], Yb[:, h, :], [NTb, Yb], True, True)
            evac(cx, 'dve', pU, pU[:, 0:256], Ub, Ub[:, 0:4, :].rearrange("p h c -> p (h c)"))
            evac(cx, 'act', pU, pU[:, 256:512], Ub, Ub[:, 4:8, :].rearrange("p h c -> p (h c)"))
            for h in range(8):
                hs = slice(h * 64, (h + 1) * 64)
                mm(cx, pY, pY[:64, hs], TM[:, 2, hs], Ub[:, h, :], [TM, Ub], True, False)
                mm(cx, pY, pY[:64, hs], TM[:, 3, hs], vv[:, hs], [TM, I6], False, True)
            for h in range(8):
                hs = slice(h * 64, (h + 1) * 64)
                mm(cx, pO, pO[:, hs], XT[:, h, 1, :], ST[:, h, :], [XT, ST], True, False)
                mm(cx, pO, pO[:, hs], MM_[:, h, 128:256], Ub[:, h, :], [MM_, Ub], False, False)
                mm(cx, pO, pO[:, hs], MM_[:, h, 384:512], vv[:, hs], [MM_, I6], False, True)
            cx.op('dve', [pY, pc], [ST],
                  lambda e: e.tensor_tensor(ST[:, :, :], pY[:64, :].rearrange("p (h c) -> p h c", h=8),
                                            pc[:, :].unsqueeze(2).broadcast_to([64, 8, 64]), ALU.mult))
            cx.op('dve', [ST, STs], [ST], lambda e: e.tensor_tensor(ST[:, :, :], ST[:, :, :], STs[:, :, :], ALU.add))
            evac(cx, 'act', pO, pO[:, :], Ob, Ob[:, :])
            g3, b8b, y = G3[c % 2], b8[c % 2], yo[c % 2]
            cx.dma('sp', g3[:, :], RW['gg'][rows, :], writes=[g3])
            cx.dma('act', b8b[:, :], RW['bc'][rows, :], writes=[b8b])
            cx.op('dve', [Ob], [m8], lambda e: e.tensor_reduce(out=m8[:, :], in_=v3(Ob[:, :]), axis=AX.X, op=ALU.add))
            cx.op('dve', [m8], [m8], lambda e: e.tensor_scalar(m8[:, :], m8[:, :], 1.0 / 64, None, ALU.mult))
            cx.op('dve', [Ob, m8], [Ob], lambda e: e.tensor_tensor(v3(Ob[:, :]), v3(Ob[:, :]), bc8(m8[:, :]), ALU.subtract))
            cx.op('pool', [Ob], [t512], lambda e: e.tensor_tensor(t512[:, :], Ob[:, :], Ob[:, :], ALU.mult))
            cx.op('dve', [t512], [r8], lambda e: e.tensor_reduce(out=r8[:, :], in_=v3(t512[:, :]), axis=AX.X, op=ALU.add))
            cx.op('dve', [r8], [r8], lambda e: e.tensor_scalar(r8[:, :], r8[:, :], 1.0 / 64, 64e-5, ALU.mult, ALU.add))
            cx.op('pool', [r8, cx.neghalf], [r8],
                  lambda e: e.tensor_tensor(r8[:, :], r8[:, :], cx.neghalf[:, 0:1].to_broadcast([128, 8]), ALU.pow))
            cx.op('dve', [Ob, r8], [y], lambda e: e.tensor_tensor(v3(y[:, :]), v3(Ob[:, :]), bc8(r8[:, :]), ALU.mult))
            cx.op('pool', [y, gwB], [y], lambda e: e.tensor_tensor(y[:, :], y[:, :], gwB[:, :], ALU.mult))
            cx.op('pool', [y, gbB], [y], lambda e: e.tensor_tensor(y[:, :], y[:, :], gbB[:, :], ALU.add))
            cx.op('dve', [I6, b8b], [t512], lambda e: e.tensor_tensor(v3(t512[:, :]), v3(vv), bc8(b8b[:, :]), ALU.mult))
            cx.op('pool', [y, t512], [y], lambda e: e.tensor_tensor(y[:, :], y[:, :], t512[:, :], ALU.add))
            cx.op('dve', [y, g3], [y], lambda e: e.tensor_tensor(y[:, :], y[:, :], g3[:, :], ALU.mult))
            cx.dma('pool', Y_d[rows, :], y[:, :], reads=[y])

        for c0 in range(0, NT, 2):
            cs = list(range(c0, min(NT, c0 + 2)))
            lists = []
            for c in cs:
                cx._rec = []
                pre(c)
                lists.append(cx._rec)
                cx._rec = None
            for i in range(max(len(l_) for l_ in lists)):
                for l_ in lists:
                    if i < len(l_):
                        l_[i]()
            for c in cs:
                seq(c)
    cx.barrier()


NSA_SCALE = 128 ** -0.5
NSA_BIG = 30000.0
NSA_STOP = 0


def phase_nsa(cx, S, l, P_d, W, Wb, C, Youts, scr):
    NT = S // 128
    G = min(512, S)
    NG = G // 128
    QW = min(512, S)
    NJ = QW // 128
    Nc = (S - 32) // 16 + 1
    NKB = (Nc + 127) // 128
    N = scr['nsa']
    with contextlib.ExitStack() as st:
        pn = [cx.sb(st, [128, 1292], F32, name="pn") for _ in range(2)]
        cs_t = [cx.sb(st, [128, 32], F32, name="cs") for _ in range(2)]
        ro = [cx.sb(st, [128, 10, 32], F32, name="ro") for _ in range(2)]
        t1s = [cx.sb(st, [128, 10, 16], F32, name="t1") for _ in range(2)]
        t2s = [cx.sb(st, [128, 10, 16], F32, name="t2") for _ in range(2)]
        fb = [cx.sb(st, [128, 8, 128], BF16, name="fb") for _ in range(2)]
        va = [cx.sb(st, [128, 2, 129], BF16, name="va") for _ in range(2)]
        sqs = [cx.sb(st, [128, 6, 128], F32, name="sq") for _ in range(2)]
        n6 = [cx.sb(st, [128, 6], F32, name="n6") for _ in range(2)]
        nqb = [cx.sb(st, [128, 4], BF16, name="nqb") for _ in range(2)]
        gt = [cx.sb(st, [128, 12], F32, name="gt") for _ in range(2)]
        kmx = cx.sb(st, [128, 2], F32, name="kmx")
        gA = cx.sb(st, [128, 8, G], BF16, name="gA")
        gN = cx.sb(st, [1, 4, G], BF16, name="gN")
        trp = TrPool(cx, st)
        trf = TrPool(cx, st, n=1, dtype=F32)
        cx.op('pool', [], [kmx], lambda e: e.memset(kmx[:, :], 0.0))
        def body(tt):
            t1, t2, sq = t1s[tt % 2], t2s[tt % 2], sqs[tt % 2]
            tl = tt % NG
            rows = slice(tt * 128, (tt + 1) * 128)
            p, c_t, r, f, v, n6b, nq_, g_ = pn[tt % 2], cs_t[tt % 2], ro[tt % 2], fb[tt % 2], va[tt % 2], n6[tt % 2], nqb[tt % 2], gt[tt % 2]
            cx.dma('sp', p[:, :], P_d[rows, OFF_NSA:OFF_NSA + 1292], writes=[p])
            cx.dma('act', c_t[:, 0:16], C['nsa_cos'][rows, :], writes=[c_t])
            cx.dma('act', c_t[:, 16:32], C['nsa_sin'][rows, :], writes=[c_t])
            blk = p[:, 0:1280].rearrange("p (b c) -> p b c", b=10)
            cb = c_t[:, 0:16].unsqueeze(1).broadcast_to([128, 10, 16])
            sb_ = c_t[:, 16:32].unsqueeze(1).broadcast_to([128, 10, 16])
            rope_tm(cx, p, blk[:, :, 0:16], blk[:, :, 16:32], [c_t, cb], [c_t, sb_],
                    r, r[:, :, 0:16], r[:, :, 16:32], [(t1, t1[:, :, :]), (t2, t2[:, :, :])])
            cx.op('act', [p], [f], lambda e: e.activation(out=f[:, 0:4, 32:128], in_=blk[:, 0:4, 32:128], func=AF.Copy, scale=NSA_SCALE))
            cx.op('act', [r], [f], lambda e: e.activation(out=f[:, 0:4, 0:32], in_=r[:, 0:4, :], func=AF.Copy, scale=NSA_SCALE))
            for dst, src in ((4, 4), (6, 6), (7, 8)):
                cx.op('pool', [p], [f], lambda e, dst=dst, src=src: e.tensor_copy(f[:, dst, 32:128], blk[:, src, 32:128]))
                cx.op('pool', [r], [f], lambda e, dst=dst, src=src: e.tensor_copy(f[:, dst, 0:32], r[:, src, :]))
            cx.op('pool', [p], [f], lambda e: e.tensor_copy(f[:, 5, :], blk[:, 5, :]))
            cx.op('pool', [p], [v], lambda e: e.tensor_copy(v[:, 0, 0:128], blk[:, 7, :]))
            cx.op('pool', [p], [v], lambda e: e.tensor_copy(v[:, 1, 0:128], blk[:, 9, :]))
            cx.op('pool', [], [v], lambda e: e.memset(v[:, :, 128:129], 1.0))
            cx.op('dve', [p], [sq], lambda e: e.tensor_tensor(sq[:, 0:4, :], blk[:, 0:4, :], blk[:, 0:4, :], ALU.mult))
            cx.op('dve', [p], [sq], lambda e: e.tensor_tensor(sq[:, 4, :], blk[:, 6, :], blk[:, 6, :], ALU.mult))
            cx.op('dve', [p], [sq], lambda e: e.tensor_tensor(sq[:, 5, :], blk[:, 8, :], blk[:, 8, :], ALU.mult))
            cx.op('dve', [sq], [n6b], lambda e: e.tensor_reduce(out=n6b[:, :], in_=sq[:, :, :], axis=AX.X, op=ALU.add))
            cx.op('dve', [n6b, kmx], [kmx], lambda e: e.tensor_tensor(kmx[:, :], kmx[:, :], n6b[:, 4:6], ALU.max))
            cx.op('act', [n6b], [n6b], lambda e: e.activation(out=n6b[:, 0:4], in_=n6b[:, 0:4], func=AF.Sqrt))
            cx.op('dve', [n6b], [nq_], lambda e: e.tensor_scalar(nq_[:, :], n6b[:, 0:4], -NSA_SCALE, None, ALU.mult))
            cx.dma('sp', g_[:, :], P_d[rows, OFF_NSA + 1280:OFF_NSA + 1292], writes=[g_])
            cx.op('act', [g_], [g_], lambda e: e.activation(out=g_[:, :], in_=g_[:, :], func=AF.Sigmoid))
            cx.dma('pool', N['ng'][rows, :], g_[:, :], reads=[g_])
            trp.transpose_cols(f, lambda j: f[:, j, :], 8, gA, lambda j0, cnt: gA[:, j0:j0 + cnt, tl * 128:(tl + 1) * 128])
            trp.transpose_cols(nq_, lambda j: nq_[:, j:j + 1], 4, gN,
                               lambda j0, cnt: gN[0:1, j0:j0 + cnt, tl * 128:(tl + 1) * 128], blkw=1)
            cx.dma('pool', N['vsa'][rows, :], v[:, 0, :], reads=[v])
            cx.dma('pool', N['vwa'][rows, :], v[:, 1, :], reads=[v])
            if tl == NG - 1:
                def post():
                    g0 = (tt // NG) * G
                    cx.dma('pool', N['qT'][:, :, g0:g0 + G].rearrange("h d s -> d h s"), gA[:, 0:4, :], reads=[gA])
                    for j, nm in ((4, 'kcT'), (5, 'vcT'), (6, 'ksT'), (7, 'kwT')):
                        cx.dma('pool', N[nm][:, g0:g0 + G], gA[:, j, :], reads=[gA])
                    cx.dma('pool', N['nq'][:, g0:g0 + G].rearrange("(o h) s -> o h s", o=1), gN[0:1, :, :], reads=[gN])
                return post
        run_tiles(cx, body, NT)
        kms = cx.sb(st, [128, 1], F32, name="kms")
        kmw = cx.sb(st, [128, 1], F32, name="kmw")
        k1 = cx.sb(st, [128, 1], F32, name="k1")
        rowsb = cx.sb(st, [1, 2, 128], BF16, name="rowsb")
        for i, dstc in enumerate((kms, kmw)):
            cx.op('pool', [kmx], [k1], lambda e: e.tensor_copy(k1[:, :], kmx[:, i:i + 1]))
            bcast_scalar_max(cx, st, trf, k1, dstc)
            cx.op('act', [dstc], [dstc], lambda e: e.activation(out=dstc[:, :], in_=dstc[:, :], func=AF.Sqrt))
            cx.op('dve', [dstc], [rowsb], lambda e: e.tensor_copy(rowsb[0:1, i, :], dstc[0:1, 0:1].to_broadcast([1, 128])))
        cx.dma('pool', N['krow'].rearrange("a b -> (a b)").rearrange("(o n) -> o n", o=1), rowsb[0:1, :, :].rearrange("o a b -> o (a b)"), reads=[rowsb])
    cx.barrier()
    if NSA_STOP == 1:
        return
    with contextlib.ExitStack() as st:
        KC = cx.sb(st, [128, 256], BF16, name="KC")
        VCA = cx.sb(st, [128, 2, 193], BF16, name="VCA")
        krow = cx.sb(st, [128, 3, 128], BF16, name="krow")
        cx.op('pool', [], [krow], lambda e: e.memset(krow[:, :, :], 0.0))
        cx.dma('sp', krow[0:1, 0:2, :].rearrange("o a b -> o (a b)"), N['krow'].rearrange("a b -> (a b)").rearrange("(o n) -> o n", o=1), writes=[krow])
        cx.dma('sp', VCA[:, :, 129:193], C['cover'].rearrange("(kb p) j -> p kb j", p=128), writes=[VCA])
        cx.op('pool', [], [VCA], lambda e: e.memset(VCA[:, :, 0:129], 0.0))
        cx.op('pool', [], [VCA], lambda e: e.memset(VCA[:, :, 128:129], 1.0))
        cx.op('pool', [], [KC], lambda e: e.memset(KC[:, :], 0.0))
        with contextlib.ExitStack() as s2:
            pm = [cx.ps(s2, [128, 512], F32, name="pm") for _ in range(2)]
            xT = [cx.sb(s2, [128, S], BF16, name="xT") for _ in range(2)]
            cx.dma('sp', xT[0][:, :], N['kcT'][:, :], writes=[xT[0]])
            cx.dma('act', xT[1][:, :], N['vcT'][:, :], writes=[xT[1]])
            w1 = [cx.sb(s2, [128, 32, 128], BF16, name="w1") for _ in range(2)]
            w2 = [cx.sb(s2, [128, 128], BF16, name="w2") for _ in range(2)]
            posf = cx.sb(s2, [32, 2, 128], F32, name="posf")
            posb = cx.sb(s2, [32, 2, 128], BF16, name="posb")
            posT = cx.sb(s2, [128, 2, 32], BF16, name="posT")
            bias = cx.sb(s2, [128, 2], F32, name="bias")
            xs = cx.sb(s2, [128, 256], F32, name="xs")
            x2 = cx.sb(s2, [128, 256], F32, name="x2")
            hid = [cx.sb(s2, [128, 256], BF16, name="hid") for _ in range(2)]
            ksq = cx.sb(s2, [128, 256], BF16, name="ksq")
            one = cx.sb(s2, [1, 2], F32, name="one")
            trp = TrPool(cx, s2, n=1)
            for z in range(2):
                cx.dma('sp', w1[z][:, :, :], Wb['nsa_cmp_w1'][l][z].rearrange("(l d) e -> d l e", d=128), writes=[w1[z]])
                cx.dma('sp', w2[z][:, :], Wb['nsa_cmp_w2'][l][z], writes=[w2[z]])
            cx.dma('sp', posf[:, :, :], W['nsa_cmp_pos'][l].rearrange("z l d -> l z d"), writes=[posf])
            cx.op('dve', [posf], [posb], lambda e: e.tensor_copy(posb[:, :, :], posf[:, :, :]))
            trp.transpose_cols(posb, lambda j: posb[:, j, :], 2, posT, lambda j0, cnt: posT[:, j0:j0 + cnt, :], rows=32)
            for z in range(2):
                pb, ph = pm
                for ll in range(32):
                    mm(cx, pb, pb[:, z:z + 1], w1[z][:, ll, :], posT[:, z, ll:ll + 1], [w1[z], posT], ll == 0, ll == 31)
                evac(cx, 'dve', pb, pb[:, z:z + 1], bias, bias[:, z:z + 1])
                for ll in range(32):
                    mm(cx, ph, ph[:, :Nc], w1[z][:, ll, :], xT[z][:, ll:ll + 16 * (Nc - 1) + 1:16], [w1[z], xT[z]], ll == 0, ll == 31)
                cx.op('act', [ph, bias], [xs], lambda e: e.activation(out=xs[:, :Nc], in_=ph[:, :Nc], func=AF.Identity, bias=bias[:, z:z + 1]))
                cx.op('dve', [xs], [x2], lambda e: e.tensor_tensor(x2[:, :Nc], xs[:, :Nc], xs[:, :Nc], ALU.mult))
                cx.op('dve', [x2], [x2], lambda e: e.tensor_scalar(x2[:, :Nc], x2[:, :Nc], 0.044715, 1.0, ALU.mult, ALU.add))
                cx.op('dve', [x2, xs], [x2], lambda e: e.tensor_tensor(x2[:, :Nc], x2[:, :Nc], xs[:, :Nc], ALU.mult))
                cx.op('act', [x2], [x2], lambda e: e.activation(out=x2[:, :Nc], in_=x2[:, :Nc], func=AF.Tanh, scale=0.7978845608028654))
                cx.op('dve', [x2], [x2], lambda e: e.tensor_scalar(x2[:, :Nc], x2[:, :Nc], 1.0, 0.5, ALU.add, ALU.mult))
                cx.op('dve', [x2, xs], [hid[z]], lambda e: e.tensor_tensor(hid[z][:, :Nc], x2[:, :Nc], xs[:, :Nc], ALU.mult))
            pk = pm[0]
            mm(cx, pk, pk[:, :Nc], w2[0][:, :], hid[0][:, :Nc], [w2[0], hid[0]], True, True)
            evac(cx, 'act', pk, pk[:, :Nc], KC, KC[:, :Nc])
            cx.op('act', [pk], [ksq], lambda e: e.activation(out=ksq[:, :Nc], in_=pk[:, :Nc], func=AF.Square))
            pr = pm[1]
            mm(cx, pr, pr[0:1, :Nc], cx.ones_bf[:, 0:1], ksq[:, :Nc], [cx.ones_bf, ksq], True, True)
            cx.op('dve', [pr], [one], lambda e: e.tensor_reduce(out=one[0:1, 0:1], in_=pr[0:1, :Nc], axis=AX.X, op=ALU.max))
            cx.op('act', [one], [one], lambda e: e.activation(out=one[0:1, 0:1], in_=one[0:1, 0:1], func=AF.Sqrt))
            cx.op('dve', [one], [krow], lambda e: e.tensor_scalar(krow[0:1, 2, :], one[0:1, 0:1].to_broadcast([1, 128]), 1.02, None, ALU.mult))
            for kb in range(NKB):
                nk = min(128, Nc - kb * 128)
                pv = pm[kb % 2]
                mm(cx, pv, pv[:nk, 0:128], hid[1][:, kb * 128:kb * 128 + nk], w2[1][:, :], [hid[1], w2[1]], True, True)
                evac(cx, 'dve', pv, pv[:nk, 0:128], VCA, VCA[:nk, kb, 0:128])
        cx.barrier()
        if NSA_STOP == 2:
            return
        res = AttnRes(cx, st, 193)
        qT = [cx.sb(st, [128, S], BF16, name="qT") for _ in range(4)]
        nq = [cx.sb(st, [128, S], BF16, name="nq") for _ in range(4)]
        for h in range(4):
            cx.op('pool', [], [nq[h]], lambda e: e.memset(nq[h][:, :], 0.0))
            cx.dma('sp', qT[h][:, :], N['qT'][h], writes=[qT[h]])
            cx.dma('act', nq[h][0:1, :], N['nq'][h:h + 1, :], writes=[nq[h]])
        ksT = cx.sb(st, [128, S], BF16, name="ksT")
        kwT = cx.sb(st, [128, S], BF16, name="kwT")
        cx.dma('sp', ksT[:, :], N['ksT'][:, :], writes=[ksT])
        cx.dma('act', kwT[:, :], N['kwT'][:, :], writes=[kwT])
        vsa = cx.sb(st, [128, NT, 129], BF16, name="vsa")
        vwa = cx.sb(st, [128, NT, 129], BF16, name="vwa")
        cx.dma('sp', vsa[:, :, :], N['vsa'].rearrange("(t p) c -> p t c", p=128), writes=[vsa])
        cx.dma('act', vwa[:, :, :], N['vwa'].rearrange("(t p) c -> p t c", p=128), writes=[vwa])
        cmask = cx.sb(st, [128, 4, 512], BF16, name="cmask")
        wmask = cx.sb(st, [128, 4, 512], BF16, name="wmask")
        cx.dma('sp', cmask[:, :, :], C['cmask'].rearrange("i k q -> k i q"), writes=[cmask])
        cx.dma('sp', wmask[:, :, :], C['wmask'].rearrange("i k q -> k i q"), writes=[wmask])
        cmpm = cx.sb(st, [128, 2, S], BF16, name="cmpm")
        cx.dma('sp', cmpm[:, :, :], C['cmpmask'].rearrange("kb p q -> p kb q"), writes=[cmpm])
        Em = cx.sb(st, [64, S], BF16, name="Em")
        cx.dma('sp', Em[:, :], C['Emat'][:, :], writes=[Em])
        gts = cx.sb(st, [128, NT, 12], F32, name="gts")
        cx.dma('sp', gts[:, :, :], N['ng'].rearrange("(t p) c -> p t c", p=128), writes=[gts])
        imp = cx.sb(st, [128, 4, 64], F32, name="imp")
        fbt = [cx.sb(st, [128, 64], F32, name="fbt") for _ in range(2)]
        m8 = cx.sb(st, [128, 16], F32, name="m8")
        val2 = cx.sb(st, [128, 64], F32, name="val2")
        selb = cx.sb(st, [128, 64], BF16, name="selb")
        selT = cx.sb(st, [64, 512], BF16, name="selT")
        rc = [cx.sb(st, [128, 1], F32, name="rc") for _ in range(2)]
        rg = [cx.sb(st, [128, 1], F32, name="rg") for _ in range(2)]
        ot = [cx.sb(st, [128, 128], F32, name="ot") for _ in range(3)]
        it = [cx.sb(st, [128, 64], F32, name="it") for _ in range(2)]
        trp2 = TrPool(cx, st, n=1)
        cnt = [0]

        def make_epi(branch, h):
            def epi(QB, j, accb, acc_ap):
                k = cnt[0]
                cnt[0] += 1
                tt = QB * NJ + j
                r, rgb, o = rc[k % 2], rg[k % 2], ot[k % 3]
                cx.op('dve', [accb], [r], lambda e: e.tensor_scalar(r[:, :], acc_ap[:, 128:129], 1e-30, None, ALU.add))
                cx.op('dve', [r], [r], lambda e: e.reciprocal(r[:, :], r[:, :]))
                cx.op('dve', [r, gts], [rgb], lambda e: e.tensor_tensor(rgb[:, :], r[:, :], gts[:, tt, h * 3 + branch:h * 3 + branch + 1], ALU.mult))
                cx.op('act', [accb, rgb], [o], lambda e: e.activation(out=o[:, :], in_=acc_ap[:, 0:128], func=AF.Copy, scale=rgb[:, 0:1]))
                cx.dma('pool', Youts[branch][tt * 128:(tt + 1) * 128, h * 128:(h + 1) * 128], o[:, :], reads=[o])
                if branch == 0:
                    if h == 0:
                        cx.op('dve', [accb, r], [imp], lambda e: e.tensor_scalar(imp[:, j, :], acc_ap[:, 129:193], r[:, 0:1], None, ALU.mult))
                    else:
                        i_ = it[k % 2]
                        cx.op('dve', [accb, r], [i_], lambda e: e.tensor_scalar(i_[:, :], acc_ap[:, 129:193], r[:, 0:1], None, ALU.mult))
                        cx.op('pool', [i_, imp], [imp], lambda e: e.tensor_tensor(imp[:, j, :], imp[:, j, :], i_[:, :], ALU.add))
            return epi

        def cmp_blocks(QB):
            q0 = QB * QW
            return [dict(kb=kb, k0=kb * 128, nk=min(128, Nc - kb * 128),
                         mask=(cmpm, cmpm[:min(128, Nc - kb * 128), kb, q0:q0 + QW])) for kb in range(NKB)]

        def slc_blocks(QB):
            bl = causal_blocks(QB, QW, cmask)
            for b in bl:
                b['extra'] = ([Em], Em[:, b['k0']:b['k0'] + 128], [selT], selT[:, :QW])
            return bl

        def win_blocks(QB):
            out = []
            nd = QW // 128
            for kb in range(max(0, nd * QB - 4), nd * (QB + 1)):
                i = kb - nd * QB
                m = (cmask, cmask[:, i, :QW]) if i >= 0 else (wmask, wmask[:, i + 4, :QW])
                out.append(dict(kb=kb, k0=kb * 128, nk=128, mask=m))
            return out

        for QB in range(S // QW):
            if NSA_STOP == 3:
                break
            for h in range(4 if NSA_STOP != 8 else 0):
                attn_core(cx, res, S, [(qT[h], qT[h][:, :]), (nq[h], nq[h][:, :])],
                          [(KC, KC[:, :]), (krow, lambda k0, nk: krow[:, 2, :nk])],
                          lambda kb, nk: (VCA, VCA[:nk, kb, :]), 193, cmp_blocks, make_epi(0, h), qbs=[QB])
            if NSA_STOP == 4:
                break
            for j in range(NJ if NSA_STOP != 8 else 0):
                tt = QB * NJ + j
                f_ = fbt[j % 2]
                cx.dma('sp', f_[:, :], C['fbias'][tt * 128:(tt + 1) * 128, :], writes=[f_])
                cx.op('dve', [imp, f_], [f_], lambda e: e.tensor_tensor(f_[:, :], f_[:, :], imp[:, j, :], ALU.add))
                cx.op('dve', [f_], [m8], lambda e: e.max(out=m8[:, 0:8], in_=f_[:, :]))
                cx.op('dve', [f_, m8], [val2], lambda e: e.match_replace(out=val2[:, :], in_to_replace=m8[:, 0:8], in_values=f_[:, :], imm_value=-3.0e38))
                cx.op('dve', [val2], [m8], lambda e: e.max(out=m8[:, 8:16], in_=val2[:, :]))
                cx.op('dve', [f_, m8], [val2], lambda e: e.tensor_scalar(val2[:, :], f_[:, :], m8[:, 15:16], None, ALU.is_ge))
                cx.op('dve', [val2], [selb], lambda e: e.tensor_scalar(selb[:, :], val2[:, :], -1.0, NSA_BIG, ALU.add, ALU.mult))
                trp2.transpose_cols(selb, lambda jj: selb[:, :], 1, selT, lambda j0, c_, j=j: selT[:64, j * 128:(j + 1) * 128].unsqueeze(1), blkw=64)
            for h in range(4):
                attn_core(cx, res, S, [(qT[h], qT[h][:, :]), (nq[h], nq[h][:, :])],
                          [(kwT, kwT[:, :]), (krow, lambda k0, nk: krow[:, 1, :nk])],
                          lambda kb, nk: (vwa, vwa[:nk, kb, :]), 129, win_blocks, make_epi(2, h), qbs=[QB])
            if NSA_STOP == 5:
                break
            for h in range(4 if NSA_STOP not in (8, 9) else 0):
                attn_core(cx, res, S, [(qT[h], qT[h][:, :]), (nq[h], nq[h][:, :])],
                          [(ksT, ksT[:, :]), (krow, lambda k0, nk: krow[:, 0, :nk])],
                          lambda kb, nk: (vsa, vsa[:nk, kb, :]), 129, slc_blocks, make_epi(1, h), qbs=[QB])
    cx.barrier()


S_FULL = 4096
ENABLE = {'mla': True, 'nsa': True, 'rwkv': True, 'ret': True}


def phase_zero(cx, S, Y_d):
    with contextlib.ExitStack() as st:
        z = cx.sb(st, [128, 512], F32, name="z")
        cx.op('pool', [], [z], lambda e: e.memset(z[:, :], 0.0))
        for tt in range(S // 128):
            cx.dma('sp', Y_d[tt * 128:(tt + 1) * 128, :], z[:, :], reads=[z])
    cx.barrier()


def host_consts(S):
    import ml_dtypes
    bf = ml_dtypes.bfloat16
    c = {}
    c['ident'] = np.eye(128, dtype=np.float32).astype(bf)
    kk = np.arange(128)[:, None]
    qq = np.arange(512)[None, :]
    c['cmask'] = np.stack([(128 * i + kk <= qq) for i in range(4)]).astype(np.float32).astype(bf)
    t = np.arange(S, dtype=np.float32)[:, None]

    def tables(inv):
        ang = (t * inv[None, :].astype(np.float32)).astype(np.float32)
        return np.cos(ang).astype(np.float32), np.sin(ang).astype(np.float32)

    inv_mla = (np.float32(500000.0) ** (-np.arange(0, 64, 2, dtype=np.float32) / np.float32(64))).astype(np.float32)
    inv_nsa = (np.float32(500000.0) ** (-np.arange(0, 32, 2, dtype=np.float32) / np.float32(32))).astype(np.float32)
    inv_ret = (np.float32(10000.0) ** (-np.linspace(0.0, 1.0, 32, dtype=np.float32))).astype(np.float32)
    c['mla_cos'], c['mla_sin'] = tables(inv_mla)
    c['nsa_cos'], c['nsa_sin'] = tables(inv_nsa)
    c['ret_cos'], c['ret_sin'] = tables(inv_ret)
    kf = kk.astype(np.float64)
    qf = qq.astype(np.float64)
    gdec = np.zeros((4, 5, 128, 512), np.float32)
    for h in range(4):
        lg = np.log1p(-2.0 ** (-5 - h))
        gdec[h, 0] = np.exp((qf - kf) * lg)
        for i in range(4):
            d = qf - kf - 128 * i
            gdec[h, 1 + i] = np.where(d >= 0, np.exp(np.maximum(d, 0) * lg), 0.0)
    c['gdec'] = gdec
    si = np.arange(128)[:, None]
    ti = np.arange(128)[None, :]
    c['wmask'] = (1.0 - c['cmask'].astype(np.float32)).astype(bf)
    Nc = (S - 32) // 16 + 1
    n = np.arange(256)
    q = np.arange(S)
    cm = ((16 * n[:, None] + 31 <= q[None, :]) & (n[:, None] < Nc)).astype(np.float32)
    c['cmpmask'] = cm.reshape(2, 128, S).astype(bf)
    nblk = S // 64
    jb = np.arange(64)
    cstart = 16 * n
    cend = cstart + 31
    cover = ((cstart[:, None] <= jb[None, :] * 64 + 63) & (cend[:, None] >= jb[None, :] * 64) & (n[:, None] < Nc)
             & (jb[None, :] < nblk)).astype(np.float32)
    c['cover'] = cover.astype(bf)
    c['Emat'] = (q[None, :] // 64 == jb[:, None]).astype(np.float32).astype(bf)
    cur = q // 64
    forced = (jb[None, :] == 0) | (jb[None, :] == cur[:, None]) | (jb[None, :] == cur[:, None] - 1)
    visible = (jb[None, :] <= cur[:, None]) & (jb[None, :] < nblk)
    c['fbias'] = np.where(visible, 1000.0 * forced, -1.0e30).astype(np.float32)
    c['rwm'] = np.concatenate([(si < ti), (si <= ti), (si > ti)], axis=1).astype(np.float32)
    return c


CONST_SPECS = {'ident': ([128, 128], BF16), 'cmask': ([4, 128, 512], BF16),
               'mla_cos': (None, F32), 'mla_sin': (None, F32), 'nsa_cos': (None, F32), 'nsa_sin': (None, F32),
               'ret_cos': (None, F32), 'ret_sin': (None, F32), 'gdec': ([4, 5, 128, 512], F32), 'rwm': ([128, 384], F32), 'wmask': ([4, 128, 512], BF16), 'cmpmask': ('cmp', BF16),
               'cover': ([256, 64], BF16), 'Emat': ('E', BF16), 'fbias': ('fb', F32)}

WEIGHT_SHAPES = {
    'w_in': [DEPTH, D_MODEL, IN_WIDTH], 'w_branch': [DEPTH, 4, BW, D_MODEL], 'w_out': [DEPTH, D_MODEL, D_MODEL],
    'w_up': [DEPTH, D_MODEL, D_FF], 'w_down': [DEPTH, D_FF, D_MODEL], 'norm_gains': [DEPTH, 4, D_MODEL],
    'mla_g_q': [DEPTH, 384], 'mla_g_kv': [DEPTH, 128], 'mla_w_uq': [DEPTH, 384, 768], 'mla_w_ukv': [DEPTH, 128, 1024],
    'nsa_cmp_pos': [DEPTH, 2, 32, 128], 'nsa_cmp_w1': [DEPTH, 2, 4096, 128], 'nsa_cmp_w2': [DEPTH, 2, 128, 128],
    'rwkv_mu': [DEPTH, 1984], 'rwkv_w0': [DEPTH, 512], 'rwkv_w2': [DEPTH, 96, 512], 'rwkv_a0': [DEPTH, 512],
    'rwkv_a2': [DEPTH, 96, 512], 'rwkv_g2': [DEPTH, 256, 512], 'rwkv_k_k': [DEPTH, 512], 'rwkv_k_a': [DEPTH, 512],
    'rwkv_r_k': [DEPTH, 8, 64], 'rwkv_gn_w': [DEPTH, 512], 'rwkv_gn_b': [DEPTH, 512],
}
CAST = ['w_in', 'w_branch', 'w_out', 'w_up', 'w_down', 'mla_w_uq', 'mla_w_ukv', 'nsa_cmp_w1', 'nsa_cmp_w2',
        'rwkv_w2', 'rwkv_a2', 'rwkv_g2']


def build_program(S, depth=DEPTH):
    cx = Ctx()
    x_d = cx.dram("x", [S, D_MODEL], F32, kind="ExternalInput")
    W = {k: cx.dram(k, shp, F32, kind="ExternalInput") for k, shp in WEIGHT_SHAPES.items()}
    C = {}
    for k, (shp, dt) in CONST_SPECS.items():
        if shp is None:
            shp = [S, 16 if k.startswith('nsa') else 32]
        elif shp == 'cmp':
            shp = [2, 128, S]
        elif shp == 'E':
            shp = [64, S]
        elif shp == 'fb':
            shp = [S, 64]
        C[k] = cx.dram("c_" + k, shp, dt, kind="ExternalInput")
    y_d = cx.dram("y", [S, D_MODEL], F32, kind="ExternalOutput")
    Wb = {k: cx.dram(k + "_bf", WEIGHT_SHAPES[k], BF16) for k in CAST}
    P_d = cx.dram("P", [S, IN_WIDTH], F32)
    Y = [cx.dram("Y%d" % m, [S, BW], F32) for m in range(4)]
    Yn = [cx.dram("Yn%d" % m, [S, BW], F32) for m in range(2)]
    M_d = cx.dram("M", [S, D_MODEL], F32)
    Z_d = cx.dram("Z", [S, D_MODEL], F32)
    xa = cx.dram("xa", [S, D_MODEL], F32)
    xb = cx.dram("xb", [S, D_MODEL], F32)
    scr = dict(qnT=cx.dram("qnT", [4, 128, S], BF16), qrT=cx.dram("qrT", [4, 65, S], BF16),
               knT=cx.dram("knT", [4, 128, S], BF16), krT=cx.dram("krT", [65, S], BF16),
               va=cx.dram("va", [S, 4, 129], BF16),
               rqT=cx.dram("rqT", [4, 64, S], BF16), rkT=cx.dram("rkT", [4, 64, S], BF16),
               rv=cx.dram("rv", [S, 4, 128], BF16))
    scr['rw'] = {nm: cx.dram("rw_" + nm, [S, 512], F32) for nm in ['rr', 'lw', 'k2', 'vv', 'kn', 'aa', 'gg']}
    scr['rw']['bc'] = cx.dram("rw_bc", [S, 8], F32)
    scr['nsa'] = dict(qT=cx.dram("n_qT", [4, 128, S], BF16), nq=cx.dram("n_nq", [4, S], BF16),
                      kcT=cx.dram("n_kcT", [128, S], BF16), vcT=cx.dram("n_vcT", [128, S], BF16),
                      ksT=cx.dram("n_ksT", [128, S], BF16), kwT=cx.dram("n_kwT", [128, S], BF16),
                      vsa=cx.dram("n_vsa", [S, 129], BF16), vwa=cx.dram("n_vwa", [S, 129], BF16),
                      ng=cx.dram("n_ng", [S, 12], F32), krow=cx.dram("n_krow", [2, 128], BF16))
    st = contextlib.ExitStack()
    cx._st = st
    setup_consts(cx, st, C['ident'])
    phase_cast(cx, [(W[k], Wb[k]) for k in CAST])
    xin = x_d
    for l in range(depth):
        g = W['norm_gains'][l]
        phase_in(cx, S, xin, g[0], Wb['w_in'][l], P_d)
        if ENABLE['mla']:
            phase_mla(cx, S, P_d, W['mla_g_q'][l], W['mla_g_kv'][l], Wb['mla_w_uq'][l], Wb['mla_w_ukv'][l],
                      C['mla_cos'], C['mla_sin'], C['cmask'], Y[0], scr)
        else:
            phase_zero(cx, S, Y[0])
        if ENABLE['nsa']:
            phase_nsa(cx, S, l, P_d, W, Wb, C, [Y[1], Yn[0], Yn[1]], scr)
            ysrc1 = [Y[1], Yn[0], Yn[1]]
        else:
            phase_zero(cx, S, Y[1])
            ysrc1 = [Y[1]]
        if ENABLE['rwkv']:
            phase_rwkv(cx, S, l, P_d, W, Wb, C, Y[2], scr)
        else:
            phase_zero(cx, S, Y[2])
        if ENABLE['ret']:
            phase_ret(cx, S, P_d, C['ret_cos'], C['ret_sin'], C['gdec'], Y[3], scr)
        else:
            phase_zero(cx, S, Y[3])
        phase_merge(cx, S, [[Y[0]], ysrc1, [Y[2]], [Y[3]]], Wb['w_branch'][l], P_d, M_d)
        phase_out(cx, S, M_d, Wb['w_out'][l], Z_d)
        phase_normres(cx, S, xin, Z_d, g[1], xa)
        phase_ffn(cx, S, xa, g[2], Wb['w_up'][l], Wb['w_down'][l], Z_d)
        xnext = y_d if l == depth - 1 else xb
        phase_normres(cx, S, xa, Z_d, g[3], xnext)
        xin = xnext
    cx.barrier()
    return cx


_CACHE = {}


def kernel(**inputs):
    x = np.ascontiguousarray(np.asarray(inputs['x'], dtype=np.float32))
    B, S, _ = x.shape
    if S not in _CACHE:
        _CACHE[S] = (build_program(S), host_consts(S))
    cx, consts = _CACHE[S]
    base = {k: np.ascontiguousarray(np.asarray(inputs[k], dtype=np.float32)) for k in WEIGHT_SHAPES}
    for k, v in consts.items():
        base["c_" + k] = np.ascontiguousarray(v)
    in_maps = []
    for b in range(B):
        m = dict(base)
        m['x'] = x[b]
        in_maps.append(m)
    res = run_bass_kernel_spmd(cx.nc, in_maps, core_ids=list(range(B)))
    return np.stack([np.asarray(r['y'], dtype=np.float32) for r in res.results], axis=0)
```

```python
import contextlib
import numpy as np
import concourse.bass as bass
import concourse.mybir as mybir
from concourse.bass_utils import run_bass_kernel_spmd

F32 = mybir.dt.float32
F32R = mybir.dt.float32r
RW_FAST = True


def fr(ap):
    return ap.bitcast(F32R) if RW_FAST else ap
BF16 = mybir.dt.bfloat16
AF = mybir.ActivationFunctionType
ALU = mybir.AluOpType
AX = mybir.AxisListType

D_MODEL = 2048
DEPTH = 2
BW = 512
D_FF = 8192
NORM_EPS = 1e-6
IN_WIDTH = 13580
OFF_MLA = 0
OFF_NSA = 576
OFF_RWKV = 1868
OFF_RET = 3852
OFF_GATE = 5388

SELF_SYNC = {'pe': False, 'act': False, 'dve': True, 'pool': True, 'sp': False}


class Reg:
    __slots__ = ('w', 'r')

    def __init__(self):
        self.w = None
        self.r = {}


class Buf:
    def __init__(self, t, nreg=1, excl=False):
        self.t = t
        self.regs = [Reg() for _ in range(nreg)]
        self.excl = excl

    @property
    def reg(self):
        return self.regs[0]

    def __getitem__(self, idx):
        return self.t[idx]


class Ctx:
    def __init__(self):
        self.nc = bass.Bass("TRN2", target_bir_lowering=False)
        nc = self.nc
        self.E = {'pe': nc.tensor, 'act': nc.scalar, 'dve': nc.vector, 'pool': nc.gpsimd, 'sp': nc.sync}
        self.sem = {e: nc.alloc_semaphore("s_" + e) for e in ['pe', 'act', 'dve', 'pool']}
        self.cnt = {e: 0 for e in self.sem}
        self.NDS = 48
        self.dsem = [nc.alloc_semaphore("d%d" % i) for i in range(self.NDS)]
        self.dcnt = [0] * self.NDS
        self.dpool = {'sp': list(range(0, 20)), 'act': list(range(20, 34)), 'pool': list(range(34, 48))}
        self.dnext = {'sp': 0, 'act': 0, 'pool': 0}
        self.known = {e: {} for e in self.E}
        self.ninst = 0
        self.uid = 0
        self._rec = None

    def name(self, p):
        self.uid += 1
        return "%s_%d" % (p, self.uid)

    def sb(self, stack, shape, dtype, nreg=1, name="sb"):
        t = stack.enter_context(self.nc.sbuf_tensor(self.name(name), list(shape), dtype))
        return Buf(t, nreg)

    def ps(self, stack, shape, dtype=F32, nreg=1, name="ps"):
        t = stack.enter_context(self.nc.psum_tensor(self.name(name), list(shape), dtype))
        return Buf(t, nreg, excl=True)

    def dram(self, name, shape, dtype, kind="Internal"):
        return self.nc.dram_tensor(name, list(shape), dtype, kind=kind).ap()

    def _wait(self, e, kind, val, force=False):
        if isinstance(kind, str):
            if kind == e and not SELF_SYNC[e] and not (force and e in self.sem):
                return
            sem = self.sem[kind]
            v = val
        else:
            idx = kind[1]
            sem = self.dsem[idx]
            v = val * 16
        k = self.known[e]
        if k.get(kind, 0) >= v:
            return
        self.E[e].wait_ge(sem, v)
        self.ninst += 1
        k[kind] = v

    def _deps(self, e, reads, writes, force=False):
        for r in reads:
            if r.w is not None:
                self._wait(e, r.w[0], r.w[1], force)
        for w in writes:
            if w.w is not None:
                self._wait(e, w.w[0], w.w[1], force)
            for kind, val in w.r.items():
                self._wait(e, kind, val, force)

    def _commit(self, tok, reads, writes):
        kind, val = tok
        for r in reads:
            if r.r.get(kind, 0) < val:
                r.r[kind] = val
        for w in writes:
            w.w = tok
            w.r = {}

    @staticmethod
    def _regs(lst):
        out = []
        for x in lst:
            if isinstance(x, Buf):
                out.extend(x.regs)
            elif isinstance(x, Reg):
                out.append(x)
            elif x is None:
                pass
            else:
                raise TypeError(type(x))
        return out

    def op(self, e, reads, writes, fn):
        if self._rec is not None:
            self._rec.append(lambda: self._op(e, reads, writes, fn))
            return None
        return self._op(e, reads, writes, fn)

    def _op(self, e, reads, writes, fn):
        writes = list(writes) + [x for x in reads if isinstance(x, Buf) and x.excl]
        reads = [x for x in reads if not (isinstance(x, Buf) and x.excl)]
        reads = self._regs(reads)
        writes = self._regs(writes)
        self._deps(e, reads, writes)
        inst = fn(self.E[e])
        self.cnt[e] += 1
        self.ninst += 1
        inst.then_inc(self.sem[e], 1)
        self._commit((e, self.cnt[e]), reads, writes)
        return inst

    def dma(self, q, out_ap, in_ap, reads=(), writes=(), **kw):
        if self._rec is not None:
            self._rec.append(lambda: self._dma(q, out_ap, in_ap, reads, writes, **kw))
            return
        self._dma(q, out_ap, in_ap, reads, writes, **kw)

    def _dma(self, q, out_ap, in_ap, reads=(), writes=(), **kw):
        reads = self._regs(reads)
        writes = self._regs(writes)
        self._deps(q, reads, writes, force=True)
        pool = self.dpool[q]
        idx = pool[self.dnext[q] % len(pool)]
        self.dnext[q] += 1
        if self.dcnt[idx] > 0:
            self._wait(q, ('d', idx), self.dcnt[idx])
        self.E[q].dma_start(out=out_ap, in_=in_ap, **kw).then_inc(self.dsem[idx], 16)
        self.dcnt[idx] += 1
        self.ninst += 1
        self._commit((('d', idx), self.dcnt[idx]), reads, writes)

    def barrier(self):
        for e in self.E:
            for o in self.sem:
                if o != e and self.cnt[o] > 0:
                    self._wait(e, o, self.cnt[o])
            for i in range(self.NDS):
                if self.dcnt[i] > 0:
                    self._wait(e, ('d', i), self.dcnt[i])
            if e in self.sem and self.cnt[e] > 0:
                k = self.known[e]
                if k.get(e, 0) < self.cnt[e]:
                    self.E[e].wait_ge(self.sem[e], self.cnt[e])
                    k[e] = self.cnt[e]


INTERLEAVE = 2


def run_tiles(cx, body, NT):
    for t0 in range(0, NT, INTERLEAVE):
        lists = []
        posts = []
        for t in range(t0, min(NT, t0 + INTERLEAVE)):
            cx._rec = []
            post = body(t)
            lists.append(cx._rec)
            cx._rec = None
            if post is not None:
                posts.append(post)
        for i in range(max(len(l) for l in lists)):
            for l in lists:
                if i < len(l):
                    l[i]()
        for p in posts:
            p()


def atomic(cx, fn):
    if cx._rec is None:
        return fn()
    rec = cx._rec

    def unit():
        saved = cx._rec
        cx._rec = None
        fn()
        cx._rec = saved
    rec.append(unit)


def mm(cx, out_buf, out_ap, lhsT_ap, rhs_ap, reads, start, stop, **kw):
    return cx.op('pe', reads, [out_buf],
                 lambda e: e.matmul(out_ap, lhsT_ap, rhs_ap, start=start, stop=stop, **kw))


def transp(cx, out_buf, out_ap, in_ap, ident_ap, reads):
    return cx.op('pe', reads, [out_buf], lambda e: e.transpose(out_ap, in_ap, ident_ap))


def phase_cast(cx, pairs):
    CH = 4096
    NB = 6
    with contextlib.ExitStack() as st:
        stg = [cx.sb(st, [128, CH], F32, name="cst") for _ in range(NB)]
        outb = [cx.sb(st, [128, CH], BF16, name="cob") for _ in range(NB)]
        engs = ['dve', 'pool', 'act']
        k = 0
        for src, dst in pairs:
            n = 1
            for s in src.shape:
                n *= s
            assert n % 128 == 0
            per = n // 128
            names = " ".join("a%d" % i for i in range(len(src.shape)))
            s2 = src.rearrange("%s -> (%s)" % (names, names)).rearrange("(p f) -> p f", p=128)
            d2 = dst.rearrange("%s -> (%s)" % (names, names)).rearrange("(p f) -> p f", p=128)
            for c0 in range(0, per, CH):
                c1 = min(per, c0 + CH)
                w = c1 - c0
                i = k % NB
                cx.dma('sp', stg[i][:, :w], s2[:, c0:c1], writes=[stg[i]])
                e = engs[k % 3]
                if e == 'act':
                    cx.op(e, [stg[i]], [outb[i]], lambda en: en.copy(outb[i][:, :w], stg[i][:, :w]))
                else:
                    cx.op(e, [stg[i]], [outb[i]], lambda en: en.tensor_copy(outb[i][:, :w], stg[i][:, :w]))
                cx.dma('act' if k % 2 else 'pool', d2[:, c0:c1], outb[i][:, :w], reads=[outb[i]])
                k += 1
    cx.barrier()


def load_bcast_row(cx, q, buf, row_ap, n):
    cx.dma(q, buf[:, :n], row_ap.partition_broadcast(128), writes=[buf])


def rms_rstd(cx, x_buf, x_ap, n, ss_buf, junk_buf, eps=NORM_EPS):
    cx.op('act', [x_buf], [junk_buf, ss_buf],
          lambda e: e.activation(out=junk_buf[:, :n], in_=x_ap, func=AF.Square, accum_out=ss_buf[:, 0:1]))
    cx.op('dve', [ss_buf], [ss_buf],
          lambda e: e.tensor_scalar(ss_buf[:, 0:1], ss_buf[:, 0:1], 1.0 / n, eps, ALU.mult, ALU.add))
    cx.op('pool', [ss_buf, cx.neghalf], [ss_buf],
          lambda e: e.tensor_tensor(ss_buf[:, 0:1], ss_buf[:, 0:1], cx.neghalf[:, 0:1], ALU.pow))


def setup_consts(cx, st, ident_d):
    cx.ident = cx.sb(st, [128, 128], BF16, name="ident")
    cx.dma('sp', cx.ident[:, :], ident_d[:, :], writes=[cx.ident])
    cx.identf = cx.sb(st, [128, 128], F32, name="identf")
    cx.op('dve', [cx.ident], [cx.identf], lambda e: e.tensor_copy(cx.identf[:, :], cx.ident[:, :]))
    cx.neghalf = cx.sb(st, [128, 1], F32, name="neghalf")
    cx.op('pool', [], [cx.neghalf], lambda e: e.memset(cx.neghalf[:, :], -0.5))
    cx.ones_bf = cx.sb(st, [128, 128], BF16, name="ones_bf")
    cx.op('pool', [], [cx.ones_bf], lambda e: e.memset(cx.ones_bf[:, :], 1.0))
    cx.ones_f = cx.sb(st, [128, 128], F32, name="ones_f")
    cx.op('pool', [], [cx.ones_f], lambda e: e.memset(cx.ones_f[:, :], 1.0))


def phase_in(cx, S, x_d, g_row, w_bf, P_d):
    G = 512
    KC = D_MODEL // 128
    chunks = []
    c = 0
    while c < OFF_GATE:
        chunks.append((c, min(c + 512, OFF_GATE), False))
        c += 512
    c = OFF_GATE
    while c < IN_WIDTH:
        chunks.append((c, c + 512, True))
        c += 512
    wv = w_bf.rearrange("(kc p) n -> p kc n", p=128)
    with contextlib.ExitStack() as st:
        gB = cx.sb(st, [128, D_MODEL], F32, name="gB")
        load_bcast_row(cx, 'sp', gB, g_row, D_MODEL)
        xt = [cx.sb(st, [128, D_MODEL], F32, name="xt") for _ in range(2)]
        junk = cx.sb(st, [128, D_MODEL], BF16, name="junk")
        ss = [cx.sb(st, [128, 1], F32, name="ss") for _ in range(2)]
        hb = [cx.sb(st, [128, D_MODEL], BF16, name="hb") for _ in range(2)]
        hT = [cx.sb(st, [128, KC, G], BF16, name="hT") for _ in range(2)]
        wb = [cx.sb(st, [128, KC, 512], BF16, name="wb") for _ in range(2)]
        ob = [cx.sb(st, [128, 512], F32, name="ob") for _ in range(4)]
        ptr = [cx.ps(st, [128, 8, 128], BF16, name="ptr") for _ in range(2)]
        pmm = [cx.ps(st, [128, 512], F32, name="pmm") for _ in range(4)]
        ntr = 0
        nmm = 0
        nw = 0
        for gi in range(S // G):
            hTg = hT[gi % 2]
            for tl in range(G // 128):
                tt = gi * (G // 128) + tl
                xb = xt[tt % 2]
                sb_ = ss[tt % 2]
                hbb = hb[tt % 2]
                cx.dma('sp', xb[:, :], x_d[tt * 128:(tt + 1) * 128, :], writes=[xb])
                rms_rstd(cx, xb, xb[:, :], D_MODEL, sb_, junk)
                cx.op('dve', [xb, sb_, gB], [hbb],
                      lambda e: e.scalar_tensor_tensor(out=hbb[:, :], in0=xb[:, :], scalar=sb_[:, 0:1],
                                                       in1=gB[:, :], op0=ALU.mult, op1=ALU.mult))
                for k4 in range(KC // 4):
                    pt = ptr[ntr % 2]
                    ntr += 1
                    for j in range(4):
                        kc = k4 * 4 + j
                        transp(cx, pt, pt[:, j, :], hbb[:, kc * 128:(kc + 1) * 128], cx.ident[:, :], [hbb, cx.ident])
                    eng = 'act' if (k4 % 2 == 0) else 'dve'
                    dst = hTg[:, k4 * 4:(k4 + 1) * 4, tl * 128:(tl + 1) * 128]
                    if eng == 'act':
                        cx.op('act', [pt], [hTg], lambda e: e.copy(dst, pt[:, 0:4, :]))
                    else:
                        cx.op('dve', [pt], [hTg], lambda e: e.tensor_copy(dst, pt[:, 0:4, :]))
            for (c0, c1, sig) in chunks:
                w = c1 - c0
                wbb = wb[nw % 2]
                nw += 1
                cx.dma('sp', wbb[:, :, :w], wv[:, :, c0:c1], writes=[wbb])
                for tl in range(G // 128):
                    tt = gi * (G // 128) + tl
                    pm = pmm[nmm % 4]
                    obb = ob[nmm % 4]
                    nmm += 1
                    for kc in range(KC):
                        mm(cx, pm, pm[:, :w], hTg[:, kc, tl * 128:(tl + 1) * 128], wbb[:, kc, :w],
                           [hTg, wbb], kc == 0, kc == KC - 1)
                    if sig:
                        cx.op('act', [pm], [obb],
                              lambda e: e.activation(out=obb[:, :w], in_=pm[:, :w], func=AF.Sigmoid))
                    elif nmm % 2 == 0:
                        cx.op('dve', [pm], [obb], lambda e: e.tensor_copy(obb[:, :w], pm[:, :w]))
                    else:
                        cx.op('act', [pm], [obb], lambda e: e.copy(obb[:, :w], pm[:, :w]))
                    cx.dma('pool', P_d[tt * 128:(tt + 1) * 128, c0:c1], obb[:, :w], reads=[obb])
    cx.barrier()


def evac(cx, eng, src_buf, src_ap, dst_buf, dst_ap, extra_reads=()):
    if eng == 'act':
        cx.op('act', [src_buf] + list(extra_reads), [dst_buf], lambda e: e.copy(dst_ap, src_ap))
    else:
        cx.op(eng, [src_buf] + list(extra_reads), [dst_buf], lambda e: e.tensor_copy(dst_ap, src_ap))


class TrPool:
    def __init__(self, cx, st, n=2, dtype=BF16):
        self.cx = cx
        self.bufs = [cx.ps(st, [128, 8 if dtype == BF16 else 4, 128], dtype, name="ptr") for _ in range(n)]
        self.k = 0
        self.dtype = dtype

    def transpose_cols(self, src_buf, src_ap_fn, nblk, dst_buf, dst_ap_fn, rows=128, blkw=128):
        cx = self.cx
        if cx._rec is not None:
            atomic(cx, lambda: self.transpose_cols(src_buf, src_ap_fn, nblk, dst_buf, dst_ap_fn, rows, blkw))
            return
        ident = cx.ident if self.dtype == BF16 else cx.identf
        j = 0
        while j < nblk:
            cnt = min(4, nblk - j)
            pt = self.bufs[self.k % len(self.bufs)]
            eng = 'act' if self.k % 2 == 0 else 'dve'
            self.k += 1
            for i in range(cnt):
                transp(cx, pt, pt[:blkw, i, :rows], src_ap_fn(j + i), ident[:rows, :rows], [src_buf, ident])
            evac(cx, eng, pt, pt[:blkw, :cnt, :rows], dst_buf, dst_ap_fn(j, cnt))
            j += cnt


def phase_merge(cx, S, ysrcs, wbr_bf, P_d, M_d):
    with contextlib.ExitStack() as st:
        wbr = cx.sb(st, [128, 16, D_MODEL], BF16, name="wbr")
        wv = wbr_bf.rearrange("m (kc p) n -> p (m kc) n", p=128)
        for q in range(4):
            cx.dma('sp', wbr[:, q * 4:(q + 1) * 4, :], wv[:, q * 4:(q + 1) * 4, :], writes=[wbr])
        yt = [cx.sb(st, [128, BW], F32, name="yt") for _ in range(3)]
        yb = [cx.sb(st, [128, BW], BF16, name="yb") for _ in range(2)]
        yT = [cx.sb(st, [128, 4, 128], BF16, name="yT") for _ in range(2)]
        sg = [cx.sb(st, [128, D_MODEL], F32, name="sg") for _ in range(2)]
        mg = [cx.sb(st, [128, D_MODEL], F32, name="mg") for _ in range(2)]
        tmp = [cx.sb(st, [128, 512], F32, name="tmp") for _ in range(2)]
        trp = TrPool(cx, st)
        pmm = [cx.ps(st, [128, 512], F32, name="pmm") for _ in range(4)]
        k = 0
        for tt in range(S // 128):
            rows = slice(tt * 128, (tt + 1) * 128)
            mgb = mg[tt % 2]
            for m in range(4):
                k += 1
                y0 = yt[k % 3]
                cx.dma('sp', y0[:, :], ysrcs[m][0][rows, :], writes=[y0])
                for extra in ysrcs[m][1:]:
                    k += 1
                    y1 = yt[k % 3]
                    cx.dma('sp', y1[:, :], extra[rows, :], writes=[y1])
                    cx.op('pool', [y0, y1], [y0], lambda e: e.tensor_tensor(y0[:, :], y0[:, :], y1[:, :], ALU.add))
                ybb = yb[m % 2]
                cx.op('act', [y0], [ybb], lambda e: e.copy(ybb[:, :], y0[:, :]))
                yTb = yT[m % 2]
                trp.transpose_cols(ybb, lambda j: ybb[:, j * 128:(j + 1) * 128], 4, yTb,
                                   lambda j0, cnt: yTb[:, j0:j0 + cnt, :])
                sgb = sg[m % 2]
                cx.dma('act', sgb[:, :], P_d[rows, OFF_GATE + m * D_MODEL:OFF_GATE + (m + 1) * D_MODEL], writes=[sgb])
                for nc_ in range(4):
                    cs = slice(nc_ * 512, (nc_ + 1) * 512)
                    pm = pmm[(m * 4 + nc_) % 4]
                    for kc in range(4):
                        mm(cx, pm, pm[:, :], yTb[:, kc, :], wbr[:, m * 4 + kc, cs], [yTb, wbr], kc == 0, kc == 3)
                    if m == 0:
                        cx.op('dve', [pm, sgb], [mgb],
                              lambda e: e.tensor_tensor(mgb[:, cs], pm[:, :], sgb[:, cs], ALU.mult))
                    else:
                        tb = tmp[nc_ % 2]
                        cx.op('dve', [pm, sgb], [tb],
                              lambda e: e.tensor_tensor(tb[:, :], pm[:, :], sgb[:, cs], ALU.mult))
                        cx.op('dve' if nc_ % 2 else 'pool', [tb, mgb], [mgb],
                              lambda e: e.tensor_tensor(mgb[:, cs], mgb[:, cs], tb[:, :], ALU.add))
            cx.dma('pool', M_d[rows, :], mgb[:, :], reads=[mgb])
    cx.barrier()


def phase_out(cx, S, M_d, wout_bf, Z_d):
    with contextlib.ExitStack() as st:
        wo = cx.sb(st, [128, 16, D_MODEL], BF16, name="wo")
        wv = wout_bf.rearrange("(kc p) n -> p kc n", p=128)
        for q in range(4):
            cx.dma('sp', wo[:, q * 4:(q + 1) * 4, :], wv[:, q * 4:(q + 1) * 4, :], writes=[wo])
        mt = [cx.sb(st, [128, D_MODEL], F32, name="mt") for _ in range(2)]
        mb = [cx.sb(st, [128, D_MODEL], BF16, name="mb") for _ in range(2)]
        mT = [cx.sb(st, [128, 16, 128], BF16, name="mT") for _ in range(2)]
        ob = [cx.sb(st, [128, 512], F32, name="ob") for _ in range(4)]
        trp = TrPool(cx, st)
        pmm = [cx.ps(st, [128, 512], F32, name="pmm") for _ in range(4)]
        k = 0
        for tt in range(S // 128):
            rows = slice(tt * 128, (tt + 1) * 128)
            mtb, mbb, mTb = mt[tt % 2], mb[tt % 2], mT[tt % 2]
            cx.dma('sp', mtb[:, :], M_d[rows, :], writes=[mtb])
            cx.op('act', [mtb], [mbb], lambda e: e.copy(mbb[:, :], mtb[:, :]))
            trp.transpose_cols(mbb, lambda j: mbb[:, j * 128:(j + 1) * 128], 16, mTb,
                               lambda j0, cnt: mTb[:, j0:j0 + cnt, :])
            for nc_ in range(4):
                cs = slice(nc_ * 512, (nc_ + 1) * 512)
                pm = pmm[k % 4]
                obb = ob[k % 4]
                k += 1
                for kc in range(16):
                    mm(cx, pm, pm[:, :], mTb[:, kc, :], wo[:, kc, cs], [mTb, wo], kc == 0, kc == 15)
                evac(cx, 'act' if k % 2 else 'dve', pm, pm[:, :], obb, obb[:, :])
                cx.dma('pool', Z_d[rows, cs], obb[:, :], reads=[obb])
    cx.barrier()


def phase_normres(cx, S, x_d, Z_d, g_row, out_d):
    with contextlib.ExitStack() as st:
        gB = cx.sb(st, [128, D_MODEL], F32, name="gB")
        load_bcast_row(cx, 'sp', gB, g_row, D_MODEL)
        zt = [cx.sb(st, [128, D_MODEL], F32, name="zt") for _ in range(2)]
        xt = [cx.sb(st, [128, D_MODEL], F32, name="xt") for _ in range(2)]
        ot = [cx.sb(st, [128, D_MODEL], F32, name="ot") for _ in range(2)]
        junk = cx.sb(st, [128, D_MODEL], BF16, name="junk")
        ss = [cx.sb(st, [128, 1], F32, name="ss") for _ in range(2)]
        def body(tt):
            rows = slice(tt * 128, (tt + 1) * 128)
            z, x, o, s_ = zt[tt % 2], xt[tt % 2], ot[tt % 2], ss[tt % 2]
            cx.dma('sp', z[:, :], Z_d[rows, :], writes=[z])
            cx.dma('act', x[:, :], x_d[rows, :], writes=[x])
            rms_rstd(cx, z, z[:, :], D_MODEL, s_, junk)
            cx.op('dve', [z, s_, gB], [o],
                  lambda e: e.scalar_tensor_tensor(out=o[:, :], in0=z[:, :], scalar=s_[:, 0:1], in1=gB[:, :],
                                                   op0=ALU.mult, op1=ALU.mult))
            cx.op('pool', [o, x], [o], lambda e: e.tensor_tensor(o[:, :], o[:, :], x[:, :], ALU.add))
            cx.dma('pool', out_d[rows, :], o[:, :], reads=[o])
        run_tiles(cx, body, S // 128)
    cx.barrier()


def phase_ffn(cx, S, x_d, g_row, wup_bf, wdn_bf, Z_d):
    G = 512 if S >= 512 else S
    NT = G // 128
    KC = D_MODEL // 128
    FC = D_FF // 128
    UW = 256
    wuv = wup_bf.rearrange("(kc p) f -> p kc f", p=128)
    wdv = wdn_bf.rearrange("(fc p) n -> p fc n", p=128)
    with contextlib.ExitStack() as st:
        gB = cx.sb(st, [128, D_MODEL], F32, name="gB")
        load_bcast_row(cx, 'sp', gB, g_row, D_MODEL)
        xt = [cx.sb(st, [128, D_MODEL], F32, name="xt") for _ in range(2)]
        junk = cx.sb(st, [128, D_MODEL], BF16, name="junk")
        ss = [cx.sb(st, [128, 1], F32, name="ss") for _ in range(2)]
        hb = [cx.sb(st, [128, D_MODEL], BF16, name="hb") for _ in range(2)]
        hT = cx.sb(st, [128, KC, G], BF16, name="hT")
        aT = cx.sb(st, [128, FC, G], BF16, name="aT")
        wu = [cx.sb(st, [128, KC, UW], BF16, name="wu") for _ in range(2)]
        wd = [cx.sb(st, [128, 8, 512], BF16, name="wd") for _ in range(2)]
        rl = [cx.sb(st, [128, G], F32, name="rl") for _ in range(2)]
        ob = [cx.sb(st, [128, 512], F32, name="ob") for _ in range(4)]
        trp = TrPool(cx, st, n=1)
        pup = [cx.ps(st, [128, G], F32, name="pup") for _ in range(2)]
        pdn = [cx.ps(st, [128, 512], F32, name="pdn") for _ in range(NT)]
        nu = 0
        nd = 0
        no = 0
        for gi in range(S // G):
            for tl in range(NT):
                tt = gi * NT + tl
                x, s_, h = xt[tt % 2], ss[tt % 2], hb[tt % 2]
                cx.dma('sp', x[:, :], x_d[tt * 128:(tt + 1) * 128, :], writes=[x])
                rms_rstd(cx, x, x[:, :], D_MODEL, s_, junk)
                cx.op('dve', [x, s_, gB], [h],
                      lambda e: e.scalar_tensor_tensor(out=h[:, :], in0=x[:, :], scalar=s_[:, 0:1], in1=gB[:, :],
                                                       op0=ALU.mult, op1=ALU.mult))
                trp.transpose_cols(h, lambda j: h[:, j * 128:(j + 1) * 128], KC, hT,
                                   lambda j0, cnt: hT[:, j0:j0 + cnt, tl * 128:(tl + 1) * 128])
            for uc in range(D_FF // UW):
                wub = wu[nu % 2]
                nu += 1
                cx.dma('sp', wub[:, :, :], wuv[:, :, uc * UW:(uc + 1) * UW], writes=[wub])
                for j in range(UW // 128):
                    fc = uc * (UW // 128) + j
                    pu = pup[fc % 2]
                    r = rl[fc % 2]
                    for kc in range(KC):
                        mm(cx, pu, pu[:, :], wub[:, kc, j * 128:(j + 1) * 128], hT[:, kc, :], [wub, hT],
                           kc == 0, kc == KC - 1)
                    cx.op('act', [pu], [r], lambda e: e.activation(out=r[:, :], in_=pu[:, :], func=AF.Relu))
                    eng = 'dve' if fc % 2 == 0 else 'pool'
                    cx.op(eng, [r], [aT], lambda e: e.tensor_tensor(aT[:, fc, :], r[:, :], r[:, :], ALU.mult))
            for nc_ in range(4):
                cs = slice(nc_ * 512, (nc_ + 1) * 512)
                for fg in range(FC // 8):
                    wdb = wd[nd % 2]
                    nd += 1
                    cx.dma('act', wdb[:, :, :], wdv[:, fg * 8:(fg + 1) * 8, cs], writes=[wdb])
                    for f8 in range(8):
                        fc = fg * 8 + f8
                        for tl in range(NT):
                            mm(cx, pdn[tl], pdn[tl][:, :], aT[:, fc, tl * 128:(tl + 1) * 128], wdb[:, f8, :],
                               [aT, wdb], fc == 0, fc == FC - 1)
                for tl in range(NT):
                    tt = gi * NT + tl
                    o = ob[no % 4]
                    no += 1
                    evac(cx, 'act' if no % 2 else 'dve', pdn[tl], pdn[tl][:, :], o, o[:, :])
                    cx.dma('pool', Z_d[tt * 128:(tt + 1) * 128, cs], o[:, :], reads=[o])
    cx.barrier()


def rope_tm(cx, src, x1, x2, c, s, dst, o1, o2, tmps, scale=None):
    (ta, tap), (tb, tbp) = tmps
    cx.op('dve', [src] + c[:1] + [], [ta], lambda e: e.tensor_tensor(tap, x1, c[1], ALU.mult))
    cx.op('pool', [src] + s[:1], [tb], lambda e: e.tensor_tensor(tbp, x2, s[1], ALU.mult))
    cx.op('dve', [ta, tb], [dst], lambda e: e.tensor_tensor(o1, tap, tbp, ALU.subtract))
    cx.op('pool', [src] + c[:1], [ta], lambda e: e.tensor_tensor(tap, x2, c[1], ALU.mult))
    cx.op('dve', [src] + s[:1], [tb], lambda e: e.tensor_tensor(tbp, x1, s[1], ALU.mult))
    cx.op('pool', [ta, tb], [dst], lambda e: e.tensor_tensor(o2, tap, tbp, ALU.add))
    if scale is not None:
        cx.op('pool', [dst], [dst], lambda e: e.tensor_scalar(o1, o1, scale, None, ALU.mult))
        cx.op('pool', [dst], [dst], lambda e: e.tensor_scalar(o2, o2, scale, None, ALU.mult))


def bcast_scalar_max(cx, st, trp_f, run_buf, out_col):
    pt = trp_f.bufs[0]
    transp(cx, pt, pt[0:1, 0, :], run_buf[:, 0:1], cx.identf[:, :], [run_buf, cx.identf])
    row = cx.sb(st, [1, 128], F32, name="mxrow")
    one = cx.sb(st, [1, 1], F32, name="mxone")
    evac(cx, 'dve', pt, pt[0:1, 0, :], row, row[:, :])
    cx.op('dve', [row], [one], lambda e: e.tensor_reduce(out=one[:, :], in_=row[:, :], axis=AX.X, op=ALU.max))
    mm(cx, pt, pt[:, 1, 0:1], cx.ones_f[0:1, :], one[0:1, 0:1], [cx.ones_f, one], True, True)
    evac(cx, 'dve', pt, pt[:, 1, 0:1], out_col, out_col[:, 0:1])


class AttnRes:
    def __init__(self, cx, st, W):
        self.sT = [cx.ps(st, [128, 512], F32, name="sT") for _ in range(2)]
        self.acc = [cx.ps(st, [128, 2, 256], F32, name="acc") for _ in range(4)]
        self.pT = [cx.sb(st, [128, 512], BF16, name="pT") for _ in range(3)]
        self.n = 0
        self.nq = 0


def attn_core(cx, res, S, qchunks, kchunks, vaug_fn, W, blocks_fn, epilogue, mode='softmax', qbs=None):
    QW = min(512, S)
    NJ = QW // 128
    for QB in (range(S // QW) if qbs is None else qbs):
        q0 = QB * QW
        blocks = blocks_fn(QB)
        accs = [res.acc[(res.nq % 2) * 2 + (j // 2)] for j in range(NJ)]
        res.nq += 1

        def stage1(b):
            sT = res.sT[res.n % 2]
            pT = res.pT[res.n % 3]
            res.n += 1
            nk, k0 = b['nk'], b['k0']
            nmm = len(qchunks) + (1 if b.get('extra') else 0)
            i = 0
            for (qb_, qap), (kb_, kap) in zip(qchunks, kchunks):
                ka = kap(k0, nk) if callable(kap) else kap[:, k0:k0 + nk]
                mm(cx, sT, sT[:nk, :QW], ka, qap[:, q0:q0 + QW], [qb_, kb_], i == 0, i == nmm - 1)
                i += 1
            if b.get('extra'):
                lb, lap, rb, rap = b['extra']
                mm(cx, sT, sT[:nk, :QW], lap, rap, list(lb) + list(rb), False, True)
            if mode == 'softmax':
                cx.op('act', [sT], [pT], lambda e: e.activation(out=pT[:nk, :QW], in_=sT[:nk, :QW], func=AF.Exp))
                if b.get('mask'):
                    mb, map_ = b['mask']
                    cx.op('pool' if nk == 128 else 'dve', [pT, mb], [pT],
                          lambda e: e.tensor_tensor(pT[:nk, :QW], pT[:nk, :QW], map_, ALU.mult))
            else:
                c, gb, gap = b['decay']
                cx.op('dve', [sT, gb], [pT],
                      lambda e: e.scalar_tensor_tensor(out=pT[:nk, :QW], in0=sT[:nk, :QW], scalar=float(c), in1=gap,
                                                       op0=ALU.mult, op1=ALU.mult))
            return pT

        def stage2(bi, b, pT):
            nk = b['nk']
            vb, vap = vaug_fn(b['kb'], nk)
            for j in range(NJ):
                a = accs[j]
                mm(cx, a, a[:, j % 2, :W], pT[:nk, j * 128:(j + 1) * 128], vap, [pT, vb],
                   bi == 0 and j % 2 == 0, bi == len(blocks) - 1, skip_group_check=True)

        prev = None
        for bi, b in enumerate(blocks):
            pT = stage1(b)
            if prev is not None:
                stage2(*prev)
            prev = (bi, b, pT)
        stage2(*prev)
        for j in range(NJ):
            epilogue(QB, j, accs[j], accs[j][:, j % 2, :W])


def causal_blocks(QB, QW, cmask):
    out = []
    nd = QW // 128
    for kb in range(nd * (QB + 1)):
        i = kb - nd * QB
        out.append(dict(kb=kb, k0=kb * 128, nk=128, mask=(cmask, cmask[:, i, :QW]) if i >= 0 else None))
    return out


MLA_SCALE = 192 ** -0.5


def phase_mla(cx, S, P_d, gq_row, gkv_row, wuq_bf, wukv_bf, cos_d, sin_d, cmask_d, Y_d, scr):
    NT = S // 128
    G = min(512, S)
    NG = G // 128
    with contextlib.ExitStack() as st:
        with contextlib.ExitStack() as s1:
            gq = cx.sb(s1, [128, 384], F32, name="gq")
            gkv = cx.sb(s1, [128, 128], F32, name="gkv")
            load_bcast_row(cx, 'sp', gq, gq_row, 384)
            load_bcast_row(cx, 'sp', gkv, gkv_row, 128)
            wuq = cx.sb(s1, [128, 3, 768], BF16, name="wuq")
            cx.dma('sp', wuq[:, :, :], wuq_bf.rearrange("(kc p) n -> p kc n", p=128), writes=[wuq])
            wukv = cx.sb(s1, [128, 1024], BF16, name="wukv")
            cx.dma('sp', wukv[:, :], wukv_bf[:, :], writes=[wukv])
            pm = [cx.sb(s1, [128, 576], F32, name="pm") for _ in range(2)]
            cs_t = [cx.sb(s1, [128, 64], F32, name="cs") for _ in range(2)]
            junk = cx.sb(s1, [128, 768], BF16, name="junk")
            ss = [cx.sb(s1, [128, 1], F32, name="ss") for _ in range(2)]
            nb = [cx.sb(s1, [128, 384], BF16, name="nb") for _ in range(2)]
            nT = [cx.sb(s1, [128, 3, 128], BF16, name="nT") for _ in range(2)]
            qf = [cx.sb(s1, [128, 4, 256], F32, name="qf") for _ in range(2)]
            qs = [cx.sb(s1, [128, 4, 193], BF16, name="qs") for _ in range(2)]
            qsf = [cx.sb(s1, [128, 4, 64], F32, name="qsf") for _ in range(2)]
            t1s = [cx.sb(s1, [128, 4, 32], F32, name="t1") for _ in range(2)]
            t2s = [cx.sb(s1, [128, 4, 32], F32, name="t2") for _ in range(2)]
            kr = [cx.sb(s1, [128, 65], BF16, name="kr") for _ in range(2)]
            krf = [cx.sb(s1, [128, 64], F32, name="krf") for _ in range(2)]
            kb16 = [cx.sb(s1, [128, 4, 128], BF16, name="kb16") for _ in range(2)]
            va = [cx.sb(s1, [128, 4, 129], BF16, name="va") for _ in range(2)]
            sqs = [cx.sb(s1, [128, 4, 256], F32, name="sq") for _ in range(2)]
            n4 = [cx.sb(s1, [128, 4], F32, name="n4") for _ in range(2)]
            n1 = [cx.sb(s1, [128, 1], F32, name="n1") for _ in range(2)]
            kmx = cx.sb(s1, [128, 1], F32, name="kmx")
            kmax = cx.sb(s1, [128, 1], F32, name="kmax")
            gA = cx.sb(s1, [128, 4, G], BF16, name="gA")
            gB_ = cx.sb(s1, [65, 4, G], BF16, name="gB_")
            trp = TrPool(cx, s1)
            trf = TrPool(cx, s1, n=1, dtype=F32)
            pq = [cx.ps(s1, [128, 512], F32, name="pq") for _ in range(2)]
            pq2 = [cx.ps(s1, [128, 512], F32, name="pq2") for _ in range(2)]
            cx.op('pool', [], [kmx], lambda e: e.memset(kmx[:, :], 0.0))

            def load_norm_T(tt, c0, n, gbuf, k):
                p = pm[k % 2]
                cx.dma('sp', p[:, :], P_d[tt * 128:(tt + 1) * 128, OFF_MLA:OFF_MLA + 576], writes=[p])
                s_ = ss[k % 2]
                rms_rstd(cx, p, p[:, c0:c0 + n], n, s_, junk)
                nbb = nb[k % 2]
                cx.op('dve', [p, s_, gbuf], [nbb],
                      lambda e: e.scalar_tensor_tensor(out=nbb[:, :n], in0=p[:, c0:c0 + n], scalar=s_[:, 0:1],
                                                       in1=gbuf[:, :n], op0=ALU.mult, op1=ALU.mult))
                nTb = nT[k % 2]
                trp.transpose_cols(nbb, lambda j: nbb[:, j * 128:(j + 1) * 128], n // 128, nTb,
                                   lambda j0, cnt: nTb[:, j0:j0 + cnt, :])
                return p, nTb

            def body(tt):
                t1, t2, sq = t1s[tt % 2], t2s[tt % 2], sqs[tt % 2]
                tl = tt % NG
                p, nTb = load_norm_T(tt, 384, 128, gkv, tt)
                c_t = cs_t[tt % 2]
                cx.dma('act', c_t[:, 0:32], cos_d[tt * 128:(tt + 1) * 128, :], writes=[c_t])
                cx.dma('act', c_t[:, 32:64], sin_d[tt * 128:(tt + 1) * 128, :], writes=[c_t])
                pa, pb = pq[tt % 2], pq2[tt % 2]
                mm(cx, pa, pa[:, :], nTb[:, 0, :], wukv[:, 0:512], [nTb, wukv], True, True)
                mm(cx, pb, pb[:, :], nTb[:, 0, :], wukv[:, 512:1024], [nTb, wukv], True, True)
                q = qf[tt % 2]
                evac(cx, 'act', pa, pa[:, :].rearrange("p (h c) -> p h c", h=2), q, q[:, 0:2, :])
                evac(cx, 'dve', pb, pb[:, :].rearrange("p (h c) -> p h c", h=2), q, q[:, 2:4, :])
                k16, vab, krb, krfb = kb16[tt % 2], va[tt % 2], kr[tt % 2], krf[tt % 2]
                cx.op('pool', [q], [k16], lambda e: e.tensor_copy(k16[:, :, :], q[:, :, 0:128]))
                cx.op('pool', [q], [vab], lambda e: e.tensor_copy(vab[:, :, 0:128], q[:, :, 128:256]))
                cx.op('pool', [], [vab], lambda e: e.memset(vab[:, :, 128:129], 1.0))
                rope_tm(cx, p, p[:, 512:544], p[:, 544:576], [c_t, c_t[:, 0:32]], [c_t, c_t[:, 32:64]],
                        krfb, krfb[:, 0:32], krfb[:, 32:64], [(t1, t1[:, 0, :]), (t2, t2[:, 0, :])])
                cx.op('pool', [krfb], [krb], lambda e: e.tensor_copy(krb[:, 0:64], krfb[:, :]))
                cx.op('pool', [], [krb], lambda e: e.memset(krb[:, 64:65], 1.0))
                cx.op('dve', [q], [sq], lambda e: e.tensor_tensor(sq[:, :, 0:128], q[:, :, 0:128], q[:, :, 0:128], ALU.mult))
                n4b, n1b = n4[tt % 2], n1[tt % 2]
                cx.op('dve', [sq], [n4b], lambda e: e.tensor_reduce(out=n4b[:, :], in_=sq[:, :, 0:128], axis=AX.X, op=ALU.add))
                cx.op('dve', [n4b], [n1b], lambda e: e.tensor_reduce(out=n1b[:, :], in_=n4b[:, :], axis=AX.X, op=ALU.max))
                cx.op('act', [krfb], [junk, n4b],
                      lambda e: e.activation(out=junk[:, :64], in_=krfb[:, :], func=AF.Square, accum_out=n4b[:, 0:1]))
                cx.op('dve', [n4b, n1b], [n1b], lambda e: e.tensor_tensor(n1b[:, :], n1b[:, :], n4b[:, 0:1], ALU.add))
                cx.op('dve', [n1b, kmx], [kmx], lambda e: e.tensor_tensor(kmx[:, :], kmx[:, :], n1b[:, :], ALU.max))
                trp.transpose_cols(k16, lambda j: k16[:, j, :], 4, gA,
                                   lambda j0, cnt: gA[:, j0:j0 + cnt, tl * 128:(tl + 1) * 128])
                trp.transpose_cols(krb, lambda j: krb[:, :], 1, gB_,
                                   lambda j0, cnt: gB_[:65, 0:1, tl * 128:(tl + 1) * 128], blkw=65)
                cx.dma('pool', scr['va'][tt * 128:(tt + 1) * 128, :, :], vab[:, :, :], reads=[vab])
                if tl == NG - 1:
                    def post():
                        g0 = (tt // NG) * G
                        cx.dma('pool', scr['knT'][:, :, g0:g0 + G].rearrange("h d s -> d h s"), gA[:, :, :], reads=[gA])
                        cx.dma('pool', scr['krT'][:, g0:g0 + G], gB_[:65, 0, :], reads=[gB_])
                    return post
            run_tiles(cx, body, NT)
            bcast_scalar_max(cx, s1, trf, kmx, kmax)
            cx.op('act', [kmax], [kmax], lambda e: e.activation(out=kmax[:, :], in_=kmax[:, :], func=AF.Sqrt))
            def body(tt):
                t1, t2, sq = t1s[tt % 2], t2s[tt % 2], sqs[tt % 2]
                tl = tt % NG
                p, nTb = load_norm_T(tt, 0, 384, gq, tt)
                c_t = cs_t[tt % 2]
                cx.dma('act', c_t[:, 0:32], cos_d[tt * 128:(tt + 1) * 128, :], writes=[c_t])
                cx.dma('act', c_t[:, 32:64], sin_d[tt * 128:(tt + 1) * 128, :], writes=[c_t])
                pa, pb = pq[tt % 2], pq2[tt % 2]
                for kc in range(3):
                    mm(cx, pa, pa[:, :], nTb[:, kc, :], wuq[:, kc, 0:512], [nTb, wuq], kc == 0, kc == 2)
                for kc in range(3):
                    mm(cx, pb, pb[:, :256], nTb[:, kc, :], wuq[:, kc, 512:768], [nTb, wuq], kc == 0, kc == 2)
                q = qf[tt % 2]
                qv = q[:, :, :].rearrange("p h c -> p (h c)")
                evac(cx, 'act', pa, pa[:, :], q, qv[:, 0:512])
                evac(cx, 'dve', pb, pb[:, :256], q, qv[:, 512:768])
                qh = qv[:, 0:768].rearrange("p (h c) -> p h c", h=4)
                cx.op('dve', [q], [sq], lambda e: e.tensor_tensor(sq[:, :, 0:192], qh, qh, ALU.mult))
                n4b = n4[tt % 2]
                cx.op('dve', [sq], [n4b], lambda e: e.tensor_reduce(out=n4b[:, :], in_=sq[:, :, 0:192], axis=AX.X, op=ALU.add))
                cx.op('act', [n4b], [n4b], lambda e: e.activation(out=n4b[:, :], in_=n4b[:, :], func=AF.Sqrt))
                cx.op('dve', [n4b, kmax], [n4b],
                      lambda e: e.tensor_scalar(n4b[:, :], n4b[:, :], kmax[:, 0:1], -MLA_SCALE, ALU.mult, ALU.mult))
                qsb, qsfb = qs[tt % 2], qsf[tt % 2]
                cb = c_t[:, 0:32].unsqueeze(1).broadcast_to([128, 4, 32])
                sb_ = c_t[:, 32:64].unsqueeze(1).broadcast_to([128, 4, 32])
                rope_tm(cx, q, qh[:, :, 128:160], qh[:, :, 160:192], [c_t, cb], [c_t, sb_],
                        qsfb, qsfb[:, :, 0:32], qsfb[:, :, 32:64], [(t1, t1[:, :, :]), (t2, t2[:, :, :])])
                cx.op('act', [q], [qsb], lambda e: e.activation(out=qsb[:, :, 0:128], in_=qh[:, :, 0:128], func=AF.Copy, scale=MLA_SCALE))
                cx.op('act', [qsfb], [qsb], lambda e: e.activation(out=qsb[:, :, 128:192], in_=qsfb[:, :, :], func=AF.Copy, scale=MLA_SCALE))
                cx.op('pool', [n4b], [qsb], lambda e: e.tensor_copy(qsb[:, :, 192:193], n4b[:, :].unsqueeze(2)))
                trp.transpose_cols(qsb, lambda j: qsb[:, j, 0:128], 4, gA,
                                   lambda j0, cnt: gA[:, j0:j0 + cnt, tl * 128:(tl + 1) * 128])
                trp.transpose_cols(qsb, lambda j: qsb[:, j, 128:193], 4, gB_,
                                   lambda j0, cnt: gB_[:65, j0:j0 + cnt, tl * 128:(tl + 1) * 128], blkw=65)
                if tl == NG - 1:
                    def post():
                        g0 = (tt // NG) * G
                        cx.dma('pool', scr['qnT'][:, :, g0:g0 + G].rearrange("h d s -> d h s"), gA[:, :, :], reads=[gA])
                        cx.dma('pool', scr['qrT'][:, :, g0:g0 + G].rearrange("h d s -> d h s"), gB_[:65, :, :], reads=[gB_])
                    return post
            run_tiles(cx, body, NT)
        cx.barrier()
        res = AttnRes(cx, st, 129)
        cmask = cx.sb(st, [128, 4, 512], BF16, name="cmask")
        cx.dma('sp', cmask[:, :, :], cmask_d.rearrange("i k q -> k i q"), writes=[cmask])
        krT = cx.sb(st, [65, S], BF16, name="krT")
        cx.dma('sp', krT[:, :], scr['krT'][:, :], writes=[krT])
        qn = [cx.sb(st, [128, S], BF16, name="qn") for _ in range(2)]
        qr = [cx.sb(st, [65, S], BF16, name="qr") for _ in range(2)]
        kn = [cx.sb(st, [128, S], BF16, name="kn") for _ in range(2)]
        vv = [cx.sb(st, [128, NT, 129], BF16, name="vv") for _ in range(2)]
        rc = [cx.sb(st, [128, 1], F32, name="rc") for _ in range(2)]
        ot = [cx.sb(st, [128, 128], F32, name="ot") for _ in range(2)]
        cnt = [0]
        QW = min(512, S)
        for h in range(4):
            a, b, c, v = qn[h % 2], qr[h % 2], kn[h % 2], vv[h % 2]
            cx.dma('sp', a[:, :], scr['qnT'][h], writes=[a])
            cx.dma('sp', b[:, :], scr['qrT'][h], writes=[b])
            cx.dma('sp', c[:, :], scr['knT'][h], writes=[c])
            cx.dma('sp', v[:, :, :], scr['va'][:, h, :].rearrange("(t p) c -> p t c", p=128), writes=[v])

            def epi(QB, j, accb, acc_ap, h=h):
                k = cnt[0]
                cnt[0] += 1
                r, o = rc[k % 2], ot[k % 2]
                cx.op('dve', [accb], [r], lambda e: e.tensor_scalar(r[:, :], acc_ap[:, 128:129], 1e-30, None, ALU.add))
                cx.op('dve', [r], [r], lambda e: e.reciprocal(r[:, :], r[:, :]))
                cx.op('act', [accb, r], [o], lambda e: e.activation(out=o[:, :], in_=acc_ap[:, 0:128], func=AF.Copy, scale=r[:, 0:1]))
                t0 = QB * QW + j * 128
                cx.dma('pool', Y_d[t0:t0 + 128, h * 128:(h + 1) * 128], o[:, :], reads=[o])

            attn_core(cx, res, S, [(a, a[:, :]), (b, b[:65, :])], [(c, c[:, :]), (krT, krT[:65, :])],
                      lambda kb, nk, v=v: (v, v[:nk, kb, :]), 129,
                      lambda QB: causal_blocks(QB, QW, cmask), epi)
    cx.barrier()


def head_norm_tm(cx, src_buf, src_ap, n, eps, cbuf, c_ap, s1, s2, junk):
    cx.op('dve', [src_buf], [s1], lambda e: e.tensor_reduce(out=s1[:, 0:1], in_=src_ap, axis=AX.X, op=ALU.add))
    cx.op('dve', [s1], [s1], lambda e: e.tensor_scalar(s1[:, 0:1], s1[:, 0:1], 1.0 / n, None, ALU.mult))
    cx.op('dve', [src_buf, s1], [cbuf], lambda e: e.tensor_scalar(c_ap, src_ap, s1[:, 0:1], None, ALU.subtract))
    rms_rstd(cx, cbuf, c_ap, n, s2, junk, eps=eps)


def phase_ret(cx, S, P_d, cos_d, sin_d, gdec_d, Y_d, scr):
    NT = S // 128
    G = min(512, S)
    NG = G // 128
    QW = min(512, S)
    with contextlib.ExitStack() as st:
        with contextlib.ExitStack() as s1:
            pr = [cx.sb(s1, [128, 1024], F32, name="pr") for _ in range(2)]
            cs_t = [cx.sb(s1, [128, 64], F32, name="cs") for _ in range(2)]
            ro = [cx.sb(s1, [128, 8, 64], F32, name="ro") for _ in range(2)]
            rb = [cx.sb(s1, [128, 8, 64], BF16, name="rb") for _ in range(2)]
            vb = [cx.sb(s1, [128, 512], BF16, name="vb") for _ in range(2)]
            t1s = [cx.sb(s1, [128, 8, 32], F32, name="t1") for _ in range(2)]
            t2s = [cx.sb(s1, [128, 8, 32], F32, name="t2") for _ in range(2)]
            gQ = cx.sb(s1, [64, 8, G], BF16, name="gQ")
            trp = TrPool(cx, s1)
            def body(tt):
                t1, t2 = t1s[tt % 2], t2s[tt % 2]
                tl = tt % NG
                rows = slice(tt * 128, (tt + 1) * 128)
                p, c_t, r, rbb, v = pr[tt % 2], cs_t[tt % 2], ro[tt % 2], rb[tt % 2], vb[tt % 2]
                cx.dma('sp', p[:, :], P_d[rows, OFF_RET:OFF_RET + 1024], writes=[p])
                cx.dma('act', c_t[:, 0:32], cos_d[rows, :], writes=[c_t])
                cx.dma('act', c_t[:, 32:64], sin_d[rows, :], writes=[c_t])
                qk = p[:, 0:512].rearrange("p (h c) -> p h c", h=8)
                cb = c_t[:, 0:32].unsqueeze(1).broadcast_to([128, 8, 32])
                sb_ = c_t[:, 32:64].unsqueeze(1).broadcast_to([128, 8, 32])
                rope_tm(cx, p, qk[:, :, 0:32], qk[:, :, 32:64], [c_t, cb], [c_t, sb_],
                        r, r[:, :, 0:32], r[:, :, 32:64], [(t1, t1[:, :, :]), (t2, t2[:, :, :])])
                cx.op('act', [r], [rbb], lambda e: e.copy(rbb[:, 0:4, :], r[:, 0:4, :]))
                cx.op('act', [r], [rbb], lambda e: e.activation(out=rbb[:, 4:8, :], in_=r[:, 4:8, :], func=AF.Copy, scale=0.125))
                cx.op('pool', [p], [v], lambda e: e.tensor_copy(v[:, :], p[:, 512:1024]))
                trp.transpose_cols(rbb, lambda j: rbb[:, j, :], 8, gQ,
                                   lambda j0, cnt: gQ[:64, j0:j0 + cnt, tl * 128:(tl + 1) * 128], blkw=64)
                cx.dma('pool', scr['rv'][rows, :, :].rearrange("s h c -> s (h c)"), v[:, :], reads=[v])
                if tl == NG - 1:
                    def post():
                        g0 = (tt // NG) * G
                        cx.dma('pool', scr['rqT'][:, :, g0:g0 + G].rearrange("h d s -> d h s"), gQ[:64, 0:4, :], reads=[gQ])
                        cx.dma('pool', scr['rkT'][:, :, g0:g0 + G].rearrange("h d s -> d h s"), gQ[:64, 4:8, :], reads=[gQ])
                    return post
            run_tiles(cx, body, NT)
        cx.barrier()
        res = AttnRes(cx, st, 128)
        gd = [cx.sb(st, [128, 5, 512], F32, name="gd") for _ in range(2)]
        qT = [cx.sb(st, [64, S], BF16, name="qT") for _ in range(2)]
        kT = [cx.sb(st, [64, S], BF16, name="kT") for _ in range(2)]
        vv = [cx.sb(st, [128, NT, 128], BF16, name="vv") for _ in range(2)]
        gt = [cx.sb(st, [128, 128], F32, name="gt") for _ in range(2)]
        cb_ = [cx.sb(st, [128, 128], F32, name="cb") for _ in range(2)]
        ot = [cx.sb(st, [128, 128], F32, name="ot") for _ in range(2)]
        sA = [cx.sb(st, [128, 1], F32, name="sA") for _ in range(2)]
        sB = [cx.sb(st, [128, 1], F32, name="sB") for _ in range(2)]
        junk = cx.sb(st, [128, 128], BF16, name="junk")
        cnt = [0]
        for h in range(4):
            gamma = 1.0 - 2.0 ** (-5 - h)
            a, c, v, g = qT[h % 2], kT[h % 2], vv[h % 2], gd[h % 2]
            cx.dma('sp', a[:, :], scr['rqT'][h], writes=[a])
            cx.dma('sp', c[:, :], scr['rkT'][h], writes=[c])
            cx.dma('sp', v[:, :, :], scr['rv'][:, h, :].rearrange("(t p) c -> p t c", p=128), writes=[v])
            cx.dma('sp', g[:, :, :], gdec_d[h].rearrange("i k q -> k i q"), writes=[g])
            if 'dbg' in scr and h == 0:
                cx.dma('sp', scr['dbg'][0], a[:, :], reads=[a])
                cx.dma('sp', scr['dbg'][1], c[:, :], reads=[c])

            def blocks(QB, g=g, gamma=gamma):
                out = []
                nd = QW // 128
                for kb in range(nd * (QB + 1)):
                    i = kb - nd * QB
                    if i >= 0:
                        out.append(dict(kb=kb, k0=kb * 128, nk=128, decay=(1.0, g, g[:, 1 + i, :QW])))
                    else:
                        cc = gamma ** (QB * QW - kb * 128)
                        if cc < 1e-30:
                            cc = 0.0
                        out.append(dict(kb=kb, k0=kb * 128, nk=128, decay=(cc, g, g[:, 0, :QW])))
                return out

            def epi(QB, j, accb, acc_ap, h=h):
                k = cnt[0]
                cnt[0] += 1
                t0 = QB * QW + j * 128
                gtb, cbb, o, s_a, s_b = gt[k % 2], cb_[k % 2], ot[k % 2], sA[k % 2], sB[k % 2]
                cx.dma('act', gtb[:, :], P_d[t0:t0 + 128, OFF_RET + 1024 + h * 128:OFF_RET + 1024 + (h + 1) * 128], writes=[gtb])
                cx.op('act', [gtb], [gtb], lambda e: e.activation(out=gtb[:, :], in_=gtb[:, :], func=AF.Silu))
                head_norm_tm(cx, accb, acc_ap, 128, NORM_EPS, cbb, cbb[:, :], s_a, s_b, junk)
                cx.op('dve', [cbb, s_b, gtb], [o],
                      lambda e: e.scalar_tensor_tensor(out=o[:, :], in0=cbb[:, :], scalar=s_b[:, 0:1], in1=gtb[:, :],
                                                       op0=ALU.mult, op1=ALU.mult))
                cx.dma('pool', Y_d[t0:t0 + 128, h * 128:(h + 1) * 128], o[:, :], reads=[o])

            attn_core(cx, res, S, [(a, a[:64, :])], [(c, c[:64, :])], lambda kb, nk, v=v: (v, v[:nk, kb, :]), 128,
                      blocks, epi, mode='decay')
    cx.barrier()


def bc8(ap):
    return ap.unsqueeze(2).broadcast_to([128, 8, 64])


def v3(ap):
    return ap.rearrange("p (h c) -> p h c", h=8)


def phase_rwkv(cx, S, l, P_d, W, Wb, C, Y_d, scr):
    NT = S // 128
    RW = scr['rw']
    names6 = ['rr', 'lw', 'k2', 'vv', 'kn', 'aa']
    with contextlib.ExitStack() as st:
        def brow(name, n, src):
            b = cx.sb(st, [128, n], F32, name=name)
            load_bcast_row(cx, 'sp', b, src, n)
            return b
        muB = brow("muB", 1984, W['rwkv_mu'][l])
        w0B = brow("w0B", 512, W['rwkv_w0'][l])
        a0B = brow("a0B", 512, W['rwkv_a0'][l])
        kkB = brow("kkB", 512, W['rwkv_k_k'][l])
        kaB = brow("kaB", 512, W['rwkv_k_a'][l])
        rkB = brow("rkB", 512, W['rwkv_r_k'][l].rearrange("h c -> (h c)"))
        w2 = cx.sb(st, [96, 512], BF16, name="w2")
        a2 = cx.sb(st, [96, 512], BF16, name="a2")
        g2 = cx.sb(st, [128, 2, 512], BF16, name="g2")
        cx.dma('sp', w2[:, :], Wb['rwkv_w2'][l], writes=[w2])
        cx.dma('sp', a2[:, :], Wb['rwkv_a2'][l], writes=[a2])
        cx.dma('sp', g2[:, :, :], Wb['rwkv_g2'][l].rearrange("(kc p) n -> p kc n", p=128), writes=[g2])
        z = [cx.sb(st, [128, 1984], F32, name="z") for _ in range(2)]
        zp = [cx.sb(st, [128, 1984], F32, name="zp") for _ in range(2)]
        lo = [cx.sb(st, [128, 512], BF16, name="lo") for _ in range(2)]
        loT = [cx.sb(st, [128, 4, 128], BF16, name="loT") for _ in range(2)]
        o7 = [cx.sb(st, [128, 7, 512], F32, name="o7") for _ in range(2)]
        s8 = [cx.sb(st, [128, 8], F32, name="s8") for _ in range(2)]
        b8 = [cx.sb(st, [128, 8], F32, name="b8") for _ in range(2)]
        trp = TrPool(cx, st)
        pps = [[cx.ps(st, [128, 512], F32, name="pp") for _ in range(3)] for _ in range(2)]
        def body(tt):
            rows = slice(tt * 128, (tt + 1) * 128)
            zb, zpb, lob, loTb, o, s8b, b8b = z[tt % 2], zp[tt % 2], lo[tt % 2], loT[tt % 2], o7[tt % 2], s8[tt % 2], b8[tt % 2]
            cx.dma('sp', zb[:, :], P_d[rows, OFF_RWKV:OFF_RWKV + 1984], writes=[zb])
            if tt == 0:
                cx.op('pool', [], [zpb], lambda e: e.memset(zpb[0:1, :], 0.0))
                cx.dma('act', zpb[1:128, :], P_d[0:127, OFF_RWKV:OFF_RWKV + 1984], writes=[zpb])
            else:
                cx.dma('act', zpb[:, :], P_d[tt * 128 - 1:tt * 128 + 127, OFF_RWKV:OFF_RWKV + 1984], writes=[zpb])
            cx.op('dve', [zpb, zb], [zpb], lambda e: e.tensor_tensor(zpb[:, :], zpb[:, :], zb[:, :], ALU.subtract))
            cx.op('dve', [zpb, muB], [zpb], lambda e: e.tensor_tensor(zpb[:, :], zpb[:, :], muB[:, :], ALU.mult))
            cx.op('pool', [zpb, zb], [zb], lambda e: e.tensor_tensor(zb[:, :], zb[:, :], zpb[:, :], ALU.add))
            r_, k_, v_ = zb[:, 0:512], zb[:, 512:1024], zb[:, 1024:1536]
            cx.op('act', [zb], [lob], lambda e: e.activation(out=lob[:, 0:96], in_=zb[:, 1536:1632], func=AF.Tanh))
            cx.op('act', [zb], [lob], lambda e: e.copy(lob[:, 128:224], zb[:, 1632:1728]))
            cx.op('act', [zb], [lob], lambda e: e.activation(out=lob[:, 256:512], in_=zb[:, 1728:1984], func=AF.Sigmoid))
            trp.transpose_cols(lob, lambda j: lob[:, j * 128:j * 128 + 96], 2, loTb,
                               lambda j0, cnt: loTb[:96, j0:j0 + cnt, :], blkw=96)
            trp.transpose_cols(lob, lambda j: lob[:, 256 + j * 128:384 + j * 128], 2, loTb,
                               lambda j0, cnt: loTb[:, 2 + j0:2 + j0 + cnt, :])
            pu, pa, pg = pps[tt % 2]
            mm(cx, pu, pu[:, :], loTb[:96, 0, :], w2[:96, :], [loTb, w2], True, True)
            mm(cx, pa, pa[:, :], loTb[:96, 1, :], a2[:96, :], [loTb, a2], True, True)
            mm(cx, pg, pg[:, :], loTb[:, 2, :], g2[:, 0, :], [loTb, g2], True, False)
            mm(cx, pg, pg[:, :], loTb[:, 3, :], g2[:, 1, :], [loTb, g2], False, True)
            lw_, k2_, kn_, aa_, gg_, t1_, t2_ = [o[:, i, :] for i in range(7)]
            cx.op('dve', [pu, w0B], [o], lambda e: e.tensor_tensor(t1_, pu[:, :], w0B[:, :], ALU.add))
            cx.op('act', [o], [o], lambda e: e.activation(out=t1_, in_=t1_, func=AF.Sigmoid))
            cx.op('act', [o], [o], lambda e: e.activation(out=lw_, in_=t1_, func=AF.Copy, scale=-0.6065306597126334))
            cx.op('dve', [pa, a0B], [o], lambda e: e.tensor_tensor(t2_, pa[:, :], a0B[:, :], ALU.add))
            cx.op('act', [o], [o], lambda e: e.activation(out=aa_, in_=t2_, func=AF.Sigmoid))
            cx.op('act', [pg], [o], lambda e: e.copy(gg_, pg[:, :]))
            cx.op('dve', [zb, kkB], [o], lambda e: e.tensor_tensor(kn_, k_, kkB[:, :], ALU.mult))
            cx.op('act', [o], [o], lambda e: e.activation(out=t1_, in_=kn_, func=AF.Square))
            cx.op('dve', [o], [s8b], lambda e: e.tensor_reduce(out=s8b[:, :], in_=v3(t1_), axis=AX.X, op=ALU.add))
            cx.op('dve', [s8b], [s8b], lambda e: e.tensor_scalar(s8b[:, :], s8b[:, :], 1e-24, None, ALU.max))
            cx.op('pool', [s8b, cx.neghalf], [s8b],
                  lambda e: e.tensor_tensor(s8b[:, :], s8b[:, :], cx.neghalf[:, 0:1].to_broadcast([128, 8]), ALU.pow))
            cx.op('dve', [o, s8b], [o], lambda e: e.tensor_tensor(v3(kn_), v3(kn_), bc8(s8b[:, :]), ALU.mult))
            cx.op('dve', [o, kaB], [o],
                  lambda e: e.scalar_tensor_tensor(out=t2_, in0=aa_, scalar=-1.0, in1=kaB[:, :], op0=ALU.add, op1=ALU.mult))
            cx.op('pool', [o], [o], lambda e: e.tensor_scalar(t2_, t2_, 1.0, None, ALU.add))
            cx.op('dve', [o, zb], [o], lambda e: e.tensor_tensor(k2_, k_, t2_, ALU.mult))
            cx.op('pool', [o, zb], [o], lambda e: e.tensor_tensor(t1_, r_, k2_, ALU.mult))
            cx.op('dve', [o, rkB], [o], lambda e: e.tensor_tensor(t1_, t1_, rkB[:, :], ALU.mult))
            cx.op('dve', [o], [b8b], lambda e: e.tensor_reduce(out=b8b[:, :], in_=v3(t1_), axis=AX.X, op=ALU.add))
            cx.dma('pool', RW['rr'][rows, :], r_, reads=[zb])
            cx.dma('pool', RW['vv'][rows, :], v_, reads=[zb])
            cx.dma('pool', RW['lw'][rows, :], lw_, reads=[o])
            cx.dma('pool', RW['k2'][rows, :], k2_, reads=[o])
            cx.dma('pool', RW['kn'][rows, :], kn_, reads=[o])
            cx.dma('pool', RW['aa'][rows, :], aa_, reads=[o])
            cx.dma('pool', RW['gg'][rows, :], gg_, reads=[o])
            cx.dma('pool', RW['bc'][rows, :], b8b[:, :], reads=[b8b])
        run_tiles(cx, body, NT)
    cx.barrier()
    with contextlib.ExitStack() as st:
        rwm = cx.sb(st, [128, 384], F32, name="rwm")
        cx.dma('sp', rwm[:, :], C['rwm'][:, :], writes=[rwm])
        mask4 = cx.sb(st, [128, 512], F32, name="mask4")
        cx.op('pool', [rwm], [mask4], lambda e: e.tensor_copy(mask4[:, 0:256], rwm[:, 0:256]))
        cx.op('pool', [rwm], [mask4], lambda e: e.tensor_copy(mask4[:, 256:512], rwm[:, 0:256]))
        gwB = cx.sb(st, [128, 512], F32, name="gwB")
        gbB = cx.sb(st, [128, 512], F32, name="gbB")
        load_bcast_row(cx, 'sp', gwB, W['rwkv_gn_w'][l], 512)
        load_bcast_row(cx, 'sp', gbB, W['rwkv_gn_b'][l], 512)
        IN = [cx.sb(st, [128, 6, 512], F32, name="IN") for _ in range(2)]
        ELs = [cx.sb(st, [128, 3, 512], F32, name="EL") for _ in range(2)]
        TMs = [cx.sb(st, [128, 4, 512], F32, name="TM") for _ in range(2)]
        XTs = [cx.sb(st, [64, 8, 4, 128], F32, name="XT") for _ in range(2)]
        MMs = [cx.sb(st, [128, 8, 512], F32, name="MM") for _ in range(2)]
        XXs = [[cx.sb(st, [128, 8, 2, 128], F32, name="XX") for _ in range(2)] for _ in range(2)]
        NTs = [cx.sb(st, [128, 8, 128], F32, name="NT") for _ in range(2)]
        pcs = [cx.sb(st, [64, 8], F32, name="pc") for _ in range(2)]
        ST = cx.sb(st, [64, 8, 64], F32, name="ST")
        STs = cx.sb(st, [64, 8, 64], F32, name="STs")
        Yb = cx.sb(st, [128, 8, 64], F32, name="Yb")
        Ub = cx.sb(st, [128, 8, 64], F32, name="Ub")
        Ob = cx.sb(st, [128, 512], F32, name="Ob")
        G3 = [cx.sb(st, [128, 512], F32, name="G3") for _ in range(2)]
        b8 = [cx.sb(st, [128, 8], F32, name="b8") for _ in range(2)]
        m8 = cx.sb(st, [128, 8], F32, name="m8")
        r8 = cx.sb(st, [128, 8], F32, name="r8")
        t512 = cx.sb(st, [128, 512], F32, name="t512")
        yo = [cx.sb(st, [128, 512], F32, name="yo") for _ in range(2)]
        PTp = [cx.ps(st, [128, 512], F32, name="ptr") for _ in range(2)]
        PDp = [cx.ps(st, [128, 4, 128], F32, name="pD") for _ in range(3)]
        pY = cx.ps(st, [128, 512], F32, name="pY")
        pU = cx.ps(st, [128, 512], F32, name="pU")
        pO = cx.ps(st, [128, 512], F32, name="pO")
        cnt = {'pt': 0, 'pd': 0}

        def get_pt():
            cnt['pt'] += 1
            return PTp[cnt['pt'] % 2]

        def get_pd():
            cnt['pd'] += 1
            return PDp[cnt['pd'] % 3]

        cx.op('pool', [], [ST], lambda e: e.memset(ST[:, :, :], 0.0))
        MUs, MUi, MLs = rwm[:, 0:128], rwm[:, 128:256], rwm[:, 256:384]

        def pre(c):
            rows = slice(c * 128, (c + 1) * 128)
            I6, EL, TM, XT, MM_, XX, NTb, pc = IN[c % 2], ELs[c % 2], TMs[c % 2], XTs[c % 2], MMs[c % 2], XXs[c % 2], NTs[c % 2], pcs[c % 2]
            for i, nm in enumerate(names6):
                cx.dma('sp' if i % 2 == 0 else 'act', I6[:, i, :], RW[nm][rows, :], writes=[I6])
            rr, lw, k2, vv, kn, aa = [I6[:, i, :] for i in range(6)]

            def u_cumsum():
                pL = get_pt()
                mm(cx, pL, pL[:, :], MUi, lw, [rwm, I6], True, True)
                cx.op('act', [pL], [EL], lambda e: e.activation(out=EL[:, 0, :], in_=pL[:, :], func=AF.Exp))
                cx.op('act', [pL], [EL], lambda e: e.activation(out=EL[:, 1, :], in_=pL[:, :], func=AF.Exp, scale=-1.0))
                cx.op('dve', [pL, I6], [EL], lambda e: e.tensor_tensor(EL[:, 2, :], pL[:, :], lw, ALU.subtract))
            atomic(cx, u_cumsum)
            cx.op('act', [EL], [EL], lambda e: e.activation(out=EL[:, 2, :], in_=EL[:, 2, :], func=AF.Exp))
            cx.op('dve', [I6, EL], [TM],
                  lambda e: e.scalar_tensor_tensor(out=TM[:, 0, :], in0=kn, scalar=-1.0, in1=EL[:, 2, :], op0=ALU.mult, op1=ALU.mult))
            cx.op('pool', [I6, EL], [TM], lambda e: e.tensor_tensor(TM[:, 1, :], rr, EL[:, 0, :], ALU.mult))
            cx.op('dve', [I6], [TM], lambda e: e.tensor_tensor(TM[:, 2, :], kn, aa, ALU.mult))
            cx.op('dve', [TM, EL], [TM], lambda e: e.tensor_tensor(TM[:, 2, :], TM[:, 2, :], EL[:, 1, :], ALU.mult))
            cx.op('pool', [I6, EL], [TM], lambda e: e.tensor_tensor(TM[:, 3, :], k2, EL[:, 1, :], ALU.mult))

            def u_pc():
                ppc = get_pt()
                for h in range(8):
                    mm(cx, ppc, ppc[:64, h:h + 1], lw[:, h * 64:(h + 1) * 64], cx.ones_f[:, 0:1], [I6, cx.ones_f], True, True)
                cx.op('act', [ppc], [pc], lambda e: e.activation(out=pc[:, :], in_=ppc[:64, 0:8], func=AF.Exp))
            atomic(cx, u_pc)

            def u_tr(q, hh, k):
                pt = get_pt()
                for j in range(4):
                    h = hh * 4 + j
                    transp(cx, pt, pt[:64, j * 128:(j + 1) * 128], TM[:, q, h * 64:(h + 1) * 64], cx.identf[:, :], [TM, cx.identf])
                evac(cx, 'act' if k % 2 else 'dve', pt, pt[:64, :].rearrange("p (j t) -> p j t", j=4), XT,
                     fr(XT[:, hh * 4:(hh + 1) * 4, q, :]))
            k = 0
            for q in range(4):
                for hh in range(2):
                    k += 1
                    atomic(cx, lambda q=q, hh=hh, k=k: u_tr(q, hh, k))

            def u_setup(h):
                pA = get_pt()
                pd = get_pd()
                ar = XT[:, h, 0:2, :].rearrange("p q t -> p (q t)")
                mm(cx, pA, pA[:, 0:256], fr(XT[:, h, 2, :]), fr(ar), [XT], True, True)
                mm(cx, pA, pA[:, 256:512], fr(XT[:, h, 3, :]), fr(ar), [XT], True, True)
                mm(cx, pd, pd[:, 0, :], fr(XT[:, h, 0, :]), fr(XT[:, h, 2, :]), [XT], True, True)
                cx.op('dve', [pA, mask4], [MM_], lambda e: e.tensor_tensor(MM_[:, h, :], pA[:, :], mask4[:, :], ALU.mult))
                cx.op('dve', [pd, rwm], [XX[0]], lambda e: e.tensor_tensor(fr(XX[0][:, h, 1, :]), pd[:, 0, :], MLs, ALU.mult))
                cx.op('pool', [MM_], [XX[0]], lambda e: e.tensor_copy(fr(XX[0][:, h, 0, :]), MM_[:, h, 0:128]))
                cx.op('pool', [MM_, cx.identf], [NTb], lambda e: e.tensor_tensor(fr(NTb[:, h, :]), MM_[:, h, 0:128], cx.identf[:, :], ALU.add))
            for h in range(8):
                atomic(cx, lambda h=h: u_setup(h))

            def u_sq(lev, p):
                cur, nxt = XX[(lev - 1) % 2], XX[lev % 2]
                pd = get_pd()
                for j in range(2):
                    h = 2 * p + j
                    if lev < 6:
                        mm(cx, pd, pd[:, 2 * j, :], fr(cur[:, h, 1, :]), fr(cur[:, h, 0, :]), [cur], True, True)
                    mm(cx, pd, pd[:, 2 * j + 1, :], fr(cur[:, h, 0, :]), fr(cur[:, h, 1, :]), [cur], True, True)
                if lev < 6:
                    evac(cx, 'act' if p % 2 else 'dve', pd, pd[:, :, :], nxt,
                         fr(nxt[:, 2 * p:2 * p + 2, :, :].rearrange("p h q t -> p (h q) t")))
                else:
                    for j in range(2):
                        evac(cx, 'act' if j else 'dve', pd, pd[:, 2 * j + 1, :], nxt, fr(nxt[:, 2 * p + j, 1, :]))

            def u_n(lev, p):
                nxt = XX[lev % 2]
                pd = get_pd()
                for j in range(2):
                    h = 2 * p + j
                    mm(cx, pd, pd[:, j, :], fr(nxt[:, h, 1, :]), fr(NTb[:, h, :]), [nxt, NTb], True, True)
                cx.op('dve', [pd, NTb], [NTb],
                      lambda e: e.tensor_tensor(fr(NTb[:, 2 * p:2 * p + 2, :]), NTb[:, 2 * p:2 * p + 2, :], pd[:, 0:2, :], ALU.add))
            for lev in range(1, 7):
                for p in range(4):
                    atomic(cx, lambda lev=lev, p=p: u_sq(lev, p))
                for p in range(4):
                    atomic(cx, lambda lev=lev, p=p: u_n(lev, p))

        def seq(c):
            rows = slice(c * 128, (c + 1) * 128)
            I6, TM, XT, MM_, NTb, pc = IN[c % 2], TMs[c % 2], XTs[c % 2], MMs[c % 2], NTs[c % 2], pcs[c % 2]
            vv = I6[:, 3, :]
            cx.op('pool', [ST, pc], [STs],
                  lambda e: e.tensor_tensor(STs[:, :, :], ST[:, :, :], pc[:, :].unsqueeze(2).broadcast_to([64, 8, 64]), ALU.mult))
            for h in range(8):
                hs = slice(h * 64, (h + 1) * 64)
                mm(cx, pY, pY[:, hs], XT[:, h, 0, :], ST[:, h, :], [XT, ST], True, False)
                mm(cx, pY, pY[:, hs], MM_[:, h, 256:384], vv[:, hs], [MM_, I6], False, True)
            evac(cx, 'dve', pY, pY[:, 0:256], Yb, Yb[:, 0:4, :].rearrange("p h c -> p (h c)"))
            evac(cx, 'act', pY, pY[:, 256:512], Yb, Yb[:, 4:8, :].rearrange("p h c -> p (h c)"))
            for h in range(8):
                hs = slice(h * 64, (h + 1) * 64)
                mm(cx, pU, pU[:, hs], NTb[:, h, :], Yb[:, h, :], [NTb, Yb], True, True)
            evac(cx, 'dve', pU, pU[:, 0:256], Ub, Ub[:, 0:4, :].rearrange("p h c -> p (h c)"))
            evac(cx, 'act', pU, pU[:, 256:512], Ub, Ub[:, 4:8, :].rearrange("p h c -> p (h c)"))
            for h in range(8):
                hs = slice(h * 64, (h + 1) * 64)
                mm(cx, pY, pY[:64, hs], TM[:, 2, hs], Ub[:, h, :], [TM, Ub], True, False)
                mm(cx, pY, pY[:64, hs], TM[:, 3, hs], vv[:, hs], [TM, I6], False, True)
            for h in range(8):
                hs = slice(h * 64, (h + 1) * 64)
                mm(cx, pO, pO[:, hs], XT[:, h, 1, :], ST[:, h, :], [XT, ST], True, False)
                mm(cx, pO, pO[:, hs], MM_[:, h, 128:256], Ub[:, h, :], [MM_, Ub], False, False)
                mm(cx, pO, pO[:, hs], MM_[:, h, 384:512], vv[:, hs], [MM_, I6], False, True)
            cx.op('dve', [pY, pc], [ST],
                  lambda e: e.tensor_tensor(ST[:, :, :], pY[:64, :].rearrange("p (h c) -> p h c", h=8),
                                            pc[:, :].unsqueeze(2).broadcast_to([64, 8, 64]), ALU.mult))
            cx.op('dve', [ST, STs], [ST], lambda e: e.tensor_tensor(ST[:, :, :], ST[:, :, :], STs[:, :, :], ALU.add))
            evac(cx, 'act', pO, pO[:, :], Ob, Ob[:, :])
            g3, b8b, y = G3[c % 2], b8[c % 2], yo[c % 2]
            cx.dma('sp', g3[:, :], RW['gg'][rows, :], writes=[g3])
            cx.dma('act', b8b[:, :], RW['bc'][rows, :], writes=[b8b])
            cx.op('dve', [Ob], [m8], lambda e: e.tensor_reduce(out=m8[:, :], in_=v3(Ob[:, :]), axis=AX.X, op=ALU.add))
            cx.op('dve', [m8], [m8], lambda e: e.tensor_scalar(m8[:, :], m8[:, :], 1.0 / 64, None, ALU.mult))
            cx.op('dve', [Ob, m8], [Ob], lambda e: e.tensor_tensor(v3(Ob[:, :]), v3(Ob[:, :]), bc8(m8[:, :]), ALU.subtract))
            cx.op('pool', [Ob], [t512], lambda e: e.tensor_tensor(t512[:, :], Ob[:, :], Ob[:, :], ALU.mult))
            cx.op('dve', [t512], [r8], lambda e: e.tensor_reduce(out=r8[:, :], in_=v3(t512[:, :]), axis=AX.X, op=ALU.add))
            cx.op('dve', [r8], [r8], lambda e: e.tensor_scalar(r8[:, :], r8[:, :], 1.0 / 64, 64e-5, ALU.mult, ALU.add))
            cx.op('pool', [r8, cx.neghalf], [r8],
                  lambda e: e.tensor_tensor(r8[:, :], r8[:, :], cx.neghalf[:, 0:1].to_broadcast([128, 8]), ALU.pow))
            cx.op('dve', [Ob, r8], [y], lambda e: e.tensor_tensor(v3(y[:, :]), v3(Ob[:, :]), bc8(r8[:, :]), ALU.mult))
            cx.op('pool', [y, gwB], [y], lambda e: e.tensor_tensor(y[:, :], y[:, :], gwB[:, :], ALU.mult))
            cx.op('pool', [y, gbB], [y], lambda e: e.tensor_tensor(y[:, :], y[:, :], gbB[:, :], ALU.add))
            cx.op('dve', [I6, b8b], [t512], lambda e: e.tensor_tensor(v3(t512[:, :]), v3(vv), bc8(b8b[:, :]), ALU.mult))
            cx.op('pool', [y, t512], [y], lambda e: e.tensor_tensor(y[:, :], y[:, :], t512[:, :], ALU.add))
            cx.op('dve', [y, g3], [y], lambda e: e.tensor_tensor(y[:, :], y[:, :], g3[:, :], ALU.mult))
            cx.dma('pool', Y_d[rows, :], y[:, :], reads=[y])

        for c0 in range(0, NT, 2):
            cs = list(range(c0, min(NT, c0 + 2)))
            lists = []
            for c in cs:
                cx._rec = []
                pre(c)
                lists.append(cx._rec)
                cx._rec = None
            for i in range(max(len(l_) for l_ in lists)):
                for l_ in lists:
                    if i < len(l_):
                        l_[i]()
            for c in cs:
                seq(c)
    cx.barrier()


NSA_SCALE = 128 ** -0.5
NSA_BIG = 30000.0
NSA_STOP = 0


def phase_nsa(cx, S, l, P_d, W, Wb, C, Youts, scr):
    NT = S // 128
    G = min(512, S)
    NG = G // 128
    QW = min(512, S)
    NJ = QW // 128
    Nc = (S - 32) // 16 + 1
    NKB = (Nc + 127) // 128
    N = scr['nsa']
    with contextlib.ExitStack() as st:
        pn = [cx.sb(st, [128, 1292], F32, name="pn") for _ in range(2)]
        cs_t = [cx.sb(st, [128, 32], F32, name="cs") for _ in range(2)]
        ro = [cx.sb(st, [128, 10, 32], F32, name="ro") for _ in range(2)]
        t1s = [cx.sb(st, [128, 10, 16], F32, name="t1") for _ in range(2)]
        t2s = [cx.sb(st, [128, 10, 16], F32, name="t2") for _ in range(2)]
        fb = [cx.sb(st, [128, 8, 128], BF16, name="fb") for _ in range(2)]
        va = [cx.sb(st, [128, 2, 129], BF16, name="va") for _ in range(2)]
        sqs = [cx.sb(st, [128, 6, 128], F32, name="sq") for _ in range(2)]
        n6 = [cx.sb(st, [128, 6], F32, name="n6") for _ in range(2)]
        nqb = [cx.sb(st, [128, 4], BF16, name="nqb") for _ in range(2)]
        gt = [cx.sb(st, [128, 12], F32, name="gt") for _ in range(2)]
        kmx = cx.sb(st, [128, 2], F32, name="kmx")
        gA = cx.sb(st, [128, 8, G], BF16, name="gA")
        gN = cx.sb(st, [1, 4, G], BF16, name="gN")
        trp = TrPool(cx, st)
        trf = TrPool(cx, st, n=1, dtype=F32)
        cx.op('pool', [], [kmx], lambda e: e.memset(kmx[:, :], 0.0))
        def body(tt):
            t1, t2, sq = t1s[tt % 2], t2s[tt % 2], sqs[tt % 2]
            tl = tt % NG
            rows = slice(tt * 128, (tt + 1) * 128)
            p, c_t, r, f, v, n6b, nq_, g_ = pn[tt % 2], cs_t[tt % 2], ro[tt % 2], fb[tt % 2], va[tt % 2], n6[tt % 2], nqb[tt % 2], gt[tt % 2]
            cx.dma('sp', p[:, :], P_d[rows, OFF_NSA:OFF_NSA + 1292], writes=[p])
            cx.dma('act', c_t[:, 0:16], C['nsa_cos'][rows, :], writes=[c_t])
            cx.dma('act', c_t[:, 16:32], C['nsa_sin'][rows, :], writes=[c_t])
            blk = p[:, 0:1280].rearrange("p (b c) -> p b c", b=10)
            cb = c_t[:, 0:16].unsqueeze(1).broadcast_to([128, 10, 16])
            sb_ = c_t[:, 16:32].unsqueeze(1).broadcast_to([128, 10, 16])
            rope_tm(cx, p, blk[:, :, 0:16], blk[:, :, 16:32], [c_t, cb], [c_t, sb_],
                    r, r[:, :, 0:16], r[:, :, 16:32], [(t1, t1[:, :, :]), (t2, t2[:, :, :])])
            cx.op('act', [p], [f], lambda e: e.activation(out=f[:, 0:4, 32:128], in_=blk[:, 0:4, 32:128], func=AF.Copy, scale=NSA_SCALE))
            cx.op('act', [r], [f], lambda e: e.activation(out=f[:, 0:4, 0:32], in_=r[:, 0:4, :], func=AF.Copy, scale=NSA_SCALE))
            for dst, src in ((4, 4), (6, 6), (7, 8)):
                cx.op('act', [p], [f], lambda e, dst=dst, src=src: e.copy(f[:, dst, 32:128], blk[:, src, 32:128]))
                cx.op('pool', [r], [f], lambda e, dst=dst, src=src: e.tensor_copy(f[:, dst, 0:32], r[:, src, :]))
            cx.op('act', [p], [f], lambda e: e.copy(f[:, 5, :], blk[:, 5, :]))
            cx.op('pool', [p], [v], lambda e: e.tensor_copy(v[:, 0, 0:128], blk[:, 7, :]))
            cx.op('pool', [p], [v], lambda e: e.tensor_copy(v[:, 1, 0:128], blk[:, 9, :]))
            cx.op('pool', [], [v], lambda e: e.memset(v[:, :, 128:129], 1.0))
            cx.op('dve', [p], [sq], lambda e: e.tensor_tensor(sq[:, 0:4, :], blk[:, 0:4, :], blk[:, 0:4, :], ALU.mult))
            cx.op('dve', [p], [sq], lambda e: e.tensor_tensor(sq[:, 4, :], blk[:, 6, :], blk[:, 6, :], ALU.mult))
            cx.op('dve', [p], [sq], lambda e: e.tensor_tensor(sq[:, 5, :], blk[:, 8, :], blk[:, 8, :], ALU.mult))
            cx.op('dve', [sq], [n6b], lambda e: e.tensor_reduce(out=n6b[:, :], in_=sq[:, :, :], axis=AX.X, op=ALU.add))
            cx.op('dve', [n6b, kmx], [kmx], lambda e: e.tensor_tensor(kmx[:, :], kmx[:, :], n6b[:, 4:6], ALU.max))
            cx.op('act', [n6b], [n6b], lambda e: e.activation(out=n6b[:, 0:4], in_=n6b[:, 0:4], func=AF.Sqrt))
            cx.op('dve', [n6b], [nq_], lambda e: e.tensor_scalar(nq_[:, :], n6b[:, 0:4], -NSA_SCALE, None, ALU.mult))
            cx.dma('sp', g_[:, :], P_d[rows, OFF_NSA + 1280:OFF_NSA + 1292], writes=[g_])
            cx.op('act', [g_], [g_], lambda e: e.activation(out=g_[:, :], in_=g_[:, :], func=AF.Sigmoid))
            cx.dma('pool', N['ng'][rows, :], g_[:, :], reads=[g_])
            trp.transpose_cols(f, lambda j: f[:, j, :], 8, gA, lambda j0, cnt: gA[:, j0:j0 + cnt, tl * 128:(tl + 1) * 128])
            trp.transpose_cols(nq_, lambda j: nq_[:, j:j + 1], 4, gN,
                               lambda j0, cnt: gN[0:1, j0:j0 + cnt, tl * 128:(tl + 1) * 128], blkw=1)
            cx.dma('pool', N['vsa'][rows, :], v[:, 0, :], reads=[v])
            cx.dma('pool', N['vwa'][rows, :], v[:, 1, :], reads=[v])
            if tl == NG - 1:
                def post():
                    g0 = (tt // NG) * G
                    cx.dma('pool', N['qT'][:, :, g0:g0 + G].rearrange("h d s -> d h s"), gA[:, 0:4, :], reads=[gA])
                    for j, nm in ((4, 'kcT'), (5, 'vcT'), (6, 'ksT'), (7, 'kwT')):
                        cx.dma('pool', N[nm][:, g0:g0 + G], gA[:, j, :], reads=[gA])
                    cx.dma('pool', N['nq'][:, g0:g0 + G].rearrange("(o h) s -> o h s", o=1), gN[0:1, :, :], reads=[gN])
                return post
        run_tiles(cx, body, NT)
        kms = cx.sb(st, [128, 1], F32, name="kms")
        kmw = cx.sb(st, [128, 1], F32, name="kmw")
        k1 = cx.sb(st, [128, 1], F32, name="k1")
        rowsb = cx.sb(st, [1, 2, 128], BF16, name="rowsb")
        for i, dstc in enumerate((kms, kmw)):
            cx.op('pool', [kmx], [k1], lambda e: e.tensor_copy(k1[:, :], kmx[:, i:i + 1]))
            bcast_scalar_max(cx, st, trf, k1, dstc)
            cx.op('act', [dstc], [dstc], lambda e: e.activation(out=dstc[:, :], in_=dstc[:, :], func=AF.Sqrt))
            cx.op('dve', [dstc], [rowsb], lambda e: e.tensor_copy(rowsb[0:1, i, :], dstc[0:1, 0:1].to_broadcast([1, 128])))
        cx.dma('pool', N['krow'].rearrange("a b -> (a b)").rearrange("(o n) -> o n", o=1), rowsb[0:1, :, :].rearrange("o a b -> o (a b)"), reads=[rowsb])
    cx.barrier()
    if NSA_STOP == 1:
        return
    with contextlib.ExitStack() as st:
        KC = cx.sb(st, [128, 256], BF16, name="KC")
        VCA = cx.sb(st, [128, 2, 193], BF16, name="VCA")
        krow = cx.sb(st, [128, 3, 128], BF16, name="krow")
        cx.op('pool', [], [krow], lambda e: e.memset(krow[:, :, :], 0.0))
        cx.dma('sp', krow[0:1, 0:2, :].rearrange("o a b -> o (a b)"), N['krow'].rearrange("a b -> (a b)").rearrange("(o n) -> o n", o=1), writes=[krow])
        cx.dma('sp', VCA[:, :, 129:193], C['cover'].rearrange("(kb p) j -> p kb j", p=128), writes=[VCA])
        cx.op('pool', [], [VCA], lambda e: e.memset(VCA[:, :, 0:129], 0.0))
        cx.op('pool', [], [VCA], lambda e: e.memset(VCA[:, :, 128:129], 1.0))
        cx.op('pool', [], [KC], lambda e: e.memset(KC[:, :], 0.0))
        with contextlib.ExitStack() as s2:
            pm = [cx.ps(s2, [128, 512], F32, name="pm") for _ in range(2)]
            xT = [cx.sb(s2, [128, S], BF16, name="xT") for _ in range(2)]
            cx.dma('sp', xT[0][:, :], N['kcT'][:, :], writes=[xT[0]])
            cx.dma('act', xT[1][:, :], N['vcT'][:, :], writes=[xT[1]])
            w1 = [cx.sb(s2, [128, 32, 128], BF16, name="w1") for _ in range(2)]
            w2 = [cx.sb(s2, [128, 128], BF16, name="w2") for _ in range(2)]
            posf = cx.sb(s2, [32, 2, 128], F32, name="posf")
            posb = cx.sb(s2, [32, 2, 128], BF16, name="posb")
            posT = cx.sb(s2, [128, 2, 32], BF16, name="posT")
            bias = cx.sb(s2, [128, 2], F32, name="bias")
            xs = cx.sb(s2, [128, 256], F32, name="xs")
            x2 = cx.sb(s2, [128, 256], F32, name="x2")
            hid = [cx.sb(s2, [128, 256], BF16, name="hid") for _ in range(2)]
            ksq = cx.sb(s2, [128, 256], BF16, name="ksq")
            one = cx.sb(s2, [1, 2], F32, name="one")
            trp = TrPool(cx, s2, n=1)
            for z in range(2):
                cx.dma('sp', w1[z][:, :, :], Wb['nsa_cmp_w1'][l][z].rearrange("(l d) e -> d l e", d=128), writes=[w1[z]])
                cx.dma('sp', w2[z][:, :], Wb['nsa_cmp_w2'][l][z], writes=[w2[z]])
            cx.dma('sp', posf[:, :, :], W['nsa_cmp_pos'][l].rearrange("z l d -> l z d"), writes=[posf])
            cx.op('dve', [posf], [posb], lambda e: e.tensor_copy(posb[:, :, :], posf[:, :, :]))
            trp.transpose_cols(posb, lambda j: posb[:, j, :], 2, posT, lambda j0, cnt: posT[:, j0:j0 + cnt, :], rows=32)
            for z in range(2):
                pb, ph = pm
                for ll in range(32):
                    mm(cx, pb, pb[:, z:z + 1], w1[z][:, ll, :], posT[:, z, ll:ll + 1], [w1[z], posT], ll == 0, ll == 31)
                evac(cx, 'dve', pb, pb[:, z:z + 1], bias, bias[:, z:z + 1])
                for ll in range(32):
                    mm(cx, ph, ph[:, :Nc], w1[z][:, ll, :], xT[z][:, ll:ll + 16 * (Nc - 1) + 1:16], [w1[z], xT[z]], ll == 0, ll == 31)
                cx.op('act', [ph, bias], [xs], lambda e: e.activation(out=xs[:, :Nc], in_=ph[:, :Nc], func=AF.Identity, bias=bias[:, z:z + 1]))
                cx.op('dve', [xs], [x2], lambda e: e.tensor_tensor(x2[:, :Nc], xs[:, :Nc], xs[:, :Nc], ALU.mult))
                cx.op('dve', [x2], [x2], lambda e: e.tensor_scalar(x2[:, :Nc], x2[:, :Nc], 0.044715, 1.0, ALU.mult, ALU.add))
                cx.op('dve', [x2, xs], [x2], lambda e: e.tensor_tensor(x2[:, :Nc], x2[:, :Nc], xs[:, :Nc], ALU.mult))
                cx.op('act', [x2], [x2], lambda e: e.activation(out=x2[:, :Nc], in_=x2[:, :Nc], func=AF.Tanh, scale=0.7978845608028654))
                cx.op('dve', [x2], [x2], lambda e: e.tensor_scalar(x2[:, :Nc], x2[:, :Nc], 1.0, 0.5, ALU.add, ALU.mult))
                cx.op('dve', [x2, xs], [hid[z]], lambda e: e.tensor_tensor(hid[z][:, :Nc], x2[:, :Nc], xs[:, :Nc], ALU.mult))
            pk = pm[0]
            mm(cx, pk, pk[:, :Nc], w2[0][:, :], hid[0][:, :Nc], [w2[0], hid[0]], True, True)
            evac(cx, 'act', pk, pk[:, :Nc], KC, KC[:, :Nc])
            cx.op('act', [pk], [ksq], lambda e: e.activation(out=ksq[:, :Nc], in_=pk[:, :Nc], func=AF.Square))
            pr = pm[1]
            mm(cx, pr, pr[0:1, :Nc], cx.ones_bf[:, 0:1], ksq[:, :Nc], [cx.ones_bf, ksq], True, True)
            cx.op('dve', [pr], [one], lambda e: e.tensor_reduce(out=one[0:1, 0:1], in_=pr[0:1, :Nc], axis=AX.X, op=ALU.max))
            cx.op('act', [one], [one], lambda e: e.activation(out=one[0:1, 0:1], in_=one[0:1, 0:1], func=AF.Sqrt))
            cx.op('dve', [one], [krow], lambda e: e.tensor_scalar(krow[0:1, 2, :], one[0:1, 0:1].to_broadcast([1, 128]), 1.02, None, ALU.mult))
            for kb in range(NKB):
                nk = min(128, Nc - kb * 128)
                pv = pm[kb % 2]
                mm(cx, pv, pv[:nk, 0:128], hid[1][:, kb * 128:kb * 128 + nk], w2[1][:, :], [hid[1], w2[1]], True, True)
                evac(cx, 'dve', pv, pv[:nk, 0:128], VCA, VCA[:nk, kb, 0:128])
        cx.barrier()
        if NSA_STOP == 2:
            return
        res = AttnRes(cx, st, 193)
        qT = [cx.sb(st, [128, S], BF16, name="qT") for _ in range(4)]
        nq = [cx.sb(st, [128, S], BF16, name="nq") for _ in range(4)]
        for h in range(4):
            cx.op('pool', [], [nq[h]], lambda e: e.memset(nq[h][:, :], 0.0))
            cx.dma('sp', qT[h][:, :], N['qT'][h], writes=[qT[h]])
            cx.dma('act', nq[h][0:1, :], N['nq'][h:h + 1, :], writes=[nq[h]])
        ksT = cx.sb(st, [128, S], BF16, name="ksT")
        kwT = cx.sb(st, [128, S], BF16, name="kwT")
        cx.dma('sp', ksT[:, :], N['ksT'][:, :], writes=[ksT])
        cx.dma('act', kwT[:, :], N['kwT'][:, :], writes=[kwT])
        vsa = cx.sb(st, [128, NT, 129], BF16, name="vsa")
        vwa = cx.sb(st, [128, NT, 129], BF16, name="vwa")
        cx.dma('sp', vsa[:, :, :], N['vsa'].rearrange("(t p) c -> p t c", p=128), writes=[vsa])
        cx.dma('act', vwa[:, :, :], N['vwa'].rearrange("(t p) c -> p t c", p=128), writes=[vwa])
        cmask = cx.sb(st, [128, 4, 512], BF16, name="cmask")
        wmask = cx.sb(st, [128, 4, 512], BF16, name="wmask")
        cx.dma('sp', cmask[:, :, :], C['cmask'].rearrange("i k q -> k i q"), writes=[cmask])
        cx.dma('sp', wmask[:, :, :], C['wmask'].rearrange("i k q -> k i q"), writes=[wmask])
        cmpm = cx.sb(st, [128, 2, S], BF16, name="cmpm")
        cx.dma('sp', cmpm[:, :, :], C['cmpmask'].rearrange("kb p q -> p kb q"), writes=[cmpm])
        Em = cx.sb(st, [64, S], BF16, name="Em")
        cx.dma('sp', Em[:, :], C['Emat'][:, :], writes=[Em])
        gts = cx.sb(st, [128, NT, 12], F32, name="gts")
        cx.dma('sp', gts[:, :, :], N['ng'].rearrange("(t p) c -> p t c", p=128), writes=[gts])
        imp = cx.sb(st, [128, 4, 64], F32, name="imp")
        fbt = [cx.sb(st, [128, 64], F32, name="fbt") for _ in range(2)]
        m8 = cx.sb(st, [128, 16], F32, name="m8")
        val2 = cx.sb(st, [128, 64], F32, name="val2")
        selb = cx.sb(st, [128, 64], BF16, name="selb")
        selT = cx.sb(st, [64, 512], BF16, name="selT")
        rc = [cx.sb(st, [128, 1], F32, name="rc") for _ in range(2)]
        rg = [cx.sb(st, [128, 1], F32, name="rg") for _ in range(2)]
        ot = [cx.sb(st, [128, 128], F32, name="ot") for _ in range(3)]
        it = [cx.sb(st, [128, 64], F32, name="it") for _ in range(2)]
        trp2 = TrPool(cx, st, n=1)
        cnt = [0]

        def make_epi(branch, h):
            def epi(QB, j, accb, acc_ap):
                k = cnt[0]
                cnt[0] += 1
                tt = QB * NJ + j
                r, rgb, o = rc[k % 2], rg[k % 2], ot[k % 3]
                cx.op('dve', [accb], [r], lambda e: e.tensor_scalar(r[:, :], acc_ap[:, 128:129], 1e-30, None, ALU.add))
                cx.op('dve', [r], [r], lambda e: e.reciprocal(r[:, :], r[:, :]))
                cx.op('dve', [r, gts], [rgb], lambda e: e.tensor_tensor(rgb[:, :], r[:, :], gts[:, tt, h * 3 + branch:h * 3 + branch + 1], ALU.mult))
                cx.op('act', [accb, rgb], [o], lambda e: e.activation(out=o[:, :], in_=acc_ap[:, 0:128], func=AF.Copy, scale=rgb[:, 0:1]))
                cx.dma('pool', Youts[branch][tt * 128:(tt + 1) * 128, h * 128:(h + 1) * 128], o[:, :], reads=[o])
                if branch == 0:
                    if h == 0:
                        cx.op('dve', [accb, r], [imp], lambda e: e.tensor_scalar(imp[:, j, :], acc_ap[:, 129:193], r[:, 0:1], None, ALU.mult))
                    else:
                        i_ = it[k % 2]
                        cx.op('dve', [accb, r], [i_], lambda e: e.tensor_scalar(i_[:, :], acc_ap[:, 129:193], r[:, 0:1], None, ALU.mult))
                        cx.op('pool', [i_, imp], [imp], lambda e: e.tensor_tensor(imp[:, j, :], imp[:, j, :], i_[:, :], ALU.add))
            return epi

        def cmp_blocks(QB):
            q0 = QB * QW
            return [dict(kb=kb, k0=kb * 128, nk=min(128, Nc - kb * 128),
                         mask=(cmpm, cmpm[:min(128, Nc - kb * 128), kb, q0:q0 + QW])) for kb in range(NKB)]

        def slc_blocks(QB):
            bl = causal_blocks(QB, QW, cmask)
            for b in bl:
                b['extra'] = ([Em], Em[:, b['k0']:b['k0'] + 128], [selT], selT[:, :QW])
            return bl

        def win_blocks(QB):
            out = []
            nd = QW // 128
            for kb in range(max(0, nd * QB - 4), nd * (QB + 1)):
                i = kb - nd * QB
                m = (cmask, cmask[:, i, :QW]) if i >= 0 else (wmask, wmask[:, i + 4, :QW])
                out.append(dict(kb=kb, k0=kb * 128, nk=128, mask=m))
            return out

        for QB in range(S // QW):
            if NSA_STOP == 3:
                break
            for h in range(4 if NSA_STOP != 8 else 0):
                attn_core(cx, res, S, [(qT[h], qT[h][:, :]), (nq[h], nq[h][:, :])],
                          [(KC, KC[:, :]), (krow, lambda k0, nk: krow[:, 2, :nk])],
                          lambda kb, nk: (VCA, VCA[:nk, kb, :]), 193, cmp_blocks, make_epi(0, h), qbs=[QB])
            if NSA_STOP == 4:
                break
            for j in range(NJ if NSA_STOP != 8 else 0):
                tt = QB * NJ + j
                f_ = fbt[j % 2]
                cx.dma('sp', f_[:, :], C['fbias'][tt * 128:(tt + 1) * 128, :], writes=[f_])
                cx.op('dve', [imp, f_], [f_], lambda e: e.tensor_tensor(f_[:, :], f_[:, :], imp[:, j, :], ALU.add))
                cx.op('dve', [f_], [m8], lambda e: e.max(out=m8[:, 0:8], in_=f_[:, :]))
                cx.op('dve', [f_, m8], [val2], lambda e: e.match_replace(out=val2[:, :], in_to_replace=m8[:, 0:8], in_values=f_[:, :], imm_value=-3.0e38))
                cx.op('dve', [val2], [m8], lambda e: e.max(out=m8[:, 8:16], in_=val2[:, :]))
                cx.op('dve', [f_, m8], [val2], lambda e: e.tensor_scalar(val2[:, :], f_[:, :], m8[:, 15:16], None, ALU.is_ge))
                cx.op('dve', [val2], [selb], lambda e: e.tensor_scalar(selb[:, :], val2[:, :], -1.0, NSA_BIG, ALU.add, ALU.mult))
                trp2.transpose_cols(selb, lambda jj: selb[:, :], 1, selT, lambda j0, c_, j=j: selT[:64, j * 128:(j + 1) * 128].unsqueeze(1), blkw=64)
            for h in range(4):
                attn_core(cx, res, S, [(qT[h], qT[h][:, :]), (nq[h], nq[h][:, :])],
                          [(kwT, kwT[:, :]), (krow, lambda k0, nk: krow[:, 1, :nk])],
                          lambda kb, nk: (vwa, vwa[:nk, kb, :]), 129, win_blocks, make_epi(2, h), qbs=[QB])
            if NSA_STOP == 5:
                break
            for h in range(4 if NSA_STOP not in (8, 9) else 0):
                attn_core(cx, res, S, [(qT[h], qT[h][:, :]), (nq[h], nq[h][:, :])],
                          [(ksT, ksT[:, :]), (krow, lambda k0, nk: krow[:, 0, :nk])],
                          lambda kb, nk: (vsa, vsa[:nk, kb, :]), 129, slc_blocks, make_epi(1, h), qbs=[QB])
    cx.barrier()


S_FULL = 4096
ENABLE = {'mla': True, 'nsa': True, 'rwkv': True, 'ret': True}


def phase_zero(cx, S, Y_d):
    with contextlib.ExitStack() as st:
        z = cx.sb(st, [128, 512], F32, name="z")
        cx.op('pool', [], [z], lambda e: e.memset(z[:, :], 0.0))
        for tt in range(S // 128):
            cx.dma('sp', Y_d[tt * 128:(tt + 1) * 128, :], z[:, :], reads=[z])
    cx.barrier()


def host_consts(S):
    import ml_dtypes
    bf = ml_dtypes.bfloat16
    c = {}
    c['ident'] = np.eye(128, dtype=np.float32).astype(bf)
    kk = np.arange(128)[:, None]
    qq = np.arange(512)[None, :]
    c['cmask'] = np.stack([(128 * i + kk <= qq) for i in range(4)]).astype(np.float32).astype(bf)
    t = np.arange(S, dtype=np.float32)[:, None]

    def tables(inv):
        ang = (t * inv[None, :].astype(np.float32)).astype(np.float32)
        return np.cos(ang).astype(np.float32), np.sin(ang).astype(np.float32)

    inv_mla = (np.float32(500000.0) ** (-np.arange(0, 64, 2, dtype=np.float32) / np.float32(64))).astype(np.float32)
    inv_nsa = (np.float32(500000.0) ** (-np.arange(0, 32, 2, dtype=np.float32) / np.float32(32))).astype(np.float32)
    inv_ret = (np.float32(10000.0) ** (-np.linspace(0.0, 1.0, 32, dtype=np.float32))).astype(np.float32)
    c['mla_cos'], c['mla_sin'] = tables(inv_mla)
    c['nsa_cos'], c['nsa_sin'] = tables(inv_nsa)
    c['ret_cos'], c['ret_sin'] = tables(inv_ret)
    kf = kk.astype(np.float64)
    qf = qq.astype(np.float64)
    gdec = np.zeros((4, 5, 128, 512), np.float32)
    for h in range(4):
        lg = np.log1p(-2.0 ** (-5 - h))
        gdec[h, 0] = np.exp((qf - kf) * lg)
        for i in range(4):
            d = qf - kf - 128 * i
            gdec[h, 1 + i] = np.where(d >= 0, np.exp(np.maximum(d, 0) * lg), 0.0)
    c['gdec'] = gdec
    si = np.arange(128)[:, None]
    ti = np.arange(128)[None, :]
    c['wmask'] = (1.0 - c['cmask'].astype(np.float32)).astype(bf)
    Nc = (S - 32) // 16 + 1
    n = np.arange(256)
    q = np.arange(S)
    cm = ((16 * n[:, None] + 31 <= q[None, :]) & (n[:, None] < Nc)).astype(np.float32)
    c['cmpmask'] = cm.reshape(2, 128, S).astype(bf)
    nblk = S // 64
    jb = np.arange(64)
    cstart = 16 * n
    cend = cstart + 31
    cover = ((cstart[:, None] <= jb[None, :] * 64 + 63) & (cend[:, None] >= jb[None, :] * 64) & (n[:, None] < Nc)
             & (jb[None, :] < nblk)).astype(np.float32)
    c['cover'] = cover.astype(bf)
    c['Emat'] = (q[None, :] // 64 == jb[:, None]).astype(np.float32).astype(bf)
    cur = q // 64
    forced = (jb[None, :] == 0) | (jb[None, :] == cur[:, None]) | (jb[None, :] == cur[:, None] - 1)
    visible = (jb[None, :] <= cur[:, None]) & (jb[None, :] < nblk)
    c['fbias'] = np.where(visible, 1000.0 * forced, -1.0e30).astype(np.float32)
    c['rwm'] = np.concatenate([(si < ti), (si <= ti), (si > ti)], axis=1).astype(np.float32)
    return c


CONST_SPECS = {'ident': ([128, 128], BF16), 'cmask': ([4, 128, 512], BF16),
               'mla_cos': (None, F32), 'mla_sin': (None, F32), 'nsa_cos': (None, F32), 'nsa_sin': (None, F32),
               'ret_cos': (None, F32), 'ret_sin': (None, F32), 'gdec': ([4, 5, 128, 512], F32), 'rwm': ([128, 384], F32), 'wmask': ([4, 128, 512], BF16), 'cmpmask': ('cmp', BF16),
               'cover': ([256, 64], BF16), 'Emat': ('E', BF16), 'fbias': ('fb', F32)}

WEIGHT_SHAPES = {
    'w_in': [DEPTH, D_MODEL, IN_WIDTH], 'w_branch': [DEPTH, 4, BW, D_MODEL], 'w_out': [DEPTH, D_MODEL, D_MODEL],
    'w_up': [DEPTH, D_MODEL, D_FF], 'w_down': [DEPTH, D_FF, D_MODEL], 'norm_gains': [DEPTH, 4, D_MODEL],
    'mla_g_q': [DEPTH, 384], 'mla_g_kv': [DEPTH, 128], 'mla_w_uq': [DEPTH, 384, 768], 'mla_w_ukv': [DEPTH, 128, 1024],
    'nsa_cmp_pos': [DEPTH, 2, 32, 128], 'nsa_cmp_w1': [DEPTH, 2, 4096, 128], 'nsa_cmp_w2': [DEPTH, 2, 128, 128],
    'rwkv_mu': [DEPTH, 1984], 'rwkv_w0': [DEPTH, 512], 'rwkv_w2': [DEPTH, 96, 512], 'rwkv_a0': [DEPTH, 512],
    'rwkv_a2': [DEPTH, 96, 512], 'rwkv_g2': [DEPTH, 256, 512], 'rwkv_k_k': [DEPTH, 512], 'rwkv_k_a': [DEPTH, 512],
    'rwkv_r_k': [DEPTH, 8, 64], 'rwkv_gn_w': [DEPTH, 512], 'rwkv_gn_b': [DEPTH, 512],
}
CAST = ['w_in', 'w_branch', 'w_out', 'w_up', 'w_down', 'mla_w_uq', 'mla_w_ukv', 'nsa_cmp_w1', 'nsa_cmp_w2',
        'rwkv_w2', 'rwkv_a2', 'rwkv_g2']


def build_program(S, depth=DEPTH):
    cx = Ctx()
    x_d = cx.dram("x", [S, D_MODEL], F32, kind="ExternalInput")
    W = {k: cx.dram(k, shp, F32, kind="ExternalInput") for k, shp in WEIGHT_SHAPES.items()}
    C = {}
    for k, (shp, dt) in CONST_SPECS.items():
        if shp is None:
            shp = [S, 16 if k.startswith('nsa') else 32]
        elif shp == 'cmp':
            shp = [2, 128, S]
        elif shp == 'E':
            shp = [64, S]
        elif shp == 'fb':
            shp = [S, 64]
        C[k] = cx.dram("c_" + k, shp, dt, kind="ExternalInput")
    y_d = cx.dram("y", [S, D_MODEL], F32, kind="ExternalOutput")
    Wb = {k: cx.dram(k + "_bf", WEIGHT_SHAPES[k], BF16) for k in CAST}
    P_d = cx.dram("P", [S, IN_WIDTH], F32)
    Y = [cx.dram("Y%d" % m, [S, BW], F32) for m in range(4)]
    Yn = [cx.dram("Yn%d" % m, [S, BW], F32) for m in range(2)]
    M_d = cx.dram("M", [S, D_MODEL], F32)
    Z_d = cx.dram("Z", [S, D_MODEL], F32)
    xa = cx.dram("xa", [S, D_MODEL], F32)
    xb = cx.dram("xb", [S, D_MODEL], F32)
    scr = dict(qnT=cx.dram("qnT", [4, 128, S], BF16), qrT=cx.dram("qrT", [4, 65, S], BF16),
               knT=cx.dram("knT", [4, 128, S], BF16), krT=cx.dram("krT", [65, S], BF16),
               va=cx.dram("va", [S, 4, 129], BF16),
               rqT=cx.dram("rqT", [4, 64, S], BF16), rkT=cx.dram("rkT", [4, 64, S], BF16),
               rv=cx.dram("rv", [S, 4, 128], BF16))
    scr['rw'] = {nm: cx.dram("rw_" + nm, [S, 512], F32) for nm in ['rr', 'lw', 'k2', 'vv', 'kn', 'aa', 'gg']}
    scr['rw']['bc'] = cx.dram("rw_bc", [S, 8], F32)
    scr['nsa'] = dict(qT=cx.dram("n_qT", [4, 128, S], BF16), nq=cx.dram("n_nq", [4, S], BF16),
                      kcT=cx.dram("n_kcT", [128, S], BF16), vcT=cx.dram("n_vcT", [128, S], BF16),
                      ksT=cx.dram("n_ksT", [128, S], BF16), kwT=cx.dram("n_kwT", [128, S], BF16),
                      vsa=cx.dram("n_vsa", [S, 129], BF16), vwa=cx.dram("n_vwa", [S, 129], BF16),
                      ng=cx.dram("n_ng", [S, 12], F32), krow=cx.dram("n_krow", [2, 128], BF16))
    st = contextlib.ExitStack()
    cx._st = st
    setup_consts(cx, st, C['ident'])
    phase_cast(cx, [(W[k], Wb[k]) for k in CAST])
    xin = x_d
    for l in range(depth):
        g = W['norm_gains'][l]
        phase_in(cx, S, xin, g[0], Wb['w_in'][l], P_d)
        if ENABLE['mla']:
            phase_mla(cx, S, P_d, W['mla_g_q'][l], W['mla_g_kv'][l], Wb['mla_w_uq'][l], Wb['mla_w_ukv'][l],
                      C['mla_cos'], C['mla_sin'], C['cmask'], Y[0], scr)
        else:
            phase_zero(cx, S, Y[0])
        if ENABLE['nsa']:
            phase_nsa(cx, S, l, P_d, W, Wb, C, [Y[1], Yn[0], Yn[1]], scr)
            ysrc1 = [Y[1], Yn[0], Yn[1]]
        else:
            phase_zero(cx, S, Y[1])
            ysrc1 = [Y[1]]
        if ENABLE['rwkv']:
            phase_rwkv(cx, S, l, P_d, W, Wb, C, Y[2], scr)
        else:
            phase_zero(cx, S, Y[2])
        if ENABLE['ret']:
            phase_ret(cx, S, P_d, C['ret_cos'], C['ret_sin'], C['gdec'], Y[3], scr)
        else:
            phase_zero(cx, S, Y[3])
        phase_merge(cx, S, [[Y[0]], ysrc1, [Y[2]], [Y[3]]], Wb['w_branch'][l], P_d, M_d)
        phase_out(cx, S, M_d, Wb['w_out'][l], Z_d)
        phase_normres(cx, S, xin, Z_d, g[1], xa)
        phase_ffn(cx, S, xa, g[2], Wb['w_up'][l], Wb['w_down'][l], Z_d)
        xnext = y_d if l == depth - 1 else xb
        phase_normres(cx, S, xa, Z_d, g[3], xnext)
        xin = xnext
    cx.barrier()
    return cx


_CACHE = {}


def kernel(**inputs):
    x = np.ascontiguousarray(np.asarray(inputs['x'], dtype=np.float32))
    B, S, _ = x.shape
    if S not in _CACHE:
        _CACHE[S] = (build_program(S), host_consts(S))
    cx, consts = _CACHE[S]
    base = {k: np.ascontiguousarray(np.asarray(inputs[k], dtype=np.float32)) for k in WEIGHT_SHAPES}
    for k, v in consts.items():
        base["c_" + k] = np.ascontiguousarray(v)
    in_maps = []
    for b in range(B):
        m = dict(base)
        m['x'] = x[b]
        in_maps.append(m)
    res = run_bass_kernel_spmd(cx.nc, in_maps, core_ids=list(range(B)))
    return np.stack([np.asarray(r['y'], dtype=np.float32) for r in res.results], axis=0)
```
